# Optimizing a Trainium2 kernel written in Bass

```python
import jax, jax.numpy as jnp
from jax import lax
import numpy as np

D_MODEL = 1024
BATCH = 8
SEQ = 4096
DEPTH = 1

MEM_LEN = 256
GDN_HEADS = 4
GDN_DK = 128
GDN_DV = 128
GDN_CONV = 4
GDN_CHUNK = 64
NSA_HEADS = 8
NSA_GROUPS = 2
NSA_DK = 64
NSA_DV = 64
NSA_CMP_BLOCK = 32
NSA_CMP_STRIDE = 16
NSA_SEL_BLOCK = 64
NSA_N_SEL = 16
NSA_WINDOW = 512
NSA_Q_BLOCK = 64
NSA_ROPE_DIM = NSA_DK // 4
MEM_HEADS = 4
MEM_DH = 128
ROPE_THETA = 500000.0
FFN_DIM = 2816
FFN_CONV = 3
EPS = 1e-6

GDN_W = GDN_HEADS * GDN_DV
NSA_W = NSA_HEADS * NSA_DV
MEM_W = MEM_HEADS * MEM_DH
MIX_W = GDN_W + NSA_W + MEM_W
NSA_KV_W = NSA_GROUPS * NSA_DK
IN_SPLITS = (
    3 * GDN_W,
    GDN_HEADS,
    GDN_HEADS,
    GDN_W,
    NSA_HEADS * NSA_DK,
    NSA_KV_W, NSA_KV_W,
    NSA_KV_W, NSA_KV_W,
    NSA_KV_W, NSA_KV_W,
    3 * NSA_HEADS,
    MEM_W,
)
IN_W = sum(IN_SPLITS)

kernel_name = 'hybrid_gdn_nsa_memory_convffn_block'


def rms_norm(x, w):
    xf = x.astype(jnp.float32)
    y = xf * lax.rsqrt(jnp.mean(xf * xf, axis=-1, keepdims=True) + EPS)
    return (y * w.astype(jnp.float32)).astype(x.dtype)


def l2_norm(x):
    xf = x.astype(jnp.float32)
    return xf * lax.rsqrt(jnp.sum(xf * xf, axis=-1, keepdims=True) + EPS)


def causal_dwconv(x, w):
    k, c = w.shape
    return lax.conv_general_dilated(x, w[:, None, :].astype(x.dtype), window_strides=(1,),
                                    padding=[(k - 1, 0)], dimension_numbers=('NWC', 'WIO', 'NWC'),
                                    feature_group_count=c)


def rope_tables(seq):
    pos = jnp.arange(seq, dtype=jnp.float32)
    inv = 1.0 / (ROPE_THETA ** (jnp.arange(0, NSA_ROPE_DIM, 2, dtype=jnp.float32) / NSA_ROPE_DIM))
    ang = pos[:, None] * inv[None, :]
    return jnp.cos(ang), jnp.sin(ang)


def partial_rope(x, cos, sin):
    half = NSA_ROPE_DIM // 2
    c = cos[:, None, :].astype(x.dtype)
    s = sin[:, None, :].astype(x.dtype)
    x1 = x[..., :half]
    x2 = x[..., half:NSA_ROPE_DIM]
    return jnp.concatenate([x1 * c - x2 * s, x2 * c + x1 * s, x[..., NSA_ROPE_DIM:]], axis=-1)


def masked_softmax(s, mask):
    p = jax.nn.softmax(jnp.where(mask, s.astype(jnp.float32), -1e30), axis=-1)
    return jnp.where(mask, p, 0.0)


def gated_delta_rule(q, k, v, a, b, a_log, dt_bias):
    bsz, seq, h, dk = q.shape
    dv = v.shape[-1]
    c = GDN_CHUNK
    n = seq // c
    f32 = jnp.float32
    q = l2_norm(q) * (dk ** -0.5)
    k = l2_norm(k)
    v = v.astype(f32)
    beta = jax.nn.sigmoid(b.astype(f32))
    g = -jnp.exp(a_log.astype(f32)) * jax.nn.softplus(a.astype(f32) + dt_bias.astype(f32))

    def chunk4(t):
        return t.reshape(bsz, n, c, h, -1).transpose(0, 3, 1, 2, 4)

    def chunk3(t):
        return t.reshape(bsz, n, c, h).transpose(0, 3, 1, 2)

    q, k, v = chunk4(q), chunk4(k), chunk4(v)
    beta, g = chunk3(beta), chunk3(g)
    gc = jnp.cumsum(g, axis=-1)
    idx = jnp.arange(c)
    causal = idx[:, None] >= idx[None, :]
    strict = idx[:, None] > idx[None, :]
    diff = gc[..., :, None] - gc[..., None, :]
    decay = jnp.where(causal, jnp.exp(jnp.where(causal, diff, 0.0)), 0.0)
    kb = k * beta[..., None]
    vb = v * beta[..., None]
    lmat = jnp.einsum('bhncd,bhnsd->bhncs', kb, k) * decay * strict
    eye = jnp.eye(c, dtype=f32)
    rhs = jnp.concatenate([vb, kb * jnp.exp(gc)[..., None]], axis=-1)
    sol = lax.linalg.triangular_solve(lmat + eye, rhs, left_side=True, lower=True, unit_diagonal=True)
    u, w = sol[..., :dv], sol[..., dv:]
    aqk = jnp.einsum('bhncd,bhnsd->bhncs', q, k) * decay
    qg = q * jnp.exp(gc)[..., None]
    glast = gc[..., -1]
    kdec = k * jnp.exp(glast[..., None] - gc)[..., None]
    xs = tuple(jnp.moveaxis(t, 2, 0) for t in (u, w, qg, aqk, kdec, glast))

    def step(state, inp):
        u_i, w_i, qg_i, aqk_i, kdec_i, gl_i = inp
        v_new = u_i - jnp.einsum('bhck,bhkv->bhcv', w_i, state)
        o_i = jnp.einsum('bhck,bhkv->bhcv', qg_i, state) + jnp.einsum('bhcs,bhsv->bhcv', aqk_i, v_new)
        state = state * jnp.exp(gl_i)[..., None, None] + jnp.einsum('bhck,bhcv->bhkv', kdec_i, v_new)
        return state, o_i

    state0 = jnp.zeros((bsz, h, dk, dv), f32)
    _, o = lax.scan(step, state0, xs)
    return o.transpose(1, 0, 3, 2, 4).reshape(bsz, seq, h, dv)


def compress_blocks(tok, pos, w1, w2):
    flat = (tok + pos.astype(tok.dtype)).reshape(tok.shape[0], tok.shape[1], tok.shape[2], -1)
    return jax.nn.silu(flat @ w1) @ w2


def nsa_attention(q, kc, vc, ks, vs, kw, vw, gate_logits, cos, sin, q_norm_w, kc_norm_w, ks_norm_w,
                  kw_norm_w, pos_k, pos_v, k_w1, k_w2, v_w1, v_w2):
    bsz, seq = q.shape[:2]
    G = NSA_GROUPS
    R = NSA_HEADS // NSA_GROUPS
    TQ = NSA_Q_BLOCK
    SEL = NSA_SEL_BLOCK
    scale = NSA_DK ** -0.5

    def kv_heads(t):
        return t.reshape(bsz, seq, G, -1)

    q = partial_rope(rms_norm(q.reshape(bsz, seq, NSA_HEADS, NSA_DK), q_norm_w), cos, sin)
    q = q.reshape(bsz, seq, G, R, NSA_DK).transpose(0, 2, 3, 1, 4)
    ks = partial_rope(rms_norm(kv_heads(ks), ks_norm_w), cos, sin).transpose(0, 2, 1, 3)
    kw = partial_rope(rms_norm(kv_heads(kw), kw_norm_w), cos, sin).transpose(0, 2, 1, 3)
    vs = kv_heads(vs).transpose(0, 2, 1, 3)
    vw = kv_heads(vw).transpose(0, 2, 1, 3)

    n_c = (seq - NSA_CMP_BLOCK) // NSA_CMP_STRIDE + 1
    cidx = jnp.arange(n_c)[:, None] * NSA_CMP_STRIDE + jnp.arange(NSA_CMP_BLOCK)[None, :]
    kc_tok = partial_rope(kv_heads(kc), cos, sin).transpose(0, 2, 1, 3)[:, :, cidx]
    vc_tok = kv_heads(vc).transpose(0, 2, 1, 3)[:, :, cidx]
    kcmp = rms_norm(compress_blocks(kc_tok, pos_k, k_w1, k_w2), kc_norm_w)
    vcmp = compress_blocks(vc_tok, pos_v, v_w1, v_w2)
    cmp_end = jnp.arange(n_c) * NSA_CMP_STRIDE + NSA_CMP_BLOCK - 1

    n_s = seq // SEL
    n_sel = min(NSA_N_SEL, n_s)
    ks_blk = ks.reshape(bsz, G, n_s, SEL, NSA_DK)
    vs_blk = vs.reshape(bsz, G, n_s, SEL, NSA_DV)
    ci = jnp.arange(n_c) * NSA_CMP_STRIDE
    sj = jnp.arange(n_s) * SEL
    overlap = (jnp.clip(jnp.minimum(ci[:, None] + NSA_CMP_BLOCK, sj[None, :] + SEL)
                        - jnp.maximum(ci[:, None], sj[None, :]), 0).astype(jnp.float32) / NSA_CMP_STRIDE)
    gather_blocks = jax.vmap(jax.vmap(lambda blocks, ids: blocks[ids]))

    kw_pad = jnp.pad(kw, ((0, 0), (0, 0), (NSA_WINDOW, 0), (0, 0)))
    vw_pad = jnp.pad(vw, ((0, 0), (0, 0), (NSA_WINDOW, 0), (0, 0)))

    gates = jax.nn.sigmoid(gate_logits.astype(jnp.float32)).astype(q.dtype)
    gates = gates.reshape(bsz, seq, G, R, 3).transpose(0, 2, 3, 1, 4)

    def block(i):
        s0 = i * TQ
        t = s0 + jnp.arange(TQ)
        qb = lax.dynamic_slice_in_dim(q, s0, TQ, axis=3)
        gb = lax.dynamic_slice_in_dim(gates, s0, TQ, axis=3)
        cmask = cmp_end[None, :] <= t[:, None]
        sc = jnp.einsum('bgrtd,bgcd->bgrtc', qb, kcmp) * scale
        p_cmp = masked_softmax(sc, cmask)
        o_cmp = jnp.einsum('bgrtc,bgcd->bgrtd', p_cmp.astype(vcmp.dtype), vcmp)
        imp = jnp.einsum('bgrtc,cj->bgtj', p_cmp, overlap)
        blk = jnp.arange(n_s)
        cur = t // SEL
        valid = blk[None, :] <= cur[:, None]
        forced = (blk[None, :] == 0) | (blk[None, :] == cur[:, None]) | (blk[None, :] == cur[:, None] - 1)
        score = jnp.where(valid, jnp.where(forced, 1e6, imp), -1e9)
        _, idx = lax.top_k(score, n_sel)
        kg = gather_blocks(ks_blk, idx).reshape(bsz, G, TQ, n_sel * SEL, NSA_DK)
        vg = gather_blocks(vs_blk, idx).reshape(bsz, G, TQ, n_sel * SEL, NSA_DV)
        kpos = (idx[..., None] * SEL + jnp.arange(SEL)).reshape(bsz, G, TQ, n_sel * SEL)
        smask = (kpos <= t[None, None, :, None])[:, :, None]
        ss = jnp.einsum('bgrtd,bgtkd->bgrtk', qb, kg) * scale
        o_slc = jnp.einsum('bgrtk,bgtkd->bgrtd', masked_softmax(ss, smask).astype(vg.dtype), vg)
        kwb = lax.dynamic_slice_in_dim(kw_pad, s0, NSA_WINDOW + TQ, axis=2)
        vwb = lax.dynamic_slice_in_dim(vw_pad, s0, NSA_WINDOW + TQ, axis=2)
        wpos = s0 - NSA_WINDOW + jnp.arange(NSA_WINDOW + TQ)
        wmask = ((wpos[None, :] >= 0) & (wpos[None, :] <= t[:, None])
                 & (t[:, None] - wpos[None, :] < NSA_WINDOW))
        sw = jnp.einsum('bgrtd,bgkd->bgrtk', qb, kwb) * scale
        o_win = jnp.einsum('bgrtk,bgkd->bgrtd', masked_softmax(sw, wmask).astype(vwb.dtype), vwb)
        return gb[..., 0:1] * o_cmp + gb[..., 1:2] * o_slc + gb[..., 2:3] * o_win

    o = lax.map(block, jnp.arange(seq // TQ))
    return o.transpose(1, 0, 4, 2, 3, 5).reshape(bsz, seq, NSA_W)


def memory_cross_attention(q, mem_n, w_kv, q_norm_w, k_norm_w):
    bsz, seq = q.shape[:2]
    q = rms_norm(q.reshape(bsz, seq, MEM_HEADS, MEM_DH), q_norm_w)
    k, v = jnp.split(mem_n @ w_kv, 2, axis=-1)
    k = rms_norm(k.reshape(bsz, -1, MEM_HEADS, MEM_DH), k_norm_w)
    v = v.reshape(bsz, -1, MEM_HEADS, MEM_DH)
    s = jnp.einsum('bshd,bmhd->bhsm', q, k).astype(jnp.float32) * (MEM_DH ** -0.5)
    p = jax.nn.softmax(s, axis=-1).astype(v.dtype)
    return jnp.einsum('bhsm,bmhd->bshd', p, v).reshape(bsz, seq, MEM_W)


def hybrid_layer(x, mem, cos, sin, attn_norm_w, mem_norm_w, w_in, gdn_conv_w, gdn_a_log, gdn_dt_bias,
                 gdn_out_norm_w, nsa_q_norm_w, nsa_kc_norm_w, nsa_ks_norm_w, nsa_kw_norm_w, nsa_cmp_pos_k,
                 nsa_cmp_pos_v, nsa_cmp_k_w1, nsa_cmp_k_w2, nsa_cmp_v_w1, nsa_cmp_v_w2, mem_w_kv, mem_q_norm_w,
                 mem_k_norm_w, w_out, ffn_norm_w, ffn_w_up, ffn_conv_w, ffn_w_down):
    bsz, seq, _ = x.shape
    xn = rms_norm(x, attn_norm_w)
    proj = xn @ w_in
    (p_qkv, p_a, p_b, p_gate, p_nq, p_kc, p_vc, p_ks, p_vs, p_kw, p_vw, p_ng, p_mq) = jnp.split(
        proj, np.cumsum(IN_SPLITS)[:-1].tolist(), axis=-1)

    qkv = jax.nn.silu(causal_dwconv(p_qkv, gdn_conv_w))
    gq, gk, gv = jnp.split(qkv, 3, axis=-1)
    o_a = gated_delta_rule(gq.reshape(bsz, seq, GDN_HEADS, GDN_DK), gk.reshape(bsz, seq, GDN_HEADS, GDN_DK),
                           gv.reshape(bsz, seq, GDN_HEADS, GDN_DV), p_a, p_b, gdn_a_log, gdn_dt_bias)
    gate_a = jax.nn.silu(p_gate.reshape(bsz, seq, GDN_HEADS, GDN_DV).astype(jnp.float32))
    o_a = (rms_norm(o_a, gdn_out_norm_w) * gate_a).astype(x.dtype).reshape(bsz, seq, GDN_W)

    o_b = nsa_attention(p_nq, p_kc, p_vc, p_ks, p_vs, p_kw, p_vw, p_ng, cos, sin, nsa_q_norm_w, nsa_kc_norm_w,
                        nsa_ks_norm_w, nsa_kw_norm_w, nsa_cmp_pos_k, nsa_cmp_pos_v, nsa_cmp_k_w1, nsa_cmp_k_w2,
                        nsa_cmp_v_w1, nsa_cmp_v_w2)

    o_c = memory_cross_attention(p_mq, rms_norm(mem, mem_norm_w), mem_w_kv, mem_q_norm_w, mem_k_norm_w)

    h = x + jnp.concatenate([o_a, o_b, o_c], axis=-1) @ w_out

    u = causal_dwconv(rms_norm(h, ffn_norm_w) @ ffn_w_up, ffn_conv_w)
    u_gate, u_val = jnp.split(u, 2, axis=-1)
    return h + (jax.nn.silu(u_gate) * u_val) @ ffn_w_down


def setup_inputs(seed: int = 0) -> dict:
    key = jax.random.key(seed)
    keys = iter(jax.random.split(key, 48))
    f32 = jnp.float32
    L = DEPTH

    def nrm(shape, scale):
        return jax.random.normal(next(keys), shape, f32) * scale

    def gain(shape):
        return 1.0 + 0.02 * jax.random.normal(next(keys), shape, f32)

    dt = jnp.exp(jax.random.uniform(next(keys), (L, GDN_HEADS), f32, float(np.log(1e-3)), float(np.log(1e-1))))
    a_log = jnp.log(jax.random.uniform(next(keys), (L, GDN_HEADS), f32, 1.0, 16.0))
    flat_c = NSA_CMP_BLOCK * NSA_DK
    return {
        'x': nrm((BATCH, SEQ, D_MODEL), 1.0),
        'mem': nrm((BATCH, MEM_LEN, D_MODEL), 1.0),
        'attn_norm_w': gain((L, D_MODEL)),
        'mem_norm_w': gain((L, D_MODEL)),
        'w_in': nrm((L, D_MODEL, IN_W), D_MODEL ** -0.5),
        'gdn_conv_w': nrm((L, GDN_CONV, 3 * GDN_W), GDN_CONV ** -0.5),
        'gdn_a_log': a_log,
        'gdn_dt_bias': dt + jnp.log(-jnp.expm1(-dt)),
        'gdn_out_norm_w': gain((L, GDN_DV)),
        'nsa_q_norm_w': gain((L, NSA_DK)),
        'nsa_kc_norm_w': gain((L, NSA_DK)),
        'nsa_ks_norm_w': gain((L, NSA_DK)),
        'nsa_kw_norm_w': gain((L, NSA_DK)),
        'nsa_cmp_pos_k': nrm((L, NSA_CMP_BLOCK, NSA_DK), 0.02),
        'nsa_cmp_pos_v': nrm((L, NSA_CMP_BLOCK, NSA_DV), 0.02),
        'nsa_cmp_k_w1': nrm((L, flat_c, NSA_DK), flat_c ** -0.5),
        'nsa_cmp_k_w2': nrm((L, NSA_DK, NSA_DK), NSA_DK ** -0.5),
        'nsa_cmp_v_w1': nrm((L, NSA_CMP_BLOCK * NSA_DV, NSA_DV), (NSA_CMP_BLOCK * NSA_DV) ** -0.5),
        'nsa_cmp_v_w2': nrm((L, NSA_DV, NSA_DV), NSA_DV ** -0.5),
        'mem_w_kv': nrm((L, D_MODEL, 2 * MEM_W), D_MODEL ** -0.5),
        'mem_q_norm_w': gain((L, MEM_DH)),
        'mem_k_norm_w': gain((L, MEM_DH)),
        'w_out': nrm((L, MIX_W, D_MODEL), MIX_W ** -0.5),
        'ffn_norm_w': gain((L, D_MODEL)),
        'ffn_w_up': nrm((L, D_MODEL, 2 * FFN_DIM), D_MODEL ** -0.5),
        'ffn_conv_w': nrm((L, FFN_CONV, 2 * FFN_DIM), FFN_CONV ** -0.5),
        'ffn_w_down': nrm((L, FFN_DIM, D_MODEL), FFN_DIM ** -0.5),
    }


def reference(x, mem, attn_norm_w, mem_norm_w, w_in, gdn_conv_w, gdn_a_log, gdn_dt_bias, gdn_out_norm_w,
              nsa_q_norm_w, nsa_kc_norm_w, nsa_ks_norm_w, nsa_kw_norm_w, nsa_cmp_pos_k, nsa_cmp_pos_v,
              nsa_cmp_k_w1, nsa_cmp_k_w2, nsa_cmp_v_w1, nsa_cmp_v_w2, mem_w_kv, mem_q_norm_w, mem_k_norm_w,
              w_out, ffn_norm_w, ffn_w_up, ffn_conv_w, ffn_w_down):
    cos, sin = rope_tables(x.shape[1])
    h = x
    for l in range(DEPTH):
        h = hybrid_layer(h, mem, cos, sin, attn_norm_w[l], mem_norm_w[l], w_in[l], gdn_conv_w[l], gdn_a_log[l],
                         gdn_dt_bias[l], gdn_out_norm_w[l], nsa_q_norm_w[l], nsa_kc_norm_w[l], nsa_ks_norm_w[l],
                         nsa_kw_norm_w[l], nsa_cmp_pos_k[l], nsa_cmp_pos_v[l], nsa_cmp_k_w1[l], nsa_cmp_k_w2[l],
                         nsa_cmp_v_w1[l], nsa_cmp_v_w2[l], mem_w_kv[l], mem_q_norm_w[l], mem_k_norm_w[l],
                         w_out[l], ffn_norm_w[l], ffn_w_up[l], ffn_conv_w[l], ffn_w_down[l])
    return h
```

```python
import os
import numpy as np
from contextlib import ExitStack
import concourse.bass as bass
import concourse.mybir as mybir
from concourse.bass_utils import run_bass_kernel_spmd
import ml_dtypes

F32 = mybir.dt.float32
BF16 = mybir.dt.bfloat16
AF = mybir.ActivationFunctionType
ALU = mybir.AluOpType
AX = mybir.AxisListType

S = 4096
D = 1024
NT = S // 128
INW = 3872
TMW = 2336
A_OFF, B_OFF, GATE_OFF, NQ_OFF = 0, 4, 8, 520
KC_OFF, VC_OFF, KS_OFF, VS_OFF, KW_OFF, VW_OFF = 1032, 1160, 1288, 1416, 1544, 1672
NG_OFF, MQ_OFF = 1800, 1824
FF = 2816
NEG = -30000.0
EPS = 1e-6


class T:
    def __init__(self, t, k):
        self.t = t
        self.k = k

    def __getitem__(self, idx):
        return self.t[idx]


class KB:
    NDS = 16

    def __init__(self, nc):
        self.nc = nc
        self.stack = ExitStack()
        self.eng = {'pe': nc.tensor, 'act': nc.scalar, 'dve': nc.vector, 'pool': nc.gpsimd, 'sp': nc.sync}
        self.sem = {}
        for e in self.eng:
            self.sem[e] = self.stack.enter_context(nc.semaphore("s_" + e))
        for j in range(self.NDS):
            self.sem[('d', j)] = self.stack.enter_context(nc.semaphore("d_%d" % j))
        self.cnt = {e: 0 for e in self.eng}
        self.seen = {e: {} for e in self.eng}
        self.state = {}
        self.dma_i = 0
        self.dma_uses = [0] * self.NDS
        self.nins = 0
        self.rr = 0
        self.excl = set()

    def _wait(self, e, evs):
        need = {}
        for (sk, v) in evs:
            if sk == e and e in ('pe', 'sp'):
                continue
            if self.seen[e].get(sk, 0) < v:
                need[sk] = max(need.get(sk, 0), v)
        for sk, v in need.items():
            self.eng[e].wait_ge(self.sem[sk], v)
            self.seen[e][sk] = v

    @staticmethod
    def _keys(lst):
        out = []
        for x in lst:
            if isinstance(x, T):
                out.append(x.k)
            elif isinstance(x, (list, tuple)) and len(x) and isinstance(x[0], T):
                out.append((x[0].k,) + tuple(x[1:]))
            else:
                out.append(x)
        return out

    def _deps(self, reads, writes):
        evs = []
        for k in reads:
            st = self.state.get(k)
            if st and st[0]:
                evs.append(st[0])
        for k in writes:
            st = self.state.get(k)
            if st:
                if st[0]:
                    evs.append(st[0])
                evs.extend(st[1])
        return evs

    def _update(self, ev, reads, writes):
        for k in reads:
            st = self.state.setdefault(k, [None, []])
            st[1].append(ev)
            if len(st[1]) > 12:
                best = {}
                for (sk, v) in st[1]:
                    best[sk] = max(best.get(sk, 0), v)
                st[1] = list(best.items())
        for k in writes:
            self.state[k] = [ev, []]

    def op(self, e, fn, r=(), w=()):
        r = self._keys(r)
        w = self._keys(w)
        w = w + [k for k in r if k in self.excl and k not in w]
        self._wait(e, self._deps(r, w))
        ins = fn(self.eng[e])
        self.cnt[e] += 1
        ins.then_inc(self.sem[e], 1)
        self._update((e, self.cnt[e]), r, w)
        self.nins += 1
        return ins

    def dma(self, q, out, in_, r=(), w=(), **kw):
        r = self._keys(r)
        w = self._keys(w)
        j = self.dma_i % self.NDS
        self.dma_i += 1
        evs = self._deps(r, w)
        if self.dma_uses[j] > 0:
            evs.append((('d', j), 16 * self.dma_uses[j]))
        self._wait(q, evs)
        ins = self.eng[q].dma_start(out=out, in_=in_, **kw)
        self.dma_uses[j] += 1
        ins.then_inc(self.sem[('d', j)], 16)
        ev = (('d', j), 16 * self.dma_uses[j])
        self._update(ev, r, w)
        self.nins += 1
        return ev

    def barrier(self):
        evs = [(f, self.cnt[f]) for f in self.eng if self.cnt[f]]
        evs += [(('d', j), 16 * self.dma_uses[j]) for j in range(self.NDS) if self.dma_uses[j]]
        for e in self.eng:
            self._wait(e, [ev for ev in evs if ev[0] != e])

    def finish(self):
        self.barrier()
        self.stack.close()

    def ew(self, with_act=False):
        self.rr += 1
        lst = ('dve', 'pool', 'act') if with_act else ('dve', 'pool')
        return lst[self.rr % len(lst)]


class Phase:
    def __init__(self, kb, tag):
        self.kb = kb
        self.nc = kb.nc
        self.tag = tag
        self.st = ExitStack()

    def sb(self, name, shape, dt):
        n = self.tag + "_" + name
        return T(self.st.enter_context(self.nc.sbuf_tensor(n, list(shape), dt)), n)

    def sbn(self, name, shape, dt, n):
        return [self.sb("%s%d" % (name, i), shape, dt) for i in range(n)]

    def ps(self, name, shape, dt=F32):
        n = self.tag + "_" + name
        self.kb.excl.add(n)
        return T(self.st.enter_context(self.nc.psum_tensor(n, list(shape), dt)), n)

    def psn(self, name, shape, dt, n):
        return [self.ps("%s%d" % (name, i), shape, dt) for i in range(n)]

    def close(self):
        self.kb.barrier()
        self.st.close()


def load_cast_weight(kb, ph, dst, src_ap, nchunks, ncols, gam=None, stage_cols=None):
    stage_cols = stage_cols or ncols
    stg = ph.sbn("stg_" + dst.k, [128, stage_cols], F32, 2)
    i = 0
    engs = ('dve', 'pool', 'act')
    for c in range(nchunks):
        for c0 in range(0, ncols, stage_cols):
            c1 = min(ncols, c0 + stage_cols)
            sg = stg[i % 2]
            kb.dma('sp', sg[:, 0:c1 - c0], src_ap[c * 128:(c + 1) * 128, c0:c1], w=[sg])
            e = engs[i % 3]
            o = dst[:, c, c0:c1]
            if gam is None:
                if e == 'act':
                    kb.op(e, lambda E: E.copy(out=o, in_=sg[:, 0:c1 - c0]), r=[sg], w=[(dst, c)])
                else:
                    kb.op(e, lambda E: E.tensor_copy(out=o, in_=sg[:, 0:c1 - c0]), r=[sg], w=[(dst, c)])
            else:
                if e == 'act':
                    kb.op(e, lambda E: E.activation(out=o, in_=sg[:, 0:c1 - c0], func=AF.Copy, scale=gam[:, c:c + 1]),
                          r=[sg, gam], w=[(dst, c)])
                else:
                    kb.op(e, lambda E: E.tensor_scalar(out=o, in0=sg[:, 0:c1 - c0], scalar1=gam[:, c:c + 1], scalar2=None,
                                                       op0=ALU.mult), r=[sg, gam], w=[(dst, c)])
            i += 1


def rms_rstd(kb, src_ap, junk, ss, rs, n, rkeys):
    kb.op('act', lambda E: E.activation(out=junk[:, 0:n], in_=src_ap, func=AF.Square, accum_out=ss[:]), r=rkeys, w=[junk, ss])
    kb.op('act', lambda E: E.activation(out=rs[:], in_=ss[:], func=AF.Sqrt, scale=1.0 / n, bias=EPS), r=[ss], w=[rs])
    kb.op('dve', lambda E: E.reciprocal(out=rs[:], in_=rs[:]), r=[rs], w=[rs])


def phase_inproj(kb, io):
    nc = kb.nc
    ph = Phase(kb, "ip")
    winb = ph.sb("winb", [128, 8, INW], BF16)
    gam = ph.sb("gam", [128, 8], F32)
    cw = ph.sb("cw", [128, 4, 12], F32)
    ident = ph.sb("ident", [128, 128], BF16)
    kb.dma('sp', gam[:], io['attn_norm_w'].rearrange("(c p) -> p c", p=128), w=[gam], allow_slow_non_contiguous=True)
    for j in range(4):
        kb.dma('sp', cw[:, j, :], io['gdn_conv_w'][j, :].rearrange("(c p) -> p c", p=128), w=[cw], allow_slow_non_contiguous=True)
    kb.dma('sp', ident[:], io['c_ident'][:, :], w=[ident])
    load_cast_weight(kb, ph, winb, io['w_in'], 8, INW, gam=gam, stage_cols=1936)

    xt = ph.sbn("xt", [128, D], F32, 2)
    junk = ph.sb("junk", [128, D], BF16)
    ss = ph.sbn("ss", [128, 1], F32, 2)
    rs = ph.sbn("rs", [128, 1], F32, 2)
    xn = ph.sbn("xn", [128, D], BF16, 2)
    xnT = ph.sbn("xnT", [128, 8, 512], BF16, 2)
    xc = ph.sbn("xc", [128, 515], F32, 3)
    acc = ph.sbn("acc", [128, 512], F32, 3)
    halo = ph.sb("halo", [128, 12, 3], F32)
    qT = ph.sb("qT", [128, 12, 512], BF16)
    qtm = ph.sbn("qtm", [128, 1536], BF16, 2)
    tmt = ph.sbn("tmt", [128, TMW], F32, 2)
    pT = ph.psn("pT", [128, 8, 128], BF16, 2)
    pF = ph.psn("pF", [128, 512], F32, 2)
    pM = ph.psn("pM", [128, 512], F32, 2)
    pQ = ph.psn("pQ", [128, 4, 128], BF16, 2)

    kb.op('pool', lambda E: E.memset(halo[:], 0.0), w=[halo])
    it = 0
    for s in range(S // 512):
        xT = xnT[s % 2]
        for t4 in range(4):
            t = s * 4 + t4
            b = it % 2
            it += 1
            kb.dma('sp', xt[b][:], io['x'][t * 128:(t + 1) * 128, :], w=[xt[b]])
            rms_rstd(kb, xt[b][:], junk, ss[b], rs[b], D, [xt[b]])
            kb.op('dve', lambda E: E.tensor_scalar(out=xn[b][:], in0=xt[b][:], scalar1=rs[b][:, 0:1], scalar2=None, op0=ALU.mult),
                  r=[xt[b], rs[b]], w=[xn[b]])
            for c in range(8):
                kb.op('pe', lambda E: E.transpose(out=pT[b][:, c, :], in_=xn[b][:, c * 128:(c + 1) * 128], identity=ident[:]),
                      r=[xn[b], ident], w=[pT[b]])
            kb.op('act', lambda E: E.copy(out=xT[:, :, t4 * 128:(t4 + 1) * 128], in_=pT[b][:]), r=[pT[b]], w=[xT])
        for cb in range(12):
            pf = pF[cb % 2]
            for c in range(8):
                kb.op('pe', lambda E: E.matmul(pf[:], lhsT=winb[:, c, cb * 128:(cb + 1) * 128], rhs=xT[:, c, :],
                                               start=(c == 0), stop=(c == 7)), r=[xT, (winb, c)], w=[pf])
            x3 = xc[cb % 3]
            ac = acc[cb % 3]
            kb.op('act', lambda E: E.copy(out=x3[:, 3:515], in_=pf[:]), r=[pf], w=[(x3, 'b')])
            kb.op('pool', lambda E: E.tensor_copy(out=x3[:, 0:3], in_=halo[:, cb, :]), r=[(halo, cb)], w=[(x3, 'h')])
            kb.op('pool', lambda E: E.tensor_copy(out=halo[:, cb, :], in_=x3[:, 512:515]), r=[(x3, 'b')], w=[(halo, cb)])
            e1 = 'dve' if cb % 2 == 0 else 'pool'
            kb.op(e1, lambda E: E.tensor_scalar(out=ac[:], in0=x3[:, 3:515], scalar1=cw[:, 3, cb:cb + 1], scalar2=None, op0=ALU.mult),
                  r=[(x3, 'b'), cw], w=[ac])
            for j in range(3):
                kb.op('dve', lambda E: E.scalar_tensor_tensor(out=ac[:], in0=x3[:, j:j + 512], scalar=cw[:, j, cb:cb + 1], in1=ac[:],
                                                           op0=ALU.mult, op1=ALU.add), r=[(x3, 'b'), (x3, 'h'), cw, ac], w=[ac])
            kb.op('act', lambda E: E.activation(out=qT[:, cb, :], in_=ac[:], func=AF.Silu), r=[ac], w=[(qT, cb)])
        for t4 in range(4):
            t = s * 4 + t4
            qm = qtm[t % 2]
            for g3 in range(3):
                pq = pQ[g3 % 2]
                for j in range(4):
                    cb = g3 * 4 + j
                    kb.op('pe', lambda E: E.transpose(out=pq[:, j, :], in_=qT[:, cb, t4 * 128:(t4 + 1) * 128], identity=ident[:]),
                          r=[(qT, cb), ident], w=[pq])
                e1 = 'dve' if g3 % 2 == 0 else 'act'
                if e1 == 'dve':
                    kb.op('dve', lambda E: E.tensor_copy(out=qm[:, g3 * 512:(g3 + 1) * 512], in_=pq[:].rearrange("p a b -> p (a b)")),
                          r=[pq], w=[(qm, g3)])
                else:
                    kb.op('act', lambda E: E.copy(out=qm[:, g3 * 512:(g3 + 1) * 512], in_=pq[:].rearrange("p a b -> p (a b)")),
                          r=[pq], w=[(qm, g3)])
            kb.dma('sp', io['qkv_tm'][t * 128:(t + 1) * 128, :], qm[:], r=[(qm, 0), (qm, 1), (qm, 2)], w=['qkv_tm'])
            tm = tmt[t % 2]
            for ci, n0 in enumerate(range(0, TMW, 512)):
                n1 = min(TMW, n0 + 512)
                pm = pM[ci % 2]
                for c in range(8):
                    kb.op('pe', lambda E: E.matmul(pm[:, 0:n1 - n0], lhsT=xT[:, c, t4 * 128:(t4 + 1) * 128],
                                                   rhs=winb[:, c, 1536 + n0:1536 + n1], start=(c == 0), stop=(c == 7)),
                          r=[xT, (winb, c)], w=[pm])
                if ci % 2 == 0:
                    kb.op('dve', lambda E: E.tensor_copy(out=tm[:, n0:n1], in_=pm[:, 0:n1 - n0]), r=[pm], w=[(tm, ci)])
                else:
                    kb.op('act', lambda E: E.copy(out=tm[:, n0:n1], in_=pm[:, 0:n1 - n0]), r=[pm], w=[(tm, ci)])
            kb.dma('sp', io['tm'][t * 128:(t + 1) * 128, :], tm[:], r=[(tm, i) for i in range(5)], w=['tm'])
    ph.close()


def attn_block(kb, o_ps, PTbuf, pti, sc_ps, kt_specs, ident, rkeys_q):
    pts = []
    for i, sp in enumerate(kt_specs):
        pss = sc_ps[pti[0] % len(sc_ps)]
        ptb = PTbuf[pti[0] % len(PTbuf)]
        pti[0] += 1
        pts.append(ptb)
        q0, q1 = sp['qt0'], sp['qt1']
        ncol = (q1 - q0) * 128
        nk = sp['nk']
        nm = len(sp['masks'])
        kb.op('pe', lambda E: E.matmul(pss[0:nk, 0:ncol], lhsT=sp['lhsT'], rhs=sp['rhs_fn'](q0, q1), start=True, stop=(nm == 0)),
              r=sp['rk'] + rkeys_q, w=[pss])
        for mi, (c0, nc_, mask) in enumerate(sp['masks']):
            kb.op('pe', lambda E: E.matmul(pss[0:nk, c0:c0 + nc_], lhsT=ident[0:nk, 0:nk], rhs=mask, start=False, stop=(mi == nm - 1)),
                  r=[ident] + sp.get('mk', []), w=[pss])
        kb.op('act', lambda E: E.activation(out=ptb[0:nk, 0:ncol], in_=pss[0:nk, 0:ncol], func=AF.Exp), r=[pss], w=[ptb])
    qts = sorted(set(q for sp in kt_specs for q in range(sp['qt0'], sp['qt1'])))
    for qt in qts:
        lst = [(sp, ptb) for sp, ptb in zip(kt_specs, pts) if sp['qt0'] <= qt < sp['qt1']]
        oap, okey = o_ps(qt)
        for j, (sp, ptb) in enumerate(lst):
            c0 = (qt - sp['qt0']) * 128
            nk = sp['nk']
            kb.op('pe', lambda E: E.matmul(oap, lhsT=ptb[0:nk, c0:c0 + 128], rhs=sp['v_fn'](qt), start=(j == 0), stop=(j == len(lst) - 1)),
                  r=[ptb] + sp['rv'], w=[okey])


def phase_mem(kb, io):
    nc = kb.nc
    ph = Phase(kb, "mm")
    ident = ph.sb("ident", [128, 128], BF16)
    kb.dma('sp', ident[:], io['c_ident'][:, :], w=[ident])
    wkv = ph.sb("wkv", [128, 8, 1024], BF16)
    gam = ph.sb("gam", [128, 8], F32)
    kb.dma('sp', gam[:], io['mem_norm_w'].rearrange("(c p) -> p c", p=128), w=[gam], allow_slow_non_contiguous=True)
    load_cast_weight(kb, ph, wkv, io['mem_w_kv'], 8, 1024, gam=gam)
    qnw = ph.sb("qnw", [128, 128], F32)
    knw = ph.sb("knw", [128, 128], F32)
    kb.dma('sp', qnw[:], io['mem_q_norm_w'].partition_broadcast(128), w=[qnw])
    kb.dma('sp', knw[:], io['mem_k_norm_w'].partition_broadcast(128), w=[knw])
    kb.op('dve', lambda E: E.tensor_scalar(out=qnw[:], in0=qnw[:], scalar1=128 ** -0.5, scalar2=None, op0=ALU.mult), r=[qnw], w=[qnw])

    mt = ph.sbn("mt", [128, D], F32, 2)
    junk = ph.sb("junk", [128, D], BF16)
    ss = ph.sb("ss", [128, 1], F32)
    rs = ph.sb("rs", [128, 1], F32)
    mn = ph.sb("mn", [128, D], BF16)
    mnT = ph.sb("mnT", [128, 8, 128], BF16)
    kvt = ph.sb("kvt", [128, 1024], F32)
    sq4 = ph.sb("sq4", [128, 4, 128], F32)
    ss4 = ph.sb("ss4", [128, 4], F32)
    rs4 = ph.sb("rs4", [128, 4], F32)
    kn = ph.sb("kn", [128, 4, 128], BF16)
    kT = ph.sb("kT", [128, 4, 256], BF16)
    v1 = ph.sb("v1", [128, 2, 4, 129], BF16)
    pT = ph.ps("pT", [128, 8, 128], BF16)
    pK = ph.psn("pK", [128, 512], F32, 2)
    kb.op('pool', lambda E: E.memset(v1[:], 1.0), w=[v1])
    for mt_i in range(2):
        m = mt[mt_i]
        kb.dma('sp', m[:], io['mem'][mt_i * 128:(mt_i + 1) * 128, :], w=[m])
        rms_rstd(kb, m[:], junk, ss, rs, D, [m])
        kb.op('dve', lambda E: E.tensor_scalar(out=mn[:], in0=m[:], scalar1=rs[:, 0:1], scalar2=None, op0=ALU.mult), r=[m, rs], w=[mn])
        for c in range(8):
            kb.op('pe', lambda E: E.transpose(out=pT[:, c, :], in_=mn[:, c * 128:(c + 1) * 128], identity=ident[:]), r=[mn, ident], w=[pT])
        kb.op('act', lambda E: E.copy(out=mnT[:], in_=pT[:]), r=[pT], w=[mnT])
        for half in range(2):
            pk = pK[half]
            for c in range(8):
                kb.op('pe', lambda E: E.matmul(pk[:], lhsT=mnT[:, c, :], rhs=wkv[:, c, half * 512:(half + 1) * 512],
                                               start=(c == 0), stop=(c == 7)), r=[mnT, (wkv, c)], w=[pk])
            kb.op('act', lambda E: E.copy(out=kvt[:, half * 512:(half + 1) * 512], in_=pk[:]), r=[pk], w=[(kvt, half)])
        k3 = kvt[:, 0:512].rearrange("p (h d) -> p h d", h=4)
        kb.op('dve', lambda E: E.tensor_tensor(out=sq4[:], in0=k3, in1=k3, op=ALU.mult), r=[(kvt, 0)], w=[sq4])
        kb.op('dve', lambda E: E.tensor_reduce(out=ss4[:], in_=sq4[:], axis=AX.X, op=ALU.add), r=[sq4], w=[ss4])
        kb.op('act', lambda E: E.activation(out=rs4[:], in_=ss4[:], func=AF.Sqrt, scale=1.0 / 128, bias=EPS), r=[ss4], w=[rs4])
        kb.op('dve', lambda E: E.reciprocal(out=rs4[:], in_=rs4[:]), r=[rs4], w=[rs4])
        kb.op('dve', lambda E: E.tensor_tensor(out=sq4[:], in0=k3, in1=rs4[:].unsqueeze(2).to_broadcast([128, 4, 128]), op=ALU.mult),
              r=[(kvt, 0), rs4], w=[sq4])
        kb.op('dve', lambda E: E.tensor_tensor(out=kn[:], in0=sq4[:], in1=knw[:].unsqueeze(1).to_broadcast([128, 4, 128]), op=ALU.mult),
              r=[sq4, knw], w=[kn])
        for h in range(4):
            kb.op('pe', lambda E: E.transpose(out=pT[:, h, :], in_=kn[:, h, :], identity=ident[:]), r=[kn, ident], w=[pT])
        kb.op('act', lambda E: E.copy(out=kT[:, :, mt_i * 128:(mt_i + 1) * 128], in_=pT[:, 0:4, :]), r=[pT], w=[kT])
        kb.op('dve', lambda E: E.tensor_copy(out=v1[:, mt_i, :, 0:128], in_=kvt[:, 512:1024].rearrange("p (h d) -> p h d", h=4)),
              r=[(kvt, 1)], w=[v1])

    qt_ = ph.sbn("qt", [128, 512], F32, 2)
    qs = ph.sb("qs", [128, 4, 128], F32)
    qn = ph.sbn("qn", [128, 4, 128], BF16, 2)
    qT = ph.sbn("qT", [128, 4, 512], BF16, 2)
    PTb = ph.sbn("PT", [128, 512], BF16, 4)
    pti = [0]
    oc = ph.sbn("oc", [128, 4, 128], BF16, 4)
    rinv = ph.sb("rinv", [128, 1], F32)
    ocT = ph.sbn("ocT", [128, 4, 128], BF16, 2)
    sc_ps = ph.psn("sc", [128, 512], F32, 2)
    o_psA = ph.ps("oA", [128, 2, 256], F32)
    o_psB = ph.ps("oB", [128, 2, 256], F32)
    pO = ph.ps("pO", [128, 4, 128], BF16)
    for s in range(S // 512):
        qTs = qT[s % 2]
        for t4 in range(4):
            t = s * 4 + t4
            q = qt_[t % 2]
            qb = qn[t % 2]
            kb.dma('sp', q[:], io['tm'][t * 128:(t + 1) * 128, MQ_OFF:MQ_OFF + 512], r=['tm'], w=[q])
            q3 = q[:].rearrange("p (h d) -> p h d", h=4)
            kb.op('pool', lambda E: E.tensor_tensor(out=qs[:], in0=q3, in1=q3, op=ALU.mult), r=[q], w=[qs])
            kb.op('dve', lambda E: E.tensor_reduce(out=ss4[:], in_=qs[:], axis=AX.X, op=ALU.add), r=[qs], w=[ss4])
            kb.op('act', lambda E: E.activation(out=rs4[:], in_=ss4[:], func=AF.Sqrt, scale=1.0 / 128, bias=EPS), r=[ss4], w=[rs4])
            kb.op('dve', lambda E: E.reciprocal(out=rs4[:], in_=rs4[:]), r=[rs4], w=[rs4])
            kb.op('dve', lambda E: E.tensor_tensor(out=qs[:], in0=q3, in1=rs4[:].unsqueeze(2).to_broadcast([128, 4, 128]), op=ALU.mult),
                  r=[q, rs4], w=[qs])
            kb.op('pool', lambda E: E.tensor_tensor(out=qb[:], in0=qs[:], in1=qnw[:].unsqueeze(1).to_broadcast([128, 4, 128]), op=ALU.mult),
                  r=[qs, qnw], w=[qb])
            for h in range(4):
                kb.op('pe', lambda E: E.transpose(out=pT[:, h, :], in_=qb[:, h, :], identity=ident[:]), r=[qb, ident], w=[pT])
            kb.op('act', lambda E: E.copy(out=qTs[:, :, t4 * 128:(t4 + 1) * 128], in_=pT[:, 0:4, :]), r=[pT], w=[qTs])
        for h in range(4):
            def o_ps(qt, h=h):
                return (o_psA[:, qt, 0:129], o_psA) if qt < 2 else (o_psB[:, qt - 2, 0:129], o_psB)
            specs = []
            for kt in range(2):
                specs.append(dict(lhsT=kT[:, h, kt * 128:(kt + 1) * 128], rhs_fn=lambda q0, q1, h=h: qTs[:, h, q0 * 128:q1 * 128],
                                  qt0=0, qt1=4, masks=[], v_fn=lambda qt, kt=kt, h=h: v1[:, kt, h, :], nk=128, rk=[kT], rv=[v1]))
            attn_block(kb, o_ps, PTb, pti, sc_ps, specs, ident, [qTs])
            for t4 in range(4):
                t = s * 4 + t4
                ob = oc[t4]
                oap, okey = o_ps(t4)
                kb.op('dve', lambda E: E.reciprocal(out=rinv[:], in_=oap[:, 128:129]), r=[okey], w=[rinv])
                kb.op('dve', lambda E: E.tensor_scalar(out=ob[:, h, :], in0=oap[:, 0:128], scalar1=rinv[:, 0:1], scalar2=None, op0=ALU.mult),
                      r=[okey, rinv], w=[(ob, h)])
        for t4 in range(4):
            t = s * 4 + t4
            ob = oc[t4]
            oT = ocT[t % 2]
            for h in range(4):
                kb.op('pe', lambda E: E.transpose(out=pO[:, h, :], in_=ob[:, h, :], identity=ident[:]), r=[(ob, h), ident], w=[pO])
            kb.op('act', lambda E: E.copy(out=oT[:], in_=pO[:]), r=[pO], w=[oT])
            kb.dma('sp', io['ocatT'][1024:1536, t * 128:(t + 1) * 128].rearrange("(c p) t -> p c t", p=128), oT[:], r=[oT], w=['ocatT_c'])
    ph.close()


def phase_gdn(kb, io):
    nc = kb.nc
    ph = Phase(kb, "gd")
    ident = ph.sb("ident", [128, 128], BF16)
    btri = ph.sb("btri", [128, 128], F32)
    bones = ph.sb("bones", [128, 128], F32)
    ones = ph.sb("ones", [128, 128], F32)
    mlow = ph.sb("mlow", [128, 4, 128], F32)
    mup = ph.sb("mup", [128, 4, 128], F32)
    strict = ph.sb("strict", [128, 128], F32)
    mch = ph.sb("mch", [128, 2], F32)
    kb.dma('sp', ident[:], io['c_ident'][:, :], w=[ident])
    kb.dma('sp', btri[:], io['c_btri'][:, :], w=[btri])
    kb.dma('sp', bones[:], io['c_bones'][:, :], w=[bones])
    kb.dma('sp', strict[:], io['c_strict'][:, :], w=[strict])
    kb.dma('sp', mch[:], io['c_mch'][:, :], w=[mch])
    for h in range(4):
        kb.dma('sp', mlow[:, h, :], io['c_mlow'][:, :], w=[mlow])
        kb.dma('sp', mup[:, h, :], io['c_mup'][:, :], w=[mup])
    kb.op('pool', lambda E: E.memset(ones[:], 1.0), w=[ones])
    dtb = ph.sb("dtb", [128, 4], F32)
    nA = ph.sb("nA", [128, 4], F32)
    gw = ph.sb("gw", [128, 128], F32)
    kb.dma('sp', dtb[:], io['gdn_dt_bias'].partition_broadcast(128), w=[dtb])
    kb.dma('sp', nA[:], io['gdn_a_log'].partition_broadcast(128), w=[nA])
    kb.dma('sp', gw[:], io['gdn_out_norm_w'].partition_broadcast(128), w=[gw])
    kb.op('act', lambda E: E.activation(out=nA[:], in_=nA[:], func=AF.Exp), r=[nA], w=[nA])
    kb.op('dve', lambda E: E.tensor_scalar(out=nA[:], in0=nA[:], scalar1=-1.0, scalar2=None, op0=ALU.mult), r=[nA], w=[nA])

    def B4(t_, n=4):
        return t_.unsqueeze(2).to_broadcast([128, n, 128])

    def M4(t_):
        return t_.unsqueeze(1).to_broadcast([128, 4, 128])

    qkv = ph.sbn("qkv", [128, 3, 4, 128], BF16, 2)
    ab = ph.sbn("ab", [128, 8], F32, 2)
    gt = ph.sbn("gt", [128, 512], F32, 2)
    sm = ph.sbn("sm", [128, 64], F32, 2)
    gs = ph.sbn("gs", [128, 16], F32, 2)
    gm = ph.sb("gm", [128, 8], F32)
    R1 = ph.sb("R1", [128, 4, 128], F32)
    R2 = ph.sb("R2", [128, 4, 128], F32)
    tmpA = ph.sb("tmpA", [128, 4, 128], F32)
    tmpB = ph.sb("tmpB", [128, 4, 128], F32)
    dec = ph.sb("dec", [128, 4, 128], F32)
    decT = ph.sb("decT", [128, 4, 128], F32)
    sq = ph.sb("sq", [128, 4, 128], F32)
    KBG = ph.sb("KBG", [128, 4, 128], BF16)
    Kd = ph.sbn("Kd", [128, 4, 128], BF16, 2)
    VB = ph.sb("VB", [128, 4, 128], BF16)
    dg = ph.sbn("dg", [128, 4, 128], BF16, 4)
    QT = ph.sb("QT", [128, 4, 128], BF16)
    QG = ph.sbn("QG", [128, 4, 128], BF16, 2)
    KT = ph.sb("KT", [128, 4, 128], BF16)
    nbs = ph.sb("nbs", [128, 4, 128], F32)
    Xb = ph.sbn("X", [128, 4, 128], BF16, 2)
    Yb = ph.sbn("Y", [128, 4, 128], BF16, 2)
    Pb = ph.sbn("P", [128, 4, 128], BF16, 2)
    aqkT = ph.sb("aqkT", [128, 4, 128], BF16)
    negWT = ph.sb("negWT", [128, 4, 128], BF16)
    vnew = ph.sb("vnew", [128, 4, 128], BF16)
    Sf = ph.sb("Sf", [128, 4, 128], F32)
    Sbf = ph.sbn("Sbf", [128, 4, 128], BF16, 3)
    osb = ph.sb("osb", [128, 4, 128], F32)
    sgt = ph.sb("sgt", [128, 512], F32)
    oa = ph.sbn("oa", [128, 4, 128], BF16, 2)
    oaT = ph.sbn("oaT", [128, 4, 128], BF16, 2)
    pS = ph.ps("pS", [128, 512], F32)
    pD = ph.ps("pD", [128, 4, 128], F32)
    pA = ph.psn("pA", [128, 4, 128], F32, 2)
    pB = ph.psn("pB", [128, 8, 128], BF16, 1)
    pV = ph.ps("pV", [128, 4, 128], F32)
    pdS = ph.ps("pdS", [128, 4, 128], F32)
    pO = ph.ps("pO", [128, 4, 128], F32)
    kb.op('pool', lambda E: E.memset(Sf[:], 0.0), w=[Sf])
    kb.op('pool', lambda E: E.memset(Sbf[0][:], 0.0), w=[Sbf[0]])
    kb.op('pool', lambda E: E.memset(vnew[:], 0.0), w=[vnew])
    pai = [0]

    def PA():
        pai[0] += 1
        return pA[pai[0] % 2]

    def mm4(p, lf, rf, rk):
        for h in range(4):
            kb.op('pe', lambda E: E.matmul(p[:, h, :], lhsT=lf(h), rhs=rf(h), start=True, stop=True), r=rk, w=[p])

    si = 0
    for t in range(NT):
        b = t % 2
        x_ = qkv[b]
        s_ = sm[b]
        kb.dma('sp', x_[:].rearrange("p a h d -> p (a h d)"), io['qkv_tm'][t * 128:(t + 1) * 128, :], r=['qkv_tm'], w=[x_])
        kb.dma('sp', ab[b][:], io['tm'][t * 128:(t + 1) * 128, 0:8], r=['tm'], w=[ab[b]])
        kb.dma('sp', gt[b][:], io['tm'][t * 128:(t + 1) * 128, GATE_OFF:GATE_OFF + 512], r=['tm'], w=[gt[b]])
        g = s_[:, 0:4]
        kb.op('dve', lambda E: E.tensor_tensor(out=g, in0=ab[b][:, 0:4], in1=dtb[:], op=ALU.add), r=[ab[b], dtb], w=[s_])
        kb.op('act', lambda E: E.activation(out=g, in_=g, func=AF.Exp), r=[s_], w=[s_])
        kb.op('act', lambda E: E.activation(out=g, in_=g, func=AF.Ln, bias=1.0), r=[s_], w=[s_])
        kb.op('dve', lambda E: E.tensor_tensor(out=g, in0=g, in1=nA[:], op=ALU.mult), r=[s_, nA], w=[s_])
        kb.op('dve', lambda E: E.tensor_scalar(out=s_[:, 4:8], in0=g, scalar1=-1.0, scalar2=None, op0=ALU.mult), r=[s_], w=[s_])
        kb.op('act', lambda E: E.activation(out=s_[:, 8:12], in_=ab[b][:, 4:8], func=AF.Exp, scale=-1.0), r=[ab[b], s_], w=[s_])
        kb.op('dve', lambda E: E.tensor_scalar(out=s_[:, 8:12], in0=s_[:, 8:12], scalar1=1.0, scalar2=None, op0=ALU.add), r=[s_], w=[s_])
        kb.op('dve', lambda E: E.reciprocal(out=s_[:, 8:12], in_=s_[:, 8:12]), r=[s_], w=[s_])
        kb.op('dve', lambda E: E.tensor_scalar(out=s_[:, 12:16], in0=s_[:, 8:12], scalar1=-1.0, scalar2=None, op0=ALU.mult), r=[s_], w=[s_])
        for j in range(2):
            kb.op('dve', lambda E: E.tensor_scalar(out=gm[:, 4 * j:4 * j + 4], in0=g, scalar1=mch[:, j:j + 1], scalar2=None, op0=ALU.mult),
                  r=[s_, mch], w=[gm])
        kb.op('pe', lambda E: E.matmul(pS[:, 0:4], lhsT=btri[:], rhs=g, start=True, stop=True), r=[btri, s_], w=[pS])
        kb.op('pe', lambda E: E.matmul(pS[:, 4:8], lhsT=bones[:], rhs=g, start=True, stop=True), r=[bones, s_], w=[pS])
        kb.op('pe', lambda E: E.matmul(pS[:, 8:16], lhsT=ones[:], rhs=gm[:], start=True, stop=True), r=[ones, gm], w=[pS])
        G = gs[b]
        kb.op('dve', lambda E: E.tensor_copy(out=G[:], in_=pS[:, 0:16]), r=[pS], w=[G])
        kb.op('act', lambda E: E.activation(out=s_[:, 16:20], in_=G[:, 0:4], func=AF.Exp), r=[G, s_], w=[s_])
        kb.op('dve', lambda E: E.tensor_tensor(out=G[:, 4:8], in0=G[:, 4:8], in1=G[:, 0:4], op=ALU.subtract), r=[G], w=[G])
        kb.op('act', lambda E: E.activation(out=s_[:, 20:24], in_=G[:, 4:8], func=AF.Exp), r=[G, s_], w=[s_])
        kb.op('act', lambda E: E.activation(out=s_[:, 56:64], in_=G[:, 8:16], func=AF.Exp), r=[G, s_], w=[s_])
        kb.op('pool', lambda E: E.tensor_copy(out=R1[:], in_=B4(g)), r=[s_], w=[R1])
        kb.op('pool', lambda E: E.tensor_tensor(out=R2[:], in0=M4(btri[:]), in1=B4(s_[:, 4:8]), op=ALU.mult), r=[s_, btri], w=[R2])
        kb.op('pe', lambda E: E.matmul(pD[:].rearrange("p a b -> p (a b)"), lhsT=btri[:], rhs=R1[:].rearrange("p a b -> p (a b)"),
                                       start=True, stop=False), r=[btri, R1], w=[pD])
        kb.op('pe', lambda E: E.matmul(pD[:].rearrange("p a b -> p (a b)"), lhsT=bones[:], rhs=R2[:].rearrange("p a b -> p (a b)"),
                                       start=False, stop=True), r=[bones, R2], w=[pD])
        kb.op('dve', lambda E: E.tensor_tensor(out=tmpA[:], in0=pD[:], in1=mlow[:], op=ALU.add), r=[pD, mlow], w=[tmpA])
        kb.op('act', lambda E: E.activation(out=dec[:], in_=tmpA[:], func=AF.Exp), r=[tmpA], w=[dec])
        kb.op('dve', lambda E: E.scalar_tensor_tensor(out=tmpB[:], in0=pD[:], scalar=-1.0, in1=mup[:], op0=ALU.mult, op1=ALU.add),
              r=[pD, mup], w=[tmpB])
        kb.op('act', lambda E: E.activation(out=decT[:], in_=tmpB[:], func=AF.Exp), r=[tmpB], w=[decT])
        for a_, c0 in ((0, 24), (1, 28)):
            kb.op('pool', lambda E: E.tensor_tensor(out=sq[:], in0=x_[:, a_, :, :], in1=x_[:, a_, :, :], op=ALU.mult), r=[x_], w=[sq])
            kb.op('dve', lambda E: E.tensor_reduce(out=s_[:, c0:c0 + 4], in_=sq[:], axis=AX.X, op=ALU.add), r=[sq, s_], w=[s_])
            kb.op('act', lambda E: E.activation(out=s_[:, c0:c0 + 4], in_=s_[:, c0:c0 + 4], func=AF.Sqrt, bias=EPS), r=[s_], w=[s_])
            kb.op('dve', lambda E: E.reciprocal(out=s_[:, c0:c0 + 4], in_=s_[:, c0:c0 + 4]), r=[s_], w=[s_])
        sc_ = lambda o, a, bb: kb.op('dve', lambda E: E.tensor_tensor(out=s_[:, o:o + 4], in0=a, in1=bb, op=ALU.mult), r=[s_, mch], w=[s_])
        kb.op('dve', lambda E: E.tensor_scalar(out=s_[:, 32:36], in0=s_[:, 24:28], scalar1=128 ** -0.5, scalar2=None, op0=ALU.mult), r=[s_], w=[s_])
        sc_(36, s_[:, 32:36], s_[:, 16:20])
        kb.op('dve', lambda E: E.tensor_scalar(out=s_[:, 40:44], in0=s_[:, 36:40], scalar1=mch[:, 1:2], scalar2=None, op0=ALU.mult), r=[s_, mch], w=[s_])
        kb.op('dve', lambda E: E.tensor_scalar(out=s_[:, 36:40], in0=s_[:, 36:40], scalar1=mch[:, 0:1], scalar2=None, op0=ALU.mult), r=[s_, mch], w=[s_])
        sc_(44, s_[:, 28:32], s_[:, 8:12])
        sc_(44, s_[:, 44:48], s_[:, 16:20])
        sc_(48, s_[:, 28:32], s_[:, 20:24])
        kb.op('dve', lambda E: E.tensor_scalar(out=s_[:, 52:56], in0=s_[:, 48:52], scalar1=mch[:, 1:2], scalar2=None, op0=ALU.mult), r=[s_, mch], w=[s_])
        kb.op('dve', lambda E: E.tensor_scalar(out=s_[:, 48:52], in0=s_[:, 48:52], scalar1=mch[:, 0:1], scalar2=None, op0=ALU.mult), r=[s_, mch], w=[s_])
        kx, vx, qx = x_[:, 1, :, :], x_[:, 2, :, :], x_[:, 0, :, :]
        kb.op('pool', lambda E: E.tensor_tensor(out=KBG[:], in0=kx, in1=B4(s_[:, 44:48]), op=ALU.mult), r=[x_, s_], w=[KBG])
        kb.op('pool', lambda E: E.tensor_tensor(out=Kd[0][:], in0=kx, in1=B4(s_[:, 48:52]), op=ALU.mult), r=[x_, s_], w=[Kd[0]])
        kb.op('pool', lambda E: E.tensor_tensor(out=Kd[1][:], in0=kx, in1=B4(s_[:, 52:56]), op=ALU.mult), r=[x_, s_], w=[Kd[1]])
        kb.op('pool', lambda E: E.tensor_tensor(out=VB[:], in0=vx, in1=B4(s_[:, 8:12]), op=ALU.mult), r=[x_, s_], w=[VB])
        for i_, c0 in enumerate((32, 36, 40, 28)):
            kb.op('dve', lambda E: E.tensor_tensor(out=dg[i_][:], in0=M4(ident[:]), in1=B4(s_[:, c0:c0 + 4]), op=ALU.mult), r=[ident, s_], w=[dg[i_]])
        for i_, (src, dst) in enumerate(((qx, QT), (qx, QG[0]), (qx, QG[1]), (kx, KT))):
            p = PA()
            mm4(p, lambda h: src[:, h, :], lambda h: dg[i_][:, h, :], [x_, dg[i_]])
            if i_ % 2 == 0:
                kb.op('act', lambda E: E.copy(out=dst[:], in_=p[:]), r=[p], w=[dst])
            else:
                kb.op('dve', lambda E: E.tensor_copy(out=dst[:], in_=p[:]), r=[p], w=[dst])
        p = PA()
        mm4(p, lambda h: KT[:, h, :], lambda h: KT[:, h, :], [KT])
        kb.op('dve', lambda E: E.tensor_tensor(out=tmpA[:], in0=p[:], in1=dec[:], op=ALU.mult), r=[p, dec], w=[tmpA])
        kb.op('pool', lambda E: E.tensor_tensor(out=nbs[:], in0=M4(strict[:]), in1=B4(s_[:, 12:16]), op=ALU.mult), r=[strict, s_], w=[nbs])
        X, Y, P = Xb[0], Yb[0], Pb[0]
        kb.op('pool', lambda E: E.tensor_tensor(out=X[:], in0=tmpA[:], in1=nbs[:], op=ALU.mult), r=[tmpA, nbs], w=[X])
        for h in range(4):
            kb.op('pe', lambda E: E.transpose(out=pB[0][:, h, :], in_=X[:, h, :], identity=ident[:]), r=[X, ident], w=[pB[0]])
        kb.op('act', lambda E: E.copy(out=Y[:], in_=pB[0][:, 0:4, :]), r=[pB[0]], w=[Y])
        kb.op('dve', lambda E: E.tensor_tensor(out=P[:], in0=pB[0][:, 0:4, :], in1=M4(ident[:]), op=ALU.add), r=[pB[0], ident], w=[P])
        p = PA()
        mm4(p, lambda h: KT[:, h, :], lambda h: QT[:, h, :], [KT, QT])
        kb.op('dve', lambda E: E.tensor_tensor(out=aqkT[:], in0=p[:], in1=decT[:], op=ALU.mult), r=[p, decT], w=[aqkT])
        for k_ in range(1, 6):
            Xn, Yn, Pn = Xb[k_ % 2], Yb[k_ % 2], Pb[k_ % 2]
            p = PA()
            mm4(p, lambda h: Y[:, h, :], lambda h: X[:, h, :], [X, Y])
            kb.op('act', lambda E: E.copy(out=Xn[:], in_=p[:]), r=[p], w=[Xn])
            if k_ < 5:
                p2 = PA()
                mm4(p2, lambda h: X[:, h, :], lambda h: Y[:, h, :], [X, Y])
                kb.op('dve', lambda E: E.tensor_copy(out=Yn[:], in_=p2[:]), r=[p2], w=[Yn])
            p3 = PA()
            mm4(p3, lambda h: Xn[:, h, :], lambda h: P[:, h, :], [Xn, P])
            kb.op('dve', lambda E: E.tensor_tensor(out=Pn[:], in0=p3[:], in1=P[:], op=ALU.add), r=[p3, P], w=[Pn])
            X, Y, P = Xn, Yn, Pn
        p = PA()
        mm4(p, lambda h: KBG[:, h, :], lambda h: P[:, h, :], [KBG, P])
        kb.op('act', lambda E: E.mul(out=negWT[:], in_=p[:], mul=-1.0), r=[p], w=[negWT])
        Sa = Sbf[si % 3]
        Sb_ = Sbf[(si + 1) % 3]
        Sc = Sbf[(si + 2) % 3]
        si += 2
        for j, (Scur, Snext) in enumerate(((Sa, Sb_), (Sb_, Sc))):
            for h in range(4):
                kb.op('pe', lambda E: E.matmul(pV[:, h, :], lhsT=P[:, h, :], rhs=VB[:, h, :], start=True, stop=False), r=[P, VB], w=[pV])
                kb.op('pe', lambda E: E.matmul(pV[:, h, :], lhsT=negWT[:, h, :], rhs=Scur[:, h, :], start=False, stop=True),
                      r=[negWT, Scur], w=[pV])
            r0 = 64 * j
            kb.op('act', lambda E: E.copy(out=vnew[r0:r0 + 64, :, :], in_=pV[r0:r0 + 64, :, :]), r=[pV], w=[vnew])
            mm4(pdS, lambda h: Kd[j][:, h, :], lambda h: vnew[:, h, :], [Kd[j], vnew])
            for h in range(4):
                kb.op('dve', lambda E: E.scalar_tensor_tensor(out=Sf[:, h, :], in0=Sf[:, h, :], scalar=s_[:, 56 + 4 * j + h:57 + 4 * j + h],
                                                              in1=pdS[:, h, :], op0=ALU.mult, op1=ALU.add), r=[Sf, s_, pdS], w=[Sf])
            kb.op('act', lambda E: E.copy(out=Snext[:], in_=Sf[:]), r=[Sf], w=[Snext])
        for h in range(4):
            kb.op('pe', lambda E: E.matmul(pO[:, h, :], lhsT=QG[0][:, h, :], rhs=Sa[:, h, :], start=True, stop=False), r=[QG[0], Sa], w=[pO])
            kb.op('pe', lambda E: E.matmul(pO[:, h, :], lhsT=QG[1][:, h, :], rhs=Sb_[:, h, :], start=False, stop=False), r=[QG[1], Sb_], w=[pO])
            kb.op('pe', lambda E: E.matmul(pO[:, h, :], lhsT=aqkT[:, h, :], rhs=vnew[:, h, :], start=False, stop=True), r=[aqkT, vnew], w=[pO])
        kb.op('act', lambda E: E.copy(out=osb[:], in_=pO[:]), r=[pO], w=[osb])
        kb.op('pool', lambda E: E.tensor_tensor(out=sq[:], in0=osb[:], in1=osb[:], op=ALU.mult), r=[osb], w=[sq])
        kb.op('dve', lambda E: E.tensor_reduce(out=G[:, 0:4], in_=sq[:], axis=AX.X, op=ALU.add), r=[sq, G], w=[G])
        kb.op('act', lambda E: E.activation(out=G[:, 0:4], in_=G[:, 0:4], func=AF.Sqrt, scale=1.0 / 128, bias=EPS), r=[G], w=[G])
        kb.op('dve', lambda E: E.reciprocal(out=G[:, 0:4], in_=G[:, 0:4]), r=[G], w=[G])
        kb.op('act', lambda E: E.activation(out=sgt[:], in_=gt[b][:], func=AF.Silu), r=[gt[b]], w=[sgt])
        kb.op('dve', lambda E: E.tensor_tensor(out=osb[:], in0=osb[:], in1=B4(G[:, 0:4]), op=ALU.mult), r=[osb, G], w=[osb])
        kb.op('pool', lambda E: E.tensor_tensor(out=osb[:], in0=osb[:], in1=M4(gw[:]), op=ALU.mult), r=[osb, gw], w=[osb])
        kb.op('dve', lambda E: E.tensor_tensor(out=oa[b][:], in0=osb[:], in1=sgt[:].rearrange("p (h d) -> p h d", h=4), op=ALU.mult),
              r=[osb, sgt], w=[oa[b]])
        for h in range(4):
            kb.op('pe', lambda E: E.transpose(out=pB[0][:, h, :], in_=oa[b][:, h, :], identity=ident[:]), r=[oa[b], ident], w=[pB[0]])
        kb.op('act', lambda E: E.copy(out=oaT[b][:], in_=pB[0][:, 0:4, :]), r=[pB[0]], w=[oaT[b]])
        kb.dma('sp', io['ocatT'][0:512, t * 128:(t + 1) * 128].rearrange("(c p) t -> p c t", p=128), oaT[b][:], r=[oaT[b]], w=['ocatT_a'])
    ph.close()


def rope16(kb, R, G, cs, tmp, e1='dve', e2='pool'):
    c = cs[:, 0:8].unsqueeze(1).to_broadcast([128, G, 8])
    sn = cs[:, 8:16].unsqueeze(1).to_broadcast([128, G, 8])
    x1 = R[:, 0:G, 0:8]
    x2 = R[:, 0:G, 8:16]
    kb.op(e1, lambda E: E.tensor_tensor(out=tmp[:, 0:G, 0:8], in0=x1, in1=c, op=ALU.mult), r=[R, cs], w=[(tmp, 0)])
    kb.op(e2, lambda E: E.tensor_tensor(out=tmp[:, 0:G, 8:16], in0=x2, in1=sn, op=ALU.mult), r=[R, cs], w=[(tmp, 1)])
    kb.op(e1, lambda E: E.tensor_tensor(out=tmp[:, 0:G, 16:24], in0=x2, in1=c, op=ALU.mult), r=[R, cs], w=[(tmp, 2)])
    kb.op(e2, lambda E: E.tensor_tensor(out=tmp[:, 0:G, 24:32], in0=x1, in1=sn, op=ALU.mult), r=[R, cs], w=[(tmp, 3)])
    kb.op(e1, lambda E: E.tensor_tensor(out=x1, in0=tmp[:, 0:G, 0:8], in1=tmp[:, 0:G, 8:16], op=ALU.subtract),
          r=[(tmp, 0), (tmp, 1), (tmp, 2), (tmp, 3)], w=[R])
    kb.op(e1, lambda E: E.tensor_tensor(out=x2, in0=tmp[:, 0:G, 16:24], in1=tmp[:, 0:G, 24:32], op=ALU.add),
          r=[(tmp, 0), (tmp, 1), (tmp, 2), (tmp, 3)], w=[R])


def rms_groups(kb, src3, G, dst3, sq, ss, wt, e_sq='pool'):
    (src_ap, src_keys) = src3
    (dst_ap, dst_keys) = dst3
    kb.op(e_sq, lambda E: E.tensor_tensor(out=sq[:, 0:G, :], in0=src_ap, in1=src_ap, op=ALU.mult), r=src_keys, w=[sq])
    kb.op('dve', lambda E: E.tensor_reduce(out=ss[:, 0:G], in_=sq[:, 0:G, :], axis=AX.X, op=ALU.add), r=[sq], w=[ss])
    kb.op('act', lambda E: E.activation(out=ss[:, 0:G], in_=ss[:, 0:G], func=AF.Sqrt, scale=1.0 / 64, bias=EPS), r=[ss], w=[ss])
    kb.op('dve', lambda E: E.reciprocal(out=ss[:, 0:G], in_=ss[:, 0:G]), r=[ss], w=[ss])
    kb.op('dve', lambda E: E.tensor_tensor(out=dst_ap, in0=src_ap, in1=ss[:, 0:G].unsqueeze(2).to_broadcast([128, G, 64]), op=ALU.mult),
          r=src_keys + [ss], w=dst_keys)
    kb.op('pool', lambda E: E.tensor_tensor(out=dst_ap, in0=dst_ap, in1=wt[:].unsqueeze(1).to_broadcast([128, G, 64]), op=ALU.mult),
          r=dst_keys + [wt], w=dst_keys)


def phase_nsa(kb, io):
    nc = kb.nc
    ph = Phase(kb, "ns")
    ident = ph.sb("ident", [128, 128], BF16)
    tril = ph.sb("tril", [128, 128], BF16)
    far = ph.sb("far", [128, 128], BF16)
    cmask = ph.sb("cmask", [128, 9, 512], BF16)
    kvT = ph.sb("kvT", [128, 4, S], BF16)
    vs1 = ph.sb("vs1", [128, NT, 2, 65], BF16)
    vw1 = ph.sb("vw1", [128, NT, 2, 65], BF16)
    kcmpT = ph.sb("kcmpT", [64, 2, 256], BF16)
    rhs_cmp = ph.sb("rhs_cmp", [128, 2, 2, 128], BF16)
    kb.dma('sp', ident[:], io['c_ident'][:, :], w=[ident])
    kb.dma('sp', tril[:], io['c_tril'][:, :], w=[tril])
    kb.dma('sp', far[:], io['c_far'][:, :], w=[far])
    for m in range(9):
        kb.dma('sp', cmask[:, m, :], io['c_cmask'][m, :, :], w=[cmask])
    for g in range(2):
        kb.dma('sp', kvT[64:128, g, :], io['c_E'][:, :], w=[(kvT, 'E')])
    kb.op('pool', lambda E: E.memset(vs1[:], 1.0), w=[vs1])
    kb.op('pool', lambda E: E.memset(vw1[:], 1.0), w=[vw1])
    kb.op('pool', lambda E: E.memset(kcmpT[:], 0.0), w=[kcmpT])
    kb.op('pool', lambda E: E.memset(rhs_cmp[:], 0.0), w=[rhs_cmp])
    for bt in range(2):
        for g in range(2):
            kb.dma('sp', rhs_cmp[:, bt, g, 64:128], io['c_ovl'][:, bt, :], r=[rhs_cmp], w=[rhs_cmp])

    pp = Phase(kb, "np")
    kcT = pp.sb("kcT", [64, 4, S], BF16)
    ksw = pp.sb("ksw", [128, 64], F32)
    kww = pp.sb("kww", [128, 64], F32)
    kcw = pp.sb("kcw", [128, 64], F32)
    kb.dma('sp', ksw[:], io['nsa_ks_norm_w'].partition_broadcast(128), w=[ksw])
    kb.dma('sp', kww[:], io['nsa_kw_norm_w'].partition_broadcast(128), w=[kww])
    kb.dma('sp', kcw[:], io['nsa_kc_norm_w'].partition_broadcast(128), w=[kcw])
    kvb = pp.sbn("kvb", [128, 768], F32, 2)
    cst = pp.sbn("cst", [128, 16], F32, 2)
    R = pp.sbn("R", [128, 6, 64], F32, 2)
    sq = pp.sb("sq", [128, 2, 64], F32)
    ss = pp.sb("ss", [128, 2], F32)
    tmp = pp.sb("tmp", [128, 6, 32], F32)
    k16 = pp.sbn("k16", [128, 8, 64], BF16, 2)
    pT8 = pp.psn("pT8", [128, 8, 128], BF16, 2)
    for t in range(NT):
        b = t % 2
        kv = kvb[b]
        Rb = R[b]
        kb.dma('sp', kv[:], io['tm'][t * 128:(t + 1) * 128, KC_OFF:KC_OFF + 768], r=['tm'], w=[kv])
        kb.dma('sp', cst[b][:], io['c_rope'][t * 128:(t + 1) * 128, :], w=[cst[b]])
        v3 = lambda off: kv[:, off:off + 128].rearrange("p (g d) -> p g d", g=2)
        kb.op('pool', lambda E: E.tensor_copy(out=Rb[:, 0:2, :], in_=v3(0)), r=[kv], w=[Rb])
        rms_groups(kb, (v3(256), [kv]), 2, (Rb[:, 2:4, :], [Rb]), sq, ss, ksw)
        rms_groups(kb, (v3(512), [kv]), 2, (Rb[:, 4:6, :], [Rb]), sq, ss, kww)
        rope16(kb, Rb, 6, cst[b], tmp)
        kk = k16[b]
        kb.op('act', lambda E: E.copy(out=kk[:, 0:6, :], in_=Rb[:]), r=[Rb], w=[kk])
        kb.op('pool', lambda E: E.tensor_copy(out=kk[:, 6:8, :], in_=v3(128)), r=[kv], w=[kk])
        kb.op('dve', lambda E: E.tensor_copy(out=vs1[:, t, :, 0:64], in_=v3(384)), r=[kv], w=[vs1])
        kb.op('pool', lambda E: E.tensor_copy(out=vw1[:, t, :, 0:64], in_=v3(640)), r=[kv], w=[vw1])
        p8 = pT8[b]
        for i in range(8):
            kb.op('pe', lambda E: E.transpose(out=p8[0:64, i, :], in_=kk[:, i, :], identity=ident[:]), r=[kk, ident], w=[p8])
        kb.op('act', lambda E: E.copy(out=kvT[0:64, :, t * 128:(t + 1) * 128], in_=p8[0:64, 2:6, :]), r=[p8], w=[(kvT, 'k')])
        kb.op('dve', lambda E: E.tensor_copy(out=kcT[0:64, 0:2, t * 128:(t + 1) * 128], in_=p8[0:64, 0:2, :]), r=[p8], w=[kcT])
        kb.op('dve', lambda E: E.tensor_copy(out=kcT[0:64, 2:4, t * 128:(t + 1) * 128], in_=p8[0:64, 6:8, :]), r=[p8], w=[kcT])
    w1f = pp.sb("w1f", [64, 32, 64], F32)
    w1b = pp.sbn("w1b", [64, 32, 64], BF16, 2)
    w2f = pp.sb("w2f", [64, 64], F32)
    w2b = pp.sbn("w2b", [64, 64], BF16, 2)
    posf = pp.sb("posf", [64, 32], F32)
    pos2 = pp.sbn("pos2", [64, 32, 2], BF16, 2)
    bias = pp.sb("bias", [64, 2], F32)
    h1T = pp.sb("h1T", [64, 256], BF16)
    o2 = pp.sb("o2", [128, 1, 64], F32)
    o2n = pp.sb("o2n", [128, 1, 64], F32)
    kcn = pp.sb("kcn", [128, 64], BF16)
    pH = pp.ps("pH", [128, 512], F32)
    pB_ = pp.ps("pBi", [128, 512], F32)
    pO2 = pp.ps("pO2", [128, 512], F32)
    kb.op('pool', lambda E: E.memset(h1T[:], 0.0), w=[h1T])
    for kind, (n1, n2, npos) in enumerate((('nsa_cmp_k_w1', 'nsa_cmp_k_w2', 'nsa_cmp_pos_k'), ('nsa_cmp_v_w1', 'nsa_cmp_v_w2', 'nsa_cmp_pos_v'))):
        kb.dma('sp', w1f[:], io[n1].rearrange("(l d) o -> d l o", d=64), w=[w1f])
        kb.op('dve', lambda E: E.tensor_copy(out=w1b[kind][:], in_=w1f[:]), r=[w1f], w=[w1b[kind]])
        kb.dma('sp', w2f[:], io[n2][:, :], w=[w2f])
        kb.op('dve', lambda E: E.tensor_copy(out=w2b[kind][:], in_=w2f[:]), r=[w2f], w=[w2b[kind]])
        kb.dma('sp', posf[:], io[npos].rearrange("l d -> d l"), w=[posf], allow_slow_non_contiguous=True)
        for j in range(2):
            kb.op('dve', lambda E: E.tensor_copy(out=pos2[kind][:, :, j], in_=posf[:]), r=[posf], w=[pos2[kind]])
        for l in range(32):
            kb.op('pe', lambda E: E.matmul(pB_[0:64, 0:2], lhsT=w1b[kind][:, l, :], rhs=pos2[kind][:, l, :], start=(l == 0), stop=(l == 31)),
                  r=[w1b[kind], pos2[kind]], w=[pB_])
        kb.op('dve', lambda E: E.tensor_copy(out=bias[:], in_=pB_[0:64, 0:2]), r=[pB_], w=[bias])
        for g in range(2):
            ki = kind * 2 + g
            for l in range(32):
                kb.op('pe', lambda E: E.matmul(pH[0:64, 0:255], lhsT=w1b[kind][:, l, :], rhs=kcT[0:64, ki, l:l + 16 * 254 + 1:16],
                                               start=(l == 0), stop=(l == 31)), r=[w1b[kind], kcT], w=[pH])
            kb.op('act', lambda E: E.activation(out=h1T[:, 0:255], in_=pH[0:64, 0:255], func=AF.Silu, bias=bias[:, 0:1]),
                  r=[pH, bias], w=[h1T])
            for bt in range(2):
                kb.op('pe', lambda E: E.matmul(pO2[:, 0:64], lhsT=h1T[:, bt * 128:(bt + 1) * 128], rhs=w2b[kind][:], start=True, stop=True),
                      r=[h1T, w2b[kind]], w=[pO2])
                if kind == 0:
                    kb.op('act', lambda E: E.copy(out=o2[:, 0, :], in_=pO2[:, 0:64]), r=[pO2], w=[o2])
                    rms_groups(kb, (o2[:], [o2]), 1, (o2n[:], [o2n]), sq, ss, kcw)
                    kb.op('act', lambda E: E.copy(out=kcn[:], in_=o2n[:, 0, :]), r=[o2n], w=[kcn])
                    p8 = pT8[0]
                    kb.op('pe', lambda E: E.transpose(out=p8[0:64, 0, :], in_=kcn[:], identity=ident[:]), r=[kcn, ident], w=[p8])
                    kb.op('act', lambda E: E.copy(out=kcmpT[:, g, bt * 128:(bt + 1) * 128], in_=p8[0:64, 0, :]), r=[p8], w=[kcmpT])
                else:
                    kb.op('act', lambda E: E.copy(out=rhs_cmp[:, bt, g, 0:64], in_=pO2[:, 0:64]), r=[pO2], w=[rhs_cmp])
    pp.close()

    pa = Phase(kb, "na")
    NPT = 36
    PTb = pa.sbn("PT", [128, 512], BF16, NPT)
    pti = [0]
    qaug = pa.sbn("qaug", [128, 8, 512], BF16, 2)
    qnw = pa.sb("qnw", [128, 64], F32)
    kb.dma('sp', qnw[:], io['nsa_q_norm_w'].partition_broadcast(128), w=[qnw])
    kb.op('dve', lambda E: E.tensor_scalar(out=qnw[:], in0=qnw[:], scalar1=0.125, scalar2=None, op0=ALU.mult), r=[qnw], w=[qnw])
    qf = pa.sbn("qf", [128, 512], F32, 2)
    cst = pa.sbn("cst", [128, 16], F32, 2)
    Rq = pa.sb("Rq", [128, 8, 64], F32)
    sq = pa.sb("sq", [128, 8, 64], F32)
    ss = pa.sb("ss", [128, 8], F32)
    tmp = pa.sb("tmp", [128, 8, 32], F32)
    qa = pa.sbn("qa", [128, 8, 128], BF16, 2)
    gts = pa.sbn("gts", [128, 24], F32, 4)
    Ab = pa.sbn("Ab", [128, 64], F32, 4)
    Bb = pa.sbn("Bb", [128, 64], F32, 4)
    ob = pa.sbn("ob", [128, 8, 64], F32, 4)
    impacc = pa.sb("impacc", [128, 4, 64], F32)
    scr = pa.sb("scr", [128, 64], F32)
    scr2 = pa.sb("scr2", [128, 64], F32)
    m8 = pa.sb("m8", [128, 16], F32)
    nst = pa.sbn("nst", [128, 128], BF16, 2)
    fs = pa.sb("fs", [128, 4], F32)
    obb = pa.sbn("obb", [128, 512], BF16, 2)
    obT = pa.sbn("obT", [128, 4, 128], BF16, 2)
    sc_ps = pa.psn("sc", [128, 512], F32, 2)
    oC = pa.ps("oC", [128, 4, 128], F32)
    oW = pa.ps("oW", [128, 4, 128], F32)
    oS = pa.ps("oS", [128, 4, 128], F32)
    pTr = pa.psn("pTr", [128, 8, 128], BF16, 2)
    for i in range(2):
        kb.op('pool', lambda E: E.memset(qa[i][:], 0.0), w=[qa[i]])
        kb.op('pool', lambda E: E.memset(nst[i][:], 0.0), w=[nst[i]])
    tri = 0

    def finalize(oX, h, br, first, cmp=False):
        for qt in range(4):
            if cmp:
                kb.op('dve', lambda E: E.tensor_reduce(out=fs[:, 0:1], in_=oX[:, qt, 64:128], axis=AX.X, op=ALU.add), r=[oX], w=[fs])
                kb.op('dve', lambda E: E.tensor_scalar(out=fs[:, 0:1], in0=fs[:, 0:1], scalar1=0.5, scalar2=1e-30, op0=ALU.mult, op1=ALU.add),
                      r=[fs], w=[fs])
            else:
                kb.op('dve', lambda E: E.tensor_scalar(out=fs[:, 0:1], in0=oX[:, qt, 64:65], scalar1=1e-30, scalar2=None, op0=ALU.add),
                      r=[oX], w=[fs])
            kb.op('dve', lambda E: E.reciprocal(out=fs[:, 1:2], in_=fs[:, 0:1]), r=[fs], w=[fs])
            if cmp:
                if h % 4 == 0:
                    kb.op('dve', lambda E: E.tensor_scalar(out=impacc[:, qt, :], in0=oX[:, qt, 64:128], scalar1=fs[:, 1:2], scalar2=None,
                                                           op0=ALU.mult), r=[oX, fs], w=[(impacc, qt)])
                else:
                    kb.op('dve', lambda E: E.scalar_tensor_tensor(out=impacc[:, qt, :], in0=oX[:, qt, 64:128], scalar=fs[:, 1:2],
                                                                  in1=impacc[:, qt, :], op0=ALU.mult, op1=ALU.add),
                          r=[oX, fs, (impacc, qt)], w=[(impacc, qt)])
            kb.op('dve', lambda E: E.tensor_tensor(out=fs[:, 2:3], in0=fs[:, 1:2], in1=gts[qt][:, h * 3 + br:h * 3 + br + 1], op=ALU.mult),
                  r=[fs, gts[qt]], w=[fs])
            if first:
                kb.op('dve', lambda E: E.tensor_scalar(out=ob[qt][:, h, :], in0=oX[:, qt, 0:64], scalar1=fs[:, 2:3], scalar2=None, op0=ALU.mult),
                      r=[oX, fs], w=[(ob[qt], h)])
            else:
                kb.op('dve', lambda E: E.scalar_tensor_tensor(out=ob[qt][:, h, :], in0=oX[:, qt, 0:64], scalar=fs[:, 2:3], in1=ob[qt][:, h, :],
                                                              op0=ALU.mult, op1=ALU.add), r=[oX, fs, (ob[qt], h)], w=[(ob[qt], h)])

    for s in range(S // 512):
        qs_ = qaug[s % 2]
        for t4 in range(4):
            t = 4 * s + t4
            b = t % 2
            q = qf[b]
            kb.dma('sp', q[:], io['tm'][t * 128:(t + 1) * 128, NQ_OFF:NQ_OFF + 512], r=['tm'], w=[q])
            kb.dma('sp', gts[t4][:], io['tm'][t * 128:(t + 1) * 128, NG_OFF:NG_OFF + 24], r=['tm'], w=[gts[t4]])
            kb.dma('sp', cst[b][:], io['c_rope'][t * 128:(t + 1) * 128, :], w=[cst[b]])
            kb.dma('sp', Ab[t4][:], io['c_A'][t, :, :], w=[Ab[t4]])
            kb.dma('sp', Bb[t4][:], io['c_B'][t, :, :], w=[Bb[t4]])
            kb.op('act', lambda E: E.activation(out=gts[t4][:], in_=gts[t4][:], func=AF.Exp, scale=-1.0), r=[gts[t4]], w=[gts[t4]])
            kb.op('dve', lambda E: E.tensor_scalar(out=gts[t4][:], in0=gts[t4][:], scalar1=1.0, scalar2=None, op0=ALU.add), r=[gts[t4]], w=[gts[t4]])
            kb.op('dve', lambda E: E.reciprocal(out=gts[t4][:], in_=gts[t4][:]), r=[gts[t4]], w=[gts[t4]])
            q3 = q[:].rearrange("p (h d) -> p h d", h=8)
            rms_groups(kb, (q3, [q]), 8, (Rq[:], [Rq]), sq, ss, qnw)
            rope16(kb, Rq, 8, cst[b], tmp)
            qab = qa[b]
            kb.op('act', lambda E: E.copy(out=qab[:, :, 0:64], in_=Rq[:]), r=[Rq], w=[qab])
            pt_ = pTr[tri % 2]
            tri += 1
            for h in range(8):
                kb.op('pe', lambda E: E.transpose(out=pt_[:, h, :], in_=qab[:, h, :], identity=ident[:]), r=[qab, ident], w=[pt_])
            kb.op('act', lambda E: E.copy(out=qs_[:, :, t4 * 128:(t4 + 1) * 128], in_=pt_[:]), r=[pt_], w=[qs_])
        nbt = 1 if s < 4 else 2
        for g in range(2):
            for h in range(4 * g, 4 * g + 4):
                specs = []
                for bt in range(nbt):
                    m = (s if s <= 4 else None) if bt == 0 else 5 + (s - 4)
                    masks = [] if m is None else [(0, 512, cmask[:, m, :])]
                    specs.append(dict(lhsT=kcmpT[0:64, g, bt * 128:(bt + 1) * 128],
                                      rhs_fn=lambda q0, q1, h=h: qs_[0:64, h, q0 * 128:q1 * 128], qt0=0, qt1=4, masks=masks, mk=[cmask],
                                      v_fn=lambda qt, bt=bt, g=g: rhs_cmp[:, bt, g, :], nk=128, rk=[kcmpT], rv=[rhs_cmp]))
                attn_block(kb, lambda qt: (oC[:, qt, :], oC), PTb, pti, sc_ps, specs, ident, [qs_])
                finalize(oC, h, 0, True, cmp=True)
            for qt in range(4):
                ns_ = nst[qt % 2]
                kb.op('dve', lambda E: E.tensor_tensor(out=scr[:], in0=impacc[:, qt, :], in1=Ab[qt][:], op=ALU.mult), r=[(impacc, qt), Ab[qt]], w=[scr])
                kb.op('dve', lambda E: E.tensor_tensor(out=scr[:], in0=scr[:], in1=Bb[qt][:], op=ALU.add), r=[scr, Bb[qt]], w=[scr])
                kb.op('dve', lambda E: E.max(out=m8[:, 0:8], in_=scr[:]), r=[scr], w=[(m8, 0)])
                kb.op('dve', lambda E: E.match_replace(out=scr2[:], in_to_replace=m8[:, 0:8], in_values=scr[:], imm_value=-1e30),
                      r=[scr, (m8, 0)], w=[scr2])
                kb.op('dve', lambda E: E.max(out=m8[:, 8:16], in_=scr2[:]), r=[scr2], w=[(m8, 1)])
                kb.op('dve', lambda E: E.tensor_scalar(out=ns_[:, 64:128], in0=scr[:], scalar1=m8[:, 15:16], scalar2=1.0, op0=ALU.is_ge,
                                                       op1=ALU.subtract), r=[scr, (m8, 1)], w=[ns_])
                pt_ = pTr[tri % 2]
                tri += 1
                kb.op('pe', lambda E: E.transpose(out=pt_[:, 0, :], in_=ns_[:], identity=ident[:]), r=[ns_, ident], w=[pt_])
                for h in range(4 * g, 4 * g + 4):
                    if h % 2 == 0:
                        kb.op('act', lambda E: E.copy(out=qs_[64:128, h, qt * 128:(qt + 1) * 128], in_=pt_[64:128, 0, :]), r=[pt_], w=[qs_])
                    else:
                        kb.op('dve', lambda E: E.tensor_copy(out=qs_[64:128, h, qt * 128:(qt + 1) * 128], in_=pt_[64:128, 0, :]), r=[pt_], w=[qs_])
        for h in range(8):
            g = h // 4
            specs = []
            for kt in range(max(0, 4 * s - 4), 4 * s + 4):
                lo = max(kt - 4 * s, 0)
                hi = min(kt + 4 - 4 * s, 3)
                masks = []
                if kt >= 4 * s:
                    masks.append(((kt - 4 * s - lo) * 128, 128, tril[:]))
                if kt + 4 <= 4 * s + 3:
                    masks.append(((kt + 4 - 4 * s - lo) * 128, 128, far[:]))
                specs.append(dict(lhsT=kvT[0:64, 2 + g, kt * 128:(kt + 1) * 128],
                                  rhs_fn=lambda q0, q1, h=h: qs_[0:64, h, q0 * 128:q1 * 128], qt0=lo, qt1=hi + 1, masks=masks, mk=[tril, far],
                                  v_fn=lambda qt, kt=kt, g=g: vw1[:, kt, g, :], nk=128, rk=[(kvT, 'k')], rv=[vw1]))
            attn_block(kb, lambda qt: (oW[:, qt, 0:65], oW), PTb, pti, sc_ps, specs, ident, [qs_])
            finalize(oW, h, 2, False)
            specs = []
            for kt in range(0, 4 * s + 4):
                lo = max(kt - 4 * s, 0)
                masks = [(0, 128, tril[:])] if kt >= 4 * s else []
                specs.append(dict(lhsT=kvT[:, g, kt * 128:(kt + 1) * 128],
                                  rhs_fn=lambda q0, q1, h=h: qs_[:, h, q0 * 128:q1 * 128], qt0=lo, qt1=4, masks=masks, mk=[tril],
                                  v_fn=lambda qt, kt=kt, g=g: vs1[:, kt, g, :], nk=128, rk=[(kvT, 'k'), (kvT, 'E')], rv=[vs1]))
            attn_block(kb, lambda qt: (oS[:, qt, 0:65], oS), PTb, pti, sc_ps, specs, ident, [qs_])
            finalize(oS, h, 1, False)
        for qt in range(4):
            t = 4 * s + qt
            b = t % 2
            kb.op('act', lambda E: E.copy(out=obb[b][:], in_=ob[qt][:].rearrange("p h d -> p (h d)")), r=[(ob[qt], h) for h in range(8)], w=[obb[b]])
            pt_ = pTr[tri % 2]
            tri += 1
            for c in range(4):
                kb.op('pe', lambda E: E.transpose(out=pt_[:, c, :], in_=obb[b][:, c * 128:(c + 1) * 128], identity=ident[:]), r=[obb[b], ident], w=[pt_])
            kb.op('act', lambda E: E.copy(out=obT[b][:], in_=pt_[:, 0:4, :]), r=[pt_], w=[obT[b]])
            kb.dma('sp', io['ocatT'][512:1024, t * 128:(t + 1) * 128].rearrange("(c p) t -> p c t", p=128), obT[b][:], r=[obT[b]], w=['ocatT_b'])
    pa.close()
    ph.close()


def phase_ffn(kb, io):
    nc = kb.nc
    ph = Phase(kb, "f1")
    ident = ph.sb("ident", [128, 128], BF16)
    kb.dma('sp', ident[:], io['c_ident'][:, :], w=[ident])
    woutb = ph.sb("woutb", [128, 12, D], BF16)
    wupb = ph.sb("wupb", [128, 8, 2 * FF], BF16)
    gam = ph.sb("gam", [128, 8], F32)
    cw = ph.sb("cw", [128, 3, 44], F32)
    kb.dma('sp', gam[:], io['ffn_norm_w'].rearrange("(c p) -> p c", p=128), w=[gam], allow_slow_non_contiguous=True)
    for j in range(3):
        kb.dma('sp', cw[:, j, :], io['ffn_conv_w'][j, :].rearrange("(c p) -> p c", p=128), w=[cw], allow_slow_non_contiguous=True)
    load_cast_weight(kb, ph, woutb, io['w_out'], 12, D)
    load_cast_weight(kb, ph, wupb, io['ffn_w_up'], 8, 2 * FF, gam=gam, stage_cols=1408)

    oT = ph.sbn("oT", [128, 12, 128], BF16, 2)
    xt = ph.sbn("xt", [128, D], F32, 2)
    hs = ph.sbn("hs", [128, D], F32, 2)
    junk = ph.sb("junk", [128, D], BF16)
    ss = ph.sbn("ss", [128, 1], F32, 2)
    rs = ph.sbn("rs", [128, 1], F32, 2)
    hn = ph.sbn("hn", [128, D], BF16, 2)
    hnT = ph.sbn("hnT", [128, 8, 512], BF16, 2)
    ug = ph.sbn("ug", [128, 514], F32, 2)
    uv = ph.sbn("uv", [128, 514], F32, 2)
    ag = ph.sbn("ag", [128, 512], F32, 2)
    av = ph.sbn("av", [128, 512], F32, 2)
    sg = ph.sbn("sg", [128, 512], F32, 2)
    act = ph.sbn("act", [128, 512], BF16, 3)
    halo = ph.sb("halo", [128, 44, 2], F32)
    pH = ph.psn("pH", [128, 512], F32, 2)
    pT = ph.ps("pT", [128, 8, 128], BF16)
    pU = ph.psn("pU", [128, 512], F32, 4)
    kb.op('pool', lambda E: E.memset(halo[:], 0.0), w=[halo])

    def conv3(e1, dst, src, fb):
        kb.op(e1, lambda E: E.tensor_scalar(out=dst[:], in0=src[:, 2:514], scalar1=cw[:, 2, fb:fb + 1], scalar2=None, op0=ALU.mult),
              r=[(src, 'b'), cw], w=[dst])
        for j in range(2):
            kb.op('dve', lambda E: E.scalar_tensor_tensor(out=dst[:], in0=src[:, j:j + 512], scalar=cw[:, j, fb:fb + 1], in1=dst[:],
                                                       op0=ALU.mult, op1=ALU.add), r=[(src, 'b'), (src, 'h'), cw, dst], w=[dst])

    for s in range(S // 512):
        hT = hnT[s % 2]
        for t4 in range(4):
            t = s * 4 + t4
            b = t % 2
            kb.dma('sp', oT[b][:], io['ocatT'][:, t * 128:(t + 1) * 128].rearrange("(c p) t -> p c t", p=128),
                   r=['ocatT_a', 'ocatT_b', 'ocatT_c'], w=[oT[b]])
            kb.dma('sp', xt[b][:], io['x'][t * 128:(t + 1) * 128, :], w=[xt[b]])
            for half in range(2):
                p = pH[half]
                for c in range(12):
                    kb.op('pe', lambda E: E.matmul(p[:], lhsT=oT[b][:, c, :], rhs=woutb[:, c, half * 512:(half + 1) * 512],
                                                   start=(c == 0), stop=(c == 11)), r=[oT[b], (woutb, c)], w=[p])
                kb.op('dve', lambda E: E.tensor_tensor(out=hs[b][:, half * 512:(half + 1) * 512], in0=p[:],
                                                       in1=xt[b][:, half * 512:(half + 1) * 512], op=ALU.add),
                      r=[p, xt[b]], w=[(hs[b], half)])
            kb.dma('sp', io['h_s'][t * 128:(t + 1) * 128, :], hs[b][:], r=[(hs[b], 0), (hs[b], 1)], w=['h_s'])
            rms_rstd(kb, hs[b][:], junk, ss[b], rs[b], D, [(hs[b], 0), (hs[b], 1)])
            kb.op('pool', lambda E: E.tensor_scalar(out=hn[b][:], in0=hs[b][:], scalar1=rs[b][:, 0:1], scalar2=None, op0=ALU.mult),
                  r=[(hs[b], 0), (hs[b], 1), rs[b]], w=[hn[b]])
            for c in range(8):
                kb.op('pe', lambda E: E.transpose(out=pT[:, c, :], in_=hn[b][:, c * 128:(c + 1) * 128], identity=ident[:]),
                      r=[hn[b], ident], w=[pT])
            kb.op('act', lambda E: E.copy(out=hT[:, :, t4 * 128:(t4 + 1) * 128], in_=pT[:]), r=[pT], w=[hT])
        for fb in range(22):
            k2 = fb % 2
            pg = pU[2 * k2]
            pv = pU[2 * k2 + 1]
            for (p_, f0) in ((pg, fb), (pv, 22 + fb)):
                for c in range(8):
                    kb.op('pe', lambda E: E.matmul(p_[:], lhsT=wupb[:, c, f0 * 128:(f0 + 1) * 128], rhs=hT[:, c, :],
                                                   start=(c == 0), stop=(c == 7)), r=[hT, (wupb, c)], w=[p_])
            g_, v_ = ug[k2], uv[k2]
            kb.op('act', lambda E: E.copy(out=g_[:, 2:514], in_=pg[:]), r=[pg], w=[(g_, 'b')])
            kb.op('act', lambda E: E.copy(out=v_[:, 2:514], in_=pv[:]), r=[pv], w=[(v_, 'b')])
            for (u_, f0) in ((g_, fb), (v_, 22 + fb)):
                kb.op('pool', lambda E: E.tensor_copy(out=u_[:, 0:2], in_=halo[:, f0, :]), r=[(halo, f0)], w=[(u_, 'h')])
                kb.op('pool', lambda E: E.tensor_copy(out=halo[:, f0, :], in_=u_[:, 512:514]), r=[(u_, 'b')], w=[(halo, f0)])
            conv3('pool', ag[k2], g_, fb)
            conv3('pool', av[k2], v_, 22 + fb)
            kb.op('act', lambda E: E.activation(out=sg[k2][:], in_=ag[k2][:], func=AF.Silu), r=[ag[k2]], w=[sg[k2]])
            a_ = act[fb % 3]
            kb.op('pool', lambda E: E.tensor_tensor(out=a_[:], in0=sg[k2][:], in1=av[k2][:], op=ALU.mult), r=[sg[k2], av[k2]], w=[a_])
            kb.dma('sp', io['actT'][fb * 128:(fb + 1) * 128, s * 512:(s + 1) * 512], a_[:], r=[a_], w=['actT'])
    ph.close()

    ph = Phase(kb, "f2")
    wdnb = ph.sb("wdnb", [128, 22, D], BF16)
    load_cast_weight(kb, ph, wdnb, io['ffn_w_down'], 22, D)
    aT = ph.sbn("aT", [128, 22, 128], BF16, 2)
    hs = ph.sbn("hs", [128, D], F32, 2)
    ot = ph.sbn("ot", [128, D], F32, 2)
    pD = ph.psn("pD", [128, 512], F32, 4)
    for t in range(NT):
        b = t % 2
        kb.dma('sp', aT[b][:], io['actT'][:, t * 128:(t + 1) * 128].rearrange("(c p) t -> p c t", p=128), r=['actT'], w=[aT[b]])
        kb.dma('sp', hs[b][:], io['h_s'][t * 128:(t + 1) * 128, :], r=['h_s'], w=[hs[b]])
        for half in range(2):
            p = pD[2 * b + half]
            for c in range(22):
                kb.op('pe', lambda E: E.matmul(p[:], lhsT=aT[b][:, c, :], rhs=wdnb[:, c, half * 512:(half + 1) * 512],
                                               start=(c == 0), stop=(c == 21)), r=[aT[b], (wdnb, c)], w=[p])
            kb.op('dve', lambda E: E.tensor_tensor(out=ot[b][:, half * 512:(half + 1) * 512], in0=p[:],
                                                   in1=hs[b][:, half * 512:(half + 1) * 512], op=ALU.add),
                  r=[p, hs[b]], w=[(ot[b], half)])
        kb.dma('sp', io['out'][t * 128:(t + 1) * 128, :], ot[b][:], r=[(ot[b], 0), (ot[b], 1)], w=['out'])
    ph.close()


W_NAMES = ['attn_norm_w', 'mem_norm_w', 'w_in', 'gdn_conv_w', 'gdn_a_log', 'gdn_dt_bias', 'gdn_out_norm_w',
           'nsa_q_norm_w', 'nsa_kc_norm_w', 'nsa_ks_norm_w', 'nsa_kw_norm_w', 'nsa_cmp_pos_k', 'nsa_cmp_pos_v',
           'nsa_cmp_k_w1', 'nsa_cmp_k_w2', 'nsa_cmp_v_w1', 'nsa_cmp_v_w2', 'mem_w_kv', 'mem_q_norm_w', 'mem_k_norm_w',
           'w_out', 'ffn_norm_w', 'ffn_w_up', 'ffn_conv_w', 'ffn_w_down']
W_SHAPES = {
    'attn_norm_w': [D], 'mem_norm_w': [D], 'w_in': [D, INW], 'gdn_conv_w': [4, 1536], 'gdn_a_log': [4], 'gdn_dt_bias': [4],
    'gdn_out_norm_w': [128], 'nsa_q_norm_w': [64], 'nsa_kc_norm_w': [64], 'nsa_ks_norm_w': [64], 'nsa_kw_norm_w': [64],
    'nsa_cmp_pos_k': [32, 64], 'nsa_cmp_pos_v': [32, 64], 'nsa_cmp_k_w1': [2048, 64], 'nsa_cmp_k_w2': [64, 64],
    'nsa_cmp_v_w1': [2048, 64], 'nsa_cmp_v_w2': [64, 64], 'mem_w_kv': [D, 1024], 'mem_q_norm_w': [128], 'mem_k_norm_w': [128],
    'w_out': [1536, D], 'ffn_norm_w': [D], 'ffn_w_up': [D, 2 * FF], 'ffn_conv_w': [3, 2 * FF], 'ffn_w_down': [FF, D],
}


def make_consts():
    c = {}
    c['c_ident'] = np.eye(128, dtype=np.float32).astype(ml_dtypes.bfloat16)
    idx = np.arange(128)
    same = (idx[:, None] // 64) == (idx[None, :] // 64)
    c['c_btri'] = (same & (idx[:, None] <= idx[None, :])).astype(np.float32)
    c['c_bones'] = same.astype(np.float32)
    c['c_strict'] = (same & (idx[:, None] > idx[None, :])).astype(np.float32)
    c['c_mlow'] = np.where(same & (idx[:, None] >= idx[None, :]), 0.0, NEG).astype(np.float32)
    c['c_mup'] = np.ascontiguousarray(c['c_mlow'].T)
    c['c_mch'] = np.stack([(idx < 64), (idx >= 64)], axis=1).astype(np.float32)
    bf = ml_dtypes.bfloat16
    c['c_tril'] = np.where(idx[:, None] <= idx[None, :], 0.0, NEG).astype(np.float32).astype(bf)
    c['c_far'] = np.where(idx[None, :] < idx[:, None], 0.0, NEG).astype(np.float32).astype(bf)
    cm = np.zeros((9, 128, 512), np.float32)
    f = np.arange(512)
    for m in range(9):
        bt, s_ = (0, m) if m < 5 else (1, m - 1)
        blk = 128 * bt + idx
        vis = (16 * blk[:, None] + 31 <= 512 * s_ + f[None, :]) & (blk[:, None] < 255)
        cm[m] = np.where(vis, 0.0, NEG)
    c['c_cmask'] = cm.astype(bf)
    kk = np.arange(S)
    c['c_E'] = np.where((kk[None, :] // 64) == np.arange(64)[:, None], -NEG, 0.0).astype(np.float32).astype(bf)
    ci = np.arange(256) * 16
    sj = np.arange(64) * 64
    ovl = np.clip(np.minimum(ci[:, None] + 32, sj[None, :] + 64) - np.maximum(ci[:, None], sj[None, :]), 0, None) / 16.0
    ovl[255] = 0.0
    c['c_ovl'] = np.ascontiguousarray(ovl.reshape(2, 128, 64).transpose(1, 0, 2)).astype(np.float32).astype(bf)
    pos = np.arange(S, dtype=np.float32)
    inv = (1.0 / (np.float32(500000.0) ** (np.arange(0, 16, 2, dtype=np.float32) / np.float32(16)))).astype(np.float32)
    ang = pos[:, None] * inv[None, :]
    c['c_rope'] = np.concatenate([np.cos(ang), np.sin(ang)], axis=1).astype(np.float32)
    tt = np.arange(S)
    cur = tt // 64
    blk = np.arange(64)
    valid = blk[None, :] <= cur[:, None]
    forced = (blk[None, :] == 0) | (blk[None, :] == cur[:, None]) | (blk[None, :] == cur[:, None] - 1)
    c['c_A'] = (valid & ~forced).astype(np.float32).reshape(NT, 128, 64)
    c['c_B'] = np.where(valid, np.where(forced, 1e6, 0.0), -1e9).astype(np.float32).reshape(NT, 128, 64)
    return c


def build_program(dbg=False, phases=('ip', 'mem', 'gdn', 'nsa', 'ffn'), dbg_ocat=False):
    nc = bass.Bass("TRN2", target_bir_lowering=False)
    io = {}
    io['x'] = nc.dram_tensor("x", [S, D], F32, kind="ExternalInput").ap()
    io['mem'] = nc.dram_tensor("mem", [256, D], F32, kind="ExternalInput").ap()
    for n in W_NAMES:
        io[n] = nc.dram_tensor(n, W_SHAPES[n], F32, kind="ExternalInput").ap()
    for n, v in make_consts().items():
        io[n] = nc.dram_tensor(n, list(v.shape), BF16 if v.dtype == ml_dtypes.bfloat16 else F32, kind="ExternalInput").ap()
    io['out'] = nc.dram_tensor("out", [S, D], F32, kind="ExternalOutput").ap()
    sk = "ExternalOutput" if dbg else "Internal"
    io['tm'] = nc.dram_tensor("tm", [S, TMW], F32, kind=sk).ap()
    io['qkv_tm'] = nc.dram_tensor("qkv_tm", [S, 1536], BF16, kind=sk).ap()
    if dbg_ocat:
        io['ocatT'] = nc.dram_tensor("ocatT", [1536, S], BF16, kind="ExternalInput").ap()
    else:
        io['ocatT'] = nc.dram_tensor("ocatT", [1536, S], BF16, kind=sk).ap()
    io['h_s'] = nc.dram_tensor("h_s", [S, D], F32, kind=sk).ap()
    io['actT'] = nc.dram_tensor("actT", [FF, S], BF16, kind="Internal").ap()
    kb = KB(nc)
    if 'ip' in phases:
        phase_inproj(kb, io)
    if 'mem' in phases:
        phase_mem(kb, io)
    if 'gdn' in phases:
        phase_gdn(kb, io)
    if 'nsa' in phases:
        phase_nsa(kb, io)
    if 'ffn' in phases:
        phase_ffn(kb, io)
    kb.finish()
    return nc, kb


def make_in_maps(inputs):
    consts = make_consts()
    maps = []
    for b in range(8):
        m = {'x': np.ascontiguousarray(inputs['x'][b]), 'mem': np.ascontiguousarray(inputs['mem'][b])}
        for n in W_NAMES:
            m[n] = np.ascontiguousarray(np.asarray(inputs[n])[0])
        m.update(consts)
        maps.append(m)
    return maps


def kernel(**inputs):
    nc, kb = build_program()
    maps = make_in_maps(inputs)
    res = run_bass_kernel_spmd(nc, maps, core_ids=list(range(8)))
    return np.stack([np.asarray(r['out'], dtype=np.float32) for r in res.results], axis=0)
```

```python
import os
import numpy as np
from contextlib import ExitStack
import concourse.bass as bass
import concourse.mybir as mybir
from concourse.bass_utils import run_bass_kernel_spmd
import ml_dtypes

F32 = mybir.dt.float32
BF16 = mybir.dt.bfloat16
AF = mybir.ActivationFunctionType
ALU = mybir.AluOpType
AX = mybir.AxisListType

S = 4096
D = 1024
NT = S // 128
INW = 3872
TMW = 2336
A_OFF, B_OFF, GATE_OFF, NQ_OFF = 0, 4, 8, 520
KC_OFF, VC_OFF, KS_OFF, VS_OFF, KW_OFF, VW_OFF = 1032, 1160, 1288, 1416, 1544, 1672
NG_OFF, MQ_OFF = 1800, 1824
FF = 2816
NEG = -30000.0
EPS = 1e-6


class T:
    def __init__(self, t, k):
        self.t = t
        self.k = k

    def __getitem__(self, idx):
        return self.t[idx]


class KB:
    NDS = 16

    def __init__(self, nc):
        self.nc = nc
        self.stack = ExitStack()
        self.eng = {'pe': nc.tensor, 'act': nc.scalar, 'dve': nc.vector, 'pool': nc.gpsimd, 'sp': nc.sync}
        self.sem = {}
        for e in self.eng:
            self.sem[e] = self.stack.enter_context(nc.semaphore("s_" + e))
        for j in range(self.NDS):
            self.sem[('d', j)] = self.stack.enter_context(nc.semaphore("d_%d" % j))
        self.cnt = {e: 0 for e in self.eng}
        self.seen = {e: {} for e in self.eng}
        self.state = {}
        self.dma_i = 0
        self.dma_uses = [0] * self.NDS
        self.nins = 0
        self.rr = 0
        self.excl = set()

    def _wait(self, e, evs):
        need = {}
        for (sk, v) in evs:
            if sk == e and e in ('pe', 'sp'):
                continue
            if self.seen[e].get(sk, 0) < v:
                need[sk] = max(need.get(sk, 0), v)
        for sk, v in need.items():
            self.eng[e].wait_ge(self.sem[sk], v)
            self.seen[e][sk] = v

    @staticmethod
    def _keys(lst):
        out = []
        for x in lst:
            if isinstance(x, T):
                out.append(x.k)
            elif isinstance(x, (list, tuple)) and len(x) and isinstance(x[0], T):
                out.append((x[0].k,) + tuple(x[1:]))
            else:
                out.append(x)
        return out

    def _deps(self, reads, writes):
        evs = []
        for k in reads:
            st = self.state.get(k)
            if st and st[0]:
                evs.append(st[0])
        for k in writes:
            st = self.state.get(k)
            if st:
                if st[0]:
                    evs.append(st[0])
                evs.extend(st[1])
        return evs

    def _update(self, ev, reads, writes):
        for k in reads:
            st = self.state.setdefault(k, [None, []])
            st[1].append(ev)
            if len(st[1]) > 12:
                best = {}
                for (sk, v) in st[1]:
                    best[sk] = max(best.get(sk, 0), v)
                st[1] = list(best.items())
        for k in writes:
            self.state[k] = [ev, []]

    def op(self, e, fn, r=(), w=()):
        r = self._keys(r)
        w = self._keys(w)
        w = w + [k for k in r if k in self.excl and k not in w]
        self._wait(e, self._deps(r, w))
        ins = fn(self.eng[e])
        self.cnt[e] += 1
        ins.then_inc(self.sem[e], 1)
        self._update((e, self.cnt[e]), r, w)
        self.nins += 1
        return ins

    def dma(self, q, out, in_, r=(), w=(), **kw):
        r = self._keys(r)
        w = self._keys(w)
        j = self.dma_i % self.NDS
        self.dma_i += 1
        evs = self._deps(r, w)
        if self.dma_uses[j] > 0:
            evs.append((('d', j), 16 * self.dma_uses[j]))
        self._wait(q, evs)
        ins = self.eng[q].dma_start(out=out, in_=in_, **kw)
        self.dma_uses[j] += 1
        ins.then_inc(self.sem[('d', j)], 16)
        ev = (('d', j), 16 * self.dma_uses[j])
        self._update(ev, r, w)
        self.nins += 1
        return ev

    def barrier(self):
        evs = [(f, self.cnt[f]) for f in self.eng if self.cnt[f]]
        evs += [(('d', j), 16 * self.dma_uses[j]) for j in range(self.NDS) if self.dma_uses[j]]
        for e in self.eng:
            self._wait(e, [ev for ev in evs if ev[0] != e])

    def finish(self):
        self.barrier()
        self.stack.close()

    def ew(self, with_act=False):
        self.rr += 1
        lst = ('dve', 'pool', 'act') if with_act else ('dve', 'pool')
        return lst[self.rr % len(lst)]


class Phase:
    def __init__(self, kb, tag):
        self.kb = kb
        self.nc = kb.nc
        self.tag = tag
        self.st = ExitStack()

    def sb(self, name, shape, dt):
        n = self.tag + "_" + name
        return T(self.st.enter_context(self.nc.sbuf_tensor(n, list(shape), dt)), n)

    def sbn(self, name, shape, dt, n):
        return [self.sb("%s%d" % (name, i), shape, dt) for i in range(n)]

    def ps(self, name, shape, dt=F32):
        n = self.tag + "_" + name
        self.kb.excl.add(n)
        return T(self.st.enter_context(self.nc.psum_tensor(n, list(shape), dt)), n)

    def psn(self, name, shape, dt, n):
        return [self.ps("%s%d" % (name, i), shape, dt) for i in range(n)]

    def close(self):
        self.kb.barrier()
        self.st.close()


def load_cast_weight(kb, ph, dst, src_ap, nchunks, ncols, gam=None, stage_cols=None):
    stage_cols = stage_cols or ncols
    stg = ph.sbn("stg_" + dst.k, [128, stage_cols], F32, 2)
    i = 0
    engs = ('dve', 'act', 'dve')
    for c in range(nchunks):
        for c0 in range(0, ncols, stage_cols):
            c1 = min(ncols, c0 + stage_cols)
            sg = stg[i % 2]
            kb.dma('sp', sg[:, 0:c1 - c0], src_ap[c * 128:(c + 1) * 128, c0:c1], w=[sg])
            e = engs[i % 3]
            o = dst[:, c, c0:c1]
            if gam is None:
                if e == 'act':
                    kb.op(e, lambda E: E.copy(out=o, in_=sg[:, 0:c1 - c0]), r=[sg], w=[(dst, c)])
                else:
                    kb.op(e, lambda E: E.tensor_copy(out=o, in_=sg[:, 0:c1 - c0]), r=[sg], w=[(dst, c)])
            else:
                if e == 'act':
                    kb.op(e, lambda E: E.activation(out=o, in_=sg[:, 0:c1 - c0], func=AF.Copy, scale=gam[:, c:c + 1]),
                          r=[sg, gam], w=[(dst, c)])
                else:
                    kb.op(e, lambda E: E.tensor_scalar(out=o, in0=sg[:, 0:c1 - c0], scalar1=gam[:, c:c + 1], scalar2=None,
                                                       op0=ALU.mult), r=[sg, gam], w=[(dst, c)])
            i += 1


def rms_rstd(kb, src_ap, junk, ss, rs, n, rkeys):
    kb.op('act', lambda E: E.activation(out=junk[:, 0:n], in_=src_ap, func=AF.Square, accum_out=ss[:]), r=rkeys, w=[junk, ss])
    kb.op('act', lambda E: E.activation(out=rs[:], in_=ss[:], func=AF.Sqrt, scale=1.0 / n, bias=EPS), r=[ss], w=[rs])
    kb.op('dve', lambda E: E.reciprocal(out=rs[:], in_=rs[:]), r=[rs], w=[rs])


def phase_inproj(kb, io):
    nc = kb.nc
    ph = Phase(kb, "ip")
    winb = ph.sb("winb", [128, 8, INW], BF16)
    gam = ph.sb("gam", [128, 8], F32)
    cw = ph.sb("cw", [128, 4, 12], F32)
    ident = ph.sb("ident", [128, 128], BF16)
    kb.dma('sp', gam[:], io['attn_norm_w'].rearrange("(c p) -> p c", p=128), w=[gam], allow_slow_non_contiguous=True)
    for j in range(4):
        kb.dma('sp', cw[:, j, :], io['gdn_conv_w'][j, :].rearrange("(c p) -> p c", p=128), w=[cw], allow_slow_non_contiguous=True)
    kb.dma('sp', ident[:], io['c_ident'][:, :], w=[ident])
    load_cast_weight(kb, ph, winb, io['w_in'], 8, INW, gam=gam, stage_cols=1936)

    xt = ph.sbn("xt", [128, D], F32, 2)
    junk = ph.sb("junk", [128, D], BF16)
    ss = ph.sbn("ss", [128, 1], F32, 2)
    rs = ph.sbn("rs", [128, 1], F32, 2)
    xn = ph.sbn("xn", [128, D], BF16, 2)
    xnT = ph.sbn("xnT", [128, 8, 512], BF16, 2)
    xc = ph.sbn("xc", [128, 515], F32, 3)
    acc = ph.sbn("acc", [128, 512], F32, 3)
    halo = ph.sb("halo", [128, 12, 3], F32)
    qT = ph.sb("qT", [128, 12, 512], BF16)
    qtm = ph.sbn("qtm", [128, 1536], BF16, 2)
    tmt = ph.sbn("tmt", [128, TMW], F32, 2)
    pT = ph.psn("pT", [128, 8, 128], BF16, 2)
    pF = ph.psn("pF", [128, 512], F32, 2)
    pM = ph.psn("pM", [128, 512], F32, 2)
    pQ = ph.psn("pQ", [128, 8, 128], BF16, 2)

    kb.op('pool', lambda E: E.memset(halo[:], 0.0), w=[halo])
    it = 0
    for s in range(S // 512):
        xT = xnT[s % 2]
        for t4 in range(4):
            t = s * 4 + t4
            b = it % 2
            it += 1
            kb.dma('sp', xt[b][:], io['x'][t * 128:(t + 1) * 128, :], w=[xt[b]])
            rms_rstd(kb, xt[b][:], junk, ss[b], rs[b], D, [xt[b]])
            kb.op('dve', lambda E: E.tensor_scalar(out=xn[b][:], in0=xt[b][:], scalar1=rs[b][:, 0:1], scalar2=None, op0=ALU.mult),
                  r=[xt[b], rs[b]], w=[xn[b]])
            for c in range(8):
                kb.op('pe', lambda E: E.transpose(out=pT[b][:, c, :], in_=xn[b][:, c * 128:(c + 1) * 128], identity=ident[:]),
                      r=[xn[b], ident], w=[pT[b]])
            kb.op('act', lambda E: E.copy(out=xT[:, :, t4 * 128:(t4 + 1) * 128], in_=pT[b][:]), r=[pT[b]], w=[xT])
        for cb in range(12):
            pf = pF[cb % 2]
            for c in range(8):
                kb.op('pe', lambda E: E.matmul(pf[:], lhsT=winb[:, c, cb * 128:(cb + 1) * 128], rhs=xT[:, c, :],
                                               start=(c == 0), stop=(c == 7)), r=[xT, (winb, c)], w=[pf])
            x3 = xc[cb % 3]
            ac = acc[cb % 3]
            kb.op('act', lambda E: E.copy(out=x3[:, 3:515], in_=pf[:]), r=[pf], w=[(x3, 'b')])
            kb.op('pool', lambda E: E.tensor_copy(out=x3[:, 0:3], in_=halo[:, cb, :]), r=[(halo, cb)], w=[(x3, 'h')])
            kb.op('pool', lambda E: E.tensor_copy(out=halo[:, cb, :], in_=x3[:, 512:515]), r=[(x3, 'b')], w=[(halo, cb)])
            e1 = 'dve' if cb % 2 == 0 else 'pool'
            kb.op(e1, lambda E: E.tensor_scalar(out=ac[:], in0=x3[:, 3:515], scalar1=cw[:, 3, cb:cb + 1], scalar2=None, op0=ALU.mult),
                  r=[(x3, 'b'), cw], w=[ac])
            for j in range(3):
                kb.op('dve', lambda E: E.scalar_tensor_tensor(out=ac[:], in0=x3[:, j:j + 512], scalar=cw[:, j, cb:cb + 1], in1=ac[:],
                                                           op0=ALU.mult, op1=ALU.add), r=[(x3, 'b'), (x3, 'h'), cw, ac], w=[ac])
            kb.op('act', lambda E: E.activation(out=qT[:, cb, :], in_=ac[:], func=AF.Silu), r=[ac], w=[(qT, cb)])
        for t4 in range(4):
            t = s * 4 + t4
            qm = qtm[t % 2]
            for g3 in range(3):
                pq = pQ[g3 % 2]
                for j in range(4):
                    cb = g3 * 4 + j
                    kb.op('pe', lambda E: E.transpose(out=pq[:, j, :], in_=qT[:, cb, t4 * 128:(t4 + 1) * 128], identity=ident[:]),
                          r=[(qT, cb), ident], w=[pq])
                e1 = 'dve' if g3 % 2 == 0 else 'act'
                if e1 == 'dve':
                    kb.op('dve', lambda E: E.tensor_copy(out=qm[:, g3 * 512:(g3 + 1) * 512], in_=pq[:, 0:4, :].rearrange("p a b -> p (a b)")),
                          r=[pq], w=[(qm, g3)])
                else:
                    kb.op('act', lambda E: E.copy(out=qm[:, g3 * 512:(g3 + 1) * 512], in_=pq[:, 0:4, :].rearrange("p a b -> p (a b)")),
                          r=[pq], w=[(qm, g3)])
            kb.dma('sp', io['qkv_tm'][t * 128:(t + 1) * 128, :], qm[:], r=[(qm, 0), (qm, 1), (qm, 2)], w=['qkv_tm'])
            tm = tmt[t % 2]
            for ci, n0 in enumerate(range(0, TMW, 512)):
                n1 = min(TMW, n0 + 512)
                pm = pM[ci % 2]
                for c in range(8):
                    kb.op('pe', lambda E: E.matmul(pm[:, 0:n1 - n0], lhsT=xT[:, c, t4 * 128:(t4 + 1) * 128],
                                                   rhs=winb[:, c, 1536 + n0:1536 + n1], start=(c == 0), stop=(c == 7)),
                          r=[xT, (winb, c)], w=[pm])
                if ci % 2 == 0:
                    kb.op('dve', lambda E: E.tensor_copy(out=tm[:, n0:n1], in_=pm[:, 0:n1 - n0]), r=[pm], w=[(tm, ci)])
                else:
                    kb.op('act', lambda E: E.copy(out=tm[:, n0:n1], in_=pm[:, 0:n1 - n0]), r=[pm], w=[(tm, ci)])
            kb.dma('sp', io['tm'][t * 128:(t + 1) * 128, :], tm[:], r=[(tm, i) for i in range(5)], w=['tm'])
    ph.close()


def attn_block(kb, o_ps, PTbuf, pti, sc_ps, kt_specs, ident, rkeys_q):
    pts = []
    for i, sp in enumerate(kt_specs):
        pss = sc_ps[pti[0] % len(sc_ps)]
        ptb = PTbuf[pti[0] % len(PTbuf)]
        pti[0] += 1
        pts.append(ptb)
        q0, q1 = sp['qt0'], sp['qt1']
        ncol = (q1 - q0) * 128
        nk = sp['nk']
        nm = len(sp['masks'])
        kb.op('pe', lambda E: E.matmul(pss[0:nk, 0:ncol], lhsT=sp['lhsT'], rhs=sp['rhs_fn'](q0, q1), start=True, stop=(nm == 0)),
              r=sp['rk'] + rkeys_q, w=[pss])
        for mi, (c0, nc_, mask) in enumerate(sp['masks']):
            kb.op('pe', lambda E: E.matmul(pss[0:nk, c0:c0 + nc_], lhsT=ident[0:nk, 0:nk], rhs=mask, start=False, stop=(mi == nm - 1)),
                  r=[ident] + sp.get('mk', []), w=[pss])
        kb.op('act', lambda E: E.activation(out=ptb[0:nk, 0:ncol], in_=pss[0:nk, 0:ncol], func=AF.Exp), r=[pss], w=[ptb])
    qts = sorted(set(q for sp in kt_specs for q in range(sp['qt0'], sp['qt1'])))
    for qt in qts:
        lst = [(sp, ptb) for sp, ptb in zip(kt_specs, pts) if sp['qt0'] <= qt < sp['qt1']]
        oap, okey = o_ps(qt)
        for j, (sp, ptb) in enumerate(lst):
            c0 = (qt - sp['qt0']) * 128
            nk = sp['nk']
            kb.op('pe', lambda E: E.matmul(oap, lhsT=ptb[0:nk, c0:c0 + 128], rhs=sp['v_fn'](qt), start=(j == 0), stop=(j == len(lst) - 1)),
                  r=[ptb] + sp['rv'], w=[okey])


def phase_mem(kb, io):
    nc = kb.nc
    ph = Phase(kb, "mm")
    ident = ph.sb("ident", [128, 128], BF16)
    kb.dma('sp', ident[:], io['c_ident'][:, :], w=[ident])
    wkv = ph.sb("wkv", [128, 8, 1024], BF16)
    gam = ph.sb("gam", [128, 8], F32)
    kb.dma('sp', gam[:], io['mem_norm_w'].rearrange("(c p) -> p c", p=128), w=[gam], allow_slow_non_contiguous=True)
    load_cast_weight(kb, ph, wkv, io['mem_w_kv'], 8, 1024, gam=gam)
    qnw = ph.sb("qnw", [128, 128], F32)
    knw = ph.sb("knw", [128, 128], F32)
    kb.dma('sp', qnw[:], io['mem_q_norm_w'].partition_broadcast(128), w=[qnw])
    kb.dma('sp', knw[:], io['mem_k_norm_w'].partition_broadcast(128), w=[knw])
    kb.op('dve', lambda E: E.tensor_scalar(out=qnw[:], in0=qnw[:], scalar1=128 ** -0.5, scalar2=None, op0=ALU.mult), r=[qnw], w=[qnw])

    mt = ph.sbn("mt", [128, D], F32, 2)
    junk = ph.sb("junk", [128, D], BF16)
    ss = ph.sb("ss", [128, 1], F32)
    rs = ph.sb("rs", [128, 1], F32)
    mn = ph.sb("mn", [128, D], BF16)
    mnT = ph.sb("mnT", [128, 8, 128], BF16)
    kvt = ph.sb("kvt", [128, 1024], F32)
    sq4 = ph.sb("sq4", [128, 4, 128], F32)
    ss4 = ph.sb("ss4", [128, 4], F32)
    rs4 = ph.sb("rs4", [128, 4], F32)
    kn = ph.sb("kn", [128, 4, 128], BF16)
    kT = ph.sb("kT", [128, 4, 256], BF16)
    v1 = ph.sb("v1", [128, 2, 4, 129], BF16)
    pT = ph.ps("pT", [128, 8, 128], BF16)
    pK = ph.psn("pK", [128, 512], F32, 2)
    kb.op('pool', lambda E: E.memset(v1[:], 1.0), w=[v1])
    for mt_i in range(2):
        m = mt[mt_i]
        kb.dma('sp', m[:], io['mem'][mt_i * 128:(mt_i + 1) * 128, :], w=[m])
        rms_rstd(kb, m[:], junk, ss, rs, D, [m])
        kb.op('dve', lambda E: E.tensor_scalar(out=mn[:], in0=m[:], scalar1=rs[:, 0:1], scalar2=None, op0=ALU.mult), r=[m, rs], w=[mn])
        for c in range(8):
            kb.op('pe', lambda E: E.transpose(out=pT[:, c, :], in_=mn[:, c * 128:(c + 1) * 128], identity=ident[:]), r=[mn, ident], w=[pT])
        kb.op('act', lambda E: E.copy(out=mnT[:], in_=pT[:]), r=[pT], w=[mnT])
        for half in range(2):
            pk = pK[half]
            for c in range(8):
                kb.op('pe', lambda E: E.matmul(pk[:], lhsT=mnT[:, c, :], rhs=wkv[:, c, half * 512:(half + 1) * 512],
                                               start=(c == 0), stop=(c == 7)), r=[mnT, (wkv, c)], w=[pk])
            kb.op('act', lambda E: E.copy(out=kvt[:, half * 512:(half + 1) * 512], in_=pk[:]), r=[pk], w=[(kvt, half)])
        k3 = kvt[:, 0:512].rearrange("p (h d) -> p h d", h=4)
        kb.op('dve', lambda E: E.tensor_tensor(out=sq4[:], in0=k3, in1=k3, op=ALU.mult), r=[(kvt, 0)], w=[sq4])
        kb.op('dve', lambda E: E.tensor_reduce(out=ss4[:], in_=sq4[:], axis=AX.X, op=ALU.add), r=[sq4], w=[ss4])
        kb.op('act', lambda E: E.activation(out=rs4[:], in_=ss4[:], func=AF.Sqrt, scale=1.0 / 128, bias=EPS), r=[ss4], w=[rs4])
        kb.op('dve', lambda E: E.reciprocal(out=rs4[:], in_=rs4[:]), r=[rs4], w=[rs4])
        kb.op('dve', lambda E: E.tensor_tensor(out=sq4[:], in0=k3, in1=rs4[:].unsqueeze(2).to_broadcast([128, 4, 128]), op=ALU.mult),
              r=[(kvt, 0), rs4], w=[sq4])
        kb.op('dve', lambda E: E.tensor_tensor(out=kn[:], in0=sq4[:], in1=knw[:].unsqueeze(1).to_broadcast([128, 4, 128]), op=ALU.mult),
              r=[sq4, knw], w=[kn])
        for h in range(4):
            kb.op('pe', lambda E: E.transpose(out=pT[:, h, :], in_=kn[:, h, :], identity=ident[:]), r=[kn, ident], w=[pT])
        kb.op('act', lambda E: E.copy(out=kT[:, :, mt_i * 128:(mt_i + 1) * 128], in_=pT[:, 0:4, :]), r=[pT], w=[kT])
        kb.op('dve', lambda E: E.tensor_copy(out=v1[:, mt_i, :, 0:128], in_=kvt[:, 512:1024].rearrange("p (h d) -> p h d", h=4)),
              r=[(kvt, 1)], w=[v1])

    qt_ = ph.sbn("qt", [128, 512], F32, 2)
    qs = ph.sb("qs", [128, 4, 128], F32)
    qn = ph.sbn("qn", [128, 4, 128], BF16, 2)
    qT = ph.sbn("qT", [128, 4, 512], BF16, 2)
    PTb = ph.sbn("PT", [128, 512], BF16, 4)
    pti = [0]
    oc = ph.sbn("oc", [128, 4, 128], BF16, 4)
    rinv = ph.sb("rinv", [128, 1], F32)
    ocT = ph.sbn("ocT", [128, 4, 128], BF16, 2)
    sc_ps = ph.psn("sc", [128, 512], F32, 2)
    o_psA = ph.ps("oA", [128, 2, 256], F32)
    o_psB = ph.ps("oB", [128, 2, 256], F32)
    pO = ph.ps("pO", [128, 8, 128], BF16)
    for s in range(S // 512):
        qTs = qT[s % 2]
        for t4 in range(4):
            t = s * 4 + t4
            q = qt_[t % 2]
            qb = qn[t % 2]
            kb.dma('sp', q[:], io['tm'][t * 128:(t + 1) * 128, MQ_OFF:MQ_OFF + 512], r=['tm'], w=[q])
            q3 = q[:].rearrange("p (h d) -> p h d", h=4)
            kb.op('pool', lambda E: E.tensor_tensor(out=qs[:], in0=q3, in1=q3, op=ALU.mult), r=[q], w=[qs])
            kb.op('dve', lambda E: E.tensor_reduce(out=ss4[:], in_=qs[:], axis=AX.X, op=ALU.add), r=[qs], w=[ss4])
            kb.op('act', lambda E: E.activation(out=rs4[:], in_=ss4[:], func=AF.Sqrt, scale=1.0 / 128, bias=EPS), r=[ss4], w=[rs4])
            kb.op('dve', lambda E: E.reciprocal(out=rs4[:], in_=rs4[:]), r=[rs4], w=[rs4])
            kb.op('dve', lambda E: E.tensor_tensor(out=qs[:], in0=q3, in1=rs4[:].unsqueeze(2).to_broadcast([128, 4, 128]), op=ALU.mult),
                  r=[q, rs4], w=[qs])
            kb.op('pool', lambda E: E.tensor_tensor(out=qb[:], in0=qs[:], in1=qnw[:].unsqueeze(1).to_broadcast([128, 4, 128]), op=ALU.mult),
                  r=[qs, qnw], w=[qb])
            for h in range(4):
                kb.op('pe', lambda E: E.transpose(out=pT[:, h, :], in_=qb[:, h, :], identity=ident[:]), r=[qb, ident], w=[pT])
            kb.op('act', lambda E: E.copy(out=qTs[:, :, t4 * 128:(t4 + 1) * 128], in_=pT[:, 0:4, :]), r=[pT], w=[qTs])
        for h in range(4):
            def o_ps(qt, h=h):
                return (o_psA[:, qt, 0:129], o_psA) if qt < 2 else (o_psB[:, qt - 2, 0:129], o_psB)
            specs = []
            for kt in range(2):
                specs.append(dict(lhsT=kT[:, h, kt * 128:(kt + 1) * 128], rhs_fn=lambda q0, q1, h=h: qTs[:, h, q0 * 128:q1 * 128],
                                  qt0=0, qt1=4, masks=[], v_fn=lambda qt, kt=kt, h=h: v1[:, kt, h, :], nk=128, rk=[kT], rv=[v1]))
            attn_block(kb, o_ps, PTb, pti, sc_ps, specs, ident, [qTs])
            for t4 in range(4):
                t = s * 4 + t4
                ob = oc[t4]
                oap, okey = o_ps(t4)
                kb.op('dve', lambda E: E.reciprocal(out=rinv[:], in_=oap[:, 128:129]), r=[okey], w=[rinv])
                kb.op('dve', lambda E: E.tensor_scalar(out=ob[:, h, :], in0=oap[:, 0:128], scalar1=rinv[:, 0:1], scalar2=None, op0=ALU.mult),
                      r=[okey, rinv], w=[(ob, h)])
        for t4 in range(4):
            t = s * 4 + t4
            ob = oc[t4]
            oT = ocT[t % 2]
            for h in range(4):
                kb.op('pe', lambda E: E.transpose(out=pO[:, h, :], in_=ob[:, h, :], identity=ident[:]), r=[(ob, h), ident], w=[pO])
            kb.op('act', lambda E: E.copy(out=oT[:], in_=pO[:, 0:4, :]), r=[pO], w=[oT])
            kb.dma('sp', io['ocatT'][1024:1536, t * 128:(t + 1) * 128].rearrange("(c p) t -> p c t", p=128), oT[:], r=[oT], w=['ocatT_c'])
    ph.close()


def phase_gdn(kb, io):
    nc = kb.nc
    ph = Phase(kb, "gd")
    ident = ph.sb("ident", [128, 128], BF16)
    btri = ph.sb("btri", [128, 128], F32)
    bones = ph.sb("bones", [128, 128], F32)
    ones = ph.sb("ones", [128, 128], F32)
    mlow = ph.sb("mlow", [128, 4, 128], F32)
    mup = ph.sb("mup", [128, 4, 128], F32)
    strict = ph.sb("strict", [128, 128], F32)
    mch = ph.sb("mch", [128, 2], F32)
    kb.dma('sp', ident[:], io['c_ident'][:, :], w=[ident])
    kb.dma('sp', btri[:], io['c_btri'][:, :], w=[btri])
    kb.dma('sp', bones[:], io['c_bones'][:, :], w=[bones])
    kb.dma('sp', strict[:], io['c_strict'][:, :], w=[strict])
    kb.dma('sp', mch[:], io['c_mch'][:, :], w=[mch])
    for h in range(4):
        kb.dma('sp', mlow[:, h, :], io['c_mlow'][:, :], w=[mlow])
        kb.dma('sp', mup[:, h, :], io['c_mup'][:, :], w=[mup])
    kb.op('pool', lambda E: E.memset(ones[:], 1.0), w=[ones])
    dtb = ph.sb("dtb", [128, 4], F32)
    nA = ph.sb("nA", [128, 4], F32)
    gw = ph.sb("gw", [128, 128], F32)
    kb.dma('sp', dtb[:], io['gdn_dt_bias'].partition_broadcast(128), w=[dtb])
    kb.dma('sp', nA[:], io['gdn_a_log'].partition_broadcast(128), w=[nA])
    kb.dma('sp', gw[:], io['gdn_out_norm_w'].partition_broadcast(128), w=[gw])
    kb.op('act', lambda E: E.activation(out=nA[:], in_=nA[:], func=AF.Exp), r=[nA], w=[nA])
    kb.op('dve', lambda E: E.tensor_scalar(out=nA[:], in0=nA[:], scalar1=-1.0, scalar2=None, op0=ALU.mult), r=[nA], w=[nA])

    def B4(t_, n=4):
        return t_.unsqueeze(2).to_broadcast([128, n, 128])

    def M4(t_):
        return t_.unsqueeze(1).to_broadcast([128, 4, 128])

    qkv = ph.sbn("qkv", [128, 3, 4, 128], BF16, 2)
    ab = ph.sbn("ab", [128, 8], F32, 2)
    gt = ph.sbn("gt", [128, 512], F32, 2)
    sm = ph.sbn("sm", [128, 64], F32, 2)
    gs = ph.sbn("gs", [128, 16], F32, 2)
    gm = ph.sb("gm", [128, 8], F32)
    R1 = ph.sb("R1", [128, 4, 128], F32)
    R2 = ph.sb("R2", [128, 4, 128], F32)
    tmpA = ph.sb("tmpA", [128, 4, 128], F32)
    tmpB = ph.sb("tmpB", [128, 4, 128], F32)
    dec = ph.sb("dec", [128, 4, 128], F32)
    decT = ph.sb("decT", [128, 4, 128], F32)
    sq = ph.sb("sq", [128, 4, 128], F32)
    KBG = ph.sb("KBG", [128, 4, 128], BF16)
    Kd = ph.sbn("Kd", [128, 4, 128], BF16, 2)
    VB = ph.sb("VB", [128, 4, 128], BF16)
    dg = ph.sbn("dg", [128, 4, 128], BF16, 4)
    QT = ph.sb("QT", [128, 4, 128], BF16)
    QG = ph.sbn("QG", [128, 4, 128], BF16, 2)
    KT = ph.sb("KT", [128, 4, 128], BF16)
    nbs = ph.sb("nbs", [128, 4, 128], F32)
    Xb = ph.sbn("X", [128, 4, 128], BF16, 2)
    Yb = ph.sbn("Y", [128, 4, 128], BF16, 2)
    Pb = ph.sbn("P", [128, 4, 128], BF16, 2)
    aqkT = ph.sb("aqkT", [128, 4, 128], BF16)
    negWT = ph.sb("negWT", [128, 4, 128], BF16)
    vnew = ph.sb("vnew", [128, 4, 128], BF16)
    Sf = ph.sb("Sf", [128, 4, 128], F32)
    Sbf = ph.sbn("Sbf", [128, 4, 128], BF16, 3)
    osb = ph.sb("osb", [128, 4, 128], F32)
    sgt = ph.sb("sgt", [128, 512], F32)
    oa = ph.sbn("oa", [128, 4, 128], BF16, 2)
    oaT = ph.sbn("oaT", [128, 4, 128], BF16, 2)
    pS = ph.ps("pS", [128, 512], F32)
    pD = ph.ps("pD", [128, 4, 128], F32)
    pA = ph.psn("pA", [128, 4, 128], F32, 2)
    pB = ph.psn("pB", [128, 8, 128], BF16, 1)
    pV = ph.ps("pV", [128, 4, 128], F32)
    pdS = ph.ps("pdS", [128, 4, 128], F32)
    pO = ph.ps("pO", [128, 4, 128], F32)
    kb.op('pool', lambda E: E.memset(Sf[:], 0.0), w=[Sf])
    kb.op('pool', lambda E: E.memset(Sbf[0][:], 0.0), w=[Sbf[0]])
    kb.op('pool', lambda E: E.memset(vnew[:], 0.0), w=[vnew])
    pai = [0]

    def PA():
        pai[0] += 1
        return pA[pai[0] % 2]

    def mm4(p, lf, rf, rk):
        for h in range(4):
            kb.op('pe', lambda E: E.matmul(p[:, h, :], lhsT=lf(h), rhs=rf(h), start=True, stop=True), r=rk, w=[p])

    si = 0
    for t in range(NT):
        b = t % 2
        x_ = qkv[b]
        s_ = sm[b]
        kb.dma('sp', x_[:].rearrange("p a h d -> p (a h d)"), io['qkv_tm'][t * 128:(t + 1) * 128, :], r=['qkv_tm'], w=[x_])
        kb.dma('sp', ab[b][:], io['tm'][t * 128:(t + 1) * 128, 0:8], r=['tm'], w=[ab[b]])
        kb.dma('sp', gt[b][:], io['tm'][t * 128:(t + 1) * 128, GATE_OFF:GATE_OFF + 512], r=['tm'], w=[gt[b]])
        g = s_[:, 0:4]
        kb.op('dve', lambda E: E.tensor_tensor(out=g, in0=ab[b][:, 0:4], in1=dtb[:], op=ALU.add), r=[ab[b], dtb], w=[s_])
        kb.op('act', lambda E: E.activation(out=g, in_=g, func=AF.Exp), r=[s_], w=[s_])
        kb.op('act', lambda E: E.activation(out=g, in_=g, func=AF.Ln, bias=1.0), r=[s_], w=[s_])
        kb.op('dve', lambda E: E.tensor_tensor(out=g, in0=g, in1=nA[:], op=ALU.mult), r=[s_, nA], w=[s_])
        kb.op('dve', lambda E: E.tensor_scalar(out=s_[:, 4:8], in0=g, scalar1=-1.0, scalar2=None, op0=ALU.mult), r=[s_], w=[s_])
        kb.op('act', lambda E: E.activation(out=s_[:, 8:12], in_=ab[b][:, 4:8], func=AF.Exp, scale=-1.0), r=[ab[b], s_], w=[s_])
        kb.op('dve', lambda E: E.tensor_scalar(out=s_[:, 8:12], in0=s_[:, 8:12], scalar1=1.0, scalar2=None, op0=ALU.add), r=[s_], w=[s_])
        kb.op('dve', lambda E: E.reciprocal(out=s_[:, 8:12], in_=s_[:, 8:12]), r=[s_], w=[s_])
        kb.op('dve', lambda E: E.tensor_scalar(out=s_[:, 12:16], in0=s_[:, 8:12], scalar1=-1.0, scalar2=None, op0=ALU.mult), r=[s_], w=[s_])
        for j in range(2):
            kb.op('dve', lambda E: E.tensor_scalar(out=gm[:, 4 * j:4 * j + 4], in0=g, scalar1=mch[:, j:j + 1], scalar2=None, op0=ALU.mult),
                  r=[s_, mch], w=[gm])
        kb.op('pe', lambda E: E.matmul(pS[:, 0:4], lhsT=btri[:], rhs=g, start=True, stop=True), r=[btri, s_], w=[pS])
        kb.op('pe', lambda E: E.matmul(pS[:, 4:8], lhsT=bones[:], rhs=g, start=True, stop=True), r=[bones, s_], w=[pS])
        kb.op('pe', lambda E: E.matmul(pS[:, 8:16], lhsT=ones[:], rhs=gm[:], start=True, stop=True), r=[ones, gm], w=[pS])
        G = gs[b]
        kb.op('dve', lambda E: E.tensor_copy(out=G[:], in_=pS[:, 0:16]), r=[pS], w=[G])
        kb.op('act', lambda E: E.activation(out=s_[:, 16:20], in_=G[:, 0:4], func=AF.Exp), r=[G, s_], w=[s_])
        kb.op('dve', lambda E: E.tensor_tensor(out=G[:, 4:8], in0=G[:, 4:8], in1=G[:, 0:4], op=ALU.subtract), r=[G], w=[G])
        kb.op('act', lambda E: E.activation(out=s_[:, 20:24], in_=G[:, 4:8], func=AF.Exp), r=[G, s_], w=[s_])
        kb.op('act', lambda E: E.activation(out=s_[:, 56:64], in_=G[:, 8:16], func=AF.Exp), r=[G, s_], w=[s_])
        kb.op('pool', lambda E: E.tensor_copy(out=R1[:], in_=B4(g)), r=[s_], w=[R1])
        kb.op('pool', lambda E: E.tensor_tensor(out=R2[:], in0=M4(btri[:]), in1=B4(s_[:, 4:8]), op=ALU.mult), r=[s_, btri], w=[R2])
        kb.op('pe', lambda E: E.matmul(pD[:].rearrange("p a b -> p (a b)"), lhsT=btri[:], rhs=R1[:].rearrange("p a b -> p (a b)"),
                                       start=True, stop=False), r=[btri, R1], w=[pD])
        kb.op('pe', lambda E: E.matmul(pD[:].rearrange("p a b -> p (a b)"), lhsT=bones[:], rhs=R2[:].rearrange("p a b -> p (a b)"),
                                       start=False, stop=True), r=[bones, R2], w=[pD])
        kb.op('dve', lambda E: E.tensor_tensor(out=tmpA[:], in0=pD[:], in1=mlow[:], op=ALU.add), r=[pD, mlow], w=[tmpA])
        kb.op('act', lambda E: E.activation(out=dec[:], in_=tmpA[:], func=AF.Exp), r=[tmpA], w=[dec])
        kb.op('dve', lambda E: E.scalar_tensor_tensor(out=tmpB[:], in0=pD[:], scalar=-1.0, in1=mup[:], op0=ALU.mult, op1=ALU.add),
              r=[pD, mup], w=[tmpB])
        kb.op('act', lambda E: E.activation(out=decT[:], in_=tmpB[:], func=AF.Exp), r=[tmpB], w=[decT])
        for a_, c0 in ((0, 24), (1, 28)):
            kb.op('pool', lambda E: E.tensor_tensor(out=sq[:], in0=x_[:, a_, :, :], in1=x_[:, a_, :, :], op=ALU.mult), r=[x_], w=[sq])
            kb.op('dve', lambda E: E.tensor_reduce(out=s_[:, c0:c0 + 4], in_=sq[:], axis=AX.X, op=ALU.add), r=[sq, s_], w=[s_])
            kb.op('act', lambda E: E.activation(out=s_[:, c0:c0 + 4], in_=s_[:, c0:c0 + 4], func=AF.Sqrt, bias=EPS), r=[s_], w=[s_])
            kb.op('dve', lambda E: E.reciprocal(out=s_[:, c0:c0 + 4], in_=s_[:, c0:c0 + 4]), r=[s_], w=[s_])
        sc_ = lambda o, a, bb: kb.op('dve', lambda E: E.tensor_tensor(out=s_[:, o:o + 4], in0=a, in1=bb, op=ALU.mult), r=[s_, mch], w=[s_])
        kb.op('dve', lambda E: E.tensor_scalar(out=s_[:, 32:36], in0=s_[:, 24:28], scalar1=128 ** -0.5, scalar2=None, op0=ALU.mult), r=[s_], w=[s_])
        sc_(36, s_[:, 32:36], s_[:, 16:20])
        kb.op('dve', lambda E: E.tensor_scalar(out=s_[:, 40:44], in0=s_[:, 36:40], scalar1=mch[:, 1:2], scalar2=None, op0=ALU.mult), r=[s_, mch], w=[s_])
        kb.op('dve', lambda E: E.tensor_scalar(out=s_[:, 36:40], in0=s_[:, 36:40], scalar1=mch[:, 0:1], scalar2=None, op0=ALU.mult), r=[s_, mch], w=[s_])
        sc_(44, s_[:, 28:32], s_[:, 8:12])
        sc_(44, s_[:, 44:48], s_[:, 16:20])
        sc_(48, s_[:, 28:32], s_[:, 20:24])
        kb.op('dve', lambda E: E.tensor_scalar(out=s_[:, 52:56], in0=s_[:, 48:52], scalar1=mch[:, 1:2], scalar2=None, op0=ALU.mult), r=[s_, mch], w=[s_])
        kb.op('dve', lambda E: E.tensor_scalar(out=s_[:, 48:52], in0=s_[:, 48:52], scalar1=mch[:, 0:1], scalar2=None, op0=ALU.mult), r=[s_, mch], w=[s_])
        kx, vx, qx = x_[:, 1, :, :], x_[:, 2, :, :], x_[:, 0, :, :]
        kb.op('pool', lambda E: E.tensor_tensor(out=KBG[:], in0=kx, in1=B4(s_[:, 44:48]), op=ALU.mult), r=[x_, s_], w=[KBG])
        kb.op('pool', lambda E: E.tensor_tensor(out=Kd[0][:], in0=kx, in1=B4(s_[:, 48:52]), op=ALU.mult), r=[x_, s_], w=[Kd[0]])
        kb.op('pool', lambda E: E.tensor_tensor(out=Kd[1][:], in0=kx, in1=B4(s_[:, 52:56]), op=ALU.mult), r=[x_, s_], w=[Kd[1]])
        kb.op('pool', lambda E: E.tensor_tensor(out=VB[:], in0=vx, in1=B4(s_[:, 8:12]), op=ALU.mult), r=[x_, s_], w=[VB])
        for i_, c0 in enumerate((32, 36, 40, 28)):
            kb.op('dve', lambda E: E.tensor_tensor(out=dg[i_][:], in0=M4(ident[:]), in1=B4(s_[:, c0:c0 + 4]), op=ALU.mult), r=[ident, s_], w=[dg[i_]])
        for i_, (src, dst) in enumerate(((qx, QT), (qx, QG[0]), (qx, QG[1]), (kx, KT))):
            p = PA()
            mm4(p, lambda h: src[:, h, :], lambda h: dg[i_][:, h, :], [x_, dg[i_]])
            if i_ % 2 == 0:
                kb.op('act', lambda E: E.copy(out=dst[:], in_=p[:]), r=[p], w=[dst])
            else:
                kb.op('dve', lambda E: E.tensor_copy(out=dst[:], in_=p[:]), r=[p], w=[dst])
        p = PA()
        mm4(p, lambda h: KT[:, h, :], lambda h: KT[:, h, :], [KT])
        kb.op('dve', lambda E: E.tensor_tensor(out=tmpA[:], in0=p[:], in1=dec[:], op=ALU.mult), r=[p, dec], w=[tmpA])
        kb.op('pool', lambda E: E.tensor_tensor(out=nbs[:], in0=M4(strict[:]), in1=B4(s_[:, 12:16]), op=ALU.mult), r=[strict, s_], w=[nbs])
        X, Y, P = Xb[0], Yb[0], Pb[0]
        kb.op('pool', lambda E: E.tensor_tensor(out=X[:], in0=tmpA[:], in1=nbs[:], op=ALU.mult), r=[tmpA, nbs], w=[X])
        for h in range(4):
            kb.op('pe', lambda E: E.transpose(out=pB[0][:, h, :], in_=X[:, h, :], identity=ident[:]), r=[X, ident], w=[pB[0]])
        kb.op('act', lambda E: E.copy(out=Y[:], in_=pB[0][:, 0:4, :]), r=[pB[0]], w=[Y])
        kb.op('dve', lambda E: E.tensor_tensor(out=P[:], in0=pB[0][:, 0:4, :], in1=M4(ident[:]), op=ALU.add), r=[pB[0], ident], w=[P])
        p = PA()
        mm4(p, lambda h: KT[:, h, :], lambda h: QT[:, h, :], [KT, QT])
        kb.op('dve', lambda E: E.tensor_tensor(out=aqkT[:], in0=p[:], in1=decT[:], op=ALU.mult), r=[p, decT], w=[aqkT])
        for k_ in range(1, 6):
            Xn, Yn, Pn = Xb[k_ % 2], Yb[k_ % 2], Pb[k_ % 2]
            p = PA()
            mm4(p, lambda h: Y[:, h, :], lambda h: X[:, h, :], [X, Y])
            kb.op('act', lambda E: E.copy(out=Xn[:], in_=p[:]), r=[p], w=[Xn])
            if k_ < 5:
                p2 = PA()
                mm4(p2, lambda h: X[:, h, :], lambda h: Y[:, h, :], [X, Y])
                kb.op('dve', lambda E: E.tensor_copy(out=Yn[:], in_=p2[:]), r=[p2], w=[Yn])
            p3 = PA()
            mm4(p3, lambda h: Xn[:, h, :], lambda h: P[:, h, :], [Xn, P])
            kb.op('dve', lambda E: E.tensor_tensor(out=Pn[:], in0=p3[:], in1=P[:], op=ALU.add), r=[p3, P], w=[Pn])
            X, Y, P = Xn, Yn, Pn
        p = PA()
        mm4(p, lambda h: KBG[:, h, :], lambda h: P[:, h, :], [KBG, P])
        kb.op('act', lambda E: E.mul(out=negWT[:], in_=p[:], mul=-1.0), r=[p], w=[negWT])
        Sa = Sbf[si % 3]
        Sb_ = Sbf[(si + 1) % 3]
        Sc = Sbf[(si + 2) % 3]
        si += 2
        for j, (Scur, Snext) in enumerate(((Sa, Sb_), (Sb_, Sc))):
            for h in range(4):
                kb.op('pe', lambda E: E.matmul(pV[:, h, :], lhsT=P[:, h, :], rhs=VB[:, h, :], start=True, stop=False), r=[P, VB], w=[pV])
                kb.op('pe', lambda E: E.matmul(pV[:, h, :], lhsT=negWT[:, h, :], rhs=Scur[:, h, :], start=False, stop=True),
                      r=[negWT, Scur], w=[pV])
            r0 = 64 * j
            kb.op('act', lambda E: E.copy(out=vnew[r0:r0 + 64, :, :], in_=pV[r0:r0 + 64, :, :]), r=[pV], w=[vnew])
            mm4(pdS, lambda h: Kd[j][:, h, :], lambda h: vnew[:, h, :], [Kd[j], vnew])
            for h in range(4):
                kb.op('dve', lambda E: E.scalar_tensor_tensor(out=Sf[:, h, :], in0=Sf[:, h, :], scalar=s_[:, 56 + 4 * j + h:57 + 4 * j + h],
                                                              in1=pdS[:, h, :], op0=ALU.mult, op1=ALU.add), r=[Sf, s_, pdS], w=[Sf])
            kb.op('act', lambda E: E.copy(out=Snext[:], in_=Sf[:]), r=[Sf], w=[Snext])
        for h in range(4):
            kb.op('pe', lambda E: E.matmul(pO[:, h, :], lhsT=QG[0][:, h, :], rhs=Sa[:, h, :], start=True, stop=False), r=[QG[0], Sa], w=[pO])
            kb.op('pe', lambda E: E.matmul(pO[:, h, :], lhsT=QG[1][:, h, :], rhs=Sb_[:, h, :], start=False, stop=False), r=[QG[1], Sb_], w=[pO])
            kb.op('pe', lambda E: E.matmul(pO[:, h, :], lhsT=aqkT[:, h, :], rhs=vnew[:, h, :], start=False, stop=True), r=[aqkT, vnew], w=[pO])
        kb.op('act', lambda E: E.copy(out=osb[:], in_=pO[:]), r=[pO], w=[osb])
        kb.op('pool', lambda E: E.tensor_tensor(out=sq[:], in0=osb[:], in1=osb[:], op=ALU.mult), r=[osb], w=[sq])
        kb.op('dve', lambda E: E.tensor_reduce(out=G[:, 0:4], in_=sq[:], axis=AX.X, op=ALU.add), r=[sq, G], w=[G])
        kb.op('act', lambda E: E.activation(out=G[:, 0:4], in_=G[:, 0:4], func=AF.Sqrt, scale=1.0 / 128, bias=EPS), r=[G], w=[G])
        kb.op('dve', lambda E: E.reciprocal(out=G[:, 0:4], in_=G[:, 0:4]), r=[G], w=[G])
        kb.op('act', lambda E: E.activation(out=sgt[:], in_=gt[b][:], func=AF.Silu), r=[gt[b]], w=[sgt])
        kb.op('dve', lambda E: E.tensor_tensor(out=osb[:], in0=osb[:], in1=B4(G[:, 0:4]), op=ALU.mult), r=[osb, G], w=[osb])
        kb.op('pool', lambda E: E.tensor_tensor(out=osb[:], in0=osb[:], in1=M4(gw[:]), op=ALU.mult), r=[osb, gw], w=[osb])
        kb.op('dve', lambda E: E.tensor_tensor(out=oa[b][:], in0=osb[:], in1=sgt[:].rearrange("p (h d) -> p h d", h=4), op=ALU.mult),
              r=[osb, sgt], w=[oa[b]])
        for h in range(4):
            kb.op('pe', lambda E: E.transpose(out=pB[0][:, h, :], in_=oa[b][:, h, :], identity=ident[:]), r=[oa[b], ident], w=[pB[0]])
        kb.op('act', lambda E: E.copy(out=oaT[b][:], in_=pB[0][:, 0:4, :]), r=[pB[0]], w=[oaT[b]])
        kb.dma('sp', io['ocatT'][0:512, t * 128:(t + 1) * 128].rearrange("(c p) t -> p c t", p=128), oaT[b][:], r=[oaT[b]], w=['ocatT_a'])
    ph.close()


def rope16(kb, R, G, cs, tmp, e1='dve', e2='pool'):
    c = cs[:, 0:8].unsqueeze(1).to_broadcast([128, G, 8])
    sn = cs[:, 8:16].unsqueeze(1).to_broadcast([128, G, 8])
    x1 = R[:, 0:G, 0:8]
    x2 = R[:, 0:G, 8:16]
    kb.op(e1, lambda E: E.tensor_tensor(out=tmp[:, 0:G, 0:8], in0=x1, in1=c, op=ALU.mult), r=[R, cs], w=[(tmp, 0)])
    kb.op(e2, lambda E: E.tensor_tensor(out=tmp[:, 0:G, 8:16], in0=x2, in1=sn, op=ALU.mult), r=[R, cs], w=[(tmp, 1)])
    kb.op(e1, lambda E: E.tensor_tensor(out=tmp[:, 0:G, 16:24], in0=x2, in1=c, op=ALU.mult), r=[R, cs], w=[(tmp, 2)])
    kb.op(e2, lambda E: E.tensor_tensor(out=tmp[:, 0:G, 24:32], in0=x1, in1=sn, op=ALU.mult), r=[R, cs], w=[(tmp, 3)])
    kb.op(e1, lambda E: E.tensor_tensor(out=x1, in0=tmp[:, 0:G, 0:8], in1=tmp[:, 0:G, 8:16], op=ALU.subtract),
          r=[(tmp, 0), (tmp, 1), (tmp, 2), (tmp, 3)], w=[R])
    kb.op(e1, lambda E: E.tensor_tensor(out=x2, in0=tmp[:, 0:G, 16:24], in1=tmp[:, 0:G, 24:32], op=ALU.add),
          r=[(tmp, 0), (tmp, 1), (tmp, 2), (tmp, 3)], w=[R])


def rms_groups(kb, src3, G, dst3, sq, ss, wt, e_sq='pool'):
    (src_ap, src_keys) = src3
    (dst_ap, dst_keys) = dst3
    kb.op(e_sq, lambda E: E.tensor_tensor(out=sq[:, 0:G, :], in0=src_ap, in1=src_ap, op=ALU.mult), r=src_keys, w=[sq])
    kb.op('dve', lambda E: E.tensor_reduce(out=ss[:, 0:G], in_=sq[:, 0:G, :], axis=AX.X, op=ALU.add), r=[sq], w=[ss])
    kb.op('act', lambda E: E.activation(out=ss[:, 0:G], in_=ss[:, 0:G], func=AF.Sqrt, scale=1.0 / 64, bias=EPS), r=[ss], w=[ss])
    kb.op('dve', lambda E: E.reciprocal(out=ss[:, 0:G], in_=ss[:, 0:G]), r=[ss], w=[ss])
    kb.op('dve', lambda E: E.tensor_tensor(out=dst_ap, in0=src_ap, in1=ss[:, 0:G].unsqueeze(2).to_broadcast([128, G, 64]), op=ALU.mult),
          r=src_keys + [ss], w=dst_keys)
    kb.op('pool', lambda E: E.tensor_tensor(out=dst_ap, in0=dst_ap, in1=wt[:].unsqueeze(1).to_broadcast([128, G, 64]), op=ALU.mult),
          r=dst_keys + [wt], w=dst_keys)


def phase_nsa(kb, io):
    nc = kb.nc
    ph = Phase(kb, "ns")
    ident = ph.sb("ident", [128, 128], BF16)
    tril = ph.sb("tril", [128, 128], BF16)
    far = ph.sb("far", [128, 128], BF16)
    cmask = ph.sb("cmask", [128, 9, 512], BF16)
    kvT = ph.sb("kvT", [128, 4, S], BF16)
    vs1 = ph.sb("vs1", [128, NT, 2, 65], BF16)
    vw1 = ph.sb("vw1", [128, NT, 2, 65], BF16)
    kcmpT = ph.sb("kcmpT", [64, 2, 256], BF16)
    rhs_cmp = ph.sb("rhs_cmp", [128, 2, 2, 128], BF16)
    kb.dma('sp', ident[:], io['c_ident'][:, :], w=[ident])
    kb.dma('sp', tril[:], io['c_tril'][:, :], w=[tril])
    kb.dma('sp', far[:], io['c_far'][:, :], w=[far])
    for m in range(9):
        kb.dma('sp', cmask[:, m, :], io['c_cmask'][m, :, :], w=[cmask])
    for g in range(2):
        kb.dma('sp', kvT[64:128, g, :], io['c_E'][:, :], w=[(kvT, 'E')])
    kb.op('pool', lambda E: E.memset(vs1[:], 1.0), w=[vs1])
    kb.op('pool', lambda E: E.memset(vw1[:], 1.0), w=[vw1])
    kb.op('pool', lambda E: E.memset(kcmpT[:], 0.0), w=[kcmpT])
    kb.op('pool', lambda E: E.memset(rhs_cmp[:], 0.0), w=[rhs_cmp])
    for bt in range(2):
        for g in range(2):
            kb.dma('sp', rhs_cmp[:, bt, g, 64:128], io['c_ovl'][:, bt, :], r=[rhs_cmp], w=[rhs_cmp])

    pp = Phase(kb, "np")
    kcT = pp.sb("kcT", [64, 4, S], BF16)
    ksw = pp.sb("ksw", [128, 64], F32)
    kww = pp.sb("kww", [128, 64], F32)
    kcw = pp.sb("kcw", [128, 64], F32)
    kb.dma('sp', ksw[:], io['nsa_ks_norm_w'].partition_broadcast(128), w=[ksw])
    kb.dma('sp', kww[:], io['nsa_kw_norm_w'].partition_broadcast(128), w=[kww])
    kb.dma('sp', kcw[:], io['nsa_kc_norm_w'].partition_broadcast(128), w=[kcw])
    kvb = pp.sbn("kvb", [128, 768], F32, 2)
    cst = pp.sbn("cst", [128, 16], F32, 2)
    R = pp.sbn("R", [128, 6, 64], F32, 2)
    sq = pp.sb("sq", [128, 2, 64], F32)
    ss = pp.sb("ss", [128, 2], F32)
    tmp = pp.sb("tmp", [128, 6, 32], F32)
    k16 = pp.sbn("k16", [128, 8, 64], BF16, 2)
    pT8 = pp.psn("pT8", [128, 8, 128], BF16, 2)
    for t in range(NT):
        b = t % 2
        kv = kvb[b]
        Rb = R[b]
        kb.dma('sp', kv[:], io['tm'][t * 128:(t + 1) * 128, KC_OFF:KC_OFF + 768], r=['tm'], w=[kv])
        kb.dma('sp', cst[b][:], io['c_rope'][t * 128:(t + 1) * 128, :], w=[cst[b]])
        v3 = lambda off: kv[:, off:off + 128].rearrange("p (g d) -> p g d", g=2)
        kb.op('pool', lambda E: E.tensor_copy(out=Rb[:, 0:2, :], in_=v3(0)), r=[kv], w=[Rb])
        rms_groups(kb, (v3(256), [kv]), 2, (Rb[:, 2:4, :], [Rb]), sq, ss, ksw)
        rms_groups(kb, (v3(512), [kv]), 2, (Rb[:, 4:6, :], [Rb]), sq, ss, kww)
        rope16(kb, Rb, 6, cst[b], tmp)
        kk = k16[b]
        kb.op('act', lambda E: E.copy(out=kk[:, 0:6, :], in_=Rb[:]), r=[Rb], w=[kk])
        kb.op('pool', lambda E: E.tensor_copy(out=kk[:, 6:8, :], in_=v3(128)), r=[kv], w=[kk])
        kb.op('dve', lambda E: E.tensor_copy(out=vs1[:, t, :, 0:64], in_=v3(384)), r=[kv], w=[vs1])
        kb.op('pool', lambda E: E.tensor_copy(out=vw1[:, t, :, 0:64], in_=v3(640)), r=[kv], w=[vw1])
        p8 = pT8[b]
        for i in range(8):
            kb.op('pe', lambda E: E.transpose(out=p8[0:64, i, :], in_=kk[:, i, :], identity=ident[:]), r=[kk, ident], w=[p8])
        kb.op('act', lambda E: E.copy(out=kvT[0:64, :, t * 128:(t + 1) * 128], in_=p8[0:64, 2:6, :]), r=[p8], w=[(kvT, 'k')])
        kb.op('dve', lambda E: E.tensor_copy(out=kcT[0:64, 0:2, t * 128:(t + 1) * 128], in_=p8[0:64, 0:2, :]), r=[p8], w=[kcT])
        kb.op('dve', lambda E: E.tensor_copy(out=kcT[0:64, 2:4, t * 128:(t + 1) * 128], in_=p8[0:64, 6:8, :]), r=[p8], w=[kcT])
    w1f = pp.sb("w1f", [64, 32, 64], F32)
    w1b = pp.sbn("w1b", [64, 32, 64], BF16, 2)
    w2f = pp.sb("w2f", [64, 64], F32)
    w2b = pp.sbn("w2b", [64, 64], BF16, 2)
    posf = pp.sb("posf", [64, 32], F32)
    pos2 = pp.sbn("pos2", [64, 32, 2], BF16, 2)
    bias = pp.sb("bias", [64, 2], F32)
    h1T = pp.sb("h1T", [64, 256], BF16)
    o2 = pp.sb("o2", [128, 1, 64], F32)
    o2n = pp.sb("o2n", [128, 1, 64], F32)
    kcn = pp.sb("kcn", [128, 64], BF16)
    pH = pp.ps("pH", [128, 512], F32)
    pB_ = pp.ps("pBi", [128, 512], F32)
    pO2 = pp.ps("pO2", [128, 512], F32)
    kb.op('pool', lambda E: E.memset(h1T[:], 0.0), w=[h1T])
    for kind, (n1, n2, npos) in enumerate((('nsa_cmp_k_w1', 'nsa_cmp_k_w2', 'nsa_cmp_pos_k'), ('nsa_cmp_v_w1', 'nsa_cmp_v_w2', 'nsa_cmp_pos_v'))):
        kb.dma('sp', w1f[:], io[n1].rearrange("(l d) o -> d l o", d=64), w=[w1f])
        kb.op('dve', lambda E: E.tensor_copy(out=w1b[kind][:], in_=w1f[:]), r=[w1f], w=[w1b[kind]])
        kb.dma('sp', w2f[:], io[n2][:, :], w=[w2f])
        kb.op('dve', lambda E: E.tensor_copy(out=w2b[kind][:], in_=w2f[:]), r=[w2f], w=[w2b[kind]])
        kb.dma('sp', posf[:], io[npos].rearrange("l d -> d l"), w=[posf], allow_slow_non_contiguous=True)
        for j in range(2):
            kb.op('dve', lambda E: E.tensor_copy(out=pos2[kind][:, :, j], in_=posf[:]), r=[posf], w=[pos2[kind]])
        for l in range(32):
            kb.op('pe', lambda E: E.matmul(pB_[0:64, 0:2], lhsT=w1b[kind][:, l, :], rhs=pos2[kind][:, l, :], start=(l == 0), stop=(l == 31)),
                  r=[w1b[kind], pos2[kind]], w=[pB_])
        kb.op('dve', lambda E: E.tensor_copy(out=bias[:], in_=pB_[0:64, 0:2]), r=[pB_], w=[bias])
        for g in range(2):
            ki = kind * 2 + g
            for l in range(32):
                kb.op('pe', lambda E: E.matmul(pH[0:64, 0:255], lhsT=w1b[kind][:, l, :], rhs=kcT[0:64, ki, l:l + 16 * 254 + 1:16],
                                               start=(l == 0), stop=(l == 31)), r=[w1b[kind], kcT], w=[pH])
            kb.op('act', lambda E: E.activation(out=h1T[:, 0:255], in_=pH[0:64, 0:255], func=AF.Silu, bias=bias[:, 0:1]),
                  r=[pH, bias], w=[h1T])
            for bt in range(2):
                kb.op('pe', lambda E: E.matmul(pO2[:, 0:64], lhsT=h1T[:, bt * 128:(bt + 1) * 128], rhs=w2b[kind][:], start=True, stop=True),
                      r=[h1T, w2b[kind]], w=[pO2])
                if kind == 0:
                    kb.op('act', lambda E: E.copy(out=o2[:, 0, :], in_=pO2[:, 0:64]), r=[pO2], w=[o2])
                    rms_groups(kb, (o2[:], [o2]), 1, (o2n[:], [o2n]), sq, ss, kcw)
                    kb.op('act', lambda E: E.copy(out=kcn[:], in_=o2n[:, 0, :]), r=[o2n], w=[kcn])
                    p8 = pT8[0]
                    kb.op('pe', lambda E: E.transpose(out=p8[0:64, 0, :], in_=kcn[:], identity=ident[:]), r=[kcn, ident], w=[p8])
                    kb.op('act', lambda E: E.copy(out=kcmpT[:, g, bt * 128:(bt + 1) * 128], in_=p8[0:64, 0, :]), r=[p8], w=[kcmpT])
                else:
                    kb.op('act', lambda E: E.copy(out=rhs_cmp[:, bt, g, 0:64], in_=pO2[:, 0:64]), r=[pO2], w=[rhs_cmp])
    pp.close()

    pa = Phase(kb, "na")
    NPT = 36
    PTb = pa.sbn("PT", [128, 512], BF16, NPT)
    pti = [0]
    qaug = pa.sbn("qaug", [128, 8, 512], BF16, 2)
    qnw = pa.sb("qnw", [128, 64], F32)
    kb.dma('sp', qnw[:], io['nsa_q_norm_w'].partition_broadcast(128), w=[qnw])
    kb.op('dve', lambda E: E.tensor_scalar(out=qnw[:], in0=qnw[:], scalar1=0.125, scalar2=None, op0=ALU.mult), r=[qnw], w=[qnw])
    qf = pa.sbn("qf", [128, 512], F32, 2)
    cst = pa.sbn("cst", [128, 16], F32, 2)
    Rq = pa.sb("Rq", [128, 8, 64], F32)
    sq = pa.sb("sq", [128, 8, 64], F32)
    ss = pa.sb("ss", [128, 8], F32)
    tmp = pa.sb("tmp", [128, 8, 32], F32)
    qa = pa.sbn("qa", [128, 8, 128], BF16, 2)
    gts = pa.sbn("gts", [128, 24], F32, 4)
    Ab = pa.sbn("Ab", [128, 64], F32, 4)
    Bb = pa.sbn("Bb", [128, 64], F32, 4)
    ob = pa.sbn("ob", [128, 8, 64], F32, 4)
    impacc = pa.sb("impacc", [128, 4, 64], F32)
    scr = pa.sb("scr", [128, 64], F32)
    scr2 = pa.sb("scr2", [128, 64], F32)
    m8 = pa.sb("m8", [128, 16], F32)
    nst = pa.sbn("nst", [128, 128], BF16, 2)
    fs = pa.sb("fs", [128, 4], F32)
    obb = pa.sbn("obb", [128, 512], BF16, 2)
    obT = pa.sbn("obT", [128, 4, 128], BF16, 2)
    sc_ps = pa.psn("sc", [128, 512], F32, 2)
    oC = pa.ps("oC", [128, 4, 128], F32)
    oW = pa.ps("oW", [128, 4, 128], F32)
    oS = pa.ps("oS", [128, 4, 128], F32)
    pTr = pa.psn("pTr", [128, 8, 128], BF16, 2)
    for i in range(2):
        kb.op('pool', lambda E: E.memset(qa[i][:], 0.0), w=[qa[i]])
        kb.op('pool', lambda E: E.memset(nst[i][:], 0.0), w=[nst[i]])
    tri = 0

    def finalize(oX, h, br, first, cmp=False):
        for qt in range(4):
            if cmp:
                kb.op('dve', lambda E: E.tensor_reduce(out=fs[:, 0:1], in_=oX[:, qt, 64:128], axis=AX.X, op=ALU.add), r=[oX], w=[fs])
                kb.op('dve', lambda E: E.tensor_scalar(out=fs[:, 0:1], in0=fs[:, 0:1], scalar1=0.5, scalar2=1e-30, op0=ALU.mult, op1=ALU.add),
                      r=[fs], w=[fs])
            else:
                kb.op('dve', lambda E: E.tensor_scalar(out=fs[:, 0:1], in0=oX[:, qt, 64:65], scalar1=1e-30, scalar2=None, op0=ALU.add),
                      r=[oX], w=[fs])
            kb.op('dve', lambda E: E.reciprocal(out=fs[:, 1:2], in_=fs[:, 0:1]), r=[fs], w=[fs])
            if cmp:
                if h % 4 == 0:
                    kb.op('dve', lambda E: E.tensor_scalar(out=impacc[:, qt, :], in0=oX[:, qt, 64:128], scalar1=fs[:, 1:2], scalar2=None,
                                                           op0=ALU.mult), r=[oX, fs], w=[(impacc, qt)])
                else:
                    kb.op('dve', lambda E: E.scalar_tensor_tensor(out=impacc[:, qt, :], in0=oX[:, qt, 64:128], scalar=fs[:, 1:2],
                                                                  in1=impacc[:, qt, :], op0=ALU.mult, op1=ALU.add),
                          r=[oX, fs, (impacc, qt)], w=[(impacc, qt)])
            kb.op('dve', lambda E: E.tensor_tensor(out=fs[:, 2:3], in0=fs[:, 1:2], in1=gts[qt][:, h * 3 + br:h * 3 + br + 1], op=ALU.mult),
                  r=[fs, gts[qt]], w=[fs])
            if first:
                kb.op('dve', lambda E: E.tensor_scalar(out=ob[qt][:, h, :], in0=oX[:, qt, 0:64], scalar1=fs[:, 2:3], scalar2=None, op0=ALU.mult),
                      r=[oX, fs], w=[(ob[qt], h)])
            else:
                kb.op('dve', lambda E: E.scalar_tensor_tensor(out=ob[qt][:, h, :], in0=oX[:, qt, 0:64], scalar=fs[:, 2:3], in1=ob[qt][:, h, :],
                                                              op0=ALU.mult, op1=ALU.add), r=[oX, fs, (ob[qt], h)], w=[(ob[qt], h)])

    for s in range(S // 512):
        qs_ = qaug[s % 2]
        for t4 in range(4):
            t = 4 * s + t4
            b = t % 2
            q = qf[b]
            kb.dma('sp', q[:], io['tm'][t * 128:(t + 1) * 128, NQ_OFF:NQ_OFF + 512], r=['tm'], w=[q])
            kb.dma('sp', gts[t4][:], io['tm'][t * 128:(t + 1) * 128, NG_OFF:NG_OFF + 24], r=['tm'], w=[gts[t4]])
            kb.dma('sp', cst[b][:], io['c_rope'][t * 128:(t + 1) * 128, :], w=[cst[b]])
            kb.dma('sp', Ab[t4][:], io['c_A'][t, :, :], w=[Ab[t4]])
            kb.dma('sp', Bb[t4][:], io['c_B'][t, :, :], w=[Bb[t4]])
            kb.op('act', lambda E: E.activation(out=gts[t4][:], in_=gts[t4][:], func=AF.Exp, scale=-1.0), r=[gts[t4]], w=[gts[t4]])
            kb.op('dve', lambda E: E.tensor_scalar(out=gts[t4][:], in0=gts[t4][:], scalar1=1.0, scalar2=None, op0=ALU.add), r=[gts[t4]], w=[gts[t4]])
            kb.op('dve', lambda E: E.reciprocal(out=gts[t4][:], in_=gts[t4][:]), r=[gts[t4]], w=[gts[t4]])
            q3 = q[:].rearrange("p (h d) -> p h d", h=8)
            rms_groups(kb, (q3, [q]), 8, (Rq[:], [Rq]), sq, ss, qnw)
            rope16(kb, Rq, 8, cst[b], tmp)
            qab = qa[b]
            kb.op('act', lambda E: E.copy(out=qab[:, :, 0:64], in_=Rq[:]), r=[Rq], w=[qab])
            pt_ = pTr[tri % 2]
            tri += 1
            for h in range(8):
                kb.op('pe', lambda E: E.transpose(out=pt_[:, h, :], in_=qab[:, h, :], identity=ident[:]), r=[qab, ident], w=[pt_])
            kb.op('act', lambda E: E.copy(out=qs_[:, :, t4 * 128:(t4 + 1) * 128], in_=pt_[:]), r=[pt_], w=[qs_])
        nbt = 1 if s < 4 else 2
        for g in range(2):
            for h in range(4 * g, 4 * g + 4):
                specs = []
                for bt in range(nbt):
                    m = (s if s <= 4 else None) if bt == 0 else 5 + (s - 4)
                    masks = [] if m is None else [(0, 512, cmask[:, m, :])]
                    specs.append(dict(lhsT=kcmpT[0:64, g, bt * 128:(bt + 1) * 128],
                                      rhs_fn=lambda q0, q1, h=h: qs_[0:64, h, q0 * 128:q1 * 128], qt0=0, qt1=4, masks=masks, mk=[cmask],
                                      v_fn=lambda qt, bt=bt, g=g: rhs_cmp[:, bt, g, :], nk=128, rk=[kcmpT], rv=[rhs_cmp]))
                attn_block(kb, lambda qt: (oC[:, qt, :], oC), PTb, pti, sc_ps, specs, ident, [qs_])
                finalize(oC, h, 0, True, cmp=True)
            for qt in range(4):
                ns_ = nst[qt % 2]
                kb.op('dve', lambda E: E.tensor_tensor(out=scr[:], in0=impacc[:, qt, :], in1=Ab[qt][:], op=ALU.mult), r=[(impacc, qt), Ab[qt]], w=[scr])
                kb.op('dve', lambda E: E.tensor_tensor(out=scr[:], in0=scr[:], in1=Bb[qt][:], op=ALU.add), r=[scr, Bb[qt]], w=[scr])
                kb.op('dve', lambda E: E.max(out=m8[:, 0:8], in_=scr[:]), r=[scr], w=[(m8, 0)])
                kb.op('dve', lambda E: E.match_replace(out=scr2[:], in_to_replace=m8[:, 0:8], in_values=scr[:], imm_value=-1e30),
                      r=[scr, (m8, 0)], w=[scr2])
                kb.op('dve', lambda E: E.max(out=m8[:, 8:16], in_=scr2[:]), r=[scr2], w=[(m8, 1)])
                kb.op('dve', lambda E: E.tensor_scalar(out=ns_[:, 64:128], in0=scr[:], scalar1=m8[:, 15:16], scalar2=1.0, op0=ALU.is_ge,
                                                       op1=ALU.subtract), r=[scr, (m8, 1)], w=[ns_])
                pt_ = pTr[tri % 2]
                tri += 1
                kb.op('pe', lambda E: E.transpose(out=pt_[:, 0, :], in_=ns_[:], identity=ident[:]), r=[ns_, ident], w=[pt_])
                for h in range(4 * g, 4 * g + 4):
                    if h % 2 == 0:
                        kb.op('act', lambda E: E.copy(out=qs_[64:128, h, qt * 128:(qt + 1) * 128], in_=pt_[64:128, 0, :]), r=[pt_], w=[qs_])
                    else:
                        kb.op('dve', lambda E: E.tensor_copy(out=qs_[64:128, h, qt * 128:(qt + 1) * 128], in_=pt_[64:128, 0, :]), r=[pt_], w=[qs_])
        for h in range(8):
            g = h // 4
            specs = []
            for kt in range(max(0, 4 * s - 4), 4 * s + 4):
                lo = max(kt - 4 * s, 0)
                hi = min(kt + 4 - 4 * s, 3)
                masks = []
                if kt >= 4 * s:
                    masks.append(((kt - 4 * s - lo) * 128, 128, tril[:]))
                if kt + 4 <= 4 * s + 3:
                    masks.append(((kt + 4 - 4 * s - lo) * 128, 128, far[:]))
                specs.append(dict(lhsT=kvT[0:64, 2 + g, kt * 128:(kt + 1) * 128],
                                  rhs_fn=lambda q0, q1, h=h: qs_[0:64, h, q0 * 128:q1 * 128], qt0=lo, qt1=hi + 1, masks=masks, mk=[tril, far],
                                  v_fn=lambda qt, kt=kt, g=g: vw1[:, kt, g, :], nk=128, rk=[(kvT, 'k')], rv=[vw1]))
            attn_block(kb, lambda qt: (oW[:, qt, 0:65], oW), PTb, pti, sc_ps, specs, ident, [qs_])
            finalize(oW, h, 2, False)
            specs = []
            for kt in range(0, 4 * s + 4):
                lo = max(kt - 4 * s, 0)
                masks = [(0, 128, tril[:])] if kt >= 4 * s else []
                specs.append(dict(lhsT=kvT[:, g, kt * 128:(kt + 1) * 128],
                                  rhs_fn=lambda q0, q1, h=h: qs_[:, h, q0 * 128:q1 * 128], qt0=lo, qt1=4, masks=masks, mk=[tril],
                                  v_fn=lambda qt, kt=kt, g=g: vs1[:, kt, g, :], nk=128, rk=[(kvT, 'k'), (kvT, 'E')], rv=[vs1]))
            attn_block(kb, lambda qt: (oS[:, qt, 0:65], oS), PTb, pti, sc_ps, specs, ident, [qs_])
            finalize(oS, h, 1, False)
        for qt in range(4):
            t = 4 * s + qt
            b = t % 2
            kb.op('act', lambda E: E.copy(out=obb[b][:], in_=ob[qt][:].rearrange("p h d -> p (h d)")), r=[(ob[qt], h) for h in range(8)], w=[obb[b]])
            pt_ = pTr[tri % 2]
            tri += 1
            for c in range(4):
                kb.op('pe', lambda E: E.transpose(out=pt_[:, c, :], in_=obb[b][:, c * 128:(c + 1) * 128], identity=ident[:]), r=[obb[b], ident], w=[pt_])
            kb.op('act', lambda E: E.copy(out=obT[b][:], in_=pt_[:, 0:4, :]), r=[pt_], w=[obT[b]])
            kb.dma('sp', io['ocatT'][512:1024, t * 128:(t + 1) * 128].rearrange("(c p) t -> p c t", p=128), obT[b][:], r=[obT[b]], w=['ocatT_b'])
    pa.close()
    ph.close()


def phase_ffn(kb, io):
    nc = kb.nc
    ph = Phase(kb, "f1")
    ident = ph.sb("ident", [128, 128], BF16)
    kb.dma('sp', ident[:], io['c_ident'][:, :], w=[ident])
    woutb = ph.sb("woutb", [128, 12, D], BF16)
    wupb = ph.sb("wupb", [128, 8, 2 * FF], BF16)
    gam = ph.sb("gam", [128, 8], F32)
    cw = ph.sb("cw", [128, 3, 44], F32)
    kb.dma('sp', gam[:], io['ffn_norm_w'].rearrange("(c p) -> p c", p=128), w=[gam], allow_slow_non_contiguous=True)
    for j in range(3):
        kb.dma('sp', cw[:, j, :], io['ffn_conv_w'][j, :].rearrange("(c p) -> p c", p=128), w=[cw], allow_slow_non_contiguous=True)
    load_cast_weight(kb, ph, woutb, io['w_out'], 12, D)
    load_cast_weight(kb, ph, wupb, io['ffn_w_up'], 8, 2 * FF, gam=gam, stage_cols=1408)

    oT = ph.sbn("oT", [128, 12, 128], BF16, 2)
    xt = ph.sbn("xt", [128, D], F32, 2)
    hs = ph.sbn("hs", [128, D], F32, 2)
    junk = ph.sb("junk", [128, D], BF16)
    ss = ph.sbn("ss", [128, 1], F32, 2)
    rs = ph.sbn("rs", [128, 1], F32, 2)
    hn = ph.sbn("hn", [128, D], BF16, 2)
    hnT = ph.sbn("hnT", [128, 8, 512], BF16, 2)
    ug = ph.sbn("ug", [128, 514], F32, 2)
    uv = ph.sbn("uv", [128, 514], F32, 2)
    ag = ph.sbn("ag", [128, 512], F32, 2)
    av = ph.sbn("av", [128, 512], F32, 2)
    sg = ph.sbn("sg", [128, 512], F32, 2)
    act = ph.sbn("act", [128, 512], BF16, 3)
    halo = ph.sb("halo", [128, 44, 2], F32)
    pH = ph.psn("pH", [128, 512], F32, 2)
    pT = ph.ps("pT", [128, 8, 128], BF16)
    pU = ph.psn("pU", [128, 512], F32, 4)
    kb.op('pool', lambda E: E.memset(halo[:], 0.0), w=[halo])

    def conv3(ps_, dst, src, fb):
        kb.op('act', lambda E: E.activation(out=dst[:], in_=ps_[:], func=AF.Copy, scale=cw[:, 2, fb:fb + 1]), r=[ps_, cw], w=[dst])
        for j in range(2):
            kb.op('dve', lambda E: E.scalar_tensor_tensor(out=dst[:], in0=src[:, j:j + 512], scalar=cw[:, j, fb:fb + 1], in1=dst[:],
                                                       op0=ALU.mult, op1=ALU.add), r=[(src, 'b'), (src, 'h'), cw, dst], w=[dst])

    for s in range(S // 512):
        hT = hnT[s % 2]
        for t4 in range(4):
            t = s * 4 + t4
            b = t % 2
            kb.dma('sp', oT[b][:], io['ocatT'][:, t * 128:(t + 1) * 128].rearrange("(c p) t -> p c t", p=128),
                   r=['ocatT_a', 'ocatT_b', 'ocatT_c'], w=[oT[b]])
            kb.dma('sp', xt[b][:], io['x'][t * 128:(t + 1) * 128, :], w=[xt[b]])
            for half in range(2):
                p = pH[half]
                for c in range(12):
                    kb.op('pe', lambda E: E.matmul(p[:], lhsT=oT[b][:, c, :], rhs=woutb[:, c, half * 512:(half + 1) * 512],
                                                   start=(c == 0), stop=(c == 11)), r=[oT[b], (woutb, c)], w=[p])
                kb.op('dve', lambda E: E.tensor_tensor(out=hs[b][:, half * 512:(half + 1) * 512], in0=p[:],
                                                       in1=xt[b][:, half * 512:(half + 1) * 512], op=ALU.add),
                      r=[p, xt[b]], w=[(hs[b], half)])
            kb.dma('sp', io['h_s'][t * 128:(t + 1) * 128, :], hs[b][:], r=[(hs[b], 0), (hs[b], 1)], w=['h_s'])
            rms_rstd(kb, hs[b][:], junk, ss[b], rs[b], D, [(hs[b], 0), (hs[b], 1)])
            kb.op('act', lambda E: E.activation(out=hn[b][:], in_=hs[b][:], func=AF.Copy, scale=rs[b][:, 0:1]),
                  r=[(hs[b], 0), (hs[b], 1), rs[b]], w=[hn[b]])
            for c in range(8):
                kb.op('pe', lambda E: E.transpose(out=pT[:, c, :], in_=hn[b][:, c * 128:(c + 1) * 128], identity=ident[:]),
                      r=[hn[b], ident], w=[pT])
            kb.op('act', lambda E: E.copy(out=hT[:, :, t4 * 128:(t4 + 1) * 128], in_=pT[:]), r=[pT], w=[hT])
        for fb in range(22):
            k2 = fb % 2
            pg = pU[2 * k2]
            pv = pU[2 * k2 + 1]
            for (p_, f0) in ((pg, fb), (pv, 22 + fb)):
                for c in range(8):
                    kb.op('pe', lambda E: E.matmul(p_[:], lhsT=wupb[:, c, f0 * 128:(f0 + 1) * 128], rhs=hT[:, c, :],
                                                   start=(c == 0), stop=(c == 7)), r=[hT, (wupb, c)], w=[p_])
            g_, v_ = ug[k2], uv[k2]
            kb.op('act', lambda E: E.copy(out=g_[:, 2:514], in_=pg[:]), r=[pg], w=[(g_, 'b')])
            kb.op('act', lambda E: E.copy(out=v_[:, 2:514], in_=pv[:]), r=[pv], w=[(v_, 'b')])
            for (u_, f0) in ((g_, fb), (v_, 22 + fb)):
                kb.op('pool', lambda E: E.tensor_copy(out=u_[:, 0:2], in_=halo[:, f0, :]), r=[(halo, f0)], w=[(u_, 'h')])
                kb.op('pool', lambda E: E.tensor_copy(out=halo[:, f0, :], in_=u_[:, 512:514]), r=[(u_, 'b')], w=[(halo, f0)])
            conv3(pg, ag[k2], g_, fb)
            conv3(pv, av[k2], v_, 22 + fb)
            kb.op('act', lambda E: E.activation(out=sg[k2][:], in_=ag[k2][:], func=AF.Silu), r=[ag[k2]], w=[sg[k2]])
            a_ = act[fb % 3]
            kb.op('dve', lambda E: E.tensor_tensor(out=a_[:], in0=sg[k2][:], in1=av[k2][:], op=ALU.mult), r=[sg[k2], av[k2]], w=[a_])
            kb.dma('sp', io['actT'][fb * 128:(fb + 1) * 128, s * 512:(s + 1) * 512], a_[:], r=[a_], w=['actT'])
    ph.close()

    ph = Phase(kb, "f2")
    wdnb = ph.sb("wdnb", [128, 22, D], BF16)
    load_cast_weight(kb, ph, wdnb, io['ffn_w_down'], 22, D)
    aT = ph.sbn("aT", [128, 22, 128], BF16, 2)
    hs = ph.sbn("hs", [128, D], F32, 2)
    ot = ph.sbn("ot", [128, D], F32, 2)
    pD = ph.psn("pD", [128, 512], F32, 4)
    for t in range(NT):
        b = t % 2
        kb.dma('sp', aT[b][:], io['actT'][:, t * 128:(t + 1) * 128].rearrange("(c p) t -> p c t", p=128), r=['actT'], w=[aT[b]])
        kb.dma('sp', hs[b][:], io['h_s'][t * 128:(t + 1) * 128, :], r=['h_s'], w=[hs[b]])
        for half in range(2):
            p = pD[2 * b + half]
            for c in range(22):
                kb.op('pe', lambda E: E.matmul(p[:], lhsT=aT[b][:, c, :], rhs=wdnb[:, c, half * 512:(half + 1) * 512],
                                               start=(c == 0), stop=(c == 21)), r=[aT[b], (wdnb, c)], w=[p])
            kb.op('dve', lambda E: E.tensor_tensor(out=ot[b][:, half * 512:(half + 1) * 512], in0=p[:],
                                                   in1=hs[b][:, half * 512:(half + 1) * 512], op=ALU.add),
                  r=[p, hs[b]], w=[(ot[b], half)])
        kb.dma('sp', io['out'][t * 128:(t + 1) * 128, :], ot[b][:], r=[(ot[b], 0), (ot[b], 1)], w=['out'])
    ph.close()


W_NAMES = ['attn_norm_w', 'mem_norm_w', 'w_in', 'gdn_conv_w', 'gdn_a_log', 'gdn_dt_bias', 'gdn_out_norm_w',
           'nsa_q_norm_w', 'nsa_kc_norm_w', 'nsa_ks_norm_w', 'nsa_kw_norm_w', 'nsa_cmp_pos_k', 'nsa_cmp_pos_v',
           'nsa_cmp_k_w1', 'nsa_cmp_k_w2', 'nsa_cmp_v_w1', 'nsa_cmp_v_w2', 'mem_w_kv', 'mem_q_norm_w', 'mem_k_norm_w',
           'w_out', 'ffn_norm_w', 'ffn_w_up', 'ffn_conv_w', 'ffn_w_down']
W_SHAPES = {
    'attn_norm_w': [D], 'mem_norm_w': [D], 'w_in': [D, INW], 'gdn_conv_w': [4, 1536], 'gdn_a_log': [4], 'gdn_dt_bias': [4],
    'gdn_out_norm_w': [128], 'nsa_q_norm_w': [64], 'nsa_kc_norm_w': [64], 'nsa_ks_norm_w': [64], 'nsa_kw_norm_w': [64],
    'nsa_cmp_pos_k': [32, 64], 'nsa_cmp_pos_v': [32, 64], 'nsa_cmp_k_w1': [2048, 64], 'nsa_cmp_k_w2': [64, 64],
    'nsa_cmp_v_w1': [2048, 64], 'nsa_cmp_v_w2': [64, 64], 'mem_w_kv': [D, 1024], 'mem_q_norm_w': [128], 'mem_k_norm_w': [128],
    'w_out': [1536, D], 'ffn_norm_w': [D], 'ffn_w_up': [D, 2 * FF], 'ffn_conv_w': [3, 2 * FF], 'ffn_w_down': [FF, D],
}


def make_consts():
    c = {}
    c['c_ident'] = np.eye(128, dtype=np.float32).astype(ml_dtypes.bfloat16)
    idx = np.arange(128)
    same = (idx[:, None] // 64) == (idx[None, :] // 64)
    c['c_btri'] = (same & (idx[:, None] <= idx[None, :])).astype(np.float32)
    c['c_bones'] = same.astype(np.float32)
    c['c_strict'] = (same & (idx[:, None] > idx[None, :])).astype(np.float32)
    c['c_mlow'] = np.where(same & (idx[:, None] >= idx[None, :]), 0.0, NEG).astype(np.float32)
    c['c_mup'] = np.ascontiguousarray(c['c_mlow'].T)
    c['c_mch'] = np.stack([(idx < 64), (idx >= 64)], axis=1).astype(np.float32)
    bf = ml_dtypes.bfloat16
    c['c_tril'] = np.where(idx[:, None] <= idx[None, :], 0.0, NEG).astype(np.float32).astype(bf)
    c['c_far'] = np.where(idx[None, :] < idx[:, None], 0.0, NEG).astype(np.float32).astype(bf)
    cm = np.zeros((9, 128, 512), np.float32)
    f = np.arange(512)
    for m in range(9):
        bt, s_ = (0, m) if m < 5 else (1, m - 1)
        blk = 128 * bt + idx
        vis = (16 * blk[:, None] + 31 <= 512 * s_ + f[None, :]) & (blk[:, None] < 255)
        cm[m] = np.where(vis, 0.0, NEG)
    c['c_cmask'] = cm.astype(bf)
    kk = np.arange(S)
    c['c_E'] = np.where((kk[None, :] // 64) == np.arange(64)[:, None], -NEG, 0.0).astype(np.float32).astype(bf)
    ci = np.arange(256) * 16
    sj = np.arange(64) * 64
    ovl = np.clip(np.minimum(ci[:, None] + 32, sj[None, :] + 64) - np.maximum(ci[:, None], sj[None, :]), 0, None) / 16.0
    ovl[255] = 0.0
    c['c_ovl'] = np.ascontiguousarray(ovl.reshape(2, 128, 64).transpose(1, 0, 2)).astype(np.float32).astype(bf)
    pos = np.arange(S, dtype=np.float32)
    inv = (1.0 / (np.float32(500000.0) ** (np.arange(0, 16, 2, dtype=np.float32) / np.float32(16)))).astype(np.float32)
    ang = pos[:, None] * inv[None, :]
    c['c_rope'] = np.concatenate([np.cos(ang), np.sin(ang)], axis=1).astype(np.float32)
    tt = np.arange(S)
    cur = tt // 64
    blk = np.arange(64)
    valid = blk[None, :] <= cur[:, None]
    forced = (blk[None, :] == 0) | (blk[None, :] == cur[:, None]) | (blk[None, :] == cur[:, None] - 1)
    c['c_A'] = (valid & ~forced).astype(np.float32).reshape(NT, 128, 64)
    c['c_B'] = np.where(valid, np.where(forced, 1e6, 0.0), -1e9).astype(np.float32).reshape(NT, 128, 64)
    return c


def build_program(dbg=False, phases=('ip', 'mem', 'gdn', 'nsa', 'ffn'), dbg_ocat=False):
    nc = bass.Bass("TRN2", target_bir_lowering=False)
    io = {}
    io['x'] = nc.dram_tensor("x", [S, D], F32, kind="ExternalInput").ap()
    io['mem'] = nc.dram_tensor("mem", [256, D], F32, kind="ExternalInput").ap()
    for n in W_NAMES:
        io[n] = nc.dram_tensor(n, W_SHAPES[n], F32, kind="ExternalInput").ap()
    for n, v in make_consts().items():
        io[n] = nc.dram_tensor(n, list(v.shape), BF16 if v.dtype == ml_dtypes.bfloat16 else F32, kind="ExternalInput").ap()
    io['out'] = nc.dram_tensor("out", [S, D], F32, kind="ExternalOutput").ap()
    sk = "ExternalOutput" if dbg else "Internal"
    io['tm'] = nc.dram_tensor("tm", [S, TMW], F32, kind=sk).ap()
    io['qkv_tm'] = nc.dram_tensor("qkv_tm", [S, 1536], BF16, kind=sk).ap()
    if dbg_ocat:
        io['ocatT'] = nc.dram_tensor("ocatT", [1536, S], BF16, kind="ExternalInput").ap()
    else:
        io['ocatT'] = nc.dram_tensor("ocatT", [1536, S], BF16, kind=sk).ap()
    io['h_s'] = nc.dram_tensor("h_s", [S, D], F32, kind=sk).ap()
    io['actT'] = nc.dram_tensor("actT", [FF, S], BF16, kind="Internal").ap()
    kb = KB(nc)
    if 'ip' in phases:
        phase_inproj(kb, io)
    if 'mem' in phases:
        phase_mem(kb, io)
    if 'gdn' in phases:
        phase_gdn(kb, io)
    if 'nsa' in phases:
        phase_nsa(kb, io)
    if 'ffn' in phases:
        phase_ffn(kb, io)
    kb.finish()
    return nc, kb


def make_in_maps(inputs):
    consts = make_consts()
    maps = []
    for b in range(8):
        m = {'x': np.ascontiguousarray(inputs['x'][b]), 'mem': np.ascontiguousarray(inputs['mem'][b])}
        for n in W_NAMES:
            m[n] = np.ascontiguousarray(np.asarray(inputs[n])[0])
        m.update(consts)
        maps.append(m)
    return maps


def kernel(**inputs):
    nc, kb = build_program()
    maps = make_in_maps(inputs)
    res = run_bass_kernel_spmd(nc, maps, core_ids=list(range(8)))
    return np.stack([np.asarray(r['out'], dtype=np.float32) for r in res.results], axis=0)
```

```python
import os
import numpy as np
from contextlib import ExitStack
import concourse.bass as bass
import concourse.mybir as mybir
from concourse.bass_utils import run_bass_kernel_spmd
import ml_dtypes

F32 = mybir.dt.float32
BF16 = mybir.dt.bfloat16
AF = mybir.ActivationFunctionType
ALU = mybir.AluOpType
AX = mybir.AxisListType

S = 4096
D = 1024
NT = S // 128
INW = 3872
TMW = 2336
A_OFF, B_OFF, GATE_OFF, NQ_OFF = 0, 4, 8, 520
KC_OFF, VC_OFF, KS_OFF, VS_OFF, KW_OFF, VW_OFF = 1032, 1160, 1288, 1416, 1544, 1672
NG_OFF, MQ_OFF = 1800, 1824
FF = 2816
NEG = -30000.0
EPS = 1e-6


class T:
    def __init__(self, t, k):
        self.t = t
        self.k = k

    def __getitem__(self, idx):
        return self.t[idx]


class KB:
    NDS = 16

    def __init__(self, nc):
        self.nc = nc
        self.stack = ExitStack()
        self.eng = {'pe': nc.tensor, 'act': nc.scalar, 'dve': nc.vector, 'pool': nc.gpsimd, 'sp': nc.sync}
        self.sem = {}
        for e in self.eng:
            self.sem[e] = self.stack.enter_context(nc.semaphore("s_" + e))
        for j in range(self.NDS):
            self.sem[('d', j)] = self.stack.enter_context(nc.semaphore("d_%d" % j))
        self.cnt = {e: 0 for e in self.eng}
        self.seen = {e: {} for e in self.eng}
        self.state = {}
        self.dma_i = 0
        self.dma_uses = [0] * self.NDS
        self.nins = 0
        self.rr = 0
        self.excl = set()

    def _wait(self, e, evs):
        need = {}
        for (sk, v) in evs:
            if sk == e and e in ('pe', 'sp'):
                continue
            if self.seen[e].get(sk, 0) < v:
                need[sk] = max(need.get(sk, 0), v)
        for sk, v in need.items():
            self.eng[e].wait_ge(self.sem[sk], v)
            self.seen[e][sk] = v

    @staticmethod
    def _keys(lst):
        out = []
        for x in lst:
            if isinstance(x, T):
                out.append(x.k)
            elif isinstance(x, (list, tuple)) and len(x) and isinstance(x[0], T):
                out.append((x[0].k,) + tuple(x[1:]))
            else:
                out.append(x)
        return out

    def _deps(self, reads, writes):
        evs = []
        for k in reads:
            st = self.state.get(k)
            if st and st[0]:
                evs.append(st[0])
        for k in writes:
            st = self.state.get(k)
            if st:
                if st[0]:
                    evs.append(st[0])
                evs.extend(st[1])
        return evs

    def _update(self, ev, reads, writes):
        for k in reads:
            st = self.state.setdefault(k, [None, []])
            st[1].append(ev)
            if len(st[1]) > 12:
                best = {}
                for (sk, v) in st[1]:
                    best[sk] = max(best.get(sk, 0), v)
                st[1] = list(best.items())
        for k in writes:
            self.state[k] = [ev, []]

    def op(self, e, fn, r=(), w=()):
        r = self._keys(r)
        w = self._keys(w)
        w = w + [k for k in r if k in self.excl and k not in w]
        self._wait(e, self._deps(r, w))
        ins = fn(self.eng[e])
        self.cnt[e] += 1
        ins.then_inc(self.sem[e], 1)
        self._update((e, self.cnt[e]), r, w)
        self.nins += 1
        return ins

    def dma(self, q, out, in_, r=(), w=(), **kw):
        r = self._keys(r)
        w = self._keys(w)
        j = self.dma_i % self.NDS
        self.dma_i += 1
        evs = self._deps(r, w)
        if self.dma_uses[j] > 0:
            evs.append((('d', j), 16 * self.dma_uses[j]))
        self._wait(q, evs)
        ins = self.eng[q].dma_start(out=out, in_=in_, **kw)
        self.dma_uses[j] += 1
        ins.then_inc(self.sem[('d', j)], 16)
        ev = (('d', j), 16 * self.dma_uses[j])
        self._update(ev, r, w)
        self.nins += 1
        return ev

    def barrier(self):
        evs = [(f, self.cnt[f]) for f in self.eng if self.cnt[f]]
        evs += [(('d', j), 16 * self.dma_uses[j]) for j in range(self.NDS) if self.dma_uses[j]]
        for e in self.eng:
            self._wait(e, [ev for ev in evs if ev[0] != e])

    def finish(self):
        self.barrier()
        self.stack.close()

    def ew(self, with_act=False):
        self.rr += 1
        lst = ('dve', 'pool', 'act') if with_act else ('dve', 'pool')
        return lst[self.rr % len(lst)]


class Phase:
    def __init__(self, kb, tag):
        self.kb = kb
        self.nc = kb.nc
        self.tag = tag
        self.st = ExitStack()

    def sb(self, name, shape, dt):
        n = self.tag + "_" + name
        return T(self.st.enter_context(self.nc.sbuf_tensor(n, list(shape), dt)), n)

    def sbn(self, name, shape, dt, n):
        return [self.sb("%s%d" % (name, i), shape, dt) for i in range(n)]

    def ps(self, name, shape, dt=F32):
        n = self.tag + "_" + name
        self.kb.excl.add(n)
        return T(self.st.enter_context(self.nc.psum_tensor(n, list(shape), dt)), n)

    def psn(self, name, shape, dt, n):
        return [self.ps("%s%d" % (name, i), shape, dt) for i in range(n)]

    def close(self):
        self.kb.barrier()
        self.st.close()


def load_cast_weight(kb, ph, dst, src_ap, nchunks, ncols, gam=None, stage_cols=None):
    stage_cols = stage_cols or ncols
    stg = ph.sbn("stg_" + dst.k, [128, stage_cols], F32, 2)
    i = 0
    engs = ('dve', 'act', 'dve')
    for c in range(nchunks):
        for c0 in range(0, ncols, stage_cols):
            c1 = min(ncols, c0 + stage_cols)
            sg = stg[i % 2]
            kb.dma('sp', sg[:, 0:c1 - c0], src_ap[c * 128:(c + 1) * 128, c0:c1], w=[sg])
            e = engs[i % 3]
            o = dst[:, c, c0:c1]
            if gam is None:
                if e == 'act':
                    kb.op(e, lambda E: E.copy(out=o, in_=sg[:, 0:c1 - c0]), r=[sg], w=[(dst, c)])
                else:
                    kb.op(e, lambda E: E.tensor_copy(out=o, in_=sg[:, 0:c1 - c0]), r=[sg], w=[(dst, c)])
            else:
                if e == 'act':
                    kb.op(e, lambda E: E.activation(out=o, in_=sg[:, 0:c1 - c0], func=AF.Copy, scale=gam[:, c:c + 1]),
                          r=[sg, gam], w=[(dst, c)])
                else:
                    kb.op(e, lambda E: E.tensor_scalar(out=o, in0=sg[:, 0:c1 - c0], scalar1=gam[:, c:c + 1], scalar2=None,
                                                       op0=ALU.mult), r=[sg, gam], w=[(dst, c)])
            i += 1


def rms_rstd(kb, src_ap, junk, ss, rs, n, rkeys):
    kb.op('act', lambda E: E.activation(out=junk[:, 0:n], in_=src_ap, func=AF.Square, accum_out=ss[:]), r=rkeys, w=[junk, ss])
    kb.op('act', lambda E: E.activation(out=rs[:], in_=ss[:], func=AF.Sqrt, scale=1.0 / n, bias=EPS), r=[ss], w=[rs])
    kb.op('dve', lambda E: E.reciprocal(out=rs[:], in_=rs[:]), r=[rs], w=[rs])


def phase_inproj(kb, io):
    nc = kb.nc
    ph = Phase(kb, "ip")
    winb = ph.sb("winb", [128, 8, INW], BF16)
    gam = ph.sb("gam", [128, 8], F32)
    cw = ph.sb("cw", [128, 4, 12], F32)
    ident = ph.sb("ident", [128, 128], BF16)
    kb.dma('sp', gam[:], io['attn_norm_w'].rearrange("(c p) -> p c", p=128), w=[gam], allow_slow_non_contiguous=True)
    for j in range(4):
        kb.dma('sp', cw[:, j, :], io['gdn_conv_w'][j, :].rearrange("(c p) -> p c", p=128), w=[cw], allow_slow_non_contiguous=True)
    kb.dma('sp', ident[:], io['c_ident'][:, :], w=[ident])
    load_cast_weight(kb, ph, winb, io['w_in'], 8, INW, gam=gam, stage_cols=1936)

    xt = ph.sbn("xt", [128, D], F32, 2)
    junk = ph.sb("junk", [128, D], BF16)
    ss = ph.sbn("ss", [128, 1], F32, 2)
    rs = ph.sbn("rs", [128, 1], F32, 2)
    xn = ph.sbn("xn", [128, D], BF16, 2)
    xnT = ph.sbn("xnT", [128, 8, 512], BF16, 2)
    xc = ph.sbn("xc", [128, 515], F32, 3)
    acc = ph.sbn("acc", [128, 512], F32, 3)
    halo = ph.sb("halo", [128, 12, 3], F32)
    qT = ph.sb("qT", [128, 12, 512], BF16)
    qtm = ph.sbn("qtm", [128, 1536], BF16, 2)
    tmt = ph.sbn("tmt", [128, TMW], F32, 2)
    pT = ph.psn("pT", [128, 8, 128], BF16, 2)
    pF = ph.psn("pF", [128, 512], F32, 2)
    pM = ph.psn("pM", [128, 512], F32, 2)
    pQ = ph.psn("pQ", [128, 8, 128], BF16, 2)

    kb.op('pool', lambda E: E.memset(halo[:], 0.0), w=[halo])
    it = 0
    for s in range(S // 512):
        xT = xnT[s % 2]
        for t4 in range(4):
            t = s * 4 + t4
            b = it % 2
            it += 1
            kb.dma('sp', xt[b][:], io['x'][t * 128:(t + 1) * 128, :], w=[xt[b]])
            rms_rstd(kb, xt[b][:], junk, ss[b], rs[b], D, [xt[b]])
            kb.op('dve', lambda E: E.tensor_scalar(out=xn[b][:], in0=xt[b][:], scalar1=rs[b][:, 0:1], scalar2=None, op0=ALU.mult),
                  r=[xt[b], rs[b]], w=[xn[b]])
            for c in range(8):
                kb.op('pe', lambda E: E.transpose(out=pT[b][:, c, :], in_=xn[b][:, c * 128:(c + 1) * 128], identity=ident[:]),
                      r=[xn[b], ident], w=[pT[b]])
            kb.op('act', lambda E: E.copy(out=xT[:, :, t4 * 128:(t4 + 1) * 128], in_=pT[b][:]), r=[pT[b]], w=[xT])
        for cb in range(12):
            pf = pF[cb % 2]
            for c in range(8):
                kb.op('pe', lambda E: E.matmul(pf[:], lhsT=winb[:, c, cb * 128:(cb + 1) * 128], rhs=xT[:, c, :],
                                               start=(c == 0), stop=(c == 7)), r=[xT, (winb, c)], w=[pf])
            x3 = xc[cb % 3]
            ac = acc[cb % 3]
            kb.op('act', lambda E: E.copy(out=x3[:, 3:515], in_=pf[:]), r=[pf], w=[(x3, 'b')])
            kb.op('pool', lambda E: E.tensor_copy(out=x3[:, 0:3], in_=halo[:, cb, :]), r=[(halo, cb)], w=[(x3, 'h')])
            kb.op('pool', lambda E: E.tensor_copy(out=halo[:, cb, :], in_=x3[:, 512:515]), r=[(x3, 'b')], w=[(halo, cb)])
            kb.op('act', lambda E: E.activation(out=ac[:], in_=pf[:], func=AF.Copy, scale=cw[:, 3, cb:cb + 1]), r=[pf, cw], w=[ac])
            for j in range(3):
                kb.op('dve', lambda E: E.scalar_tensor_tensor(out=ac[:], in0=x3[:, j:j + 512], scalar=cw[:, j, cb:cb + 1], in1=ac[:],
                                                           op0=ALU.mult, op1=ALU.add), r=[(x3, 'b'), (x3, 'h'), cw, ac], w=[ac])
            kb.op('act', lambda E: E.activation(out=qT[:, cb, :], in_=ac[:], func=AF.Silu), r=[ac], w=[(qT, cb)])
        for t4 in range(4):
            t = s * 4 + t4
            qm = qtm[t % 2]
            for g3 in range(3):
                pq = pQ[g3 % 2]
                for j in range(4):
                    cb = g3 * 4 + j
                    kb.op('pe', lambda E: E.transpose(out=pq[:, j, :], in_=qT[:, cb, t4 * 128:(t4 + 1) * 128], identity=ident[:]),
                          r=[(qT, cb), ident], w=[pq])
                e1 = 'dve' if g3 % 2 == 0 else 'act'
                if e1 == 'dve':
                    kb.op('dve', lambda E: E.tensor_copy(out=qm[:, g3 * 512:(g3 + 1) * 512], in_=pq[:, 0:4, :].rearrange("p a b -> p (a b)")),
                          r=[pq], w=[(qm, g3)])
                else:
                    kb.op('act', lambda E: E.copy(out=qm[:, g3 * 512:(g3 + 1) * 512], in_=pq[:, 0:4, :].rearrange("p a b -> p (a b)")),
                          r=[pq], w=[(qm, g3)])
            kb.dma('sp', io['qkv_tm'][t * 128:(t + 1) * 128, :], qm[:], r=[(qm, 0), (qm, 1), (qm, 2)], w=['qkv_tm'])
            tm = tmt[t % 2]
            for ci, n0 in enumerate(range(0, TMW, 512)):
                n1 = min(TMW, n0 + 512)
                pm = pM[ci % 2]
                for c in range(8):
                    kb.op('pe', lambda E: E.matmul(pm[:, 0:n1 - n0], lhsT=xT[:, c, t4 * 128:(t4 + 1) * 128],
                                                   rhs=winb[:, c, 1536 + n0:1536 + n1], start=(c == 0), stop=(c == 7)),
                          r=[xT, (winb, c)], w=[pm])
                if ci % 2 == 0:
                    kb.op('dve', lambda E: E.tensor_copy(out=tm[:, n0:n1], in_=pm[:, 0:n1 - n0]), r=[pm], w=[(tm, ci)])
                else:
                    kb.op('act', lambda E: E.copy(out=tm[:, n0:n1], in_=pm[:, 0:n1 - n0]), r=[pm], w=[(tm, ci)])
            kb.dma('sp', io['tm'][t * 128:(t + 1) * 128, :], tm[:], r=[(tm, i) for i in range(5)], w=['tm'])
    ph.close()


def attn_block(kb, o_ps, PTbuf, pti, sc_ps, kt_specs, ident, rkeys_q, LA=2):
    n = len(kt_specs)
    pts = [None] * n
    started = set()
    npv = sum(sp['qt1'] - sp['qt0'] for sp in kt_specs)
    done = 0
    for i in range(n + LA):
        if i < n:
            sp = kt_specs[i]
            pss = sc_ps[pti[0] % len(sc_ps)]
            ptb = PTbuf[pti[0] % len(PTbuf)]
            pti[0] += 1
            pts[i] = ptb
            q0, q1 = sp['qt0'], sp['qt1']
            ncol = (q1 - q0) * 128
            nk = sp['nk']
            nm = len(sp['masks'])
            kb.op('pe', lambda E: E.matmul(pss[0:nk, 0:ncol], lhsT=sp['lhsT'], rhs=sp['rhs_fn'](q0, q1), start=True, stop=(nm == 0)),
                  r=sp['rk'] + rkeys_q, w=[pss])
            for mi, (c0, nc_, mask) in enumerate(sp['masks']):
                kb.op('pe', lambda E: E.matmul(pss[0:nk, c0:c0 + nc_], lhsT=ident[0:nk, 0:nk], rhs=mask, start=False, stop=(mi == nm - 1)),
                      r=[ident] + sp.get('mk', []), w=[pss])
            kb.op('act', lambda E: E.activation(out=ptb[0:nk, 0:ncol], in_=pss[0:nk, 0:ncol], func=AF.Exp), r=[pss], w=[ptb])
        j = i - LA
        if j >= 0:
            sp = kt_specs[j]
            ptb = pts[j]
            nk = sp['nk']
            for qt in range(sp['qt0'], sp['qt1']):
                c0 = (qt - sp['qt0']) * 128
                oap, okey, bank = o_ps(qt)
                st = bank not in started
                started.add(bank)
                done += 1
                kb.op('pe', lambda E: E.matmul(oap, lhsT=ptb[0:nk, c0:c0 + 128], rhs=sp['v_fn'](qt), start=st, stop=(done == npv),
                                               skip_group_check=True), r=[ptb] + sp['rv'], w=[okey])


def phase_mem(kb, io):
    nc = kb.nc
    ph = Phase(kb, "mm")
    ident = ph.sb("ident", [128, 128], BF16)
    kb.dma('sp', ident[:], io['c_ident'][:, :], w=[ident])
    wkv = ph.sb("wkv", [128, 8, 1024], BF16)
    gam = ph.sb("gam", [128, 8], F32)
    kb.dma('sp', gam[:], io['mem_norm_w'].rearrange("(c p) -> p c", p=128), w=[gam], allow_slow_non_contiguous=True)
    load_cast_weight(kb, ph, wkv, io['mem_w_kv'], 8, 1024, gam=gam)
    qnw = ph.sb("qnw", [128, 128], F32)
    knw = ph.sb("knw", [128, 128], F32)
    kb.dma('sp', qnw[:], io['mem_q_norm_w'].partition_broadcast(128), w=[qnw])
    kb.dma('sp', knw[:], io['mem_k_norm_w'].partition_broadcast(128), w=[knw])
    kb.op('dve', lambda E: E.tensor_scalar(out=qnw[:], in0=qnw[:], scalar1=128 ** -0.5, scalar2=None, op0=ALU.mult), r=[qnw], w=[qnw])

    mt = ph.sbn("mt", [128, D], F32, 2)
    junk = ph.sb("junk", [128, D], BF16)
    ss = ph.sb("ss", [128, 1], F32)
    rs = ph.sb("rs", [128, 1], F32)
    mn = ph.sb("mn", [128, D], BF16)
    mnT = ph.sb("mnT", [128, 8, 128], BF16)
    kvt = ph.sb("kvt", [128, 1024], F32)
    sq4 = ph.sb("sq4", [128, 4, 128], F32)
    ss4 = ph.sb("ss4", [128, 4], F32)
    rs4 = ph.sb("rs4", [128, 4], F32)
    kn = ph.sb("kn", [128, 4, 128], BF16)
    kT = ph.sb("kT", [128, 4, 256], BF16)
    v1 = ph.sb("v1", [128, 2, 4, 129], BF16)
    pT = ph.ps("pT", [128, 8, 128], BF16)
    pK = ph.psn("pK", [128, 512], F32, 2)
    kb.op('pool', lambda E: E.memset(v1[:], 1.0), w=[v1])
    for mt_i in range(2):
        m = mt[mt_i]
        kb.dma('sp', m[:], io['mem'][mt_i * 128:(mt_i + 1) * 128, :], w=[m])
        rms_rstd(kb, m[:], junk, ss, rs, D, [m])
        kb.op('dve', lambda E: E.tensor_scalar(out=mn[:], in0=m[:], scalar1=rs[:, 0:1], scalar2=None, op0=ALU.mult), r=[m, rs], w=[mn])
        for c in range(8):
            kb.op('pe', lambda E: E.transpose(out=pT[:, c, :], in_=mn[:, c * 128:(c + 1) * 128], identity=ident[:]), r=[mn, ident], w=[pT])
        kb.op('act', lambda E: E.copy(out=mnT[:], in_=pT[:]), r=[pT], w=[mnT])
        for half in range(2):
            pk = pK[half]
            for c in range(8):
                kb.op('pe', lambda E: E.matmul(pk[:], lhsT=mnT[:, c, :], rhs=wkv[:, c, half * 512:(half + 1) * 512],
                                               start=(c == 0), stop=(c == 7)), r=[mnT, (wkv, c)], w=[pk])
            kb.op('act', lambda E: E.copy(out=kvt[:, half * 512:(half + 1) * 512], in_=pk[:]), r=[pk], w=[(kvt, half)])
        k3 = kvt[:, 0:512].rearrange("p (h d) -> p h d", h=4)
        kb.op('dve', lambda E: E.tensor_tensor(out=sq4[:], in0=k3, in1=k3, op=ALU.mult), r=[(kvt, 0)], w=[sq4])
        kb.op('dve', lambda E: E.tensor_reduce(out=ss4[:], in_=sq4[:], axis=AX.X, op=ALU.add), r=[sq4], w=[ss4])
        kb.op('act', lambda E: E.activation(out=rs4[:], in_=ss4[:], func=AF.Sqrt, scale=1.0 / 128, bias=EPS), r=[ss4], w=[rs4])
        kb.op('dve', lambda E: E.reciprocal(out=rs4[:], in_=rs4[:]), r=[rs4], w=[rs4])
        kb.op('dve', lambda E: E.tensor_tensor(out=sq4[:], in0=k3, in1=rs4[:].unsqueeze(2).to_broadcast([128, 4, 128]), op=ALU.mult),
              r=[(kvt, 0), rs4], w=[sq4])
        kb.op('dve', lambda E: E.tensor_tensor(out=kn[:], in0=sq4[:], in1=knw[:].unsqueeze(1).to_broadcast([128, 4, 128]), op=ALU.mult),
              r=[sq4, knw], w=[kn])
        for h in range(4):
            kb.op('pe', lambda E: E.transpose(out=pT[:, h, :], in_=kn[:, h, :], identity=ident[:]), r=[kn, ident], w=[pT])
        kb.op('act', lambda E: E.copy(out=kT[:, :, mt_i * 128:(mt_i + 1) * 128], in_=pT[:, 0:4, :]), r=[pT], w=[kT])
        kb.op('dve', lambda E: E.tensor_copy(out=v1[:, mt_i, :, 0:128], in_=kvt[:, 512:1024].rearrange("p (h d) -> p h d", h=4)),
              r=[(kvt, 1)], w=[v1])

    qt_ = ph.sbn("qt", [128, 512], F32, 2)
    qs = ph.sb("qs", [128, 4, 128], F32)
    qn = ph.sbn("qn", [128, 4, 128], BF16, 2)
    qT = ph.sbn("qT", [128, 4, 512], BF16, 2)
    PTb = ph.sbn("PT", [128, 512], BF16, 4)
    pti = [0]
    oc = ph.sbn("oc", [128, 4, 128], BF16, 4)
    rinv = ph.sb("rinv", [128, 1], F32)
    ocT = ph.sbn("ocT", [128, 4, 128], BF16, 2)
    sc_ps = ph.psn("sc", [128, 512], F32, 2)
    o_psA = ph.ps("oA", [128, 2, 256], F32)
    o_psB = ph.ps("oB", [128, 2, 256], F32)
    pO = ph.ps("pO", [128, 8, 128], BF16)
    for s in range(S // 512):
        qTs = qT[s % 2]
        for t4 in range(4):
            t = s * 4 + t4
            q = qt_[t % 2]
            qb = qn[t % 2]
            kb.dma('sp', q[:], io['tm'][t * 128:(t + 1) * 128, MQ_OFF:MQ_OFF + 512], r=['tm'], w=[q])
            q3 = q[:].rearrange("p (h d) -> p h d", h=4)
            kb.op('pool', lambda E: E.tensor_tensor(out=qs[:], in0=q3, in1=q3, op=ALU.mult), r=[q], w=[qs])
            kb.op('dve', lambda E: E.tensor_reduce(out=ss4[:], in_=qs[:], axis=AX.X, op=ALU.add), r=[qs], w=[ss4])
            kb.op('act', lambda E: E.activation(out=rs4[:], in_=ss4[:], func=AF.Sqrt, scale=1.0 / 128, bias=EPS), r=[ss4], w=[rs4])
            kb.op('dve', lambda E: E.reciprocal(out=rs4[:], in_=rs4[:]), r=[rs4], w=[rs4])
            kb.op('dve', lambda E: E.tensor_tensor(out=qs[:], in0=q3, in1=rs4[:].unsqueeze(2).to_broadcast([128, 4, 128]), op=ALU.mult),
                  r=[q, rs4], w=[qs])
            kb.op('pool', lambda E: E.tensor_tensor(out=qb[:], in0=qs[:], in1=qnw[:].unsqueeze(1).to_broadcast([128, 4, 128]), op=ALU.mult),
                  r=[qs, qnw], w=[qb])
            for h in range(4):
                kb.op('pe', lambda E: E.transpose(out=pT[:, h, :], in_=qb[:, h, :], identity=ident[:]), r=[qb, ident], w=[pT])
            kb.op('act', lambda E: E.copy(out=qTs[:, :, t4 * 128:(t4 + 1) * 128], in_=pT[:, 0:4, :]), r=[pT], w=[qTs])
        for h in range(4):
            def o_ps(qt, h=h):
                return (o_psA[:, qt, 0:129], o_psA, 'A') if qt < 2 else (o_psB[:, qt - 2, 0:129], o_psB, 'B')
            specs = []
            for kt in range(2):
                specs.append(dict(lhsT=kT[:, h, kt * 128:(kt + 1) * 128], rhs_fn=lambda q0, q1, h=h: qTs[:, h, q0 * 128:q1 * 128],
                                  qt0=0, qt1=4, masks=[], v_fn=lambda qt, kt=kt, h=h: v1[:, kt, h, :], nk=128, rk=[kT], rv=[v1]))
            attn_block(kb, o_ps, PTb, pti, sc_ps, specs, ident, [qTs])
            for t4 in range(4):
                t = s * 4 + t4
                ob = oc[t4]
                oap, okey, _ = o_ps(t4)
                kb.op('dve', lambda E: E.reciprocal(out=rinv[:], in_=oap[:, 128:129]), r=[okey], w=[rinv])
                kb.op('dve', lambda E: E.tensor_scalar(out=ob[:, h, :], in0=oap[:, 0:128], scalar1=rinv[:, 0:1], scalar2=None, op0=ALU.mult),
                      r=[okey, rinv], w=[(ob, h)])
        for t4 in range(4):
            t = s * 4 + t4
            ob = oc[t4]
            oT = ocT[t % 2]
            for h in range(4):
                kb.op('pe', lambda E: E.transpose(out=pO[:, h, :], in_=ob[:, h, :], identity=ident[:]), r=[(ob, h), ident], w=[pO])
            kb.op('act', lambda E: E.copy(out=oT[:], in_=pO[:, 0:4, :]), r=[pO], w=[oT])
            kb.dma('sp', io['ocatT'][1024:1536, t * 128:(t + 1) * 128].rearrange("(c p) t -> p c t", p=128), oT[:], r=[oT], w=['ocatT_c'])
    ph.close()


def phase_gdn(kb, io):
    nc = kb.nc
    ph = Phase(kb, "gd")
    ident = ph.sb("ident", [128, 128], BF16)
    btri = ph.sb("btri", [128, 128], F32)
    bones = ph.sb("bones", [128, 128], F32)
    ones = ph.sb("ones", [128, 128], F32)
    mlow = ph.sb("mlow", [128, 4, 128], F32)
    mup = ph.sb("mup", [128, 4, 128], F32)
    strict = ph.sb("strict", [128, 128], F32)
    mch = ph.sb("mch", [128, 2], F32)
    kb.dma('sp', ident[:], io['c_ident'][:, :], w=[ident])
    kb.dma('sp', btri[:], io['c_btri'][:, :], w=[btri])
    kb.dma('sp', bones[:], io['c_bones'][:, :], w=[bones])
    kb.dma('sp', strict[:], io['c_strict'][:, :], w=[strict])
    kb.dma('sp', mch[:], io['c_mch'][:, :], w=[mch])
    for h in range(4):
        kb.dma('sp', mlow[:, h, :], io['c_mlow'][:, :], w=[mlow])
        kb.dma('sp', mup[:, h, :], io['c_mup'][:, :], w=[mup])
    kb.op('pool', lambda E: E.memset(ones[:], 1.0), w=[ones])
    dtb = ph.sb("dtb", [128, 4], F32)
    nA = ph.sb("nA", [128, 4], F32)
    gw = ph.sb("gw", [128, 128], F32)
    kb.dma('sp', dtb[:], io['gdn_dt_bias'].partition_broadcast(128), w=[dtb])
    kb.dma('sp', nA[:], io['gdn_a_log'].partition_broadcast(128), w=[nA])
    kb.dma('sp', gw[:], io['gdn_out_norm_w'].partition_broadcast(128), w=[gw])
    kb.op('act', lambda E: E.activation(out=nA[:], in_=nA[:], func=AF.Exp), r=[nA], w=[nA])
    kb.op('dve', lambda E: E.tensor_scalar(out=nA[:], in0=nA[:], scalar1=-1.0, scalar2=None, op0=ALU.mult), r=[nA], w=[nA])

    def B4(t_, n=4):
        return t_.unsqueeze(2).to_broadcast([128, n, 128])

    def M4(t_):
        return t_.unsqueeze(1).to_broadcast([128, 4, 128])

    qkv = ph.sbn("qkv", [128, 3, 4, 128], BF16, 2)
    ab = ph.sbn("ab", [128, 8], F32, 2)
    gt = ph.sbn("gt", [128, 512], F32, 2)
    sm = ph.sbn("sm", [128, 64], F32, 2)
    gs = ph.sbn("gs", [128, 16], F32, 2)
    gm = ph.sb("gm", [128, 8], F32)
    R1 = ph.sb("R1", [128, 4, 128], F32)
    R2 = ph.sb("R2", [128, 4, 128], F32)
    tmpA = ph.sb("tmpA", [128, 4, 128], F32)
    tmpB = ph.sb("tmpB", [128, 4, 128], F32)
    dec = ph.sb("dec", [128, 4, 128], F32)
    decT = ph.sb("decT", [128, 4, 128], F32)
    sq = ph.sb("sq", [128, 4, 128], F32)
    KBG = ph.sb("KBG", [128, 4, 128], BF16)
    Kdb = ph.sbn("Kd", [128, 4, 128], BF16, 4)
    VBb = ph.sbn("VB", [128, 4, 128], BF16, 2)
    dg = ph.sbn("dg", [128, 4, 128], BF16, 4)
    QT = ph.sb("QT", [128, 4, 128], BF16)
    QGb = ph.sbn("QG", [128, 4, 128], BF16, 4)
    KT = ph.sb("KT", [128, 4, 128], BF16)
    nbs = ph.sb("nbs", [128, 4, 128], F32)
    Xb = ph.sbn("X", [128, 4, 128], BF16, 2)
    Yb = ph.sbn("Y", [128, 4, 128], BF16, 2)
    Pbb = ph.sbn("P", [128, 4, 128], BF16, 4)
    aqkTb = ph.sbn("aqkT", [128, 4, 128], BF16, 2)
    negWTb = ph.sbn("negWT", [128, 4, 128], BF16, 2)
    vnew = ph.sb("vnew", [128, 4, 128], BF16)
    Sf = ph.sb("Sf", [128, 4, 128], F32)
    Sbf = ph.sbn("Sbf", [128, 4, 128], BF16, 3)
    osb = ph.sb("osb", [128, 4, 128], F32)
    sgt = ph.sb("sgt", [128, 512], F32)
    oa = ph.sbn("oa", [128, 4, 128], BF16, 2)
    oaT = ph.sbn("oaT", [128, 4, 128], BF16, 2)
    pS = ph.ps("pS", [128, 512], F32)
    pD = ph.ps("pD", [128, 4, 128], F32)
    pA = ph.psn("pA", [128, 4, 128], F32, 2)
    pB = ph.psn("pB", [128, 8, 128], BF16, 1)
    pV = ph.ps("pV", [128, 4, 128], F32)
    pdS = ph.ps("pdS", [128, 4, 128], F32)
    pO = ph.ps("pO", [128, 4, 128], F32)
    kb.op('pool', lambda E: E.memset(Sf[:], 0.0), w=[Sf])
    kb.op('pool', lambda E: E.memset(Sbf[0][:], 0.0), w=[Sbf[0]])
    kb.op('pool', lambda E: E.memset(vnew[:], 0.0), w=[vnew])
    pai = [0]

    def PA():
        pai[0] += 1
        return pA[pai[0] % 2]

    def mm4(p, lf, rf, rk):
        for h in range(4):
            kb.op('pe', lambda E: E.matmul(p[:, h, :], lhsT=lf(h), rhs=rf(h), start=True, stop=True), r=rk, w=[p])

    si = [0]

    def tile(t):
        b = t % 2
        x_ = qkv[b]
        VB, aqkT, negWT = VBb[b], aqkTb[b], negWTb[b]
        Kd = Kdb[2 * b:2 * b + 2]
        QG = QGb[2 * b:2 * b + 2]
        Pb = Pbb[2 * b:2 * b + 2]
        s_ = sm[b]
        kb.dma('sp', x_[:].rearrange("p a h d -> p (a h d)"), io['qkv_tm'][t * 128:(t + 1) * 128, :], r=['qkv_tm'], w=[x_])
        kb.dma('sp', ab[b][:], io['tm'][t * 128:(t + 1) * 128, 0:8], r=['tm'], w=[ab[b]])
        kb.dma('sp', gt[b][:], io['tm'][t * 128:(t + 1) * 128, GATE_OFF:GATE_OFF + 512], r=['tm'], w=[gt[b]])
        g = s_[:, 0:4]
        kb.op('dve', lambda E: E.tensor_tensor(out=g, in0=ab[b][:, 0:4], in1=dtb[:], op=ALU.add), r=[ab[b], dtb], w=[s_])
        kb.op('act', lambda E: E.activation(out=g, in_=g, func=AF.Exp), r=[s_], w=[s_])
        kb.op('act', lambda E: E.activation(out=g, in_=g, func=AF.Ln, bias=1.0), r=[s_], w=[s_])
        kb.op('dve', lambda E: E.tensor_tensor(out=g, in0=g, in1=nA[:], op=ALU.mult), r=[s_, nA], w=[s_])
        kb.op('dve', lambda E: E.tensor_scalar(out=s_[:, 4:8], in0=g, scalar1=-1.0, scalar2=None, op0=ALU.mult), r=[s_], w=[s_])
        kb.op('act', lambda E: E.activation(out=s_[:, 8:12], in_=ab[b][:, 4:8], func=AF.Exp, scale=-1.0), r=[ab[b], s_], w=[s_])
        kb.op('dve', lambda E: E.tensor_scalar(out=s_[:, 8:12], in0=s_[:, 8:12], scalar1=1.0, scalar2=None, op0=ALU.add), r=[s_], w=[s_])
        kb.op('dve', lambda E: E.reciprocal(out=s_[:, 8:12], in_=s_[:, 8:12]), r=[s_], w=[s_])
        kb.op('dve', lambda E: E.tensor_scalar(out=s_[:, 12:16], in0=s_[:, 8:12], scalar1=-1.0, scalar2=None, op0=ALU.mult), r=[s_], w=[s_])
        for j in range(2):
            kb.op('dve', lambda E: E.tensor_scalar(out=gm[:, 4 * j:4 * j + 4], in0=g, scalar1=mch[:, j:j + 1], scalar2=None, op0=ALU.mult),
                  r=[s_, mch], w=[gm])
        kb.op('pe', lambda E: E.matmul(pS[:, 0:4], lhsT=btri[:], rhs=g, start=True, stop=True), r=[btri, s_], w=[pS])
        kb.op('pe', lambda E: E.matmul(pS[:, 4:8], lhsT=bones[:], rhs=g, start=True, stop=True), r=[bones, s_], w=[pS])
        kb.op('pe', lambda E: E.matmul(pS[:, 8:16], lhsT=ones[:], rhs=gm[:], start=True, stop=True), r=[ones, gm], w=[pS])
        G = gs[b]
        kb.op('dve', lambda E: E.tensor_copy(out=G[:], in_=pS[:, 0:16]), r=[pS], w=[G])
        kb.op('act', lambda E: E.activation(out=s_[:, 16:20], in_=G[:, 0:4], func=AF.Exp), r=[G, s_], w=[s_])
        kb.op('dve', lambda E: E.tensor_tensor(out=G[:, 4:8], in0=G[:, 4:8], in1=G[:, 0:4], op=ALU.subtract), r=[G], w=[G])
        kb.op('act', lambda E: E.activation(out=s_[:, 20:24], in_=G[:, 4:8], func=AF.Exp), r=[G, s_], w=[s_])
        kb.op('act', lambda E: E.activation(out=s_[:, 56:64], in_=G[:, 8:16], func=AF.Exp), r=[G, s_], w=[s_])
        yield
        kb.op('pool', lambda E: E.tensor_copy(out=R1[:], in_=B4(g)), r=[s_], w=[R1])
        kb.op('pool', lambda E: E.tensor_tensor(out=R2[:], in0=M4(btri[:]), in1=B4(s_[:, 4:8]), op=ALU.mult), r=[s_, btri], w=[R2])
        kb.op('pe', lambda E: E.matmul(pD[:].rearrange("p a b -> p (a b)"), lhsT=btri[:], rhs=R1[:].rearrange("p a b -> p (a b)"),
                                       start=True, stop=False), r=[btri, R1], w=[pD])
        kb.op('pe', lambda E: E.matmul(pD[:].rearrange("p a b -> p (a b)"), lhsT=bones[:], rhs=R2[:].rearrange("p a b -> p (a b)"),
                                       start=False, stop=True), r=[bones, R2], w=[pD])
        kb.op('dve', lambda E: E.tensor_tensor(out=tmpA[:], in0=pD[:], in1=mlow[:], op=ALU.add), r=[pD, mlow], w=[tmpA])
        kb.op('act', lambda E: E.activation(out=dec[:], in_=tmpA[:], func=AF.Exp), r=[tmpA], w=[dec])
        kb.op('dve', lambda E: E.scalar_tensor_tensor(out=tmpB[:], in0=pD[:], scalar=-1.0, in1=mup[:], op0=ALU.mult, op1=ALU.add),
              r=[pD, mup], w=[tmpB])
        kb.op('act', lambda E: E.activation(out=decT[:], in_=tmpB[:], func=AF.Exp), r=[tmpB], w=[decT])
        yield
        for a_, c0 in ((0, 24), (1, 28)):
            kb.op('pool', lambda E: E.tensor_tensor(out=sq[:], in0=x_[:, a_, :, :], in1=x_[:, a_, :, :], op=ALU.mult), r=[x_], w=[sq])
            kb.op('dve', lambda E: E.tensor_reduce(out=s_[:, c0:c0 + 4], in_=sq[:], axis=AX.X, op=ALU.add), r=[sq, s_], w=[s_])
            kb.op('act', lambda E: E.activation(out=s_[:, c0:c0 + 4], in_=s_[:, c0:c0 + 4], func=AF.Sqrt, bias=EPS), r=[s_], w=[s_])
            kb.op('dve', lambda E: E.reciprocal(out=s_[:, c0:c0 + 4], in_=s_[:, c0:c0 + 4]), r=[s_], w=[s_])
        sc_ = lambda o, a, bb: kb.op('dve', lambda E: E.tensor_tensor(out=s_[:, o:o + 4], in0=a, in1=bb, op=ALU.mult), r=[s_, mch], w=[s_])
        kb.op('dve', lambda E: E.tensor_scalar(out=s_[:, 32:36], in0=s_[:, 24:28], scalar1=128 ** -0.5, scalar2=None, op0=ALU.mult), r=[s_], w=[s_])
        sc_(36, s_[:, 32:36], s_[:, 16:20])
        kb.op('dve', lambda E: E.tensor_scalar(out=s_[:, 40:44], in0=s_[:, 36:40], scalar1=mch[:, 1:2], scalar2=None, op0=ALU.mult), r=[s_, mch], w=[s_])
        kb.op('dve', lambda E: E.tensor_scalar(out=s_[:, 36:40], in0=s_[:, 36:40], scalar1=mch[:, 0:1], scalar2=None, op0=ALU.mult), r=[s_, mch], w=[s_])
        sc_(44, s_[:, 28:32], s_[:, 8:12])
        sc_(44, s_[:, 44:48], s_[:, 16:20])
        sc_(48, s_[:, 28:32], s_[:, 20:24])
        kb.op('dve', lambda E: E.tensor_scalar(out=s_[:, 52:56], in0=s_[:, 48:52], scalar1=mch[:, 1:2], scalar2=None, op0=ALU.mult), r=[s_, mch], w=[s_])
        kb.op('dve', lambda E: E.tensor_scalar(out=s_[:, 48:52], in0=s_[:, 48:52], scalar1=mch[:, 0:1], scalar2=None, op0=ALU.mult), r=[s_, mch], w=[s_])
        yield
        kx, vx, qx = x_[:, 1, :, :], x_[:, 2, :, :], x_[:, 0, :, :]
        kb.op('pool', lambda E: E.tensor_tensor(out=KBG[:], in0=kx, in1=B4(s_[:, 44:48]), op=ALU.mult), r=[x_, s_], w=[KBG])
        kb.op('pool', lambda E: E.tensor_tensor(out=Kd[0][:], in0=kx, in1=B4(s_[:, 48:52]), op=ALU.mult), r=[x_, s_], w=[Kd[0]])
        kb.op('pool', lambda E: E.tensor_tensor(out=Kd[1][:], in0=kx, in1=B4(s_[:, 52:56]), op=ALU.mult), r=[x_, s_], w=[Kd[1]])
        kb.op('pool', lambda E: E.tensor_tensor(out=VB[:], in0=vx, in1=B4(s_[:, 8:12]), op=ALU.mult), r=[x_, s_], w=[VB])
        yield
        for i_, c0 in enumerate((32, 36, 40, 28)):
            kb.op('dve', lambda E: E.tensor_tensor(out=dg[i_][:], in0=M4(ident[:]), in1=B4(s_[:, c0:c0 + 4]), op=ALU.mult), r=[ident, s_], w=[dg[i_]])
        for i_, (src, dst) in enumerate(((qx, QT), (qx, QG[0]), (qx, QG[1]), (kx, KT))):
            p = PA()
            mm4(p, lambda h: src[:, h, :], lambda h: dg[i_][:, h, :], [x_, dg[i_]])
            if i_ % 2 == 0:
                kb.op('act', lambda E: E.copy(out=dst[:], in_=p[:]), r=[p], w=[dst])
            else:
                kb.op('dve', lambda E: E.tensor_copy(out=dst[:], in_=p[:]), r=[p], w=[dst])
        yield
        p = PA()
        mm4(p, lambda h: KT[:, h, :], lambda h: KT[:, h, :], [KT])
        kb.op('dve', lambda E: E.tensor_tensor(out=tmpA[:], in0=p[:], in1=dec[:], op=ALU.mult), r=[p, dec], w=[tmpA])
        kb.op('pool', lambda E: E.tensor_tensor(out=nbs[:], in0=M4(strict[:]), in1=B4(s_[:, 12:16]), op=ALU.mult), r=[strict, s_], w=[nbs])
        X, Y, P = Xb[0], Yb[0], Pb[0]
        kb.op('pool', lambda E: E.tensor_tensor(out=X[:], in0=tmpA[:], in1=nbs[:], op=ALU.mult), r=[tmpA, nbs], w=[X])
        for h in range(4):
            kb.op('pe', lambda E: E.transpose(out=pB[0][:, h, :], in_=X[:, h, :], identity=ident[:]), r=[X, ident], w=[pB[0]])
        kb.op('act', lambda E: E.copy(out=Y[:], in_=pB[0][:, 0:4, :]), r=[pB[0]], w=[Y])
        kb.op('dve', lambda E: E.tensor_tensor(out=P[:], in0=pB[0][:, 0:4, :], in1=M4(ident[:]), op=ALU.add), r=[pB[0], ident], w=[P])
        yield
        p = PA()
        mm4(p, lambda h: KT[:, h, :], lambda h: QT[:, h, :], [KT, QT])
        kb.op('dve', lambda E: E.tensor_tensor(out=aqkT[:], in0=p[:], in1=decT[:], op=ALU.mult), r=[p, decT], w=[aqkT])
        yield
        for k_ in range(1, 6):
            Xn, Yn, Pn = Xb[k_ % 2], Yb[k_ % 2], Pb[k_ % 2]
            p = PA()
            mm4(p, lambda h: Y[:, h, :], lambda h: X[:, h, :], [X, Y])
            kb.op('act', lambda E: E.copy(out=Xn[:], in_=p[:]), r=[p], w=[Xn])
            if k_ < 5:
                p2 = PA()
                mm4(p2, lambda h: X[:, h, :], lambda h: Y[:, h, :], [X, Y])
                kb.op('dve', lambda E: E.tensor_copy(out=Yn[:], in_=p2[:]), r=[p2], w=[Yn])
            p3 = PA()
            mm4(p3, lambda h: Xn[:, h, :], lambda h: P[:, h, :], [Xn, P])
            kb.op('dve', lambda E: E.tensor_tensor(out=Pn[:], in0=p3[:], in1=P[:], op=ALU.add), r=[p3, P], w=[Pn])
            X, Y, P = Xn, Yn, Pn
            yield
        yield
        p = PA()
        mm4(p, lambda h: KBG[:, h, :], lambda h: P[:, h, :], [KBG, P])
        kb.op('act', lambda E: E.mul(out=negWT[:], in_=p[:], mul=-1.0), r=[p], w=[negWT])
        yield 'B'
        Sa = Sbf[si[0] % 3]
        Sb_ = Sbf[(si[0] + 1) % 3]
        Sc = Sbf[(si[0] + 2) % 3]
        si[0] += 2
        for j, (Scur, Snext) in enumerate(((Sa, Sb_), (Sb_, Sc))):
            for h in range(4):
                kb.op('pe', lambda E: E.matmul(pV[:, h, :], lhsT=P[:, h, :], rhs=VB[:, h, :], start=True, stop=False), r=[P, VB], w=[pV])
                kb.op('pe', lambda E: E.matmul(pV[:, h, :], lhsT=negWT[:, h, :], rhs=Scur[:, h, :], start=False, stop=True),
                      r=[negWT, Scur], w=[pV])
            yield
            r0 = 64 * j
            kb.op('act', lambda E: E.copy(out=vnew[r0:r0 + 64, :, :], in_=pV[r0:r0 + 64, :, :]), r=[pV], w=[vnew])
            yield
            mm4(pdS, lambda h: Kd[j][:, h, :], lambda h: vnew[:, h, :], [Kd[j], vnew])
            yield
            for h in range(4):
                kb.op('dve', lambda E: E.scalar_tensor_tensor(out=Sf[:, h, :], in0=Sf[:, h, :], scalar=s_[:, 56 + 4 * j + h:57 + 4 * j + h],
                                                              in1=pdS[:, h, :], op0=ALU.mult, op1=ALU.add), r=[Sf, s_, pdS], w=[Sf])
            kb.op('act', lambda E: E.copy(out=Snext[:], in_=Sf[:]), r=[Sf], w=[Snext])
            yield
        for h in range(4):
            kb.op('pe', lambda E: E.matmul(pO[:, h, :], lhsT=QG[0][:, h, :], rhs=Sa[:, h, :], start=True, stop=False), r=[QG[0], Sa], w=[pO])
            kb.op('pe', lambda E: E.matmul(pO[:, h, :], lhsT=QG[1][:, h, :], rhs=Sb_[:, h, :], start=False, stop=False), r=[QG[1], Sb_], w=[pO])
            kb.op('pe', lambda E: E.matmul(pO[:, h, :], lhsT=aqkT[:, h, :], rhs=vnew[:, h, :], start=False, stop=True), r=[aqkT, vnew], w=[pO])
        yield
        kb.op('act', lambda E: E.copy(out=osb[:], in_=pO[:]), r=[pO], w=[osb])
        kb.op('pool', lambda E: E.tensor_tensor(out=sq[:], in0=osb[:], in1=osb[:], op=ALU.mult), r=[osb], w=[sq])
        kb.op('dve', lambda E: E.tensor_reduce(out=G[:, 0:4], in_=sq[:], axis=AX.X, op=ALU.add), r=[sq, G], w=[G])
        kb.op('act', lambda E: E.activation(out=G[:, 0:4], in_=G[:, 0:4], func=AF.Sqrt, scale=1.0 / 128, bias=EPS), r=[G], w=[G])
        kb.op('dve', lambda E: E.reciprocal(out=G[:, 0:4], in_=G[:, 0:4]), r=[G], w=[G])
        yield
        kb.op('act', lambda E: E.activation(out=sgt[:], in_=gt[b][:], func=AF.Silu), r=[gt[b]], w=[sgt])
        kb.op('dve', lambda E: E.tensor_tensor(out=osb[:], in0=osb[:], in1=B4(G[:, 0:4]), op=ALU.mult), r=[osb, G], w=[osb])
        kb.op('pool', lambda E: E.tensor_tensor(out=osb[:], in0=osb[:], in1=M4(gw[:]), op=ALU.mult), r=[osb, gw], w=[osb])
        kb.op('dve', lambda E: E.tensor_tensor(out=oa[b][:], in0=osb[:], in1=sgt[:].rearrange("p (h d) -> p h d", h=4), op=ALU.mult),
              r=[osb, sgt], w=[oa[b]])
        for h in range(4):
            kb.op('pe', lambda E: E.transpose(out=pB[0][:, h, :], in_=oa[b][:, h, :], identity=ident[:]), r=[oa[b], ident], w=[pB[0]])
        kb.op('act', lambda E: E.copy(out=oaT[b][:], in_=pB[0][:, 0:4, :]), r=[pB[0]], w=[oaT[b]])
        kb.dma('sp', io['ocatT'][0:512, t * 128:(t + 1) * 128].rearrange("(c p) t -> p c t", p=128), oaT[b][:], r=[oaT[b]], w=['ocatT_a'])

    def to_boundary(g):
        for r in g:
            if r == 'B':
                return

    cur = tile(0)
    to_boundary(cur)
    for t in range(NT):
        nxt = tile(t + 1) if t + 1 < NT else None
        cur_done, nxt_done = False, nxt is None
        while not (cur_done and nxt_done):
            if not cur_done:
                try:
                    next(cur)
                except StopIteration:
                    cur_done = True
            if not nxt_done:
                if next(nxt) == 'B':
                    nxt_done = True
        cur = nxt
    ph.close()


def rope16(kb, R, G, cs, tmp, e1='dve', e2='pool'):
    c = cs[:, 0:8].unsqueeze(1).to_broadcast([128, G, 8])
    sn = cs[:, 8:16].unsqueeze(1).to_broadcast([128, G, 8])
    x1 = R[:, 0:G, 0:8]
    x2 = R[:, 0:G, 8:16]
    kb.op(e1, lambda E: E.tensor_tensor(out=tmp[:, 0:G, 0:8], in0=x1, in1=c, op=ALU.mult), r=[R, cs], w=[(tmp, 0)])
    kb.op(e2, lambda E: E.tensor_tensor(out=tmp[:, 0:G, 8:16], in0=x2, in1=sn, op=ALU.mult), r=[R, cs], w=[(tmp, 1)])
    kb.op(e1, lambda E: E.tensor_tensor(out=tmp[:, 0:G, 16:24], in0=x2, in1=c, op=ALU.mult), r=[R, cs], w=[(tmp, 2)])
    kb.op(e2, lambda E: E.tensor_tensor(out=tmp[:, 0:G, 24:32], in0=x1, in1=sn, op=ALU.mult), r=[R, cs], w=[(tmp, 3)])
    kb.op(e1, lambda E: E.tensor_tensor(out=x1, in0=tmp[:, 0:G, 0:8], in1=tmp[:, 0:G, 8:16], op=ALU.subtract),
          r=[(tmp, 0), (tmp, 1), (tmp, 2), (tmp, 3)], w=[R])
    kb.op(e1, lambda E: E.tensor_tensor(out=x2, in0=tmp[:, 0:G, 16:24], in1=tmp[:, 0:G, 24:32], op=ALU.add),
          r=[(tmp, 0), (tmp, 1), (tmp, 2), (tmp, 3)], w=[R])


def rms_groups(kb, src3, G, dst3, sq, ss, wt, e_sq='pool'):
    (src_ap, src_keys) = src3
    (dst_ap, dst_keys) = dst3
    kb.op(e_sq, lambda E: E.tensor_tensor(out=sq[:, 0:G, :], in0=src_ap, in1=src_ap, op=ALU.mult), r=src_keys, w=[sq])
    kb.op('dve', lambda E: E.tensor_reduce(out=ss[:, 0:G], in_=sq[:, 0:G, :], axis=AX.X, op=ALU.add), r=[sq], w=[ss])
    kb.op('act', lambda E: E.activation(out=ss[:, 0:G], in_=ss[:, 0:G], func=AF.Sqrt, scale=1.0 / 64, bias=EPS), r=[ss], w=[ss])
    kb.op('dve', lambda E: E.reciprocal(out=ss[:, 0:G], in_=ss[:, 0:G]), r=[ss], w=[ss])
    kb.op('dve', lambda E: E.tensor_tensor(out=dst_ap, in0=src_ap, in1=ss[:, 0:G].unsqueeze(2).to_broadcast([128, G, 64]), op=ALU.mult),
          r=src_keys + [ss], w=dst_keys)
    kb.op('pool', lambda E: E.tensor_tensor(out=dst_ap, in0=dst_ap, in1=wt[:].unsqueeze(1).to_broadcast([128, G, 64]), op=ALU.mult),
          r=dst_keys + [wt], w=dst_keys)


def phase_nsa(kb, io):
    nc = kb.nc
    ph = Phase(kb, "ns")
    ident = ph.sb("ident", [128, 128], BF16)
    tril = ph.sb("tril", [128, 128], BF16)
    far = ph.sb("far", [128, 128], BF16)
    cmask = ph.sb("cmask", [128, 9, 512], BF16)
    kvT = ph.sb("kvT", [128, 4, S], BF16)
    vs1 = ph.sb("vs1", [128, NT, 2, 65], BF16)
    vw1 = ph.sb("vw1", [128, NT, 2, 65], BF16)
    kcmpT = ph.sb("kcmpT", [64, 2, 256], BF16)
    rhs_cmp = ph.sb("rhs_cmp", [128, 2, 2, 128], BF16)
    kb.dma('sp', ident[:], io['c_ident'][:, :], w=[ident])
    kb.dma('sp', tril[:], io['c_tril'][:, :], w=[tril])
    kb.dma('sp', far[:], io['c_far'][:, :], w=[far])
    for m in range(9):
        kb.dma('sp', cmask[:, m, :], io['c_cmask'][m, :, :], w=[cmask])
    for g in range(2):
        kb.dma('sp', kvT[64:128, g, :], io['c_E'][:, :], w=[(kvT, 'E')])
    kb.op('pool', lambda E: E.memset(vs1[:], 1.0), w=[vs1])
    kb.op('pool', lambda E: E.memset(vw1[:], 1.0), w=[vw1])
    kb.op('pool', lambda E: E.memset(kcmpT[:], 0.0), w=[kcmpT])
    kb.op('pool', lambda E: E.memset(rhs_cmp[:], 0.0), w=[rhs_cmp])
    for bt in range(2):
        for g in range(2):
            kb.dma('sp', rhs_cmp[:, bt, g, 64:128], io['c_ovl'][:, bt, :], r=[rhs_cmp], w=[rhs_cmp])

    pp = Phase(kb, "np")
    kcT = pp.sb("kcT", [64, 4, S], BF16)
    ksw = pp.sb("ksw", [128, 64], F32)
    kww = pp.sb("kww", [128, 64], F32)
    kcw = pp.sb("kcw", [128, 64], F32)
    kb.dma('sp', ksw[:], io['nsa_ks_norm_w'].partition_broadcast(128), w=[ksw])
    kb.dma('sp', kww[:], io['nsa_kw_norm_w'].partition_broadcast(128), w=[kww])
    kb.dma('sp', kcw[:], io['nsa_kc_norm_w'].partition_broadcast(128), w=[kcw])
    kvb = pp.sbn("kvb", [128, 768], F32, 2)
    cst = pp.sbn("cst", [128, 16], F32, 2)
    R = pp.sbn("R", [128, 6, 64], F32, 2)
    sq = pp.sb("sq", [128, 2, 64], F32)
    ss = pp.sb("ss", [128, 2], F32)
    tmp = pp.sb("tmp", [128, 6, 32], F32)
    k16 = pp.sbn("k16", [128, 8, 64], BF16, 2)
    pT8 = pp.psn("pT8", [128, 8, 128], BF16, 2)
    for t in range(NT):
        b = t % 2
        kv = kvb[b]
        Rb = R[b]
        kb.dma('sp', kv[:], io['tm'][t * 128:(t + 1) * 128, KC_OFF:KC_OFF + 768], r=['tm'], w=[kv])
        kb.dma('sp', cst[b][:], io['c_rope'][t * 128:(t + 1) * 128, :], w=[cst[b]])
        v3 = lambda off: kv[:, off:off + 128].rearrange("p (g d) -> p g d", g=2)
        kb.op('pool', lambda E: E.tensor_copy(out=Rb[:, 0:2, :], in_=v3(0)), r=[kv], w=[Rb])
        rms_groups(kb, (v3(256), [kv]), 2, (Rb[:, 2:4, :], [Rb]), sq, ss, ksw)
        rms_groups(kb, (v3(512), [kv]), 2, (Rb[:, 4:6, :], [Rb]), sq, ss, kww)
        rope16(kb, Rb, 6, cst[b], tmp)
        kk = k16[b]
        kb.op('act', lambda E: E.copy(out=kk[:, 0:6, :], in_=Rb[:]), r=[Rb], w=[kk])
        kb.op('pool', lambda E: E.tensor_copy(out=kk[:, 6:8, :], in_=v3(128)), r=[kv], w=[kk])
        kb.op('dve', lambda E: E.tensor_copy(out=vs1[:, t, :, 0:64], in_=v3(384)), r=[kv], w=[vs1])
        kb.op('pool', lambda E: E.tensor_copy(out=vw1[:, t, :, 0:64], in_=v3(640)), r=[kv], w=[vw1])
        p8 = pT8[b]
        for i in range(8):
            kb.op('pe', lambda E: E.transpose(out=p8[0:64, i, :], in_=kk[:, i, :], identity=ident[:]), r=[kk, ident], w=[p8])
        kb.op('act', lambda E: E.copy(out=kvT[0:64, :, t * 128:(t + 1) * 128], in_=p8[0:64, 2:6, :]), r=[p8], w=[(kvT, 'k')])
        kb.op('dve', lambda E: E.tensor_copy(out=kcT[0:64, 0:2, t * 128:(t + 1) * 128], in_=p8[0:64, 0:2, :]), r=[p8], w=[kcT])
        kb.op('dve', lambda E: E.tensor_copy(out=kcT[0:64, 2:4, t * 128:(t + 1) * 128], in_=p8[0:64, 6:8, :]), r=[p8], w=[kcT])
    w1f = pp.sb("w1f", [64, 32, 64], F32)
    w1b = pp.sbn("w1b", [64, 32, 64], BF16, 2)
    w2f = pp.sb("w2f", [64, 64], F32)
    w2b = pp.sbn("w2b", [64, 64], BF16, 2)
    posf = pp.sb("posf", [64, 32], F32)
    pos2 = pp.sbn("pos2", [64, 32, 2], BF16, 2)
    bias = pp.sb("bias", [64, 2], F32)
    h1T = pp.sb("h1T", [64, 256], BF16)
    o2 = pp.sb("o2", [128, 1, 64], F32)
    o2n = pp.sb("o2n", [128, 1, 64], F32)
    kcn = pp.sb("kcn", [128, 64], BF16)
    pH = pp.ps("pH", [128, 512], F32)
    pB_ = pp.ps("pBi", [128, 512], F32)
    pO2 = pp.ps("pO2", [128, 512], F32)
    kb.op('pool', lambda E: E.memset(h1T[:], 0.0), w=[h1T])
    for kind, (n1, n2, npos) in enumerate((('nsa_cmp_k_w1', 'nsa_cmp_k_w2', 'nsa_cmp_pos_k'), ('nsa_cmp_v_w1', 'nsa_cmp_v_w2', 'nsa_cmp_pos_v'))):
        kb.dma('sp', w1f[:], io[n1].rearrange("(l d) o -> d l o", d=64), w=[w1f])
        kb.op('dve', lambda E: E.tensor_copy(out=w1b[kind][:], in_=w1f[:]), r=[w1f], w=[w1b[kind]])
        kb.dma('sp', w2f[:], io[n2][:, :], w=[w2f])
        kb.op('dve', lambda E: E.tensor_copy(out=w2b[kind][:], in_=w2f[:]), r=[w2f], w=[w2b[kind]])
        kb.dma('sp', posf[:], io[npos].rearrange("l d -> d l"), w=[posf], allow_slow_non_contiguous=True)
        for j in range(2):
            kb.op('dve', lambda E: E.tensor_copy(out=pos2[kind][:, :, j], in_=posf[:]), r=[posf], w=[pos2[kind]])
        for l in range(32):
            kb.op('pe', lambda E: E.matmul(pB_[0:64, 0:2], lhsT=w1b[kind][:, l, :], rhs=pos2[kind][:, l, :], start=(l == 0), stop=(l == 31)),
                  r=[w1b[kind], pos2[kind]], w=[pB_])
        kb.op('dve', lambda E: E.tensor_copy(out=bias[:], in_=pB_[0:64, 0:2]), r=[pB_], w=[bias])
        for g in range(2):
            ki = kind * 2 + g
            for l in range(32):
                kb.op('pe', lambda E: E.matmul(pH[0:64, 0:255], lhsT=w1b[kind][:, l, :], rhs=kcT[0:64, ki, l:l + 16 * 254 + 1:16],
                                               start=(l == 0), stop=(l == 31)), r=[w1b[kind], kcT], w=[pH])
            kb.op('act', lambda E: E.activation(out=h1T[:, 0:255], in_=pH[0:64, 0:255], func=AF.Silu, bias=bias[:, 0:1]),
                  r=[pH, bias], w=[h1T])
            for bt in range(2):
                kb.op('pe', lambda E: E.matmul(pO2[:, 0:64], lhsT=h1T[:, bt * 128:(bt + 1) * 128], rhs=w2b[kind][:], start=True, stop=True),
                      r=[h1T, w2b[kind]], w=[pO2])
                if kind == 0:
                    kb.op('act', lambda E: E.copy(out=o2[:, 0, :], in_=pO2[:, 0:64]), r=[pO2], w=[o2])
                    rms_groups(kb, (o2[:], [o2]), 1, (o2n[:], [o2n]), sq, ss, kcw)
                    kb.op('act', lambda E: E.copy(out=kcn[:], in_=o2n[:, 0, :]), r=[o2n], w=[kcn])
                    p8 = pT8[0]
                    kb.op('pe', lambda E: E.transpose(out=p8[0:64, 0, :], in_=kcn[:], identity=ident[:]), r=[kcn, ident], w=[p8])
                    kb.op('act', lambda E: E.copy(out=kcmpT[:, g, bt * 128:(bt + 1) * 128], in_=p8[0:64, 0, :]), r=[p8], w=[kcmpT])
                else:
                    kb.op('act', lambda E: E.copy(out=rhs_cmp[:, bt, g, 0:64], in_=pO2[:, 0:64]), r=[pO2], w=[rhs_cmp])
    pp.close()

    pa = Phase(kb, "na")
    NPT = 6
    PTb = pa.sbn("PT", [128, 512], BF16, NPT)
    pti = [0]
    qaug = pa.sbn("qaug", [128, 8, 512], BF16, 2)
    qnw = pa.sb("qnw", [128, 64], F32)
    kb.dma('sp', qnw[:], io['nsa_q_norm_w'].partition_broadcast(128), w=[qnw])
    kb.op('dve', lambda E: E.tensor_scalar(out=qnw[:], in0=qnw[:], scalar1=0.125, scalar2=None, op0=ALU.mult), r=[qnw], w=[qnw])
    qf = pa.sbn("qf", [128, 512], F32, 2)
    cst = pa.sbn("cst", [128, 16], F32, 2)
    Rq = pa.sb("Rq", [128, 8, 64], F32)
    sq = pa.sb("sq", [128, 8, 64], F32)
    ss = pa.sb("ss", [128, 8], F32)
    tmp = pa.sb("tmp", [128, 8, 32], F32)
    qa = pa.sbn("qa", [128, 8, 128], BF16, 2)
    gts = pa.sb("gts", [128, 4, 24], F32)
    Ab = pa.sbn("Ab", [128, 64], F32, 4)
    Bb = pa.sbn("Bb", [128, 64], F32, 4)
    ob = pa.sb("ob", [128, 4, 8, 64], F32)
    impacc = pa.sb("impacc", [128, 4, 64], F32)
    tmpi = pa.sb("tmpi", [128, 4, 64], F32)
    tmpo = pa.sb("tmpo", [128, 4, 64], F32)
    scr = pa.sb("scr", [128, 64], F32)
    scr2 = pa.sb("scr2", [128, 64], F32)
    m8 = pa.sb("m8", [128, 16], F32)
    nst = pa.sbn("nst", [128, 128], BF16, 2)
    fs = pa.sb("fs", [128, 12], F32)
    obb = pa.sbn("obb", [128, 512], BF16, 2)
    obT = pa.sbn("obT", [128, 4, 128], BF16, 2)
    sc_ps = pa.psn("sc", [128, 512], F32, 2)
    oC = pa.ps("oC", [128, 4, 128], F32)
    oW = pa.psn("oW", [128, 4, 128], F32, 2)
    oS = pa.psn("oS", [128, 4, 128], F32, 2)
    pTr = pa.psn("pTr", [128, 8, 128], BF16, 1)
    for i in range(2):
        kb.op('pool', lambda E: E.memset(qa[i][:], 0.0), w=[qa[i]])
        kb.op('pool', lambda E: E.memset(nst[i][:], 0.0), w=[nst[i]])
    tri = 0

    def finalize(oX, h, br, first, cmp=False):
        if cmp:
            kb.op('dve', lambda E: E.tensor_reduce(out=fs[:, 0:4], in_=oX[:, :, 64:128], axis=AX.X, op=ALU.add), r=[oX], w=[fs])
            kb.op('dve', lambda E: E.tensor_scalar(out=fs[:, 0:4], in0=fs[:, 0:4], scalar1=0.5, scalar2=1e-30, op0=ALU.mult, op1=ALU.add),
                  r=[fs], w=[fs])
        else:
            kb.op('dve', lambda E: E.tensor_scalar(out=fs[:, 0:4], in0=oX[:, :, 64], scalar1=1e-30, scalar2=None, op0=ALU.add), r=[oX], w=[fs])
        kb.op('dve', lambda E: E.reciprocal(out=fs[:, 4:8], in_=fs[:, 0:4]), r=[fs], w=[fs])
        if cmp:
            dsti = impacc if h % 4 == 0 else tmpi
            kb.op('dve', lambda E: E.tensor_tensor(out=dsti[:], in0=oX[:, :, 64:128], in1=fs[:, 4:8].unsqueeze(2).to_broadcast([128, 4, 64]),
                                                   op=ALU.mult), r=[oX, fs], w=[dsti])
            if h % 4 != 0:
                kb.op('pool', lambda E: E.tensor_tensor(out=impacc[:], in0=impacc[:], in1=tmpi[:], op=ALU.add), r=[impacc, tmpi], w=[impacc])
        kb.op('dve', lambda E: E.tensor_tensor(out=fs[:, 8:12], in0=fs[:, 4:8], in1=gts[:, :, h * 3 + br], op=ALU.mult), r=[fs, gts], w=[fs])
        dst = ob[:, :, h, :] if first else tmpo[:]
        kb.op('dve', lambda E: E.tensor_tensor(out=dst, in0=oX[:, :, 0:64], in1=fs[:, 8:12].unsqueeze(2).to_broadcast([128, 4, 64]),
                                               op=ALU.mult), r=[oX, fs], w=[(ob, h) if first else tmpo])
        if not first:
            kb.op('pool', lambda E: E.tensor_tensor(out=ob[:, :, h, :], in0=ob[:, :, h, :], in1=tmpo[:], op=ALU.add),
                  r=[(ob, h), tmpo], w=[(ob, h)])

    for s in range(S // 512):
        qs_ = qaug[s % 2]
        for t4 in range(4):
            t = 4 * s + t4
            b = t % 2
            q = qf[b]
            kb.dma('sp', q[:], io['tm'][t * 128:(t + 1) * 128, NQ_OFF:NQ_OFF + 512], r=['tm'], w=[q])
            kb.dma('sp', gts[:, t4, :], io['tm'][t * 128:(t + 1) * 128, NG_OFF:NG_OFF + 24], r=['tm'], w=[gts])
            kb.dma('sp', cst[b][:], io['c_rope'][t * 128:(t + 1) * 128, :], w=[cst[b]])
            kb.dma('sp', Ab[t4][:], io['c_A'][t, :, :], w=[Ab[t4]])
            kb.dma('sp', Bb[t4][:], io['c_B'][t, :, :], w=[Bb[t4]])
            kb.op('act', lambda E: E.activation(out=gts[:, t4, :], in_=gts[:, t4, :], func=AF.Exp, scale=-1.0), r=[gts], w=[gts])
            kb.op('dve', lambda E: E.tensor_scalar(out=gts[:, t4, :], in0=gts[:, t4, :], scalar1=1.0, scalar2=None, op0=ALU.add), r=[gts], w=[gts])
            kb.op('dve', lambda E: E.reciprocal(out=gts[:, t4, :], in_=gts[:, t4, :]), r=[gts], w=[gts])
            q3 = q[:].rearrange("p (h d) -> p h d", h=8)
            rms_groups(kb, (q3, [q]), 8, (Rq[:], [Rq]), sq, ss, qnw)
            rope16(kb, Rq, 8, cst[b], tmp)
            qab = qa[b]
            kb.op('act', lambda E: E.copy(out=qab[:, :, 0:64], in_=Rq[:]), r=[Rq], w=[qab])
            pt_ = pTr[0]
            tri += 1
            for h in range(8):
                kb.op('pe', lambda E: E.transpose(out=pt_[:, h, :], in_=qab[:, h, :], identity=ident[:]), r=[qab, ident], w=[pt_])
            kb.op('act', lambda E: E.copy(out=qs_[:, :, t4 * 128:(t4 + 1) * 128], in_=pt_[:]), r=[pt_], w=[qs_])
        nbt = 1 if s < 4 else 2
        for g in range(2):
            for h in range(4 * g, 4 * g + 4):
                specs = []
                for bt in range(nbt):
                    m = (s if s <= 4 else None) if bt == 0 else 5 + (s - 4)
                    masks = [] if m is None else [(0, 512, cmask[:, m, :])]
                    specs.append(dict(lhsT=kcmpT[0:64, g, bt * 128:(bt + 1) * 128],
                                      rhs_fn=lambda q0, q1, h=h: qs_[0:64, h, q0 * 128:q1 * 128], qt0=0, qt1=4, masks=masks, mk=[cmask],
                                      v_fn=lambda qt, bt=bt, g=g: rhs_cmp[:, bt, g, :], nk=128, rk=[kcmpT], rv=[rhs_cmp]))
                attn_block(kb, lambda qt: (oC[:, qt, :], oC, 'C'), PTb, pti, sc_ps, specs, ident, [qs_])
                finalize(oC, h, 0, True, cmp=True)
            for qt in range(4):
                ns_ = nst[qt % 2]
                kb.op('dve', lambda E: E.tensor_tensor(out=scr[:], in0=impacc[:, qt, :], in1=Ab[qt][:], op=ALU.mult), r=[impacc, Ab[qt]], w=[scr])
                kb.op('dve', lambda E: E.tensor_tensor(out=scr[:], in0=scr[:], in1=Bb[qt][:], op=ALU.add), r=[scr, Bb[qt]], w=[scr])
                kb.op('dve', lambda E: E.max(out=m8[:, 0:8], in_=scr[:]), r=[scr], w=[(m8, 0)])
                kb.op('dve', lambda E: E.match_replace(out=scr2[:], in_to_replace=m8[:, 0:8], in_values=scr[:], imm_value=-1e30),
                      r=[scr, (m8, 0)], w=[scr2])
                kb.op('dve', lambda E: E.max(out=m8[:, 8:16], in_=scr2[:]), r=[scr2], w=[(m8, 1)])
                kb.op('dve', lambda E: E.tensor_scalar(out=ns_[:, 64:128], in0=scr[:], scalar1=m8[:, 15:16], scalar2=1.0, op0=ALU.is_ge,
                                                       op1=ALU.subtract), r=[scr, (m8, 1)], w=[ns_])
                pt_ = pTr[0]
                tri += 1
                kb.op('pe', lambda E: E.transpose(out=pt_[:, 0, :], in_=ns_[:], identity=ident[:]), r=[ns_, ident], w=[pt_])
                for h in range(4 * g, 4 * g + 4):
                    if h % 2 == 0:
                        kb.op('act', lambda E: E.copy(out=qs_[64:128, h, qt * 128:(qt + 1) * 128], in_=pt_[64:128, 0, :]), r=[pt_], w=[qs_])
                    else:
                        kb.op('dve', lambda E: E.tensor_copy(out=qs_[64:128, h, qt * 128:(qt + 1) * 128], in_=pt_[64:128, 0, :]), r=[pt_], w=[qs_])
        for h in range(8):
            g = h // 4
            specs = []
            for kt in range(max(0, 4 * s - 4), 4 * s + 4):
                lo = max(kt - 4 * s, 0)
                hi = min(kt + 4 - 4 * s, 3)
                masks = []
                if kt >= 4 * s:
                    masks.append(((kt - 4 * s - lo) * 128, 128, tril[:]))
                if kt + 4 <= 4 * s + 3:
                    masks.append(((kt + 4 - 4 * s - lo) * 128, 128, far[:]))
                specs.append(dict(lhsT=kvT[0:64, 2 + g, kt * 128:(kt + 1) * 128],
                                  rhs_fn=lambda q0, q1, h=h: qs_[0:64, h, q0 * 128:q1 * 128], qt0=lo, qt1=hi + 1, masks=masks, mk=[tril, far],
                                  v_fn=lambda qt, kt=kt, g=g: vw1[:, kt, g, :], nk=128, rk=[(kvT, 'k')], rv=[vw1]))
            oW_ = oW[h % 2]
            attn_block(kb, lambda qt: (oW_[:, qt, 0:65], oW_, 'W'), PTb, pti, sc_ps, specs, ident, [qs_])
            finalize(oW_, h, 2, False)
            specs = []
            for kt in range(0, 4 * s + 4):
                lo = max(kt - 4 * s, 0)
                masks = [(0, 128, tril[:])] if kt >= 4 * s else []
                specs.append(dict(lhsT=kvT[:, g, kt * 128:(kt + 1) * 128],
                                  rhs_fn=lambda q0, q1, h=h: qs_[:, h, q0 * 128:q1 * 128], qt0=lo, qt1=4, masks=masks, mk=[tril],
                                  v_fn=lambda qt, kt=kt, g=g: vs1[:, kt, g, :], nk=128, rk=[(kvT, 'k'), (kvT, 'E')], rv=[vs1]))
            oS_ = oS[h % 2]
            attn_block(kb, lambda qt: (oS_[:, qt, 0:65], oS_, 'S'), PTb, pti, sc_ps, specs, ident, [qs_])
            finalize(oS_, h, 1, False)
        for qt in range(4):
            t = 4 * s + qt
            b = t % 2
            kb.op('act', lambda E: E.copy(out=obb[b][:], in_=ob[:, qt, :, :].rearrange("p h d -> p (h d)")), r=[(ob, h) for h in range(8)], w=[obb[b]])
            pt_ = pTr[0]
            tri += 1
            for c in range(4):
                kb.op('pe', lambda E: E.transpose(out=pt_[:, c, :], in_=obb[b][:, c * 128:(c + 1) * 128], identity=ident[:]), r=[obb[b], ident], w=[pt_])
            kb.op('act', lambda E: E.copy(out=obT[b][:], in_=pt_[:, 0:4, :]), r=[pt_], w=[obT[b]])
            kb.dma('sp', io['ocatT'][512:1024, t * 128:(t + 1) * 128].rearrange("(c p) t -> p c t", p=128), obT[b][:], r=[obT[b]], w=['ocatT_b'])
    pa.close()
    ph.close()


def phase_ffn(kb, io):
    nc = kb.nc
    ph = Phase(kb, "f1")
    ident = ph.sb("ident", [128, 128], BF16)
    kb.dma('sp', ident[:], io['c_ident'][:, :], w=[ident])
    woutb = ph.sb("woutb", [128, 12, D], BF16)
    wupb = ph.sb("wupb", [128, 8, 2 * FF], BF16)
    gam = ph.sb("gam", [128, 8], F32)
    cw = ph.sb("cw", [128, 3, 44], F32)
    kb.dma('sp', gam[:], io['ffn_norm_w'].rearrange("(c p) -> p c", p=128), w=[gam], allow_slow_non_contiguous=True)
    for j in range(3):
        kb.dma('sp', cw[:, j, :], io['ffn_conv_w'][j, :].rearrange("(c p) -> p c", p=128), w=[cw], allow_slow_non_contiguous=True)
    load_cast_weight(kb, ph, woutb, io['w_out'], 12, D)
    load_cast_weight(kb, ph, wupb, io['ffn_w_up'], 8, 2 * FF, gam=gam, stage_cols=1408)

    oT = ph.sbn("oT", [128, 12, 128], BF16, 2)
    xt = ph.sbn("xt", [128, D], F32, 2)
    hs = ph.sbn("hs", [128, D], F32, 2)
    junk = ph.sb("junk", [128, D], BF16)
    ss = ph.sbn("ss", [128, 1], F32, 2)
    rs = ph.sbn("rs", [128, 1], F32, 2)
    hn = ph.sbn("hn", [128, D], BF16, 2)
    hnT = ph.sbn("hnT", [128, 8, 512], BF16, 2)
    ug = ph.sbn("ug", [128, 514], F32, 2)
    uv = ph.sbn("uv", [128, 514], F32, 2)
    ag = ph.sbn("ag", [128, 512], F32, 2)
    av = ph.sbn("av", [128, 512], F32, 2)
    sg = ph.sbn("sg", [128, 512], F32, 2)
    act = ph.sbn("act", [128, 512], BF16, 3)
    halo = ph.sb("halo", [128, 44, 2], F32)
    pH = ph.psn("pH", [128, 512], F32, 2)
    pT = ph.ps("pT", [128, 8, 128], BF16)
    pU = ph.psn("pU", [128, 512], F32, 4)
    kb.op('pool', lambda E: E.memset(halo[:], 0.0), w=[halo])

    def conv3(ps_, dst, src, fb):
        kb.op('act', lambda E: E.activation(out=dst[:], in_=ps_[:], func=AF.Copy, scale=cw[:, 2, fb:fb + 1]), r=[ps_, cw], w=[dst])
        for j in range(2):
            kb.op('dve', lambda E: E.scalar_tensor_tensor(out=dst[:], in0=src[:, j:j + 512], scalar=cw[:, j, fb:fb + 1], in1=dst[:],
                                                       op0=ALU.mult, op1=ALU.add), r=[(src, 'b'), (src, 'h'), cw, dst], w=[dst])

    for s in range(S // 512):
        hT = hnT[s % 2]
        for t4 in range(4):
            t = s * 4 + t4
            b = t % 2
            kb.dma('sp', oT[b][:], io['ocatT'][:, t * 128:(t + 1) * 128].rearrange("(c p) t -> p c t", p=128),
                   r=['ocatT_a', 'ocatT_b', 'ocatT_c'], w=[oT[b]])
            kb.dma('sp', xt[b][:], io['x'][t * 128:(t + 1) * 128, :], w=[xt[b]])
            for half in range(2):
                p = pH[half]
                for c in range(12):
                    kb.op('pe', lambda E: E.matmul(p[:], lhsT=oT[b][:, c, :], rhs=woutb[:, c, half * 512:(half + 1) * 512],
                                                   start=(c == 0), stop=(c == 11)), r=[oT[b], (woutb, c)], w=[p])
                kb.op('dve', lambda E: E.tensor_tensor(out=hs[b][:, half * 512:(half + 1) * 512], in0=p[:],
                                                       in1=xt[b][:, half * 512:(half + 1) * 512], op=ALU.add),
                      r=[p, xt[b]], w=[(hs[b], half)])
            kb.dma('sp', io['h_s'][t * 128:(t + 1) * 128, :], hs[b][:], r=[(hs[b], 0), (hs[b], 1)], w=['h_s'])
            rms_rstd(kb, hs[b][:], junk, ss[b], rs[b], D, [(hs[b], 0), (hs[b], 1)])
            kb.op('act', lambda E: E.activation(out=hn[b][:], in_=hs[b][:], func=AF.Copy, scale=rs[b][:, 0:1]),
                  r=[(hs[b], 0), (hs[b], 1), rs[b]], w=[hn[b]])
            for c in range(8):
                kb.op('pe', lambda E: E.transpose(out=pT[:, c, :], in_=hn[b][:, c * 128:(c + 1) * 128], identity=ident[:]),
                      r=[hn[b], ident], w=[pT])
            kb.op('act', lambda E: E.copy(out=hT[:, :, t4 * 128:(t4 + 1) * 128], in_=pT[:]), r=[pT], w=[hT])
        for fb in range(22):
            k2 = fb % 2
            pg = pU[2 * k2]
            pv = pU[2 * k2 + 1]
            for (p_, f0) in ((pg, fb), (pv, 22 + fb)):
                for c in range(8):
                    kb.op('pe', lambda E: E.matmul(p_[:], lhsT=wupb[:, c, f0 * 128:(f0 + 1) * 128], rhs=hT[:, c, :],
                                                   start=(c == 0), stop=(c == 7)), r=[hT, (wupb, c)], w=[p_])
            g_, v_ = ug[k2], uv[k2]
            kb.op('act', lambda E: E.copy(out=g_[:, 2:514], in_=pg[:]), r=[pg], w=[(g_, 'b')])
            kb.op('act', lambda E: E.copy(out=v_[:, 2:514], in_=pv[:]), r=[pv], w=[(v_, 'b')])
            for (u_, f0) in ((g_, fb), (v_, 22 + fb)):
                kb.op('pool', lambda E: E.tensor_copy(out=u_[:, 0:2], in_=halo[:, f0, :]), r=[(halo, f0)], w=[(u_, 'h')])
                kb.op('pool', lambda E: E.tensor_copy(out=halo[:, f0, :], in_=u_[:, 512:514]), r=[(u_, 'b')], w=[(halo, f0)])
            conv3(pg, ag[k2], g_, fb)
            conv3(pv, av[k2], v_, 22 + fb)
            kb.op('act', lambda E: E.activation(out=sg[k2][:], in_=ag[k2][:], func=AF.Silu), r=[ag[k2]], w=[sg[k2]])
            a_ = act[fb % 3]
            kb.op('dve', lambda E: E.tensor_tensor(out=a_[:], in0=sg[k2][:], in1=av[k2][:], op=ALU.mult), r=[sg[k2], av[k2]], w=[a_])
            kb.dma('sp', io['actT'][fb * 128:(fb + 1) * 128, s * 512:(s + 1) * 512], a_[:], r=[a_], w=['actT'])
    ph.close()

    ph = Phase(kb, "f2")
    wdnb = ph.sb("wdnb", [128, 22, D], BF16)
    load_cast_weight(kb, ph, wdnb, io['ffn_w_down'], 22, D)
    aT = ph.sbn("aT", [128, 22, 128], BF16, 2)
    hs = ph.sbn("hs", [128, D], F32, 2)
    ot = ph.sbn("ot", [128, D], F32, 2)
    pD = ph.psn("pD", [128, 512], F32, 4)
    for t in range(NT):
        b = t % 2
        kb.dma('sp', aT[b][:], io['actT'][:, t * 128:(t + 1) * 128].rearrange("(c p) t -> p c t", p=128), r=['actT'], w=[aT[b]])
        kb.dma('sp', hs[b][:], io['h_s'][t * 128:(t + 1) * 128, :], r=['h_s'], w=[hs[b]])
        for half in range(2):
            p = pD[2 * b + half]
            for c in range(22):
                kb.op('pe', lambda E: E.matmul(p[:], lhsT=aT[b][:, c, :], rhs=wdnb[:, c, half * 512:(half + 1) * 512],
                                               start=(c == 0), stop=(c == 21)), r=[aT[b], (wdnb, c)], w=[p])
            kb.op('dve', lambda E: E.tensor_tensor(out=ot[b][:, half * 512:(half + 1) * 512], in0=p[:],
                                                   in1=hs[b][:, half * 512:(half + 1) * 512], op=ALU.add),
                  r=[p, hs[b]], w=[(ot[b], half)])
        kb.dma('sp', io['out'][t * 128:(t + 1) * 128, :], ot[b][:], r=[(ot[b], 0), (ot[b], 1)], w=['out'])
    ph.close()


W_NAMES = ['attn_norm_w', 'mem_norm_w', 'w_in', 'gdn_conv_w', 'gdn_a_log', 'gdn_dt_bias', 'gdn_out_norm_w',
           'nsa_q_norm_w', 'nsa_kc_norm_w', 'nsa_ks_norm_w', 'nsa_kw_norm_w', 'nsa_cmp_pos_k', 'nsa_cmp_pos_v',
           'nsa_cmp_k_w1', 'nsa_cmp_k_w2', 'nsa_cmp_v_w1', 'nsa_cmp_v_w2', 'mem_w_kv', 'mem_q_norm_w', 'mem_k_norm_w',
           'w_out', 'ffn_norm_w', 'ffn_w_up', 'ffn_conv_w', 'ffn_w_down']
W_SHAPES = {
    'attn_norm_w': [D], 'mem_norm_w': [D], 'w_in': [D, INW], 'gdn_conv_w': [4, 1536], 'gdn_a_log': [4], 'gdn_dt_bias': [4],
    'gdn_out_norm_w': [128], 'nsa_q_norm_w': [64], 'nsa_kc_norm_w': [64], 'nsa_ks_norm_w': [64], 'nsa_kw_norm_w': [64],
    'nsa_cmp_pos_k': [32, 64], 'nsa_cmp_pos_v': [32, 64], 'nsa_cmp_k_w1': [2048, 64], 'nsa_cmp_k_w2': [64, 64],
    'nsa_cmp_v_w1': [2048, 64], 'nsa_cmp_v_w2': [64, 64], 'mem_w_kv': [D, 1024], 'mem_q_norm_w': [128], 'mem_k_norm_w': [128],
    'w_out': [1536, D], 'ffn_norm_w': [D], 'ffn_w_up': [D, 2 * FF], 'ffn_conv_w': [3, 2 * FF], 'ffn_w_down': [FF, D],
}


def make_consts():
    c = {}
    c['c_ident'] = np.eye(128, dtype=np.float32).astype(ml_dtypes.bfloat16)
    idx = np.arange(128)
    same = (idx[:, None] // 64) == (idx[None, :] // 64)
    c['c_btri'] = (same & (idx[:, None] <= idx[None, :])).astype(np.float32)
    c['c_bones'] = same.astype(np.float32)
    c['c_strict'] = (same & (idx[:, None] > idx[None, :])).astype(np.float32)
    c['c_mlow'] = np.where(same & (idx[:, None] >= idx[None, :]), 0.0, NEG).astype(np.float32)
    c['c_mup'] = np.ascontiguousarray(c['c_mlow'].T)
    c['c_mch'] = np.stack([(idx < 64), (idx >= 64)], axis=1).astype(np.float32)
    bf = ml_dtypes.bfloat16
    c['c_tril'] = np.where(idx[:, None] <= idx[None, :], 0.0, NEG).astype(np.float32).astype(bf)
    c['c_far'] = np.where(idx[None, :] < idx[:, None], 0.0, NEG).astype(np.float32).astype(bf)
    cm = np.zeros((9, 128, 512), np.float32)
    f = np.arange(512)
    for m in range(9):
        bt, s_ = (0, m) if m < 5 else (1, m - 1)
        blk = 128 * bt + idx
        vis = (16 * blk[:, None] + 31 <= 512 * s_ + f[None, :]) & (blk[:, None] < 255)
        cm[m] = np.where(vis, 0.0, NEG)
    c['c_cmask'] = cm.astype(bf)
    kk = np.arange(S)
    c['c_E'] = np.where((kk[None, :] // 64) == np.arange(64)[:, None], -NEG, 0.0).astype(np.float32).astype(bf)
    ci = np.arange(256) * 16
    sj = np.arange(64) * 64
    ovl = np.clip(np.minimum(ci[:, None] + 32, sj[None, :] + 64) - np.maximum(ci[:, None], sj[None, :]), 0, None) / 16.0
    ovl[255] = 0.0
    c['c_ovl'] = np.ascontiguousarray(ovl.reshape(2, 128, 64).transpose(1, 0, 2)).astype(np.float32).astype(bf)
    pos = np.arange(S, dtype=np.float32)
    inv = (1.0 / (np.float32(500000.0) ** (np.arange(0, 16, 2, dtype=np.float32) / np.float32(16)))).astype(np.float32)
    ang = pos[:, None] * inv[None, :]
    c['c_rope'] = np.concatenate([np.cos(ang), np.sin(ang)], axis=1).astype(np.float32)
    tt = np.arange(S)
    cur = tt // 64
    blk = np.arange(64)
    valid = blk[None, :] <= cur[:, None]
    forced = (blk[None, :] == 0) | (blk[None, :] == cur[:, None]) | (blk[None, :] == cur[:, None] - 1)
    c['c_A'] = (valid & ~forced).astype(np.float32).reshape(NT, 128, 64)
    c['c_B'] = np.where(valid, np.where(forced, 1e6, 0.0), -1e9).astype(np.float32).reshape(NT, 128, 64)
    return c


def build_program(dbg=False, phases=('ip', 'mem', 'gdn', 'nsa', 'ffn'), dbg_ocat=False):
    nc = bass.Bass("TRN2", target_bir_lowering=False)
    io = {}
    io['x'] = nc.dram_tensor("x", [S, D], F32, kind="ExternalInput").ap()
    io['mem'] = nc.dram_tensor("mem", [256, D], F32, kind="ExternalInput").ap()
    for n in W_NAMES:
        io[n] = nc.dram_tensor(n, W_SHAPES[n], F32, kind="ExternalInput").ap()
    for n, v in make_consts().items():
        io[n] = nc.dram_tensor(n, list(v.shape), BF16 if v.dtype == ml_dtypes.bfloat16 else F32, kind="ExternalInput").ap()
    io['out'] = nc.dram_tensor("out", [S, D], F32, kind="ExternalOutput").ap()
    sk = "ExternalOutput" if dbg else "Internal"
    io['tm'] = nc.dram_tensor("tm", [S, TMW], F32, kind=sk).ap()
    io['qkv_tm'] = nc.dram_tensor("qkv_tm", [S, 1536], BF16, kind=sk).ap()
    if dbg_ocat:
        io['ocatT'] = nc.dram_tensor("ocatT", [1536, S], BF16, kind="ExternalInput").ap()
    else:
        io['ocatT'] = nc.dram_tensor("ocatT", [1536, S], BF16, kind=sk).ap()
    io['h_s'] = nc.dram_tensor("h_s", [S, D], F32, kind=sk).ap()
    io['actT'] = nc.dram_tensor("actT", [FF, S], BF16, kind="Internal").ap()
    kb = KB(nc)
    if 'ip' in phases:
        phase_inproj(kb, io)
    if 'mem' in phases:
        phase_mem(kb, io)
    if 'gdn' in phases:
        phase_gdn(kb, io)
    if 'nsa' in phases:
        phase_nsa(kb, io)
    if 'ffn' in phases:
        phase_ffn(kb, io)
    kb.finish()
    return nc, kb


def make_in_maps(inputs):
    consts = make_consts()
    maps = []
    for b in range(8):
        m = {'x': np.ascontiguousarray(inputs['x'][b]), 'mem': np.ascontiguousarray(inputs['mem'][b])}
        for n in W_NAMES:
            m[n] = np.ascontiguousarray(np.asarray(inputs[n])[0])
        m.update(consts)
        maps.append(m)
    return maps


def kernel(**inputs):
    nc, kb = build_program()
    maps = make_in_maps(inputs)
    res = run_bass_kernel_spmd(nc, maps, core_ids=list(range(8)))
    return np.stack([np.asarray(r['out'], dtype=np.float32) for r in res.results], axis=0)
```

```python
import os
import numpy as np
from contextlib import ExitStack
import concourse.bass as bass
import concourse.mybir as mybir
from concourse.bass_utils import run_bass_kernel_spmd
import ml_dtypes

F32 = mybir.dt.float32
BF16 = mybir.dt.bfloat16
AF = mybir.ActivationFunctionType
ALU = mybir.AluOpType
AX = mybir.AxisListType

S = 4096
D = 1024
NT = S // 128
INW = 3872
TMW = 2336
A_OFF, B_OFF, GATE_OFF, NQ_OFF = 0, 4, 8, 520
KC_OFF, VC_OFF, KS_OFF, VS_OFF, KW_OFF, VW_OFF = 1032, 1160, 1288, 1416, 1544, 1672
NG_OFF, MQ_OFF = 1800, 1824
FF = 2816
NEG = -30000.0
EPS = 1e-6


class T:
    def __init__(self, t, k):
        self.t = t
        self.k = k

    def __getitem__(self, idx):
        return self.t[idx]


class KB:
    NDS = 16

    def __init__(self, nc):
        self.nc = nc
        self.stack = ExitStack()
        self.eng = {'pe': nc.tensor, 'act': nc.scalar, 'dve': nc.vector, 'pool': nc.gpsimd, 'sp': nc.sync}
        self.sem = {}
        for e in self.eng:
            self.sem[e] = self.stack.enter_context(nc.semaphore("s_" + e))
        for j in range(self.NDS):
            self.sem[('d', j)] = self.stack.enter_context(nc.semaphore("d_%d" % j))
        self.cnt = {e: 0 for e in self.eng}
        self.seen = {e: {} for e in self.eng}
        self.state = {}
        self.dma_i = 0
        self.dma_uses = [0] * self.NDS
        self.nins = 0
        self.rr = 0
        self.excl = set()

    def _wait(self, e, evs):
        need = {}
        for (sk, v) in evs:
            if sk == e and e in ('pe', 'sp'):
                continue
            if self.seen[e].get(sk, 0) < v:
                need[sk] = max(need.get(sk, 0), v)
        for sk, v in need.items():
            self.eng[e].wait_ge(self.sem[sk], v)
            self.seen[e][sk] = v

    @staticmethod
    def _keys(lst):
        out = []
        for x in lst:
            if isinstance(x, T):
                out.append(x.k)
            elif isinstance(x, (list, tuple)) and len(x) and isinstance(x[0], T):
                out.append((x[0].k,) + tuple(x[1:]))
            else:
                out.append(x)
        return out

    def _deps(self, reads, writes):
        evs = []
        for k in reads:
            st = self.state.get(k)
            if st and st[0]:
                evs.append(st[0])
        for k in writes:
            st = self.state.get(k)
            if st:
                if st[0]:
                    evs.append(st[0])
                evs.extend(st[1])
        return evs

    def _update(self, ev, reads, writes):
        for k in reads:
            st = self.state.setdefault(k, [None, []])
            st[1].append(ev)
            if len(st[1]) > 12:
                best = {}
                for (sk, v) in st[1]:
                    best[sk] = max(best.get(sk, 0), v)
                st[1] = list(best.items())
        for k in writes:
            self.state[k] = [ev, []]

    def op(self, e, fn, r=(), w=()):
        r = self._keys(r)
        w = self._keys(w)
        w = w + [k for k in r if k in self.excl and k not in w]
        self._wait(e, self._deps(r, w))
        ins = fn(self.eng[e])
        self.cnt[e] += 1
        ins.then_inc(self.sem[e], 1)
        self._update((e, self.cnt[e]), r, w)
        self.nins += 1
        return ins

    def dma(self, q, out, in_, r=(), w=(), **kw):
        r = self._keys(r)
        w = self._keys(w)
        j = self.dma_i % self.NDS
        self.dma_i += 1
        evs = self._deps(r, w)
        if self.dma_uses[j] > 0:
            evs.append((('d', j), 16 * self.dma_uses[j]))
        self._wait(q, evs)
        ins = self.eng[q].dma_start(out=out, in_=in_, **kw)
        self.dma_uses[j] += 1
        ins.then_inc(self.sem[('d', j)], 16)
        ev = (('d', j), 16 * self.dma_uses[j])
        self._update(ev, r, w)
        self.nins += 1
        return ev

    def barrier(self):
        evs = [(f, self.cnt[f]) for f in self.eng if self.cnt[f]]
        evs += [(('d', j), 16 * self.dma_uses[j]) for j in range(self.NDS) if self.dma_uses[j]]
        for e in self.eng:
            self._wait(e, [ev for ev in evs if ev[0] != e])

    def finish(self):
        self.barrier()
        self.stack.close()

    def ew(self, with_act=False):
        self.rr += 1
        lst = ('dve', 'pool', 'act') if with_act else ('dve', 'pool')
        return lst[self.rr % len(lst)]


class Phase:
    def __init__(self, kb, tag):
        self.kb = kb
        self.nc = kb.nc
        self.tag = tag
        self.st = ExitStack()

    def sb(self, name, shape, dt):
        n = self.tag + "_" + name
        return T(self.st.enter_context(self.nc.sbuf_tensor(n, list(shape), dt)), n)

    def sbn(self, name, shape, dt, n):
        return [self.sb("%s%d" % (name, i), shape, dt) for i in range(n)]

    def ps(self, name, shape, dt=F32):
        n = self.tag + "_" + name
        self.kb.excl.add(n)
        return T(self.st.enter_context(self.nc.psum_tensor(n, list(shape), dt)), n)

    def psn(self, name, shape, dt, n):
        return [self.ps("%s%d" % (name, i), shape, dt) for i in range(n)]

    def close(self):
        self.kb.barrier()
        self.st.close()


def load_cast_weight(kb, ph, dst, src_ap, nchunks, ncols, gam=None, stage_cols=None):
    stage_cols = stage_cols or ncols
    stg = ph.sbn("stg_" + dst.k, [128, stage_cols], F32, 2)
    i = 0
    engs = ('dve', 'act', 'dve')
    for c in range(nchunks):
        for c0 in range(0, ncols, stage_cols):
            c1 = min(ncols, c0 + stage_cols)
            sg = stg[i % 2]
            kb.dma('sp', sg[:, 0:c1 - c0], src_ap[c * 128:(c + 1) * 128, c0:c1], w=[sg])
            e = engs[i % 3]
            o = dst[:, c, c0:c1]
            if gam is None:
                if e == 'act':
                    kb.op(e, lambda E: E.copy(out=o, in_=sg[:, 0:c1 - c0]), r=[sg], w=[(dst, c)])
                else:
                    kb.op(e, lambda E: E.tensor_copy(out=o, in_=sg[:, 0:c1 - c0]), r=[sg], w=[(dst, c)])
            else:
                if e == 'act':
                    kb.op(e, lambda E: E.activation(out=o, in_=sg[:, 0:c1 - c0], func=AF.Copy, scale=gam[:, c:c + 1]),
                          r=[sg, gam], w=[(dst, c)])
                else:
                    kb.op(e, lambda E: E.tensor_scalar(out=o, in0=sg[:, 0:c1 - c0], scalar1=gam[:, c:c + 1], scalar2=None,
                                                       op0=ALU.mult), r=[sg, gam], w=[(dst, c)])
            i += 1


def rms_rstd(kb, src_ap, junk, ss, rs, n, rkeys):
    kb.op('act', lambda E: E.activation(out=junk[:, 0:n], in_=src_ap, func=AF.Square, accum_out=ss[:]), r=rkeys, w=[junk, ss])
    kb.op('act', lambda E: E.activation(out=rs[:], in_=ss[:], func=AF.Sqrt, scale=1.0 / n, bias=EPS), r=[ss], w=[rs])
    kb.op('dve', lambda E: E.reciprocal(out=rs[:], in_=rs[:]), r=[rs], w=[rs])


def phase_inproj(kb, io):
    nc = kb.nc
    ph = Phase(kb, "ip")
    winb = ph.sb("winb", [128, 8, INW], BF16)
    gam = ph.sb("gam", [128, 8], F32)
    cw = ph.sb("cw", [128, 4, 12], F32)
    ident = ph.sb("ident", [128, 128], BF16)
    kb.dma('sp', gam[:], io['attn_norm_w'].rearrange("(c p) -> p c", p=128), w=[gam], allow_slow_non_contiguous=True)
    for j in range(4):
        kb.dma('sp', cw[:, j, :], io['gdn_conv_w'][j, :].rearrange("(c p) -> p c", p=128), w=[cw], allow_slow_non_contiguous=True)
    kb.dma('sp', ident[:], io['c_ident'][:, :], w=[ident])
    load_cast_weight(kb, ph, winb, io['w_in'], 8, INW, gam=gam, stage_cols=1936)

    xt = ph.sbn("xt", [128, D], F32, 2)
    junk = ph.sb("junk", [128, D], BF16)
    ss = ph.sbn("ss", [128, 1], F32, 2)
    rs = ph.sbn("rs", [128, 1], F32, 2)
    xn = ph.sbn("xn", [128, D], BF16, 2)
    xnT = ph.sbn("xnT", [128, 8, 512], BF16, 2)
    xc = ph.sbn("xc", [128, 515], F32, 3)
    acc = ph.sbn("acc", [128, 512], F32, 3)
    halo = ph.sb("halo", [128, 12, 3], F32)
    qT = ph.sb("qT", [128, 12, 512], BF16)
    qtm = ph.sbn("qtm", [128, 1536], BF16, 2)
    tmt = ph.sbn("tmt", [128, TMW], F32, 2)
    pT = ph.psn("pT", [128, 8, 128], BF16, 2)
    pF = ph.psn("pF", [128, 512], F32, 2)
    pM = ph.psn("pM", [128, 512], F32, 2)
    pQ = ph.psn("pQ", [128, 8, 128], BF16, 2)

    kb.op('pool', lambda E: E.memset(halo[:], 0.0), w=[halo])
    it = 0
    for s in range(S // 512):
        xT = xnT[s % 2]
        for t4 in range(4):
            t = s * 4 + t4
            b = it % 2
            it += 1
            kb.dma('sp', xt[b][:], io['x'][t * 128:(t + 1) * 128, :], w=[xt[b]])
            rms_rstd(kb, xt[b][:], junk, ss[b], rs[b], D, [xt[b]])
            kb.op('dve', lambda E: E.tensor_scalar(out=xn[b][:], in0=xt[b][:], scalar1=rs[b][:, 0:1], scalar2=None, op0=ALU.mult),
                  r=[xt[b], rs[b]], w=[xn[b]])
            for c in range(8):
                kb.op('pe', lambda E: E.transpose(out=pT[b][:, c, :], in_=xn[b][:, c * 128:(c + 1) * 128], identity=ident[:]),
                      r=[xn[b], ident], w=[pT[b]])
            kb.op('act', lambda E: E.copy(out=xT[:, :, t4 * 128:(t4 + 1) * 128], in_=pT[b][:]), r=[pT[b]], w=[xT])
        for cb in range(12):
            pf = pF[cb % 2]
            for c in range(8):
                kb.op('pe', lambda E: E.matmul(pf[:], lhsT=winb[:, c, cb * 128:(cb + 1) * 128], rhs=xT[:, c, :],
                                               start=(c == 0), stop=(c == 7)), r=[xT, (winb, c)], w=[pf])
            x3 = xc[cb % 3]
            ac = acc[cb % 3]
            kb.op('act', lambda E: E.copy(out=x3[:, 3:515], in_=pf[:]), r=[pf], w=[(x3, 'b')])
            kb.op('pool', lambda E: E.tensor_copy(out=x3[:, 0:3], in_=halo[:, cb, :]), r=[(halo, cb)], w=[(x3, 'h')])
            kb.op('pool', lambda E: E.tensor_copy(out=halo[:, cb, :], in_=x3[:, 512:515]), r=[(x3, 'b')], w=[(halo, cb)])
            kb.op('act', lambda E: E.activation(out=ac[:], in_=pf[:], func=AF.Copy, scale=cw[:, 3, cb:cb + 1]), r=[pf, cw], w=[ac])
            for j in range(3):
                kb.op('dve', lambda E: E.scalar_tensor_tensor(out=ac[:], in0=x3[:, j:j + 512], scalar=cw[:, j, cb:cb + 1], in1=ac[:],
                                                           op0=ALU.mult, op1=ALU.add), r=[(x3, 'b'), (x3, 'h'), cw, ac], w=[ac])
            kb.op('act', lambda E: E.activation(out=qT[:, cb, :], in_=ac[:], func=AF.Silu), r=[ac], w=[(qT, cb)])
        for t4 in range(4):
            t = s * 4 + t4
            qm = qtm[t % 2]
            for g3 in range(3):
                pq = pQ[g3 % 2]
                for j in range(4):
                    cb = g3 * 4 + j
                    kb.op('pe', lambda E: E.transpose(out=pq[:, j, :], in_=qT[:, cb, t4 * 128:(t4 + 1) * 128], identity=ident[:]),
                          r=[(qT, cb), ident], w=[pq])
                e1 = 'dve' if g3 % 2 == 0 else 'act'
                if e1 == 'dve':
                    kb.op('dve', lambda E: E.tensor_copy(out=qm[:, g3 * 512:(g3 + 1) * 512], in_=pq[:, 0:4, :].rearrange("p a b -> p (a b)")),
                          r=[pq], w=[(qm, g3)])
                else:
                    kb.op('act', lambda E: E.copy(out=qm[:, g3 * 512:(g3 + 1) * 512], in_=pq[:, 0:4, :].rearrange("p a b -> p (a b)")),
                          r=[pq], w=[(qm, g3)])
            kb.dma('sp', io['qkv_tm'][t * 128:(t + 1) * 128, :], qm[:], r=[(qm, 0), (qm, 1), (qm, 2)], w=['qkv_tm'])
            tm = tmt[t % 2]
            for ci, n0 in enumerate(range(0, TMW, 512)):
                n1 = min(TMW, n0 + 512)
                pm = pM[ci % 2]
                for c in range(8):
                    kb.op('pe', lambda E: E.matmul(pm[:, 0:n1 - n0], lhsT=xT[:, c, t4 * 128:(t4 + 1) * 128],
                                                   rhs=winb[:, c, 1536 + n0:1536 + n1], start=(c == 0), stop=(c == 7)),
                          r=[xT, (winb, c)], w=[pm])
                if ci % 2 == 0:
                    kb.op('dve', lambda E: E.tensor_copy(out=tm[:, n0:n1], in_=pm[:, 0:n1 - n0]), r=[pm], w=[(tm, ci)])
                else:
                    kb.op('act', lambda E: E.copy(out=tm[:, n0:n1], in_=pm[:, 0:n1 - n0]), r=[pm], w=[(tm, ci)])
            kb.dma('sp', io['tm'][t * 128:(t + 1) * 128, :], tm[:], r=[(tm, i) for i in range(5)], w=['tm'])
    ph.close()


def attn_block(kb, o_ps, PTbuf, pti, sc_ps, kt_specs, ident, rkeys_q, LA=2):
    n = len(kt_specs)
    pts = [None] * n
    started = set()
    npv = sum(sp['qt1'] - sp['qt0'] for sp in kt_specs)
    done = 0
    for i in range(n + LA):
        if i < n:
            sp = kt_specs[i]
            pss = sc_ps[pti[0] % len(sc_ps)]
            ptb = PTbuf[pti[0] % len(PTbuf)]
            pti[0] += 1
            pts[i] = ptb
            q0, q1 = sp['qt0'], sp['qt1']
            ncol = (q1 - q0) * 128
            nk = sp['nk']
            nm = len(sp['masks'])
            kb.op('pe', lambda E: E.matmul(pss[0:nk, 0:ncol], lhsT=sp['lhsT'], rhs=sp['rhs_fn'](q0, q1), start=True, stop=(nm == 0)),
                  r=sp['rk'] + rkeys_q, w=[pss])
            for mi, (c0, nc_, mask) in enumerate(sp['masks']):
                kb.op('pe', lambda E: E.matmul(pss[0:nk, c0:c0 + nc_], lhsT=ident[0:nk, 0:nk], rhs=mask, start=False, stop=(mi == nm - 1)),
                      r=[ident] + sp.get('mk', []), w=[pss])
            kb.op('act', lambda E: E.activation(out=ptb[0:nk, 0:ncol], in_=pss[0:nk, 0:ncol], func=AF.Exp), r=[pss], w=[ptb])
        j = i - LA
        if j >= 0:
            sp = kt_specs[j]
            ptb = pts[j]
            nk = sp['nk']
            for qt in range(sp['qt0'], sp['qt1']):
                c0 = (qt - sp['qt0']) * 128
                oap, okey, bank = o_ps(qt)
                st = bank not in started
                started.add(bank)
                done += 1
                kb.op('pe', lambda E: E.matmul(oap, lhsT=ptb[0:nk, c0:c0 + 128], rhs=sp['v_fn'](qt), start=st, stop=(done == npv),
                                               skip_group_check=True), r=[ptb] + sp['rv'], w=[okey])


def phase_mem(kb, io):
    nc = kb.nc
    ph = Phase(kb, "mm")
    ident = ph.sb("ident", [128, 128], BF16)
    kb.dma('sp', ident[:], io['c_ident'][:, :], w=[ident])
    wkv = ph.sb("wkv", [128, 8, 1024], BF16)
    gam = ph.sb("gam", [128, 8], F32)
    kb.dma('sp', gam[:], io['mem_norm_w'].rearrange("(c p) -> p c", p=128), w=[gam], allow_slow_non_contiguous=True)
    load_cast_weight(kb, ph, wkv, io['mem_w_kv'], 8, 1024, gam=gam)
    qnw = ph.sb("qnw", [128, 128], F32)
    knw = ph.sb("knw", [128, 128], F32)
    kb.dma('sp', qnw[:], io['mem_q_norm_w'].partition_broadcast(128), w=[qnw])
    kb.dma('sp', knw[:], io['mem_k_norm_w'].partition_broadcast(128), w=[knw])
    kb.op('dve', lambda E: E.tensor_scalar(out=qnw[:], in0=qnw[:], scalar1=128 ** -0.5, scalar2=None, op0=ALU.mult), r=[qnw], w=[qnw])

    mt = ph.sbn("mt", [128, D], F32, 2)
    junk = ph.sb("junk", [128, D], BF16)
    ss = ph.sb("ss", [128, 1], F32)
    rs = ph.sb("rs", [128, 1], F32)
    mn = ph.sb("mn", [128, D], BF16)
    mnT = ph.sb("mnT", [128, 8, 128], BF16)
    kvt = ph.sb("kvt", [128, 1024], F32)
    sq4 = ph.sb("sq4", [128, 4, 128], F32)
    ss4 = ph.sb("ss4", [128, 4], F32)
    rs4 = ph.sb("rs4", [128, 4], F32)
    kn = ph.sb("kn", [128, 4, 128], BF16)
    kT = ph.sb("kT", [128, 4, 256], BF16)
    v1 = ph.sb("v1", [128, 2, 4, 129], BF16)
    pT = ph.ps("pT", [128, 8, 128], BF16)
    pK = ph.psn("pK", [128, 512], F32, 2)
    kb.op('pool', lambda E: E.memset(v1[:], 1.0), w=[v1])
    for mt_i in range(2):
        m = mt[mt_i]
        kb.dma('sp', m[:], io['mem'][mt_i * 128:(mt_i + 1) * 128, :], w=[m])
        rms_rstd(kb, m[:], junk, ss, rs, D, [m])
        kb.op('dve', lambda E: E.tensor_scalar(out=mn[:], in0=m[:], scalar1=rs[:, 0:1], scalar2=None, op0=ALU.mult), r=[m, rs], w=[mn])
        for c in range(8):
            kb.op('pe', lambda E: E.transpose(out=pT[:, c, :], in_=mn[:, c * 128:(c + 1) * 128], identity=ident[:]), r=[mn, ident], w=[pT])
        kb.op('act', lambda E: E.copy(out=mnT[:], in_=pT[:]), r=[pT], w=[mnT])
        for half in range(2):
            pk = pK[half]
            for c in range(8):
                kb.op('pe', lambda E: E.matmul(pk[:], lhsT=mnT[:, c, :], rhs=wkv[:, c, half * 512:(half + 1) * 512],
                                               start=(c == 0), stop=(c == 7)), r=[mnT, (wkv, c)], w=[pk])
            kb.op('act', lambda E: E.copy(out=kvt[:, half * 512:(half + 1) * 512], in_=pk[:]), r=[pk], w=[(kvt, half)])
        k3 = kvt[:, 0:512].rearrange("p (h d) -> p h d", h=4)
        kb.op('dve', lambda E: E.tensor_tensor(out=sq4[:], in0=k3, in1=k3, op=ALU.mult), r=[(kvt, 0)], w=[sq4])
        kb.op('dve', lambda E: E.tensor_reduce(out=ss4[:], in_=sq4[:], axis=AX.X, op=ALU.add), r=[sq4], w=[ss4])
        kb.op('act', lambda E: E.activation(out=rs4[:], in_=ss4[:], func=AF.Sqrt, scale=1.0 / 128, bias=EPS), r=[ss4], w=[rs4])
        kb.op('dve', lambda E: E.reciprocal(out=rs4[:], in_=rs4[:]), r=[rs4], w=[rs4])
        kb.op('dve', lambda E: E.tensor_tensor(out=sq4[:], in0=k3, in1=rs4[:].unsqueeze(2).to_broadcast([128, 4, 128]), op=ALU.mult),
              r=[(kvt, 0), rs4], w=[sq4])
        kb.op('dve', lambda E: E.tensor_tensor(out=kn[:], in0=sq4[:], in1=knw[:].unsqueeze(1).to_broadcast([128, 4, 128]), op=ALU.mult),
              r=[sq4, knw], w=[kn])
        for h in range(4):
            kb.op('pe', lambda E: E.transpose(out=pT[:, h, :], in_=kn[:, h, :], identity=ident[:]), r=[kn, ident], w=[pT])
        kb.op('act', lambda E: E.copy(out=kT[:, :, mt_i * 128:(mt_i + 1) * 128], in_=pT[:, 0:4, :]), r=[pT], w=[kT])
        kb.op('dve', lambda E: E.tensor_copy(out=v1[:, mt_i, :, 0:128], in_=kvt[:, 512:1024].rearrange("p (h d) -> p h d", h=4)),
              r=[(kvt, 1)], w=[v1])

    qt_ = ph.sbn("qt", [128, 512], F32, 2)
    qs = ph.sb("qs", [128, 4, 128], F32)
    qn = ph.sbn("qn", [128, 4, 128], BF16, 2)
    qT = ph.sbn("qT", [128, 4, 512], BF16, 2)
    PTb = ph.sbn("PT", [128, 512], BF16, 4)
    pti = [0]
    oc = ph.sbn("oc", [128, 4, 128], BF16, 4)
    rinv = ph.sb("rinv", [128, 1], F32)
    ocT = ph.sbn("ocT", [128, 4, 128], BF16, 2)
    sc_ps = ph.psn("sc", [128, 512], F32, 2)
    o_psA = ph.ps("oA", [128, 2, 256], F32)
    o_psB = ph.ps("oB", [128, 2, 256], F32)
    pO = ph.ps("pO", [128, 8, 128], BF16)
    for s in range(S // 512):
        qTs = qT[s % 2]
        for t4 in range(4):
            t = s * 4 + t4
            q = qt_[t % 2]
            qb = qn[t % 2]
            kb.dma('sp', q[:], io['tm'][t * 128:(t + 1) * 128, MQ_OFF:MQ_OFF + 512], r=['tm'], w=[q])
            q3 = q[:].rearrange("p (h d) -> p h d", h=4)
            kb.op('pool', lambda E: E.tensor_tensor(out=qs[:], in0=q3, in1=q3, op=ALU.mult), r=[q], w=[qs])
            kb.op('dve', lambda E: E.tensor_reduce(out=ss4[:], in_=qs[:], axis=AX.X, op=ALU.add), r=[qs], w=[ss4])
            kb.op('act', lambda E: E.activation(out=rs4[:], in_=ss4[:], func=AF.Sqrt, scale=1.0 / 128, bias=EPS), r=[ss4], w=[rs4])
            kb.op('dve', lambda E: E.reciprocal(out=rs4[:], in_=rs4[:]), r=[rs4], w=[rs4])
            kb.op('dve', lambda E: E.tensor_tensor(out=qs[:], in0=q3, in1=rs4[:].unsqueeze(2).to_broadcast([128, 4, 128]), op=ALU.mult),
                  r=[q, rs4], w=[qs])
            kb.op('pool', lambda E: E.tensor_tensor(out=qb[:], in0=qs[:], in1=qnw[:].unsqueeze(1).to_broadcast([128, 4, 128]), op=ALU.mult),
                  r=[qs, qnw], w=[qb])
            for h in range(4):
                kb.op('pe', lambda E: E.transpose(out=pT[:, h, :], in_=qb[:, h, :], identity=ident[:]), r=[qb, ident], w=[pT])
            kb.op('act', lambda E: E.copy(out=qTs[:, :, t4 * 128:(t4 + 1) * 128], in_=pT[:, 0:4, :]), r=[pT], w=[qTs])
        for h in range(4):
            def o_ps(qt, h=h):
                return (o_psA[:, qt, 0:129], o_psA, 'A') if qt < 2 else (o_psB[:, qt - 2, 0:129], o_psB, 'B')
            specs = []
            for kt in range(2):
                specs.append(dict(lhsT=kT[:, h, kt * 128:(kt + 1) * 128], rhs_fn=lambda q0, q1, h=h: qTs[:, h, q0 * 128:q1 * 128],
                                  qt0=0, qt1=4, masks=[], v_fn=lambda qt, kt=kt, h=h: v1[:, kt, h, :], nk=128, rk=[kT], rv=[v1]))
            attn_block(kb, o_ps, PTb, pti, sc_ps, specs, ident, [qTs])
            for t4 in range(4):
                t = s * 4 + t4
                ob = oc[t4]
                oap, okey, _ = o_ps(t4)
                kb.op('dve', lambda E: E.reciprocal(out=rinv[:], in_=oap[:, 128:129]), r=[okey], w=[rinv])
                kb.op('dve', lambda E: E.tensor_scalar(out=ob[:, h, :], in0=oap[:, 0:128], scalar1=rinv[:, 0:1], scalar2=None, op0=ALU.mult),
                      r=[okey, rinv], w=[(ob, h)])
        for t4 in range(4):
            t = s * 4 + t4
            ob = oc[t4]
            oT = ocT[t % 2]
            for h in range(4):
                kb.op('pe', lambda E: E.transpose(out=pO[:, h, :], in_=ob[:, h, :], identity=ident[:]), r=[(ob, h), ident], w=[pO])
            kb.op('act', lambda E: E.copy(out=oT[:], in_=pO[:, 0:4, :]), r=[pO], w=[oT])
            kb.dma('sp', io['ocatT'][1024:1536, t * 128:(t + 1) * 128].rearrange("(c p) t -> p c t", p=128), oT[:], r=[oT], w=['ocatT_c'])
    ph.close()


def phase_gdn(kb, io):
    nc = kb.nc
    ph = Phase(kb, "gd")
    ident = ph.sb("ident", [128, 128], BF16)
    btri = ph.sb("btri", [128, 128], F32)
    bones = ph.sb("bones", [128, 128], F32)
    ones = ph.sb("ones", [128, 128], F32)
    mlow = ph.sb("mlow", [128, 4, 128], F32)
    mup = ph.sb("mup", [128, 4, 128], F32)
    strict = ph.sb("strict", [128, 128], F32)
    mch = ph.sb("mch", [128, 2], F32)
    kb.dma('sp', ident[:], io['c_ident'][:, :], w=[ident])
    kb.dma('sp', btri[:], io['c_btri'][:, :], w=[btri])
    kb.dma('sp', bones[:], io['c_bones'][:, :], w=[bones])
    kb.dma('sp', strict[:], io['c_strict'][:, :], w=[strict])
    kb.dma('sp', mch[:], io['c_mch'][:, :], w=[mch])
    for h in range(4):
        kb.dma('sp', mlow[:, h, :], io['c_mlow'][:, :], w=[mlow])
        kb.dma('sp', mup[:, h, :], io['c_mup'][:, :], w=[mup])
    kb.op('pool', lambda E: E.memset(ones[:], 1.0), w=[ones])
    dtb = ph.sb("dtb", [128, 4], F32)
    nA = ph.sb("nA", [128, 4], F32)
    gw = ph.sb("gw", [128, 128], F32)
    kb.dma('sp', dtb[:], io['gdn_dt_bias'].partition_broadcast(128), w=[dtb])
    kb.dma('sp', nA[:], io['gdn_a_log'].partition_broadcast(128), w=[nA])
    kb.dma('sp', gw[:], io['gdn_out_norm_w'].partition_broadcast(128), w=[gw])
    kb.op('act', lambda E: E.activation(out=nA[:], in_=nA[:], func=AF.Exp), r=[nA], w=[nA])
    kb.op('dve', lambda E: E.tensor_scalar(out=nA[:], in0=nA[:], scalar1=-1.0, scalar2=None, op0=ALU.mult), r=[nA], w=[nA])

    def B4(t_, n=4):
        return t_.unsqueeze(2).to_broadcast([128, n, 128])

    def M4(t_):
        return t_.unsqueeze(1).to_broadcast([128, 4, 128])

    qkv = ph.sbn("qkv", [128, 3, 4, 128], BF16, 2)
    ab = ph.sbn("ab", [128, 8], F32, 2)
    gt = ph.sbn("gt", [128, 512], F32, 2)
    sm = ph.sbn("sm", [128, 64], F32, 2)
    gs = ph.sbn("gs", [128, 16], F32, 2)
    gm = ph.sb("gm", [128, 8], F32)
    R1 = ph.sb("R1", [128, 4, 128], F32)
    R2 = ph.sb("R2", [128, 4, 128], F32)
    tmpA = ph.sb("tmpA", [128, 4, 128], F32)
    tmpB = ph.sb("tmpB", [128, 4, 128], F32)
    dec = ph.sb("dec", [128, 4, 128], F32)
    decT = ph.sb("decT", [128, 4, 128], F32)
    sq = ph.sb("sq", [128, 4, 128], F32)
    KBG = ph.sb("KBG", [128, 4, 128], BF16)
    Kdb = ph.sbn("Kd", [128, 4, 128], BF16, 4)
    VBb = ph.sbn("VB", [128, 4, 128], BF16, 2)
    dg = ph.sbn("dg", [128, 4, 128], BF16, 4)
    QT = ph.sb("QT", [128, 4, 128], BF16)
    QGb = ph.sbn("QG", [128, 4, 128], BF16, 4)
    KT = ph.sb("KT", [128, 4, 128], BF16)
    nbs = ph.sb("nbs", [128, 4, 128], F32)
    Xb = ph.sbn("X", [128, 4, 128], BF16, 2)
    Yb = ph.sbn("Y", [128, 4, 128], BF16, 2)
    Pbb = ph.sbn("P", [128, 4, 128], BF16, 4)
    aqkTb = ph.sbn("aqkT", [128, 4, 128], BF16, 2)
    negWTb = ph.sbn("negWT", [128, 4, 128], BF16, 2)
    vnew = ph.sb("vnew", [128, 4, 128], BF16)
    Sf = ph.sb("Sf", [128, 4, 128], F32)
    Sbf = ph.sbn("Sbf", [128, 4, 128], BF16, 3)
    osb = ph.sb("osb", [128, 4, 128], F32)
    sgt = ph.sb("sgt", [128, 512], F32)
    oa = ph.sbn("oa", [128, 4, 128], BF16, 2)
    oaT = ph.sbn("oaT", [128, 4, 128], BF16, 2)
    pS = ph.ps("pS", [128, 512], F32)
    pD = ph.ps("pD", [128, 4, 128], F32)
    pA = ph.psn("pA", [128, 4, 128], F32, 2)
    pB = ph.psn("pB", [128, 8, 128], BF16, 1)
    pV = ph.ps("pV", [128, 4, 128], F32)
    pdS = ph.ps("pdS", [128, 4, 128], F32)
    pO = ph.ps("pO", [128, 4, 128], F32)
    kb.op('pool', lambda E: E.memset(Sf[:], 0.0), w=[Sf])
    kb.op('pool', lambda E: E.memset(Sbf[0][:], 0.0), w=[Sbf[0]])
    kb.op('pool', lambda E: E.memset(vnew[:], 0.0), w=[vnew])
    pai = [0]

    def PA():
        pai[0] += 1
        return pA[pai[0] % 2]

    def mm4(p, lf, rf, rk):
        for h in range(4):
            kb.op('pe', lambda E: E.matmul(p[:, h, :], lhsT=lf(h), rhs=rf(h), start=True, stop=True), r=rk, w=[p])

    si = [0]

    def tile(t):
        b = t % 2
        x_ = qkv[b]
        VB, aqkT, negWT = VBb[b], aqkTb[b], negWTb[b]
        Kd = Kdb[2 * b:2 * b + 2]
        QG = QGb[2 * b:2 * b + 2]
        Pb = Pbb[2 * b:2 * b + 2]
        s_ = sm[b]
        kb.dma('sp', x_[:].rearrange("p a h d -> p (a h d)"), io['qkv_tm'][t * 128:(t + 1) * 128, :], r=['qkv_tm'], w=[x_])
        kb.dma('sp', ab[b][:], io['tm'][t * 128:(t + 1) * 128, 0:8], r=['tm'], w=[ab[b]])
        kb.dma('sp', gt[b][:], io['tm'][t * 128:(t + 1) * 128, GATE_OFF:GATE_OFF + 512], r=['tm'], w=[gt[b]])
        g = s_[:, 0:4]
        kb.op('dve', lambda E: E.tensor_tensor(out=g, in0=ab[b][:, 0:4], in1=dtb[:], op=ALU.add), r=[ab[b], dtb], w=[s_])
        kb.op('act', lambda E: E.activation(out=g, in_=g, func=AF.Exp), r=[s_], w=[s_])
        kb.op('act', lambda E: E.activation(out=g, in_=g, func=AF.Ln, bias=1.0), r=[s_], w=[s_])
        kb.op('dve', lambda E: E.tensor_tensor(out=g, in0=g, in1=nA[:], op=ALU.mult), r=[s_, nA], w=[s_])
        kb.op('dve', lambda E: E.tensor_scalar(out=s_[:, 4:8], in0=g, scalar1=-1.0, scalar2=None, op0=ALU.mult), r=[s_], w=[s_])
        kb.op('act', lambda E: E.activation(out=s_[:, 8:12], in_=ab[b][:, 4:8], func=AF.Exp, scale=-1.0), r=[ab[b], s_], w=[s_])
        kb.op('dve', lambda E: E.tensor_scalar(out=s_[:, 8:12], in0=s_[:, 8:12], scalar1=1.0, scalar2=None, op0=ALU.add), r=[s_], w=[s_])
        kb.op('dve', lambda E: E.reciprocal(out=s_[:, 8:12], in_=s_[:, 8:12]), r=[s_], w=[s_])
        kb.op('dve', lambda E: E.tensor_scalar(out=s_[:, 12:16], in0=s_[:, 8:12], scalar1=-1.0, scalar2=None, op0=ALU.mult), r=[s_], w=[s_])
        for j in range(2):
            kb.op('dve', lambda E: E.tensor_scalar(out=gm[:, 4 * j:4 * j + 4], in0=g, scalar1=mch[:, j:j + 1], scalar2=None, op0=ALU.mult),
                  r=[s_, mch], w=[gm])
        kb.op('pe', lambda E: E.matmul(pS[:, 0:4], lhsT=btri[:], rhs=g, start=True, stop=True), r=[btri, s_], w=[pS])
        kb.op('pe', lambda E: E.matmul(pS[:, 4:8], lhsT=bones[:], rhs=g, start=True, stop=True), r=[bones, s_], w=[pS])
        kb.op('pe', lambda E: E.matmul(pS[:, 8:16], lhsT=ones[:], rhs=gm[:], start=True, stop=True), r=[ones, gm], w=[pS])
        G = gs[b]
        kb.op('dve', lambda E: E.tensor_copy(out=G[:], in_=pS[:, 0:16]), r=[pS], w=[G])
        kb.op('act', lambda E: E.activation(out=s_[:, 16:20], in_=G[:, 0:4], func=AF.Exp), r=[G, s_], w=[s_])
        kb.op('dve', lambda E: E.tensor_tensor(out=G[:, 4:8], in0=G[:, 4:8], in1=G[:, 0:4], op=ALU.subtract), r=[G], w=[G])
        kb.op('act', lambda E: E.activation(out=s_[:, 20:24], in_=G[:, 4:8], func=AF.Exp), r=[G, s_], w=[s_])
        kb.op('act', lambda E: E.activation(out=s_[:, 56:64], in_=G[:, 8:16], func=AF.Exp), r=[G, s_], w=[s_])
        yield
        kb.op('pool', lambda E: E.tensor_copy(out=R1[:], in_=B4(g)), r=[s_], w=[R1])
        kb.op('pool', lambda E: E.tensor_tensor(out=R2[:], in0=M4(btri[:]), in1=B4(s_[:, 4:8]), op=ALU.mult), r=[s_, btri], w=[R2])
        kb.op('pe', lambda E: E.matmul(pD[:].rearrange("p a b -> p (a b)"), lhsT=btri[:], rhs=R1[:].rearrange("p a b -> p (a b)"),
                                       start=True, stop=False), r=[btri, R1], w=[pD])
        kb.op('pe', lambda E: E.matmul(pD[:].rearrange("p a b -> p (a b)"), lhsT=bones[:], rhs=R2[:].rearrange("p a b -> p (a b)"),
                                       start=False, stop=True), r=[bones, R2], w=[pD])
        kb.op('dve', lambda E: E.tensor_tensor(out=tmpA[:], in0=pD[:], in1=mlow[:], op=ALU.add), r=[pD, mlow], w=[tmpA])
        kb.op('act', lambda E: E.activation(out=dec[:], in_=tmpA[:], func=AF.Exp), r=[tmpA], w=[dec])
        kb.op('dve', lambda E: E.scalar_tensor_tensor(out=tmpB[:], in0=pD[:], scalar=-1.0, in1=mup[:], op0=ALU.mult, op1=ALU.add),
              r=[pD, mup], w=[tmpB])
        kb.op('act', lambda E: E.activation(out=decT[:], in_=tmpB[:], func=AF.Exp), r=[tmpB], w=[decT])
        yield
        for a_, c0 in ((0, 24), (1, 28)):
            kb.op('pool', lambda E: E.tensor_tensor(out=sq[:], in0=x_[:, a_, :, :], in1=x_[:, a_, :, :], op=ALU.mult), r=[x_], w=[sq])
            kb.op('dve', lambda E: E.tensor_reduce(out=s_[:, c0:c0 + 4], in_=sq[:], axis=AX.X, op=ALU.add), r=[sq, s_], w=[s_])
            kb.op('act', lambda E: E.activation(out=s_[:, c0:c0 + 4], in_=s_[:, c0:c0 + 4], func=AF.Sqrt, bias=EPS), r=[s_], w=[s_])
            kb.op('dve', lambda E: E.reciprocal(out=s_[:, c0:c0 + 4], in_=s_[:, c0:c0 + 4]), r=[s_], w=[s_])
        sc_ = lambda o, a, bb: kb.op('dve', lambda E: E.tensor_tensor(out=s_[:, o:o + 4], in0=a, in1=bb, op=ALU.mult), r=[s_, mch], w=[s_])
        kb.op('dve', lambda E: E.tensor_scalar(out=s_[:, 32:36], in0=s_[:, 24:28], scalar1=128 ** -0.5, scalar2=None, op0=ALU.mult), r=[s_], w=[s_])
        sc_(36, s_[:, 32:36], s_[:, 16:20])
        kb.op('dve', lambda E: E.tensor_scalar(out=s_[:, 40:44], in0=s_[:, 36:40], scalar1=mch[:, 1:2], scalar2=None, op0=ALU.mult), r=[s_, mch], w=[s_])
        kb.op('dve', lambda E: E.tensor_scalar(out=s_[:, 36:40], in0=s_[:, 36:40], scalar1=mch[:, 0:1], scalar2=None, op0=ALU.mult), r=[s_, mch], w=[s_])
        sc_(44, s_[:, 28:32], s_[:, 8:12])
        sc_(44, s_[:, 44:48], s_[:, 16:20])
        sc_(48, s_[:, 28:32], s_[:, 20:24])
        kb.op('dve', lambda E: E.tensor_scalar(out=s_[:, 52:56], in0=s_[:, 48:52], scalar1=mch[:, 1:2], scalar2=None, op0=ALU.mult), r=[s_, mch], w=[s_])
        kb.op('dve', lambda E: E.tensor_scalar(out=s_[:, 48:52], in0=s_[:, 48:52], scalar1=mch[:, 0:1], scalar2=None, op0=ALU.mult), r=[s_, mch], w=[s_])
        yield
        kx, vx, qx = x_[:, 1, :, :], x_[:, 2, :, :], x_[:, 0, :, :]
        kb.op('pool', lambda E: E.tensor_tensor(out=KBG[:], in0=kx, in1=B4(s_[:, 44:48]), op=ALU.mult), r=[x_, s_], w=[KBG])
        kb.op('pool', lambda E: E.tensor_tensor(out=Kd[0][:], in0=kx, in1=B4(s_[:, 48:52]), op=ALU.mult), r=[x_, s_], w=[Kd[0]])
        kb.op('pool', lambda E: E.tensor_tensor(out=Kd[1][:], in0=kx, in1=B4(s_[:, 52:56]), op=ALU.mult), r=[x_, s_], w=[Kd[1]])
        kb.op('pool', lambda E: E.tensor_tensor(out=VB[:], in0=vx, in1=B4(s_[:, 8:12]), op=ALU.mult), r=[x_, s_], w=[VB])
        yield
        for i_, c0 in enumerate((32, 36, 40, 28)):
            kb.op('dve', lambda E: E.tensor_tensor(out=dg[i_][:], in0=M4(ident[:]), in1=B4(s_[:, c0:c0 + 4]), op=ALU.mult), r=[ident, s_], w=[dg[i_]])
        for i_, (src, dst) in enumerate(((qx, QT), (qx, QG[0]), (qx, QG[1]), (kx, KT))):
            p = PA()
            mm4(p, lambda h: src[:, h, :], lambda h: dg[i_][:, h, :], [x_, dg[i_]])
            if i_ % 2 == 0:
                kb.op('act', lambda E: E.copy(out=dst[:], in_=p[:]), r=[p], w=[dst])
            else:
                kb.op('dve', lambda E: E.tensor_copy(out=dst[:], in_=p[:]), r=[p], w=[dst])
        yield
        p = PA()
        mm4(p, lambda h: KT[:, h, :], lambda h: KT[:, h, :], [KT])
        kb.op('dve', lambda E: E.tensor_tensor(out=tmpA[:], in0=p[:], in1=dec[:], op=ALU.mult), r=[p, dec], w=[tmpA])
        kb.op('pool', lambda E: E.tensor_tensor(out=nbs[:], in0=M4(strict[:]), in1=B4(s_[:, 12:16]), op=ALU.mult), r=[strict, s_], w=[nbs])
        X, Y, P = Xb[0], Yb[0], Pb[0]
        kb.op('pool', lambda E: E.tensor_tensor(out=X[:], in0=tmpA[:], in1=nbs[:], op=ALU.mult), r=[tmpA, nbs], w=[X])
        for h in range(4):
            kb.op('pe', lambda E: E.transpose(out=pB[0][:, h, :], in_=X[:, h, :], identity=ident[:]), r=[X, ident], w=[pB[0]])
        kb.op('act', lambda E: E.copy(out=Y[:], in_=pB[0][:, 0:4, :]), r=[pB[0]], w=[Y])
        kb.op('dve', lambda E: E.tensor_tensor(out=P[:], in0=pB[0][:, 0:4, :], in1=M4(ident[:]), op=ALU.add), r=[pB[0], ident], w=[P])
        yield
        p = PA()
        mm4(p, lambda h: KT[:, h, :], lambda h: QT[:, h, :], [KT, QT])
        kb.op('dve', lambda E: E.tensor_tensor(out=aqkT[:], in0=p[:], in1=decT[:], op=ALU.mult), r=[p, decT], w=[aqkT])
        yield
        for k_ in range(1, 6):
            Xn, Yn, Pn = Xb[k_ % 2], Yb[k_ % 2], Pb[k_ % 2]
            p = PA()
            mm4(p, lambda h: Y[:, h, :], lambda h: X[:, h, :], [X, Y])
            kb.op('act', lambda E: E.copy(out=Xn[:], in_=p[:]), r=[p], w=[Xn])
            if k_ < 5:
                p2 = PA()
                mm4(p2, lambda h: X[:, h, :], lambda h: Y[:, h, :], [X, Y])
                kb.op('dve', lambda E: E.tensor_copy(out=Yn[:], in_=p2[:]), r=[p2], w=[Yn])
            p3 = PA()
            mm4(p3, lambda h: Xn[:, h, :], lambda h: P[:, h, :], [Xn, P])
            kb.op('dve', lambda E: E.tensor_tensor(out=Pn[:], in0=p3[:], in1=P[:], op=ALU.add), r=[p3, P], w=[Pn])
            X, Y, P = Xn, Yn, Pn
            yield
        yield
        p = PA()
        mm4(p, lambda h: KBG[:, h, :], lambda h: P[:, h, :], [KBG, P])
        kb.op('act', lambda E: E.mul(out=negWT[:], in_=p[:], mul=-1.0), r=[p], w=[negWT])
        yield 'B'
        Sa = Sbf[si[0] % 3]
        Sb_ = Sbf[(si[0] + 1) % 3]
        Sc = Sbf[(si[0] + 2) % 3]
        si[0] += 2
        for j, (Scur, Snext) in enumerate(((Sa, Sb_), (Sb_, Sc))):
            for h in range(4):
                kb.op('pe', lambda E: E.matmul(pV[:, h, :], lhsT=P[:, h, :], rhs=VB[:, h, :], start=True, stop=False), r=[P, VB], w=[pV])
                kb.op('pe', lambda E: E.matmul(pV[:, h, :], lhsT=negWT[:, h, :], rhs=Scur[:, h, :], start=False, stop=True),
                      r=[negWT, Scur], w=[pV])
            yield
            r0 = 64 * j
            kb.op('act', lambda E: E.copy(out=vnew[r0:r0 + 64, :, :], in_=pV[r0:r0 + 64, :, :]), r=[pV], w=[vnew])
            yield
            mm4(pdS, lambda h: Kd[j][:, h, :], lambda h: vnew[:, h, :], [Kd[j], vnew])
            yield
            for h in range(4):
                kb.op('dve', lambda E: E.scalar_tensor_tensor(out=Sf[:, h, :], in0=Sf[:, h, :], scalar=s_[:, 56 + 4 * j + h:57 + 4 * j + h],
                                                              in1=pdS[:, h, :], op0=ALU.mult, op1=ALU.add), r=[Sf, s_, pdS], w=[Sf])
            kb.op('act', lambda E: E.copy(out=Snext[:], in_=Sf[:]), r=[Sf], w=[Snext])
            yield
        for h in range(4):
            kb.op('pe', lambda E: E.matmul(pO[:, h, :], lhsT=QG[0][:, h, :], rhs=Sa[:, h, :], start=True, stop=False), r=[QG[0], Sa], w=[pO])
            kb.op('pe', lambda E: E.matmul(pO[:, h, :], lhsT=QG[1][:, h, :], rhs=Sb_[:, h, :], start=False, stop=False), r=[QG[1], Sb_], w=[pO])
            kb.op('pe', lambda E: E.matmul(pO[:, h, :], lhsT=aqkT[:, h, :], rhs=vnew[:, h, :], start=False, stop=True), r=[aqkT, vnew], w=[pO])
        yield
        kb.op('act', lambda E: E.copy(out=osb[:], in_=pO[:]), r=[pO], w=[osb])
        kb.op('pool', lambda E: E.tensor_tensor(out=sq[:], in0=osb[:], in1=osb[:], op=ALU.mult), r=[osb], w=[sq])
        kb.op('dve', lambda E: E.tensor_reduce(out=G[:, 0:4], in_=sq[:], axis=AX.X, op=ALU.add), r=[sq, G], w=[G])
        kb.op('act', lambda E: E.activation(out=G[:, 0:4], in_=G[:, 0:4], func=AF.Sqrt, scale=1.0 / 128, bias=EPS), r=[G], w=[G])
        kb.op('dve', lambda E: E.reciprocal(out=G[:, 0:4], in_=G[:, 0:4]), r=[G], w=[G])
        yield
        kb.op('act', lambda E: E.activation(out=sgt[:], in_=gt[b][:], func=AF.Silu), r=[gt[b]], w=[sgt])
        kb.op('dve', lambda E: E.tensor_tensor(out=osb[:], in0=osb[:], in1=B4(G[:, 0:4]), op=ALU.mult), r=[osb, G], w=[osb])
        kb.op('pool', lambda E: E.tensor_tensor(out=osb[:], in0=osb[:], in1=M4(gw[:]), op=ALU.mult), r=[osb, gw], w=[osb])
        kb.op('dve', lambda E: E.tensor_tensor(out=oa[b][:], in0=osb[:], in1=sgt[:].rearrange("p (h d) -> p h d", h=4), op=ALU.mult),
              r=[osb, sgt], w=[oa[b]])
        for h in range(4):
            kb.op('pe', lambda E: E.transpose(out=pB[0][:, h, :], in_=oa[b][:, h, :], identity=ident[:]), r=[oa[b], ident], w=[pB[0]])
        kb.op('act', lambda E: E.copy(out=oaT[b][:], in_=pB[0][:, 0:4, :]), r=[pB[0]], w=[oaT[b]])
        kb.dma('sp', io['ocatT'][0:512, t * 128:(t + 1) * 128].rearrange("(c p) t -> p c t", p=128), oaT[b][:], r=[oaT[b]], w=['ocatT_a'])

    def to_boundary(g):
        for r in g:
            if r == 'B':
                return

    cur = tile(0)
    to_boundary(cur)
    for t in range(NT):
        nxt = tile(t + 1) if t + 1 < NT else None
        cur_done, nxt_done = False, nxt is None
        while not (cur_done and nxt_done):
            if not cur_done:
                try:
                    next(cur)
                except StopIteration:
                    cur_done = True
            if not nxt_done:
                if next(nxt) == 'B':
                    nxt_done = True
        cur = nxt
    ph.close()


def rope16(kb, R, G, cs, tmp, e1='dve', e2='pool'):
    c = cs[:, 0:8].unsqueeze(1).to_broadcast([128, G, 8])
    sn = cs[:, 8:16].unsqueeze(1).to_broadcast([128, G, 8])
    x1 = R[:, 0:G, 0:8]
    x2 = R[:, 0:G, 8:16]
    kb.op(e1, lambda E: E.tensor_tensor(out=tmp[:, 0:G, 0:8], in0=x1, in1=c, op=ALU.mult), r=[R, cs], w=[(tmp, 0)])
    yield
    kb.op(e2, lambda E: E.tensor_tensor(out=tmp[:, 0:G, 8:16], in0=x2, in1=sn, op=ALU.mult), r=[R, cs], w=[(tmp, 1)])
    yield
    kb.op(e1, lambda E: E.tensor_tensor(out=tmp[:, 0:G, 16:24], in0=x2, in1=c, op=ALU.mult), r=[R, cs], w=[(tmp, 2)])
    yield
    kb.op(e2, lambda E: E.tensor_tensor(out=tmp[:, 0:G, 24:32], in0=x1, in1=sn, op=ALU.mult), r=[R, cs], w=[(tmp, 3)])
    yield
    kb.op(e1, lambda E: E.tensor_tensor(out=x1, in0=tmp[:, 0:G, 0:8], in1=tmp[:, 0:G, 8:16], op=ALU.subtract),
          r=[(tmp, 0), (tmp, 1), (tmp, 2), (tmp, 3)], w=[R])
    yield
    kb.op(e1, lambda E: E.tensor_tensor(out=x2, in0=tmp[:, 0:G, 16:24], in1=tmp[:, 0:G, 24:32], op=ALU.add),
          r=[(tmp, 0), (tmp, 1), (tmp, 2), (tmp, 3)], w=[R])
    yield


def rms_groups(kb, src3, G, dst3, sq, ss, wt, e_sq='pool'):
    (src_ap, src_keys) = src3
    (dst_ap, dst_keys) = dst3
    kb.op(e_sq, lambda E: E.tensor_tensor(out=sq[:, 0:G, :], in0=src_ap, in1=src_ap, op=ALU.mult), r=src_keys, w=[sq])
    yield
    kb.op('dve', lambda E: E.tensor_reduce(out=ss[:, 0:G], in_=sq[:, 0:G, :], axis=AX.X, op=ALU.add), r=[sq], w=[ss])
    yield
    kb.op('act', lambda E: E.activation(out=ss[:, 0:G], in_=ss[:, 0:G], func=AF.Sqrt, scale=1.0 / 64, bias=EPS), r=[ss], w=[ss])
    yield
    kb.op('dve', lambda E: E.reciprocal(out=ss[:, 0:G], in_=ss[:, 0:G]), r=[ss], w=[ss])
    yield
    kb.op('dve', lambda E: E.tensor_tensor(out=dst_ap, in0=src_ap, in1=ss[:, 0:G].unsqueeze(2).to_broadcast([128, G, 64]), op=ALU.mult),
          r=src_keys + [ss], w=dst_keys)
    yield
    kb.op('pool', lambda E: E.tensor_tensor(out=dst_ap, in0=dst_ap, in1=wt[:].unsqueeze(1).to_broadcast([128, G, 64]), op=ALU.mult),
          r=dst_keys + [wt], w=dst_keys)
    yield


def phase_nsa(kb, io):
    nc = kb.nc
    ph = Phase(kb, "ns")
    ident = ph.sb("ident", [128, 128], BF16)
    tril = ph.sb("tril", [128, 128], BF16)
    far = ph.sb("far", [128, 128], BF16)
    cmask = ph.sb("cmask", [128, 9, 512], BF16)
    kvT = ph.sb("kvT", [128, 4, S], BF16)
    vs1 = ph.sb("vs1", [128, NT, 2, 65], BF16)
    vw1 = ph.sb("vw1", [128, NT, 2, 65], BF16)
    kcmpT = ph.sb("kcmpT", [64, 2, 256], BF16)
    rhs_cmp = ph.sb("rhs_cmp", [128, 2, 2, 128], BF16)
    kb.dma('sp', ident[:], io['c_ident'][:, :], w=[ident])
    kb.dma('sp', tril[:], io['c_tril'][:, :], w=[tril])
    kb.dma('sp', far[:], io['c_far'][:, :], w=[far])
    for m in range(9):
        kb.dma('sp', cmask[:, m, :], io['c_cmask'][m, :, :], w=[cmask])
    for g in range(2):
        kb.dma('sp', kvT[64:128, g, :], io['c_E'][:, :], w=[(kvT, 'E')])
    kb.op('pool', lambda E: E.memset(vs1[:], 1.0), w=[vs1])
    kb.op('pool', lambda E: E.memset(vw1[:], 1.0), w=[vw1])
    kb.op('pool', lambda E: E.memset(kcmpT[:], 0.0), w=[kcmpT])
    kb.op('pool', lambda E: E.memset(rhs_cmp[:], 0.0), w=[rhs_cmp])
    for bt in range(2):
        for g in range(2):
            kb.dma('sp', rhs_cmp[:, bt, g, 64:128], io['c_ovl'][:, bt, :], r=[rhs_cmp], w=[rhs_cmp])

    pp = Phase(kb, "np")
    kcT = pp.sb("kcT", [64, 4, S], BF16)
    ksw = pp.sb("ksw", [128, 64], F32)
    kww = pp.sb("kww", [128, 64], F32)
    kcw = pp.sb("kcw", [128, 64], F32)
    kb.dma('sp', ksw[:], io['nsa_ks_norm_w'].partition_broadcast(128), w=[ksw])
    kb.dma('sp', kww[:], io['nsa_kw_norm_w'].partition_broadcast(128), w=[kww])
    kb.dma('sp', kcw[:], io['nsa_kc_norm_w'].partition_broadcast(128), w=[kcw])
    kvb = pp.sbn("kvb", [128, 768], F32, 2)
    cst = pp.sbn("cst", [128, 16], F32, 2)
    R = pp.sbn("R", [128, 6, 64], F32, 2)
    sqb = pp.sbn("sq", [128, 2, 64], F32, 2)
    ssb = pp.sbn("ss", [128, 2], F32, 2)
    tmpb = pp.sbn("tmp", [128, 6, 32], F32, 2)
    sq, ss = sqb[0], ssb[0]
    k16 = pp.sbn("k16", [128, 8, 64], BF16, 2)
    pT8 = pp.psn("pT8", [128, 8, 128], BF16, 2)
    def kvtile(t):
        b = t % 2
        kv = kvb[b]
        Rb = R[b]
        sq, ss, tmp = sqb[b], ssb[b], tmpb[b]
        kb.dma('sp', kv[:], io['tm'][t * 128:(t + 1) * 128, KC_OFF:KC_OFF + 768], r=['tm'], w=[kv])
        kb.dma('sp', cst[b][:], io['c_rope'][t * 128:(t + 1) * 128, :], w=[cst[b]])
        v3 = lambda off: kv[:, off:off + 128].rearrange("p (g d) -> p g d", g=2)
        kb.op('pool', lambda E: E.tensor_copy(out=Rb[:, 0:2, :], in_=v3(0)), r=[kv], w=[Rb])
        yield
        yield from rms_groups(kb, (v3(256), [kv]), 2, (Rb[:, 2:4, :], [Rb]), sq, ss, ksw)
        yield from rms_groups(kb, (v3(512), [kv]), 2, (Rb[:, 4:6, :], [Rb]), sq, ss, kww)
        yield from rope16(kb, Rb, 6, cst[b], tmp)
        kk = k16[b]
        kb.op('act', lambda E: E.copy(out=kk[:, 0:6, :], in_=Rb[:]), r=[Rb], w=[kk])
        kb.op('pool', lambda E: E.tensor_copy(out=kk[:, 6:8, :], in_=v3(128)), r=[kv], w=[kk])
        kb.op('dve', lambda E: E.tensor_copy(out=vs1[:, t, :, 0:64], in_=v3(384)), r=[kv], w=[vs1])
        kb.op('pool', lambda E: E.tensor_copy(out=vw1[:, t, :, 0:64], in_=v3(640)), r=[kv], w=[vw1])
        yield
        p8 = pT8[b]
        for i in range(8):
            kb.op('pe', lambda E: E.transpose(out=p8[0:64, i, :], in_=kk[:, i, :], identity=ident[:]), r=[kk, ident], w=[p8])
        kb.op('act', lambda E: E.copy(out=kvT[0:64, :, t * 128:(t + 1) * 128], in_=p8[0:64, 2:6, :]), r=[p8], w=[(kvT, 'k')])
        kb.op('dve', lambda E: E.tensor_copy(out=kcT[0:64, 0:2, t * 128:(t + 1) * 128], in_=p8[0:64, 0:2, :]), r=[p8], w=[kcT])
        kb.op('dve', lambda E: E.tensor_copy(out=kcT[0:64, 2:4, t * 128:(t + 1) * 128], in_=p8[0:64, 6:8, :]), r=[p8], w=[kcT])

    def interleave(gens):
        gens = list(gens)
        while gens:
            for g_ in list(gens):
                try:
                    next(g_)
                except StopIteration:
                    gens.remove(g_)

    for t in range(0, NT, 2):
        interleave([kvtile(t), kvtile(t + 1)])
    w1f = pp.sb("w1f", [64, 32, 64], F32)
    w1b = pp.sbn("w1b", [64, 32, 64], BF16, 2)
    w2f = pp.sb("w2f", [64, 64], F32)
    w2b = pp.sbn("w2b", [64, 64], BF16, 2)
    posf = pp.sb("posf", [64, 32], F32)
    pos2 = pp.sbn("pos2", [64, 32, 2], BF16, 2)
    bias = pp.sb("bias", [64, 2], F32)
    h1T = pp.sb("h1T", [64, 256], BF16)
    o2 = pp.sb("o2", [128, 1, 64], F32)
    o2n = pp.sb("o2n", [128, 1, 64], F32)
    kcn = pp.sb("kcn", [128, 64], BF16)
    pH = pp.ps("pH", [128, 512], F32)
    pB_ = pp.ps("pBi", [128, 512], F32)
    pO2 = pp.ps("pO2", [128, 512], F32)
    kb.op('pool', lambda E: E.memset(h1T[:], 0.0), w=[h1T])
    for kind, (n1, n2, npos) in enumerate((('nsa_cmp_k_w1', 'nsa_cmp_k_w2', 'nsa_cmp_pos_k'), ('nsa_cmp_v_w1', 'nsa_cmp_v_w2', 'nsa_cmp_pos_v'))):
        kb.dma('sp', w1f[:], io[n1].rearrange("(l d) o -> d l o", d=64), w=[w1f])
        kb.op('dve', lambda E: E.tensor_copy(out=w1b[kind][:], in_=w1f[:]), r=[w1f], w=[w1b[kind]])
        kb.dma('sp', w2f[:], io[n2][:, :], w=[w2f])
        kb.op('dve', lambda E: E.tensor_copy(out=w2b[kind][:], in_=w2f[:]), r=[w2f], w=[w2b[kind]])
        kb.dma('sp', posf[:], io[npos].rearrange("l d -> d l"), w=[posf], allow_slow_non_contiguous=True)
        for j in range(2):
            kb.op('dve', lambda E: E.tensor_copy(out=pos2[kind][:, :, j], in_=posf[:]), r=[posf], w=[pos2[kind]])
        for l in range(32):
            kb.op('pe', lambda E: E.matmul(pB_[0:64, 0:2], lhsT=w1b[kind][:, l, :], rhs=pos2[kind][:, l, :], start=(l == 0), stop=(l == 31)),
                  r=[w1b[kind], pos2[kind]], w=[pB_])
        kb.op('dve', lambda E: E.tensor_copy(out=bias[:], in_=pB_[0:64, 0:2]), r=[pB_], w=[bias])
        for g in range(2):
            ki = kind * 2 + g
            for l in range(32):
                kb.op('pe', lambda E: E.matmul(pH[0:64, 0:255], lhsT=w1b[kind][:, l, :], rhs=kcT[0:64, ki, l:l + 16 * 254 + 1:16],
                                               start=(l == 0), stop=(l == 31)), r=[w1b[kind], kcT], w=[pH])
            kb.op('act', lambda E: E.activation(out=h1T[:, 0:255], in_=pH[0:64, 0:255], func=AF.Silu, bias=bias[:, 0:1]),
                  r=[pH, bias], w=[h1T])
            for bt in range(2):
                kb.op('pe', lambda E: E.matmul(pO2[:, 0:64], lhsT=h1T[:, bt * 128:(bt + 1) * 128], rhs=w2b[kind][:], start=True, stop=True),
                      r=[h1T, w2b[kind]], w=[pO2])
                if kind == 0:
                    kb.op('act', lambda E: E.copy(out=o2[:, 0, :], in_=pO2[:, 0:64]), r=[pO2], w=[o2])
                    for _ in rms_groups(kb, (o2[:], [o2]), 1, (o2n[:], [o2n]), sq, ss, kcw):
                        pass
                    kb.op('act', lambda E: E.copy(out=kcn[:], in_=o2n[:, 0, :]), r=[o2n], w=[kcn])
                    p8 = pT8[0]
                    kb.op('pe', lambda E: E.transpose(out=p8[0:64, 0, :], in_=kcn[:], identity=ident[:]), r=[kcn, ident], w=[p8])
                    kb.op('act', lambda E: E.copy(out=kcmpT[:, g, bt * 128:(bt + 1) * 128], in_=p8[0:64, 0, :]), r=[p8], w=[kcmpT])
                else:
                    kb.op('act', lambda E: E.copy(out=rhs_cmp[:, bt, g, 0:64], in_=pO2[:, 0:64]), r=[pO2], w=[rhs_cmp])
    pp.close()

    pa = Phase(kb, "na")
    NPT = 8
    PTb = pa.sbn("PT", [128, 512], BF16, NPT)
    pti = [0]
    qaug = pa.sbn("qaug", [128, 8, 512], BF16, 2)
    qnw = pa.sb("qnw", [128, 64], F32)
    kb.dma('sp', qnw[:], io['nsa_q_norm_w'].partition_broadcast(128), w=[qnw])
    kb.op('dve', lambda E: E.tensor_scalar(out=qnw[:], in0=qnw[:], scalar1=0.125, scalar2=None, op0=ALU.mult), r=[qnw], w=[qnw])
    qf = pa.sbn("qf", [128, 512], F32, 4)
    cst = pa.sbn("cst", [128, 16], F32, 4)
    Rqb = pa.sbn("Rq", [128, 8, 64], F32, 4)
    sqb = pa.sbn("sq", [128, 8, 64], F32, 4)
    ssb = pa.sbn("ss", [128, 8], F32, 4)
    tmpb = pa.sbn("tmp", [128, 8, 32], F32, 4)
    qa = pa.sbn("qa", [128, 8, 128], BF16, 4)
    gts = pa.sb("gts", [128, 4, 24], F32)
    Ab = pa.sbn("Ab", [128, 64], F32, 4)
    Bb = pa.sbn("Bb", [128, 64], F32, 4)
    ob = pa.sb("ob", [128, 4, 8, 64], F32)
    impacc = pa.sb("impacc", [128, 4, 64], F32)
    tmpi = pa.sb("tmpi", [128, 4, 64], F32)
    tmpo = pa.sb("tmpo", [128, 4, 64], F32)
    scr = pa.sb("scr", [128, 64], F32)
    scr2 = pa.sb("scr2", [128, 64], F32)
    m8 = pa.sb("m8", [128, 16], F32)
    nst = pa.sbn("nst", [128, 128], BF16, 2)
    fs = pa.sb("fs", [128, 12], F32)
    obb = pa.sbn("obb", [128, 512], BF16, 2)
    obT = pa.sbn("obT", [128, 4, 128], BF16, 2)
    sc_ps = pa.psn("sc", [128, 512], F32, 3)
    oC = pa.ps("oC", [128, 4, 128], F32)
    oW = pa.psn("oW", [128, 4, 128], F32, 1)
    oS = pa.psn("oS", [128, 4, 128], F32, 2)
    pTr = pa.psn("pTr", [128, 8, 128], BF16, 1)
    for i in range(4):
        kb.op('pool', lambda E: E.memset(qa[i][:], 0.0), w=[qa[i]])
    for i in range(2):
        kb.op('pool', lambda E: E.memset(nst[i][:], 0.0), w=[nst[i]])
    tri = 0

    def finalize(oX, h, br, first, cmp=False):
        if cmp:
            kb.op('dve', lambda E: E.tensor_reduce(out=fs[:, 0:4], in_=oX[:, :, 64:128], axis=AX.X, op=ALU.add), r=[oX], w=[fs])
            kb.op('dve', lambda E: E.tensor_scalar(out=fs[:, 0:4], in0=fs[:, 0:4], scalar1=0.5, scalar2=1e-30, op0=ALU.mult, op1=ALU.add),
                  r=[fs], w=[fs])
        else:
            kb.op('dve', lambda E: E.tensor_scalar(out=fs[:, 0:4], in0=oX[:, :, 64], scalar1=1e-30, scalar2=None, op0=ALU.add), r=[oX], w=[fs])
        kb.op('dve', lambda E: E.reciprocal(out=fs[:, 4:8], in_=fs[:, 0:4]), r=[fs], w=[fs])
        if cmp:
            dsti = impacc if h % 4 == 0 else tmpi
            kb.op('dve', lambda E: E.tensor_tensor(out=dsti[:], in0=oX[:, :, 64:128], in1=fs[:, 4:8].unsqueeze(2).to_broadcast([128, 4, 64]),
                                                   op=ALU.mult), r=[oX, fs], w=[dsti])
            if h % 4 != 0:
                kb.op('pool', lambda E: E.tensor_tensor(out=impacc[:], in0=impacc[:], in1=tmpi[:], op=ALU.add), r=[impacc, tmpi], w=[impacc])
        kb.op('dve', lambda E: E.tensor_tensor(out=fs[:, 8:12], in0=fs[:, 4:8], in1=gts[:, :, h * 3 + br], op=ALU.mult), r=[fs, gts], w=[fs])
        dst = ob[:, :, h, :] if first else tmpo[:]
        kb.op('dve', lambda E: E.tensor_tensor(out=dst, in0=oX[:, :, 0:64], in1=fs[:, 8:12].unsqueeze(2).to_broadcast([128, 4, 64]),
                                               op=ALU.mult), r=[oX, fs], w=[(ob, h) if first else tmpo])
        if not first:
            kb.op('pool', lambda E: E.tensor_tensor(out=ob[:, :, h, :], in0=ob[:, :, h, :], in1=tmpo[:], op=ALU.add),
                  r=[(ob, h), tmpo], w=[(ob, h)])

    for s in range(S // 512):
        qs_ = qaug[s % 2]
        def qtile(t4, s=s, qs_=qs_):
            t = 4 * s + t4
            b = t4
            q = qf[b]
            Rq, sq, ss, tmp = Rqb[b], sqb[b], ssb[b], tmpb[b]
            kb.dma('sp', q[:], io['tm'][t * 128:(t + 1) * 128, NQ_OFF:NQ_OFF + 512], r=['tm'], w=[q])
            kb.dma('sp', gts[:, t4, :], io['tm'][t * 128:(t + 1) * 128, NG_OFF:NG_OFF + 24], r=['tm'], w=[gts])
            kb.dma('sp', cst[b][:], io['c_rope'][t * 128:(t + 1) * 128, :], w=[cst[b]])
            kb.dma('sp', Ab[t4][:], io['c_A'][t, :, :], w=[Ab[t4]])
            kb.dma('sp', Bb[t4][:], io['c_B'][t, :, :], w=[Bb[t4]])
            kb.op('act', lambda E: E.activation(out=gts[:, t4, :], in_=gts[:, t4, :], func=AF.Exp, scale=-1.0), r=[gts], w=[gts])
            kb.op('dve', lambda E: E.tensor_scalar(out=gts[:, t4, :], in0=gts[:, t4, :], scalar1=1.0, scalar2=None, op0=ALU.add), r=[gts], w=[gts])
            kb.op('dve', lambda E: E.reciprocal(out=gts[:, t4, :], in_=gts[:, t4, :]), r=[gts], w=[gts])
            q3 = q[:].rearrange("p (h d) -> p h d", h=8)
            yield
            yield from rms_groups(kb, (q3, [q]), 8, (Rq[:], [Rq]), sq, ss, qnw)
            yield from rope16(kb, Rq, 8, cst[b], tmp)
            qab = qa[b]
            kb.op('act', lambda E: E.copy(out=qab[:, :, 0:64], in_=Rq[:]), r=[Rq], w=[qab])
            yield
            pt_ = pTr[0]
            for h in range(8):
                kb.op('pe', lambda E: E.transpose(out=pt_[:, h, :], in_=qab[:, h, :], identity=ident[:]), r=[qab, ident], w=[pt_])
            kb.op('act', lambda E: E.copy(out=qs_[:, :, t4 * 128:(t4 + 1) * 128], in_=pt_[:]), r=[pt_], w=[qs_])
        interleave([qtile(i) for i in range(4)])
        nbt = 1 if s < 4 else 2
        for g in range(2):
            for h in range(4 * g, 4 * g + 4):
                specs = []
                for bt in range(nbt):
                    m = (s if s <= 4 else None) if bt == 0 else 5 + (s - 4)
                    masks = [] if m is None else [(0, 512, cmask[:, m, :])]
                    specs.append(dict(lhsT=kcmpT[0:64, g, bt * 128:(bt + 1) * 128],
                                      rhs_fn=lambda q0, q1, h=h: qs_[0:64, h, q0 * 128:q1 * 128], qt0=0, qt1=4, masks=masks, mk=[cmask],
                                      v_fn=lambda qt, bt=bt, g=g: rhs_cmp[:, bt, g, :], nk=128, rk=[kcmpT], rv=[rhs_cmp]))
                attn_block(kb, lambda qt: (oC[:, qt, :], oC, 'C'), PTb, pti, sc_ps, specs, ident, [qs_])
                finalize(oC, h, 0, True, cmp=True)
            for qt in range(4):
                ns_ = nst[qt % 2]
                kb.op('dve', lambda E: E.tensor_tensor(out=scr[:], in0=impacc[:, qt, :], in1=Ab[qt][:], op=ALU.mult), r=[impacc, Ab[qt]], w=[scr])
                kb.op('dve', lambda E: E.tensor_tensor(out=scr[:], in0=scr[:], in1=Bb[qt][:], op=ALU.add), r=[scr, Bb[qt]], w=[scr])
                kb.op('dve', lambda E: E.max(out=m8[:, 0:8], in_=scr[:]), r=[scr], w=[(m8, 0)])
                kb.op('dve', lambda E: E.match_replace(out=scr2[:], in_to_replace=m8[:, 0:8], in_values=scr[:], imm_value=-1e30),
                      r=[scr, (m8, 0)], w=[scr2])
                kb.op('dve', lambda E: E.max(out=m8[:, 8:16], in_=scr2[:]), r=[scr2], w=[(m8, 1)])
                kb.op('dve', lambda E: E.tensor_scalar(out=ns_[:, 64:128], in0=scr[:], scalar1=m8[:, 15:16], scalar2=1.0, op0=ALU.is_ge,
                                                       op1=ALU.subtract), r=[scr, (m8, 1)], w=[ns_])
                pt_ = pTr[0]
                tri += 1
                kb.op('pe', lambda E: E.transpose(out=pt_[:, 0, :], in_=ns_[:], identity=ident[:]), r=[ns_, ident], w=[pt_])
                for h in range(4 * g, 4 * g + 4):
                    if h % 2 == 0:
                        kb.op('act', lambda E: E.copy(out=qs_[64:128, h, qt * 128:(qt + 1) * 128], in_=pt_[64:128, 0, :]), r=[pt_], w=[qs_])
                    else:
                        kb.op('dve', lambda E: E.tensor_copy(out=qs_[64:128, h, qt * 128:(qt + 1) * 128], in_=pt_[64:128, 0, :]), r=[pt_], w=[qs_])
        for h in range(8):
            g = h // 4
            specs = []
            for kt in range(max(0, 4 * s - 4), 4 * s + 4):
                lo = max(kt - 4 * s, 0)
                hi = min(kt + 4 - 4 * s, 3)
                masks = []
                if kt >= 4 * s:
                    masks.append(((kt - 4 * s - lo) * 128, 128, tril[:]))
                if kt + 4 <= 4 * s + 3:
                    masks.append(((kt + 4 - 4 * s - lo) * 128, 128, far[:]))
                specs.append(dict(lhsT=kvT[0:64, 2 + g, kt * 128:(kt + 1) * 128],
                                  rhs_fn=lambda q0, q1, h=h: qs_[0:64, h, q0 * 128:q1 * 128], qt0=lo, qt1=hi + 1, masks=masks, mk=[tril, far],
                                  v_fn=lambda qt, kt=kt, g=g: vw1[:, kt, g, :], nk=128, rk=[(kvT, 'k')], rv=[vw1]))
            oW_ = oW[0]
            attn_block(kb, lambda qt: (oW_[:, qt, 0:65], oW_, 'W'), PTb, pti, sc_ps, specs, ident, [qs_], LA=3)
            finalize(oW_, h, 2, False)
            specs = []
            for kt in range(0, 4 * s + 4):
                lo = max(kt - 4 * s, 0)
                masks = [(0, 128, tril[:])] if kt >= 4 * s else []
                specs.append(dict(lhsT=kvT[:, g, kt * 128:(kt + 1) * 128],
                                  rhs_fn=lambda q0, q1, h=h: qs_[:, h, q0 * 128:q1 * 128], qt0=lo, qt1=4, masks=masks, mk=[tril],
                                  v_fn=lambda qt, kt=kt, g=g: vs1[:, kt, g, :], nk=128, rk=[(kvT, 'k'), (kvT, 'E')], rv=[vs1]))
            oS_ = oS[h % 2]
            attn_block(kb, lambda qt: (oS_[:, qt, 0:65], oS_, 'S'), PTb, pti, sc_ps, specs, ident, [qs_], LA=3)
            finalize(oS_, h, 1, False)
        for qt in range(4):
            t = 4 * s + qt
            b = t % 2
            kb.op('act', lambda E: E.copy(out=obb[b][:], in_=ob[:, qt, :, :].rearrange("p h d -> p (h d)")), r=[(ob, h) for h in range(8)], w=[obb[b]])
            pt_ = pTr[0]
            tri += 1
            for c in range(4):
                kb.op('pe', lambda E: E.transpose(out=pt_[:, c, :], in_=obb[b][:, c * 128:(c + 1) * 128], identity=ident[:]), r=[obb[b], ident], w=[pt_])
            kb.op('act', lambda E: E.copy(out=obT[b][:], in_=pt_[:, 0:4, :]), r=[pt_], w=[obT[b]])
            kb.dma('sp', io['ocatT'][512:1024, t * 128:(t + 1) * 128].rearrange("(c p) t -> p c t", p=128), obT[b][:], r=[obT[b]], w=['ocatT_b'])
    pa.close()
    ph.close()


def phase_ffn(kb, io):
    nc = kb.nc
    ph = Phase(kb, "f1")
    ident = ph.sb("ident", [128, 128], BF16)
    kb.dma('sp', ident[:], io['c_ident'][:, :], w=[ident])
    woutb = ph.sb("woutb", [128, 12, D], BF16)
    wupb = ph.sb("wupb", [128, 8, 2 * FF], BF16)
    gam = ph.sb("gam", [128, 8], F32)
    cw = ph.sb("cw", [128, 3, 44], F32)
    kb.dma('sp', gam[:], io['ffn_norm_w'].rearrange("(c p) -> p c", p=128), w=[gam], allow_slow_non_contiguous=True)
    for j in range(3):
        kb.dma('sp', cw[:, j, :], io['ffn_conv_w'][j, :].rearrange("(c p) -> p c", p=128), w=[cw], allow_slow_non_contiguous=True)
    load_cast_weight(kb, ph, woutb, io['w_out'], 12, D)
    load_cast_weight(kb, ph, wupb, io['ffn_w_up'], 8, 2 * FF, gam=gam, stage_cols=1408)

    oT = ph.sbn("oT", [128, 12, 128], BF16, 2)
    xt = ph.sbn("xt", [128, D], F32, 2)
    hs = ph.sbn("hs", [128, D], F32, 2)
    junk = ph.sb("junk", [128, D], BF16)
    ss = ph.sbn("ss", [128, 1], F32, 2)
    rs = ph.sbn("rs", [128, 1], F32, 2)
    hn = ph.sbn("hn", [128, D], BF16, 2)
    hnT = ph.sbn("hnT", [128, 8, 512], BF16, 2)
    ug = ph.sbn("ug", [128, 514], F32, 2)
    uv = ph.sbn("uv", [128, 514], F32, 2)
    ag = ph.sbn("ag", [128, 512], F32, 2)
    av = ph.sbn("av", [128, 512], F32, 2)
    sg = ph.sbn("sg", [128, 512], F32, 2)
    act = ph.sbn("act", [128, 512], BF16, 3)
    halo = ph.sb("halo", [128, 44, 2], F32)
    pH = ph.psn("pH", [128, 512], F32, 2)
    pT = ph.ps("pT", [128, 8, 128], BF16)
    pU = ph.psn("pU", [128, 512], F32, 4)
    kb.op('pool', lambda E: E.memset(halo[:], 0.0), w=[halo])

    def conv3(ps_, dst, src, fb):
        kb.op('act', lambda E: E.activation(out=dst[:], in_=ps_[:], func=AF.Copy, scale=cw[:, 2, fb:fb + 1]), r=[ps_, cw], w=[dst])
        for j in range(2):
            kb.op('dve', lambda E: E.scalar_tensor_tensor(out=dst[:], in0=src[:, j:j + 512], scalar=cw[:, j, fb:fb + 1], in1=dst[:],
                                                       op0=ALU.mult, op1=ALU.add), r=[(src, 'b'), (src, 'h'), cw, dst], w=[dst])

    def ftiles(s):
        hT = hnT[s % 2]
        for t4 in range(4):
            t = s * 4 + t4
            b = t % 2
            kb.dma('sp', oT[b][:], io['ocatT'][:, t * 128:(t + 1) * 128].rearrange("(c p) t -> p c t", p=128),
                   r=['ocatT_a', 'ocatT_b', 'ocatT_c'], w=[oT[b]])
            kb.dma('sp', xt[b][:], io['x'][t * 128:(t + 1) * 128, :], w=[xt[b]])
            yield
            for half in range(2):
                p = pH[half]
                for c in range(12):
                    kb.op('pe', lambda E: E.matmul(p[:], lhsT=oT[b][:, c, :], rhs=woutb[:, c, half * 512:(half + 1) * 512],
                                                   start=(c == 0), stop=(c == 11)), r=[oT[b], (woutb, c)], w=[p])
                kb.op('dve', lambda E: E.tensor_tensor(out=hs[b][:, half * 512:(half + 1) * 512], in0=p[:],
                                                       in1=xt[b][:, half * 512:(half + 1) * 512], op=ALU.add),
                      r=[p, xt[b]], w=[(hs[b], half)])
            yield
            kb.dma('sp', io['h_s'][t * 128:(t + 1) * 128, :], hs[b][:], r=[(hs[b], 0), (hs[b], 1)], w=['h_s'])
            rms_rstd(kb, hs[b][:], junk, ss[b], rs[b], D, [(hs[b], 0), (hs[b], 1)])
            kb.op('act', lambda E: E.activation(out=hn[b][:], in_=hs[b][:], func=AF.Copy, scale=rs[b][:, 0:1]),
                  r=[(hs[b], 0), (hs[b], 1), rs[b]], w=[hn[b]])
            yield
            yield
            for c in range(8):
                kb.op('pe', lambda E: E.transpose(out=pT[:, c, :], in_=hn[b][:, c * 128:(c + 1) * 128], identity=ident[:]),
                      r=[hn[b], ident], w=[pT])
            kb.op('act', lambda E: E.copy(out=hT[:, :, t4 * 128:(t4 + 1) * 128], in_=pT[:]), r=[pT], w=[hT])
            yield

    def fup(s):
        hT = hnT[s % 2]
        for fb in range(22):
            k2 = fb % 2
            pg = pU[2 * k2]
            pv = pU[2 * k2 + 1]
            for (p_, f0) in ((pg, fb), (pv, 22 + fb)):
                for c in range(8):
                    kb.op('pe', lambda E: E.matmul(p_[:], lhsT=wupb[:, c, f0 * 128:(f0 + 1) * 128], rhs=hT[:, c, :],
                                                   start=(c == 0), stop=(c == 7)), r=[hT, (wupb, c)], w=[p_])
            yield
            g_, v_ = ug[k2], uv[k2]
            kb.op('act', lambda E: E.copy(out=g_[:, 2:514], in_=pg[:]), r=[pg], w=[(g_, 'b')])
            kb.op('act', lambda E: E.copy(out=v_[:, 2:514], in_=pv[:]), r=[pv], w=[(v_, 'b')])
            for (u_, f0) in ((g_, fb), (v_, 22 + fb)):
                kb.op('pool', lambda E: E.tensor_copy(out=u_[:, 0:2], in_=halo[:, f0, :]), r=[(halo, f0)], w=[(u_, 'h')])
                kb.op('pool', lambda E: E.tensor_copy(out=halo[:, f0, :], in_=u_[:, 512:514]), r=[(u_, 'b')], w=[(halo, f0)])
            conv3(pg, ag[k2], g_, fb)
            conv3(pv, av[k2], v_, 22 + fb)
            kb.op('act', lambda E: E.activation(out=sg[k2][:], in_=ag[k2][:], func=AF.Silu), r=[ag[k2]], w=[sg[k2]])
            a_ = act[fb % 3]
            kb.op('dve', lambda E: E.tensor_tensor(out=a_[:], in0=sg[k2][:], in1=av[k2][:], op=ALU.mult), r=[sg[k2], av[k2]], w=[a_])
            kb.dma('sp', io['actT'][fb * 128:(fb + 1) * 128, s * 512:(s + 1) * 512], a_[:], r=[a_], w=['actT'])
            yield

    def interleave(gens):
        gens = list(gens)
        while gens:
            for g_ in list(gens):
                try:
                    next(g_)
                except StopIteration:
                    gens.remove(g_)

    NS = S // 512
    interleave([ftiles(0)])
    for s in range(NS):
        gl = [fup(s)]
        if s + 1 < NS:
            gl.append(ftiles(s + 1))
        interleave(gl)
    ph.close()

    ph = Phase(kb, "f2")
    wdnb = ph.sb("wdnb", [128, 22, D], BF16)
    load_cast_weight(kb, ph, wdnb, io['ffn_w_down'], 22, D)
    aT = ph.sbn("aT", [128, 22, 128], BF16, 2)
    hs = ph.sbn("hs", [128, D], F32, 2)
    ot = ph.sbn("ot", [128, D], F32, 2)
    pD = ph.psn("pD", [128, 512], F32, 4)
    for t in range(NT):
        b = t % 2
        kb.dma('sp', aT[b][:], io['actT'][:, t * 128:(t + 1) * 128].rearrange("(c p) t -> p c t", p=128), r=['actT'], w=[aT[b]])
        kb.dma('sp', hs[b][:], io['h_s'][t * 128:(t + 1) * 128, :], r=['h_s'], w=[hs[b]])
        for half in range(2):
            p = pD[2 * b + half]
            for c in range(22):
                kb.op('pe', lambda E: E.matmul(p[:], lhsT=aT[b][:, c, :], rhs=wdnb[:, c, half * 512:(half + 1) * 512],
                                               start=(c == 0), stop=(c == 21)), r=[aT[b], (wdnb, c)], w=[p])
            kb.op('dve', lambda E: E.tensor_tensor(out=ot[b][:, half * 512:(half + 1) * 512], in0=p[:],
                                                   in1=hs[b][:, half * 512:(half + 1) * 512], op=ALU.add),
                  r=[p, hs[b]], w=[(ot[b], half)])
        kb.dma('sp', io['out'][t * 128:(t + 1) * 128, :], ot[b][:], r=[(ot[b], 0), (ot[b], 1)], w=['out'])
    ph.close()


W_NAMES = ['attn_norm_w', 'mem_norm_w', 'w_in', 'gdn_conv_w', 'gdn_a_log', 'gdn_dt_bias', 'gdn_out_norm_w',
           'nsa_q_norm_w', 'nsa_kc_norm_w', 'nsa_ks_norm_w', 'nsa_kw_norm_w', 'nsa_cmp_pos_k', 'nsa_cmp_pos_v',
           'nsa_cmp_k_w1', 'nsa_cmp_k_w2', 'nsa_cmp_v_w1', 'nsa_cmp_v_w2', 'mem_w_kv', 'mem_q_norm_w', 'mem_k_norm_w',
           'w_out', 'ffn_norm_w', 'ffn_w_up', 'ffn_conv_w', 'ffn_w_down']
W_SHAPES = {
    'attn_norm_w': [D], 'mem_norm_w': [D], 'w_in': [D, INW], 'gdn_conv_w': [4, 1536], 'gdn_a_log': [4], 'gdn_dt_bias': [4],
    'gdn_out_norm_w': [128], 'nsa_q_norm_w': [64], 'nsa_kc_norm_w': [64], 'nsa_ks_norm_w': [64], 'nsa_kw_norm_w': [64],
    'nsa_cmp_pos_k': [32, 64], 'nsa_cmp_pos_v': [32, 64], 'nsa_cmp_k_w1': [2048, 64], 'nsa_cmp_k_w2': [64, 64],
    'nsa_cmp_v_w1': [2048, 64], 'nsa_cmp_v_w2': [64, 64], 'mem_w_kv': [D, 1024], 'mem_q_norm_w': [128], 'mem_k_norm_w': [128],
    'w_out': [1536, D], 'ffn_norm_w': [D], 'ffn_w_up': [D, 2 * FF], 'ffn_conv_w': [3, 2 * FF], 'ffn_w_down': [FF, D],
}


def make_consts():
    c = {}
    c['c_ident'] = np.eye(128, dtype=np.float32).astype(ml_dtypes.bfloat16)
    idx = np.arange(128)
    same = (idx[:, None] // 64) == (idx[None, :] // 64)
    c['c_btri'] = (same & (idx[:, None] <= idx[None, :])).astype(np.float32)
    c['c_bones'] = same.astype(np.float32)
    c['c_strict'] = (same & (idx[:, None] > idx[None, :])).astype(np.float32)
    c['c_mlow'] = np.where(same & (idx[:, None] >= idx[None, :]), 0.0, NEG).astype(np.float32)
    c['c_mup'] = np.ascontiguousarray(c['c_mlow'].T)
    c['c_mch'] = np.stack([(idx < 64), (idx >= 64)], axis=1).astype(np.float32)
    bf = ml_dtypes.bfloat16
    c['c_tril'] = np.where(idx[:, None] <= idx[None, :], 0.0, NEG).astype(np.float32).astype(bf)
    c['c_far'] = np.where(idx[None, :] < idx[:, None], 0.0, NEG).astype(np.float32).astype(bf)
    cm = np.zeros((9, 128, 512), np.float32)
    f = np.arange(512)
    for m in range(9):
        bt, s_ = (0, m) if m < 5 else (1, m - 1)
        blk = 128 * bt + idx
        vis = (16 * blk[:, None] + 31 <= 512 * s_ + f[None, :]) & (blk[:, None] < 255)
        cm[m] = np.where(vis, 0.0, NEG)
    c['c_cmask'] = cm.astype(bf)
    kk = np.arange(S)
    c['c_E'] = np.where((kk[None, :] // 64) == np.arange(64)[:, None], -NEG, 0.0).astype(np.float32).astype(bf)
    ci = np.arange(256) * 16
    sj = np.arange(64) * 64
    ovl = np.clip(np.minimum(ci[:, None] + 32, sj[None, :] + 64) - np.maximum(ci[:, None], sj[None, :]), 0, None) / 16.0
    ovl[255] = 0.0
    c['c_ovl'] = np.ascontiguousarray(ovl.reshape(2, 128, 64).transpose(1, 0, 2)).astype(np.float32).astype(bf)
    pos = np.arange(S, dtype=np.float32)
    inv = (1.0 / (np.float32(500000.0) ** (np.arange(0, 16, 2, dtype=np.float32) / np.float32(16)))).astype(np.float32)
    ang = pos[:, None] * inv[None, :]
    c['c_rope'] = np.concatenate([np.cos(ang), np.sin(ang)], axis=1).astype(np.float32)
    tt = np.arange(S)
    cur = tt // 64
    blk = np.arange(64)
    valid = blk[None, :] <= cur[:, None]
    forced = (blk[None, :] == 0) | (blk[None, :] == cur[:, None]) | (blk[None, :] == cur[:, None] - 1)
    c['c_A'] = (valid & ~forced).astype(np.float32).reshape(NT, 128, 64)
    c['c_B'] = np.where(valid, np.where(forced, 1e6, 0.0), -1e9).astype(np.float32).reshape(NT, 128, 64)
    return c


def build_program(dbg=False, phases=('ip', 'mem', 'gdn', 'nsa', 'ffn'), dbg_ocat=False):
    nc = bass.Bass("TRN2", target_bir_lowering=False)
    io = {}
    io['x'] = nc.dram_tensor("x", [S, D], F32, kind="ExternalInput").ap()
    io['mem'] = nc.dram_tensor("mem", [256, D], F32, kind="ExternalInput").ap()
    for n in W_NAMES:
        io[n] = nc.dram_tensor(n, W_SHAPES[n], F32, kind="ExternalInput").ap()
    for n, v in make_consts().items():
        io[n] = nc.dram_tensor(n, list(v.shape), BF16 if v.dtype == ml_dtypes.bfloat16 else F32, kind="ExternalInput").ap()
    io['out'] = nc.dram_tensor("out", [S, D], F32, kind="ExternalOutput").ap()
    sk = "ExternalOutput" if dbg else "Internal"
    io['tm'] = nc.dram_tensor("tm", [S, TMW], F32, kind=sk).ap()
    io['qkv_tm'] = nc.dram_tensor("qkv_tm", [S, 1536], BF16, kind=sk).ap()
    if dbg_ocat:
        io['ocatT'] = nc.dram_tensor("ocatT", [1536, S], BF16, kind="ExternalInput").ap()
    else:
        io['ocatT'] = nc.dram_tensor("ocatT", [1536, S], BF16, kind=sk).ap()
    io['h_s'] = nc.dram_tensor("h_s", [S, D], F32, kind=sk).ap()
    io['actT'] = nc.dram_tensor("actT", [FF, S], BF16, kind="Internal").ap()
    kb = KB(nc)
    if 'ip' in phases:
        phase_inproj(kb, io)
    if 'mem' in phases:
        phase_mem(kb, io)
    if 'gdn' in phases:
        phase_gdn(kb, io)
    if 'nsa' in phases:
        phase_nsa(kb, io)
    if 'ffn' in phases:
        phase_ffn(kb, io)
    kb.finish()
    return nc, kb


def make_in_maps(inputs):
    consts = make_consts()
    maps = []
    for b in range(8):
        m = {'x': np.ascontiguousarray(inputs['x'][b]), 'mem': np.ascontiguousarray(inputs['mem'][b])}
        for n in W_NAMES:
            m[n] = np.ascontiguousarray(np.asarray(inputs[n])[0])
        m.update(consts)
        maps.append(m)
    return maps


def kernel(**inputs):
    nc, kb = build_program()
    maps = make_in_maps(inputs)
    res = run_bass_kernel_spmd(nc, maps, core_ids=list(range(8)))
    return np.stack([np.asarray(r['out'], dtype=np.float32) for r in res.results], axis=0)
```

```python
import os
import numpy as np
from contextlib import ExitStack
import concourse.bass as bass
import concourse.mybir as mybir
from concourse.bass_utils import run_bass_kernel_spmd
import ml_dtypes

F32 = mybir.dt.float32
BF16 = mybir.dt.bfloat16
AF = mybir.ActivationFunctionType
ALU = mybir.AluOpType
AX = mybir.AxisListType

S = 4096
D = 1024
NT = S // 128
INW = 3872
TMW = 2336
A_OFF, B_OFF, GATE_OFF, NQ_OFF = 0, 4, 8, 520
KC_OFF, VC_OFF, KS_OFF, VS_OFF, KW_OFF, VW_OFF = 1032, 1160, 1288, 1416, 1544, 1672
NG_OFF, MQ_OFF = 1800, 1824
FF = 2816
NEG = -30000.0
EPS = 1e-6


class T:
    def __init__(self, t, k):
        self.t = t
        self.k = k

    def __getitem__(self, idx):
        return self.t[idx]


class KB:
    NDS = 16

    def __init__(self, nc):
        self.nc = nc
        self.stack = ExitStack()
        self.eng = {'pe': nc.tensor, 'act': nc.scalar, 'dve': nc.vector, 'pool': nc.gpsimd, 'sp': nc.sync}
        self.sem = {}
        for e in self.eng:
            self.sem[e] = self.stack.enter_context(nc.semaphore("s_" + e))
        for j in range(self.NDS):
            self.sem[('d', j)] = self.stack.enter_context(nc.semaphore("d_%d" % j))
        self.cnt = {e: 0 for e in self.eng}
        self.seen = {e: {} for e in self.eng}
        self.state = {}
        self.dma_i = 0
        self.dma_uses = [0] * self.NDS
        self.nins = 0
        self.rr = 0
        self.excl = set()

    def _wait(self, e, evs):
        need = {}
        for (sk, v) in evs:
            if sk == e and e in ('pe', 'sp'):
                continue
            if self.seen[e].get(sk, 0) < v:
                need[sk] = max(need.get(sk, 0), v)
        for sk, v in need.items():
            self.eng[e].wait_ge(self.sem[sk], v)
            self.seen[e][sk] = v

    @staticmethod
    def _keys(lst):
        out = []
        for x in lst:
            if isinstance(x, T):
                out.append(x.k)
            elif isinstance(x, (list, tuple)) and len(x) and isinstance(x[0], T):
                out.append((x[0].k,) + tuple(x[1:]))
            else:
                out.append(x)
        return out

    def _deps(self, reads, writes):
        evs = []
        for k in reads:
            st = self.state.get(k)
            if st and st[0]:
                evs.append(st[0])
        for k in writes:
            st = self.state.get(k)
            if st:
                if st[0]:
                    evs.append(st[0])
                evs.extend(st[1])
        return evs

    def _update(self, ev, reads, writes):
        for k in reads:
            st = self.state.setdefault(k, [None, []])
            st[1].append(ev)
            if len(st[1]) > 12:
                best = {}
                for (sk, v) in st[1]:
                    best[sk] = max(best.get(sk, 0), v)
                st[1] = list(best.items())
        for k in writes:
            self.state[k] = [ev, []]

    def op(self, e, fn, r=(), w=()):
        r = self._keys(r)
        w = self._keys(w)
        w = w + [k for k in r if k in self.excl and k not in w]
        self._wait(e, self._deps(r, w))
        ins = fn(self.eng[e])
        self.cnt[e] += 1
        ins.then_inc(self.sem[e], 1)
        self._update((e, self.cnt[e]), r, w)
        self.nins += 1
        return ins

    def dma(self, q, out, in_, r=(), w=(), **kw):
        r = self._keys(r)
        w = self._keys(w)
        j = self.dma_i % self.NDS
        self.dma_i += 1
        evs = self._deps(r, w)
        if self.dma_uses[j] > 0:
            evs.append((('d', j), 16 * self.dma_uses[j]))
        self._wait(q, evs)
        ins = self.eng[q].dma_start(out=out, in_=in_, **kw)
        self.dma_uses[j] += 1
        ins.then_inc(self.sem[('d', j)], 16)
        ev = (('d', j), 16 * self.dma_uses[j])
        self._update(ev, r, w)
        self.nins += 1
        return ev

    def barrier(self):
        evs = [(f, self.cnt[f]) for f in self.eng if self.cnt[f]]
        evs += [(('d', j), 16 * self.dma_uses[j]) for j in range(self.NDS) if self.dma_uses[j]]
        for e in self.eng:
            self._wait(e, [ev for ev in evs if ev[0] != e])

    def finish(self):
        self.barrier()
        self.stack.close()

    def ew(self, with_act=False):
        self.rr += 1
        lst = ('dve', 'pool', 'act') if with_act else ('dve', 'pool')
        return lst[self.rr % len(lst)]


class Phase:
    def __init__(self, kb, tag):
        self.kb = kb
        self.nc = kb.nc
        self.tag = tag
        self.st = ExitStack()

    def sb(self, name, shape, dt):
        n = self.tag + "_" + name
        return T(self.st.enter_context(self.nc.sbuf_tensor(n, list(shape), dt)), n)

    def sbn(self, name, shape, dt, n):
        return [self.sb("%s%d" % (name, i), shape, dt) for i in range(n)]

    def ps(self, name, shape, dt=F32):
        n = self.tag + "_" + name
        self.kb.excl.add(n)
        return T(self.st.enter_context(self.nc.psum_tensor(n, list(shape), dt)), n)

    def psn(self, name, shape, dt, n):
        return [self.ps("%s%d" % (name, i), shape, dt) for i in range(n)]

    def close(self):
        self.kb.barrier()
        self.st.close()


def load_cast_weight(kb, ph, dst, src_ap, nchunks, ncols, gam=None, stage_cols=None):
    stage_cols = stage_cols or ncols
    stg = ph.sbn("stg_" + dst.k, [128, stage_cols], F32, 2)
    i = 0
    engs = ('dve', 'act', 'dve')
    for c in range(nchunks):
        for c0 in range(0, ncols, stage_cols):
            c1 = min(ncols, c0 + stage_cols)
            sg = stg[i % 2]
            kb.dma('sp', sg[:, 0:c1 - c0], src_ap[c * 128:(c + 1) * 128, c0:c1], w=[sg])
            e = engs[i % 3]
            o = dst[:, c, c0:c1]
            if gam is None:
                if e == 'act':
                    kb.op(e, lambda E: E.copy(out=o, in_=sg[:, 0:c1 - c0]), r=[sg], w=[(dst, c)])
                else:
                    kb.op(e, lambda E: E.tensor_copy(out=o, in_=sg[:, 0:c1 - c0]), r=[sg], w=[(dst, c)])
            else:
                if e == 'act':
                    kb.op(e, lambda E: E.activation(out=o, in_=sg[:, 0:c1 - c0], func=AF.Copy, scale=gam[:, c:c + 1]),
                          r=[sg, gam], w=[(dst, c)])
                else:
                    kb.op(e, lambda E: E.tensor_scalar(out=o, in0=sg[:, 0:c1 - c0], scalar1=gam[:, c:c + 1], scalar2=None,
                                                       op0=ALU.mult), r=[sg, gam], w=[(dst, c)])
            i += 1


def rms_rstd(kb, src_ap, junk, ss, rs, n, rkeys):
    kb.op('act', lambda E: E.activation(out=junk[:, 0:n], in_=src_ap, func=AF.Square, accum_out=ss[:]), r=rkeys, w=[junk, ss])
    kb.op('act', lambda E: E.activation(out=rs[:], in_=ss[:], func=AF.Sqrt, scale=1.0 / n, bias=EPS), r=[ss], w=[rs])
    kb.op('dve', lambda E: E.reciprocal(out=rs[:], in_=rs[:]), r=[rs], w=[rs])


def phase_inproj(kb, io):
    nc = kb.nc
    ph = Phase(kb, "ip")
    winb = ph.sb("winb", [128, 8, INW], BF16)
    gam = ph.sb("gam", [128, 8], F32)
    cw = ph.sb("cw", [128, 4, 12], F32)
    ident = ph.sb("ident", [128, 128], BF16)
    kb.dma('sp', gam[:], io['attn_norm_w'].rearrange("(c p) -> p c", p=128), w=[gam], allow_slow_non_contiguous=True)
    for j in range(4):
        kb.dma('sp', cw[:, j, :], io['gdn_conv_w'][j, :].rearrange("(c p) -> p c", p=128), w=[cw], allow_slow_non_contiguous=True)
    kb.dma('sp', ident[:], io['c_ident'][:, :], w=[ident])
    load_cast_weight(kb, ph, winb, io['w_in'], 8, INW, gam=gam, stage_cols=1936)

    xt = ph.sbn("xt", [128, D], F32, 2)
    junk = ph.sb("junk", [128, D], BF16)
    ss = ph.sbn("ss", [128, 1], F32, 2)
    rs = ph.sbn("rs", [128, 1], F32, 2)
    xn = ph.sbn("xn", [128, D], BF16, 2)
    xnT = ph.sbn("xnT", [128, 8, 512], BF16, 2)
    xc = ph.sbn("xc", [128, 515], F32, 3)
    acc = ph.sbn("acc", [128, 512], F32, 3)
    halo = ph.sb("halo", [128, 12, 3], F32)
    qT = ph.sb("qT", [128, 12, 512], BF16)
    qtm = ph.sbn("qtm", [128, 1536], BF16, 2)
    tmt = ph.sbn("tmt", [128, TMW], F32, 2)
    pT = ph.psn("pT", [128, 8, 128], BF16, 2)
    pF = ph.psn("pF", [128, 512], F32, 2)
    pM = ph.psn("pM", [128, 512], F32, 2)
    pQ = ph.psn("pQ", [128, 8, 128], BF16, 2)

    kb.op('pool', lambda E: E.memset(halo[:], 0.0), w=[halo])
    it = 0
    for s in range(S // 512):
        xT = xnT[s % 2]
        for t4 in range(4):
            t = s * 4 + t4
            b = it % 2
            it += 1
            kb.dma('sp', xt[b][:], io['x'][t * 128:(t + 1) * 128, :], w=[xt[b]])
            rms_rstd(kb, xt[b][:], junk, ss[b], rs[b], D, [xt[b]])
            kb.op('dve', lambda E: E.tensor_scalar(out=xn[b][:], in0=xt[b][:], scalar1=rs[b][:, 0:1], scalar2=None, op0=ALU.mult),
                  r=[xt[b], rs[b]], w=[xn[b]])
            for c in range(8):
                kb.op('pe', lambda E: E.transpose(out=pT[b][:, c, :], in_=xn[b][:, c * 128:(c + 1) * 128], identity=ident[:]),
                      r=[xn[b], ident], w=[pT[b]])
            kb.op('act', lambda E: E.copy(out=xT[:, :, t4 * 128:(t4 + 1) * 128], in_=pT[b][:]), r=[pT[b]], w=[xT])
        for cb in range(12):
            pf = pF[cb % 2]
            for c in range(8):
                kb.op('pe', lambda E: E.matmul(pf[:], lhsT=winb[:, c, cb * 128:(cb + 1) * 128], rhs=xT[:, c, :],
                                               start=(c == 0), stop=(c == 7)), r=[xT, (winb, c)], w=[pf])
            x3 = xc[cb % 3]
            ac = acc[cb % 3]
            kb.op('act', lambda E: E.copy(out=x3[:, 3:515], in_=pf[:]), r=[pf], w=[(x3, 'b')])
            kb.op('pool', lambda E: E.tensor_copy(out=x3[:, 0:3], in_=halo[:, cb, :]), r=[(halo, cb)], w=[(x3, 'h')])
            kb.op('pool', lambda E: E.tensor_copy(out=halo[:, cb, :], in_=x3[:, 512:515]), r=[(x3, 'b')], w=[(halo, cb)])
            kb.op('act', lambda E: E.activation(out=ac[:], in_=pf[:], func=AF.Copy, scale=cw[:, 3, cb:cb + 1]), r=[pf, cw], w=[ac])
            for j in range(3):
                kb.op('dve', lambda E: E.scalar_tensor_tensor(out=ac[:], in0=x3[:, j:j + 512], scalar=cw[:, j, cb:cb + 1], in1=ac[:],
                                                           op0=ALU.mult, op1=ALU.add), r=[(x3, 'b'), (x3, 'h'), cw, ac], w=[ac])
            kb.op('act', lambda E: E.activation(out=qT[:, cb, :], in_=ac[:], func=AF.Silu), r=[ac], w=[(qT, cb)])
        for t4 in range(4):
            t = s * 4 + t4
            qm = qtm[t % 2]
            for g3 in range(3):
                pq = pQ[g3 % 2]
                for j in range(4):
                    cb = g3 * 4 + j
                    kb.op('pe', lambda E: E.transpose(out=pq[:, j, :], in_=qT[:, cb, t4 * 128:(t4 + 1) * 128], identity=ident[:]),
                          r=[(qT, cb), ident], w=[pq])
                e1 = 'dve' if g3 % 2 == 0 else 'act'
                if e1 == 'dve':
                    kb.op('dve', lambda E: E.tensor_copy(out=qm[:, g3 * 512:(g3 + 1) * 512], in_=pq[:, 0:4, :].rearrange("p a b -> p (a b)")),
                          r=[pq], w=[(qm, g3)])
                else:
                    kb.op('act', lambda E: E.copy(out=qm[:, g3 * 512:(g3 + 1) * 512], in_=pq[:, 0:4, :].rearrange("p a b -> p (a b)")),
                          r=[pq], w=[(qm, g3)])
            kb.dma('sp', io['qkv_tm'][t * 128:(t + 1) * 128, :], qm[:], r=[(qm, 0), (qm, 1), (qm, 2)], w=['qkv_tm'])
            tm = tmt[t % 2]
            for ci, n0 in enumerate(range(0, TMW, 512)):
                n1 = min(TMW, n0 + 512)
                pm = pM[ci % 2]
                for c in range(8):
                    kb.op('pe', lambda E: E.matmul(pm[:, 0:n1 - n0], lhsT=xT[:, c, t4 * 128:(t4 + 1) * 128],
                                                   rhs=winb[:, c, 1536 + n0:1536 + n1], start=(c == 0), stop=(c == 7)),
                          r=[xT, (winb, c)], w=[pm])
                if ci % 2 == 0:
                    kb.op('dve', lambda E: E.tensor_copy(out=tm[:, n0:n1], in_=pm[:, 0:n1 - n0]), r=[pm], w=[(tm, ci)])
                else:
                    kb.op('act', lambda E: E.copy(out=tm[:, n0:n1], in_=pm[:, 0:n1 - n0]), r=[pm], w=[(tm, ci)])
            kb.dma('sp', io['tm'][t * 128:(t + 1) * 128, :], tm[:], r=[(tm, i) for i in range(5)], w=['tm'])
    ph.close()


def attn_block(kb, o_ps, PTbuf, pti, sc_ps, kt_specs, ident, rkeys_q, LA=2):
    n = len(kt_specs)
    pts = [None] * n
    started = set()
    npv = sum(sp['qt1'] - sp['qt0'] for sp in kt_specs)
    done = 0
    for i in range(n + LA):
        if i < n:
            sp = kt_specs[i]
            pss = sc_ps[pti[0] % len(sc_ps)]
            ptb = PTbuf[pti[0] % len(PTbuf)]
            pti[0] += 1
            pts[i] = ptb
            q0, q1 = sp['qt0'], sp['qt1']
            ncol = (q1 - q0) * 128
            nk = sp['nk']
            nm = len(sp['masks'])
            kb.op('pe', lambda E: E.matmul(pss[0:nk, 0:ncol], lhsT=sp['lhsT'], rhs=sp['rhs_fn'](q0, q1), start=True, stop=(nm == 0)),
                  r=sp['rk'] + rkeys_q, w=[pss])
            for mi, (c0, nc_, mask) in enumerate(sp['masks']):
                kb.op('pe', lambda E: E.matmul(pss[0:nk, c0:c0 + nc_], lhsT=ident[0:nk, 0:nk], rhs=mask, start=False, stop=(mi == nm - 1)),
                      r=[ident] + sp.get('mk', []), w=[pss])
            kb.op('act', lambda E: E.activation(out=ptb[0:nk, 0:ncol], in_=pss[0:nk, 0:ncol], func=AF.Exp), r=[pss], w=[ptb])
        j = i - LA
        if j >= 0:
            sp = kt_specs[j]
            ptb = pts[j]
            nk = sp['nk']
            for qt in range(sp['qt0'], sp['qt1']):
                c0 = (qt - sp['qt0']) * 128
                oap, okey, bank = o_ps(qt)
                st = bank not in started
                started.add(bank)
                done += 1
                kb.op('pe', lambda E: E.matmul(oap, lhsT=ptb[0:nk, c0:c0 + 128], rhs=sp['v_fn'](qt), start=st, stop=(done == npv),
                                               skip_group_check=True), r=[ptb] + sp['rv'], w=[okey])


def phase_mem(kb, io):
    nc = kb.nc
    ph = Phase(kb, "mm")
    ident = ph.sb("ident", [128, 128], BF16)
    kb.dma('sp', ident[:], io['c_ident'][:, :], w=[ident])
    wkv = ph.sb("wkv", [128, 8, 1024], BF16)
    gam = ph.sb("gam", [128, 8], F32)
    kb.dma('sp', gam[:], io['mem_norm_w'].rearrange("(c p) -> p c", p=128), w=[gam], allow_slow_non_contiguous=True)
    load_cast_weight(kb, ph, wkv, io['mem_w_kv'], 8, 1024, gam=gam)
    qnw = ph.sb("qnw", [128, 128], F32)
    knw = ph.sb("knw", [128, 128], F32)
    kb.dma('sp', qnw[:], io['mem_q_norm_w'].partition_broadcast(128), w=[qnw])
    kb.dma('sp', knw[:], io['mem_k_norm_w'].partition_broadcast(128), w=[knw])
    kb.op('dve', lambda E: E.tensor_scalar(out=qnw[:], in0=qnw[:], scalar1=128 ** -0.5, scalar2=None, op0=ALU.mult), r=[qnw], w=[qnw])

    mt = ph.sbn("mt", [128, D], F32, 2)
    junk = ph.sb("junk", [128, D], BF16)
    ss = ph.sb("ss", [128, 1], F32)
    rs = ph.sb("rs", [128, 1], F32)
    mn = ph.sb("mn", [128, D], BF16)
    mnT = ph.sb("mnT", [128, 8, 128], BF16)
    kvt = ph.sb("kvt", [128, 1024], F32)
    sq4 = ph.sb("sq4", [128, 4, 128], F32)
    ss4 = ph.sb("ss4", [128, 4], F32)
    rs4 = ph.sb("rs4", [128, 4], F32)
    kn = ph.sb("kn", [128, 4, 128], BF16)
    kT = ph.sb("kT", [128, 4, 256], BF16)
    v1 = ph.sb("v1", [128, 2, 4, 129], BF16)
    pT = ph.ps("pT", [128, 8, 128], BF16)
    pK = ph.psn("pK", [128, 512], F32, 2)
    kb.op('pool', lambda E: E.memset(v1[:], 1.0), w=[v1])
    for mt_i in range(2):
        m = mt[mt_i]
        kb.dma('sp', m[:], io['mem'][mt_i * 128:(mt_i + 1) * 128, :], w=[m])
        rms_rstd(kb, m[:], junk, ss, rs, D, [m])
        kb.op('dve', lambda E: E.tensor_scalar(out=mn[:], in0=m[:], scalar1=rs[:, 0:1], scalar2=None, op0=ALU.mult), r=[m, rs], w=[mn])
        for c in range(8):
            kb.op('pe', lambda E: E.transpose(out=pT[:, c, :], in_=mn[:, c * 128:(c + 1) * 128], identity=ident[:]), r=[mn, ident], w=[pT])
        kb.op('act', lambda E: E.copy(out=mnT[:], in_=pT[:]), r=[pT], w=[mnT])
        for half in range(2):
            pk = pK[half]
            for c in range(8):
                kb.op('pe', lambda E: E.matmul(pk[:], lhsT=mnT[:, c, :], rhs=wkv[:, c, half * 512:(half + 1) * 512],
                                               start=(c == 0), stop=(c == 7)), r=[mnT, (wkv, c)], w=[pk])
            kb.op('act', lambda E: E.copy(out=kvt[:, half * 512:(half + 1) * 512], in_=pk[:]), r=[pk], w=[(kvt, half)])
        k3 = kvt[:, 0:512].rearrange("p (h d) -> p h d", h=4)
        kb.op('dve', lambda E: E.tensor_tensor(out=sq4[:], in0=k3, in1=k3, op=ALU.mult), r=[(kvt, 0)], w=[sq4])
        kb.op('dve', lambda E: E.tensor_reduce(out=ss4[:], in_=sq4[:], axis=AX.X, op=ALU.add), r=[sq4], w=[ss4])
        kb.op('act', lambda E: E.activation(out=rs4[:], in_=ss4[:], func=AF.Sqrt, scale=1.0 / 128, bias=EPS), r=[ss4], w=[rs4])
        kb.op('dve', lambda E: E.reciprocal(out=rs4[:], in_=rs4[:]), r=[rs4], w=[rs4])
        kb.op('dve', lambda E: E.tensor_tensor(out=sq4[:], in0=k3, in1=rs4[:].unsqueeze(2).to_broadcast([128, 4, 128]), op=ALU.mult),
              r=[(kvt, 0), rs4], w=[sq4])
        kb.op('dve', lambda E: E.tensor_tensor(out=kn[:], in0=sq4[:], in1=knw[:].unsqueeze(1).to_broadcast([128, 4, 128]), op=ALU.mult),
              r=[sq4, knw], w=[kn])
        for h in range(4):
            kb.op('pe', lambda E: E.transpose(out=pT[:, h, :], in_=kn[:, h, :], identity=ident[:]), r=[kn, ident], w=[pT])
        kb.op('act', lambda E: E.copy(out=kT[:, :, mt_i * 128:(mt_i + 1) * 128], in_=pT[:, 0:4, :]), r=[pT], w=[kT])
        kb.op('dve', lambda E: E.tensor_copy(out=v1[:, mt_i, :, 0:128], in_=kvt[:, 512:1024].rearrange("p (h d) -> p h d", h=4)),
              r=[(kvt, 1)], w=[v1])

    qt_ = ph.sbn("qt", [128, 512], F32, 2)
    qs = ph.sb("qs", [128, 4, 128], F32)
    qn = ph.sbn("qn", [128, 4, 128], BF16, 2)
    qT = ph.sbn("qT", [128, 4, 512], BF16, 2)
    PTb = ph.sbn("PT", [128, 512], BF16, 4)
    pti = [0]
    oc = ph.sbn("oc", [128, 4, 128], BF16, 4)
    rinv = ph.sb("rinv", [128, 1], F32)
    ocT = ph.sbn("ocT", [128, 4, 128], BF16, 2)
    sc_ps = ph.psn("sc", [128, 512], F32, 2)
    o_psA = ph.ps("oA", [128, 2, 256], F32)
    o_psB = ph.ps("oB", [128, 2, 256], F32)
    pO = ph.ps("pO", [128, 8, 128], BF16)
    for s in range(S // 512):
        qTs = qT[s % 2]
        for t4 in range(4):
            t = s * 4 + t4
            q = qt_[t % 2]
            qb = qn[t % 2]
            kb.dma('sp', q[:], io['tm'][t * 128:(t + 1) * 128, MQ_OFF:MQ_OFF + 512], r=['tm'], w=[q])
            q3 = q[:].rearrange("p (h d) -> p h d", h=4)
            kb.op('pool', lambda E: E.tensor_tensor(out=qs[:], in0=q3, in1=q3, op=ALU.mult), r=[q], w=[qs])
            kb.op('dve', lambda E: E.tensor_reduce(out=ss4[:], in_=qs[:], axis=AX.X, op=ALU.add), r=[qs], w=[ss4])
            kb.op('act', lambda E: E.activation(out=rs4[:], in_=ss4[:], func=AF.Sqrt, scale=1.0 / 128, bias=EPS), r=[ss4], w=[rs4])
            kb.op('dve', lambda E: E.reciprocal(out=rs4[:], in_=rs4[:]), r=[rs4], w=[rs4])
            kb.op('dve', lambda E: E.tensor_tensor(out=qs[:], in0=q3, in1=rs4[:].unsqueeze(2).to_broadcast([128, 4, 128]), op=ALU.mult),
                  r=[q, rs4], w=[qs])
            kb.op('pool', lambda E: E.tensor_tensor(out=qb[:], in0=qs[:], in1=qnw[:].unsqueeze(1).to_broadcast([128, 4, 128]), op=ALU.mult),
                  r=[qs, qnw], w=[qb])
            for h in range(4):
                kb.op('pe', lambda E: E.transpose(out=pT[:, h, :], in_=qb[:, h, :], identity=ident[:]), r=[qb, ident], w=[pT])
            kb.op('act', lambda E: E.copy(out=qTs[:, :, t4 * 128:(t4 + 1) * 128], in_=pT[:, 0:4, :]), r=[pT], w=[qTs])
        for h in range(4):
            def o_ps(qt, h=h):
                return (o_psA[:, qt, 0:129], o_psA, 'A') if qt < 2 else (o_psB[:, qt - 2, 0:129], o_psB, 'B')
            specs = []
            for kt in range(2):
                specs.append(dict(lhsT=kT[:, h, kt * 128:(kt + 1) * 128], rhs_fn=lambda q0, q1, h=h: qTs[:, h, q0 * 128:q1 * 128],
                                  qt0=0, qt1=4, masks=[], v_fn=lambda qt, kt=kt, h=h: v1[:, kt, h, :], nk=128, rk=[kT], rv=[v1]))
            attn_block(kb, o_ps, PTb, pti, sc_ps, specs, ident, [qTs])
            for t4 in range(4):
                t = s * 4 + t4
                ob = oc[t4]
                oap, okey, _ = o_ps(t4)
                kb.op('dve', lambda E: E.reciprocal(out=rinv[:], in_=oap[:, 128:129]), r=[okey], w=[rinv])
                kb.op('dve', lambda E: E.tensor_scalar(out=ob[:, h, :], in0=oap[:, 0:128], scalar1=rinv[:, 0:1], scalar2=None, op0=ALU.mult),
                      r=[okey, rinv], w=[(ob, h)])
        for t4 in range(4):
            t = s * 4 + t4
            ob = oc[t4]
            oT = ocT[t % 2]
            for h in range(4):
                kb.op('pe', lambda E: E.transpose(out=pO[:, h, :], in_=ob[:, h, :], identity=ident[:]), r=[(ob, h), ident], w=[pO])
            kb.op('act', lambda E: E.copy(out=oT[:], in_=pO[:, 0:4, :]), r=[pO], w=[oT])
            kb.dma('sp', io['ocatT'][t, :, 8:12, :], oT[:], r=[oT], w=['ocatT_c'])
    ph.close()


def phase_gdn(kb, io):
    nc = kb.nc
    ph = Phase(kb, "gd")
    ident = ph.sb("ident", [128, 128], BF16)
    btri = ph.sb("btri", [128, 128], F32)
    bones = ph.sb("bones", [128, 128], F32)
    ones = ph.sb("ones", [128, 128], F32)
    mlow = ph.sb("mlow", [128, 4, 128], F32)
    mup = ph.sb("mup", [128, 4, 128], F32)
    strict = ph.sb("strict", [128, 128], F32)
    mch = ph.sb("mch", [128, 2], F32)
    kb.dma('sp', ident[:], io['c_ident'][:, :], w=[ident])
    kb.dma('sp', btri[:], io['c_btri'][:, :], w=[btri])
    kb.dma('sp', bones[:], io['c_bones'][:, :], w=[bones])
    kb.dma('sp', strict[:], io['c_strict'][:, :], w=[strict])
    kb.dma('sp', mch[:], io['c_mch'][:, :], w=[mch])
    for h in range(4):
        kb.dma('sp', mlow[:, h, :], io['c_mlow'][:, :], w=[mlow])
        kb.dma('sp', mup[:, h, :], io['c_mup'][:, :], w=[mup])
    kb.op('pool', lambda E: E.memset(ones[:], 1.0), w=[ones])
    dtb = ph.sb("dtb", [128, 4], F32)
    nA = ph.sb("nA", [128, 4], F32)
    gw = ph.sb("gw", [128, 128], F32)
    kb.dma('sp', dtb[:], io['gdn_dt_bias'].partition_broadcast(128), w=[dtb])
    kb.dma('sp', nA[:], io['gdn_a_log'].partition_broadcast(128), w=[nA])
    kb.dma('sp', gw[:], io['gdn_out_norm_w'].partition_broadcast(128), w=[gw])
    kb.op('act', lambda E: E.activation(out=nA[:], in_=nA[:], func=AF.Exp), r=[nA], w=[nA])
    kb.op('dve', lambda E: E.tensor_scalar(out=nA[:], in0=nA[:], scalar1=-1.0, scalar2=None, op0=ALU.mult), r=[nA], w=[nA])

    def B4(t_, n=4):
        return t_.unsqueeze(2).to_broadcast([128, n, 128])

    def M4(t_):
        return t_.unsqueeze(1).to_broadcast([128, 4, 128])

    qkv = ph.sbn("qkv", [128, 3, 4, 128], BF16, 2)
    ab = ph.sbn("ab", [128, 8], F32, 2)
    gt = ph.sbn("gt", [128, 512], F32, 2)
    sm = ph.sbn("sm", [128, 64], F32, 2)
    gs = ph.sbn("gs", [128, 16], F32, 2)
    gm = ph.sb("gm", [128, 8], F32)
    R1 = ph.sb("R1", [128, 4, 128], F32)
    R2 = ph.sb("R2", [128, 4, 128], F32)
    tmpA = ph.sb("tmpA", [128, 4, 128], F32)
    tmpB = ph.sb("tmpB", [128, 4, 128], F32)
    dec = ph.sb("dec", [128, 4, 128], F32)
    decT = ph.sb("decT", [128, 4, 128], F32)
    sq = ph.sb("sq", [128, 4, 128], F32)
    KBG = ph.sb("KBG", [128, 4, 128], BF16)
    Kdb = ph.sbn("Kd", [128, 4, 128], BF16, 4)
    VBb = ph.sbn("VB", [128, 4, 128], BF16, 2)
    dg = ph.sbn("dg", [128, 4, 128], BF16, 4)
    QT = ph.sb("QT", [128, 4, 128], BF16)
    QGb = ph.sbn("QG", [128, 4, 128], BF16, 4)
    KT = ph.sb("KT", [128, 4, 128], BF16)
    nbs = ph.sb("nbs", [128, 4, 128], F32)
    Xb = ph.sbn("X", [128, 4, 128], BF16, 2)
    Yb = ph.sbn("Y", [128, 4, 128], BF16, 2)
    Pbb = ph.sbn("P", [128, 4, 128], BF16, 4)
    aqkTb = ph.sbn("aqkT", [128, 4, 128], BF16, 2)
    negWTb = ph.sbn("negWT", [128, 4, 128], BF16, 2)
    vnew = ph.sb("vnew", [128, 4, 128], BF16)
    Sf = ph.sb("Sf", [128, 4, 128], F32)
    Sbf = ph.sbn("Sbf", [128, 4, 128], BF16, 3)
    osb = ph.sb("osb", [128, 4, 128], F32)
    sgt = ph.sb("sgt", [128, 512], F32)
    oa = ph.sbn("oa", [128, 4, 128], BF16, 2)
    oaT = ph.sbn("oaT", [128, 4, 128], BF16, 2)
    pS = ph.ps("pS", [128, 512], F32)
    pD = ph.ps("pD", [128, 4, 128], F32)
    pA = ph.psn("pA", [128, 4, 128], F32, 2)
    pB = ph.psn("pB", [128, 8, 128], BF16, 1)
    pV = ph.ps("pV", [128, 4, 128], F32)
    pdS = ph.ps("pdS", [128, 4, 128], F32)
    pO = ph.ps("pO", [128, 4, 128], F32)
    kb.op('pool', lambda E: E.memset(Sf[:], 0.0), w=[Sf])
    kb.op('pool', lambda E: E.memset(Sbf[0][:], 0.0), w=[Sbf[0]])
    kb.op('pool', lambda E: E.memset(vnew[:], 0.0), w=[vnew])
    pai = [0]

    def PA():
        pai[0] += 1
        return pA[pai[0] % 2]

    def mm4(p, lf, rf, rk):
        for h in range(4):
            kb.op('pe', lambda E: E.matmul(p[:, h, :], lhsT=lf(h), rhs=rf(h), start=True, stop=True), r=rk, w=[p])

    si = [0]

    def tile(t):
        b = t % 2
        x_ = qkv[b]
        VB, aqkT, negWT = VBb[b], aqkTb[b], negWTb[b]
        Kd = Kdb[2 * b:2 * b + 2]
        QG = QGb[2 * b:2 * b + 2]
        Pb = Pbb[2 * b:2 * b + 2]
        s_ = sm[b]
        kb.dma('sp', x_[:].rearrange("p a h d -> p (a h d)"), io['qkv_tm'][t * 128:(t + 1) * 128, :], r=['qkv_tm'], w=[x_])
        kb.dma('sp', ab[b][:], io['tm'][t * 128:(t + 1) * 128, 0:8], r=['tm'], w=[ab[b]])
        kb.dma('sp', gt[b][:], io['tm'][t * 128:(t + 1) * 128, GATE_OFF:GATE_OFF + 512], r=['tm'], w=[gt[b]])
        g = s_[:, 0:4]
        kb.op('dve', lambda E: E.tensor_tensor(out=g, in0=ab[b][:, 0:4], in1=dtb[:], op=ALU.add), r=[ab[b], dtb], w=[s_])
        kb.op('act', lambda E: E.activation(out=g, in_=g, func=AF.Exp), r=[s_], w=[s_])
        kb.op('act', lambda E: E.activation(out=g, in_=g, func=AF.Ln, bias=1.0), r=[s_], w=[s_])
        kb.op('dve', lambda E: E.tensor_tensor(out=g, in0=g, in1=nA[:], op=ALU.mult), r=[s_, nA], w=[s_])
        kb.op('dve', lambda E: E.tensor_scalar(out=s_[:, 4:8], in0=g, scalar1=-1.0, scalar2=None, op0=ALU.mult), r=[s_], w=[s_])
        kb.op('act', lambda E: E.activation(out=s_[:, 8:12], in_=ab[b][:, 4:8], func=AF.Exp, scale=-1.0), r=[ab[b], s_], w=[s_])
        kb.op('dve', lambda E: E.tensor_scalar(out=s_[:, 8:12], in0=s_[:, 8:12], scalar1=1.0, scalar2=None, op0=ALU.add), r=[s_], w=[s_])
        kb.op('dve', lambda E: E.reciprocal(out=s_[:, 8:12], in_=s_[:, 8:12]), r=[s_], w=[s_])
        kb.op('dve', lambda E: E.tensor_scalar(out=s_[:, 12:16], in0=s_[:, 8:12], scalar1=-1.0, scalar2=None, op0=ALU.mult), r=[s_], w=[s_])
        for j in range(2):
            kb.op('dve', lambda E: E.tensor_scalar(out=gm[:, 4 * j:4 * j + 4], in0=g, scalar1=mch[:, j:j + 1], scalar2=None, op0=ALU.mult),
                  r=[s_, mch], w=[gm])
        kb.op('pe', lambda E: E.matmul(pS[:, 0:4], lhsT=btri[:], rhs=g, start=True, stop=True), r=[btri, s_], w=[pS])
        kb.op('pe', lambda E: E.matmul(pS[:, 4:8], lhsT=bones[:], rhs=g, start=True, stop=True), r=[bones, s_], w=[pS])
        kb.op('pe', lambda E: E.matmul(pS[:, 8:16], lhsT=ones[:], rhs=gm[:], start=True, stop=True), r=[ones, gm], w=[pS])
        G = gs[b]
        kb.op('dve', lambda E: E.tensor_copy(out=G[:], in_=pS[:, 0:16]), r=[pS], w=[G])
        kb.op('act', lambda E: E.activation(out=s_[:, 16:20], in_=G[:, 0:4], func=AF.Exp), r=[G, s_], w=[s_])
        kb.op('dve', lambda E: E.tensor_tensor(out=G[:, 4:8], in0=G[:, 4:8], in1=G[:, 0:4], op=ALU.subtract), r=[G], w=[G])
        kb.op('act', lambda E: E.activation(out=s_[:, 20:24], in_=G[:, 4:8], func=AF.Exp), r=[G, s_], w=[s_])
        kb.op('act', lambda E: E.activation(out=s_[:, 56:64], in_=G[:, 8:16], func=AF.Exp), r=[G, s_], w=[s_])
        yield
        kb.op('pool', lambda E: E.tensor_copy(out=R1[:], in_=B4(g)), r=[s_], w=[R1])
        kb.op('pool', lambda E: E.tensor_tensor(out=R2[:], in0=M4(btri[:]), in1=B4(s_[:, 4:8]), op=ALU.mult), r=[s_, btri], w=[R2])
        kb.op('pe', lambda E: E.matmul(pD[:].rearrange("p a b -> p (a b)"), lhsT=btri[:], rhs=R1[:].rearrange("p a b -> p (a b)"),
                                       start=True, stop=False), r=[btri, R1], w=[pD])
        kb.op('pe', lambda E: E.matmul(pD[:].rearrange("p a b -> p (a b)"), lhsT=bones[:], rhs=R2[:].rearrange("p a b -> p (a b)"),
                                       start=False, stop=True), r=[bones, R2], w=[pD])
        kb.op('dve', lambda E: E.tensor_tensor(out=tmpA[:], in0=pD[:], in1=mlow[:], op=ALU.add), r=[pD, mlow], w=[tmpA])
        kb.op('act', lambda E: E.activation(out=dec[:], in_=tmpA[:], func=AF.Exp), r=[tmpA], w=[dec])
        kb.op('dve', lambda E: E.scalar_tensor_tensor(out=tmpB[:], in0=pD[:], scalar=-1.0, in1=mup[:], op0=ALU.mult, op1=ALU.add),
              r=[pD, mup], w=[tmpB])
        kb.op('act', lambda E: E.activation(out=decT[:], in_=tmpB[:], func=AF.Exp), r=[tmpB], w=[decT])
        yield
        for a_, c0 in ((0, 24), (1, 28)):
            kb.op('pool', lambda E: E.tensor_tensor(out=sq[:], in0=x_[:, a_, :, :], in1=x_[:, a_, :, :], op=ALU.mult), r=[x_], w=[sq])
            kb.op('dve', lambda E: E.tensor_reduce(out=s_[:, c0:c0 + 4], in_=sq[:], axis=AX.X, op=ALU.add), r=[sq, s_], w=[s_])
            kb.op('act', lambda E: E.activation(out=s_[:, c0:c0 + 4], in_=s_[:, c0:c0 + 4], func=AF.Sqrt, bias=EPS), r=[s_], w=[s_])
            kb.op('dve', lambda E: E.reciprocal(out=s_[:, c0:c0 + 4], in_=s_[:, c0:c0 + 4]), r=[s_], w=[s_])
        sc_ = lambda o, a, bb: kb.op('dve', lambda E: E.tensor_tensor(out=s_[:, o:o + 4], in0=a, in1=bb, op=ALU.mult), r=[s_, mch], w=[s_])
        kb.op('dve', lambda E: E.tensor_scalar(out=s_[:, 32:36], in0=s_[:, 24:28], scalar1=128 ** -0.5, scalar2=None, op0=ALU.mult), r=[s_], w=[s_])
        sc_(36, s_[:, 32:36], s_[:, 16:20])
        kb.op('dve', lambda E: E.tensor_scalar(out=s_[:, 40:44], in0=s_[:, 36:40], scalar1=mch[:, 1:2], scalar2=None, op0=ALU.mult), r=[s_, mch], w=[s_])
        kb.op('dve', lambda E: E.tensor_scalar(out=s_[:, 36:40], in0=s_[:, 36:40], scalar1=mch[:, 0:1], scalar2=None, op0=ALU.mult), r=[s_, mch], w=[s_])
        sc_(44, s_[:, 28:32], s_[:, 8:12])
        sc_(44, s_[:, 44:48], s_[:, 16:20])
        sc_(48, s_[:, 28:32], s_[:, 20:24])
        kb.op('dve', lambda E: E.tensor_scalar(out=s_[:, 52:56], in0=s_[:, 48:52], scalar1=mch[:, 1:2], scalar2=None, op0=ALU.mult), r=[s_, mch], w=[s_])
        kb.op('dve', lambda E: E.tensor_scalar(out=s_[:, 48:52], in0=s_[:, 48:52], scalar1=mch[:, 0:1], scalar2=None, op0=ALU.mult), r=[s_, mch], w=[s_])
        yield
        kx, vx, qx = x_[:, 1, :, :], x_[:, 2, :, :], x_[:, 0, :, :]
        kb.op('pool', lambda E: E.tensor_tensor(out=KBG[:], in0=kx, in1=B4(s_[:, 44:48]), op=ALU.mult), r=[x_, s_], w=[KBG])
        kb.op('pool', lambda E: E.tensor_tensor(out=Kd[0][:], in0=kx, in1=B4(s_[:, 48:52]), op=ALU.mult), r=[x_, s_], w=[Kd[0]])
        kb.op('pool', lambda E: E.tensor_tensor(out=Kd[1][:], in0=kx, in1=B4(s_[:, 52:56]), op=ALU.mult), r=[x_, s_], w=[Kd[1]])
        kb.op('pool', lambda E: E.tensor_tensor(out=VB[:], in0=vx, in1=B4(s_[:, 8:12]), op=ALU.mult), r=[x_, s_], w=[VB])
        yield
        for i_, c0 in enumerate((32, 36, 40, 28)):
            kb.op('dve', lambda E: E.tensor_tensor(out=dg[i_][:], in0=M4(ident[:]), in1=B4(s_[:, c0:c0 + 4]), op=ALU.mult), r=[ident, s_], w=[dg[i_]])
        for i_, (src, dst) in enumerate(((qx, QT), (qx, QG[0]), (qx, QG[1]), (kx, KT))):
            p = PA()
            mm4(p, lambda h: src[:, h, :], lambda h: dg[i_][:, h, :], [x_, dg[i_]])
            if i_ % 2 == 0:
                kb.op('act', lambda E: E.copy(out=dst[:], in_=p[:]), r=[p], w=[dst])
            else:
                kb.op('dve', lambda E: E.tensor_copy(out=dst[:], in_=p[:]), r=[p], w=[dst])
        yield
        p = PA()
        mm4(p, lambda h: KT[:, h, :], lambda h: KT[:, h, :], [KT])
        kb.op('dve', lambda E: E.tensor_tensor(out=tmpA[:], in0=p[:], in1=dec[:], op=ALU.mult), r=[p, dec], w=[tmpA])
        kb.op('pool', lambda E: E.tensor_tensor(out=nbs[:], in0=M4(strict[:]), in1=B4(s_[:, 12:16]), op=ALU.mult), r=[strict, s_], w=[nbs])
        X, Y, P = Xb[0], Yb[0], Pb[0]
        kb.op('pool', lambda E: E.tensor_tensor(out=X[:], in0=tmpA[:], in1=nbs[:], op=ALU.mult), r=[tmpA, nbs], w=[X])
        for h in range(4):
            kb.op('pe', lambda E: E.transpose(out=pB[0][:, h, :], in_=X[:, h, :], identity=ident[:]), r=[X, ident], w=[pB[0]])
        kb.op('act', lambda E: E.copy(out=Y[:], in_=pB[0][:, 0:4, :]), r=[pB[0]], w=[Y])
        kb.op('dve', lambda E: E.tensor_tensor(out=P[:], in0=pB[0][:, 0:4, :], in1=M4(ident[:]), op=ALU.add), r=[pB[0], ident], w=[P])
        yield
        p = PA()
        mm4(p, lambda h: KT[:, h, :], lambda h: QT[:, h, :], [KT, QT])
        kb.op('dve', lambda E: E.tensor_tensor(out=aqkT[:], in0=p[:], in1=decT[:], op=ALU.mult), r=[p, decT], w=[aqkT])
        yield
        for k_ in range(1, 6):
            Xn, Yn, Pn = Xb[k_ % 2], Yb[k_ % 2], Pb[k_ % 2]
            p = PA()
            mm4(p, lambda h: Y[:, h, :], lambda h: X[:, h, :], [X, Y])
            kb.op('act', lambda E: E.copy(out=Xn[:], in_=p[:]), r=[p], w=[Xn])
            if k_ < 5:
                p2 = PA()
                mm4(p2, lambda h: X[:, h, :], lambda h: Y[:, h, :], [X, Y])
                kb.op('dve', lambda E: E.tensor_copy(out=Yn[:], in_=p2[:]), r=[p2], w=[Yn])
            p3 = PA()
            mm4(p3, lambda h: Xn[:, h, :], lambda h: P[:, h, :], [Xn, P])
            kb.op('dve', lambda E: E.tensor_tensor(out=Pn[:], in0=p3[:], in1=P[:], op=ALU.add), r=[p3, P], w=[Pn])
            X, Y, P = Xn, Yn, Pn
            yield
        yield
        p = PA()
        mm4(p, lambda h: KBG[:, h, :], lambda h: P[:, h, :], [KBG, P])
        kb.op('act', lambda E: E.mul(out=negWT[:], in_=p[:], mul=-1.0), r=[p], w=[negWT])
        yield 'B'
        Sa = Sbf[si[0] % 3]
        Sb_ = Sbf[(si[0] + 1) % 3]
        Sc = Sbf[(si[0] + 2) % 3]
        si[0] += 2
        for j, (Scur, Snext) in enumerate(((Sa, Sb_), (Sb_, Sc))):
            for h in range(4):
                kb.op('pe', lambda E: E.matmul(pV[:, h, :], lhsT=P[:, h, :], rhs=VB[:, h, :], start=True, stop=False), r=[P, VB], w=[pV])
                kb.op('pe', lambda E: E.matmul(pV[:, h, :], lhsT=negWT[:, h, :], rhs=Scur[:, h, :], start=False, stop=True),
                      r=[negWT, Scur], w=[pV])
            yield
            r0 = 64 * j
            kb.op('act', lambda E: E.copy(out=vnew[r0:r0 + 64, :, :], in_=pV[r0:r0 + 64, :, :]), r=[pV], w=[vnew])
            yield
            mm4(pdS, lambda h: Kd[j][:, h, :], lambda h: vnew[:, h, :], [Kd[j], vnew])
            yield
            for h in range(4):
                kb.op('dve', lambda E: E.scalar_tensor_tensor(out=Sf[:, h, :], in0=Sf[:, h, :], scalar=s_[:, 56 + 4 * j + h:57 + 4 * j + h],
                                                              in1=pdS[:, h, :], op0=ALU.mult, op1=ALU.add), r=[Sf, s_, pdS], w=[Sf])
            kb.op('act', lambda E: E.copy(out=Snext[:], in_=Sf[:]), r=[Sf], w=[Snext])
            yield
        for h in range(4):
            kb.op('pe', lambda E: E.matmul(pO[:, h, :], lhsT=QG[0][:, h, :], rhs=Sa[:, h, :], start=True, stop=False), r=[QG[0], Sa], w=[pO])
            kb.op('pe', lambda E: E.matmul(pO[:, h, :], lhsT=QG[1][:, h, :], rhs=Sb_[:, h, :], start=False, stop=False), r=[QG[1], Sb_], w=[pO])
            kb.op('pe', lambda E: E.matmul(pO[:, h, :], lhsT=aqkT[:, h, :], rhs=vnew[:, h, :], start=False, stop=True), r=[aqkT, vnew], w=[pO])
        yield
        kb.op('act', lambda E: E.copy(out=osb[:], in_=pO[:]), r=[pO], w=[osb])
        kb.op('pool', lambda E: E.tensor_tensor(out=sq[:], in0=osb[:], in1=osb[:], op=ALU.mult), r=[osb], w=[sq])
        kb.op('dve', lambda E: E.tensor_reduce(out=G[:, 0:4], in_=sq[:], axis=AX.X, op=ALU.add), r=[sq, G], w=[G])
        kb.op('act', lambda E: E.activation(out=G[:, 0:4], in_=G[:, 0:4], func=AF.Sqrt, scale=1.0 / 128, bias=EPS), r=[G], w=[G])
        kb.op('dve', lambda E: E.reciprocal(out=G[:, 0:4], in_=G[:, 0:4]), r=[G], w=[G])
        yield
        kb.op('act', lambda E: E.activation(out=sgt[:], in_=gt[b][:], func=AF.Silu), r=[gt[b]], w=[sgt])
        kb.op('dve', lambda E: E.tensor_tensor(out=osb[:], in0=osb[:], in1=B4(G[:, 0:4]), op=ALU.mult), r=[osb, G], w=[osb])
        kb.op('pool', lambda E: E.tensor_tensor(out=osb[:], in0=osb[:], in1=M4(gw[:]), op=ALU.mult), r=[osb, gw], w=[osb])
        kb.op('dve', lambda E: E.tensor_tensor(out=oa[b][:], in0=osb[:], in1=sgt[:].rearrange("p (h d) -> p h d", h=4), op=ALU.mult),
              r=[osb, sgt], w=[oa[b]])
        for h in range(4):
            kb.op('pe', lambda E: E.transpose(out=pB[0][:, h, :], in_=oa[b][:, h, :], identity=ident[:]), r=[oa[b], ident], w=[pB[0]])
        kb.op('act', lambda E: E.copy(out=oaT[b][:], in_=pB[0][:, 0:4, :]), r=[pB[0]], w=[oaT[b]])
        kb.dma('sp', io['ocatT'][t, :, 0:4, :], oaT[b][:], r=[oaT[b]], w=['ocatT_a'])

    def to_boundary(g):
        for r in g:
            if r == 'B':
                return

    cur = tile(0)
    to_boundary(cur)
    for t in range(NT):
        nxt = tile(t + 1) if t + 1 < NT else None
        cur_done, nxt_done = False, nxt is None
        while not (cur_done and nxt_done):
            if not cur_done:
                try:
                    next(cur)
                except StopIteration:
                    cur_done = True
            if not nxt_done:
                if next(nxt) == 'B':
                    nxt_done = True
        cur = nxt
    ph.close()


def rope16(kb, R, G, cs, tmp, e1='dve', e2='pool'):
    c = cs[:, 0:8].unsqueeze(1).to_broadcast([128, G, 8])
    sn = cs[:, 8:16].unsqueeze(1).to_broadcast([128, G, 8])
    x1 = R[:, 0:G, 0:8]
    x2 = R[:, 0:G, 8:16]
    kb.op(e1, lambda E: E.tensor_tensor(out=tmp[:, 0:G, 0:8], in0=x1, in1=c, op=ALU.mult), r=[R, cs], w=[(tmp, 0)])
    yield
    kb.op(e2, lambda E: E.tensor_tensor(out=tmp[:, 0:G, 8:16], in0=x2, in1=sn, op=ALU.mult), r=[R, cs], w=[(tmp, 1)])
    yield
    kb.op(e1, lambda E: E.tensor_tensor(out=tmp[:, 0:G, 16:24], in0=x2, in1=c, op=ALU.mult), r=[R, cs], w=[(tmp, 2)])
    yield
    kb.op(e2, lambda E: E.tensor_tensor(out=tmp[:, 0:G, 24:32], in0=x1, in1=sn, op=ALU.mult), r=[R, cs], w=[(tmp, 3)])
    yield
    kb.op(e1, lambda E: E.tensor_tensor(out=x1, in0=tmp[:, 0:G, 0:8], in1=tmp[:, 0:G, 8:16], op=ALU.subtract),
          r=[(tmp, 0), (tmp, 1), (tmp, 2), (tmp, 3)], w=[R])
    yield
    kb.op(e1, lambda E: E.tensor_tensor(out=x2, in0=tmp[:, 0:G, 16:24], in1=tmp[:, 0:G, 24:32], op=ALU.add),
          r=[(tmp, 0), (tmp, 1), (tmp, 2), (tmp, 3)], w=[R])
    yield


def rms_groups(kb, src3, G, dst3, sq, ss, wt, e_sq='pool'):
    (src_ap, src_keys) = src3
    (dst_ap, dst_keys) = dst3
    kb.op(e_sq, lambda E: E.tensor_tensor(out=sq[:, 0:G, :], in0=src_ap, in1=src_ap, op=ALU.mult), r=src_keys, w=[sq])
    yield
    kb.op('dve', lambda E: E.tensor_reduce(out=ss[:, 0:G], in_=sq[:, 0:G, :], axis=AX.X, op=ALU.add), r=[sq], w=[ss])
    yield
    kb.op('act', lambda E: E.activation(out=ss[:, 0:G], in_=ss[:, 0:G], func=AF.Sqrt, scale=1.0 / 64, bias=EPS), r=[ss], w=[ss])
    yield
    kb.op('dve', lambda E: E.reciprocal(out=ss[:, 0:G], in_=ss[:, 0:G]), r=[ss], w=[ss])
    yield
    kb.op('dve', lambda E: E.tensor_tensor(out=dst_ap, in0=src_ap, in1=ss[:, 0:G].unsqueeze(2).to_broadcast([128, G, 64]), op=ALU.mult),
          r=src_keys + [ss], w=dst_keys)
    yield
    kb.op('pool', lambda E: E.tensor_tensor(out=dst_ap, in0=dst_ap, in1=wt[:].unsqueeze(1).to_broadcast([128, G, 64]), op=ALU.mult),
          r=dst_keys + [wt], w=dst_keys)
    yield


def phase_nsa(kb, io):
    nc = kb.nc
    ph = Phase(kb, "ns")
    ident = ph.sb("ident", [128, 128], BF16)
    tril = ph.sb("tril", [128, 128], BF16)
    far = ph.sb("far", [128, 128], BF16)
    cmask = ph.sb("cmask", [128, 9, 512], BF16)
    kvT = ph.sb("kvT", [128, 4, S], BF16)
    vs1 = ph.sb("vs1", [128, NT, 2, 65], BF16)
    vw1 = ph.sb("vw1", [128, NT, 2, 65], BF16)
    kcmpT = ph.sb("kcmpT", [64, 2, 256], BF16)
    rhs_cmp = ph.sb("rhs_cmp", [128, 2, 2, 128], BF16)
    kb.dma('sp', ident[:], io['c_ident'][:, :], w=[ident])
    kb.dma('sp', tril[:], io['c_tril'][:, :], w=[tril])
    kb.dma('sp', far[:], io['c_far'][:, :], w=[far])
    for m in range(9):
        kb.dma('sp', cmask[:, m, :], io['c_cmask'][m, :, :], w=[cmask])
    for g in range(2):
        kb.dma('sp', kvT[64:128, g, :], io['c_E'][:, :], w=[(kvT, 'E')])
    kb.op('pool', lambda E: E.memset(vs1[:], 1.0), w=[vs1])
    kb.op('pool', lambda E: E.memset(vw1[:], 1.0), w=[vw1])
    kb.op('pool', lambda E: E.memset(kcmpT[:], 0.0), w=[kcmpT])
    kb.op('pool', lambda E: E.memset(rhs_cmp[:], 0.0), w=[rhs_cmp])
    for bt in range(2):
        for g in range(2):
            kb.dma('sp', rhs_cmp[:, bt, g, 64:128], io['c_ovl'][:, bt, :], r=[rhs_cmp], w=[rhs_cmp])

    pp = Phase(kb, "np")
    kcT = pp.sb("kcT", [64, 4, S], BF16)
    ksw = pp.sb("ksw", [128, 64], F32)
    kww = pp.sb("kww", [128, 64], F32)
    kcw = pp.sb("kcw", [128, 64], F32)
    kb.dma('sp', ksw[:], io['nsa_ks_norm_w'].partition_broadcast(128), w=[ksw])
    kb.dma('sp', kww[:], io['nsa_kw_norm_w'].partition_broadcast(128), w=[kww])
    kb.dma('sp', kcw[:], io['nsa_kc_norm_w'].partition_broadcast(128), w=[kcw])
    kvb = pp.sbn("kvb", [128, 768], F32, 2)
    cst = pp.sbn("cst", [128, 16], F32, 2)
    R = pp.sbn("R", [128, 6, 64], F32, 2)
    sqb = pp.sbn("sq", [128, 2, 64], F32, 2)
    ssb = pp.sbn("ss", [128, 2], F32, 2)
    tmpb = pp.sbn("tmp", [128, 6, 32], F32, 2)
    sq, ss = sqb[0], ssb[0]
    k16 = pp.sbn("k16", [128, 8, 64], BF16, 2)
    pT8 = pp.psn("pT8", [128, 8, 128], BF16, 2)
    def kvtile(t):
        b = t % 2
        kv = kvb[b]
        Rb = R[b]
        sq, ss, tmp = sqb[b], ssb[b], tmpb[b]
        kb.dma('sp', kv[:], io['tm'][t * 128:(t + 1) * 128, KC_OFF:KC_OFF + 768], r=['tm'], w=[kv])
        kb.dma('sp', cst[b][:], io['c_rope'][t * 128:(t + 1) * 128, :], w=[cst[b]])
        v3 = lambda off: kv[:, off:off + 128].rearrange("p (g d) -> p g d", g=2)
        kb.op('pool', lambda E: E.tensor_copy(out=Rb[:, 0:2, :], in_=v3(0)), r=[kv], w=[Rb])
        yield
        yield from rms_groups(kb, (v3(256), [kv]), 2, (Rb[:, 2:4, :], [Rb]), sq, ss, ksw)
        yield from rms_groups(kb, (v3(512), [kv]), 2, (Rb[:, 4:6, :], [Rb]), sq, ss, kww)
        yield from rope16(kb, Rb, 6, cst[b], tmp)
        kk = k16[b]
        kb.op('act', lambda E: E.copy(out=kk[:, 0:6, :], in_=Rb[:]), r=[Rb], w=[kk])
        kb.op('pool', lambda E: E.tensor_copy(out=kk[:, 6:8, :], in_=v3(128)), r=[kv], w=[kk])
        kb.op('dve', lambda E: E.tensor_copy(out=vs1[:, t, :, 0:64], in_=v3(384)), r=[kv], w=[vs1])
        kb.op('pool', lambda E: E.tensor_copy(out=vw1[:, t, :, 0:64], in_=v3(640)), r=[kv], w=[vw1])
        yield
        p8 = pT8[b]
        for i in range(8):
            kb.op('pe', lambda E: E.transpose(out=p8[0:64, i, :], in_=kk[:, i, :], identity=ident[:]), r=[kk, ident], w=[p8])
        kb.op('act', lambda E: E.copy(out=kvT[0:64, :, t * 128:(t + 1) * 128], in_=p8[0:64, 2:6, :]), r=[p8], w=[(kvT, 'k')])
        kb.op('dve', lambda E: E.tensor_copy(out=kcT[0:64, 0:2, t * 128:(t + 1) * 128], in_=p8[0:64, 0:2, :]), r=[p8], w=[kcT])
        kb.op('dve', lambda E: E.tensor_copy(out=kcT[0:64, 2:4, t * 128:(t + 1) * 128], in_=p8[0:64, 6:8, :]), r=[p8], w=[kcT])

    def interleave(gens):
        gens = list(gens)
        while gens:
            for g_ in list(gens):
                try:
                    next(g_)
                except StopIteration:
                    gens.remove(g_)

    for t in range(0, NT, 2):
        interleave([kvtile(t), kvtile(t + 1)])
    w1f = pp.sb("w1f", [64, 32, 64], F32)
    w1b = pp.sbn("w1b", [64, 32, 64], BF16, 2)
    w2f = pp.sb("w2f", [64, 64], F32)
    w2b = pp.sbn("w2b", [64, 64], BF16, 2)
    posf = pp.sb("posf", [64, 32], F32)
    pos2 = pp.sbn("pos2", [64, 32, 2], BF16, 2)
    bias = pp.sb("bias", [64, 2], F32)
    h1T = pp.sb("h1T", [64, 256], BF16)
    o2 = pp.sb("o2", [128, 1, 64], F32)
    o2n = pp.sb("o2n", [128, 1, 64], F32)
    kcn = pp.sb("kcn", [128, 64], BF16)
    pH = pp.ps("pH", [128, 512], F32)
    pB_ = pp.ps("pBi", [128, 512], F32)
    pO2 = pp.ps("pO2", [128, 512], F32)
    kb.op('pool', lambda E: E.memset(h1T[:], 0.0), w=[h1T])
    for kind, (n1, n2, npos) in enumerate((('nsa_cmp_k_w1', 'nsa_cmp_k_w2', 'nsa_cmp_pos_k'), ('nsa_cmp_v_w1', 'nsa_cmp_v_w2', 'nsa_cmp_pos_v'))):
        kb.dma('sp', w1f[:], io[n1].rearrange("(l d) o -> d l o", d=64), w=[w1f])
        kb.op('dve', lambda E: E.tensor_copy(out=w1b[kind][:], in_=w1f[:]), r=[w1f], w=[w1b[kind]])
        kb.dma('sp', w2f[:], io[n2][:, :], w=[w2f])
        kb.op('dve', lambda E: E.tensor_copy(out=w2b[kind][:], in_=w2f[:]), r=[w2f], w=[w2b[kind]])
        kb.dma('sp', posf[:], io[npos].rearrange("l d -> d l"), w=[posf], allow_slow_non_contiguous=True)
        for j in range(2):
            kb.op('dve', lambda E: E.tensor_copy(out=pos2[kind][:, :, j], in_=posf[:]), r=[posf], w=[pos2[kind]])
        for l in range(32):
            kb.op('pe', lambda E: E.matmul(pB_[0:64, 0:2], lhsT=w1b[kind][:, l, :], rhs=pos2[kind][:, l, :], start=(l == 0), stop=(l == 31)),
                  r=[w1b[kind], pos2[kind]], w=[pB_])
        kb.op('dve', lambda E: E.tensor_copy(out=bias[:], in_=pB_[0:64, 0:2]), r=[pB_], w=[bias])
        for g in range(2):
            ki = kind * 2 + g
            for l in range(32):
                kb.op('pe', lambda E: E.matmul(pH[0:64, 0:255], lhsT=w1b[kind][:, l, :], rhs=kcT[0:64, ki, l:l + 16 * 254 + 1:16],
                                               start=(l == 0), stop=(l == 31)), r=[w1b[kind], kcT], w=[pH])
            kb.op('act', lambda E: E.activation(out=h1T[:, 0:255], in_=pH[0:64, 0:255], func=AF.Silu, bias=bias[:, 0:1]),
                  r=[pH, bias], w=[h1T])
            for bt in range(2):
                kb.op('pe', lambda E: E.matmul(pO2[:, 0:64], lhsT=h1T[:, bt * 128:(bt + 1) * 128], rhs=w2b[kind][:], start=True, stop=True),
                      r=[h1T, w2b[kind]], w=[pO2])
                if kind == 0:
                    kb.op('act', lambda E: E.copy(out=o2[:, 0, :], in_=pO2[:, 0:64]), r=[pO2], w=[o2])
                    for _ in rms_groups(kb, (o2[:], [o2]), 1, (o2n[:], [o2n]), sq, ss, kcw):
                        pass
                    kb.op('act', lambda E: E.copy(out=kcn[:], in_=o2n[:, 0, :]), r=[o2n], w=[kcn])
                    p8 = pT8[0]
                    kb.op('pe', lambda E: E.transpose(out=p8[0:64, 0, :], in_=kcn[:], identity=ident[:]), r=[kcn, ident], w=[p8])
                    kb.op('act', lambda E: E.copy(out=kcmpT[:, g, bt * 128:(bt + 1) * 128], in_=p8[0:64, 0, :]), r=[p8], w=[kcmpT])
                else:
                    kb.op('act', lambda E: E.copy(out=rhs_cmp[:, bt, g, 0:64], in_=pO2[:, 0:64]), r=[pO2], w=[rhs_cmp])
    pp.close()

    pa = Phase(kb, "na")
    NPT = 8
    PTb = pa.sbn("PT", [128, 512], BF16, NPT)
    pti = [0]
    qaug = pa.sbn("qaug", [128, 8, 512], BF16, 2)
    qnw = pa.sb("qnw", [128, 64], F32)
    kb.dma('sp', qnw[:], io['nsa_q_norm_w'].partition_broadcast(128), w=[qnw])
    kb.op('dve', lambda E: E.tensor_scalar(out=qnw[:], in0=qnw[:], scalar1=0.125, scalar2=None, op0=ALU.mult), r=[qnw], w=[qnw])
    qf = pa.sbn("qf", [128, 512], F32, 4)
    cst = pa.sbn("cst", [128, 16], F32, 4)
    Rqb = pa.sbn("Rq", [128, 8, 64], F32, 4)
    sqb = pa.sbn("sq", [128, 8, 64], F32, 4)
    ssb = pa.sbn("ss", [128, 8], F32, 4)
    tmpb = pa.sbn("tmp", [128, 8, 32], F32, 4)
    qa = pa.sbn("qa", [128, 8, 128], BF16, 4)
    gts = pa.sb("gts", [128, 4, 24], F32)
    Ab = pa.sbn("Ab", [128, 64], F32, 4)
    Bb = pa.sbn("Bb", [128, 64], F32, 4)
    ob = pa.sb("ob", [128, 4, 8, 64], F32)
    impacc = pa.sb("impacc", [128, 4, 64], F32)
    tmpi = pa.sb("tmpi", [128, 4, 64], F32)
    tmpo = pa.sb("tmpo", [128, 4, 64], F32)
    scr = pa.sb("scr", [128, 64], F32)
    scr2 = pa.sb("scr2", [128, 64], F32)
    m8 = pa.sb("m8", [128, 16], F32)
    nst = pa.sbn("nst", [128, 128], BF16, 2)
    fs = pa.sb("fs", [128, 12], F32)
    obb = pa.sbn("obb", [128, 512], BF16, 2)
    obT = pa.sbn("obT", [128, 4, 128], BF16, 2)
    sc_ps = pa.psn("sc", [128, 512], F32, 3)
    oC = pa.ps("oC", [128, 4, 128], F32)
    oW = pa.psn("oW", [128, 4, 128], F32, 1)
    oS = pa.psn("oS", [128, 4, 128], F32, 2)
    pTr = pa.psn("pTr", [128, 8, 128], BF16, 1)
    for i in range(4):
        kb.op('pool', lambda E: E.memset(qa[i][:], 0.0), w=[qa[i]])
    for i in range(2):
        kb.op('pool', lambda E: E.memset(nst[i][:], 0.0), w=[nst[i]])
    tri = 0

    def finalize(oX, h, br, first, cmp=False):
        if cmp:
            kb.op('dve', lambda E: E.tensor_reduce(out=fs[:, 0:4], in_=oX[:, :, 64:128], axis=AX.X, op=ALU.add), r=[oX], w=[fs])
            kb.op('dve', lambda E: E.tensor_scalar(out=fs[:, 0:4], in0=fs[:, 0:4], scalar1=0.5, scalar2=1e-30, op0=ALU.mult, op1=ALU.add),
                  r=[fs], w=[fs])
        else:
            kb.op('dve', lambda E: E.tensor_scalar(out=fs[:, 0:4], in0=oX[:, :, 64], scalar1=1e-30, scalar2=None, op0=ALU.add), r=[oX], w=[fs])
        kb.op('dve', lambda E: E.reciprocal(out=fs[:, 4:8], in_=fs[:, 0:4]), r=[fs], w=[fs])
        if cmp:
            dsti = impacc if h % 4 == 0 else tmpi
            kb.op('dve', lambda E: E.tensor_tensor(out=dsti[:], in0=oX[:, :, 64:128], in1=fs[:, 4:8].unsqueeze(2).to_broadcast([128, 4, 64]),
                                                   op=ALU.mult), r=[oX, fs], w=[dsti])
            if h % 4 != 0:
                kb.op('pool', lambda E: E.tensor_tensor(out=impacc[:], in0=impacc[:], in1=tmpi[:], op=ALU.add), r=[impacc, tmpi], w=[impacc])
        kb.op('dve', lambda E: E.tensor_tensor(out=fs[:, 8:12], in0=fs[:, 4:8], in1=gts[:, :, h * 3 + br], op=ALU.mult), r=[fs, gts], w=[fs])
        dst = ob[:, :, h, :] if first else tmpo[:]
        kb.op('dve', lambda E: E.tensor_tensor(out=dst, in0=oX[:, :, 0:64], in1=fs[:, 8:12].unsqueeze(2).to_broadcast([128, 4, 64]),
                                               op=ALU.mult), r=[oX, fs], w=[(ob, h) if first else tmpo])
        if not first:
            kb.op('pool', lambda E: E.tensor_tensor(out=ob[:, :, h, :], in0=ob[:, :, h, :], in1=tmpo[:], op=ALU.add),
                  r=[(ob, h), tmpo], w=[(ob, h)])

    for s in range(S // 512):
        qs_ = qaug[s % 2]
        def qtile(t4, s=s, qs_=qs_):
            t = 4 * s + t4
            b = t4
            q = qf[b]
            Rq, sq, ss, tmp = Rqb[b], sqb[b], ssb[b], tmpb[b]
            kb.dma('sp', q[:], io['tm'][t * 128:(t + 1) * 128, NQ_OFF:NQ_OFF + 512], r=['tm'], w=[q])
            kb.dma('sp', gts[:, t4, :], io['tm'][t * 128:(t + 1) * 128, NG_OFF:NG_OFF + 24], r=['tm'], w=[gts])
            kb.dma('sp', cst[b][:], io['c_rope'][t * 128:(t + 1) * 128, :], w=[cst[b]])
            kb.dma('sp', Ab[t4][:], io['c_A'][t, :, :], w=[Ab[t4]])
            kb.dma('sp', Bb[t4][:], io['c_B'][t, :, :], w=[Bb[t4]])
            kb.op('act', lambda E: E.activation(out=gts[:, t4, :], in_=gts[:, t4, :], func=AF.Exp, scale=-1.0), r=[gts], w=[gts])
            kb.op('dve', lambda E: E.tensor_scalar(out=gts[:, t4, :], in0=gts[:, t4, :], scalar1=1.0, scalar2=None, op0=ALU.add), r=[gts], w=[gts])
            kb.op('dve', lambda E: E.reciprocal(out=gts[:, t4, :], in_=gts[:, t4, :]), r=[gts], w=[gts])
            q3 = q[:].rearrange("p (h d) -> p h d", h=8)
            yield
            yield from rms_groups(kb, (q3, [q]), 8, (Rq[:], [Rq]), sq, ss, qnw)
            yield from rope16(kb, Rq, 8, cst[b], tmp)
            qab = qa[b]
            kb.op('act', lambda E: E.copy(out=qab[:, :, 0:64], in_=Rq[:]), r=[Rq], w=[qab])
            yield
            pt_ = pTr[0]
            for h in range(8):
                kb.op('pe', lambda E: E.transpose(out=pt_[:, h, :], in_=qab[:, h, :], identity=ident[:]), r=[qab, ident], w=[pt_])
            kb.op('act', lambda E: E.copy(out=qs_[:, :, t4 * 128:(t4 + 1) * 128], in_=pt_[:]), r=[pt_], w=[qs_])
        interleave([qtile(i) for i in range(4)])
        nbt = 1 if s < 4 else 2
        for g in range(2):
            for h in range(4 * g, 4 * g + 4):
                specs = []
                for bt in range(nbt):
                    m = (s if s <= 4 else None) if bt == 0 else 5 + (s - 4)
                    masks = [] if m is None else [(0, 512, cmask[:, m, :])]
                    specs.append(dict(lhsT=kcmpT[0:64, g, bt * 128:(bt + 1) * 128],
                                      rhs_fn=lambda q0, q1, h=h: qs_[0:64, h, q0 * 128:q1 * 128], qt0=0, qt1=4, masks=masks, mk=[cmask],
                                      v_fn=lambda qt, bt=bt, g=g: rhs_cmp[:, bt, g, :], nk=128, rk=[kcmpT], rv=[rhs_cmp]))
                attn_block(kb, lambda qt: (oC[:, qt, :], oC, 'C'), PTb, pti, sc_ps, specs, ident, [qs_])
                finalize(oC, h, 0, True, cmp=True)
            for qt in range(4):
                ns_ = nst[qt % 2]
                kb.op('dve', lambda E: E.tensor_tensor(out=scr[:], in0=impacc[:, qt, :], in1=Ab[qt][:], op=ALU.mult), r=[impacc, Ab[qt]], w=[scr])
                kb.op('dve', lambda E: E.tensor_tensor(out=scr[:], in0=scr[:], in1=Bb[qt][:], op=ALU.add), r=[scr, Bb[qt]], w=[scr])
                kb.op('dve', lambda E: E.max(out=m8[:, 0:8], in_=scr[:]), r=[scr], w=[(m8, 0)])
                kb.op('dve', lambda E: E.match_replace(out=scr2[:], in_to_replace=m8[:, 0:8], in_values=scr[:], imm_value=-1e30),
                      r=[scr, (m8, 0)], w=[scr2])
                kb.op('dve', lambda E: E.max(out=m8[:, 8:16], in_=scr2[:]), r=[scr2], w=[(m8, 1)])
                kb.op('dve', lambda E: E.tensor_scalar(out=ns_[:, 64:128], in0=scr[:], scalar1=m8[:, 15:16], scalar2=1.0, op0=ALU.is_ge,
                                                       op1=ALU.subtract), r=[scr, (m8, 1)], w=[ns_])
                pt_ = pTr[0]
                tri += 1
                kb.op('pe', lambda E: E.transpose(out=pt_[:, 0, :], in_=ns_[:], identity=ident[:]), r=[ns_, ident], w=[pt_])
                for h in range(4 * g, 4 * g + 4):
                    if h % 2 == 0:
                        kb.op('act', lambda E: E.copy(out=qs_[64:128, h, qt * 128:(qt + 1) * 128], in_=pt_[64:128, 0, :]), r=[pt_], w=[qs_])
                    else:
                        kb.op('dve', lambda E: E.tensor_copy(out=qs_[64:128, h, qt * 128:(qt + 1) * 128], in_=pt_[64:128, 0, :]), r=[pt_], w=[qs_])
        for h in range(8):
            g = h // 4
            specs = []
            for kt in range(max(0, 4 * s - 4), 4 * s + 4):
                lo = max(kt - 4 * s, 0)
                hi = min(kt + 4 - 4 * s, 3)
                masks = []
                if kt >= 4 * s:
                    masks.append(((kt - 4 * s - lo) * 128, 128, tril[:]))
                if kt + 4 <= 4 * s + 3:
                    masks.append(((kt + 4 - 4 * s - lo) * 128, 128, far[:]))
                specs.append(dict(lhsT=kvT[0:64, 2 + g, kt * 128:(kt + 1) * 128],
                                  rhs_fn=lambda q0, q1, h=h: qs_[0:64, h, q0 * 128:q1 * 128], qt0=lo, qt1=hi + 1, masks=masks, mk=[tril, far],
                                  v_fn=lambda qt, kt=kt, g=g: vw1[:, kt, g, :], nk=128, rk=[(kvT, 'k')], rv=[vw1]))
            oW_ = oW[0]
            attn_block(kb, lambda qt: (oW_[:, qt, 0:65], oW_, 'W'), PTb, pti, sc_ps, specs, ident, [qs_], LA=3)
            finalize(oW_, h, 2, False)
            specs = []
            for kt in range(0, 4 * s + 4):
                lo = max(kt - 4 * s, 0)
                masks = [(0, 128, tril[:])] if kt >= 4 * s else []
                specs.append(dict(lhsT=kvT[:, g, kt * 128:(kt + 1) * 128],
                                  rhs_fn=lambda q0, q1, h=h: qs_[:, h, q0 * 128:q1 * 128], qt0=lo, qt1=4, masks=masks, mk=[tril],
                                  v_fn=lambda qt, kt=kt, g=g: vs1[:, kt, g, :], nk=128, rk=[(kvT, 'k'), (kvT, 'E')], rv=[vs1]))
            oS_ = oS[h % 2]
            attn_block(kb, lambda qt: (oS_[:, qt, 0:65], oS_, 'S'), PTb, pti, sc_ps, specs, ident, [qs_], LA=3)
            finalize(oS_, h, 1, False)
        for qt in range(4):
            t = 4 * s + qt
            b = t % 2
            kb.op('act', lambda E: E.copy(out=obb[b][:], in_=ob[:, qt, :, :].rearrange("p h d -> p (h d)")), r=[(ob, h) for h in range(8)], w=[obb[b]])
            pt_ = pTr[0]
            tri += 1
            for c in range(4):
                kb.op('pe', lambda E: E.transpose(out=pt_[:, c, :], in_=obb[b][:, c * 128:(c + 1) * 128], identity=ident[:]), r=[obb[b], ident], w=[pt_])
            kb.op('act', lambda E: E.copy(out=obT[b][:], in_=pt_[:, 0:4, :]), r=[pt_], w=[obT[b]])
            kb.dma('sp', io['ocatT'][t, :, 4:8, :], obT[b][:], r=[obT[b]], w=['ocatT_b'])
    pa.close()
    ph.close()


def phase_ffn(kb, io):
    nc = kb.nc
    ph = Phase(kb, "f1")
    ident = ph.sb("ident", [128, 128], BF16)
    kb.dma('sp', ident[:], io['c_ident'][:, :], w=[ident])
    woutb = ph.sb("woutb", [128, 12, D], BF16)
    wupb = ph.sb("wupb", [128, 8, 2 * FF], BF16)
    gam = ph.sb("gam", [128, 8], F32)
    cw = ph.sb("cw", [128, 3, 44], F32)
    kb.dma('sp', gam[:], io['ffn_norm_w'].rearrange("(c p) -> p c", p=128), w=[gam], allow_slow_non_contiguous=True)
    for j in range(3):
        kb.dma('sp', cw[:, j, :], io['ffn_conv_w'][j, :].rearrange("(c p) -> p c", p=128), w=[cw], allow_slow_non_contiguous=True)
    load_cast_weight(kb, ph, woutb, io['w_out'], 12, D)
    load_cast_weight(kb, ph, wupb, io['ffn_w_up'], 8, 2 * FF, gam=gam, stage_cols=1408)

    oT = ph.sbn("oT", [128, 12, 128], BF16, 2)
    xt = ph.sbn("xt", [128, D], F32, 2)
    hs = ph.sbn("hs", [128, D], F32, 2)
    junk = ph.sb("junk", [128, D], BF16)
    ss = ph.sbn("ss", [128, 1], F32, 2)
    rs = ph.sbn("rs", [128, 1], F32, 2)
    hn = ph.sbn("hn", [128, D], BF16, 2)
    hnT = ph.sbn("hnT", [128, 8, 512], BF16, 2)
    ug = ph.sbn("ug", [128, 514], F32, 2)
    uv = ph.sbn("uv", [128, 514], F32, 2)
    ag = ph.sbn("ag", [128, 512], F32, 2)
    av = ph.sbn("av", [128, 512], F32, 2)
    sg = ph.sbn("sg", [128, 512], F32, 2)
    act = ph.sbn("act", [128, 512], BF16, 3)
    halo = ph.sb("halo", [128, 44, 2], F32)
    pH = ph.psn("pH", [128, 512], F32, 2)
    pT = ph.ps("pT", [128, 8, 128], BF16)
    pU = ph.psn("pU", [128, 512], F32, 4)
    kb.op('pool', lambda E: E.memset(halo[:], 0.0), w=[halo])

    def conv3(ps_, dst, src, fb):
        kb.op('act', lambda E: E.activation(out=dst[:], in_=ps_[:], func=AF.Copy, scale=cw[:, 2, fb:fb + 1]), r=[ps_, cw], w=[dst])
        for j in range(2):
            kb.op('dve', lambda E: E.scalar_tensor_tensor(out=dst[:], in0=src[:, j:j + 512], scalar=cw[:, j, fb:fb + 1], in1=dst[:],
                                                       op0=ALU.mult, op1=ALU.add), r=[(src, 'b'), (src, 'h'), cw, dst], w=[dst])

    def ftiles(s):
        hT = hnT[s % 2]
        for t4 in range(4):
            t = s * 4 + t4
            b = t % 2
            kb.dma('sp', oT[b][:], io['ocatT'][t, :, :, :],
                   r=['ocatT_a', 'ocatT_b', 'ocatT_c'], w=[oT[b]])
            kb.dma('sp', xt[b][:], io['x'][t * 128:(t + 1) * 128, :], w=[xt[b]])
            yield
            for half in range(2):
                p = pH[half]
                for c in range(12):
                    kb.op('pe', lambda E: E.matmul(p[:], lhsT=oT[b][:, c, :], rhs=woutb[:, c, half * 512:(half + 1) * 512],
                                                   start=(c == 0), stop=(c == 11)), r=[oT[b], (woutb, c)], w=[p])
                kb.op('dve', lambda E: E.tensor_tensor(out=hs[b][:, half * 512:(half + 1) * 512], in0=p[:],
                                                       in1=xt[b][:, half * 512:(half + 1) * 512], op=ALU.add),
                      r=[p, xt[b]], w=[(hs[b], half)])
            yield
            kb.dma('sp', io['h_s'][t * 128:(t + 1) * 128, :], hs[b][:], r=[(hs[b], 0), (hs[b], 1)], w=['h_s'])
            rms_rstd(kb, hs[b][:], junk, ss[b], rs[b], D, [(hs[b], 0), (hs[b], 1)])
            kb.op('act', lambda E: E.activation(out=hn[b][:], in_=hs[b][:], func=AF.Copy, scale=rs[b][:, 0:1]),
                  r=[(hs[b], 0), (hs[b], 1), rs[b]], w=[hn[b]])
            yield
            yield
            for c in range(8):
                kb.op('pe', lambda E: E.transpose(out=pT[:, c, :], in_=hn[b][:, c * 128:(c + 1) * 128], identity=ident[:]),
                      r=[hn[b], ident], w=[pT])
            kb.op('act', lambda E: E.copy(out=hT[:, :, t4 * 128:(t4 + 1) * 128], in_=pT[:]), r=[pT], w=[hT])
            yield

    def fup(s):
        hT = hnT[s % 2]
        for fb in range(22):
            k2 = fb % 2
            pg = pU[2 * k2]
            pv = pU[2 * k2 + 1]
            for (p_, f0) in ((pg, fb), (pv, 22 + fb)):
                for c in range(8):
                    kb.op('pe', lambda E: E.matmul(p_[:], lhsT=wupb[:, c, f0 * 128:(f0 + 1) * 128], rhs=hT[:, c, :],
                                                   start=(c == 0), stop=(c == 7)), r=[hT, (wupb, c)], w=[p_])
            yield
            g_, v_ = ug[k2], uv[k2]
            kb.op('act', lambda E: E.copy(out=g_[:, 2:514], in_=pg[:]), r=[pg], w=[(g_, 'b')])
            kb.op('act', lambda E: E.copy(out=v_[:, 2:514], in_=pv[:]), r=[pv], w=[(v_, 'b')])
            for (u_, f0) in ((g_, fb), (v_, 22 + fb)):
                kb.op('pool', lambda E: E.tensor_copy(out=u_[:, 0:2], in_=halo[:, f0, :]), r=[(halo, f0)], w=[(u_, 'h')])
                kb.op('pool', lambda E: E.tensor_copy(out=halo[:, f0, :], in_=u_[:, 512:514]), r=[(u_, 'b')], w=[(halo, f0)])
            conv3(pg, ag[k2], g_, fb)
            conv3(pv, av[k2], v_, 22 + fb)
            kb.op('act', lambda E: E.activation(out=sg[k2][:], in_=ag[k2][:], func=AF.Silu), r=[ag[k2]], w=[sg[k2]])
            a_ = act[fb % 3]
            kb.op('dve', lambda E: E.tensor_tensor(out=a_[:], in0=sg[k2][:], in1=av[k2][:], op=ALU.mult), r=[sg[k2], av[k2]], w=[a_])
            kb.dma('sp', io['actT'][4 * s:4 * s + 4, :, fb, :].rearrange("j p t -> p j t"), a_[:].rearrange("p (j t) -> p j t", j=4), r=[a_], w=['actT'])
            yield

    def interleave(gens):
        gens = list(gens)
        while gens:
            for g_ in list(gens):
                try:
                    next(g_)
                except StopIteration:
                    gens.remove(g_)

    NS = S // 512
    interleave([ftiles(0)])
    for s in range(NS):
        gl = [fup(s)]
        if s + 1 < NS:
            gl.append(ftiles(s + 1))
        interleave(gl)
    ph.close()

    ph = Phase(kb, "f2")
    wdnb = ph.sb("wdnb", [128, 22, D], BF16)
    load_cast_weight(kb, ph, wdnb, io['ffn_w_down'], 22, D)
    aT = ph.sbn("aT", [128, 22, 128], BF16, 2)
    hs = ph.sbn("hs", [128, D], F32, 2)
    ot = ph.sbn("ot", [128, D], F32, 2)
    pD = ph.psn("pD", [128, 512], F32, 4)
    for t in range(NT):
        b = t % 2
        kb.dma('sp', aT[b][:], io['actT'][t, :, :, :], r=['actT'], w=[aT[b]])
        kb.dma('sp', hs[b][:], io['h_s'][t * 128:(t + 1) * 128, :], r=['h_s'], w=[hs[b]])
        for half in range(2):
            p = pD[2 * b + half]
            for c in range(22):
                kb.op('pe', lambda E: E.matmul(p[:], lhsT=aT[b][:, c, :], rhs=wdnb[:, c, half * 512:(half + 1) * 512],
                                               start=(c == 0), stop=(c == 21)), r=[aT[b], (wdnb, c)], w=[p])
            kb.op('dve', lambda E: E.tensor_tensor(out=ot[b][:, half * 512:(half + 1) * 512], in0=p[:],
                                                   in1=hs[b][:, half * 512:(half + 1) * 512], op=ALU.add),
                  r=[p, hs[b]], w=[(ot[b], half)])
        kb.dma('sp', io['out'][t * 128:(t + 1) * 128, :], ot[b][:], r=[(ot[b], 0), (ot[b], 1)], w=['out'])
    ph.close()


W_NAMES = ['attn_norm_w', 'mem_norm_w', 'w_in', 'gdn_conv_w', 'gdn_a_log', 'gdn_dt_bias', 'gdn_out_norm_w',
           'nsa_q_norm_w', 'nsa_kc_norm_w', 'nsa_ks_norm_w', 'nsa_kw_norm_w', 'nsa_cmp_pos_k', 'nsa_cmp_pos_v',
           'nsa_cmp_k_w1', 'nsa_cmp_k_w2', 'nsa_cmp_v_w1', 'nsa_cmp_v_w2', 'mem_w_kv', 'mem_q_norm_w', 'mem_k_norm_w',
           'w_out', 'ffn_norm_w', 'ffn_w_up', 'ffn_conv_w', 'ffn_w_down']
W_SHAPES = {
    'attn_norm_w': [D], 'mem_norm_w': [D], 'w_in': [D, INW], 'gdn_conv_w': [4, 1536], 'gdn_a_log': [4], 'gdn_dt_bias': [4],
    'gdn_out_norm_w': [128], 'nsa_q_norm_w': [64], 'nsa_kc_norm_w': [64], 'nsa_ks_norm_w': [64], 'nsa_kw_norm_w': [64],
    'nsa_cmp_pos_k': [32, 64], 'nsa_cmp_pos_v': [32, 64], 'nsa_cmp_k_w1': [2048, 64], 'nsa_cmp_k_w2': [64, 64],
    'nsa_cmp_v_w1': [2048, 64], 'nsa_cmp_v_w2': [64, 64], 'mem_w_kv': [D, 1024], 'mem_q_norm_w': [128], 'mem_k_norm_w': [128],
    'w_out': [1536, D], 'ffn_norm_w': [D], 'ffn_w_up': [D, 2 * FF], 'ffn_conv_w': [3, 2 * FF], 'ffn_w_down': [FF, D],
}


def make_consts():
    c = {}
    c['c_ident'] = np.eye(128, dtype=np.float32).astype(ml_dtypes.bfloat16)
    idx = np.arange(128)
    same = (idx[:, None] // 64) == (idx[None, :] // 64)
    c['c_btri'] = (same & (idx[:, None] <= idx[None, :])).astype(np.float32)
    c['c_bones'] = same.astype(np.float32)
    c['c_strict'] = (same & (idx[:, None] > idx[None, :])).astype(np.float32)
    c['c_mlow'] = np.where(same & (idx[:, None] >= idx[None, :]), 0.0, NEG).astype(np.float32)
    c['c_mup'] = np.ascontiguousarray(c['c_mlow'].T)
    c['c_mch'] = np.stack([(idx < 64), (idx >= 64)], axis=1).astype(np.float32)
    bf = ml_dtypes.bfloat16
    c['c_tril'] = np.where(idx[:, None] <= idx[None, :], 0.0, NEG).astype(np.float32).astype(bf)
    c['c_far'] = np.where(idx[None, :] < idx[:, None], 0.0, NEG).astype(np.float32).astype(bf)
    cm = np.zeros((9, 128, 512), np.float32)
    f = np.arange(512)
    for m in range(9):
        bt, s_ = (0, m) if m < 5 else (1, m - 1)
        blk = 128 * bt + idx
        vis = (16 * blk[:, None] + 31 <= 512 * s_ + f[None, :]) & (blk[:, None] < 255)
        cm[m] = np.where(vis, 0.0, NEG)
    c['c_cmask'] = cm.astype(bf)
    kk = np.arange(S)
    c['c_E'] = np.where((kk[None, :] // 64) == np.arange(64)[:, None], -NEG, 0.0).astype(np.float32).astype(bf)
    ci = np.arange(256) * 16
    sj = np.arange(64) * 64
    ovl = np.clip(np.minimum(ci[:, None] + 32, sj[None, :] + 64) - np.maximum(ci[:, None], sj[None, :]), 0, None) / 16.0
    ovl[255] = 0.0
    c['c_ovl'] = np.ascontiguousarray(ovl.reshape(2, 128, 64).transpose(1, 0, 2)).astype(np.float32).astype(bf)
    pos = np.arange(S, dtype=np.float32)
    inv = (1.0 / (np.float32(500000.0) ** (np.arange(0, 16, 2, dtype=np.float32) / np.float32(16)))).astype(np.float32)
    ang = pos[:, None] * inv[None, :]
    c['c_rope'] = np.concatenate([np.cos(ang), np.sin(ang)], axis=1).astype(np.float32)
    tt = np.arange(S)
    cur = tt // 64
    blk = np.arange(64)
    valid = blk[None, :] <= cur[:, None]
    forced = (blk[None, :] == 0) | (blk[None, :] == cur[:, None]) | (blk[None, :] == cur[:, None] - 1)
    c['c_A'] = (valid & ~forced).astype(np.float32).reshape(NT, 128, 64)
    c['c_B'] = np.where(valid, np.where(forced, 1e6, 0.0), -1e9).astype(np.float32).reshape(NT, 128, 64)
    return c


def build_program(dbg=False, phases=('ip', 'mem', 'gdn', 'nsa', 'ffn'), dbg_ocat=False):
    nc = bass.Bass("TRN2", target_bir_lowering=False)
    io = {}
    io['x'] = nc.dram_tensor("x", [S, D], F32, kind="ExternalInput").ap()
    io['mem'] = nc.dram_tensor("mem", [256, D], F32, kind="ExternalInput").ap()
    for n in W_NAMES:
        io[n] = nc.dram_tensor(n, W_SHAPES[n], F32, kind="ExternalInput").ap()
    for n, v in make_consts().items():
        io[n] = nc.dram_tensor(n, list(v.shape), BF16 if v.dtype == ml_dtypes.bfloat16 else F32, kind="ExternalInput").ap()
    io['out'] = nc.dram_tensor("out", [S, D], F32, kind="ExternalOutput").ap()
    sk = "ExternalOutput" if dbg else "Internal"
    io['tm'] = nc.dram_tensor("tm", [S, TMW], F32, kind=sk).ap()
    io['qkv_tm'] = nc.dram_tensor("qkv_tm", [S, 1536], BF16, kind=sk).ap()
    if dbg_ocat:
        io['ocatT'] = nc.dram_tensor("ocatT", [NT, 128, 12, 128], BF16, kind="ExternalInput").ap()
    else:
        io['ocatT'] = nc.dram_tensor("ocatT", [NT, 128, 12, 128], BF16, kind=sk).ap()
    io['h_s'] = nc.dram_tensor("h_s", [S, D], F32, kind=sk).ap()
    io['actT'] = nc.dram_tensor("actT", [NT, 128, 22, 128], BF16, kind="Internal").ap()
    kb = KB(nc)
    if 'ip' in phases:
        phase_inproj(kb, io)
    if 'mem' in phases:
        phase_mem(kb, io)
    if 'gdn' in phases:
        phase_gdn(kb, io)
    if 'nsa' in phases:
        phase_nsa(kb, io)
    if 'ffn' in phases:
        phase_ffn(kb, io)
    kb.finish()
    return nc, kb


def make_in_maps(inputs):
    consts = make_consts()
    maps = []
    for b in range(8):
        m = {'x': np.ascontiguousarray(inputs['x'][b]), 'mem': np.ascontiguousarray(inputs['mem'][b])}
        for n in W_NAMES:
            m[n] = np.ascontiguousarray(np.asarray(inputs[n])[0])
        m.update(consts)
        maps.append(m)
    return maps


def kernel(**inputs):
    nc, kb = build_program()
    maps = make_in_maps(inputs)
    res = run_bass_kernel_spmd(nc, maps, core_ids=list(range(8)))
    return np.stack([np.asarray(r['out'], dtype=np.float32) for r in res.results], axis=0)
```

```python
import os
import numpy as np
from contextlib import ExitStack
import concourse.bass as bass
import concourse.mybir as mybir
from concourse.bass_utils import run_bass_kernel_spmd
import ml_dtypes

F32 = mybir.dt.float32
BF16 = mybir.dt.bfloat16
AF = mybir.ActivationFunctionType
ALU = mybir.AluOpType
AX = mybir.AxisListType

S = 4096
D = 1024
NT = S // 128
INW = 3872
TMW = 2336
A_OFF, B_OFF, GATE_OFF, NQ_OFF = 0, 4, 8, 520
KC_OFF, VC_OFF, KS_OFF, VS_OFF, KW_OFF, VW_OFF = 1032, 1160, 1288, 1416, 1544, 1672
NG_OFF, MQ_OFF = 1800, 1824
FF = 2816
NEG = -30000.0
EPS = 1e-6


class T:
    def __init__(self, t, k):
        self.t = t
        self.k = k

    def __getitem__(self, idx):
        return self.t[idx]


class KB:
    NDS = 16

    def __init__(self, nc):
        self.nc = nc
        self.stack = ExitStack()
        self.eng = {'pe': nc.tensor, 'act': nc.scalar, 'dve': nc.vector, 'pool': nc.gpsimd, 'sp': nc.sync}
        self.sem = {}
        for e in self.eng:
            self.sem[e] = self.stack.enter_context(nc.semaphore("s_" + e))
        for j in range(self.NDS):
            self.sem[('d', j)] = self.stack.enter_context(nc.semaphore("d_%d" % j))
        self.cnt = {e: 0 for e in self.eng}
        self.seen = {e: {} for e in self.eng}
        self.state = {}
        self.dma_i = 0
        self.dma_uses = [0] * self.NDS
        self.nins = 0
        self.rr = 0
        self.excl = set()

    def _wait(self, e, evs):
        need = {}
        for (sk, v) in evs:
            if sk == e and e in ('pe', 'sp'):
                continue
            if self.seen[e].get(sk, 0) < v:
                need[sk] = max(need.get(sk, 0), v)
        for sk, v in need.items():
            self.eng[e].wait_ge(self.sem[sk], v)
            self.seen[e][sk] = v

    @staticmethod
    def _keys(lst):
        out = []
        for x in lst:
            if isinstance(x, T):
                out.append(x.k)
            elif isinstance(x, (list, tuple)) and len(x) and isinstance(x[0], T):
                out.append((x[0].k,) + tuple(x[1:]))
            else:
                out.append(x)
        return out

    def _deps(self, reads, writes):
        evs = []
        for k in reads:
            st = self.state.get(k)
            if st and st[0]:
                evs.append(st[0])
        for k in writes:
            st = self.state.get(k)
            if st:
                if st[0]:
                    evs.append(st[0])
                evs.extend(st[1])
        return evs

    def _update(self, ev, reads, writes):
        for k in reads:
            st = self.state.setdefault(k, [None, []])
            st[1].append(ev)
            if len(st[1]) > 12:
                best = {}
                for (sk, v) in st[1]:
                    best[sk] = max(best.get(sk, 0), v)
                st[1] = list(best.items())
        for k in writes:
            self.state[k] = [ev, []]

    def op(self, e, fn, r=(), w=()):
        r = self._keys(r)
        w = self._keys(w)
        w = w + [k for k in r if k in self.excl and k not in w]
        self._wait(e, self._deps(r, w))
        ins = fn(self.eng[e])
        self.cnt[e] += 1
        ins.then_inc(self.sem[e], 1)
        self._update((e, self.cnt[e]), r, w)
        self.nins += 1
        return ins

    def dma(self, q, out, in_, r=(), w=(), **kw):
        r = self._keys(r)
        w = self._keys(w)
        j = self.dma_i % self.NDS
        self.dma_i += 1
        evs = self._deps(r, w)
        if self.dma_uses[j] > 0:
            evs.append((('d', j), 16 * self.dma_uses[j]))
        self._wait(q, evs)
        ins = self.eng[q].dma_start(out=out, in_=in_, **kw)
        self.dma_uses[j] += 1
        ins.then_inc(self.sem[('d', j)], 16)
        ev = (('d', j), 16 * self.dma_uses[j])
        self._update(ev, r, w)
        self.nins += 1
        return ev

    def barrier(self):
        evs = [(f, self.cnt[f]) for f in self.eng if self.cnt[f]]
        evs += [(('d', j), 16 * self.dma_uses[j]) for j in range(self.NDS) if self.dma_uses[j]]
        for e in self.eng:
            self._wait(e, [ev for ev in evs if ev[0] != e])

    def finish(self):
        self.barrier()
        self.stack.close()

    def ew(self, with_act=False):
        self.rr += 1
        lst = ('dve', 'pool', 'act') if with_act else ('dve', 'pool')
        return lst[self.rr % len(lst)]


class Phase:
    def __init__(self, kb, tag):
        self.kb = kb
        self.nc = kb.nc
        self.tag = tag
        self.st = ExitStack()

    def sb(self, name, shape, dt):
        n = self.tag + "_" + name
        return T(self.st.enter_context(self.nc.sbuf_tensor(n, list(shape), dt)), n)

    def sbn(self, name, shape, dt, n):
        return [self.sb("%s%d" % (name, i), shape, dt) for i in range(n)]

    def ps(self, name, shape, dt=F32):
        n = self.tag + "_" + name
        self.kb.excl.add(n)
        return T(self.st.enter_context(self.nc.psum_tensor(n, list(shape), dt)), n)

    def psn(self, name, shape, dt, n):
        return [self.ps("%s%d" % (name, i), shape, dt) for i in range(n)]

    def close(self):
        self.kb.barrier()
        self.st.close()


def load_cast_weight(kb, ph, dst, src_ap, nchunks, ncols, gam=None, stage_cols=None):
    stage_cols = stage_cols or ncols
    stg = ph.sbn("stg_" + dst.k, [128, stage_cols], F32, 2)
    i = 0
    engs = ('dve', 'act', 'dve')
    for c in range(nchunks):
        for c0 in range(0, ncols, stage_cols):
            c1 = min(ncols, c0 + stage_cols)
            sg = stg[i % 2]
            kb.dma('sp', sg[:, 0:c1 - c0], src_ap[c * 128:(c + 1) * 128, c0:c1], w=[sg])
            e = engs[i % 3]
            o = dst[:, c, c0:c1]
            if gam is None:
                if e == 'act':
                    kb.op(e, lambda E: E.copy(out=o, in_=sg[:, 0:c1 - c0]), r=[sg], w=[(dst, c)])
                else:
                    kb.op(e, lambda E: E.tensor_copy(out=o, in_=sg[:, 0:c1 - c0]), r=[sg], w=[(dst, c)])
            else:
                if e == 'act':
                    kb.op(e, lambda E: E.activation(out=o, in_=sg[:, 0:c1 - c0], func=AF.Copy, scale=gam[:, c:c + 1]),
                          r=[sg, gam], w=[(dst, c)])
                else:
                    kb.op(e, lambda E: E.tensor_scalar(out=o, in0=sg[:, 0:c1 - c0], scalar1=gam[:, c:c + 1], scalar2=None,
                                                       op0=ALU.mult), r=[sg, gam], w=[(dst, c)])
            i += 1


def rms_rstd(kb, src_ap, junk, ss, rs, n, rkeys):
    kb.op('act', lambda E: E.activation(out=junk[:, 0:n], in_=src_ap, func=AF.Square, accum_out=ss[:]), r=rkeys, w=[junk, ss])
    kb.op('act', lambda E: E.activation(out=rs[:], in_=ss[:], func=AF.Sqrt, scale=1.0 / n, bias=EPS), r=[ss], w=[rs])
    kb.op('dve', lambda E: E.reciprocal(out=rs[:], in_=rs[:]), r=[rs], w=[rs])


def phase_inproj(kb, io):
    nc = kb.nc
    ph = Phase(kb, "ip")
    winb = ph.sb("winb", [128, 8, INW], BF16)
    gam = ph.sb("gam", [128, 8], F32)
    cw = ph.sb("cw", [128, 4, 12], F32)
    ident = ph.sb("ident", [128, 128], BF16)
    kb.dma('sp', gam[:], io['attn_norm_w'].rearrange("(c p) -> p c", p=128), w=[gam], allow_slow_non_contiguous=True)
    for j in range(4):
        kb.dma('sp', cw[:, j, :], io['gdn_conv_w'][j, :].rearrange("(c p) -> p c", p=128), w=[cw], allow_slow_non_contiguous=True)
    kb.dma('sp', ident[:], io['c_ident'][:, :], w=[ident])
    load_cast_weight(kb, ph, winb, io['w_in'], 8, INW, gam=gam, stage_cols=1936)

    xt = ph.sbn("xt", [128, D], F32, 2)
    junk = ph.sb("junk", [128, D], BF16)
    ss = ph.sbn("ss", [128, 1], F32, 2)
    rs = ph.sbn("rs", [128, 1], F32, 2)
    xn = ph.sbn("xn", [128, D], BF16, 2)
    xnT = ph.sbn("xnT", [128, 8, 512], BF16, 2)
    xc = ph.sbn("xc", [128, 515], F32, 3)
    acc = ph.sbn("acc", [128, 512], F32, 3)
    halo = ph.sb("halo", [128, 12, 3], F32)
    qT = ph.sb("qT", [128, 12, 512], BF16)
    qtm = ph.sbn("qtm", [128, 1536], BF16, 2)
    tmt = ph.sbn("tmt", [128, TMW], F32, 2)
    pT = ph.psn("pT", [128, 8, 128], BF16, 2)
    pF = ph.psn("pF", [128, 512], F32, 2)
    pM = ph.psn("pM", [128, 512], F32, 2)
    pQ = ph.psn("pQ", [128, 8, 128], BF16, 2)

    kb.op('pool', lambda E: E.memset(halo[:], 0.0), w=[halo])
    it = 0
    for s in range(S // 512):
        xT = xnT[s % 2]
        for t4 in range(4):
            t = s * 4 + t4
            b = it % 2
            it += 1
            kb.dma('sp', xt[b][:], io['x'][t * 128:(t + 1) * 128, :], w=[xt[b]])
            rms_rstd(kb, xt[b][:], junk, ss[b], rs[b], D, [xt[b]])
            kb.op('dve', lambda E: E.tensor_scalar(out=xn[b][:], in0=xt[b][:], scalar1=rs[b][:, 0:1], scalar2=None, op0=ALU.mult),
                  r=[xt[b], rs[b]], w=[xn[b]])
            for c in range(8):
                kb.op('pe', lambda E: E.transpose(out=pT[b][:, c, :], in_=xn[b][:, c * 128:(c + 1) * 128], identity=ident[:]),
                      r=[xn[b], ident], w=[pT[b]])
            kb.op('act', lambda E: E.copy(out=xT[:, :, t4 * 128:(t4 + 1) * 128], in_=pT[b][:]), r=[pT[b]], w=[xT])
        for cb in range(12):
            pf = pF[cb % 2]
            for c in range(8):
                kb.op('pe', lambda E: E.matmul(pf[:], lhsT=winb[:, c, cb * 128:(cb + 1) * 128], rhs=xT[:, c, :],
                                               start=(c == 0), stop=(c == 7)), r=[xT, (winb, c)], w=[pf])
            x3 = xc[cb % 3]
            ac = acc[cb % 3]
            kb.op('act', lambda E: E.copy(out=x3[:, 3:515], in_=pf[:]), r=[pf], w=[(x3, 'b')])
            kb.op('dve', lambda E: E.tensor_copy(out=x3[:, 0:3], in_=halo[:, cb, :]), r=[(halo, cb)], w=[(x3, 'h')])
            kb.op('dve', lambda E: E.tensor_copy(out=halo[:, cb, :], in_=x3[:, 512:515]), r=[(x3, 'b')], w=[(halo, cb)])
            kb.op('act', lambda E: E.activation(out=ac[:], in_=pf[:], func=AF.Copy, scale=cw[:, 3, cb:cb + 1]), r=[pf, cw], w=[ac])
            for j in range(3):
                kb.op('dve', lambda E: E.scalar_tensor_tensor(out=ac[:], in0=x3[:, j:j + 512], scalar=cw[:, j, cb:cb + 1], in1=ac[:],
                                                           op0=ALU.mult, op1=ALU.add), r=[(x3, 'b'), (x3, 'h'), cw, ac], w=[ac])
            kb.op('act', lambda E: E.activation(out=qT[:, cb, :], in_=ac[:], func=AF.Silu), r=[ac], w=[(qT, cb)])
        for t4 in range(4):
            t = s * 4 + t4
            qm = qtm[t % 2]
            for g3 in range(3):
                pq = pQ[g3 % 2]
                for j in range(4):
                    cb = g3 * 4 + j
                    kb.op('pe', lambda E: E.transpose(out=pq[:, j, :], in_=qT[:, cb, t4 * 128:(t4 + 1) * 128], identity=ident[:]),
                          r=[(qT, cb), ident], w=[pq])
                e1 = 'dve' if g3 % 2 == 0 else 'act'
                if e1 == 'dve':
                    kb.op('dve', lambda E: E.tensor_copy(out=qm[:, g3 * 512:(g3 + 1) * 512], in_=pq[:, 0:4, :].rearrange("p a b -> p (a b)")),
                          r=[pq], w=[(qm, g3)])
                else:
                    kb.op('act', lambda E: E.copy(out=qm[:, g3 * 512:(g3 + 1) * 512], in_=pq[:, 0:4, :].rearrange("p a b -> p (a b)")),
                          r=[pq], w=[(qm, g3)])
            kb.dma('pool', io['qkv_tm'][t * 128:(t + 1) * 128, :], qm[:], r=[(qm, 0), (qm, 1), (qm, 2)], w=['qkv_tm'])
            tm = tmt[t % 2]
            for ci, n0 in enumerate(range(0, TMW, 512)):
                n1 = min(TMW, n0 + 512)
                pm = pM[ci % 2]
                for c in range(8):
                    kb.op('pe', lambda E: E.matmul(pm[:, 0:n1 - n0], lhsT=xT[:, c, t4 * 128:(t4 + 1) * 128],
                                                   rhs=winb[:, c, 1536 + n0:1536 + n1], start=(c == 0), stop=(c == 7)),
                          r=[xT, (winb, c)], w=[pm])
                if ci % 2 == 0:
                    kb.op('dve', lambda E: E.tensor_copy(out=tm[:, n0:n1], in_=pm[:, 0:n1 - n0]), r=[pm], w=[(tm, ci)])
                else:
                    kb.op('act', lambda E: E.copy(out=tm[:, n0:n1], in_=pm[:, 0:n1 - n0]), r=[pm], w=[(tm, ci)])
            kb.dma('pool', io['tm'][t * 128:(t + 1) * 128, :], tm[:], r=[(tm, i) for i in range(5)], w=['tm'])
    ph.close()


def attn_block(kb, o_ps, PTbuf, pti, sc_ps, kt_specs, ident, rkeys_q, LA=2):
    n = len(kt_specs)
    pts = [None] * n
    started = set()
    npv = sum(sp['qt1'] - sp['qt0'] for sp in kt_specs)
    done = 0
    for i in range(n + LA):
        if i < n:
            sp = kt_specs[i]
            pss = sc_ps[pti[0] % len(sc_ps)]
            ptb = PTbuf[pti[0] % len(PTbuf)]
            pti[0] += 1
            pts[i] = ptb
            q0, q1 = sp['qt0'], sp['qt1']
            ncol = (q1 - q0) * 128
            nk = sp['nk']
            nm = len(sp['masks'])
            kb.op('pe', lambda E: E.matmul(pss[0:nk, 0:ncol], lhsT=sp['lhsT'], rhs=sp['rhs_fn'](q0, q1), start=True, stop=(nm == 0)),
                  r=sp['rk'] + rkeys_q, w=[pss])
            for mi, (c0, nc_, mask) in enumerate(sp['masks']):
                kb.op('pe', lambda E: E.matmul(pss[0:nk, c0:c0 + nc_], lhsT=ident[0:nk, 0:nk], rhs=mask, start=False, stop=(mi == nm - 1)),
                      r=[ident] + sp.get('mk', []), w=[pss])
            kb.op('act', lambda E: E.activation(out=ptb[0:nk, 0:ncol], in_=pss[0:nk, 0:ncol], func=AF.Exp), r=[pss], w=[ptb])
        j = i - LA
        if j >= 0:
            sp = kt_specs[j]
            ptb = pts[j]
            nk = sp['nk']
            for qt in range(sp['qt0'], sp['qt1']):
                c0 = (qt - sp['qt0']) * 128
                oap, okey, bank = o_ps(qt)
                st = bank not in started
                started.add(bank)
                done += 1
                kb.op('pe', lambda E: E.matmul(oap, lhsT=ptb[0:nk, c0:c0 + 128], rhs=sp['v_fn'](qt), start=st, stop=(done == npv),
                                               skip_group_check=True), r=[ptb] + sp['rv'], w=[okey])


def phase_mem(kb, io):
    nc = kb.nc
    ph = Phase(kb, "mm")
    ident = ph.sb("ident", [128, 128], BF16)
    kb.dma('sp', ident[:], io['c_ident'][:, :], w=[ident])
    wkv = ph.sb("wkv", [128, 8, 1024], BF16)
    gam = ph.sb("gam", [128, 8], F32)
    kb.dma('sp', gam[:], io['mem_norm_w'].rearrange("(c p) -> p c", p=128), w=[gam], allow_slow_non_contiguous=True)
    load_cast_weight(kb, ph, wkv, io['mem_w_kv'], 8, 1024, gam=gam)
    qnw = ph.sb("qnw", [128, 128], F32)
    knw = ph.sb("knw", [128, 128], F32)
    kb.dma('sp', qnw[:], io['mem_q_norm_w'].partition_broadcast(128), w=[qnw])
    kb.dma('sp', knw[:], io['mem_k_norm_w'].partition_broadcast(128), w=[knw])
    kb.op('dve', lambda E: E.tensor_scalar(out=qnw[:], in0=qnw[:], scalar1=128 ** -0.5, scalar2=None, op0=ALU.mult), r=[qnw], w=[qnw])

    mt = ph.sbn("mt", [128, D], F32, 2)
    junk = ph.sb("junk", [128, D], BF16)
    ss = ph.sb("ss", [128, 1], F32)
    rs = ph.sb("rs", [128, 1], F32)
    mn = ph.sb("mn", [128, D], BF16)
    mnT = ph.sb("mnT", [128, 8, 128], BF16)
    kvt = ph.sb("kvt", [128, 1024], F32)
    sq4 = ph.sb("sq4", [128, 4, 128], F32)
    ss4 = ph.sb("ss4", [128, 4], F32)
    rs4 = ph.sb("rs4", [128, 4], F32)
    kn = ph.sb("kn", [128, 4, 128], BF16)
    kT = ph.sb("kT", [128, 4, 256], BF16)
    v1 = ph.sb("v1", [128, 2, 4, 129], BF16)
    pT = ph.ps("pT", [128, 8, 128], BF16)
    pK = ph.psn("pK", [128, 512], F32, 2)
    kb.op('pool', lambda E: E.memset(v1[:], 1.0), w=[v1])
    for mt_i in range(2):
        m = mt[mt_i]
        kb.dma('sp', m[:], io['mem'][mt_i * 128:(mt_i + 1) * 128, :], w=[m])
        rms_rstd(kb, m[:], junk, ss, rs, D, [m])
        kb.op('dve', lambda E: E.tensor_scalar(out=mn[:], in0=m[:], scalar1=rs[:, 0:1], scalar2=None, op0=ALU.mult), r=[m, rs], w=[mn])
        for c in range(8):
            kb.op('pe', lambda E: E.transpose(out=pT[:, c, :], in_=mn[:, c * 128:(c + 1) * 128], identity=ident[:]), r=[mn, ident], w=[pT])
        kb.op('act', lambda E: E.copy(out=mnT[:], in_=pT[:]), r=[pT], w=[mnT])
        for half in range(2):
            pk = pK[half]
            for c in range(8):
                kb.op('pe', lambda E: E.matmul(pk[:], lhsT=mnT[:, c, :], rhs=wkv[:, c, half * 512:(half + 1) * 512],
                                               start=(c == 0), stop=(c == 7)), r=[mnT, (wkv, c)], w=[pk])
            kb.op('act', lambda E: E.copy(out=kvt[:, half * 512:(half + 1) * 512], in_=pk[:]), r=[pk], w=[(kvt, half)])
        k3 = kvt[:, 0:512].rearrange("p (h d) -> p h d", h=4)
        kb.op('dve', lambda E: E.tensor_tensor(out=sq4[:], in0=k3, in1=k3, op=ALU.mult), r=[(kvt, 0)], w=[sq4])
        kb.op('dve', lambda E: E.tensor_reduce(out=ss4[:], in_=sq4[:], axis=AX.X, op=ALU.add), r=[sq4], w=[ss4])
        kb.op('act', lambda E: E.activation(out=rs4[:], in_=ss4[:], func=AF.Sqrt, scale=1.0 / 128, bias=EPS), r=[ss4], w=[rs4])
        kb.op('dve', lambda E: E.reciprocal(out=rs4[:], in_=rs4[:]), r=[rs4], w=[rs4])
        kb.op('dve', lambda E: E.tensor_tensor(out=sq4[:], in0=k3, in1=rs4[:].unsqueeze(2).to_broadcast([128, 4, 128]), op=ALU.mult),
              r=[(kvt, 0), rs4], w=[sq4])
        kb.op('dve', lambda E: E.tensor_tensor(out=kn[:], in0=sq4[:], in1=knw[:].unsqueeze(1).to_broadcast([128, 4, 128]), op=ALU.mult),
              r=[sq4, knw], w=[kn])
        for h in range(4):
            kb.op('pe', lambda E: E.transpose(out=pT[:, h, :], in_=kn[:, h, :], identity=ident[:]), r=[kn, ident], w=[pT])
        kb.op('act', lambda E: E.copy(out=kT[:, :, mt_i * 128:(mt_i + 1) * 128], in_=pT[:, 0:4, :]), r=[pT], w=[kT])
        kb.op('dve', lambda E: E.tensor_copy(out=v1[:, mt_i, :, 0:128], in_=kvt[:, 512:1024].rearrange("p (h d) -> p h d", h=4)),
              r=[(kvt, 1)], w=[v1])

    qt_ = ph.sbn("qt", [128, 512], F32, 2)
    qs = ph.sb("qs", [128, 4, 128], F32)
    qn = ph.sbn("qn", [128, 4, 128], BF16, 2)
    qT = ph.sbn("qT", [128, 4, 512], BF16, 2)
    PTb = ph.sbn("PT", [128, 512], BF16, 4)
    pti = [0]
    oc = ph.sbn("oc", [128, 4, 128], BF16, 4)
    rinv = ph.sb("rinv", [128, 1], F32)
    ocT = ph.sbn("ocT", [128, 4, 128], BF16, 2)
    sc_ps = ph.psn("sc", [128, 512], F32, 2)
    o_psA = ph.ps("oA", [128, 2, 256], F32)
    o_psB = ph.ps("oB", [128, 2, 256], F32)
    pO = ph.ps("pO", [128, 8, 128], BF16)
    for s in range(S // 512):
        qTs = qT[s % 2]
        for t4 in range(4):
            t = s * 4 + t4
            q = qt_[t % 2]
            qb = qn[t % 2]
            kb.dma('sp', q[:], io['tm'][t * 128:(t + 1) * 128, MQ_OFF:MQ_OFF + 512], r=['tm'], w=[q])
            q3 = q[:].rearrange("p (h d) -> p h d", h=4)
            kb.op('pool', lambda E: E.tensor_tensor(out=qs[:], in0=q3, in1=q3, op=ALU.mult), r=[q], w=[qs])
            kb.op('dve', lambda E: E.tensor_reduce(out=ss4[:], in_=qs[:], axis=AX.X, op=ALU.add), r=[qs], w=[ss4])
            kb.op('act', lambda E: E.activation(out=rs4[:], in_=ss4[:], func=AF.Sqrt, scale=1.0 / 128, bias=EPS), r=[ss4], w=[rs4])
            kb.op('dve', lambda E: E.reciprocal(out=rs4[:], in_=rs4[:]), r=[rs4], w=[rs4])
            kb.op('dve', lambda E: E.tensor_tensor(out=qs[:], in0=q3, in1=rs4[:].unsqueeze(2).to_broadcast([128, 4, 128]), op=ALU.mult),
                  r=[q, rs4], w=[qs])
            kb.op('pool', lambda E: E.tensor_tensor(out=qb[:], in0=qs[:], in1=qnw[:].unsqueeze(1).to_broadcast([128, 4, 128]), op=ALU.mult),
                  r=[qs, qnw], w=[qb])
            for h in range(4):
                kb.op('pe', lambda E: E.transpose(out=pT[:, h, :], in_=qb[:, h, :], identity=ident[:]), r=[qb, ident], w=[pT])
            kb.op('act', lambda E: E.copy(out=qTs[:, :, t4 * 128:(t4 + 1) * 128], in_=pT[:, 0:4, :]), r=[pT], w=[qTs])
        for h in range(4):
            def o_ps(qt, h=h):
                return (o_psA[:, qt, 0:129], o_psA, 'A') if qt < 2 else (o_psB[:, qt - 2, 0:129], o_psB, 'B')
            specs = []
            for kt in range(2):
                specs.append(dict(lhsT=kT[:, h, kt * 128:(kt + 1) * 128], rhs_fn=lambda q0, q1, h=h: qTs[:, h, q0 * 128:q1 * 128],
                                  qt0=0, qt1=4, masks=[], v_fn=lambda qt, kt=kt, h=h: v1[:, kt, h, :], nk=128, rk=[kT], rv=[v1]))
            attn_block(kb, o_ps, PTb, pti, sc_ps, specs, ident, [qTs])
            for t4 in range(4):
                t = s * 4 + t4
                ob = oc[t4]
                oap, okey, _ = o_ps(t4)
                kb.op('dve', lambda E: E.reciprocal(out=rinv[:], in_=oap[:, 128:129]), r=[okey], w=[rinv])
                kb.op('dve', lambda E: E.tensor_scalar(out=ob[:, h, :], in0=oap[:, 0:128], scalar1=rinv[:, 0:1], scalar2=None, op0=ALU.mult),
                      r=[okey, rinv], w=[(ob, h)])
        for t4 in range(4):
            t = s * 4 + t4
            ob = oc[t4]
            oT = ocT[t % 2]
            for h in range(4):
                kb.op('pe', lambda E: E.transpose(out=pO[:, h, :], in_=ob[:, h, :], identity=ident[:]), r=[(ob, h), ident], w=[pO])
            kb.op('act', lambda E: E.copy(out=oT[:], in_=pO[:, 0:4, :]), r=[pO], w=[oT])
            kb.dma('sp', io['ocatT'][t, :, 8:12, :], oT[:], r=[oT], w=['ocatT_c'])
    ph.close()


def phase_gdn(kb, io):
    nc = kb.nc
    ph = Phase(kb, "gd")
    ident = ph.sb("ident", [128, 128], BF16)
    btri = ph.sb("btri", [128, 128], F32)
    bones = ph.sb("bones", [128, 128], F32)
    ones = ph.sb("ones", [128, 128], F32)
    mlow = ph.sb("mlow", [128, 4, 128], F32)
    mup = ph.sb("mup", [128, 4, 128], F32)
    strict = ph.sb("strict", [128, 128], F32)
    mch = ph.sb("mch", [128, 2], F32)
    kb.dma('sp', ident[:], io['c_ident'][:, :], w=[ident])
    kb.dma('sp', btri[:], io['c_btri'][:, :], w=[btri])
    kb.dma('sp', bones[:], io['c_bones'][:, :], w=[bones])
    kb.dma('sp', strict[:], io['c_strict'][:, :], w=[strict])
    kb.dma('sp', mch[:], io['c_mch'][:, :], w=[mch])
    for h in range(4):
        kb.dma('sp', mlow[:, h, :], io['c_mlow'][:, :], w=[mlow])
        kb.dma('sp', mup[:, h, :], io['c_mup'][:, :], w=[mup])
    kb.op('pool', lambda E: E.memset(ones[:], 1.0), w=[ones])
    dtb = ph.sb("dtb", [128, 4], F32)
    nA = ph.sb("nA", [128, 4], F32)
    gw = ph.sb("gw", [128, 128], F32)
    kb.dma('sp', dtb[:], io['gdn_dt_bias'].partition_broadcast(128), w=[dtb])
    kb.dma('sp', nA[:], io['gdn_a_log'].partition_broadcast(128), w=[nA])
    kb.dma('sp', gw[:], io['gdn_out_norm_w'].partition_broadcast(128), w=[gw])
    kb.op('act', lambda E: E.activation(out=nA[:], in_=nA[:], func=AF.Exp), r=[nA], w=[nA])
    kb.op('dve', lambda E: E.tensor_scalar(out=nA[:], in0=nA[:], scalar1=-1.0, scalar2=None, op0=ALU.mult), r=[nA], w=[nA])

    def B4(t_, n=4):
        return t_.unsqueeze(2).to_broadcast([128, n, 128])

    def M4(t_):
        return t_.unsqueeze(1).to_broadcast([128, 4, 128])

    qkv = ph.sbn("qkv", [128, 3, 4, 128], BF16, 2)
    ab = ph.sbn("ab", [128, 8], F32, 2)
    gt = ph.sbn("gt", [128, 512], F32, 2)
    sm = ph.sbn("sm", [128, 64], F32, 2)
    gs = ph.sbn("gs", [128, 16], F32, 2)
    gm = ph.sb("gm", [128, 8], F32)
    R1 = ph.sb("R1", [128, 4, 128], F32)
    R2 = ph.sb("R2", [128, 4, 128], F32)
    tmpA = ph.sb("tmpA", [128, 4, 128], F32)
    tmpB = ph.sb("tmpB", [128, 4, 128], F32)
    dec = ph.sb("dec", [128, 4, 128], F32)
    decT = ph.sb("decT", [128, 4, 128], F32)
    sq = ph.sb("sq", [128, 4, 128], F32)
    KBG = ph.sb("KBG", [128, 4, 128], BF16)
    Kdb = ph.sbn("Kd", [128, 4, 128], BF16, 4)
    VBb = ph.sbn("VB", [128, 4, 128], BF16, 2)
    dg = ph.sbn("dg", [128, 4, 128], BF16, 4)
    QT = ph.sb("QT", [128, 4, 128], BF16)
    QGb = ph.sbn("QG", [128, 4, 128], BF16, 4)
    KT = ph.sb("KT", [128, 4, 128], BF16)
    nbs = ph.sb("nbs", [128, 4, 128], F32)
    Xb = ph.sbn("X", [128, 4, 128], BF16, 2)
    Yb = ph.sbn("Y", [128, 4, 128], BF16, 2)
    Pbb = ph.sbn("P", [128, 4, 128], BF16, 4)
    aqkTb = ph.sbn("aqkT", [128, 4, 128], BF16, 2)
    negWTb = ph.sbn("negWT", [128, 4, 128], BF16, 2)
    vnew = ph.sb("vnew", [128, 4, 128], BF16)
    Sf = ph.sb("Sf", [128, 4, 128], F32)
    Sbf = ph.sbn("Sbf", [128, 4, 128], BF16, 3)
    osb = ph.sb("osb", [128, 4, 128], F32)
    sgt = ph.sb("sgt", [128, 512], F32)
    oa = ph.sbn("oa", [128, 4, 128], BF16, 2)
    oaT = ph.sbn("oaT", [128, 4, 128], BF16, 2)
    pS = ph.ps("pS", [128, 512], F32)
    pD = ph.ps("pD", [128, 4, 128], F32)
    pA = ph.psn("pA", [128, 4, 128], F32, 2)
    pB = ph.psn("pB", [128, 8, 128], BF16, 1)
    pV = ph.ps("pV", [128, 4, 128], F32)
    pdS = ph.ps("pdS", [128, 4, 128], F32)
    pO = ph.ps("pO", [128, 4, 128], F32)
    kb.op('pool', lambda E: E.memset(Sf[:], 0.0), w=[Sf])
    kb.op('pool', lambda E: E.memset(Sbf[0][:], 0.0), w=[Sbf[0]])
    kb.op('pool', lambda E: E.memset(vnew[:], 0.0), w=[vnew])
    pai = [0]

    def PA():
        pai[0] += 1
        return pA[pai[0] % 2]

    def mm4(p, lf, rf, rk):
        for h in range(4):
            kb.op('pe', lambda E: E.matmul(p[:, h, :], lhsT=lf(h), rhs=rf(h), start=True, stop=True), r=rk, w=[p])

    si = [0]

    def tile(t):
        b = t % 2
        x_ = qkv[b]
        VB, aqkT, negWT = VBb[b], aqkTb[b], negWTb[b]
        Kd = Kdb[2 * b:2 * b + 2]
        QG = QGb[2 * b:2 * b + 2]
        Pb = Pbb[2 * b:2 * b + 2]
        s_ = sm[b]
        kb.dma('sp', x_[:].rearrange("p a h d -> p (a h d)"), io['qkv_tm'][t * 128:(t + 1) * 128, :], r=['qkv_tm'], w=[x_])
        kb.dma('sp', ab[b][:], io['tm'][t * 128:(t + 1) * 128, 0:8], r=['tm'], w=[ab[b]])
        kb.dma('sp', gt[b][:], io['tm'][t * 128:(t + 1) * 128, GATE_OFF:GATE_OFF + 512], r=['tm'], w=[gt[b]])
        g = s_[:, 0:4]
        kb.op('dve', lambda E: E.tensor_tensor(out=g, in0=ab[b][:, 0:4], in1=dtb[:], op=ALU.add), r=[ab[b], dtb], w=[s_])
        kb.op('act', lambda E: E.activation(out=g, in_=g, func=AF.Exp), r=[s_], w=[s_])
        kb.op('act', lambda E: E.activation(out=g, in_=g, func=AF.Ln, bias=1.0), r=[s_], w=[s_])
        kb.op('dve', lambda E: E.tensor_tensor(out=g, in0=g, in1=nA[:], op=ALU.mult), r=[s_, nA], w=[s_])
        kb.op('dve', lambda E: E.tensor_scalar(out=s_[:, 4:8], in0=g, scalar1=-1.0, scalar2=None, op0=ALU.mult), r=[s_], w=[s_])
        kb.op('act', lambda E: E.activation(out=s_[:, 8:12], in_=ab[b][:, 4:8], func=AF.Exp, scale=-1.0), r=[ab[b], s_], w=[s_])
        kb.op('dve', lambda E: E.tensor_scalar(out=s_[:, 8:12], in0=s_[:, 8:12], scalar1=1.0, scalar2=None, op0=ALU.add), r=[s_], w=[s_])
        kb.op('dve', lambda E: E.reciprocal(out=s_[:, 8:12], in_=s_[:, 8:12]), r=[s_], w=[s_])
        kb.op('dve', lambda E: E.tensor_scalar(out=s_[:, 12:16], in0=s_[:, 8:12], scalar1=-1.0, scalar2=None, op0=ALU.mult), r=[s_], w=[s_])
        for j in range(2):
            kb.op('dve', lambda E: E.tensor_scalar(out=gm[:, 4 * j:4 * j + 4], in0=g, scalar1=mch[:, j:j + 1], scalar2=None, op0=ALU.mult),
                  r=[s_, mch], w=[gm])
        kb.op('pe', lambda E: E.matmul(pS[:, 0:4], lhsT=btri[:], rhs=g, start=True, stop=True), r=[btri, s_], w=[pS])
        kb.op('pe', lambda E: E.matmul(pS[:, 4:8], lhsT=bones[:], rhs=g, start=True, stop=True), r=[bones, s_], w=[pS])
        kb.op('pe', lambda E: E.matmul(pS[:, 8:16], lhsT=ones[:], rhs=gm[:], start=True, stop=True), r=[ones, gm], w=[pS])
        G = gs[b]
        kb.op('dve', lambda E: E.tensor_copy(out=G[:], in_=pS[:, 0:16]), r=[pS], w=[G])
        kb.op('act', lambda E: E.activation(out=s_[:, 16:20], in_=G[:, 0:4], func=AF.Exp), r=[G, s_], w=[s_])
        kb.op('dve', lambda E: E.tensor_tensor(out=G[:, 4:8], in0=G[:, 4:8], in1=G[:, 0:4], op=ALU.subtract), r=[G], w=[G])
        kb.op('act', lambda E: E.activation(out=s_[:, 20:24], in_=G[:, 4:8], func=AF.Exp), r=[G, s_], w=[s_])
        kb.op('act', lambda E: E.activation(out=s_[:, 56:64], in_=G[:, 8:16], func=AF.Exp), r=[G, s_], w=[s_])
        yield
        kb.op('pool', lambda E: E.tensor_copy(out=R1[:], in_=B4(g)), r=[s_], w=[R1])
        kb.op('pool', lambda E: E.tensor_tensor(out=R2[:], in0=M4(btri[:]), in1=B4(s_[:, 4:8]), op=ALU.mult), r=[s_, btri], w=[R2])
        kb.op('pe', lambda E: E.matmul(pD[:].rearrange("p a b -> p (a b)"), lhsT=btri[:], rhs=R1[:].rearrange("p a b -> p (a b)"),
                                       start=True, stop=False), r=[btri, R1], w=[pD])
        kb.op('pe', lambda E: E.matmul(pD[:].rearrange("p a b -> p (a b)"), lhsT=bones[:], rhs=R2[:].rearrange("p a b -> p (a b)"),
                                       start=False, stop=True), r=[bones, R2], w=[pD])
        kb.op('dve', lambda E: E.tensor_tensor(out=tmpA[:], in0=pD[:], in1=mlow[:], op=ALU.add), r=[pD, mlow], w=[tmpA])
        kb.op('act', lambda E: E.activation(out=dec[:], in_=tmpA[:], func=AF.Exp), r=[tmpA], w=[dec])
        kb.op('dve', lambda E: E.scalar_tensor_tensor(out=tmpB[:], in0=pD[:], scalar=-1.0, in1=mup[:], op0=ALU.mult, op1=ALU.add),
              r=[pD, mup], w=[tmpB])
        kb.op('act', lambda E: E.activation(out=decT[:], in_=tmpB[:], func=AF.Exp), r=[tmpB], w=[decT])
        yield
        for a_, c0 in ((0, 24), (1, 28)):
            kb.op('pool', lambda E: E.tensor_tensor(out=sq[:], in0=x_[:, a_, :, :], in1=x_[:, a_, :, :], op=ALU.mult), r=[x_], w=[sq])
            kb.op('dve', lambda E: E.tensor_reduce(out=s_[:, c0:c0 + 4], in_=sq[:], axis=AX.X, op=ALU.add), r=[sq, s_], w=[s_])
            kb.op('act', lambda E: E.activation(out=s_[:, c0:c0 + 4], in_=s_[:, c0:c0 + 4], func=AF.Sqrt, bias=EPS), r=[s_], w=[s_])
            kb.op('dve', lambda E: E.reciprocal(out=s_[:, c0:c0 + 4], in_=s_[:, c0:c0 + 4]), r=[s_], w=[s_])
        sc_ = lambda o, a, bb: kb.op('dve', lambda E: E.tensor_tensor(out=s_[:, o:o + 4], in0=a, in1=bb, op=ALU.mult), r=[s_, mch], w=[s_])
        kb.op('dve', lambda E: E.tensor_scalar(out=s_[:, 32:36], in0=s_[:, 24:28], scalar1=128 ** -0.5, scalar2=None, op0=ALU.mult), r=[s_], w=[s_])
        sc_(36, s_[:, 32:36], s_[:, 16:20])
        kb.op('dve', lambda E: E.tensor_scalar(out=s_[:, 40:44], in0=s_[:, 36:40], scalar1=mch[:, 1:2], scalar2=None, op0=ALU.mult), r=[s_, mch], w=[s_])
        kb.op('dve', lambda E: E.tensor_scalar(out=s_[:, 36:40], in0=s_[:, 36:40], scalar1=mch[:, 0:1], scalar2=None, op0=ALU.mult), r=[s_, mch], w=[s_])
        sc_(44, s_[:, 28:32], s_[:, 8:12])
        sc_(44, s_[:, 44:48], s_[:, 16:20])
        sc_(48, s_[:, 28:32], s_[:, 20:24])
        kb.op('dve', lambda E: E.tensor_scalar(out=s_[:, 52:56], in0=s_[:, 48:52], scalar1=mch[:, 1:2], scalar2=None, op0=ALU.mult), r=[s_, mch], w=[s_])
        kb.op('dve', lambda E: E.tensor_scalar(out=s_[:, 48:52], in0=s_[:, 48:52], scalar1=mch[:, 0:1], scalar2=None, op0=ALU.mult), r=[s_, mch], w=[s_])
        yield
        kx, vx, qx = x_[:, 1, :, :], x_[:, 2, :, :], x_[:, 0, :, :]
        kb.op('pool', lambda E: E.tensor_tensor(out=KBG[:], in0=kx, in1=B4(s_[:, 44:48]), op=ALU.mult), r=[x_, s_], w=[KBG])
        kb.op('pool', lambda E: E.tensor_tensor(out=Kd[0][:], in0=kx, in1=B4(s_[:, 48:52]), op=ALU.mult), r=[x_, s_], w=[Kd[0]])
        kb.op('pool', lambda E: E.tensor_tensor(out=Kd[1][:], in0=kx, in1=B4(s_[:, 52:56]), op=ALU.mult), r=[x_, s_], w=[Kd[1]])
        kb.op('pool', lambda E: E.tensor_tensor(out=VB[:], in0=vx, in1=B4(s_[:, 8:12]), op=ALU.mult), r=[x_, s_], w=[VB])
        yield
        for i_, c0 in enumerate((32, 36, 40, 28)):
            kb.op('dve', lambda E: E.tensor_tensor(out=dg[i_][:], in0=M4(ident[:]), in1=B4(s_[:, c0:c0 + 4]), op=ALU.mult), r=[ident, s_], w=[dg[i_]])
        for i_, (src, dst) in enumerate(((qx, QT), (qx, QG[0]), (qx, QG[1]), (kx, KT))):
            p = PA()
            mm4(p, lambda h: src[:, h, :], lambda h: dg[i_][:, h, :], [x_, dg[i_]])
            if i_ % 2 == 0:
                kb.op('act', lambda E: E.copy(out=dst[:], in_=p[:]), r=[p], w=[dst])
            else:
                kb.op('dve', lambda E: E.tensor_copy(out=dst[:], in_=p[:]), r=[p], w=[dst])
        yield
        p = PA()
        mm4(p, lambda h: KT[:, h, :], lambda h: KT[:, h, :], [KT])
        kb.op('dve', lambda E: E.tensor_tensor(out=tmpA[:], in0=p[:], in1=dec[:], op=ALU.mult), r=[p, dec], w=[tmpA])
        kb.op('pool', lambda E: E.tensor_tensor(out=nbs[:], in0=M4(strict[:]), in1=B4(s_[:, 12:16]), op=ALU.mult), r=[strict, s_], w=[nbs])
        X, Y, P = Xb[0], Yb[0], Pb[0]
        kb.op('pool', lambda E: E.tensor_tensor(out=X[:], in0=tmpA[:], in1=nbs[:], op=ALU.mult), r=[tmpA, nbs], w=[X])
        for h in range(4):
            kb.op('pe', lambda E: E.transpose(out=pB[0][:, h, :], in_=X[:, h, :], identity=ident[:]), r=[X, ident], w=[pB[0]])
        kb.op('act', lambda E: E.copy(out=Y[:], in_=pB[0][:, 0:4, :]), r=[pB[0]], w=[Y])
        kb.op('dve', lambda E: E.tensor_tensor(out=P[:], in0=pB[0][:, 0:4, :], in1=M4(ident[:]), op=ALU.add), r=[pB[0], ident], w=[P])
        yield
        p = PA()
        mm4(p, lambda h: KT[:, h, :], lambda h: QT[:, h, :], [KT, QT])
        kb.op('dve', lambda E: E.tensor_tensor(out=aqkT[:], in0=p[:], in1=decT[:], op=ALU.mult), r=[p, decT], w=[aqkT])
        yield
        for k_ in range(1, 6):
            Xn, Yn, Pn = Xb[k_ % 2], Yb[k_ % 2], Pb[k_ % 2]
            p = PA()
            mm4(p, lambda h: Y[:, h, :], lambda h: X[:, h, :], [X, Y])
            kb.op('act', lambda E: E.copy(out=Xn[:], in_=p[:]), r=[p], w=[Xn])
            if k_ < 5:
                p2 = PA()
                mm4(p2, lambda h: X[:, h, :], lambda h: Y[:, h, :], [X, Y])
                kb.op('dve', lambda E: E.tensor_copy(out=Yn[:], in_=p2[:]), r=[p2], w=[Yn])
            p3 = PA()
            mm4(p3, lambda h: Xn[:, h, :], lambda h: P[:, h, :], [Xn, P])
            kb.op('dve', lambda E: E.tensor_tensor(out=Pn[:], in0=p3[:], in1=P[:], op=ALU.add), r=[p3, P], w=[Pn])
            X, Y, P = Xn, Yn, Pn
            yield
        yield
        p = PA()
        mm4(p, lambda h: KBG[:, h, :], lambda h: P[:, h, :], [KBG, P])
        kb.op('act', lambda E: E.mul(out=negWT[:], in_=p[:], mul=-1.0), r=[p], w=[negWT])
        yield 'B'
        Sa = Sbf[si[0] % 3]
        Sb_ = Sbf[(si[0] + 1) % 3]
        Sc = Sbf[(si[0] + 2) % 3]
        si[0] += 2
        for j, (Scur, Snext) in enumerate(((Sa, Sb_), (Sb_, Sc))):
            for h in range(4):
                kb.op('pe', lambda E: E.matmul(pV[:, h, :], lhsT=P[:, h, :], rhs=VB[:, h, :], start=True, stop=False), r=[P, VB], w=[pV])
                kb.op('pe', lambda E: E.matmul(pV[:, h, :], lhsT=negWT[:, h, :], rhs=Scur[:, h, :], start=False, stop=True),
                      r=[negWT, Scur], w=[pV])
            yield
            r0 = 64 * j
            kb.op('act', lambda E: E.copy(out=vnew[r0:r0 + 64, :, :], in_=pV[r0:r0 + 64, :, :]), r=[pV], w=[vnew])
            yield
            mm4(pdS, lambda h: Kd[j][:, h, :], lambda h: vnew[:, h, :], [Kd[j], vnew])
            yield
            for h in range(4):
                kb.op('dve', lambda E: E.scalar_tensor_tensor(out=Sf[:, h, :], in0=Sf[:, h, :], scalar=s_[:, 56 + 4 * j + h:57 + 4 * j + h],
                                                              in1=pdS[:, h, :], op0=ALU.mult, op1=ALU.add), r=[Sf, s_, pdS], w=[Sf])
            kb.op('act', lambda E: E.copy(out=Snext[:], in_=Sf[:]), r=[Sf], w=[Snext])
            yield
        for h in range(4):
            kb.op('pe', lambda E: E.matmul(pO[:, h, :], lhsT=QG[0][:, h, :], rhs=Sa[:, h, :], start=True, stop=False), r=[QG[0], Sa], w=[pO])
            kb.op('pe', lambda E: E.matmul(pO[:, h, :], lhsT=QG[1][:, h, :], rhs=Sb_[:, h, :], start=False, stop=False), r=[QG[1], Sb_], w=[pO])
            kb.op('pe', lambda E: E.matmul(pO[:, h, :], lhsT=aqkT[:, h, :], rhs=vnew[:, h, :], start=False, stop=True), r=[aqkT, vnew], w=[pO])
        yield
        kb.op('act', lambda E: E.copy(out=osb[:], in_=pO[:]), r=[pO], w=[osb])
        kb.op('pool', lambda E: E.tensor_tensor(out=sq[:], in0=osb[:], in1=osb[:], op=ALU.mult), r=[osb], w=[sq])
        kb.op('dve', lambda E: E.tensor_reduce(out=G[:, 0:4], in_=sq[:], axis=AX.X, op=ALU.add), r=[sq, G], w=[G])
        kb.op('act', lambda E: E.activation(out=G[:, 0:4], in_=G[:, 0:4], func=AF.Sqrt, scale=1.0 / 128, bias=EPS), r=[G], w=[G])
        kb.op('dve', lambda E: E.reciprocal(out=G[:, 0:4], in_=G[:, 0:4]), r=[G], w=[G])
        yield
        kb.op('act', lambda E: E.activation(out=sgt[:], in_=gt[b][:], func=AF.Silu), r=[gt[b]], w=[sgt])
        kb.op('dve', lambda E: E.tensor_tensor(out=osb[:], in0=osb[:], in1=B4(G[:, 0:4]), op=ALU.mult), r=[osb, G], w=[osb])
        kb.op('pool', lambda E: E.tensor_tensor(out=osb[:], in0=osb[:], in1=M4(gw[:]), op=ALU.mult), r=[osb, gw], w=[osb])
        kb.op('dve', lambda E: E.tensor_tensor(out=oa[b][:], in0=osb[:], in1=sgt[:].rearrange("p (h d) -> p h d", h=4), op=ALU.mult),
              r=[osb, sgt], w=[oa[b]])
        for h in range(4):
            kb.op('pe', lambda E: E.transpose(out=pB[0][:, h, :], in_=oa[b][:, h, :], identity=ident[:]), r=[oa[b], ident], w=[pB[0]])
        kb.op('act', lambda E: E.copy(out=oaT[b][:], in_=pB[0][:, 0:4, :]), r=[pB[0]], w=[oaT[b]])
        kb.dma('sp', io['ocatT'][t, :, 0:4, :], oaT[b][:], r=[oaT[b]], w=['ocatT_a'])

    def to_boundary(g):
        for r in g:
            if r == 'B':
                return

    cur = tile(0)
    to_boundary(cur)
    for t in range(NT):
        nxt = tile(t + 1) if t + 1 < NT else None
        cur_done, nxt_done = False, nxt is None
        while not (cur_done and nxt_done):
            if not cur_done:
                try:
                    next(cur)
                except StopIteration:
                    cur_done = True
            if not nxt_done:
                if next(nxt) == 'B':
                    nxt_done = True
        cur = nxt
    ph.close()


def rope16(kb, R, G, cs, tmp, e1='dve', e2='pool'):
    c = cs[:, 0:8].unsqueeze(1).to_broadcast([128, G, 8])
    sn = cs[:, 8:16].unsqueeze(1).to_broadcast([128, G, 8])
    x1 = R[:, 0:G, 0:8]
    x2 = R[:, 0:G, 8:16]
    kb.op(e1, lambda E: E.tensor_tensor(out=tmp[:, 0:G, 0:8], in0=x1, in1=c, op=ALU.mult), r=[R, cs], w=[(tmp, 0)])
    yield
    kb.op(e2, lambda E: E.tensor_tensor(out=tmp[:, 0:G, 8:16], in0=x2, in1=sn, op=ALU.mult), r=[R, cs], w=[(tmp, 1)])
    yield
    kb.op(e1, lambda E: E.tensor_tensor(out=tmp[:, 0:G, 16:24], in0=x2, in1=c, op=ALU.mult), r=[R, cs], w=[(tmp, 2)])
    yield
    kb.op(e2, lambda E: E.tensor_tensor(out=tmp[:, 0:G, 24:32], in0=x1, in1=sn, op=ALU.mult), r=[R, cs], w=[(tmp, 3)])
    yield
    kb.op(e1, lambda E: E.tensor_tensor(out=x1, in0=tmp[:, 0:G, 0:8], in1=tmp[:, 0:G, 8:16], op=ALU.subtract),
          r=[(tmp, 0), (tmp, 1), (tmp, 2), (tmp, 3)], w=[R])
    yield
    kb.op(e1, lambda E: E.tensor_tensor(out=x2, in0=tmp[:, 0:G, 16:24], in1=tmp[:, 0:G, 24:32], op=ALU.add),
          r=[(tmp, 0), (tmp, 1), (tmp, 2), (tmp, 3)], w=[R])
    yield


def rms_groups(kb, src3, G, dst3, sq, ss, wt, e_sq='pool'):
    (src_ap, src_keys) = src3
    (dst_ap, dst_keys) = dst3
    kb.op(e_sq, lambda E: E.tensor_tensor(out=sq[:, 0:G, :], in0=src_ap, in1=src_ap, op=ALU.mult), r=src_keys, w=[sq])
    yield
    kb.op('dve', lambda E: E.tensor_reduce(out=ss[:, 0:G], in_=sq[:, 0:G, :], axis=AX.X, op=ALU.add), r=[sq], w=[ss])
    yield
    kb.op('act', lambda E: E.activation(out=ss[:, 0:G], in_=ss[:, 0:G], func=AF.Sqrt, scale=1.0 / 64, bias=EPS), r=[ss], w=[ss])
    yield
    kb.op('dve', lambda E: E.reciprocal(out=ss[:, 0:G], in_=ss[:, 0:G]), r=[ss], w=[ss])
    yield
    kb.op('dve', lambda E: E.tensor_tensor(out=dst_ap, in0=src_ap, in1=ss[:, 0:G].unsqueeze(2).to_broadcast([128, G, 64]), op=ALU.mult),
          r=src_keys + [ss], w=dst_keys)
    yield
    kb.op('pool', lambda E: E.tensor_tensor(out=dst_ap, in0=dst_ap, in1=wt[:].unsqueeze(1).to_broadcast([128, G, 64]), op=ALU.mult),
          r=dst_keys + [wt], w=dst_keys)
    yield


def phase_nsa(kb, io):
    nc = kb.nc
    ph = Phase(kb, "ns")
    ident = ph.sb("ident", [128, 128], BF16)
    tril = ph.sb("tril", [128, 128], BF16)
    far = ph.sb("far", [128, 128], BF16)
    cmask = ph.sb("cmask", [128, 9, 512], BF16)
    kvT = ph.sb("kvT", [128, 4, S], BF16)
    vs1 = ph.sb("vs1", [128, NT, 2, 65], BF16)
    vw1 = ph.sb("vw1", [128, NT, 2, 65], BF16)
    kcmpT = ph.sb("kcmpT", [64, 2, 256], BF16)
    rhs_cmp = ph.sb("rhs_cmp", [128, 2, 2, 128], BF16)
    kb.dma('sp', ident[:], io['c_ident'][:, :], w=[ident])
    kb.dma('sp', tril[:], io['c_tril'][:, :], w=[tril])
    kb.dma('sp', far[:], io['c_far'][:, :], w=[far])
    for m in range(9):
        kb.dma('sp', cmask[:, m, :], io['c_cmask'][m, :, :], w=[cmask])
    for g in range(2):
        kb.dma('sp', kvT[64:128, g, :], io['c_E'][:, :], w=[(kvT, 'E')])
    kb.op('pool', lambda E: E.memset(vs1[:], 1.0), w=[vs1])
    kb.op('pool', lambda E: E.memset(vw1[:], 1.0), w=[vw1])
    kb.op('pool', lambda E: E.memset(kcmpT[:], 0.0), w=[kcmpT])
    kb.op('pool', lambda E: E.memset(rhs_cmp[:], 0.0), w=[rhs_cmp])
    for bt in range(2):
        for g in range(2):
            kb.dma('sp', rhs_cmp[:, bt, g, 64:128], io['c_ovl'][:, bt, :], r=[rhs_cmp], w=[rhs_cmp])

    pp = Phase(kb, "np")
    kcT = pp.sb("kcT", [64, 4, S], BF16)
    ksw = pp.sb("ksw", [128, 64], F32)
    kww = pp.sb("kww", [128, 64], F32)
    kcw = pp.sb("kcw", [128, 64], F32)
    kb.dma('sp', ksw[:], io['nsa_ks_norm_w'].partition_broadcast(128), w=[ksw])
    kb.dma('sp', kww[:], io['nsa_kw_norm_w'].partition_broadcast(128), w=[kww])
    kb.dma('sp', kcw[:], io['nsa_kc_norm_w'].partition_broadcast(128), w=[kcw])
    kvb = pp.sbn("kvb", [128, 768], F32, 2)
    cst = pp.sbn("cst", [128, 16], F32, 2)
    R = pp.sbn("R", [128, 6, 64], F32, 2)
    sqb = pp.sbn("sq", [128, 2, 64], F32, 2)
    ssb = pp.sbn("ss", [128, 2], F32, 2)
    tmpb = pp.sbn("tmp", [128, 6, 32], F32, 2)
    sq, ss = sqb[0], ssb[0]
    k16 = pp.sbn("k16", [128, 8, 64], BF16, 2)
    pT8 = pp.psn("pT8", [128, 8, 128], BF16, 2)
    def kvtile(t):
        b = t % 2
        kv = kvb[b]
        Rb = R[b]
        sq, ss, tmp = sqb[b], ssb[b], tmpb[b]
        kb.dma('sp', kv[:], io['tm'][t * 128:(t + 1) * 128, KC_OFF:KC_OFF + 768], r=['tm'], w=[kv])
        kb.dma('sp', cst[b][:], io['c_rope'][t * 128:(t + 1) * 128, :], w=[cst[b]])
        v3 = lambda off: kv[:, off:off + 128].rearrange("p (g d) -> p g d", g=2)
        kb.op('pool', lambda E: E.tensor_copy(out=Rb[:, 0:2, :], in_=v3(0)), r=[kv], w=[Rb])
        yield
        yield from rms_groups(kb, (v3(256), [kv]), 2, (Rb[:, 2:4, :], [Rb]), sq, ss, ksw)
        yield from rms_groups(kb, (v3(512), [kv]), 2, (Rb[:, 4:6, :], [Rb]), sq, ss, kww)
        yield from rope16(kb, Rb, 6, cst[b], tmp)
        kk = k16[b]
        kb.op('act', lambda E: E.copy(out=kk[:, 0:6, :], in_=Rb[:]), r=[Rb], w=[kk])
        kb.op('pool', lambda E: E.tensor_copy(out=kk[:, 6:8, :], in_=v3(128)), r=[kv], w=[kk])
        kb.op('dve', lambda E: E.tensor_copy(out=vs1[:, t, :, 0:64], in_=v3(384)), r=[kv], w=[vs1])
        kb.op('pool', lambda E: E.tensor_copy(out=vw1[:, t, :, 0:64], in_=v3(640)), r=[kv], w=[vw1])
        yield
        p8 = pT8[b]
        for i in range(8):
            kb.op('pe', lambda E: E.transpose(out=p8[0:64, i, :], in_=kk[:, i, :], identity=ident[:]), r=[kk, ident], w=[p8])
        kb.op('act', lambda E: E.copy(out=kvT[0:64, :, t * 128:(t + 1) * 128], in_=p8[0:64, 2:6, :]), r=[p8], w=[(kvT, 'k')])
        kb.op('dve', lambda E: E.tensor_copy(out=kcT[0:64, 0:2, t * 128:(t + 1) * 128], in_=p8[0:64, 0:2, :]), r=[p8], w=[kcT])
        kb.op('dve', lambda E: E.tensor_copy(out=kcT[0:64, 2:4, t * 128:(t + 1) * 128], in_=p8[0:64, 6:8, :]), r=[p8], w=[kcT])

    def interleave(gens):
        gens = list(gens)
        while gens:
            for g_ in list(gens):
                try:
                    next(g_)
                except StopIteration:
                    gens.remove(g_)

    for t in range(0, NT, 2):
        interleave([kvtile(t), kvtile(t + 1)])
    w1f = pp.sb("w1f", [64, 32, 64], F32)
    w1b = pp.sbn("w1b", [64, 32, 64], BF16, 2)
    w2f = pp.sb("w2f", [64, 64], F32)
    w2b = pp.sbn("w2b", [64, 64], BF16, 2)
    posf = pp.sb("posf", [64, 32], F32)
    pos2 = pp.sbn("pos2", [64, 32, 2], BF16, 2)
    bias = pp.sb("bias", [64, 2], F32)
    h1T = pp.sb("h1T", [64, 256], BF16)
    o2 = pp.sb("o2", [128, 1, 64], F32)
    o2n = pp.sb("o2n", [128, 1, 64], F32)
    kcn = pp.sb("kcn", [128, 64], BF16)
    pH = pp.ps("pH", [128, 512], F32)
    pB_ = pp.ps("pBi", [128, 512], F32)
    pO2 = pp.ps("pO2", [128, 512], F32)
    kb.op('pool', lambda E: E.memset(h1T[:], 0.0), w=[h1T])
    for kind, (n1, n2, npos) in enumerate((('nsa_cmp_k_w1', 'nsa_cmp_k_w2', 'nsa_cmp_pos_k'), ('nsa_cmp_v_w1', 'nsa_cmp_v_w2', 'nsa_cmp_pos_v'))):
        kb.dma('sp', w1f[:], io[n1].rearrange("(l d) o -> d l o", d=64), w=[w1f])
        kb.op('dve', lambda E: E.tensor_copy(out=w1b[kind][:], in_=w1f[:]), r=[w1f], w=[w1b[kind]])
        kb.dma('sp', w2f[:], io[n2][:, :], w=[w2f])
        kb.op('dve', lambda E: E.tensor_copy(out=w2b[kind][:], in_=w2f[:]), r=[w2f], w=[w2b[kind]])
        kb.dma('sp', posf[:], io[npos].rearrange("l d -> d l"), w=[posf], allow_slow_non_contiguous=True)
        for j in range(2):
            kb.op('dve', lambda E: E.tensor_copy(out=pos2[kind][:, :, j], in_=posf[:]), r=[posf], w=[pos2[kind]])
        for l in range(32):
            kb.op('pe', lambda E: E.matmul(pB_[0:64, 0:2], lhsT=w1b[kind][:, l, :], rhs=pos2[kind][:, l, :], start=(l == 0), stop=(l == 31)),
                  r=[w1b[kind], pos2[kind]], w=[pB_])
        kb.op('dve', lambda E: E.tensor_copy(out=bias[:], in_=pB_[0:64, 0:2]), r=[pB_], w=[bias])
        for g in range(2):
            ki = kind * 2 + g
            for l in range(32):
                kb.op('pe', lambda E: E.matmul(pH[0:64, 0:255], lhsT=w1b[kind][:, l, :], rhs=kcT[0:64, ki, l:l + 16 * 254 + 1:16],
                                               start=(l == 0), stop=(l == 31)), r=[w1b[kind], kcT], w=[pH])
            kb.op('act', lambda E: E.activation(out=h1T[:, 0:255], in_=pH[0:64, 0:255], func=AF.Silu, bias=bias[:, 0:1]),
                  r=[pH, bias], w=[h1T])
            for bt in range(2):
                kb.op('pe', lambda E: E.matmul(pO2[:, 0:64], lhsT=h1T[:, bt * 128:(bt + 1) * 128], rhs=w2b[kind][:], start=True, stop=True),
                      r=[h1T, w2b[kind]], w=[pO2])
                if kind == 0:
                    kb.op('act', lambda E: E.copy(out=o2[:, 0, :], in_=pO2[:, 0:64]), r=[pO2], w=[o2])
                    for _ in rms_groups(kb, (o2[:], [o2]), 1, (o2n[:], [o2n]), sq, ss, kcw):
                        pass
                    kb.op('act', lambda E: E.copy(out=kcn[:], in_=o2n[:, 0, :]), r=[o2n], w=[kcn])
                    p8 = pT8[0]
                    kb.op('pe', lambda E: E.transpose(out=p8[0:64, 0, :], in_=kcn[:], identity=ident[:]), r=[kcn, ident], w=[p8])
                    kb.op('act', lambda E: E.copy(out=kcmpT[:, g, bt * 128:(bt + 1) * 128], in_=p8[0:64, 0, :]), r=[p8], w=[kcmpT])
                else:
                    kb.op('act', lambda E: E.copy(out=rhs_cmp[:, bt, g, 0:64], in_=pO2[:, 0:64]), r=[pO2], w=[rhs_cmp])
    pp.close()

    pa = Phase(kb, "na")
    NPT = 8
    PTb = pa.sbn("PT", [128, 512], BF16, NPT)
    pti = [0]
    qaug = pa.sbn("qaug", [128, 8, 512], BF16, 2)
    qnw = pa.sb("qnw", [128, 64], F32)
    kb.dma('sp', qnw[:], io['nsa_q_norm_w'].partition_broadcast(128), w=[qnw])
    kb.op('dve', lambda E: E.tensor_scalar(out=qnw[:], in0=qnw[:], scalar1=0.125, scalar2=None, op0=ALU.mult), r=[qnw], w=[qnw])
    qf = pa.sbn("qf", [128, 512], F32, 4)
    cst = pa.sbn("cst", [128, 16], F32, 4)
    Rqb = pa.sbn("Rq", [128, 8, 64], F32, 4)
    sqb = pa.sbn("sq", [128, 8, 64], F32, 4)
    ssb = pa.sbn("ss", [128, 8], F32, 4)
    tmpb = pa.sbn("tmp", [128, 8, 32], F32, 4)
    qa = pa.sbn("qa", [128, 8, 128], BF16, 4)
    gts = pa.sb("gts", [128, 4, 24], F32)
    Ab = pa.sbn("Ab", [128, 64], F32, 4)
    Bb = pa.sbn("Bb", [128, 64], F32, 4)
    ob = pa.sb("ob", [128, 4, 8, 64], F32)
    impacc = pa.sb("impacc", [128, 4, 64], F32)
    tmpi = pa.sb("tmpi", [128, 4, 64], F32)
    tmpo = pa.sb("tmpo", [128, 4, 64], F32)
    scr = pa.sb("scr", [128, 64], F32)
    scr2 = pa.sb("scr2", [128, 64], F32)
    m8 = pa.sb("m8", [128, 16], F32)
    nst = pa.sbn("nst", [128, 128], BF16, 2)
    fs = pa.sb("fs", [128, 12], F32)
    obb = pa.sbn("obb", [128, 512], BF16, 2)
    obT = pa.sbn("obT", [128, 4, 128], BF16, 2)
    sc_ps = pa.psn("sc", [128, 512], F32, 3)
    oC = pa.ps("oC", [128, 4, 128], F32)
    oW = pa.psn("oW", [128, 4, 128], F32, 1)
    oS = pa.psn("oS", [128, 4, 128], F32, 2)
    pTr = pa.psn("pTr", [128, 8, 128], BF16, 1)
    for i in range(4):
        kb.op('pool', lambda E: E.memset(qa[i][:], 0.0), w=[qa[i]])
    for i in range(2):
        kb.op('pool', lambda E: E.memset(nst[i][:], 0.0), w=[nst[i]])
    tri = 0

    def finalize(oX, h, br, first, cmp=False):
        if cmp:
            kb.op('dve', lambda E: E.tensor_reduce(out=fs[:, 0:4], in_=oX[:, :, 64:128], axis=AX.X, op=ALU.add), r=[oX], w=[fs])
            kb.op('dve', lambda E: E.tensor_scalar(out=fs[:, 0:4], in0=fs[:, 0:4], scalar1=0.5, scalar2=1e-30, op0=ALU.mult, op1=ALU.add),
                  r=[fs], w=[fs])
        else:
            kb.op('dve', lambda E: E.tensor_scalar(out=fs[:, 0:4], in0=oX[:, :, 64], scalar1=1e-30, scalar2=None, op0=ALU.add), r=[oX], w=[fs])
        kb.op('dve', lambda E: E.reciprocal(out=fs[:, 4:8], in_=fs[:, 0:4]), r=[fs], w=[fs])
        if cmp:
            dsti = impacc if h % 4 == 0 else tmpi
            kb.op('dve', lambda E: E.tensor_tensor(out=dsti[:], in0=oX[:, :, 64:128], in1=fs[:, 4:8].unsqueeze(2).to_broadcast([128, 4, 64]),
                                                   op=ALU.mult), r=[oX, fs], w=[dsti])
            if h % 4 != 0:
                kb.op('pool', lambda E: E.tensor_tensor(out=impacc[:], in0=impacc[:], in1=tmpi[:], op=ALU.add), r=[impacc, tmpi], w=[impacc])
        kb.op('dve', lambda E: E.tensor_tensor(out=fs[:, 8:12], in0=fs[:, 4:8], in1=gts[:, :, h * 3 + br], op=ALU.mult), r=[fs, gts], w=[fs])
        dst = ob[:, :, h, :] if first else tmpo[:]
        kb.op('dve', lambda E: E.tensor_tensor(out=dst, in0=oX[:, :, 0:64], in1=fs[:, 8:12].unsqueeze(2).to_broadcast([128, 4, 64]),
                                               op=ALU.mult), r=[oX, fs], w=[(ob, h) if first else tmpo])
        if not first:
            kb.op('pool', lambda E: E.tensor_tensor(out=ob[:, :, h, :], in0=ob[:, :, h, :], in1=tmpo[:], op=ALU.add),
                  r=[(ob, h), tmpo], w=[(ob, h)])

    for s in range(S // 512):
        qs_ = qaug[s % 2]
        def qtile(t4, s=s, qs_=qs_):
            t = 4 * s + t4
            b = t4
            q = qf[b]
            Rq, sq, ss, tmp = Rqb[b], sqb[b], ssb[b], tmpb[b]
            kb.dma('sp', q[:], io['tm'][t * 128:(t + 1) * 128, NQ_OFF:NQ_OFF + 512], r=['tm'], w=[q])
            kb.dma('sp', gts[:, t4, :], io['tm'][t * 128:(t + 1) * 128, NG_OFF:NG_OFF + 24], r=['tm'], w=[gts])
            kb.dma('sp', cst[b][:], io['c_rope'][t * 128:(t + 1) * 128, :], w=[cst[b]])
            kb.dma('sp', Ab[t4][:], io['c_A'][t, :, :], w=[Ab[t4]])
            kb.dma('sp', Bb[t4][:], io['c_B'][t, :, :], w=[Bb[t4]])
            kb.op('act', lambda E: E.activation(out=gts[:, t4, :], in_=gts[:, t4, :], func=AF.Exp, scale=-1.0), r=[gts], w=[gts])
            kb.op('dve', lambda E: E.tensor_scalar(out=gts[:, t4, :], in0=gts[:, t4, :], scalar1=1.0, scalar2=None, op0=ALU.add), r=[gts], w=[gts])
            kb.op('dve', lambda E: E.reciprocal(out=gts[:, t4, :], in_=gts[:, t4, :]), r=[gts], w=[gts])
            q3 = q[:].rearrange("p (h d) -> p h d", h=8)
            yield
            yield from rms_groups(kb, (q3, [q]), 8, (Rq[:], [Rq]), sq, ss, qnw)
            yield from rope16(kb, Rq, 8, cst[b], tmp)
            qab = qa[b]
            kb.op('act', lambda E: E.copy(out=qab[:, :, 0:64], in_=Rq[:]), r=[Rq], w=[qab])
            yield
            pt_ = pTr[0]
            for h in range(8):
                kb.op('pe', lambda E: E.transpose(out=pt_[:, h, :], in_=qab[:, h, :], identity=ident[:]), r=[qab, ident], w=[pt_])
            kb.op('act', lambda E: E.copy(out=qs_[:, :, t4 * 128:(t4 + 1) * 128], in_=pt_[:]), r=[pt_], w=[qs_])
        interleave([qtile(i) for i in range(4)])
        nbt = 1 if s < 4 else 2
        for g in range(2):
            for h in range(4 * g, 4 * g + 4):
                specs = []
                for bt in range(nbt):
                    m = (s if s <= 4 else None) if bt == 0 else 5 + (s - 4)
                    masks = [] if m is None else [(0, 512, cmask[:, m, :])]
                    specs.append(dict(lhsT=kcmpT[0:64, g, bt * 128:(bt + 1) * 128],
                                      rhs_fn=lambda q0, q1, h=h: qs_[0:64, h, q0 * 128:q1 * 128], qt0=0, qt1=4, masks=masks, mk=[cmask],
                                      v_fn=lambda qt, bt=bt, g=g: rhs_cmp[:, bt, g, :], nk=128, rk=[kcmpT], rv=[rhs_cmp]))
                attn_block(kb, lambda qt: (oC[:, qt, :], oC, 'C'), PTb, pti, sc_ps, specs, ident, [qs_])
                finalize(oC, h, 0, True, cmp=True)
            for qt in range(4):
                ns_ = nst[qt % 2]
                kb.op('dve', lambda E: E.tensor_tensor(out=scr[:], in0=impacc[:, qt, :], in1=Ab[qt][:], op=ALU.mult), r=[impacc, Ab[qt]], w=[scr])
                kb.op('dve', lambda E: E.tensor_tensor(out=scr[:], in0=scr[:], in1=Bb[qt][:], op=ALU.add), r=[scr, Bb[qt]], w=[scr])
                kb.op('dve', lambda E: E.max(out=m8[:, 0:8], in_=scr[:]), r=[scr], w=[(m8, 0)])
                kb.op('dve', lambda E: E.match_replace(out=scr2[:], in_to_replace=m8[:, 0:8], in_values=scr[:], imm_value=-1e30),
                      r=[scr, (m8, 0)], w=[scr2])
                kb.op('dve', lambda E: E.max(out=m8[:, 8:16], in_=scr2[:]), r=[scr2], w=[(m8, 1)])
                kb.op('dve', lambda E: E.tensor_scalar(out=ns_[:, 64:128], in0=scr[:], scalar1=m8[:, 15:16], scalar2=1.0, op0=ALU.is_ge,
                                                       op1=ALU.subtract), r=[scr, (m8, 1)], w=[ns_])
                pt_ = pTr[0]
                tri += 1
                kb.op('pe', lambda E: E.transpose(out=pt_[:, 0, :], in_=ns_[:], identity=ident[:]), r=[ns_, ident], w=[pt_])
                for h in range(4 * g, 4 * g + 4):
                    if h % 2 == 0:
                        kb.op('act', lambda E: E.copy(out=qs_[64:128, h, qt * 128:(qt + 1) * 128], in_=pt_[64:128, 0, :]), r=[pt_], w=[qs_])
                    else:
                        kb.op('dve', lambda E: E.tensor_copy(out=qs_[64:128, h, qt * 128:(qt + 1) * 128], in_=pt_[64:128, 0, :]), r=[pt_], w=[qs_])
        for h in range(8):
            g = h // 4
            specs = []
            for kt in range(max(0, 4 * s - 4), 4 * s + 4):
                lo = max(kt - 4 * s, 0)
                hi = min(kt + 4 - 4 * s, 3)
                masks = []
                if kt >= 4 * s:
                    masks.append(((kt - 4 * s - lo) * 128, 128, tril[:]))
                if kt + 4 <= 4 * s + 3:
                    masks.append(((kt + 4 - 4 * s - lo) * 128, 128, far[:]))
                specs.append(dict(lhsT=kvT[0:64, 2 + g, kt * 128:(kt + 1) * 128],
                                  rhs_fn=lambda q0, q1, h=h: qs_[0:64, h, q0 * 128:q1 * 128], qt0=lo, qt1=hi + 1, masks=masks, mk=[tril, far],
                                  v_fn=lambda qt, kt=kt, g=g: vw1[:, kt, g, :], nk=128, rk=[(kvT, 'k')], rv=[vw1]))
            oW_ = oW[0]
            attn_block(kb, lambda qt: (oW_[:, qt, 0:65], oW_, 'W'), PTb, pti, sc_ps, specs, ident, [qs_], LA=3)
            finalize(oW_, h, 2, False)
            specs = []
            for kt in range(0, 4 * s + 4):
                lo = max(kt - 4 * s, 0)
                masks = [(0, 128, tril[:])] if kt >= 4 * s else []
                specs.append(dict(lhsT=kvT[:, g, kt * 128:(kt + 1) * 128],
                                  rhs_fn=lambda q0, q1, h=h: qs_[:, h, q0 * 128:q1 * 128], qt0=lo, qt1=4, masks=masks, mk=[tril],
                                  v_fn=lambda qt, kt=kt, g=g: vs1[:, kt, g, :], nk=128, rk=[(kvT, 'k'), (kvT, 'E')], rv=[vs1]))
            oS_ = oS[h % 2]
            attn_block(kb, lambda qt: (oS_[:, qt, 0:65], oS_, 'S'), PTb, pti, sc_ps, specs, ident, [qs_], LA=3)
            finalize(oS_, h, 1, False)
        for qt in range(4):
            t = 4 * s + qt
            b = t % 2
            kb.op('act', lambda E: E.copy(out=obb[b][:], in_=ob[:, qt, :, :].rearrange("p h d -> p (h d)")), r=[(ob, h) for h in range(8)], w=[obb[b]])
            pt_ = pTr[0]
            tri += 1
            for c in range(4):
                kb.op('pe', lambda E: E.transpose(out=pt_[:, c, :], in_=obb[b][:, c * 128:(c + 1) * 128], identity=ident[:]), r=[obb[b], ident], w=[pt_])
            kb.op('act', lambda E: E.copy(out=obT[b][:], in_=pt_[:, 0:4, :]), r=[pt_], w=[obT[b]])
            kb.dma('sp', io['ocatT'][t, :, 4:8, :], obT[b][:], r=[obT[b]], w=['ocatT_b'])
    pa.close()
    ph.close()


def phase_ffn(kb, io):
    nc = kb.nc
    ph = Phase(kb, "f1")
    ident = ph.sb("ident", [128, 128], BF16)
    kb.dma('sp', ident[:], io['c_ident'][:, :], w=[ident])
    woutb = ph.sb("woutb", [128, 12, D], BF16)
    wupb = ph.sb("wupb", [128, 8, 2 * FF], BF16)
    gam = ph.sb("gam", [128, 8], F32)
    cw = ph.sb("cw", [128, 3, 44], F32)
    kb.dma('sp', gam[:], io['ffn_norm_w'].rearrange("(c p) -> p c", p=128), w=[gam], allow_slow_non_contiguous=True)
    for j in range(3):
        kb.dma('sp', cw[:, j, :], io['ffn_conv_w'][j, :].rearrange("(c p) -> p c", p=128), w=[cw], allow_slow_non_contiguous=True)
    load_cast_weight(kb, ph, woutb, io['w_out'], 12, D)
    load_cast_weight(kb, ph, wupb, io['ffn_w_up'], 8, 2 * FF, gam=gam, stage_cols=1408)

    oT = ph.sbn("oT", [128, 12, 128], BF16, 2)
    xt = ph.sbn("xt", [128, D], F32, 2)
    hs = ph.sbn("hs", [128, D], F32, 2)
    junk = ph.sb("junk", [128, D], BF16)
    ss = ph.sbn("ss", [128, 1], F32, 2)
    rs = ph.sbn("rs", [128, 1], F32, 2)
    hn = ph.sbn("hn", [128, D], BF16, 2)
    hnT = ph.sbn("hnT", [128, 8, 512], BF16, 2)
    ug = ph.sbn("ug", [128, 514], F32, 2)
    uv = ph.sbn("uv", [128, 514], F32, 2)
    ag = ph.sbn("ag", [128, 512], F32, 2)
    av = ph.sbn("av", [128, 512], F32, 2)
    sg = ph.sbn("sg", [128, 512], F32, 2)
    act = ph.sbn("act", [128, 512], BF16, 3)
    halo = ph.sb("halo", [128, 44, 2], F32)
    pH = ph.psn("pH", [128, 512], F32, 2)
    pT = ph.ps("pT", [128, 8, 128], BF16)
    pU = ph.psn("pU", [128, 512], F32, 4)
    kb.op('pool', lambda E: E.memset(halo[:], 0.0), w=[halo])

    def conv3(ps_, dst, src, fb):
        kb.op('act', lambda E: E.activation(out=dst[:], in_=ps_[:], func=AF.Copy, scale=cw[:, 2, fb:fb + 1]), r=[ps_, cw], w=[dst])
        for j in range(2):
            kb.op('dve', lambda E: E.scalar_tensor_tensor(out=dst[:], in0=src[:, j:j + 512], scalar=cw[:, j, fb:fb + 1], in1=dst[:],
                                                       op0=ALU.mult, op1=ALU.add), r=[(src, 'b'), (src, 'h'), cw, dst], w=[dst])

    def ftiles(s):
        hT = hnT[s % 2]
        for t4 in range(4):
            t = s * 4 + t4
            b = t % 2
            kb.dma('sp', oT[b][:], io['ocatT'][t, :, :, :],
                   r=['ocatT_a', 'ocatT_b', 'ocatT_c'], w=[oT[b]])
            kb.dma('sp', xt[b][:], io['x'][t * 128:(t + 1) * 128, :], w=[xt[b]])
            yield
            for half in range(2):
                p = pH[half]
                for c in range(12):
                    kb.op('pe', lambda E: E.matmul(p[:], lhsT=oT[b][:, c, :], rhs=woutb[:, c, half * 512:(half + 1) * 512],
                                                   start=(c == 0), stop=(c == 11)), r=[oT[b], (woutb, c)], w=[p])
                kb.op('dve', lambda E: E.tensor_tensor(out=hs[b][:, half * 512:(half + 1) * 512], in0=p[:],
                                                       in1=xt[b][:, half * 512:(half + 1) * 512], op=ALU.add),
                      r=[p, xt[b]], w=[(hs[b], half)])
            yield
            kb.dma('pool', io['h_s'][t * 128:(t + 1) * 128, :], hs[b][:], r=[(hs[b], 0), (hs[b], 1)], w=['h_s'])
            rms_rstd(kb, hs[b][:], junk, ss[b], rs[b], D, [(hs[b], 0), (hs[b], 1)])
            kb.op('act', lambda E: E.activation(out=hn[b][:], in_=hs[b][:], func=AF.Copy, scale=rs[b][:, 0:1]),
                  r=[(hs[b], 0), (hs[b], 1), rs[b]], w=[hn[b]])
            yield
            yield
            for c in range(8):
                kb.op('pe', lambda E: E.transpose(out=pT[:, c, :], in_=hn[b][:, c * 128:(c + 1) * 128], identity=ident[:]),
                      r=[hn[b], ident], w=[pT])
            kb.op('act', lambda E: E.copy(out=hT[:, :, t4 * 128:(t4 + 1) * 128], in_=pT[:]), r=[pT], w=[hT])
            yield

    def fup(s):
        hT = hnT[s % 2]
        for fb in range(22):
            k2 = fb % 2
            pg = pU[2 * k2]
            pv = pU[2 * k2 + 1]
            for (p_, f0) in ((pg, fb), (pv, 22 + fb)):
                for c in range(8):
                    kb.op('pe', lambda E: E.matmul(p_[:], lhsT=wupb[:, c, f0 * 128:(f0 + 1) * 128], rhs=hT[:, c, :],
                                                   start=(c == 0), stop=(c == 7)), r=[hT, (wupb, c)], w=[p_])
            yield
            g_, v_ = ug[k2], uv[k2]
            kb.op('act', lambda E: E.copy(out=g_[:, 2:514], in_=pg[:]), r=[pg], w=[(g_, 'b')])
            kb.op('act', lambda E: E.copy(out=v_[:, 2:514], in_=pv[:]), r=[pv], w=[(v_, 'b')])
            for (u_, f0) in ((g_, fb), (v_, 22 + fb)):
                kb.op('dve', lambda E: E.tensor_copy(out=u_[:, 0:2], in_=halo[:, f0, :]), r=[(halo, f0)], w=[(u_, 'h')])
                kb.op('dve', lambda E: E.tensor_copy(out=halo[:, f0, :], in_=u_[:, 512:514]), r=[(u_, 'b')], w=[(halo, f0)])
            conv3(pg, ag[k2], g_, fb)
            conv3(pv, av[k2], v_, 22 + fb)
            kb.op('act', lambda E: E.activation(out=sg[k2][:], in_=ag[k2][:], func=AF.Silu), r=[ag[k2]], w=[sg[k2]])
            a_ = act[fb % 3]
            kb.op('dve', lambda E: E.tensor_tensor(out=a_[:], in0=sg[k2][:], in1=av[k2][:], op=ALU.mult), r=[sg[k2], av[k2]], w=[a_])
            kb.dma('pool', io['actT'][4 * s:4 * s + 4, :, fb, :].rearrange("j p t -> p j t"), a_[:].rearrange("p (j t) -> p j t", j=4), r=[a_], w=['actT'])
            yield

    def interleave(gens):
        gens = list(gens)
        while gens:
            for g_ in list(gens):
                try:
                    next(g_)
                except StopIteration:
                    gens.remove(g_)

    NS = S // 512
    interleave([ftiles(0)])
    for s in range(NS):
        gl = [fup(s)]
        if s + 1 < NS:
            gl.append(ftiles(s + 1))
        interleave(gl)
    ph.close()

    ph = Phase(kb, "f2")
    wdnb = ph.sb("wdnb", [128, 22, D], BF16)
    load_cast_weight(kb, ph, wdnb, io['ffn_w_down'], 22, D)
    aT = ph.sbn("aT", [128, 22, 128], BF16, 3)
    hs = ph.sbn("hs", [128, D], F32, 3)
    ot = ph.sbn("ot", [128, D], F32, 2)
    pD = ph.psn("pD", [128, 512], F32, 4)

    def ld(t):
        kb.dma('sp', aT[t % 3][:], io['actT'][t, :, :, :], r=['actT'], w=[aT[t % 3]])
        kb.dma('sp', hs[t % 3][:], io['h_s'][t * 128:(t + 1) * 128, :], r=['h_s'], w=[hs[t % 3]])

    ld(0)
    ld(1)
    for t in range(NT):
        b = t % 2
        a_, h_ = aT[t % 3], hs[t % 3]
        if t + 2 < NT:
            ld(t + 2)
        for half in range(2):
            p = pD[2 * b + half]
            for c in range(22):
                kb.op('pe', lambda E: E.matmul(p[:], lhsT=a_[:, c, :], rhs=wdnb[:, c, half * 512:(half + 1) * 512],
                                               start=(c == 0), stop=(c == 21)), r=[a_, (wdnb, c)], w=[p])
            kb.op('dve', lambda E: E.tensor_tensor(out=ot[b][:, half * 512:(half + 1) * 512], in0=p[:],
                                                   in1=h_[:, half * 512:(half + 1) * 512], op=ALU.add),
                  r=[p, h_], w=[(ot[b], half)])
        kb.dma('pool', io['out'][t * 128:(t + 1) * 128, :], ot[b][:], r=[(ot[b], 0), (ot[b], 1)], w=['out'])
    ph.close()


W_NAMES = ['attn_norm_w', 'mem_norm_w', 'w_in', 'gdn_conv_w', 'gdn_a_log', 'gdn_dt_bias', 'gdn_out_norm_w',
           'nsa_q_norm_w', 'nsa_kc_norm_w', 'nsa_ks_norm_w', 'nsa_kw_norm_w', 'nsa_cmp_pos_k', 'nsa_cmp_pos_v',
           'nsa_cmp_k_w1', 'nsa_cmp_k_w2', 'nsa_cmp_v_w1', 'nsa_cmp_v_w2', 'mem_w_kv', 'mem_q_norm_w', 'mem_k_norm_w',
           'w_out', 'ffn_norm_w', 'ffn_w_up', 'ffn_conv_w', 'ffn_w_down']
W_SHAPES = {
    'attn_norm_w': [D], 'mem_norm_w': [D], 'w_in': [D, INW], 'gdn_conv_w': [4, 1536], 'gdn_a_log': [4], 'gdn_dt_bias': [4],
    'gdn_out_norm_w': [128], 'nsa_q_norm_w': [64], 'nsa_kc_norm_w': [64], 'nsa_ks_norm_w': [64], 'nsa_kw_norm_w': [64],
    'nsa_cmp_pos_k': [32, 64], 'nsa_cmp_pos_v': [32, 64], 'nsa_cmp_k_w1': [2048, 64], 'nsa_cmp_k_w2': [64, 64],
    'nsa_cmp_v_w1': [2048, 64], 'nsa_cmp_v_w2': [64, 64], 'mem_w_kv': [D, 1024], 'mem_q_norm_w': [128], 'mem_k_norm_w': [128],
    'w_out': [1536, D], 'ffn_norm_w': [D], 'ffn_w_up': [D, 2 * FF], 'ffn_conv_w': [3, 2 * FF], 'ffn_w_down': [FF, D],
}


def make_consts():
    c = {}
    c['c_ident'] = np.eye(128, dtype=np.float32).astype(ml_dtypes.bfloat16)
    idx = np.arange(128)
    same = (idx[:, None] // 64) == (idx[None, :] // 64)
    c['c_btri'] = (same & (idx[:, None] <= idx[None, :])).astype(np.float32)
    c['c_bones'] = same.astype(np.float32)
    c['c_strict'] = (same & (idx[:, None] > idx[None, :])).astype(np.float32)
    c['c_mlow'] = np.where(same & (idx[:, None] >= idx[None, :]), 0.0, NEG).astype(np.float32)
    c['c_mup'] = np.ascontiguousarray(c['c_mlow'].T)
    c['c_mch'] = np.stack([(idx < 64), (idx >= 64)], axis=1).astype(np.float32)
    bf = ml_dtypes.bfloat16
    c['c_tril'] = np.where(idx[:, None] <= idx[None, :], 0.0, NEG).astype(np.float32).astype(bf)
    c['c_far'] = np.where(idx[None, :] < idx[:, None], 0.0, NEG).astype(np.float32).astype(bf)
    cm = np.zeros((9, 128, 512), np.float32)
    f = np.arange(512)
    for m in range(9):
        bt, s_ = (0, m) if m < 5 else (1, m - 1)
        blk = 128 * bt + idx
        vis = (16 * blk[:, None] + 31 <= 512 * s_ + f[None, :]) & (blk[:, None] < 255)
        cm[m] = np.where(vis, 0.0, NEG)
    c['c_cmask'] = cm.astype(bf)
    kk = np.arange(S)
    c['c_E'] = np.where((kk[None, :] // 64) == np.arange(64)[:, None], -NEG, 0.0).astype(np.float32).astype(bf)
    ci = np.arange(256) * 16
    sj = np.arange(64) * 64
    ovl = np.clip(np.minimum(ci[:, None] + 32, sj[None, :] + 64) - np.maximum(ci[:, None], sj[None, :]), 0, None) / 16.0
    ovl[255] = 0.0
    c['c_ovl'] = np.ascontiguousarray(ovl.reshape(2, 128, 64).transpose(1, 0, 2)).astype(np.float32).astype(bf)
    pos = np.arange(S, dtype=np.float32)
    inv = (1.0 / (np.float32(500000.0) ** (np.arange(0, 16, 2, dtype=np.float32) / np.float32(16)))).astype(np.float32)
    ang = pos[:, None] * inv[None, :]
    c['c_rope'] = np.concatenate([np.cos(ang), np.sin(ang)], axis=1).astype(np.float32)
    tt = np.arange(S)
    cur = tt // 64
    blk = np.arange(64)
    valid = blk[None, :] <= cur[:, None]
    forced = (blk[None, :] == 0) | (blk[None, :] == cur[:, None]) | (blk[None, :] == cur[:, None] - 1)
    c['c_A'] = (valid & ~forced).astype(np.float32).reshape(NT, 128, 64)
    c['c_B'] = np.where(valid, np.where(forced, 1e6, 0.0), -1e9).astype(np.float32).reshape(NT, 128, 64)
    return c


def build_program(dbg=False, phases=('ip', 'mem', 'gdn', 'nsa', 'ffn'), dbg_ocat=False):
    nc = bass.Bass("TRN2", target_bir_lowering=False)
    io = {}
    io['x'] = nc.dram_tensor("x", [S, D], F32, kind="ExternalInput").ap()
    io['mem'] = nc.dram_tensor("mem", [256, D], F32, kind="ExternalInput").ap()
    for n in W_NAMES:
        io[n] = nc.dram_tensor(n, W_SHAPES[n], F32, kind="ExternalInput").ap()
    for n, v in make_consts().items():
        io[n] = nc.dram_tensor(n, list(v.shape), BF16 if v.dtype == ml_dtypes.bfloat16 else F32, kind="ExternalInput").ap()
    io['out'] = nc.dram_tensor("out", [S, D], F32, kind="ExternalOutput").ap()
    sk = "ExternalOutput" if dbg else "Internal"
    io['tm'] = nc.dram_tensor("tm", [S, TMW], F32, kind=sk).ap()
    io['qkv_tm'] = nc.dram_tensor("qkv_tm", [S, 1536], BF16, kind=sk).ap()
    if dbg_ocat:
        io['ocatT'] = nc.dram_tensor("ocatT", [NT, 128, 12, 128], BF16, kind="ExternalInput").ap()
    else:
        io['ocatT'] = nc.dram_tensor("ocatT", [NT, 128, 12, 128], BF16, kind=sk).ap()
    io['h_s'] = nc.dram_tensor("h_s", [S, D], F32, kind=sk).ap()
    io['actT'] = nc.dram_tensor("actT", [NT, 128, 22, 128], BF16, kind="Internal").ap()
    kb = KB(nc)
    if 'ip' in phases:
        phase_inproj(kb, io)
    if 'mem' in phases:
        phase_mem(kb, io)
    if 'gdn' in phases:
        phase_gdn(kb, io)
    if 'nsa' in phases:
        phase_nsa(kb, io)
    if 'ffn' in phases:
        phase_ffn(kb, io)
    kb.finish()
    return nc, kb


def make_in_maps(inputs):
    consts = make_consts()
    maps = []
    for b in range(8):
        m = {'x': np.ascontiguousarray(inputs['x'][b]), 'mem': np.ascontiguousarray(inputs['mem'][b])}
        for n in W_NAMES:
            m[n] = np.ascontiguousarray(np.asarray(inputs[n])[0])
        m.update(consts)
        maps.append(m)
    return maps


def kernel(**inputs):
    nc, kb = build_program()
    maps = make_in_maps(inputs)
    res = run_bass_kernel_spmd(nc, maps, core_ids=list(range(8)))
    return np.stack([np.asarray(r['out'], dtype=np.float32) for r in res.results], axis=0)
```

```python
import os
import numpy as np
from contextlib import ExitStack
import concourse.bass as bass
import concourse.mybir as mybir
from concourse.bass_utils import run_bass_kernel_spmd
import ml_dtypes

F32 = mybir.dt.float32
BF16 = mybir.dt.bfloat16
AF = mybir.ActivationFunctionType
ALU = mybir.AluOpType
AX = mybir.AxisListType

S = 4096
D = 1024
NT = S // 128
INW = 3872
TMW = 2336
A_OFF, B_OFF, GATE_OFF, NQ_OFF = 0, 4, 8, 520
KC_OFF, VC_OFF, KS_OFF, VS_OFF, KW_OFF, VW_OFF = 1032, 1160, 1288, 1416, 1544, 1672
NG_OFF, MQ_OFF = 1800, 1824
FF = 2816
NEG = -30000.0
EPS = 1e-6


class T:
    def __init__(self, t, k):
        self.t = t
        self.k = k

    def __getitem__(self, idx):
        return self.t[idx]


class KB:
    NDS = 16

    def __init__(self, nc):
        self.nc = nc
        self.stack = ExitStack()
        self.eng = {'pe': nc.tensor, 'act': nc.scalar, 'dve': nc.vector, 'pool': nc.gpsimd, 'sp': nc.sync}
        self.sem = {}
        for e in self.eng:
            self.sem[e] = self.stack.enter_context(nc.semaphore("s_" + e))
        for j in range(self.NDS):
            self.sem[('d', j)] = self.stack.enter_context(nc.semaphore("d_%d" % j))
        self.cnt = {e: 0 for e in self.eng}
        self.seen = {e: {} for e in self.eng}
        self.state = {}
        self.dma_i = 0
        self.dma_uses = [0] * self.NDS
        self.nins = 0
        self.rr = 0
        self.excl = set()

    def _wait(self, e, evs):
        need = {}
        for (sk, v) in evs:
            if sk == e and e in ('pe', 'sp'):
                continue
            if self.seen[e].get(sk, 0) < v:
                need[sk] = max(need.get(sk, 0), v)
        for sk, v in need.items():
            self.eng[e].wait_ge(self.sem[sk], v)
            self.seen[e][sk] = v

    @staticmethod
    def _keys(lst):
        out = []
        for x in lst:
            if isinstance(x, T):
                out.append(x.k)
            elif isinstance(x, (list, tuple)) and len(x) and isinstance(x[0], T):
                out.append((x[0].k,) + tuple(x[1:]))
            else:
                out.append(x)
        return out

    def _deps(self, reads, writes):
        evs = []
        for k in reads:
            st = self.state.get(k)
            if st and st[0]:
                evs.append(st[0])
        for k in writes:
            st = self.state.get(k)
            if st:
                if st[0]:
                    evs.append(st[0])
                evs.extend(st[1])
        return evs

    def _update(self, ev, reads, writes):
        for k in reads:
            st = self.state.setdefault(k, [None, []])
            st[1].append(ev)
            if len(st[1]) > 12:
                best = {}
                for (sk, v) in st[1]:
                    best[sk] = max(best.get(sk, 0), v)
                st[1] = list(best.items())
        for k in writes:
            self.state[k] = [ev, []]

    def op(self, e, fn, r=(), w=()):
        r = self._keys(r)
        w = self._keys(w)
        w = w + [k for k in r if k in self.excl and k not in w]
        self._wait(e, self._deps(r, w))
        ins = fn(self.eng[e])
        self.cnt[e] += 1
        ins.then_inc(self.sem[e], 1)
        self._update((e, self.cnt[e]), r, w)
        self.nins += 1
        return ins

    def dma(self, q, out, in_, r=(), w=(), **kw):
        r = self._keys(r)
        w = self._keys(w)
        j = self.dma_i % self.NDS
        self.dma_i += 1
        evs = self._deps(r, w)
        if self.dma_uses[j] > 0:
            evs.append((('d', j), 16 * self.dma_uses[j]))
        self._wait(q, evs)
        ins = self.eng[q].dma_start(out=out, in_=in_, **kw)
        self.dma_uses[j] += 1
        ins.then_inc(self.sem[('d', j)], 16)
        ev = (('d', j), 16 * self.dma_uses[j])
        self._update(ev, r, w)
        self.nins += 1
        return ev

    def barrier(self):
        evs = [(f, self.cnt[f]) for f in self.eng if self.cnt[f]]
        evs += [(('d', j), 16 * self.dma_uses[j]) for j in range(self.NDS) if self.dma_uses[j]]
        for e in self.eng:
            self._wait(e, [ev for ev in evs if ev[0] != e])

    def finish(self):
        self.barrier()
        self.stack.close()

    def ew(self, with_act=False):
        self.rr += 1
        lst = ('dve', 'pool', 'act') if with_act else ('dve', 'pool')
        return lst[self.rr % len(lst)]


class Phase:
    def __init__(self, kb, tag):
        self.kb = kb
        self.nc = kb.nc
        self.tag = tag
        self.st = ExitStack()

    def sb(self, name, shape, dt):
        n = self.tag + "_" + name
        return T(self.st.enter_context(self.nc.sbuf_tensor(n, list(shape), dt)), n)

    def sbn(self, name, shape, dt, n):
        return [self.sb("%s%d" % (name, i), shape, dt) for i in range(n)]

    def ps(self, name, shape, dt=F32):
        n = self.tag + "_" + name
        self.kb.excl.add(n)
        return T(self.st.enter_context(self.nc.psum_tensor(n, list(shape), dt)), n)

    def psn(self, name, shape, dt, n):
        return [self.ps("%s%d" % (name, i), shape, dt) for i in range(n)]

    def close(self):
        self.kb.barrier()
        self.st.close()


def load_cast_weight(kb, ph, dst, src_ap, nchunks, ncols, gam=None, stage_cols=None):
    stage_cols = stage_cols or ncols
    stg = ph.sbn("stg_" + dst.k, [128, stage_cols], F32, 2)
    i = 0
    engs = ('dve', 'act', 'dve')
    for c in range(nchunks):
        for c0 in range(0, ncols, stage_cols):
            c1 = min(ncols, c0 + stage_cols)
            sg = stg[i % 2]
            kb.dma('sp', sg[:, 0:c1 - c0], src_ap[c * 128:(c + 1) * 128, c0:c1], w=[sg])
            e = engs[i % 3]
            o = dst[:, c, c0:c1]
            if gam is None:
                if e == 'act':
                    kb.op(e, lambda E: E.copy(out=o, in_=sg[:, 0:c1 - c0]), r=[sg], w=[(dst, c)])
                else:
                    kb.op(e, lambda E: E.tensor_copy(out=o, in_=sg[:, 0:c1 - c0]), r=[sg], w=[(dst, c)])
            else:
                if e == 'act':
                    kb.op(e, lambda E: E.activation(out=o, in_=sg[:, 0:c1 - c0], func=AF.Copy, scale=gam[:, c:c + 1]),
                          r=[sg, gam], w=[(dst, c)])
                else:
                    kb.op(e, lambda E: E.tensor_scalar(out=o, in0=sg[:, 0:c1 - c0], scalar1=gam[:, c:c + 1], scalar2=None,
                                                       op0=ALU.mult), r=[sg, gam], w=[(dst, c)])
            i += 1


def rms_rstd(kb, src_ap, junk, ss, rs, n, rkeys):
    kb.op('act', lambda E: E.activation(out=junk[:, 0:n], in_=src_ap, func=AF.Square, accum_out=ss[:]), r=rkeys, w=[junk, ss])
    kb.op('act', lambda E: E.activation(out=rs[:], in_=ss[:], func=AF.Sqrt, scale=1.0 / n, bias=EPS), r=[ss], w=[rs])
    kb.op('dve', lambda E: E.reciprocal(out=rs[:], in_=rs[:]), r=[rs], w=[rs])


def phase_inproj(kb, io):
    nc = kb.nc
    ph = Phase(kb, "ip")
    winb = ph.sb("winb", [128, 8, INW], BF16)
    gam = ph.sb("gam", [128, 8], F32)
    cw = ph.sb("cw", [128, 4, 12], F32)
    ident = ph.sb("ident", [128, 128], BF16)
    kb.dma('sp', gam[:], io['attn_norm_w'].rearrange("(c p) -> p c", p=128), w=[gam], allow_slow_non_contiguous=True)
    for j in range(4):
        kb.dma('sp', cw[:, j, :], io['gdn_conv_w'][j, :].rearrange("(c p) -> p c", p=128), w=[cw], allow_slow_non_contiguous=True)
    kb.dma('sp', ident[:], io['c_ident'][:, :], w=[ident])
    load_cast_weight(kb, ph, winb, io['w_in'], 8, INW, gam=gam, stage_cols=1936)

    xt = ph.sbn("xt", [128, D], F32, 2)
    junk = ph.sb("junk", [128, D], BF16)
    ss = ph.sbn("ss", [128, 1], F32, 2)
    rs = ph.sbn("rs", [128, 1], F32, 2)
    xn = ph.sbn("xn", [128, D], BF16, 2)
    xnT = ph.sbn("xnT", [128, 8, 512], BF16, 2)
    xc = ph.sbn("xc", [128, 515], F32, 3)
    acc = ph.sbn("acc", [128, 512], F32, 3)
    halo = ph.sb("halo", [128, 12, 3], F32)
    qT = ph.sb("qT", [128, 12, 512], BF16)
    qtm = ph.sbn("qtm", [128, 1536], BF16, 2)
    tmt = ph.sbn("tmt", [128, TMW], F32, 2)
    pT = ph.psn("pT", [128, 8, 128], BF16, 2)
    pF = ph.psn("pF", [128, 512], F32, 2)
    pM = ph.psn("pM", [128, 512], F32, 2)
    pQ = ph.psn("pQ", [128, 8, 128], BF16, 2)

    kb.op('pool', lambda E: E.memset(halo[:], 0.0), w=[halo])
    def iptiles(s):
        xT = xnT[s % 2]
        for t4 in range(4):
            t = s * 4 + t4
            b = t % 2
            kb.dma('sp', xt[b][:], io['x'][t * 128:(t + 1) * 128, :], w=[xt[b]])
            rms_rstd(kb, xt[b][:], junk, ss[b], rs[b], D, [xt[b]])
            kb.op('dve', lambda E: E.tensor_scalar(out=xn[b][:], in0=xt[b][:], scalar1=rs[b][:, 0:1], scalar2=None, op0=ALU.mult),
                  r=[xt[b], rs[b]], w=[xn[b]])
            yield
            for c in range(8):
                kb.op('pe', lambda E: E.transpose(out=pT[b][:, c, :], in_=xn[b][:, c * 128:(c + 1) * 128], identity=ident[:]),
                      r=[xn[b], ident], w=[pT[b]])
            kb.op('act', lambda E: E.copy(out=xT[:, :, t4 * 128:(t4 + 1) * 128], in_=pT[b][:]), r=[pT[b]], w=[xT])
            yield

    def ipmain(s):
        xT = xnT[s % 2]
        for cb in range(12):
            pf = pF[cb % 2]
            for c in range(8):
                kb.op('pe', lambda E: E.matmul(pf[:], lhsT=winb[:, c, cb * 128:(cb + 1) * 128], rhs=xT[:, c, :],
                                               start=(c == 0), stop=(c == 7)), r=[xT, (winb, c)], w=[pf])
            yield
            x3 = xc[cb % 3]
            ac = acc[cb % 3]
            kb.op('act', lambda E: E.copy(out=x3[:, 3:515], in_=pf[:]), r=[pf], w=[(x3, 'b')])
            kb.op('dve', lambda E: E.tensor_copy(out=x3[:, 0:3], in_=halo[:, cb, :]), r=[(halo, cb)], w=[(x3, 'h')])
            kb.op('dve', lambda E: E.tensor_copy(out=halo[:, cb, :], in_=x3[:, 512:515]), r=[(x3, 'b')], w=[(halo, cb)])
            kb.op('act', lambda E: E.activation(out=ac[:], in_=pf[:], func=AF.Copy, scale=cw[:, 3, cb:cb + 1]), r=[pf, cw], w=[ac])
            for j in range(3):
                kb.op('dve', lambda E: E.scalar_tensor_tensor(out=ac[:], in0=x3[:, j:j + 512], scalar=cw[:, j, cb:cb + 1], in1=ac[:],
                                                           op0=ALU.mult, op1=ALU.add), r=[(x3, 'b'), (x3, 'h'), cw, ac], w=[ac])
            kb.op('act', lambda E: E.activation(out=qT[:, cb, :], in_=ac[:], func=AF.Silu), r=[ac], w=[(qT, cb)])
            yield
        for t4 in range(4):
            t = s * 4 + t4
            qm = qtm[t % 2]
            for g3 in range(3):
                pq = pQ[g3 % 2]
                for j in range(4):
                    cb = g3 * 4 + j
                    kb.op('pe', lambda E: E.transpose(out=pq[:, j, :], in_=qT[:, cb, t4 * 128:(t4 + 1) * 128], identity=ident[:]),
                          r=[(qT, cb), ident], w=[pq])
                e1 = 'dve' if g3 % 2 == 0 else 'act'
                if e1 == 'dve':
                    kb.op('dve', lambda E: E.tensor_copy(out=qm[:, g3 * 512:(g3 + 1) * 512], in_=pq[:, 0:4, :].rearrange("p a b -> p (a b)")),
                          r=[pq], w=[(qm, g3)])
                else:
                    kb.op('act', lambda E: E.copy(out=qm[:, g3 * 512:(g3 + 1) * 512], in_=pq[:, 0:4, :].rearrange("p a b -> p (a b)")),
                          r=[pq], w=[(qm, g3)])
            kb.dma('pool', io['qkv_tm'][t * 128:(t + 1) * 128, :], qm[:], r=[(qm, 0), (qm, 1), (qm, 2)], w=['qkv_tm'])
            yield
            tm = tmt[t % 2]
            for ci, n0 in enumerate(range(0, TMW, 512)):
                n1 = min(TMW, n0 + 512)
                pm = pM[ci % 2]
                for c in range(8):
                    kb.op('pe', lambda E: E.matmul(pm[:, 0:n1 - n0], lhsT=xT[:, c, t4 * 128:(t4 + 1) * 128],
                                                   rhs=winb[:, c, 1536 + n0:1536 + n1], start=(c == 0), stop=(c == 7)),
                          r=[xT, (winb, c)], w=[pm])
                if ci % 2 == 0:
                    kb.op('dve', lambda E: E.tensor_copy(out=tm[:, n0:n1], in_=pm[:, 0:n1 - n0]), r=[pm], w=[(tm, ci)])
                else:
                    kb.op('act', lambda E: E.copy(out=tm[:, n0:n1], in_=pm[:, 0:n1 - n0]), r=[pm], w=[(tm, ci)])
            kb.dma('pool', io['tm'][t * 128:(t + 1) * 128, :], tm[:], r=[(tm, i) for i in range(5)], w=['tm'])

    def interleave(gens):
        gens = list(gens)
        while gens:
            for g_ in list(gens):
                try:
                    next(g_)
                except StopIteration:
                    gens.remove(g_)

    NS = S // 512
    interleave([iptiles(0)])
    for s in range(NS):
        gl = [ipmain(s)]
        if s + 1 < NS:
            gl.append(iptiles(s + 1))
        interleave(gl)
    ph.close()


def attn_block(kb, o_ps, PTbuf, pti, sc_ps, kt_specs, ident, rkeys_q, LA=2):
    n = len(kt_specs)
    pts = [None] * n
    started = set()
    npv = sum(sp['qt1'] - sp['qt0'] for sp in kt_specs)
    done = 0
    for i in range(n + LA):
        if i < n:
            sp = kt_specs[i]
            pss = sc_ps[pti[0] % len(sc_ps)]
            ptb = PTbuf[pti[0] % len(PTbuf)]
            pti[0] += 1
            pts[i] = ptb
            q0, q1 = sp['qt0'], sp['qt1']
            ncol = (q1 - q0) * 128
            nk = sp['nk']
            nm = len(sp['masks'])
            kb.op('pe', lambda E: E.matmul(pss[0:nk, 0:ncol], lhsT=sp['lhsT'], rhs=sp['rhs_fn'](q0, q1), start=True, stop=(nm == 0)),
                  r=sp['rk'] + rkeys_q, w=[pss])
            for mi, (c0, nc_, mask) in enumerate(sp['masks']):
                kb.op('pe', lambda E: E.matmul(pss[0:nk, c0:c0 + nc_], lhsT=ident[0:nk, 0:nk], rhs=mask, start=False, stop=(mi == nm - 1)),
                      r=[ident] + sp.get('mk', []), w=[pss])
            kb.op('act', lambda E: E.activation(out=ptb[0:nk, 0:ncol], in_=pss[0:nk, 0:ncol], func=AF.Exp), r=[pss], w=[ptb])
        j = i - LA
        if j >= 0:
            sp = kt_specs[j]
            ptb = pts[j]
            nk = sp['nk']
            for qt in range(sp['qt0'], sp['qt1']):
                c0 = (qt - sp['qt0']) * 128
                oap, okey, bank = o_ps(qt)
                st = bank not in started
                started.add(bank)
                done += 1
                kb.op('pe', lambda E: E.matmul(oap, lhsT=ptb[0:nk, c0:c0 + 128], rhs=sp['v_fn'](qt), start=st, stop=(done == npv),
                                               skip_group_check=True), r=[ptb] + sp['rv'], w=[okey])


def phase_mem(kb, io):
    nc = kb.nc
    ph = Phase(kb, "mm")
    ident = ph.sb("ident", [128, 128], BF16)
    kb.dma('sp', ident[:], io['c_ident'][:, :], w=[ident])
    wkv = ph.sb("wkv", [128, 8, 1024], BF16)
    gam = ph.sb("gam", [128, 8], F32)
    kb.dma('sp', gam[:], io['mem_norm_w'].rearrange("(c p) -> p c", p=128), w=[gam], allow_slow_non_contiguous=True)
    load_cast_weight(kb, ph, wkv, io['mem_w_kv'], 8, 1024, gam=gam)
    qnw = ph.sb("qnw", [128, 128], F32)
    knw = ph.sb("knw", [128, 128], F32)
    kb.dma('sp', qnw[:], io['mem_q_norm_w'].partition_broadcast(128), w=[qnw])
    kb.dma('sp', knw[:], io['mem_k_norm_w'].partition_broadcast(128), w=[knw])
    kb.op('dve', lambda E: E.tensor_scalar(out=qnw[:], in0=qnw[:], scalar1=128 ** -0.5, scalar2=None, op0=ALU.mult), r=[qnw], w=[qnw])

    mt = ph.sbn("mt", [128, D], F32, 2)
    junk = ph.sb("junk", [128, D], BF16)
    ss = ph.sb("ss", [128, 1], F32)
    rs = ph.sb("rs", [128, 1], F32)
    mn = ph.sb("mn", [128, D], BF16)
    mnT = ph.sb("mnT", [128, 8, 128], BF16)
    kvt = ph.sb("kvt", [128, 1024], F32)
    sq4 = ph.sb("sq4", [128, 4, 128], F32)
    ss4 = ph.sb("ss4", [128, 4], F32)
    rs4 = ph.sb("rs4", [128, 4], F32)
    kn = ph.sb("kn", [128, 4, 128], BF16)
    kT = ph.sb("kT", [128, 4, 256], BF16)
    v1 = ph.sb("v1", [128, 2, 4, 129], BF16)
    pT = ph.ps("pT", [128, 8, 128], BF16)
    pK = ph.psn("pK", [128, 512], F32, 2)
    kb.op('pool', lambda E: E.memset(v1[:], 1.0), w=[v1])
    for mt_i in range(2):
        m = mt[mt_i]
        kb.dma('sp', m[:], io['mem'][mt_i * 128:(mt_i + 1) * 128, :], w=[m])
        rms_rstd(kb, m[:], junk, ss, rs, D, [m])
        kb.op('dve', lambda E: E.tensor_scalar(out=mn[:], in0=m[:], scalar1=rs[:, 0:1], scalar2=None, op0=ALU.mult), r=[m, rs], w=[mn])
        for c in range(8):
            kb.op('pe', lambda E: E.transpose(out=pT[:, c, :], in_=mn[:, c * 128:(c + 1) * 128], identity=ident[:]), r=[mn, ident], w=[pT])
        kb.op('act', lambda E: E.copy(out=mnT[:], in_=pT[:]), r=[pT], w=[mnT])
        for half in range(2):
            pk = pK[half]
            for c in range(8):
                kb.op('pe', lambda E: E.matmul(pk[:], lhsT=mnT[:, c, :], rhs=wkv[:, c, half * 512:(half + 1) * 512],
                                               start=(c == 0), stop=(c == 7)), r=[mnT, (wkv, c)], w=[pk])
            kb.op('act', lambda E: E.copy(out=kvt[:, half * 512:(half + 1) * 512], in_=pk[:]), r=[pk], w=[(kvt, half)])
        k3 = kvt[:, 0:512].rearrange("p (h d) -> p h d", h=4)
        kb.op('dve', lambda E: E.tensor_tensor(out=sq4[:], in0=k3, in1=k3, op=ALU.mult), r=[(kvt, 0)], w=[sq4])
        kb.op('dve', lambda E: E.tensor_reduce(out=ss4[:], in_=sq4[:], axis=AX.X, op=ALU.add), r=[sq4], w=[ss4])
        kb.op('act', lambda E: E.activation(out=rs4[:], in_=ss4[:], func=AF.Sqrt, scale=1.0 / 128, bias=EPS), r=[ss4], w=[rs4])
        kb.op('dve', lambda E: E.reciprocal(out=rs4[:], in_=rs4[:]), r=[rs4], w=[rs4])
        kb.op('dve', lambda E: E.tensor_tensor(out=sq4[:], in0=k3, in1=rs4[:].unsqueeze(2).to_broadcast([128, 4, 128]), op=ALU.mult),
              r=[(kvt, 0), rs4], w=[sq4])
        kb.op('dve', lambda E: E.tensor_tensor(out=kn[:], in0=sq4[:], in1=knw[:].unsqueeze(1).to_broadcast([128, 4, 128]), op=ALU.mult),
              r=[sq4, knw], w=[kn])
        for h in range(4):
            kb.op('pe', lambda E: E.transpose(out=pT[:, h, :], in_=kn[:, h, :], identity=ident[:]), r=[kn, ident], w=[pT])
        kb.op('act', lambda E: E.copy(out=kT[:, :, mt_i * 128:(mt_i + 1) * 128], in_=pT[:, 0:4, :]), r=[pT], w=[kT])
        kb.op('dve', lambda E: E.tensor_copy(out=v1[:, mt_i, :, 0:128], in_=kvt[:, 512:1024].rearrange("p (h d) -> p h d", h=4)),
              r=[(kvt, 1)], w=[v1])

    qt_ = ph.sbn("qt", [128, 512], F32, 2)
    qs = ph.sb("qs", [128, 4, 128], F32)
    qn = ph.sbn("qn", [128, 4, 128], BF16, 2)
    qT = ph.sbn("qT", [128, 4, 512], BF16, 2)
    PTb = ph.sbn("PT", [128, 512], BF16, 4)
    pti = [0]
    oc = ph.sbn("oc", [128, 4, 128], BF16, 4)
    rinv = ph.sb("rinv", [128, 1], F32)
    ocT = ph.sbn("ocT", [128, 4, 128], BF16, 2)
    sc_ps = ph.psn("sc", [128, 512], F32, 2)
    o_psA = ph.ps("oA", [128, 2, 256], F32)
    o_psB = ph.ps("oB", [128, 2, 256], F32)
    pO = ph.ps("pO", [128, 8, 128], BF16)
    for s in range(S // 512):
        qTs = qT[s % 2]
        for t4 in range(4):
            t = s * 4 + t4
            q = qt_[t % 2]
            qb = qn[t % 2]
            kb.dma('sp', q[:], io['tm'][t * 128:(t + 1) * 128, MQ_OFF:MQ_OFF + 512], r=['tm'], w=[q])
            q3 = q[:].rearrange("p (h d) -> p h d", h=4)
            kb.op('pool', lambda E: E.tensor_tensor(out=qs[:], in0=q3, in1=q3, op=ALU.mult), r=[q], w=[qs])
            kb.op('dve', lambda E: E.tensor_reduce(out=ss4[:], in_=qs[:], axis=AX.X, op=ALU.add), r=[qs], w=[ss4])
            kb.op('act', lambda E: E.activation(out=rs4[:], in_=ss4[:], func=AF.Sqrt, scale=1.0 / 128, bias=EPS), r=[ss4], w=[rs4])
            kb.op('dve', lambda E: E.reciprocal(out=rs4[:], in_=rs4[:]), r=[rs4], w=[rs4])
            kb.op('dve', lambda E: E.tensor_tensor(out=qs[:], in0=q3, in1=rs4[:].unsqueeze(2).to_broadcast([128, 4, 128]), op=ALU.mult),
                  r=[q, rs4], w=[qs])
            kb.op('pool', lambda E: E.tensor_tensor(out=qb[:], in0=qs[:], in1=qnw[:].unsqueeze(1).to_broadcast([128, 4, 128]), op=ALU.mult),
                  r=[qs, qnw], w=[qb])
            for h in range(4):
                kb.op('pe', lambda E: E.transpose(out=pT[:, h, :], in_=qb[:, h, :], identity=ident[:]), r=[qb, ident], w=[pT])
            kb.op('act', lambda E: E.copy(out=qTs[:, :, t4 * 128:(t4 + 1) * 128], in_=pT[:, 0:4, :]), r=[pT], w=[qTs])
        for h in range(4):
            def o_ps(qt, h=h):
                return (o_psA[:, qt, 0:129], o_psA, 'A') if qt < 2 else (o_psB[:, qt - 2, 0:129], o_psB, 'B')
            specs = []
            for kt in range(2):
                specs.append(dict(lhsT=kT[:, h, kt * 128:(kt + 1) * 128], rhs_fn=lambda q0, q1, h=h: qTs[:, h, q0 * 128:q1 * 128],
                                  qt0=0, qt1=4, masks=[], v_fn=lambda qt, kt=kt, h=h: v1[:, kt, h, :], nk=128, rk=[kT], rv=[v1]))
            attn_block(kb, o_ps, PTb, pti, sc_ps, specs, ident, [qTs])
            for t4 in range(4):
                t = s * 4 + t4
                ob = oc[t4]
                oap, okey, _ = o_ps(t4)
                kb.op('dve', lambda E: E.reciprocal(out=rinv[:], in_=oap[:, 128:129]), r=[okey], w=[rinv])
                kb.op('dve', lambda E: E.tensor_scalar(out=ob[:, h, :], in0=oap[:, 0:128], scalar1=rinv[:, 0:1], scalar2=None, op0=ALU.mult),
                      r=[okey, rinv], w=[(ob, h)])
        for t4 in range(4):
            t = s * 4 + t4
            ob = oc[t4]
            oT = ocT[t % 2]
            for h in range(4):
                kb.op('pe', lambda E: E.transpose(out=pO[:, h, :], in_=ob[:, h, :], identity=ident[:]), r=[(ob, h), ident], w=[pO])
            kb.op('act', lambda E: E.copy(out=oT[:], in_=pO[:, 0:4, :]), r=[pO], w=[oT])
            kb.dma('sp', io['ocatT'][t, :, 8:12, :], oT[:], r=[oT], w=['ocatT_c'])
    ph.close()


def phase_gdn(kb, io):
    nc = kb.nc
    ph = Phase(kb, "gd")
    ident = ph.sb("ident", [128, 128], BF16)
    btri = ph.sb("btri", [128, 128], F32)
    bones = ph.sb("bones", [128, 128], F32)
    ones = ph.sb("ones", [128, 128], F32)
    mlow = ph.sb("mlow", [128, 4, 128], F32)
    mup = ph.sb("mup", [128, 4, 128], F32)
    strict = ph.sb("strict", [128, 128], F32)
    mch = ph.sb("mch", [128, 2], F32)
    kb.dma('sp', ident[:], io['c_ident'][:, :], w=[ident])
    kb.dma('sp', btri[:], io['c_btri'][:, :], w=[btri])
    kb.dma('sp', bones[:], io['c_bones'][:, :], w=[bones])
    kb.dma('sp', strict[:], io['c_strict'][:, :], w=[strict])
    kb.dma('sp', mch[:], io['c_mch'][:, :], w=[mch])
    for h in range(4):
        kb.dma('sp', mlow[:, h, :], io['c_mlow'][:, :], w=[mlow])
        kb.dma('sp', mup[:, h, :], io['c_mup'][:, :], w=[mup])
    kb.op('pool', lambda E: E.memset(ones[:], 1.0), w=[ones])
    dtb = ph.sb("dtb", [128, 4], F32)
    nA = ph.sb("nA", [128, 4], F32)
    gw = ph.sb("gw", [128, 128], F32)
    kb.dma('sp', dtb[:], io['gdn_dt_bias'].partition_broadcast(128), w=[dtb])
    kb.dma('sp', nA[:], io['gdn_a_log'].partition_broadcast(128), w=[nA])
    kb.dma('sp', gw[:], io['gdn_out_norm_w'].partition_broadcast(128), w=[gw])
    kb.op('act', lambda E: E.activation(out=nA[:], in_=nA[:], func=AF.Exp), r=[nA], w=[nA])
    kb.op('dve', lambda E: E.tensor_scalar(out=nA[:], in0=nA[:], scalar1=-1.0, scalar2=None, op0=ALU.mult), r=[nA], w=[nA])

    def B4(t_, n=4):
        return t_.unsqueeze(2).to_broadcast([128, n, 128])

    def M4(t_):
        return t_.unsqueeze(1).to_broadcast([128, 4, 128])

    qkv = ph.sbn("qkv", [128, 3, 4, 128], BF16, 2)
    ab = ph.sbn("ab", [128, 8], F32, 2)
    gt = ph.sbn("gt", [128, 512], F32, 2)
    sm = ph.sbn("sm", [128, 64], F32, 2)
    gs = ph.sbn("gs", [128, 16], F32, 2)
    gm = ph.sb("gm", [128, 8], F32)
    R1 = ph.sb("R1", [128, 4, 128], F32)
    R2 = ph.sb("R2", [128, 4, 128], F32)
    tmpA = ph.sb("tmpA", [128, 4, 128], F32)
    tmpB = ph.sb("tmpB", [128, 4, 128], F32)
    dec = ph.sb("dec", [128, 4, 128], F32)
    decT = ph.sb("decT", [128, 4, 128], F32)
    sq = ph.sb("sq", [128, 4, 128], F32)
    KBG = ph.sb("KBG", [128, 4, 128], BF16)
    Kdb = ph.sbn("Kd", [128, 4, 128], BF16, 4)
    VBb = ph.sbn("VB", [128, 4, 128], BF16, 2)
    dg = ph.sbn("dg", [128, 4, 128], BF16, 4)
    QT = ph.sb("QT", [128, 4, 128], BF16)
    QGb = ph.sbn("QG", [128, 4, 128], BF16, 4)
    KT = ph.sb("KT", [128, 4, 128], BF16)
    nbs = ph.sb("nbs", [128, 4, 128], F32)
    Xb = ph.sbn("X", [128, 4, 128], BF16, 2)
    Yb = ph.sbn("Y", [128, 4, 128], BF16, 2)
    Pbb = ph.sbn("P", [128, 4, 128], BF16, 4)
    aqkTb = ph.sbn("aqkT", [128, 4, 128], BF16, 2)
    negWTb = ph.sbn("negWT", [128, 4, 128], BF16, 2)
    vnew = ph.sb("vnew", [128, 4, 128], BF16)
    Sf = ph.sb("Sf", [128, 4, 128], F32)
    Sbf = ph.sbn("Sbf", [128, 4, 128], BF16, 3)
    osb = ph.sb("osb", [128, 4, 128], F32)
    sgt = ph.sb("sgt", [128, 512], F32)
    oa = ph.sbn("oa", [128, 4, 128], BF16, 2)
    oaT = ph.sbn("oaT", [128, 4, 128], BF16, 2)
    pS = ph.ps("pS", [128, 512], F32)
    pD = ph.ps("pD", [128, 4, 128], F32)
    pA = ph.psn("pA", [128, 4, 128], F32, 2)
    pB = ph.psn("pB", [128, 8, 128], BF16, 1)
    pV = ph.ps("pV", [128, 4, 128], F32)
    pdS = ph.ps("pdS", [128, 4, 128], F32)
    pO = ph.ps("pO", [128, 4, 128], F32)
    kb.op('pool', lambda E: E.memset(Sf[:], 0.0), w=[Sf])
    kb.op('pool', lambda E: E.memset(Sbf[0][:], 0.0), w=[Sbf[0]])
    kb.op('pool', lambda E: E.memset(vnew[:], 0.0), w=[vnew])
    pai = [0]

    def PA():
        pai[0] += 1
        return pA[pai[0] % 2]

    def mm4(p, lf, rf, rk):
        for h in range(4):
            kb.op('pe', lambda E: E.matmul(p[:, h, :], lhsT=lf(h), rhs=rf(h), start=True, stop=True), r=rk, w=[p])

    si = [0]

    def tile(t):
        b = t % 2
        x_ = qkv[b]
        VB, aqkT, negWT = VBb[b], aqkTb[b], negWTb[b]
        Kd = Kdb[2 * b:2 * b + 2]
        QG = QGb[2 * b:2 * b + 2]
        Pb = Pbb[2 * b:2 * b + 2]
        s_ = sm[b]
        kb.dma('sp', x_[:].rearrange("p a h d -> p (a h d)"), io['qkv_tm'][t * 128:(t + 1) * 128, :], r=['qkv_tm'], w=[x_])
        kb.dma('sp', ab[b][:], io['tm'][t * 128:(t + 1) * 128, 0:8], r=['tm'], w=[ab[b]])
        kb.dma('sp', gt[b][:], io['tm'][t * 128:(t + 1) * 128, GATE_OFF:GATE_OFF + 512], r=['tm'], w=[gt[b]])
        g = s_[:, 0:4]
        kb.op('dve', lambda E: E.tensor_tensor(out=g, in0=ab[b][:, 0:4], in1=dtb[:], op=ALU.add), r=[ab[b], dtb], w=[s_])
        kb.op('act', lambda E: E.activation(out=g, in_=g, func=AF.Exp), r=[s_], w=[s_])
        kb.op('act', lambda E: E.activation(out=g, in_=g, func=AF.Ln, bias=1.0), r=[s_], w=[s_])
        kb.op('dve', lambda E: E.tensor_tensor(out=g, in0=g, in1=nA[:], op=ALU.mult), r=[s_, nA], w=[s_])
        kb.op('dve', lambda E: E.tensor_scalar(out=s_[:, 4:8], in0=g, scalar1=-1.0, scalar2=None, op0=ALU.mult), r=[s_], w=[s_])
        kb.op('act', lambda E: E.activation(out=s_[:, 8:12], in_=ab[b][:, 4:8], func=AF.Exp, scale=-1.0), r=[ab[b], s_], w=[s_])
        kb.op('dve', lambda E: E.tensor_scalar(out=s_[:, 8:12], in0=s_[:, 8:12], scalar1=1.0, scalar2=None, op0=ALU.add), r=[s_], w=[s_])
        kb.op('dve', lambda E: E.reciprocal(out=s_[:, 8:12], in_=s_[:, 8:12]), r=[s_], w=[s_])
        kb.op('dve', lambda E: E.tensor_scalar(out=s_[:, 12:16], in0=s_[:, 8:12], scalar1=-1.0, scalar2=None, op0=ALU.mult), r=[s_], w=[s_])
        for j in range(2):
            kb.op('dve', lambda E: E.tensor_scalar(out=gm[:, 4 * j:4 * j + 4], in0=g, scalar1=mch[:, j:j + 1], scalar2=None, op0=ALU.mult),
                  r=[s_, mch], w=[gm])
        kb.op('pe', lambda E: E.matmul(pS[:, 0:4], lhsT=btri[:], rhs=g, start=True, stop=True), r=[btri, s_], w=[pS])
        kb.op('pe', lambda E: E.matmul(pS[:, 4:8], lhsT=bones[:], rhs=g, start=True, stop=True), r=[bones, s_], w=[pS])
        kb.op('pe', lambda E: E.matmul(pS[:, 8:16], lhsT=ones[:], rhs=gm[:], start=True, stop=True), r=[ones, gm], w=[pS])
        G = gs[b]
        kb.op('dve', lambda E: E.tensor_copy(out=G[:], in_=pS[:, 0:16]), r=[pS], w=[G])
        kb.op('act', lambda E: E.activation(out=s_[:, 16:20], in_=G[:, 0:4], func=AF.Exp), r=[G, s_], w=[s_])
        kb.op('dve', lambda E: E.tensor_tensor(out=G[:, 4:8], in0=G[:, 4:8], in1=G[:, 0:4], op=ALU.subtract), r=[G], w=[G])
        kb.op('act', lambda E: E.activation(out=s_[:, 20:24], in_=G[:, 4:8], func=AF.Exp), r=[G, s_], w=[s_])
        kb.op('act', lambda E: E.activation(out=s_[:, 56:64], in_=G[:, 8:16], func=AF.Exp), r=[G, s_], w=[s_])
        yield
        kb.op('pool', lambda E: E.tensor_copy(out=R1[:], in_=B4(g)), r=[s_], w=[R1])
        kb.op('pool', lambda E: E.tensor_tensor(out=R2[:], in0=M4(btri[:]), in1=B4(s_[:, 4:8]), op=ALU.mult), r=[s_, btri], w=[R2])
        kb.op('pe', lambda E: E.matmul(pD[:].rearrange("p a b -> p (a b)"), lhsT=btri[:], rhs=R1[:].rearrange("p a b -> p (a b)"),
                                       start=True, stop=False), r=[btri, R1], w=[pD])
        kb.op('pe', lambda E: E.matmul(pD[:].rearrange("p a b -> p (a b)"), lhsT=bones[:], rhs=R2[:].rearrange("p a b -> p (a b)"),
                                       start=False, stop=True), r=[bones, R2], w=[pD])
        kb.op('dve', lambda E: E.tensor_tensor(out=tmpA[:], in0=pD[:], in1=mlow[:], op=ALU.add), r=[pD, mlow], w=[tmpA])
        kb.op('act', lambda E: E.activation(out=dec[:], in_=tmpA[:], func=AF.Exp), r=[tmpA], w=[dec])
        kb.op('dve', lambda E: E.scalar_tensor_tensor(out=tmpB[:], in0=pD[:], scalar=-1.0, in1=mup[:], op0=ALU.mult, op1=ALU.add),
              r=[pD, mup], w=[tmpB])
        kb.op('act', lambda E: E.activation(out=decT[:], in_=tmpB[:], func=AF.Exp), r=[tmpB], w=[decT])
        yield
        for a_, c0 in ((0, 24), (1, 28)):
            kb.op('pool', lambda E: E.tensor_tensor(out=sq[:], in0=x_[:, a_, :, :], in1=x_[:, a_, :, :], op=ALU.mult), r=[x_], w=[sq])
            kb.op('dve', lambda E: E.tensor_reduce(out=s_[:, c0:c0 + 4], in_=sq[:], axis=AX.X, op=ALU.add), r=[sq, s_], w=[s_])
            kb.op('act', lambda E: E.activation(out=s_[:, c0:c0 + 4], in_=s_[:, c0:c0 + 4], func=AF.Sqrt, bias=EPS), r=[s_], w=[s_])
            kb.op('dve', lambda E: E.reciprocal(out=s_[:, c0:c0 + 4], in_=s_[:, c0:c0 + 4]), r=[s_], w=[s_])
        sc_ = lambda o, a, bb: kb.op('dve', lambda E: E.tensor_tensor(out=s_[:, o:o + 4], in0=a, in1=bb, op=ALU.mult), r=[s_, mch], w=[s_])
        kb.op('dve', lambda E: E.tensor_scalar(out=s_[:, 32:36], in0=s_[:, 24:28], scalar1=128 ** -0.5, scalar2=None, op0=ALU.mult), r=[s_], w=[s_])
        sc_(36, s_[:, 32:36], s_[:, 16:20])
        kb.op('dve', lambda E: E.tensor_scalar(out=s_[:, 40:44], in0=s_[:, 36:40], scalar1=mch[:, 1:2], scalar2=None, op0=ALU.mult), r=[s_, mch], w=[s_])
        kb.op('dve', lambda E: E.tensor_scalar(out=s_[:, 36:40], in0=s_[:, 36:40], scalar1=mch[:, 0:1], scalar2=None, op0=ALU.mult), r=[s_, mch], w=[s_])
        sc_(44, s_[:, 28:32], s_[:, 8:12])
        sc_(44, s_[:, 44:48], s_[:, 16:20])
        sc_(48, s_[:, 28:32], s_[:, 20:24])
        kb.op('dve', lambda E: E.tensor_scalar(out=s_[:, 52:56], in0=s_[:, 48:52], scalar1=mch[:, 1:2], scalar2=None, op0=ALU.mult), r=[s_, mch], w=[s_])
        kb.op('dve', lambda E: E.tensor_scalar(out=s_[:, 48:52], in0=s_[:, 48:52], scalar1=mch[:, 0:1], scalar2=None, op0=ALU.mult), r=[s_, mch], w=[s_])
        yield
        kx, vx, qx = x_[:, 1, :, :], x_[:, 2, :, :], x_[:, 0, :, :]
        kb.op('pool', lambda E: E.tensor_tensor(out=KBG[:], in0=kx, in1=B4(s_[:, 44:48]), op=ALU.mult), r=[x_, s_], w=[KBG])
        kb.op('pool', lambda E: E.tensor_tensor(out=Kd[0][:], in0=kx, in1=B4(s_[:, 48:52]), op=ALU.mult), r=[x_, s_], w=[Kd[0]])
        kb.op('pool', lambda E: E.tensor_tensor(out=Kd[1][:], in0=kx, in1=B4(s_[:, 52:56]), op=ALU.mult), r=[x_, s_], w=[Kd[1]])
        kb.op('pool', lambda E: E.tensor_tensor(out=VB[:], in0=vx, in1=B4(s_[:, 8:12]), op=ALU.mult), r=[x_, s_], w=[VB])
        yield
        for i_, c0 in enumerate((32, 36, 40, 28)):
            kb.op('dve', lambda E: E.tensor_tensor(out=dg[i_][:], in0=M4(ident[:]), in1=B4(s_[:, c0:c0 + 4]), op=ALU.mult), r=[ident, s_], w=[dg[i_]])
        for i_, (src, dst) in enumerate(((qx, QT), (qx, QG[0]), (qx, QG[1]), (kx, KT))):
            p = PA()
            mm4(p, lambda h: src[:, h, :], lambda h: dg[i_][:, h, :], [x_, dg[i_]])
            if i_ % 2 == 0:
                kb.op('act', lambda E: E.copy(out=dst[:], in_=p[:]), r=[p], w=[dst])
            else:
                kb.op('dve', lambda E: E.tensor_copy(out=dst[:], in_=p[:]), r=[p], w=[dst])
        yield
        p = PA()
        mm4(p, lambda h: KT[:, h, :], lambda h: KT[:, h, :], [KT])
        kb.op('dve', lambda E: E.tensor_tensor(out=tmpA[:], in0=p[:], in1=dec[:], op=ALU.mult), r=[p, dec], w=[tmpA])
        kb.op('pool', lambda E: E.tensor_tensor(out=nbs[:], in0=M4(strict[:]), in1=B4(s_[:, 12:16]), op=ALU.mult), r=[strict, s_], w=[nbs])
        X, Y, P = Xb[0], Yb[0], Pb[0]
        kb.op('pool', lambda E: E.tensor_tensor(out=X[:], in0=tmpA[:], in1=nbs[:], op=ALU.mult), r=[tmpA, nbs], w=[X])
        for h in range(4):
            kb.op('pe', lambda E: E.transpose(out=pB[0][:, h, :], in_=X[:, h, :], identity=ident[:]), r=[X, ident], w=[pB[0]])
        kb.op('act', lambda E: E.copy(out=Y[:], in_=pB[0][:, 0:4, :]), r=[pB[0]], w=[Y])
        kb.op('dve', lambda E: E.tensor_tensor(out=P[:], in0=pB[0][:, 0:4, :], in1=M4(ident[:]), op=ALU.add), r=[pB[0], ident], w=[P])
        yield
        p = PA()
        mm4(p, lambda h: KT[:, h, :], lambda h: QT[:, h, :], [KT, QT])
        kb.op('dve', lambda E: E.tensor_tensor(out=aqkT[:], in0=p[:], in1=decT[:], op=ALU.mult), r=[p, decT], w=[aqkT])
        yield
        for k_ in range(1, 6):
            Xn, Yn, Pn = Xb[k_ % 2], Yb[k_ % 2], Pb[k_ % 2]
            p = PA()
            mm4(p, lambda h: Y[:, h, :], lambda h: X[:, h, :], [X, Y])
            kb.op('act', lambda E: E.copy(out=Xn[:], in_=p[:]), r=[p], w=[Xn])
            if k_ < 5:
                p2 = PA()
                mm4(p2, lambda h: X[:, h, :], lambda h: Y[:, h, :], [X, Y])
                kb.op('dve', lambda E: E.tensor_copy(out=Yn[:], in_=p2[:]), r=[p2], w=[Yn])
            p3 = PA()
            mm4(p3, lambda h: Xn[:, h, :], lambda h: P[:, h, :], [Xn, P])
            kb.op('dve', lambda E: E.tensor_tensor(out=Pn[:], in0=p3[:], in1=P[:], op=ALU.add), r=[p3, P], w=[Pn])
            X, Y, P = Xn, Yn, Pn
            yield
        yield
        p = PA()
        mm4(p, lambda h: KBG[:, h, :], lambda h: P[:, h, :], [KBG, P])
        kb.op('act', lambda E: E.mul(out=negWT[:], in_=p[:], mul=-1.0), r=[p], w=[negWT])
        yield 'B'
        Sa = Sbf[si[0] % 3]
        Sb_ = Sbf[(si[0] + 1) % 3]
        Sc = Sbf[(si[0] + 2) % 3]
        si[0] += 2
        for j, (Scur, Snext) in enumerate(((Sa, Sb_), (Sb_, Sc))):
            for h in range(4):
                kb.op('pe', lambda E: E.matmul(pV[:, h, :], lhsT=P[:, h, :], rhs=VB[:, h, :], start=True, stop=False), r=[P, VB], w=[pV])
                kb.op('pe', lambda E: E.matmul(pV[:, h, :], lhsT=negWT[:, h, :], rhs=Scur[:, h, :], start=False, stop=True),
                      r=[negWT, Scur], w=[pV])
            yield
            r0 = 64 * j
            kb.op('act', lambda E: E.copy(out=vnew[r0:r0 + 64, :, :], in_=pV[r0:r0 + 64, :, :]), r=[pV], w=[vnew])
            yield
            mm4(pdS, lambda h: Kd[j][:, h, :], lambda h: vnew[:, h, :], [Kd[j], vnew])
            yield
            for h in range(4):
                kb.op('dve', lambda E: E.scalar_tensor_tensor(out=Sf[:, h, :], in0=Sf[:, h, :], scalar=s_[:, 56 + 4 * j + h:57 + 4 * j + h],
                                                              in1=pdS[:, h, :], op0=ALU.mult, op1=ALU.add), r=[Sf, s_, pdS], w=[Sf])
            kb.op('act', lambda E: E.copy(out=Snext[:], in_=Sf[:]), r=[Sf], w=[Snext])
            yield
        for h in range(4):
            kb.op('pe', lambda E: E.matmul(pO[:, h, :], lhsT=QG[0][:, h, :], rhs=Sa[:, h, :], start=True, stop=False), r=[QG[0], Sa], w=[pO])
            kb.op('pe', lambda E: E.matmul(pO[:, h, :], lhsT=QG[1][:, h, :], rhs=Sb_[:, h, :], start=False, stop=False), r=[QG[1], Sb_], w=[pO])
            kb.op('pe', lambda E: E.matmul(pO[:, h, :], lhsT=aqkT[:, h, :], rhs=vnew[:, h, :], start=False, stop=True), r=[aqkT, vnew], w=[pO])
        yield
        kb.op('act', lambda E: E.copy(out=osb[:], in_=pO[:]), r=[pO], w=[osb])
        kb.op('pool', lambda E: E.tensor_tensor(out=sq[:], in0=osb[:], in1=osb[:], op=ALU.mult), r=[osb], w=[sq])
        kb.op('dve', lambda E: E.tensor_reduce(out=G[:, 0:4], in_=sq[:], axis=AX.X, op=ALU.add), r=[sq, G], w=[G])
        kb.op('act', lambda E: E.activation(out=G[:, 0:4], in_=G[:, 0:4], func=AF.Sqrt, scale=1.0 / 128, bias=EPS), r=[G], w=[G])
        kb.op('dve', lambda E: E.reciprocal(out=G[:, 0:4], in_=G[:, 0:4]), r=[G], w=[G])
        yield
        kb.op('act', lambda E: E.activation(out=sgt[:], in_=gt[b][:], func=AF.Silu), r=[gt[b]], w=[sgt])
        kb.op('dve', lambda E: E.tensor_tensor(out=osb[:], in0=osb[:], in1=B4(G[:, 0:4]), op=ALU.mult), r=[osb, G], w=[osb])
        kb.op('pool', lambda E: E.tensor_tensor(out=osb[:], in0=osb[:], in1=M4(gw[:]), op=ALU.mult), r=[osb, gw], w=[osb])
        kb.op('dve', lambda E: E.tensor_tensor(out=oa[b][:], in0=osb[:], in1=sgt[:].rearrange("p (h d) -> p h d", h=4), op=ALU.mult),
              r=[osb, sgt], w=[oa[b]])
        for h in range(4):
            kb.op('pe', lambda E: E.transpose(out=pB[0][:, h, :], in_=oa[b][:, h, :], identity=ident[:]), r=[oa[b], ident], w=[pB[0]])
        kb.op('act', lambda E: E.copy(out=oaT[b][:], in_=pB[0][:, 0:4, :]), r=[pB[0]], w=[oaT[b]])
        kb.dma('sp', io['ocatT'][t, :, 0:4, :], oaT[b][:], r=[oaT[b]], w=['ocatT_a'])

    def to_boundary(g):
        for r in g:
            if r == 'B':
                return

    cur = tile(0)
    to_boundary(cur)
    for t in range(NT):
        nxt = tile(t + 1) if t + 1 < NT else None
        cur_done, nxt_done = False, nxt is None
        while not (cur_done and nxt_done):
            if not cur_done:
                try:
                    next(cur)
                except StopIteration:
                    cur_done = True
            if not nxt_done:
                if next(nxt) == 'B':
                    nxt_done = True
        cur = nxt
    ph.close()


def rope16(kb, R, G, cs, tmp, e1='dve', e2='pool'):
    c = cs[:, 0:8].unsqueeze(1).to_broadcast([128, G, 8])
    sn = cs[:, 8:16].unsqueeze(1).to_broadcast([128, G, 8])
    x1 = R[:, 0:G, 0:8]
    x2 = R[:, 0:G, 8:16]
    kb.op(e1, lambda E: E.tensor_tensor(out=tmp[:, 0:G, 0:8], in0=x1, in1=c, op=ALU.mult), r=[R, cs], w=[(tmp, 0)])
    yield
    kb.op(e2, lambda E: E.tensor_tensor(out=tmp[:, 0:G, 8:16], in0=x2, in1=sn, op=ALU.mult), r=[R, cs], w=[(tmp, 1)])
    yield
    kb.op(e1, lambda E: E.tensor_tensor(out=tmp[:, 0:G, 16:24], in0=x2, in1=c, op=ALU.mult), r=[R, cs], w=[(tmp, 2)])
    yield
    kb.op(e2, lambda E: E.tensor_tensor(out=tmp[:, 0:G, 24:32], in0=x1, in1=sn, op=ALU.mult), r=[R, cs], w=[(tmp, 3)])
    yield
    kb.op(e1, lambda E: E.tensor_tensor(out=x1, in0=tmp[:, 0:G, 0:8], in1=tmp[:, 0:G, 8:16], op=ALU.subtract),
          r=[(tmp, 0), (tmp, 1), (tmp, 2), (tmp, 3)], w=[R])
    yield
    kb.op(e1, lambda E: E.tensor_tensor(out=x2, in0=tmp[:, 0:G, 16:24], in1=tmp[:, 0:G, 24:32], op=ALU.add),
          r=[(tmp, 0), (tmp, 1), (tmp, 2), (tmp, 3)], w=[R])
    yield


def rms_groups(kb, src3, G, dst3, sq, ss, wt, e_sq='pool'):
    (src_ap, src_keys) = src3
    (dst_ap, dst_keys) = dst3
    kb.op(e_sq, lambda E: E.tensor_tensor(out=sq[:, 0:G, :], in0=src_ap, in1=src_ap, op=ALU.mult), r=src_keys, w=[sq])
    yield
    kb.op('dve', lambda E: E.tensor_reduce(out=ss[:, 0:G], in_=sq[:, 0:G, :], axis=AX.X, op=ALU.add), r=[sq], w=[ss])
    yield
    kb.op('act', lambda E: E.activation(out=ss[:, 0:G], in_=ss[:, 0:G], func=AF.Sqrt, scale=1.0 / 64, bias=EPS), r=[ss], w=[ss])
    yield
    kb.op('dve', lambda E: E.reciprocal(out=ss[:, 0:G], in_=ss[:, 0:G]), r=[ss], w=[ss])
    yield
    kb.op('dve', lambda E: E.tensor_tensor(out=dst_ap, in0=src_ap, in1=ss[:, 0:G].unsqueeze(2).to_broadcast([128, G, 64]), op=ALU.mult),
          r=src_keys + [ss], w=dst_keys)
    yield
    kb.op('pool', lambda E: E.tensor_tensor(out=dst_ap, in0=dst_ap, in1=wt[:].unsqueeze(1).to_broadcast([128, G, 64]), op=ALU.mult),
          r=dst_keys + [wt], w=dst_keys)
    yield


def phase_nsa(kb, io):
    nc = kb.nc
    ph = Phase(kb, "ns")
    ident = ph.sb("ident", [128, 128], BF16)
    tril = ph.sb("tril", [128, 128], BF16)
    far = ph.sb("far", [128, 128], BF16)
    cmask = ph.sb("cmask", [128, 9, 512], BF16)
    kvT = ph.sb("kvT", [128, 4, S], BF16)
    vs1 = ph.sb("vs1", [128, NT, 2, 65], BF16)
    vw1 = ph.sb("vw1", [128, NT, 2, 65], BF16)
    kcmpT = ph.sb("kcmpT", [64, 2, 256], BF16)
    rhs_cmp = ph.sb("rhs_cmp", [128, 2, 2, 128], BF16)
    kb.dma('sp', ident[:], io['c_ident'][:, :], w=[ident])
    kb.dma('sp', tril[:], io['c_tril'][:, :], w=[tril])
    kb.dma('sp', far[:], io['c_far'][:, :], w=[far])
    for m in range(9):
        kb.dma('sp', cmask[:, m, :], io['c_cmask'][m, :, :], w=[cmask])
    for g in range(2):
        kb.dma('sp', kvT[64:128, g, :], io['c_E'][:, :], w=[(kvT, 'E')])
    kb.op('pool', lambda E: E.memset(vs1[:], 1.0), w=[vs1])
    kb.op('pool', lambda E: E.memset(vw1[:], 1.0), w=[vw1])
    kb.op('pool', lambda E: E.memset(kcmpT[:], 0.0), w=[kcmpT])
    kb.op('pool', lambda E: E.memset(rhs_cmp[:], 0.0), w=[rhs_cmp])
    for bt in range(2):
        for g in range(2):
            kb.dma('sp', rhs_cmp[:, bt, g, 64:128], io['c_ovl'][:, bt, :], r=[rhs_cmp], w=[rhs_cmp])

    pp = Phase(kb, "np")
    kcT = pp.sb("kcT", [64, 4, S], BF16)
    ksw = pp.sb("ksw", [128, 64], F32)
    kww = pp.sb("kww", [128, 64], F32)
    kcw = pp.sb("kcw", [128, 64], F32)
    kb.dma('sp', ksw[:], io['nsa_ks_norm_w'].partition_broadcast(128), w=[ksw])
    kb.dma('sp', kww[:], io['nsa_kw_norm_w'].partition_broadcast(128), w=[kww])
    kb.dma('sp', kcw[:], io['nsa_kc_norm_w'].partition_broadcast(128), w=[kcw])
    kvb = pp.sbn("kvb", [128, 768], F32, 2)
    cst = pp.sbn("cst", [128, 16], F32, 2)
    R = pp.sbn("R", [128, 6, 64], F32, 2)
    sqb = pp.sbn("sq", [128, 2, 64], F32, 2)
    ssb = pp.sbn("ss", [128, 2], F32, 2)
    tmpb = pp.sbn("tmp", [128, 6, 32], F32, 2)
    sq, ss = sqb[0], ssb[0]
    k16 = pp.sbn("k16", [128, 8, 64], BF16, 2)
    pT8 = pp.psn("pT8", [128, 8, 128], BF16, 2)
    def kvtile(t):
        b = t % 2
        kv = kvb[b]
        Rb = R[b]
        sq, ss, tmp = sqb[b], ssb[b], tmpb[b]
        kb.dma('sp', kv[:], io['tm'][t * 128:(t + 1) * 128, KC_OFF:KC_OFF + 768], r=['tm'], w=[kv])
        kb.dma('sp', cst[b][:], io['c_rope'][t * 128:(t + 1) * 128, :], w=[cst[b]])
        v3 = lambda off: kv[:, off:off + 128].rearrange("p (g d) -> p g d", g=2)
        kb.op('pool', lambda E: E.tensor_copy(out=Rb[:, 0:2, :], in_=v3(0)), r=[kv], w=[Rb])
        yield
        yield from rms_groups(kb, (v3(256), [kv]), 2, (Rb[:, 2:4, :], [Rb]), sq, ss, ksw)
        yield from rms_groups(kb, (v3(512), [kv]), 2, (Rb[:, 4:6, :], [Rb]), sq, ss, kww)
        yield from rope16(kb, Rb, 6, cst[b], tmp)
        kk = k16[b]
        kb.op('act', lambda E: E.copy(out=kk[:, 0:6, :], in_=Rb[:]), r=[Rb], w=[kk])
        kb.op('pool', lambda E: E.tensor_copy(out=kk[:, 6:8, :], in_=v3(128)), r=[kv], w=[kk])
        kb.op('dve', lambda E: E.tensor_copy(out=vs1[:, t, :, 0:64], in_=v3(384)), r=[kv], w=[vs1])
        kb.op('pool', lambda E: E.tensor_copy(out=vw1[:, t, :, 0:64], in_=v3(640)), r=[kv], w=[vw1])
        yield
        p8 = pT8[b]
        for i in range(8):
            kb.op('pe', lambda E: E.transpose(out=p8[0:64, i, :], in_=kk[:, i, :], identity=ident[:]), r=[kk, ident], w=[p8])
        kb.op('act', lambda E: E.copy(out=kvT[0:64, :, t * 128:(t + 1) * 128], in_=p8[0:64, 2:6, :]), r=[p8], w=[(kvT, 'k')])
        kb.op('dve', lambda E: E.tensor_copy(out=kcT[0:64, 0:2, t * 128:(t + 1) * 128], in_=p8[0:64, 0:2, :]), r=[p8], w=[kcT])
        kb.op('dve', lambda E: E.tensor_copy(out=kcT[0:64, 2:4, t * 128:(t + 1) * 128], in_=p8[0:64, 6:8, :]), r=[p8], w=[kcT])

    def interleave(gens):
        gens = list(gens)
        while gens:
            for g_ in list(gens):
                try:
                    next(g_)
                except StopIteration:
                    gens.remove(g_)

    for t in range(0, NT, 2):
        interleave([kvtile(t), kvtile(t + 1)])
    w1f = pp.sb("w1f", [64, 32, 64], F32)
    w1b = pp.sbn("w1b", [64, 32, 64], BF16, 2)
    w2f = pp.sb("w2f", [64, 64], F32)
    w2b = pp.sbn("w2b", [64, 64], BF16, 2)
    posf = pp.sb("posf", [64, 32], F32)
    pos2 = pp.sbn("pos2", [64, 32, 2], BF16, 2)
    bias = pp.sb("bias", [64, 2], F32)
    h1T = pp.sb("h1T", [64, 256], BF16)
    o2 = pp.sb("o2", [128, 1, 64], F32)
    o2n = pp.sb("o2n", [128, 1, 64], F32)
    kcn = pp.sb("kcn", [128, 64], BF16)
    pH = pp.ps("pH", [128, 512], F32)
    pB_ = pp.ps("pBi", [128, 512], F32)
    pO2 = pp.ps("pO2", [128, 512], F32)
    kb.op('pool', lambda E: E.memset(h1T[:], 0.0), w=[h1T])
    for kind, (n1, n2, npos) in enumerate((('nsa_cmp_k_w1', 'nsa_cmp_k_w2', 'nsa_cmp_pos_k'), ('nsa_cmp_v_w1', 'nsa_cmp_v_w2', 'nsa_cmp_pos_v'))):
        kb.dma('sp', w1f[:], io[n1].rearrange("(l d) o -> d l o", d=64), w=[w1f])
        kb.op('dve', lambda E: E.tensor_copy(out=w1b[kind][:], in_=w1f[:]), r=[w1f], w=[w1b[kind]])
        kb.dma('sp', w2f[:], io[n2][:, :], w=[w2f])
        kb.op('dve', lambda E: E.tensor_copy(out=w2b[kind][:], in_=w2f[:]), r=[w2f], w=[w2b[kind]])
        kb.dma('sp', posf[:], io[npos].rearrange("l d -> d l"), w=[posf], allow_slow_non_contiguous=True)
        for j in range(2):
            kb.op('dve', lambda E: E.tensor_copy(out=pos2[kind][:, :, j], in_=posf[:]), r=[posf], w=[pos2[kind]])
        for l in range(32):
            kb.op('pe', lambda E: E.matmul(pB_[0:64, 0:2], lhsT=w1b[kind][:, l, :], rhs=pos2[kind][:, l, :], start=(l == 0), stop=(l == 31)),
                  r=[w1b[kind], pos2[kind]], w=[pB_])
        kb.op('dve', lambda E: E.tensor_copy(out=bias[:], in_=pB_[0:64, 0:2]), r=[pB_], w=[bias])
        for g in range(2):
            ki = kind * 2 + g
            for l in range(32):
                kb.op('pe', lambda E: E.matmul(pH[0:64, 0:255], lhsT=w1b[kind][:, l, :], rhs=kcT[0:64, ki, l:l + 16 * 254 + 1:16],
                                               start=(l == 0), stop=(l == 31)), r=[w1b[kind], kcT], w=[pH])
            kb.op('act', lambda E: E.activation(out=h1T[:, 0:255], in_=pH[0:64, 0:255], func=AF.Silu, bias=bias[:, 0:1]),
                  r=[pH, bias], w=[h1T])
            for bt in range(2):
                kb.op('pe', lambda E: E.matmul(pO2[:, 0:64], lhsT=h1T[:, bt * 128:(bt + 1) * 128], rhs=w2b[kind][:], start=True, stop=True),
                      r=[h1T, w2b[kind]], w=[pO2])
                if kind == 0:
                    kb.op('act', lambda E: E.copy(out=o2[:, 0, :], in_=pO2[:, 0:64]), r=[pO2], w=[o2])
                    for _ in rms_groups(kb, (o2[:], [o2]), 1, (o2n[:], [o2n]), sq, ss, kcw):
                        pass
                    kb.op('act', lambda E: E.copy(out=kcn[:], in_=o2n[:, 0, :]), r=[o2n], w=[kcn])
                    p8 = pT8[0]
                    kb.op('pe', lambda E: E.transpose(out=p8[0:64, 0, :], in_=kcn[:], identity=ident[:]), r=[kcn, ident], w=[p8])
                    kb.op('act', lambda E: E.copy(out=kcmpT[:, g, bt * 128:(bt + 1) * 128], in_=p8[0:64, 0, :]), r=[p8], w=[kcmpT])
                else:
                    kb.op('act', lambda E: E.copy(out=rhs_cmp[:, bt, g, 0:64], in_=pO2[:, 0:64]), r=[pO2], w=[rhs_cmp])
    pp.close()

    pa = Phase(kb, "na")
    NPT = 8
    PTb = pa.sbn("PT", [128, 512], BF16, NPT)
    pti = [0]
    qaug = pa.sbn("qaug", [128, 8, 512], BF16, 2)
    qnw = pa.sb("qnw", [128, 64], F32)
    kb.dma('sp', qnw[:], io['nsa_q_norm_w'].partition_broadcast(128), w=[qnw])
    kb.op('dve', lambda E: E.tensor_scalar(out=qnw[:], in0=qnw[:], scalar1=0.125, scalar2=None, op0=ALU.mult), r=[qnw], w=[qnw])
    qf = pa.sbn("qf", [128, 512], F32, 4)
    cst = pa.sbn("cst", [128, 16], F32, 4)
    Rqb = pa.sbn("Rq", [128, 8, 64], F32, 4)
    sqb = pa.sbn("sq", [128, 8, 64], F32, 4)
    ssb = pa.sbn("ss", [128, 8], F32, 4)
    tmpb = pa.sbn("tmp", [128, 8, 32], F32, 4)
    qa = pa.sbn("qa", [128, 8, 128], BF16, 4)
    gts = pa.sb("gts", [128, 4, 24], F32)
    Ab = pa.sbn("Ab", [128, 64], F32, 4)
    Bb = pa.sbn("Bb", [128, 64], F32, 4)
    ob = pa.sb("ob", [128, 4, 8, 64], F32)
    impacc = pa.sb("impacc", [128, 4, 64], F32)
    tmpi = pa.sb("tmpi", [128, 4, 64], F32)
    tmpo = pa.sb("tmpo", [128, 4, 64], F32)
    scr = pa.sb("scr", [128, 64], F32)
    scr2 = pa.sb("scr2", [128, 64], F32)
    m8 = pa.sb("m8", [128, 16], F32)
    nst = pa.sbn("nst", [128, 128], BF16, 2)
    fs = pa.sb("fs", [128, 12], F32)
    obb = pa.sbn("obb", [128, 512], BF16, 2)
    obT = pa.sbn("obT", [128, 4, 128], BF16, 2)
    sc_ps = pa.psn("sc", [128, 512], F32, 3)
    oC = pa.ps("oC", [128, 4, 128], F32)
    oW = pa.psn("oW", [128, 4, 128], F32, 1)
    oS = pa.psn("oS", [128, 4, 128], F32, 2)
    pTr = pa.psn("pTr", [128, 8, 128], BF16, 1)
    for i in range(4):
        kb.op('pool', lambda E: E.memset(qa[i][:], 0.0), w=[qa[i]])
    for i in range(2):
        kb.op('pool', lambda E: E.memset(nst[i][:], 0.0), w=[nst[i]])
    tri = 0

    def finalize(oX, h, br, first, cmp=False):
        if cmp:
            kb.op('dve', lambda E: E.tensor_reduce(out=fs[:, 0:4], in_=oX[:, :, 64:128], axis=AX.X, op=ALU.add), r=[oX], w=[fs])
            kb.op('dve', lambda E: E.tensor_scalar(out=fs[:, 0:4], in0=fs[:, 0:4], scalar1=0.5, scalar2=1e-30, op0=ALU.mult, op1=ALU.add),
                  r=[fs], w=[fs])
        else:
            kb.op('dve', lambda E: E.tensor_scalar(out=fs[:, 0:4], in0=oX[:, :, 64], scalar1=1e-30, scalar2=None, op0=ALU.add), r=[oX], w=[fs])
        kb.op('dve', lambda E: E.reciprocal(out=fs[:, 4:8], in_=fs[:, 0:4]), r=[fs], w=[fs])
        if cmp:
            dsti = impacc if h % 4 == 0 else tmpi
            kb.op('dve', lambda E: E.tensor_tensor(out=dsti[:], in0=oX[:, :, 64:128], in1=fs[:, 4:8].unsqueeze(2).to_broadcast([128, 4, 64]),
                                                   op=ALU.mult), r=[oX, fs], w=[dsti])
            if h % 4 != 0:
                kb.op('pool', lambda E: E.tensor_tensor(out=impacc[:], in0=impacc[:], in1=tmpi[:], op=ALU.add), r=[impacc, tmpi], w=[impacc])
        kb.op('dve', lambda E: E.tensor_tensor(out=fs[:, 8:12], in0=fs[:, 4:8], in1=gts[:, :, h * 3 + br], op=ALU.mult), r=[fs, gts], w=[fs])
        dst = ob[:, :, h, :] if first else tmpo[:]
        kb.op('dve', lambda E: E.tensor_tensor(out=dst, in0=oX[:, :, 0:64], in1=fs[:, 8:12].unsqueeze(2).to_broadcast([128, 4, 64]),
                                               op=ALU.mult), r=[oX, fs], w=[(ob, h) if first else tmpo])
        if not first:
            kb.op('pool', lambda E: E.tensor_tensor(out=ob[:, :, h, :], in0=ob[:, :, h, :], in1=tmpo[:], op=ALU.add),
                  r=[(ob, h), tmpo], w=[(ob, h)])

    for s in range(S // 512):
        qs_ = qaug[s % 2]
        def qtile(t4, s=s, qs_=qs_):
            t = 4 * s + t4
            b = t4
            q = qf[b]
            Rq, sq, ss, tmp = Rqb[b], sqb[b], ssb[b], tmpb[b]
            kb.dma('sp', q[:], io['tm'][t * 128:(t + 1) * 128, NQ_OFF:NQ_OFF + 512], r=['tm'], w=[q])
            kb.dma('sp', gts[:, t4, :], io['tm'][t * 128:(t + 1) * 128, NG_OFF:NG_OFF + 24], r=['tm'], w=[gts])
            kb.dma('sp', cst[b][:], io['c_rope'][t * 128:(t + 1) * 128, :], w=[cst[b]])
            kb.dma('sp', Ab[t4][:], io['c_A'][t, :, :], w=[Ab[t4]])
            kb.dma('sp', Bb[t4][:], io['c_B'][t, :, :], w=[Bb[t4]])
            kb.op('act', lambda E: E.activation(out=gts[:, t4, :], in_=gts[:, t4, :], func=AF.Exp, scale=-1.0), r=[gts], w=[gts])
            kb.op('dve', lambda E: E.tensor_scalar(out=gts[:, t4, :], in0=gts[:, t4, :], scalar1=1.0, scalar2=None, op0=ALU.add), r=[gts], w=[gts])
            kb.op('dve', lambda E: E.reciprocal(out=gts[:, t4, :], in_=gts[:, t4, :]), r=[gts], w=[gts])
            q3 = q[:].rearrange("p (h d) -> p h d", h=8)
            yield
            yield from rms_groups(kb, (q3, [q]), 8, (Rq[:], [Rq]), sq, ss, qnw)
            yield from rope16(kb, Rq, 8, cst[b], tmp)
            qab = qa[b]
            kb.op('act', lambda E: E.copy(out=qab[:, :, 0:64], in_=Rq[:]), r=[Rq], w=[qab])
            yield
            pt_ = pTr[0]
            for h in range(8):
                kb.op('pe', lambda E: E.transpose(out=pt_[:, h, :], in_=qab[:, h, :], identity=ident[:]), r=[qab, ident], w=[pt_])
            kb.op('act', lambda E: E.copy(out=qs_[:, :, t4 * 128:(t4 + 1) * 128], in_=pt_[:]), r=[pt_], w=[qs_])
        interleave([qtile(i) for i in range(4)])
        nbt = 1 if s < 4 else 2
        for g in range(2):
            for h in range(4 * g, 4 * g + 4):
                specs = []
                for bt in range(nbt):
                    m = (s if s <= 4 else None) if bt == 0 else 5 + (s - 4)
                    masks = [] if m is None else [(0, 512, cmask[:, m, :])]
                    specs.append(dict(lhsT=kcmpT[0:64, g, bt * 128:(bt + 1) * 128],
                                      rhs_fn=lambda q0, q1, h=h: qs_[0:64, h, q0 * 128:q1 * 128], qt0=0, qt1=4, masks=masks, mk=[cmask],
                                      v_fn=lambda qt, bt=bt, g=g: rhs_cmp[:, bt, g, :], nk=128, rk=[kcmpT], rv=[rhs_cmp]))
                attn_block(kb, lambda qt: (oC[:, qt, :], oC, 'C'), PTb, pti, sc_ps, specs, ident, [qs_])
                finalize(oC, h, 0, True, cmp=True)
            for qt in range(4):
                ns_ = nst[qt % 2]
                kb.op('dve', lambda E: E.tensor_tensor(out=scr[:], in0=impacc[:, qt, :], in1=Ab[qt][:], op=ALU.mult), r=[impacc, Ab[qt]], w=[scr])
                kb.op('dve', lambda E: E.tensor_tensor(out=scr[:], in0=scr[:], in1=Bb[qt][:], op=ALU.add), r=[scr, Bb[qt]], w=[scr])
                kb.op('dve', lambda E: E.max(out=m8[:, 0:8], in_=scr[:]), r=[scr], w=[(m8, 0)])
                kb.op('dve', lambda E: E.match_replace(out=scr2[:], in_to_replace=m8[:, 0:8], in_values=scr[:], imm_value=-1e30),
                      r=[scr, (m8, 0)], w=[scr2])
                kb.op('dve', lambda E: E.max(out=m8[:, 8:16], in_=scr2[:]), r=[scr2], w=[(m8, 1)])
                kb.op('dve', lambda E: E.tensor_scalar(out=ns_[:, 64:128], in0=scr[:], scalar1=m8[:, 15:16], scalar2=1.0, op0=ALU.is_ge,
                                                       op1=ALU.subtract), r=[scr, (m8, 1)], w=[ns_])
                pt_ = pTr[0]
                tri += 1
                kb.op('pe', lambda E: E.transpose(out=pt_[:, 0, :], in_=ns_[:], identity=ident[:]), r=[ns_, ident], w=[pt_])
                for h in range(4 * g, 4 * g + 4):
                    if h % 2 == 0:
                        kb.op('act', lambda E: E.copy(out=qs_[64:128, h, qt * 128:(qt + 1) * 128], in_=pt_[64:128, 0, :]), r=[pt_], w=[qs_])
                    else:
                        kb.op('dve', lambda E: E.tensor_copy(out=qs_[64:128, h, qt * 128:(qt + 1) * 128], in_=pt_[64:128, 0, :]), r=[pt_], w=[qs_])
        for h in range(8):
            g = h // 4
            specs = []
            for kt in range(max(0, 4 * s - 4), 4 * s + 4):
                lo = max(kt - 4 * s, 0)
                hi = min(kt + 4 - 4 * s, 3)
                masks = []
                if kt >= 4 * s:
                    masks.append(((kt - 4 * s - lo) * 128, 128, tril[:]))
                if kt + 4 <= 4 * s + 3:
                    masks.append(((kt + 4 - 4 * s - lo) * 128, 128, far[:]))
                specs.append(dict(lhsT=kvT[0:64, 2 + g, kt * 128:(kt + 1) * 128],
                                  rhs_fn=lambda q0, q1, h=h: qs_[0:64, h, q0 * 128:q1 * 128], qt0=lo, qt1=hi + 1, masks=masks, mk=[tril, far],
                                  v_fn=lambda qt, kt=kt, g=g: vw1[:, kt, g, :], nk=128, rk=[(kvT, 'k')], rv=[vw1]))
            oW_ = oW[0]
            attn_block(kb, lambda qt: (oW_[:, qt, 0:65], oW_, 'W'), PTb, pti, sc_ps, specs, ident, [qs_], LA=3)
            finalize(oW_, h, 2, False)
            specs = []
            for kt in range(0, 4 * s + 4):
                lo = max(kt - 4 * s, 0)
                masks = [(0, 128, tril[:])] if kt >= 4 * s else []
                specs.append(dict(lhsT=kvT[:, g, kt * 128:(kt + 1) * 128],
                                  rhs_fn=lambda q0, q1, h=h: qs_[:, h, q0 * 128:q1 * 128], qt0=lo, qt1=4, masks=masks, mk=[tril],
                                  v_fn=lambda qt, kt=kt, g=g: vs1[:, kt, g, :], nk=128, rk=[(kvT, 'k'), (kvT, 'E')], rv=[vs1]))
            oS_ = oS[h % 2]
            attn_block(kb, lambda qt: (oS_[:, qt, 0:65], oS_, 'S'), PTb, pti, sc_ps, specs, ident, [qs_], LA=3)
            finalize(oS_, h, 1, False)
        for qt in range(4):
            t = 4 * s + qt
            b = t % 2
            kb.op('act', lambda E: E.copy(out=obb[b][:], in_=ob[:, qt, :, :].rearrange("p h d -> p (h d)")), r=[(ob, h) for h in range(8)], w=[obb[b]])
            pt_ = pTr[0]
            tri += 1
            for c in range(4):
                kb.op('pe', lambda E: E.transpose(out=pt_[:, c, :], in_=obb[b][:, c * 128:(c + 1) * 128], identity=ident[:]), r=[obb[b], ident], w=[pt_])
            kb.op('act', lambda E: E.copy(out=obT[b][:], in_=pt_[:, 0:4, :]), r=[pt_], w=[obT[b]])
            kb.dma('sp', io['ocatT'][t, :, 4:8, :], obT[b][:], r=[obT[b]], w=['ocatT_b'])
    pa.close()
    ph.close()


def phase_ffn(kb, io):
    nc = kb.nc
    ph = Phase(kb, "f1")
    ident = ph.sb("ident", [128, 128], BF16)
    kb.dma('sp', ident[:], io['c_ident'][:, :], w=[ident])
    woutb = ph.sb("woutb", [128, 12, D], BF16)
    wupb = ph.sb("wupb", [128, 8, 2 * FF], BF16)
    gam = ph.sb("gam", [128, 8], F32)
    cw = ph.sb("cw", [128, 3, 44], F32)
    kb.dma('sp', gam[:], io['ffn_norm_w'].rearrange("(c p) -> p c", p=128), w=[gam], allow_slow_non_contiguous=True)
    for j in range(3):
        kb.dma('sp', cw[:, j, :], io['ffn_conv_w'][j, :].rearrange("(c p) -> p c", p=128), w=[cw], allow_slow_non_contiguous=True)
    load_cast_weight(kb, ph, woutb, io['w_out'], 12, D)
    load_cast_weight(kb, ph, wupb, io['ffn_w_up'], 8, 2 * FF, gam=gam, stage_cols=1408)

    oT = ph.sbn("oT", [128, 12, 128], BF16, 2)
    xt = ph.sbn("xt", [128, D], F32, 2)
    hs = ph.sbn("hs", [128, D], F32, 2)
    junk = ph.sb("junk", [128, D], BF16)
    ss = ph.sbn("ss", [128, 1], F32, 2)
    rs = ph.sbn("rs", [128, 1], F32, 2)
    hn = ph.sbn("hn", [128, D], BF16, 2)
    hnT = ph.sbn("hnT", [128, 8, 512], BF16, 2)
    ug = ph.sbn("ug", [128, 514], F32, 2)
    uv = ph.sbn("uv", [128, 514], F32, 2)
    ag = ph.sbn("ag", [128, 512], F32, 2)
    av = ph.sbn("av", [128, 512], F32, 2)
    sg = ph.sbn("sg", [128, 512], F32, 2)
    act = ph.sbn("act", [128, 512], BF16, 3)
    halo = ph.sb("halo", [128, 44, 2], F32)
    pH = ph.psn("pH", [128, 512], F32, 2)
    pT = ph.ps("pT", [128, 8, 128], BF16)
    pU = ph.psn("pU", [128, 512], F32, 4)
    kb.op('pool', lambda E: E.memset(halo[:], 0.0), w=[halo])

    def conv3(ps_, dst, src, fb):
        kb.op('act', lambda E: E.activation(out=dst[:], in_=ps_[:], func=AF.Copy, scale=cw[:, 2, fb:fb + 1]), r=[ps_, cw], w=[dst])
        for j in range(2):
            kb.op('dve', lambda E: E.scalar_tensor_tensor(out=dst[:], in0=src[:, j:j + 512], scalar=cw[:, j, fb:fb + 1], in1=dst[:],
                                                       op0=ALU.mult, op1=ALU.add), r=[(src, 'b'), (src, 'h'), cw, dst], w=[dst])

    def ftiles(s):
        hT = hnT[s % 2]
        for t4 in range(4):
            t = s * 4 + t4
            b = t % 2
            kb.dma('sp', oT[b][:], io['ocatT'][t, :, :, :],
                   r=['ocatT_a', 'ocatT_b', 'ocatT_c'], w=[oT[b]])
            kb.dma('sp', xt[b][:], io['x'][t * 128:(t + 1) * 128, :], w=[xt[b]])
            yield
            for half in range(2):
                p = pH[half]
                for c in range(12):
                    kb.op('pe', lambda E: E.matmul(p[:], lhsT=oT[b][:, c, :], rhs=woutb[:, c, half * 512:(half + 1) * 512],
                                                   start=(c == 0), stop=(c == 11)), r=[oT[b], (woutb, c)], w=[p])
                kb.op('dve', lambda E: E.tensor_tensor(out=hs[b][:, half * 512:(half + 1) * 512], in0=p[:],
                                                       in1=xt[b][:, half * 512:(half + 1) * 512], op=ALU.add),
                      r=[p, xt[b]], w=[(hs[b], half)])
            yield
            kb.dma('pool', io['h_s'][t * 128:(t + 1) * 128, :], hs[b][:], r=[(hs[b], 0), (hs[b], 1)], w=['h_s'])
            rms_rstd(kb, hs[b][:], junk, ss[b], rs[b], D, [(hs[b], 0), (hs[b], 1)])
            kb.op('act', lambda E: E.activation(out=hn[b][:], in_=hs[b][:], func=AF.Copy, scale=rs[b][:, 0:1]),
                  r=[(hs[b], 0), (hs[b], 1), rs[b]], w=[hn[b]])
            yield
            yield
            for c in range(8):
                kb.op('pe', lambda E: E.transpose(out=pT[:, c, :], in_=hn[b][:, c * 128:(c + 1) * 128], identity=ident[:]),
                      r=[hn[b], ident], w=[pT])
            kb.op('act', lambda E: E.copy(out=hT[:, :, t4 * 128:(t4 + 1) * 128], in_=pT[:]), r=[pT], w=[hT])
            yield

    def fup(s):
        hT = hnT[s % 2]
        for fb in range(22):
            k2 = fb % 2
            pg = pU[2 * k2]
            pv = pU[2 * k2 + 1]
            for (p_, f0) in ((pg, fb), (pv, 22 + fb)):
                for c in range(8):
                    kb.op('pe', lambda E: E.matmul(p_[:], lhsT=wupb[:, c, f0 * 128:(f0 + 1) * 128], rhs=hT[:, c, :],
                                                   start=(c == 0), stop=(c == 7)), r=[hT, (wupb, c)], w=[p_])
            yield
            g_, v_ = ug[k2], uv[k2]
            kb.op('act', lambda E: E.copy(out=g_[:, 2:514], in_=pg[:]), r=[pg], w=[(g_, 'b')])
            kb.op('act', lambda E: E.copy(out=v_[:, 2:514], in_=pv[:]), r=[pv], w=[(v_, 'b')])
            for (u_, f0) in ((g_, fb), (v_, 22 + fb)):
                kb.op('dve', lambda E: E.tensor_copy(out=u_[:, 0:2], in_=halo[:, f0, :]), r=[(halo, f0)], w=[(u_, 'h')])
                kb.op('dve', lambda E: E.tensor_copy(out=halo[:, f0, :], in_=u_[:, 512:514]), r=[(u_, 'b')], w=[(halo, f0)])
            conv3(pg, ag[k2], g_, fb)
            conv3(pv, av[k2], v_, 22 + fb)
            kb.op('act', lambda E: E.activation(out=sg[k2][:], in_=ag[k2][:], func=AF.Silu), r=[ag[k2]], w=[sg[k2]])
            a_ = act[fb % 3]
            kb.op('dve', lambda E: E.tensor_tensor(out=a_[:], in0=sg[k2][:], in1=av[k2][:], op=ALU.mult), r=[sg[k2], av[k2]], w=[a_])
            kb.dma('pool', io['actT'][4 * s:4 * s + 4, :, fb, :].rearrange("j p t -> p j t"), a_[:].rearrange("p (j t) -> p j t", j=4), r=[a_], w=['actT'])
            yield

    def interleave(gens):
        gens = list(gens)
        while gens:
            for g_ in list(gens):
                try:
                    next(g_)
                except StopIteration:
                    gens.remove(g_)

    NS = S // 512
    interleave([ftiles(0)])
    for s in range(NS):
        gl = [fup(s)]
        if s + 1 < NS:
            gl.append(ftiles(s + 1))
        interleave(gl)
    ph.close()

    ph = Phase(kb, "f2")
    wdnb = ph.sb("wdnb", [128, 22, D], BF16)
    load_cast_weight(kb, ph, wdnb, io['ffn_w_down'], 22, D)
    aT = ph.sbn("aT", [128, 22, 128], BF16, 3)
    hs = ph.sbn("hs", [128, D], F32, 3)
    ot = ph.sbn("ot", [128, D], F32, 2)
    pD = ph.psn("pD", [128, 512], F32, 4)

    def ld(t):
        kb.dma('sp', aT[t % 3][:], io['actT'][t, :, :, :], r=['actT'], w=[aT[t % 3]])
        kb.dma('sp', hs[t % 3][:], io['h_s'][t * 128:(t + 1) * 128, :], r=['h_s'], w=[hs[t % 3]])

    ld(0)
    ld(1)
    for t in range(NT):
        b = t % 2
        a_, h_ = aT[t % 3], hs[t % 3]
        if t + 2 < NT:
            ld(t + 2)
        for half in range(2):
            p = pD[2 * b + half]
            for c in range(22):
                kb.op('pe', lambda E: E.matmul(p[:], lhsT=a_[:, c, :], rhs=wdnb[:, c, half * 512:(half + 1) * 512],
                                               start=(c == 0), stop=(c == 21)), r=[a_, (wdnb, c)], w=[p])
            kb.op('dve', lambda E: E.tensor_tensor(out=ot[b][:, half * 512:(half + 1) * 512], in0=p[:],
                                                   in1=h_[:, half * 512:(half + 1) * 512], op=ALU.add),
                  r=[p, h_], w=[(ot[b], half)])
        kb.dma('pool', io['out'][t * 128:(t + 1) * 128, :], ot[b][:], r=[(ot[b], 0), (ot[b], 1)], w=['out'])
    ph.close()


W_NAMES = ['attn_norm_w', 'mem_norm_w', 'w_in', 'gdn_conv_w', 'gdn_a_log', 'gdn_dt_bias', 'gdn_out_norm_w',
           'nsa_q_norm_w', 'nsa_kc_norm_w', 'nsa_ks_norm_w', 'nsa_kw_norm_w', 'nsa_cmp_pos_k', 'nsa_cmp_pos_v',
           'nsa_cmp_k_w1', 'nsa_cmp_k_w2', 'nsa_cmp_v_w1', 'nsa_cmp_v_w2', 'mem_w_kv', 'mem_q_norm_w', 'mem_k_norm_w',
           'w_out', 'ffn_norm_w', 'ffn_w_up', 'ffn_conv_w', 'ffn_w_down']
W_SHAPES = {
    'attn_norm_w': [D], 'mem_norm_w': [D], 'w_in': [D, INW], 'gdn_conv_w': [4, 1536], 'gdn_a_log': [4], 'gdn_dt_bias': [4],
    'gdn_out_norm_w': [128], 'nsa_q_norm_w': [64], 'nsa_kc_norm_w': [64], 'nsa_ks_norm_w': [64], 'nsa_kw_norm_w': [64],
    'nsa_cmp_pos_k': [32, 64], 'nsa_cmp_pos_v': [32, 64], 'nsa_cmp_k_w1': [2048, 64], 'nsa_cmp_k_w2': [64, 64],
    'nsa_cmp_v_w1': [2048, 64], 'nsa_cmp_v_w2': [64, 64], 'mem_w_kv': [D, 1024], 'mem_q_norm_w': [128], 'mem_k_norm_w': [128],
    'w_out': [1536, D], 'ffn_norm_w': [D], 'ffn_w_up': [D, 2 * FF], 'ffn_conv_w': [3, 2 * FF], 'ffn_w_down': [FF, D],
}


def make_consts():
    c = {}
    c['c_ident'] = np.eye(128, dtype=np.float32).astype(ml_dtypes.bfloat16)
    idx = np.arange(128)
    same = (idx[:, None] // 64) == (idx[None, :] // 64)
    c['c_btri'] = (same & (idx[:, None] <= idx[None, :])).astype(np.float32)
    c['c_bones'] = same.astype(np.float32)
    c['c_strict'] = (same & (idx[:, None] > idx[None, :])).astype(np.float32)
    c['c_mlow'] = np.where(same & (idx[:, None] >= idx[None, :]), 0.0, NEG).astype(np.float32)
    c['c_mup'] = np.ascontiguousarray(c['c_mlow'].T)
    c['c_mch'] = np.stack([(idx < 64), (idx >= 64)], axis=1).astype(np.float32)
    bf = ml_dtypes.bfloat16
    c['c_tril'] = np.where(idx[:, None] <= idx[None, :], 0.0, NEG).astype(np.float32).astype(bf)
    c['c_far'] = np.where(idx[None, :] < idx[:, None], 0.0, NEG).astype(np.float32).astype(bf)
    cm = np.zeros((9, 128, 512), np.float32)
    f = np.arange(512)
    for m in range(9):
        bt, s_ = (0, m) if m < 5 else (1, m - 1)
        blk = 128 * bt + idx
        vis = (16 * blk[:, None] + 31 <= 512 * s_ + f[None, :]) & (blk[:, None] < 255)
        cm[m] = np.where(vis, 0.0, NEG)
    c['c_cmask'] = cm.astype(bf)
    kk = np.arange(S)
    c['c_E'] = np.where((kk[None, :] // 64) == np.arange(64)[:, None], -NEG, 0.0).astype(np.float32).astype(bf)
    ci = np.arange(256) * 16
    sj = np.arange(64) * 64
    ovl = np.clip(np.minimum(ci[:, None] + 32, sj[None, :] + 64) - np.maximum(ci[:, None], sj[None, :]), 0, None) / 16.0
    ovl[255] = 0.0
    c['c_ovl'] = np.ascontiguousarray(ovl.reshape(2, 128, 64).transpose(1, 0, 2)).astype(np.float32).astype(bf)
    pos = np.arange(S, dtype=np.float32)
    inv = (1.0 / (np.float32(500000.0) ** (np.arange(0, 16, 2, dtype=np.float32) / np.float32(16)))).astype(np.float32)
    ang = pos[:, None] * inv[None, :]
    c['c_rope'] = np.concatenate([np.cos(ang), np.sin(ang)], axis=1).astype(np.float32)
    tt = np.arange(S)
    cur = tt // 64
    blk = np.arange(64)
    valid = blk[None, :] <= cur[:, None]
    forced = (blk[None, :] == 0) | (blk[None, :] == cur[:, None]) | (blk[None, :] == cur[:, None] - 1)
    c['c_A'] = (valid & ~forced).astype(np.float32).reshape(NT, 128, 64)
    c['c_B'] = np.where(valid, np.where(forced, 1e6, 0.0), -1e9).astype(np.float32).reshape(NT, 128, 64)
    return c


def build_program(dbg=False, phases=('ip', 'mem', 'gdn', 'nsa', 'ffn'), dbg_ocat=False):
    nc = bass.Bass("TRN2", target_bir_lowering=False)
    io = {}
    io['x'] = nc.dram_tensor("x", [S, D], F32, kind="ExternalInput").ap()
    io['mem'] = nc.dram_tensor("mem", [256, D], F32, kind="ExternalInput").ap()
    for n in W_NAMES:
        io[n] = nc.dram_tensor(n, W_SHAPES[n], F32, kind="ExternalInput").ap()
    for n, v in make_consts().items():
        io[n] = nc.dram_tensor(n, list(v.shape), BF16 if v.dtype == ml_dtypes.bfloat16 else F32, kind="ExternalInput").ap()
    io['out'] = nc.dram_tensor("out", [S, D], F32, kind="ExternalOutput").ap()
    sk = "ExternalOutput" if dbg else "Internal"
    io['tm'] = nc.dram_tensor("tm", [S, TMW], F32, kind=sk).ap()
    io['qkv_tm'] = nc.dram_tensor("qkv_tm", [S, 1536], BF16, kind=sk).ap()
    if dbg_ocat:
        io['ocatT'] = nc.dram_tensor("ocatT", [NT, 128, 12, 128], BF16, kind="ExternalInput").ap()
    else:
        io['ocatT'] = nc.dram_tensor("ocatT", [NT, 128, 12, 128], BF16, kind=sk).ap()
    io['h_s'] = nc.dram_tensor("h_s", [S, D], F32, kind=sk).ap()
    io['actT'] = nc.dram_tensor("actT", [NT, 128, 22, 128], BF16, kind="Internal").ap()
    kb = KB(nc)
    if 'ip' in phases:
        phase_inproj(kb, io)
    if 'mem' in phases:
        phase_mem(kb, io)
    if 'gdn' in phases:
        phase_gdn(kb, io)
    if 'nsa' in phases:
        phase_nsa(kb, io)
    if 'ffn' in phases:
        phase_ffn(kb, io)
    kb.finish()
    return nc, kb


def make_in_maps(inputs):
    consts = make_consts()
    maps = []
    for b in range(8):
        m = {'x': np.ascontiguousarray(inputs['x'][b]), 'mem': np.ascontiguousarray(inputs['mem'][b])}
        for n in W_NAMES:
            m[n] = np.ascontiguousarray(np.asarray(inputs[n])[0])
        m.update(consts)
        maps.append(m)
    return maps


def kernel(**inputs):
    nc, kb = build_program()
    maps = make_in_maps(inputs)
    res = run_bass_kernel_spmd(nc, maps, core_ids=list(range(8)))
    return np.stack([np.asarray(r['out'], dtype=np.float32) for r in res.results], axis=0)
```

```python
import os
import numpy as np
from contextlib import ExitStack
import concourse.bass as bass
import concourse.mybir as mybir
from concourse.bass_utils import run_bass_kernel_spmd
import ml_dtypes

F32 = mybir.dt.float32
BF16 = mybir.dt.bfloat16
AF = mybir.ActivationFunctionType
ALU = mybir.AluOpType
AX = mybir.AxisListType

S = 4096
D = 1024
NT = S // 128
INW = 3872
TMW = 2336
A_OFF, B_OFF, GATE_OFF, NQ_OFF = 0, 4, 8, 520
KC_OFF, VC_OFF, KS_OFF, VS_OFF, KW_OFF, VW_OFF = 1032, 1160, 1288, 1416, 1544, 1672
NG_OFF, MQ_OFF = 1800, 1824
FF = 2816
NEG = -30000.0
EPS = 1e-6


class T:
    def __init__(self, t, k):
        self.t = t
        self.k = k

    def __getitem__(self, idx):
        return self.t[idx]


class KB:
    NDS = 16

    def __init__(self, nc):
        self.nc = nc
        self.stack = ExitStack()
        self.eng = {'pe': nc.tensor, 'act': nc.scalar, 'dve': nc.vector, 'pool': nc.gpsimd, 'sp': nc.sync}
        self.sem = {}
        for e in self.eng:
            self.sem[e] = self.stack.enter_context(nc.semaphore("s_" + e))
        for j in range(self.NDS):
            self.sem[('d', j)] = self.stack.enter_context(nc.semaphore("d_%d" % j))
        self.cnt = {e: 0 for e in self.eng}
        self.seen = {e: {} for e in self.eng}
        self.state = {}
        self.dma_i = 0
        self.dma_uses = [0] * self.NDS
        self.nins = 0
        self.rr = 0
        self.excl = set()

    def _wait(self, e, evs):
        need = {}
        for (sk, v) in evs:
            if sk == e and e in ('pe', 'sp'):
                continue
            if self.seen[e].get(sk, 0) < v:
                need[sk] = max(need.get(sk, 0), v)
        for sk, v in need.items():
            self.eng[e].wait_ge(self.sem[sk], v)
            self.seen[e][sk] = v

    @staticmethod
    def _keys(lst):
        out = []
        for x in lst:
            if isinstance(x, T):
                out.append(x.k)
            elif isinstance(x, (list, tuple)) and len(x) and isinstance(x[0], T):
                out.append((x[0].k,) + tuple(x[1:]))
            else:
                out.append(x)
        return out

    def _deps(self, reads, writes):
        evs = []
        for k in reads:
            st = self.state.get(k)
            if st and st[0]:
                evs.append(st[0])
        for k in writes:
            st = self.state.get(k)
            if st:
                if st[0]:
                    evs.append(st[0])
                evs.extend(st[1])
        return evs

    def _update(self, ev, reads, writes):
        for k in reads:
            st = self.state.setdefault(k, [None, []])
            st[1].append(ev)
            if len(st[1]) > 12:
                best = {}
                for (sk, v) in st[1]:
                    best[sk] = max(best.get(sk, 0), v)
                st[1] = list(best.items())
        for k in writes:
            self.state[k] = [ev, []]

    def op(self, e, fn, r=(), w=()):
        r = self._keys(r)
        w = self._keys(w)
        w = w + [k for k in r if k in self.excl and k not in w]
        self._wait(e, self._deps(r, w))
        ins = fn(self.eng[e])
        self.cnt[e] += 1
        ins.then_inc(self.sem[e], 1)
        self._update((e, self.cnt[e]), r, w)
        self.nins += 1
        return ins

    def dma(self, q, out, in_, r=(), w=(), **kw):
        r = self._keys(r)
        w = self._keys(w)
        j = self.dma_i % self.NDS
        self.dma_i += 1
        evs = self._deps(r, w)
        if self.dma_uses[j] > 0:
            evs.append((('d', j), 16 * self.dma_uses[j]))
        self._wait(q, evs)
        ins = self.eng[q].dma_start(out=out, in_=in_, **kw)
        self.dma_uses[j] += 1
        ins.then_inc(self.sem[('d', j)], 16)
        ev = (('d', j), 16 * self.dma_uses[j])
        self._update(ev, r, w)
        self.nins += 1
        return ev

    def barrier(self):
        evs = [(f, self.cnt[f]) for f in self.eng if self.cnt[f]]
        evs += [(('d', j), 16 * self.dma_uses[j]) for j in range(self.NDS) if self.dma_uses[j]]
        for e in self.eng:
            self._wait(e, [ev for ev in evs if ev[0] != e])

    def finish(self):
        self.barrier()
        self.stack.close()

    def ew(self, with_act=False):
        self.rr += 1
        lst = ('dve', 'pool', 'act') if with_act else ('dve', 'pool')
        return lst[self.rr % len(lst)]


class Phase:
    def __init__(self, kb, tag):
        self.kb = kb
        self.nc = kb.nc
        self.tag = tag
        self.st = ExitStack()

    def sb(self, name, shape, dt):
        n = self.tag + "_" + name
        return T(self.st.enter_context(self.nc.sbuf_tensor(n, list(shape), dt)), n)

    def sbn(self, name, shape, dt, n):
        return [self.sb("%s%d" % (name, i), shape, dt) for i in range(n)]

    def ps(self, name, shape, dt=F32):
        n = self.tag + "_" + name
        self.kb.excl.add(n)
        return T(self.st.enter_context(self.nc.psum_tensor(n, list(shape), dt)), n)

    def psn(self, name, shape, dt, n):
        return [self.ps("%s%d" % (name, i), shape, dt) for i in range(n)]

    def close(self):
        self.kb.barrier()
        self.st.close()


def load_cast_weight(kb, ph, dst, src_ap, nchunks, ncols, gam=None, stage_cols=None):
    stage_cols = stage_cols or ncols
    stg = ph.sbn("stg_" + dst.k, [128, stage_cols], F32, 2)
    i = 0
    engs = ('dve', 'act', 'dve')
    for c in range(nchunks):
        for c0 in range(0, ncols, stage_cols):
            c1 = min(ncols, c0 + stage_cols)
            sg = stg[i % 2]
            kb.dma('sp', sg[:, 0:c1 - c0], src_ap[c * 128:(c + 1) * 128, c0:c1], w=[sg])
            e = engs[i % 3]
            o = dst[:, c, c0:c1]
            if gam is None:
                if e == 'act':
                    kb.op(e, lambda E: E.copy(out=o, in_=sg[:, 0:c1 - c0]), r=[sg], w=[(dst, c)])
                else:
                    kb.op(e, lambda E: E.tensor_copy(out=o, in_=sg[:, 0:c1 - c0]), r=[sg], w=[(dst, c)])
            else:
                if e == 'act':
                    kb.op(e, lambda E: E.activation(out=o, in_=sg[:, 0:c1 - c0], func=AF.Copy, scale=gam[:, c:c + 1]),
                          r=[sg, gam], w=[(dst, c)])
                else:
                    kb.op(e, lambda E: E.tensor_scalar(out=o, in0=sg[:, 0:c1 - c0], scalar1=gam[:, c:c + 1], scalar2=None,
                                                       op0=ALU.mult), r=[sg, gam], w=[(dst, c)])
            i += 1


def rms_rstd(kb, src_ap, junk, ss, rs, n, rkeys):
    kb.op('act', lambda E: E.activation(out=junk[:, 0:n], in_=src_ap, func=AF.Square, accum_out=ss[:]), r=rkeys, w=[junk, ss])
    kb.op('act', lambda E: E.activation(out=rs[:], in_=ss[:], func=AF.Sqrt, scale=1.0 / n, bias=EPS), r=[ss], w=[rs])
    kb.op('dve', lambda E: E.reciprocal(out=rs[:], in_=rs[:]), r=[rs], w=[rs])


def phase_inproj(kb, io):
    nc = kb.nc
    ph = Phase(kb, "ip")
    winb = ph.sb("winb", [128, 8, INW], BF16)
    gam = ph.sb("gam", [128, 8], F32)
    cw = ph.sb("cw", [128, 4, 12], F32)
    ident = ph.sb("ident", [128, 128], BF16)
    kb.dma('sp', gam[:], io['attn_norm_w'].rearrange("(c p) -> p c", p=128), w=[gam], allow_slow_non_contiguous=True)
    for j in range(4):
        kb.dma('sp', cw[:, j, :], io['gdn_conv_w'][j, :].rearrange("(c p) -> p c", p=128), w=[cw], allow_slow_non_contiguous=True)
    kb.dma('sp', ident[:], io['c_ident'][:, :], w=[ident])
    load_cast_weight(kb, ph, winb, io['w_in'], 8, INW, gam=gam, stage_cols=1936)

    xt = ph.sbn("xt", [128, D], F32, 2)
    junk = ph.sb("junk", [128, D], BF16)
    ss = ph.sbn("ss", [128, 1], F32, 2)
    rs = ph.sbn("rs", [128, 1], F32, 2)
    xn = ph.sbn("xn", [128, D], BF16, 2)
    xnT = ph.sbn("xnT", [128, 8, 512], BF16, 2)
    xc = ph.sbn("xc", [128, 515], F32, 3)
    acc = ph.sbn("acc", [128, 512], F32, 3)
    halo = ph.sb("halo", [128, 12, 3], F32)
    qT = ph.sb("qT", [128, 12, 512], BF16)
    qtm = ph.sbn("qtm", [128, 1536], BF16, 2)
    tmt = ph.sbn("tmt", [128, TMW], F32, 2)
    pT = ph.psn("pT", [128, 8, 128], BF16, 2)
    pF = ph.psn("pF", [128, 512], F32, 2)
    pM = ph.psn("pM", [128, 512], F32, 2)
    pQ = ph.psn("pQ", [128, 8, 128], BF16, 2)

    kb.op('pool', lambda E: E.memset(halo[:], 0.0), w=[halo])
    def iptiles(s):
        xT = xnT[s % 2]
        for t4 in range(4):
            t = s * 4 + t4
            b = t % 2
            kb.dma('sp', xt[b][:], io['x'][t * 128:(t + 1) * 128, :], w=[xt[b]])
            rms_rstd(kb, xt[b][:], junk, ss[b], rs[b], D, [xt[b]])
            kb.op('dve', lambda E: E.tensor_scalar(out=xn[b][:], in0=xt[b][:], scalar1=rs[b][:, 0:1], scalar2=None, op0=ALU.mult),
                  r=[xt[b], rs[b]], w=[xn[b]])
            yield
            for c in range(8):
                kb.op('pe', lambda E: E.transpose(out=pT[b][:, c, :], in_=xn[b][:, c * 128:(c + 1) * 128], identity=ident[:]),
                      r=[xn[b], ident], w=[pT[b]])
            kb.op('act', lambda E: E.copy(out=xT[:, :, t4 * 128:(t4 + 1) * 128], in_=pT[b][:]), r=[pT[b]], w=[xT])
            yield

    def ipmain(s):
        xT = xnT[s % 2]
        for cb in range(12):
            pf = pF[cb % 2]
            for c in range(8):
                kb.op('pe', lambda E: E.matmul(pf[:], lhsT=winb[:, c, cb * 128:(cb + 1) * 128], rhs=xT[:, c, :],
                                               start=(c == 0), stop=(c == 7)), r=[xT, (winb, c)], w=[pf])
            yield
            x3 = xc[cb % 3]
            ac = acc[cb % 3]
            kb.op('act', lambda E: E.copy(out=x3[:, 3:515], in_=pf[:]), r=[pf], w=[(x3, 'b')])
            kb.op('dve', lambda E: E.tensor_copy(out=x3[:, 0:3], in_=halo[:, cb, :]), r=[(halo, cb)], w=[(x3, 'h')])
            kb.op('dve', lambda E: E.tensor_copy(out=halo[:, cb, :], in_=x3[:, 512:515]), r=[(x3, 'b')], w=[(halo, cb)])
            kb.op('act', lambda E: E.activation(out=ac[:], in_=pf[:], func=AF.Copy, scale=cw[:, 3, cb:cb + 1]), r=[pf, cw], w=[ac])
            for j in range(3):
                kb.op('dve', lambda E: E.scalar_tensor_tensor(out=ac[:], in0=x3[:, j:j + 512], scalar=cw[:, j, cb:cb + 1], in1=ac[:],
                                                           op0=ALU.mult, op1=ALU.add), r=[(x3, 'b'), (x3, 'h'), cw, ac], w=[ac])
            kb.op('act', lambda E: E.activation(out=qT[:, cb, :], in_=ac[:], func=AF.Silu), r=[ac], w=[(qT, cb)])
            yield
        for t4 in range(4):
            t = s * 4 + t4
            qm = qtm[t % 2]
            for g3 in range(3):
                pq = pQ[g3 % 2]
                for j in range(4):
                    cb = g3 * 4 + j
                    kb.op('pe', lambda E: E.transpose(out=pq[:, j, :], in_=qT[:, cb, t4 * 128:(t4 + 1) * 128], identity=ident[:]),
                          r=[(qT, cb), ident], w=[pq])
                e1 = 'dve' if g3 % 2 == 0 else 'act'
                if e1 == 'dve':
                    kb.op('dve', lambda E: E.tensor_copy(out=qm[:, g3 * 512:(g3 + 1) * 512], in_=pq[:, 0:4, :].rearrange("p a b -> p (a b)")),
                          r=[pq], w=[(qm, g3)])
                else:
                    kb.op('act', lambda E: E.copy(out=qm[:, g3 * 512:(g3 + 1) * 512], in_=pq[:, 0:4, :].rearrange("p a b -> p (a b)")),
                          r=[pq], w=[(qm, g3)])
            kb.dma('pool', io['qkv_tm'][t * 128:(t + 1) * 128, :], qm[:], r=[(qm, 0), (qm, 1), (qm, 2)], w=['qkv_tm'])
            yield
            tm = tmt[t % 2]
            for ci, n0 in enumerate(range(0, TMW, 512)):
                n1 = min(TMW, n0 + 512)
                pm = pM[ci % 2]
                for c in range(8):
                    kb.op('pe', lambda E: E.matmul(pm[:, 0:n1 - n0], lhsT=xT[:, c, t4 * 128:(t4 + 1) * 128],
                                                   rhs=winb[:, c, 1536 + n0:1536 + n1], start=(c == 0), stop=(c == 7)),
                          r=[xT, (winb, c)], w=[pm])
                if ci % 2 == 0:
                    kb.op('dve', lambda E: E.tensor_copy(out=tm[:, n0:n1], in_=pm[:, 0:n1 - n0]), r=[pm], w=[(tm, ci)])
                else:
                    kb.op('act', lambda E: E.copy(out=tm[:, n0:n1], in_=pm[:, 0:n1 - n0]), r=[pm], w=[(tm, ci)])
            kb.dma('pool', io['tm'][t * 128:(t + 1) * 128, :], tm[:], r=[(tm, i) for i in range(5)], w=['tm'])

    def interleave(gens):
        gens = list(gens)
        while gens:
            for g_ in list(gens):
                try:
                    next(g_)
                except StopIteration:
                    gens.remove(g_)

    NS = S // 512
    interleave([iptiles(0)])
    for s in range(NS):
        gl = [ipmain(s)]
        if s + 1 < NS:
            gl.append(iptiles(s + 1))
        interleave(gl)
    ph.close()


def attn_block(kb, o_ps, PTbuf, pti, sc_ps, kt_specs, ident, rkeys_q, LA=2):
    n = len(kt_specs)
    pts = [None] * n
    started = set()
    npv = sum(sp['qt1'] - sp['qt0'] for sp in kt_specs)
    done = 0
    for i in range(n + LA):
        if i < n:
            sp = kt_specs[i]
            pss = sc_ps[pti[0] % len(sc_ps)]
            ptb = PTbuf[pti[0] % len(PTbuf)]
            pti[0] += 1
            pts[i] = ptb
            q0, q1 = sp['qt0'], sp['qt1']
            ncol = (q1 - q0) * 128
            nk = sp['nk']
            nm = len(sp['masks'])
            kb.op('pe', lambda E: E.matmul(pss[0:nk, 0:ncol], lhsT=sp['lhsT'], rhs=sp['rhs_fn'](q0, q1), start=True, stop=(nm == 0)),
                  r=sp['rk'] + rkeys_q, w=[pss])
            for mi, (c0, nc_, mask) in enumerate(sp['masks']):
                kb.op('pe', lambda E: E.matmul(pss[0:nk, c0:c0 + nc_], lhsT=ident[0:nk, 0:nk], rhs=mask, start=False, stop=(mi == nm - 1)),
                      r=[ident] + sp.get('mk', []), w=[pss])
            kb.op('act', lambda E: E.activation(out=ptb[0:nk, 0:ncol], in_=pss[0:nk, 0:ncol], func=AF.Exp), r=[pss], w=[ptb])
        j = i - LA
        if j >= 0:
            sp = kt_specs[j]
            ptb = pts[j]
            nk = sp['nk']
            for qt in range(sp['qt0'], sp['qt1']):
                c0 = (qt - sp['qt0']) * 128
                oap, okey, bank = o_ps(qt)
                st = bank not in started
                started.add(bank)
                done += 1
                kb.op('pe', lambda E: E.matmul(oap, lhsT=ptb[0:nk, c0:c0 + 128], rhs=sp['v_fn'](qt), start=st, stop=(done == npv),
                                               skip_group_check=True), r=[ptb] + sp['rv'], w=[okey])


def phase_mem(kb, io):
    nc = kb.nc
    ph = Phase(kb, "mm")
    ident = ph.sb("ident", [128, 128], BF16)
    kb.dma('sp', ident[:], io['c_ident'][:, :], w=[ident])
    wkv = ph.sb("wkv", [128, 8, 1024], BF16)
    gam = ph.sb("gam", [128, 8], F32)
    kb.dma('sp', gam[:], io['mem_norm_w'].rearrange("(c p) -> p c", p=128), w=[gam], allow_slow_non_contiguous=True)
    load_cast_weight(kb, ph, wkv, io['mem_w_kv'], 8, 1024, gam=gam)
    qnw = ph.sb("qnw", [128, 128], F32)
    knw = ph.sb("knw", [128, 128], F32)
    kb.dma('sp', qnw[:], io['mem_q_norm_w'].partition_broadcast(128), w=[qnw])
    kb.dma('sp', knw[:], io['mem_k_norm_w'].partition_broadcast(128), w=[knw])
    kb.op('dve', lambda E: E.tensor_scalar(out=qnw[:], in0=qnw[:], scalar1=128 ** -0.5, scalar2=None, op0=ALU.mult), r=[qnw], w=[qnw])

    mt = ph.sbn("mt", [128, D], F32, 2)
    junk = ph.sb("junk", [128, D], BF16)
    ss = ph.sb("ss", [128, 1], F32)
    rs = ph.sb("rs", [128, 1], F32)
    mn = ph.sb("mn", [128, D], BF16)
    mnT = ph.sb("mnT", [128, 8, 128], BF16)
    kvt = ph.sb("kvt", [128, 1024], F32)
    sq4 = ph.sb("sq4", [128, 4, 128], F32)
    ss4 = ph.sb("ss4", [128, 4], F32)
    rs4 = ph.sb("rs4", [128, 4], F32)
    kn = ph.sb("kn", [128, 4, 128], BF16)
    kT = ph.sb("kT", [128, 4, 256], BF16)
    v1 = ph.sb("v1", [128, 2, 4, 129], BF16)
    pT = ph.ps("pT", [128, 8, 128], BF16)
    pK = ph.psn("pK", [128, 512], F32, 2)
    kb.op('pool', lambda E: E.memset(v1[:], 1.0), w=[v1])
    for mt_i in range(2):
        m = mt[mt_i]
        kb.dma('sp', m[:], io['mem'][mt_i * 128:(mt_i + 1) * 128, :], w=[m])
        rms_rstd(kb, m[:], junk, ss, rs, D, [m])
        kb.op('dve', lambda E: E.tensor_scalar(out=mn[:], in0=m[:], scalar1=rs[:, 0:1], scalar2=None, op0=ALU.mult), r=[m, rs], w=[mn])
        for c in range(8):
            kb.op('pe', lambda E: E.transpose(out=pT[:, c, :], in_=mn[:, c * 128:(c + 1) * 128], identity=ident[:]), r=[mn, ident], w=[pT])
        kb.op('act', lambda E: E.copy(out=mnT[:], in_=pT[:]), r=[pT], w=[mnT])
        for half in range(2):
            pk = pK[half]
            for c in range(8):
                kb.op('pe', lambda E: E.matmul(pk[:], lhsT=mnT[:, c, :], rhs=wkv[:, c, half * 512:(half + 1) * 512],
                                               start=(c == 0), stop=(c == 7)), r=[mnT, (wkv, c)], w=[pk])
            kb.op('act', lambda E: E.copy(out=kvt[:, half * 512:(half + 1) * 512], in_=pk[:]), r=[pk], w=[(kvt, half)])
        k3 = kvt[:, 0:512].rearrange("p (h d) -> p h d", h=4)
        kb.op('dve', lambda E: E.tensor_tensor(out=sq4[:], in0=k3, in1=k3, op=ALU.mult), r=[(kvt, 0)], w=[sq4])
        kb.op('dve', lambda E: E.tensor_reduce(out=ss4[:], in_=sq4[:], axis=AX.X, op=ALU.add), r=[sq4], w=[ss4])
        kb.op('act', lambda E: E.activation(out=rs4[:], in_=ss4[:], func=AF.Sqrt, scale=1.0 / 128, bias=EPS), r=[ss4], w=[rs4])
        kb.op('dve', lambda E: E.reciprocal(out=rs4[:], in_=rs4[:]), r=[rs4], w=[rs4])
        kb.op('dve', lambda E: E.tensor_tensor(out=sq4[:], in0=k3, in1=rs4[:].unsqueeze(2).to_broadcast([128, 4, 128]), op=ALU.mult),
              r=[(kvt, 0), rs4], w=[sq4])
        kb.op('dve', lambda E: E.tensor_tensor(out=kn[:], in0=sq4[:], in1=knw[:].unsqueeze(1).to_broadcast([128, 4, 128]), op=ALU.mult),
              r=[sq4, knw], w=[kn])
        for h in range(4):
            kb.op('pe', lambda E: E.transpose(out=pT[:, h, :], in_=kn[:, h, :], identity=ident[:]), r=[kn, ident], w=[pT])
        kb.op('act', lambda E: E.copy(out=kT[:, :, mt_i * 128:(mt_i + 1) * 128], in_=pT[:, 0:4, :]), r=[pT], w=[kT])
        kb.op('dve', lambda E: E.tensor_copy(out=v1[:, mt_i, :, 0:128], in_=kvt[:, 512:1024].rearrange("p (h d) -> p h d", h=4)),
              r=[(kvt, 1)], w=[v1])

    qt_ = ph.sbn("qt", [128, 512], F32, 2)
    qs = ph.sb("qs", [128, 4, 128], F32)
    qn = ph.sbn("qn", [128, 4, 128], BF16, 2)
    qT = ph.sbn("qT", [128, 4, 512], BF16, 2)
    PTb = ph.sbn("PT", [128, 512], BF16, 4)
    pti = [0]
    oc = ph.sbn("oc", [128, 4, 128], BF16, 4)
    rinv = ph.sb("rinv", [128, 1], F32)
    ocT = ph.sbn("ocT", [128, 4, 128], BF16, 2)
    sc_ps = ph.psn("sc", [128, 512], F32, 2)
    o_psA = ph.ps("oA", [128, 2, 256], F32)
    o_psB = ph.ps("oB", [128, 2, 256], F32)
    pO = ph.ps("pO", [128, 8, 128], BF16)
    for s in range(S // 512):
        qTs = qT[s % 2]
        for t4 in range(4):
            t = s * 4 + t4
            q = qt_[t % 2]
            qb = qn[t % 2]
            kb.dma('sp', q[:], io['tm'][t * 128:(t + 1) * 128, MQ_OFF:MQ_OFF + 512], r=['tm'], w=[q])
            q3 = q[:].rearrange("p (h d) -> p h d", h=4)
            kb.op('pool', lambda E: E.tensor_tensor(out=qs[:], in0=q3, in1=q3, op=ALU.mult), r=[q], w=[qs])
            kb.op('dve', lambda E: E.tensor_reduce(out=ss4[:], in_=qs[:], axis=AX.X, op=ALU.add), r=[qs], w=[ss4])
            kb.op('act', lambda E: E.activation(out=rs4[:], in_=ss4[:], func=AF.Sqrt, scale=1.0 / 128, bias=EPS), r=[ss4], w=[rs4])
            kb.op('dve', lambda E: E.reciprocal(out=rs4[:], in_=rs4[:]), r=[rs4], w=[rs4])
            kb.op('dve', lambda E: E.tensor_tensor(out=qs[:], in0=q3, in1=rs4[:].unsqueeze(2).to_broadcast([128, 4, 128]), op=ALU.mult),
                  r=[q, rs4], w=[qs])
            kb.op('pool', lambda E: E.tensor_tensor(out=qb[:], in0=qs[:], in1=qnw[:].unsqueeze(1).to_broadcast([128, 4, 128]), op=ALU.mult),
                  r=[qs, qnw], w=[qb])
            for h in range(4):
                kb.op('pe', lambda E: E.transpose(out=pT[:, h, :], in_=qb[:, h, :], identity=ident[:]), r=[qb, ident], w=[pT])
            kb.op('act', lambda E: E.copy(out=qTs[:, :, t4 * 128:(t4 + 1) * 128], in_=pT[:, 0:4, :]), r=[pT], w=[qTs])
        for h in range(4):
            def o_ps(qt, h=h):
                return (o_psA[:, qt, 0:129], o_psA, 'A') if qt < 2 else (o_psB[:, qt - 2, 0:129], o_psB, 'B')
            specs = []
            for kt in range(2):
                specs.append(dict(lhsT=kT[:, h, kt * 128:(kt + 1) * 128], rhs_fn=lambda q0, q1, h=h: qTs[:, h, q0 * 128:q1 * 128],
                                  qt0=0, qt1=4, masks=[], v_fn=lambda qt, kt=kt, h=h: v1[:, kt, h, :], nk=128, rk=[kT], rv=[v1]))
            attn_block(kb, o_ps, PTb, pti, sc_ps, specs, ident, [qTs])
            for t4 in range(4):
                t = s * 4 + t4
                ob = oc[t4]
                oap, okey, _ = o_ps(t4)
                kb.op('dve', lambda E: E.reciprocal(out=rinv[:], in_=oap[:, 128:129]), r=[okey], w=[rinv])
                kb.op('dve', lambda E: E.tensor_scalar(out=ob[:, h, :], in0=oap[:, 0:128], scalar1=rinv[:, 0:1], scalar2=None, op0=ALU.mult),
                      r=[okey, rinv], w=[(ob, h)])
        for t4 in range(4):
            t = s * 4 + t4
            ob = oc[t4]
            oT = ocT[t % 2]
            for h in range(4):
                kb.op('pe', lambda E: E.transpose(out=pO[:, h, :], in_=ob[:, h, :], identity=ident[:]), r=[(ob, h), ident], w=[pO])
            kb.op('act', lambda E: E.copy(out=oT[:], in_=pO[:, 0:4, :]), r=[pO], w=[oT])
            kb.dma('sp', io['ocatT'][t, :, 8:12, :], oT[:], r=[oT], w=['ocatT_c'])
    ph.close()


def phase_gdn(kb, io):
    nc = kb.nc
    ph = Phase(kb, "gd")
    ident = ph.sb("ident", [128, 128], BF16)
    btri = ph.sb("btri", [128, 128], F32)
    bones = ph.sb("bones", [128, 128], F32)
    ones = ph.sb("ones", [128, 128], F32)
    mlow = ph.sb("mlow", [128, 4, 128], F32)
    mup = ph.sb("mup", [128, 4, 128], F32)
    strict = ph.sb("strict", [128, 128], F32)
    mch = ph.sb("mch", [128, 2], F32)
    kb.dma('sp', ident[:], io['c_ident'][:, :], w=[ident])
    kb.dma('sp', btri[:], io['c_btri'][:, :], w=[btri])
    kb.dma('sp', bones[:], io['c_bones'][:, :], w=[bones])
    kb.dma('sp', strict[:], io['c_strict'][:, :], w=[strict])
    kb.dma('sp', mch[:], io['c_mch'][:, :], w=[mch])
    for h in range(4):
        kb.dma('sp', mlow[:, h, :], io['c_mlow'][:, :], w=[mlow])
        kb.dma('sp', mup[:, h, :], io['c_mup'][:, :], w=[mup])
    kb.op('pool', lambda E: E.memset(ones[:], 1.0), w=[ones])
    dtb = ph.sb("dtb", [128, 4], F32)
    nA = ph.sb("nA", [128, 4], F32)
    gw = ph.sb("gw", [128, 128], F32)
    kb.dma('sp', dtb[:], io['gdn_dt_bias'].partition_broadcast(128), w=[dtb])
    kb.dma('sp', nA[:], io['gdn_a_log'].partition_broadcast(128), w=[nA])
    kb.dma('sp', gw[:], io['gdn_out_norm_w'].partition_broadcast(128), w=[gw])
    kb.op('act', lambda E: E.activation(out=nA[:], in_=nA[:], func=AF.Exp), r=[nA], w=[nA])
    kb.op('dve', lambda E: E.tensor_scalar(out=nA[:], in0=nA[:], scalar1=-1.0, scalar2=None, op0=ALU.mult), r=[nA], w=[nA])

    def B4(t_, n=4):
        return t_.unsqueeze(2).to_broadcast([128, n, 128])

    def M4(t_):
        return t_.unsqueeze(1).to_broadcast([128, 4, 128])

    qkv = ph.sbn("qkv", [128, 3, 4, 128], BF16, 2)
    ab = ph.sbn("ab", [128, 8], F32, 2)
    gt = ph.sbn("gt", [128, 512], F32, 2)
    sm = ph.sbn("sm", [128, 64], F32, 2)
    gs = ph.sbn("gs", [128, 16], F32, 2)
    gm = ph.sb("gm", [128, 8], F32)
    R1 = ph.sb("R1", [128, 4, 128], F32)
    R2 = ph.sb("R2", [128, 4, 128], F32)
    tmpA = ph.sb("tmpA", [128, 4, 128], F32)
    tmpB = ph.sb("tmpB", [128, 4, 128], F32)
    dec = ph.sb("dec", [128, 4, 128], F32)
    decT = ph.sb("decT", [128, 4, 128], F32)
    sq = ph.sb("sq", [128, 4, 128], F32)
    KBG = ph.sb("KBG", [128, 4, 128], BF16)
    Kdb = ph.sbn("Kd", [128, 4, 128], BF16, 4)
    VBb = ph.sbn("VB", [128, 4, 128], BF16, 2)
    dg = ph.sbn("dg", [128, 4, 128], BF16, 4)
    QT = ph.sb("QT", [128, 4, 128], BF16)
    QGb = ph.sbn("QG", [128, 4, 128], BF16, 4)
    KT = ph.sb("KT", [128, 4, 128], BF16)
    nbs = ph.sb("nbs", [128, 4, 128], F32)
    Xb = ph.sbn("X", [128, 4, 128], BF16, 2)
    Yb = ph.sbn("Y", [128, 4, 128], BF16, 2)
    Pbb = ph.sbn("P", [128, 4, 128], BF16, 4)
    aqkTb = ph.sbn("aqkT", [128, 4, 128], BF16, 2)
    negWTb = ph.sbn("negWT", [128, 4, 128], BF16, 2)
    vnew = ph.sb("vnew", [128, 4, 128], BF16)
    Sf = ph.sb("Sf", [128, 4, 128], F32)
    Sbf = ph.sbn("Sbf", [128, 4, 128], BF16, 3)
    osb = ph.sb("osb", [128, 4, 128], F32)
    sgt = ph.sb("sgt", [128, 512], F32)
    oa = ph.sbn("oa", [128, 4, 128], BF16, 2)
    oaT = ph.sbn("oaT", [128, 4, 128], BF16, 2)
    pS = ph.ps("pS", [128, 512], F32)
    pD = ph.ps("pD", [128, 4, 128], F32)
    pA = ph.psn("pA", [128, 4, 128], F32, 2)
    pB = ph.psn("pB", [128, 8, 128], BF16, 1)
    pV = ph.ps("pV", [128, 4, 128], F32)
    pdS = ph.ps("pdS", [128, 4, 128], F32)
    pO = ph.ps("pO", [128, 4, 128], F32)
    kb.op('pool', lambda E: E.memset(Sf[:], 0.0), w=[Sf])
    kb.op('pool', lambda E: E.memset(Sbf[0][:], 0.0), w=[Sbf[0]])
    kb.op('pool', lambda E: E.memset(vnew[:], 0.0), w=[vnew])
    pai = [0]

    def PA():
        pai[0] += 1
        return pA[pai[0] % 2]

    def mm4(p, lf, rf, rk):
        for h in range(4):
            kb.op('pe', lambda E: E.matmul(p[:, h, :], lhsT=lf(h), rhs=rf(h), start=True, stop=True), r=rk, w=[p])

    si = [0]

    def tile(t):
        b = t % 2
        x_ = qkv[b]
        VB, aqkT, negWT = VBb[b], aqkTb[b], negWTb[b]
        Kd = Kdb[2 * b:2 * b + 2]
        QG = QGb[2 * b:2 * b + 2]
        Pb = Pbb[2 * b:2 * b + 2]
        s_ = sm[b]
        kb.dma('sp', x_[:].rearrange("p a h d -> p (a h d)"), io['qkv_tm'][t * 128:(t + 1) * 128, :], r=['qkv_tm'], w=[x_])
        kb.dma('sp', ab[b][:], io['tm'][t * 128:(t + 1) * 128, 0:8], r=['tm'], w=[ab[b]])
        kb.dma('sp', gt[b][:], io['tm'][t * 128:(t + 1) * 128, GATE_OFF:GATE_OFF + 512], r=['tm'], w=[gt[b]])
        g = s_[:, 0:4]
        kb.op('dve', lambda E: E.tensor_tensor(out=g, in0=ab[b][:, 0:4], in1=dtb[:], op=ALU.add), r=[ab[b], dtb], w=[s_])
        kb.op('act', lambda E: E.activation(out=g, in_=g, func=AF.Exp), r=[s_], w=[s_])
        kb.op('act', lambda E: E.activation(out=g, in_=g, func=AF.Ln, bias=1.0), r=[s_], w=[s_])
        kb.op('dve', lambda E: E.tensor_tensor(out=g, in0=g, in1=nA[:], op=ALU.mult), r=[s_, nA], w=[s_])
        kb.op('dve', lambda E: E.tensor_scalar(out=s_[:, 4:8], in0=g, scalar1=-1.0, scalar2=None, op0=ALU.mult), r=[s_], w=[s_])
        kb.op('act', lambda E: E.activation(out=s_[:, 8:12], in_=ab[b][:, 4:8], func=AF.Exp, scale=-1.0), r=[ab[b], s_], w=[s_])
        kb.op('dve', lambda E: E.tensor_scalar(out=s_[:, 8:12], in0=s_[:, 8:12], scalar1=1.0, scalar2=None, op0=ALU.add), r=[s_], w=[s_])
        kb.op('dve', lambda E: E.reciprocal(out=s_[:, 8:12], in_=s_[:, 8:12]), r=[s_], w=[s_])
        kb.op('dve', lambda E: E.tensor_scalar(out=s_[:, 12:16], in0=s_[:, 8:12], scalar1=-1.0, scalar2=None, op0=ALU.mult), r=[s_], w=[s_])
        for j in range(2):
            kb.op('dve', lambda E: E.tensor_scalar(out=gm[:, 4 * j:4 * j + 4], in0=g, scalar1=mch[:, j:j + 1], scalar2=None, op0=ALU.mult),
                  r=[s_, mch], w=[gm])
        kb.op('pe', lambda E: E.matmul(pS[:, 0:4], lhsT=btri[:], rhs=g, start=True, stop=True), r=[btri, s_], w=[pS])
        kb.op('pe', lambda E: E.matmul(pS[:, 4:8], lhsT=bones[:], rhs=g, start=True, stop=True), r=[bones, s_], w=[pS])
        kb.op('pe', lambda E: E.matmul(pS[:, 8:16], lhsT=ones[:], rhs=gm[:], start=True, stop=True), r=[ones, gm], w=[pS])
        G = gs[b]
        kb.op('dve', lambda E: E.tensor_copy(out=G[:], in_=pS[:, 0:16]), r=[pS], w=[G])
        kb.op('act', lambda E: E.activation(out=s_[:, 16:20], in_=G[:, 0:4], func=AF.Exp), r=[G, s_], w=[s_])
        kb.op('dve', lambda E: E.tensor_tensor(out=G[:, 4:8], in0=G[:, 4:8], in1=G[:, 0:4], op=ALU.subtract), r=[G], w=[G])
        kb.op('act', lambda E: E.activation(out=s_[:, 20:24], in_=G[:, 4:8], func=AF.Exp), r=[G, s_], w=[s_])
        kb.op('act', lambda E: E.activation(out=s_[:, 56:64], in_=G[:, 8:16], func=AF.Exp), r=[G, s_], w=[s_])
        yield
        kb.op('pool', lambda E: E.tensor_copy(out=R1[:], in_=B4(g)), r=[s_], w=[R1])
        kb.op('pool', lambda E: E.tensor_tensor(out=R2[:], in0=M4(btri[:]), in1=B4(s_[:, 4:8]), op=ALU.mult), r=[s_, btri], w=[R2])
        kb.op('pe', lambda E: E.matmul(pD[:].rearrange("p a b -> p (a b)"), lhsT=btri[:], rhs=R1[:].rearrange("p a b -> p (a b)"),
                                       start=True, stop=False), r=[btri, R1], w=[pD])
        kb.op('pe', lambda E: E.matmul(pD[:].rearrange("p a b -> p (a b)"), lhsT=bones[:], rhs=R2[:].rearrange("p a b -> p (a b)"),
                                       start=False, stop=True), r=[bones, R2], w=[pD])
        kb.op('dve', lambda E: E.tensor_tensor(out=tmpA[:], in0=pD[:], in1=mlow[:], op=ALU.add), r=[pD, mlow], w=[tmpA])
        kb.op('act', lambda E: E.activation(out=dec[:], in_=tmpA[:], func=AF.Exp), r=[tmpA], w=[dec])
        kb.op('dve', lambda E: E.scalar_tensor_tensor(out=tmpB[:], in0=pD[:], scalar=-1.0, in1=mup[:], op0=ALU.mult, op1=ALU.add),
              r=[pD, mup], w=[tmpB])
        kb.op('act', lambda E: E.activation(out=decT[:], in_=tmpB[:], func=AF.Exp), r=[tmpB], w=[decT])
        yield
        for a_, c0 in ((0, 24), (1, 28)):
            kb.op('pool', lambda E: E.tensor_tensor(out=sq[:], in0=x_[:, a_, :, :], in1=x_[:, a_, :, :], op=ALU.mult), r=[x_], w=[sq])
            kb.op('dve', lambda E: E.tensor_reduce(out=s_[:, c0:c0 + 4], in_=sq[:], axis=AX.X, op=ALU.add), r=[sq, s_], w=[s_])
            kb.op('act', lambda E: E.activation(out=s_[:, c0:c0 + 4], in_=s_[:, c0:c0 + 4], func=AF.Sqrt, bias=EPS), r=[s_], w=[s_])
            kb.op('dve', lambda E: E.reciprocal(out=s_[:, c0:c0 + 4], in_=s_[:, c0:c0 + 4]), r=[s_], w=[s_])
        sc_ = lambda o, a, bb: kb.op('dve', lambda E: E.tensor_tensor(out=s_[:, o:o + 4], in0=a, in1=bb, op=ALU.mult), r=[s_, mch], w=[s_])
        kb.op('dve', lambda E: E.tensor_scalar(out=s_[:, 32:36], in0=s_[:, 24:28], scalar1=128 ** -0.5, scalar2=None, op0=ALU.mult), r=[s_], w=[s_])
        sc_(36, s_[:, 32:36], s_[:, 16:20])
        kb.op('dve', lambda E: E.tensor_scalar(out=s_[:, 40:44], in0=s_[:, 36:40], scalar1=mch[:, 1:2], scalar2=None, op0=ALU.mult), r=[s_, mch], w=[s_])
        kb.op('dve', lambda E: E.tensor_scalar(out=s_[:, 36:40], in0=s_[:, 36:40], scalar1=mch[:, 0:1], scalar2=None, op0=ALU.mult), r=[s_, mch], w=[s_])
        sc_(44, s_[:, 28:32], s_[:, 8:12])
        sc_(44, s_[:, 44:48], s_[:, 16:20])
        sc_(48, s_[:, 28:32], s_[:, 20:24])
        kb.op('dve', lambda E: E.tensor_scalar(out=s_[:, 52:56], in0=s_[:, 48:52], scalar1=mch[:, 1:2], scalar2=None, op0=ALU.mult), r=[s_, mch], w=[s_])
        kb.op('dve', lambda E: E.tensor_scalar(out=s_[:, 48:52], in0=s_[:, 48:52], scalar1=mch[:, 0:1], scalar2=None, op0=ALU.mult), r=[s_, mch], w=[s_])
        yield
        kx, vx, qx = x_[:, 1, :, :], x_[:, 2, :, :], x_[:, 0, :, :]
        kb.op('pool', lambda E: E.tensor_tensor(out=KBG[:], in0=kx, in1=B4(s_[:, 44:48]), op=ALU.mult), r=[x_, s_], w=[KBG])
        kb.op('pool', lambda E: E.tensor_tensor(out=Kd[0][:], in0=kx, in1=B4(s_[:, 48:52]), op=ALU.mult), r=[x_, s_], w=[Kd[0]])
        kb.op('pool', lambda E: E.tensor_tensor(out=Kd[1][:], in0=kx, in1=B4(s_[:, 52:56]), op=ALU.mult), r=[x_, s_], w=[Kd[1]])
        kb.op('pool', lambda E: E.tensor_tensor(out=VB[:], in0=vx, in1=B4(s_[:, 8:12]), op=ALU.mult), r=[x_, s_], w=[VB])
        yield
        for i_, c0 in enumerate((32, 36, 40, 28)):
            kb.op('dve', lambda E: E.tensor_tensor(out=dg[i_][:], in0=M4(ident[:]), in1=B4(s_[:, c0:c0 + 4]), op=ALU.mult), r=[ident, s_], w=[dg[i_]])
        for i_, (src, dst) in enumerate(((qx, QT), (qx, QG[0]), (qx, QG[1]), (kx, KT))):
            p = PA()
            mm4(p, lambda h: src[:, h, :], lambda h: dg[i_][:, h, :], [x_, dg[i_]])
            if i_ % 2 == 0:
                kb.op('act', lambda E: E.copy(out=dst[:], in_=p[:]), r=[p], w=[dst])
            else:
                kb.op('dve', lambda E: E.tensor_copy(out=dst[:], in_=p[:]), r=[p], w=[dst])
        yield
        p = PA()
        mm4(p, lambda h: KT[:, h, :], lambda h: KT[:, h, :], [KT])
        kb.op('dve', lambda E: E.tensor_tensor(out=tmpA[:], in0=p[:], in1=dec[:], op=ALU.mult), r=[p, dec], w=[tmpA])
        kb.op('pool', lambda E: E.tensor_tensor(out=nbs[:], in0=M4(strict[:]), in1=B4(s_[:, 12:16]), op=ALU.mult), r=[strict, s_], w=[nbs])
        X, Y, P = Xb[0], Yb[0], Pb[0]
        kb.op('pool', lambda E: E.tensor_tensor(out=X[:], in0=tmpA[:], in1=nbs[:], op=ALU.mult), r=[tmpA, nbs], w=[X])
        for h in range(4):
            kb.op('pe', lambda E: E.transpose(out=pB[0][:, h, :], in_=X[:, h, :], identity=ident[:]), r=[X, ident], w=[pB[0]])
        kb.op('act', lambda E: E.copy(out=Y[:], in_=pB[0][:, 0:4, :]), r=[pB[0]], w=[Y])
        kb.op('dve', lambda E: E.tensor_tensor(out=P[:], in0=pB[0][:, 0:4, :], in1=M4(ident[:]), op=ALU.add), r=[pB[0], ident], w=[P])
        yield
        p = PA()
        mm4(p, lambda h: KT[:, h, :], lambda h: QT[:, h, :], [KT, QT])
        kb.op('dve', lambda E: E.tensor_tensor(out=aqkT[:], in0=p[:], in1=decT[:], op=ALU.mult), r=[p, decT], w=[aqkT])
        yield
        for k_ in range(1, 6):
            Xn, Yn, Pn = Xb[k_ % 2], Yb[k_ % 2], Pb[k_ % 2]
            p = PA()
            mm4(p, lambda h: Y[:, h, :], lambda h: X[:, h, :], [X, Y])
            kb.op('act', lambda E: E.copy(out=Xn[:], in_=p[:]), r=[p], w=[Xn])
            if k_ < 5:
                p2 = PA()
                mm4(p2, lambda h: X[:, h, :], lambda h: Y[:, h, :], [X, Y])
                kb.op('dve', lambda E: E.tensor_copy(out=Yn[:], in_=p2[:]), r=[p2], w=[Yn])
            p3 = PA()
            mm4(p3, lambda h: Xn[:, h, :], lambda h: P[:, h, :], [Xn, P])
            kb.op('dve', lambda E: E.tensor_tensor(out=Pn[:], in0=p3[:], in1=P[:], op=ALU.add), r=[p3, P], w=[Pn])
            X, Y, P = Xn, Yn, Pn
            yield
        yield
        p = PA()
        mm4(p, lambda h: KBG[:, h, :], lambda h: P[:, h, :], [KBG, P])
        kb.op('act', lambda E: E.mul(out=negWT[:], in_=p[:], mul=-1.0), r=[p], w=[negWT])
        yield 'B'
        Sa = Sbf[si[0] % 3]
        Sb_ = Sbf[(si[0] + 1) % 3]
        Sc = Sbf[(si[0] + 2) % 3]
        si[0] += 2
        for j, (Scur, Snext) in enumerate(((Sa, Sb_), (Sb_, Sc))):
            for h in range(4):
                kb.op('pe', lambda E: E.matmul(pV[:, h, :], lhsT=P[:, h, :], rhs=VB[:, h, :], start=True, stop=False), r=[P, VB], w=[pV])
                kb.op('pe', lambda E: E.matmul(pV[:, h, :], lhsT=negWT[:, h, :], rhs=Scur[:, h, :], start=False, stop=True),
                      r=[negWT, Scur], w=[pV])
            yield
            r0 = 64 * j
            kb.op('act', lambda E: E.copy(out=vnew[r0:r0 + 64, :, :], in_=pV[r0:r0 + 64, :, :]), r=[pV], w=[vnew])
            yield
            mm4(pdS, lambda h: Kd[j][:, h, :], lambda h: vnew[:, h, :], [Kd[j], vnew])
            yield
            for h in range(4):
                kb.op('dve', lambda E: E.scalar_tensor_tensor(out=Sf[:, h, :], in0=Sf[:, h, :], scalar=s_[:, 56 + 4 * j + h:57 + 4 * j + h],
                                                              in1=pdS[:, h, :], op0=ALU.mult, op1=ALU.add), r=[Sf, s_, pdS], w=[Sf])
            kb.op('act', lambda E: E.copy(out=Snext[:], in_=Sf[:]), r=[Sf], w=[Snext])
            yield
        for h in range(4):
            kb.op('pe', lambda E: E.matmul(pO[:, h, :], lhsT=QG[0][:, h, :], rhs=Sa[:, h, :], start=True, stop=False), r=[QG[0], Sa], w=[pO])
            kb.op('pe', lambda E: E.matmul(pO[:, h, :], lhsT=QG[1][:, h, :], rhs=Sb_[:, h, :], start=False, stop=False), r=[QG[1], Sb_], w=[pO])
            kb.op('pe', lambda E: E.matmul(pO[:, h, :], lhsT=aqkT[:, h, :], rhs=vnew[:, h, :], start=False, stop=True), r=[aqkT, vnew], w=[pO])
        yield
        kb.op('act', lambda E: E.copy(out=osb[:], in_=pO[:]), r=[pO], w=[osb])
        kb.op('pool', lambda E: E.tensor_tensor(out=sq[:], in0=osb[:], in1=osb[:], op=ALU.mult), r=[osb], w=[sq])
        kb.op('dve', lambda E: E.tensor_reduce(out=G[:, 0:4], in_=sq[:], axis=AX.X, op=ALU.add), r=[sq, G], w=[G])
        kb.op('act', lambda E: E.activation(out=G[:, 0:4], in_=G[:, 0:4], func=AF.Sqrt, scale=1.0 / 128, bias=EPS), r=[G], w=[G])
        kb.op('dve', lambda E: E.reciprocal(out=G[:, 0:4], in_=G[:, 0:4]), r=[G], w=[G])
        yield
        kb.op('act', lambda E: E.activation(out=sgt[:], in_=gt[b][:], func=AF.Silu), r=[gt[b]], w=[sgt])
        kb.op('dve', lambda E: E.tensor_tensor(out=osb[:], in0=osb[:], in1=B4(G[:, 0:4]), op=ALU.mult), r=[osb, G], w=[osb])
        kb.op('pool', lambda E: E.tensor_tensor(out=osb[:], in0=osb[:], in1=M4(gw[:]), op=ALU.mult), r=[osb, gw], w=[osb])
        kb.op('dve', lambda E: E.tensor_tensor(out=oa[b][:], in0=osb[:], in1=sgt[:].rearrange("p (h d) -> p h d", h=4), op=ALU.mult),
              r=[osb, sgt], w=[oa[b]])
        for h in range(4):
            kb.op('pe', lambda E: E.transpose(out=pB[0][:, h, :], in_=oa[b][:, h, :], identity=ident[:]), r=[oa[b], ident], w=[pB[0]])
        kb.op('act', lambda E: E.copy(out=oaT[b][:], in_=pB[0][:, 0:4, :]), r=[pB[0]], w=[oaT[b]])
        kb.dma('sp', io['ocatT'][t, :, 0:4, :], oaT[b][:], r=[oaT[b]], w=['ocatT_a'])

    def to_boundary(g):
        for r in g:
            if r == 'B':
                return

    cur = tile(0)
    to_boundary(cur)
    for t in range(NT):
        nxt = tile(t + 1) if t + 1 < NT else None
        cur_done, nxt_done = False, nxt is None
        while not (cur_done and nxt_done):
            if not cur_done:
                try:
                    next(cur)
                except StopIteration:
                    cur_done = True
            if not nxt_done:
                if next(nxt) == 'B':
                    nxt_done = True
        cur = nxt
    ph.close()


def rope16(kb, R, G, cs, tmp, e1='dve', e2='pool'):
    c = cs[:, 0:8].unsqueeze(1).to_broadcast([128, G, 8])
    sn = cs[:, 8:16].unsqueeze(1).to_broadcast([128, G, 8])
    x1 = R[:, 0:G, 0:8]
    x2 = R[:, 0:G, 8:16]
    kb.op(e1, lambda E: E.tensor_tensor(out=tmp[:, 0:G, 0:8], in0=x1, in1=c, op=ALU.mult), r=[R, cs], w=[(tmp, 0)])
    yield
    kb.op(e2, lambda E: E.tensor_tensor(out=tmp[:, 0:G, 8:16], in0=x2, in1=sn, op=ALU.mult), r=[R, cs], w=[(tmp, 1)])
    yield
    kb.op(e1, lambda E: E.tensor_tensor(out=tmp[:, 0:G, 16:24], in0=x2, in1=c, op=ALU.mult), r=[R, cs], w=[(tmp, 2)])
    yield
    kb.op(e2, lambda E: E.tensor_tensor(out=tmp[:, 0:G, 24:32], in0=x1, in1=sn, op=ALU.mult), r=[R, cs], w=[(tmp, 3)])
    yield
    kb.op(e1, lambda E: E.tensor_tensor(out=x1, in0=tmp[:, 0:G, 0:8], in1=tmp[:, 0:G, 8:16], op=ALU.subtract),
          r=[(tmp, 0), (tmp, 1), (tmp, 2), (tmp, 3)], w=[R])
    yield
    kb.op(e1, lambda E: E.tensor_tensor(out=x2, in0=tmp[:, 0:G, 16:24], in1=tmp[:, 0:G, 24:32], op=ALU.add),
          r=[(tmp, 0), (tmp, 1), (tmp, 2), (tmp, 3)], w=[R])
    yield


def rms_groups(kb, src3, G, dst3, sq, ss, wt, e_sq='pool'):
    (src_ap, src_keys) = src3
    (dst_ap, dst_keys) = dst3
    kb.op(e_sq, lambda E: E.tensor_tensor(out=sq[:, 0:G, :], in0=src_ap, in1=src_ap, op=ALU.mult), r=src_keys, w=[sq])
    yield
    kb.op('dve', lambda E: E.tensor_reduce(out=ss[:, 0:G], in_=sq[:, 0:G, :], axis=AX.X, op=ALU.add), r=[sq], w=[ss])
    yield
    kb.op('act', lambda E: E.activation(out=ss[:, 0:G], in_=ss[:, 0:G], func=AF.Sqrt, scale=1.0 / 64, bias=EPS), r=[ss], w=[ss])
    yield
    kb.op('dve', lambda E: E.reciprocal(out=ss[:, 0:G], in_=ss[:, 0:G]), r=[ss], w=[ss])
    yield
    kb.op('dve', lambda E: E.tensor_tensor(out=dst_ap, in0=src_ap, in1=ss[:, 0:G].unsqueeze(2).to_broadcast([128, G, 64]), op=ALU.mult),
          r=src_keys + [ss], w=dst_keys)
    yield
    kb.op('pool', lambda E: E.tensor_tensor(out=dst_ap, in0=dst_ap, in1=wt[:].unsqueeze(1).to_broadcast([128, G, 64]), op=ALU.mult),
          r=dst_keys + [wt], w=dst_keys)
    yield


def phase_nsa(kb, io):
    nc = kb.nc
    ph = Phase(kb, "ns")
    ident = ph.sb("ident", [128, 128], BF16)
    tril = ph.sb("tril", [128, 128], BF16)
    far = ph.sb("far", [128, 128], BF16)
    cmask = ph.sb("cmask", [128, 9, 512], BF16)
    kvT = ph.sb("kvT", [128, 4, S], BF16)
    vs1 = ph.sb("vs1", [128, NT, 2, 65], BF16)
    vw1 = ph.sb("vw1", [128, NT, 2, 65], BF16)
    kcmpT = ph.sb("kcmpT", [64, 2, 256], BF16)
    rhs_cmp = ph.sb("rhs_cmp", [128, 2, 2, 128], BF16)
    kb.dma('sp', ident[:], io['c_ident'][:, :], w=[ident])
    kb.dma('sp', tril[:], io['c_tril'][:, :], w=[tril])
    kb.dma('sp', far[:], io['c_far'][:, :], w=[far])
    for m in range(9):
        kb.dma('sp', cmask[:, m, :], io['c_cmask'][m, :, :], w=[cmask])
    for g in range(2):
        kb.dma('sp', kvT[64:128, g, :], io['c_E'][:, :], w=[(kvT, 'E')])
    kb.op('pool', lambda E: E.memset(vs1[:], 1.0), w=[vs1])
    kb.op('pool', lambda E: E.memset(vw1[:], 1.0), w=[vw1])
    kb.op('pool', lambda E: E.memset(kcmpT[:], 0.0), w=[kcmpT])
    kb.op('pool', lambda E: E.memset(rhs_cmp[:], 0.0), w=[rhs_cmp])
    for bt in range(2):
        for g in range(2):
            kb.dma('sp', rhs_cmp[:, bt, g, 64:128], io['c_ovl'][:, bt, :], r=[rhs_cmp], w=[rhs_cmp])

    pp = Phase(kb, "np")
    kcT = pp.sb("kcT", [64, 4, S], BF16)
    ksw = pp.sb("ksw", [128, 64], F32)
    kww = pp.sb("kww", [128, 64], F32)
    kcw = pp.sb("kcw", [128, 64], F32)
    kb.dma('sp', ksw[:], io['nsa_ks_norm_w'].partition_broadcast(128), w=[ksw])
    kb.dma('sp', kww[:], io['nsa_kw_norm_w'].partition_broadcast(128), w=[kww])
    kb.dma('sp', kcw[:], io['nsa_kc_norm_w'].partition_broadcast(128), w=[kcw])
    kvb = pp.sbn("kvb", [128, 768], F32, 4)
    cst = pp.sbn("cst", [128, 16], F32, 4)
    R = pp.sbn("R", [128, 6, 64], F32, 4)
    sqb = pp.sbn("sq", [128, 2, 64], F32, 4)
    ssb = pp.sbn("ss", [128, 2], F32, 4)
    tmpb = pp.sbn("tmp", [128, 6, 32], F32, 4)
    sq, ss = sqb[0], ssb[0]
    k16 = pp.sbn("k16", [128, 8, 64], BF16, 4)
    pT8 = pp.psn("pT8", [128, 8, 128], BF16, 2)
    def kvtile(t):
        b = t % 4
        kv = kvb[b]
        Rb = R[b]
        sq, ss, tmp = sqb[b], ssb[b], tmpb[b]
        kb.dma('sp', kv[:], io['tm'][t * 128:(t + 1) * 128, KC_OFF:KC_OFF + 768], r=['tm'], w=[kv])
        kb.dma('sp', cst[b][:], io['c_rope'][t * 128:(t + 1) * 128, :], w=[cst[b]])
        v3 = lambda off: kv[:, off:off + 128].rearrange("p (g d) -> p g d", g=2)
        kb.op('pool', lambda E: E.tensor_copy(out=Rb[:, 0:2, :], in_=v3(0)), r=[kv], w=[Rb])
        yield
        yield from rms_groups(kb, (v3(256), [kv]), 2, (Rb[:, 2:4, :], [Rb]), sq, ss, ksw)
        yield from rms_groups(kb, (v3(512), [kv]), 2, (Rb[:, 4:6, :], [Rb]), sq, ss, kww)
        yield from rope16(kb, Rb, 6, cst[b], tmp)
        kk = k16[b]
        kb.op('act', lambda E: E.copy(out=kk[:, 0:6, :], in_=Rb[:]), r=[Rb], w=[kk])
        kb.op('pool', lambda E: E.tensor_copy(out=kk[:, 6:8, :], in_=v3(128)), r=[kv], w=[kk])
        kb.op('dve', lambda E: E.tensor_copy(out=vs1[:, t, :, 0:64], in_=v3(384)), r=[kv], w=[vs1])
        kb.op('pool', lambda E: E.tensor_copy(out=vw1[:, t, :, 0:64], in_=v3(640)), r=[kv], w=[vw1])
        yield
        p8 = pT8[t % 2]
        for i in range(8):
            kb.op('pe', lambda E: E.transpose(out=p8[0:64, i, :], in_=kk[:, i, :], identity=ident[:]), r=[kk, ident], w=[p8])
        kb.op('act', lambda E: E.copy(out=kvT[0:64, :, t * 128:(t + 1) * 128], in_=p8[0:64, 2:6, :]), r=[p8], w=[(kvT, 'k')])
        kb.op('dve', lambda E: E.tensor_copy(out=kcT[0:64, 0:2, t * 128:(t + 1) * 128], in_=p8[0:64, 0:2, :]), r=[p8], w=[kcT])
        kb.op('dve', lambda E: E.tensor_copy(out=kcT[0:64, 2:4, t * 128:(t + 1) * 128], in_=p8[0:64, 6:8, :]), r=[p8], w=[kcT])

    def interleave(gens):
        gens = list(gens)
        while gens:
            for g_ in list(gens):
                try:
                    next(g_)
                except StopIteration:
                    gens.remove(g_)

    for t in range(0, NT, 4):
        interleave([kvtile(t + i) for i in range(4)])
    w1f = pp.sb("w1f", [64, 32, 64], F32)
    w1b = pp.sbn("w1b", [64, 32, 64], BF16, 2)
    w2f = pp.sb("w2f", [64, 64], F32)
    w2b = pp.sbn("w2b", [64, 64], BF16, 2)
    posf = pp.sb("posf", [64, 32], F32)
    pos2 = pp.sbn("pos2", [64, 32, 2], BF16, 2)
    bias = pp.sb("bias", [64, 2], F32)
    h1T = pp.sb("h1T", [64, 256], BF16)
    o2 = pp.sb("o2", [128, 1, 64], F32)
    o2n = pp.sb("o2n", [128, 1, 64], F32)
    kcn = pp.sb("kcn", [128, 64], BF16)
    pH = pp.ps("pH", [128, 512], F32)
    pB_ = pp.ps("pBi", [128, 512], F32)
    pO2 = pp.ps("pO2", [128, 512], F32)
    kb.op('pool', lambda E: E.memset(h1T[:], 0.0), w=[h1T])
    for kind, (n1, n2, npos) in enumerate((('nsa_cmp_k_w1', 'nsa_cmp_k_w2', 'nsa_cmp_pos_k'), ('nsa_cmp_v_w1', 'nsa_cmp_v_w2', 'nsa_cmp_pos_v'))):
        kb.dma('sp', w1f[:], io[n1].rearrange("(l d) o -> d l o", d=64), w=[w1f])
        kb.op('dve', lambda E: E.tensor_copy(out=w1b[kind][:], in_=w1f[:]), r=[w1f], w=[w1b[kind]])
        kb.dma('sp', w2f[:], io[n2][:, :], w=[w2f])
        kb.op('dve', lambda E: E.tensor_copy(out=w2b[kind][:], in_=w2f[:]), r=[w2f], w=[w2b[kind]])
        kb.dma('sp', posf[:], io[npos].rearrange("l d -> d l"), w=[posf], allow_slow_non_contiguous=True)
        for j in range(2):
            kb.op('dve', lambda E: E.tensor_copy(out=pos2[kind][:, :, j], in_=posf[:]), r=[posf], w=[pos2[kind]])
        for l in range(32):
            kb.op('pe', lambda E: E.matmul(pB_[0:64, 0:2], lhsT=w1b[kind][:, l, :], rhs=pos2[kind][:, l, :], start=(l == 0), stop=(l == 31)),
                  r=[w1b[kind], pos2[kind]], w=[pB_])
        kb.op('dve', lambda E: E.tensor_copy(out=bias[:], in_=pB_[0:64, 0:2]), r=[pB_], w=[bias])
        for g in range(2):
            ki = kind * 2 + g
            for l in range(32):
                kb.op('pe', lambda E: E.matmul(pH[0:64, 0:255], lhsT=w1b[kind][:, l, :], rhs=kcT[0:64, ki, l:l + 16 * 254 + 1:16],
                                               start=(l == 0), stop=(l == 31)), r=[w1b[kind], kcT], w=[pH])
            kb.op('act', lambda E: E.activation(out=h1T[:, 0:255], in_=pH[0:64, 0:255], func=AF.Silu, bias=bias[:, 0:1]),
                  r=[pH, bias], w=[h1T])
            for bt in range(2):
                kb.op('pe', lambda E: E.matmul(pO2[:, 0:64], lhsT=h1T[:, bt * 128:(bt + 1) * 128], rhs=w2b[kind][:], start=True, stop=True),
                      r=[h1T, w2b[kind]], w=[pO2])
                if kind == 0:
                    kb.op('act', lambda E: E.copy(out=o2[:, 0, :], in_=pO2[:, 0:64]), r=[pO2], w=[o2])
                    for _ in rms_groups(kb, (o2[:], [o2]), 1, (o2n[:], [o2n]), sq, ss, kcw):
                        pass
                    kb.op('act', lambda E: E.copy(out=kcn[:], in_=o2n[:, 0, :]), r=[o2n], w=[kcn])
                    p8 = pT8[0]
                    kb.op('pe', lambda E: E.transpose(out=p8[0:64, 0, :], in_=kcn[:], identity=ident[:]), r=[kcn, ident], w=[p8])
                    kb.op('act', lambda E: E.copy(out=kcmpT[:, g, bt * 128:(bt + 1) * 128], in_=p8[0:64, 0, :]), r=[p8], w=[kcmpT])
                else:
                    kb.op('act', lambda E: E.copy(out=rhs_cmp[:, bt, g, 0:64], in_=pO2[:, 0:64]), r=[pO2], w=[rhs_cmp])
    pp.close()

    pa = Phase(kb, "na")
    NPT = 8
    PTb = pa.sbn("PT", [128, 512], BF16, NPT)
    pti = [0]
    qaug = pa.sbn("qaug", [128, 8, 512], BF16, 2)
    qnw = pa.sb("qnw", [128, 64], F32)
    kb.dma('sp', qnw[:], io['nsa_q_norm_w'].partition_broadcast(128), w=[qnw])
    kb.op('dve', lambda E: E.tensor_scalar(out=qnw[:], in0=qnw[:], scalar1=0.125, scalar2=None, op0=ALU.mult), r=[qnw], w=[qnw])
    qf = pa.sbn("qf", [128, 512], F32, 4)
    cst = pa.sbn("cst", [128, 16], F32, 4)
    Rqb = pa.sbn("Rq", [128, 8, 64], F32, 4)
    sqb = pa.sbn("sq", [128, 8, 64], F32, 4)
    ssb = pa.sbn("ss", [128, 8], F32, 4)
    tmpb = pa.sbn("tmp", [128, 8, 32], F32, 4)
    qa = pa.sbn("qa", [128, 8, 128], BF16, 4)
    gts = pa.sb("gts", [128, 4, 24], F32)
    Ab = pa.sbn("Ab", [128, 64], F32, 4)
    Bb = pa.sbn("Bb", [128, 64], F32, 4)
    ob = pa.sb("ob", [128, 4, 8, 64], F32)
    impacc = pa.sb("impacc", [128, 4, 64], F32)
    tmpi = pa.sb("tmpi", [128, 4, 64], F32)
    tmpo = pa.sb("tmpo", [128, 4, 64], F32)
    scr = pa.sb("scr", [128, 64], F32)
    scr2 = pa.sb("scr2", [128, 64], F32)
    m8 = pa.sb("m8", [128, 16], F32)
    nst = pa.sbn("nst", [128, 128], BF16, 2)
    fs = pa.sb("fs", [128, 12], F32)
    obb = pa.sbn("obb", [128, 512], BF16, 2)
    obT = pa.sbn("obT", [128, 4, 128], BF16, 2)
    sc_ps = pa.psn("sc", [128, 512], F32, 3)
    oC = pa.ps("oC", [128, 4, 128], F32)
    oW = pa.psn("oW", [128, 4, 128], F32, 1)
    oS = pa.psn("oS", [128, 4, 128], F32, 2)
    pTr = pa.psn("pTr", [128, 8, 128], BF16, 1)
    for i in range(4):
        kb.op('pool', lambda E: E.memset(qa[i][:], 0.0), w=[qa[i]])
    for i in range(2):
        kb.op('pool', lambda E: E.memset(nst[i][:], 0.0), w=[nst[i]])
    tri = 0

    def finalize(oX, h, br, first, cmp=False):
        if cmp:
            kb.op('dve', lambda E: E.tensor_reduce(out=fs[:, 0:4], in_=oX[:, :, 64:128], axis=AX.X, op=ALU.add), r=[oX], w=[fs])
            kb.op('dve', lambda E: E.tensor_scalar(out=fs[:, 0:4], in0=fs[:, 0:4], scalar1=0.5, scalar2=1e-30, op0=ALU.mult, op1=ALU.add),
                  r=[fs], w=[fs])
        else:
            kb.op('dve', lambda E: E.tensor_scalar(out=fs[:, 0:4], in0=oX[:, :, 64], scalar1=1e-30, scalar2=None, op0=ALU.add), r=[oX], w=[fs])
        kb.op('dve', lambda E: E.reciprocal(out=fs[:, 4:8], in_=fs[:, 0:4]), r=[fs], w=[fs])
        if cmp:
            dsti = impacc if h % 4 == 0 else tmpi
            kb.op('dve', lambda E: E.tensor_tensor(out=dsti[:], in0=oX[:, :, 64:128], in1=fs[:, 4:8].unsqueeze(2).to_broadcast([128, 4, 64]),
                                                   op=ALU.mult), r=[oX, fs], w=[dsti])
            if h % 4 != 0:
                kb.op('pool', lambda E: E.tensor_tensor(out=impacc[:], in0=impacc[:], in1=tmpi[:], op=ALU.add), r=[impacc, tmpi], w=[impacc])
        kb.op('dve', lambda E: E.tensor_tensor(out=fs[:, 8:12], in0=fs[:, 4:8], in1=gts[:, :, h * 3 + br], op=ALU.mult), r=[fs, gts], w=[fs])
        dst = ob[:, :, h, :] if first else tmpo[:]
        kb.op('dve', lambda E: E.tensor_tensor(out=dst, in0=oX[:, :, 0:64], in1=fs[:, 8:12].unsqueeze(2).to_broadcast([128, 4, 64]),
                                               op=ALU.mult), r=[oX, fs], w=[(ob, h) if first else tmpo])
        if not first:
            kb.op('pool', lambda E: E.tensor_tensor(out=ob[:, :, h, :], in0=ob[:, :, h, :], in1=tmpo[:], op=ALU.add),
                  r=[(ob, h), tmpo], w=[(ob, h)])

    for s in range(S // 512):
        qs_ = qaug[s % 2]
        def qtile(t4, s=s, qs_=qs_):
            t = 4 * s + t4
            b = t4
            q = qf[b]
            Rq, sq, ss, tmp = Rqb[b], sqb[b], ssb[b], tmpb[b]
            kb.dma('sp', q[:], io['tm'][t * 128:(t + 1) * 128, NQ_OFF:NQ_OFF + 512], r=['tm'], w=[q])
            kb.dma('sp', gts[:, t4, :], io['tm'][t * 128:(t + 1) * 128, NG_OFF:NG_OFF + 24], r=['tm'], w=[gts])
            kb.dma('sp', cst[b][:], io['c_rope'][t * 128:(t + 1) * 128, :], w=[cst[b]])
            kb.dma('sp', Ab[t4][:], io['c_A'][t, :, :], w=[Ab[t4]])
            kb.dma('sp', Bb[t4][:], io['c_B'][t, :, :], w=[Bb[t4]])
            kb.op('act', lambda E: E.activation(out=gts[:, t4, :], in_=gts[:, t4, :], func=AF.Exp, scale=-1.0), r=[gts], w=[gts])
            kb.op('dve', lambda E: E.tensor_scalar(out=gts[:, t4, :], in0=gts[:, t4, :], scalar1=1.0, scalar2=None, op0=ALU.add), r=[gts], w=[gts])
            kb.op('dve', lambda E: E.reciprocal(out=gts[:, t4, :], in_=gts[:, t4, :]), r=[gts], w=[gts])
            q3 = q[:].rearrange("p (h d) -> p h d", h=8)
            yield
            yield from rms_groups(kb, (q3, [q]), 8, (Rq[:], [Rq]), sq, ss, qnw)
            yield from rope16(kb, Rq, 8, cst[b], tmp)
            qab = qa[b]
            kb.op('act', lambda E: E.copy(out=qab[:, :, 0:64], in_=Rq[:]), r=[Rq], w=[qab])
            yield
            pt_ = pTr[0]
            for h in range(8):
                kb.op('pe', lambda E: E.transpose(out=pt_[:, h, :], in_=qab[:, h, :], identity=ident[:]), r=[qab, ident], w=[pt_])
            kb.op('act', lambda E: E.copy(out=qs_[:, :, t4 * 128:(t4 + 1) * 128], in_=pt_[:]), r=[pt_], w=[qs_])
        interleave([qtile(i) for i in range(4)])
        nbt = 1 if s < 4 else 2
        for g in range(2):
            for h in range(4 * g, 4 * g + 4):
                specs = []
                for bt in range(nbt):
                    m = (s if s <= 4 else None) if bt == 0 else 5 + (s - 4)
                    masks = [] if m is None else [(0, 512, cmask[:, m, :])]
                    specs.append(dict(lhsT=kcmpT[0:64, g, bt * 128:(bt + 1) * 128],
                                      rhs_fn=lambda q0, q1, h=h: qs_[0:64, h, q0 * 128:q1 * 128], qt0=0, qt1=4, masks=masks, mk=[cmask],
                                      v_fn=lambda qt, bt=bt, g=g: rhs_cmp[:, bt, g, :], nk=128, rk=[kcmpT], rv=[rhs_cmp]))
                attn_block(kb, lambda qt: (oC[:, qt, :], oC, 'C'), PTb, pti, sc_ps, specs, ident, [qs_])
                finalize(oC, h, 0, True, cmp=True)
            for qt in range(4):
                ns_ = nst[qt % 2]
                kb.op('dve', lambda E: E.tensor_tensor(out=scr[:], in0=impacc[:, qt, :], in1=Ab[qt][:], op=ALU.mult), r=[impacc, Ab[qt]], w=[scr])
                kb.op('dve', lambda E: E.tensor_tensor(out=scr[:], in0=scr[:], in1=Bb[qt][:], op=ALU.add), r=[scr, Bb[qt]], w=[scr])
                kb.op('dve', lambda E: E.max(out=m8[:, 0:8], in_=scr[:]), r=[scr], w=[(m8, 0)])
                kb.op('dve', lambda E: E.match_replace(out=scr2[:], in_to_replace=m8[:, 0:8], in_values=scr[:], imm_value=-1e30),
                      r=[scr, (m8, 0)], w=[scr2])
                kb.op('dve', lambda E: E.max(out=m8[:, 8:16], in_=scr2[:]), r=[scr2], w=[(m8, 1)])
                kb.op('dve', lambda E: E.tensor_scalar(out=ns_[:, 64:128], in0=scr[:], scalar1=m8[:, 15:16], scalar2=1.0, op0=ALU.is_ge,
                                                       op1=ALU.subtract), r=[scr, (m8, 1)], w=[ns_])
                pt_ = pTr[0]
                tri += 1
                kb.op('pe', lambda E: E.transpose(out=pt_[:, 0, :], in_=ns_[:], identity=ident[:]), r=[ns_, ident], w=[pt_])
                for h in range(4 * g, 4 * g + 4):
                    if h % 2 == 0:
                        kb.op('act', lambda E: E.copy(out=qs_[64:128, h, qt * 128:(qt + 1) * 128], in_=pt_[64:128, 0, :]), r=[pt_], w=[qs_])
                    else:
                        kb.op('dve', lambda E: E.tensor_copy(out=qs_[64:128, h, qt * 128:(qt + 1) * 128], in_=pt_[64:128, 0, :]), r=[pt_], w=[qs_])
        for h in range(8):
            g = h // 4
            specs = []
            for kt in range(max(0, 4 * s - 4), 4 * s + 4):
                lo = max(kt - 4 * s, 0)
                hi = min(kt + 4 - 4 * s, 3)
                masks = []
                if kt >= 4 * s:
                    masks.append(((kt - 4 * s - lo) * 128, 128, tril[:]))
                if kt + 4 <= 4 * s + 3:
                    masks.append(((kt + 4 - 4 * s - lo) * 128, 128, far[:]))
                specs.append(dict(lhsT=kvT[0:64, 2 + g, kt * 128:(kt + 1) * 128],
                                  rhs_fn=lambda q0, q1, h=h: qs_[0:64, h, q0 * 128:q1 * 128], qt0=lo, qt1=hi + 1, masks=masks, mk=[tril, far],
                                  v_fn=lambda qt, kt=kt, g=g: vw1[:, kt, g, :], nk=128, rk=[(kvT, 'k')], rv=[vw1]))
            oW_ = oW[0]
            attn_block(kb, lambda qt: (oW_[:, qt, 0:65], oW_, 'W'), PTb, pti, sc_ps, specs, ident, [qs_], LA=3)
            finalize(oW_, h, 2, False)
            specs = []
            for kt in range(0, 4 * s + 4):
                lo = max(kt - 4 * s, 0)
                masks = [(0, 128, tril[:])] if kt >= 4 * s else []
                specs.append(dict(lhsT=kvT[:, g, kt * 128:(kt + 1) * 128],
                                  rhs_fn=lambda q0, q1, h=h: qs_[:, h, q0 * 128:q1 * 128], qt0=lo, qt1=4, masks=masks, mk=[tril],
                                  v_fn=lambda qt, kt=kt, g=g: vs1[:, kt, g, :], nk=128, rk=[(kvT, 'k'), (kvT, 'E')], rv=[vs1]))
            oS_ = oS[h % 2]
            attn_block(kb, lambda qt: (oS_[:, qt, 0:65], oS_, 'S'), PTb, pti, sc_ps, specs, ident, [qs_], LA=3)
            finalize(oS_, h, 1, False)
        for qt in range(4):
            t = 4 * s + qt
            b = t % 2
            kb.op('act', lambda E: E.copy(out=obb[b][:], in_=ob[:, qt, :, :].rearrange("p h d -> p (h d)")), r=[(ob, h) for h in range(8)], w=[obb[b]])
            pt_ = pTr[0]
            tri += 1
            for c in range(4):
                kb.op('pe', lambda E: E.transpose(out=pt_[:, c, :], in_=obb[b][:, c * 128:(c + 1) * 128], identity=ident[:]), r=[obb[b], ident], w=[pt_])
            kb.op('act', lambda E: E.copy(out=obT[b][:], in_=pt_[:, 0:4, :]), r=[pt_], w=[obT[b]])
            kb.dma('sp', io['ocatT'][t, :, 4:8, :], obT[b][:], r=[obT[b]], w=['ocatT_b'])
    pa.close()
    ph.close()


def phase_ffn(kb, io):
    nc = kb.nc
    ph = Phase(kb, "f1")
    ident = ph.sb("ident", [128, 128], BF16)
    kb.dma('sp', ident[:], io['c_ident'][:, :], w=[ident])
    woutb = ph.sb("woutb", [128, 12, D], BF16)
    wupb = ph.sb("wupb", [128, 8, 2 * FF], BF16)
    gam = ph.sb("gam", [128, 8], F32)
    cw = ph.sb("cw", [128, 3, 44], F32)
    kb.dma('sp', gam[:], io['ffn_norm_w'].rearrange("(c p) -> p c", p=128), w=[gam], allow_slow_non_contiguous=True)
    for j in range(3):
        kb.dma('sp', cw[:, j, :], io['ffn_conv_w'][j, :].rearrange("(c p) -> p c", p=128), w=[cw], allow_slow_non_contiguous=True)
    load_cast_weight(kb, ph, woutb, io['w_out'], 12, D)
    load_cast_weight(kb, ph, wupb, io['ffn_w_up'], 8, 2 * FF, gam=gam, stage_cols=1408)

    oT = ph.sbn("oT", [128, 12, 128], BF16, 2)
    xt = ph.sbn("xt", [128, D], F32, 2)
    hs = ph.sbn("hs", [128, D], F32, 2)
    junk = ph.sb("junk", [128, D], BF16)
    ss = ph.sbn("ss", [128, 1], F32, 2)
    rs = ph.sbn("rs", [128, 1], F32, 2)
    hn = ph.sbn("hn", [128, D], BF16, 2)
    hnT = ph.sbn("hnT", [128, 8, 512], BF16, 2)
    ug = ph.sbn("ug", [128, 514], F32, 2)
    uv = ph.sbn("uv", [128, 514], F32, 2)
    ag = ph.sbn("ag", [128, 512], F32, 2)
    av = ph.sbn("av", [128, 512], F32, 2)
    sg = ph.sbn("sg", [128, 512], F32, 2)
    act = ph.sbn("act", [128, 512], BF16, 3)
    halo = ph.sb("halo", [128, 44, 2], F32)
    pH = ph.psn("pH", [128, 512], F32, 2)
    pT = ph.ps("pT", [128, 8, 128], BF16)
    pU = ph.psn("pU", [128, 512], F32, 4)
    kb.op('pool', lambda E: E.memset(halo[:], 0.0), w=[halo])

    def conv3(ps_, dst, src, fb):
        kb.op('act', lambda E: E.activation(out=dst[:], in_=ps_[:], func=AF.Copy, scale=cw[:, 2, fb:fb + 1]), r=[ps_, cw], w=[dst])
        for j in range(2):
            kb.op('dve', lambda E: E.scalar_tensor_tensor(out=dst[:], in0=src[:, j:j + 512], scalar=cw[:, j, fb:fb + 1], in1=dst[:],
                                                       op0=ALU.mult, op1=ALU.add), r=[(src, 'b'), (src, 'h'), cw, dst], w=[dst])

    def ftiles(s):
        hT = hnT[s % 2]
        for t4 in range(4):
            t = s * 4 + t4
            b = t % 2
            kb.dma('sp', oT[b][:], io['ocatT'][t, :, :, :],
                   r=['ocatT_a', 'ocatT_b', 'ocatT_c'], w=[oT[b]])
            kb.dma('sp', xt[b][:], io['x'][t * 128:(t + 1) * 128, :], w=[xt[b]])
            yield
            for half in range(2):
                p = pH[half]
                for c in range(12):
                    kb.op('pe', lambda E: E.matmul(p[:], lhsT=oT[b][:, c, :], rhs=woutb[:, c, half * 512:(half + 1) * 512],
                                                   start=(c == 0), stop=(c == 11)), r=[oT[b], (woutb, c)], w=[p])
                kb.op('dve', lambda E: E.tensor_tensor(out=hs[b][:, half * 512:(half + 1) * 512], in0=p[:],
                                                       in1=xt[b][:, half * 512:(half + 1) * 512], op=ALU.add),
                      r=[p, xt[b]], w=[(hs[b], half)])
            yield
            kb.dma('pool', io['h_s'][t * 128:(t + 1) * 128, :], hs[b][:], r=[(hs[b], 0), (hs[b], 1)], w=['h_s'])
            rms_rstd(kb, hs[b][:], junk, ss[b], rs[b], D, [(hs[b], 0), (hs[b], 1)])
            kb.op('act', lambda E: E.activation(out=hn[b][:], in_=hs[b][:], func=AF.Copy, scale=rs[b][:, 0:1]),
                  r=[(hs[b], 0), (hs[b], 1), rs[b]], w=[hn[b]])
            yield
            yield
            for c in range(8):
                kb.op('pe', lambda E: E.transpose(out=pT[:, c, :], in_=hn[b][:, c * 128:(c + 1) * 128], identity=ident[:]),
                      r=[hn[b], ident], w=[pT])
            kb.op('act', lambda E: E.copy(out=hT[:, :, t4 * 128:(t4 + 1) * 128], in_=pT[:]), r=[pT], w=[hT])
            yield

    def fup(s):
        hT = hnT[s % 2]
        for fb in range(22):
            k2 = fb % 2
            pg = pU[2 * k2]
            pv = pU[2 * k2 + 1]
            for (p_, f0) in ((pg, fb), (pv, 22 + fb)):
                for c in range(8):
                    kb.op('pe', lambda E: E.matmul(p_[:], lhsT=wupb[:, c, f0 * 128:(f0 + 1) * 128], rhs=hT[:, c, :],
                                                   start=(c == 0), stop=(c == 7)), r=[hT, (wupb, c)], w=[p_])
            yield
            g_, v_ = ug[k2], uv[k2]
            kb.op('act', lambda E: E.copy(out=g_[:, 2:514], in_=pg[:]), r=[pg], w=[(g_, 'b')])
            kb.op('act', lambda E: E.copy(out=v_[:, 2:514], in_=pv[:]), r=[pv], w=[(v_, 'b')])
            for (u_, f0) in ((g_, fb), (v_, 22 + fb)):
                kb.op('dve', lambda E: E.tensor_copy(out=u_[:, 0:2], in_=halo[:, f0, :]), r=[(halo, f0)], w=[(u_, 'h')])
                kb.op('dve', lambda E: E.tensor_copy(out=halo[:, f0, :], in_=u_[:, 512:514]), r=[(u_, 'b')], w=[(halo, f0)])
            conv3(pg, ag[k2], g_, fb)
            conv3(pv, av[k2], v_, 22 + fb)
            kb.op('act', lambda E: E.activation(out=sg[k2][:], in_=ag[k2][:], func=AF.Silu), r=[ag[k2]], w=[sg[k2]])
            a_ = act[fb % 3]
            kb.op('dve', lambda E: E.tensor_tensor(out=a_[:], in0=sg[k2][:], in1=av[k2][:], op=ALU.mult), r=[sg[k2], av[k2]], w=[a_])
            kb.dma('pool', io['actT'][4 * s:4 * s + 4, :, fb, :].rearrange("j p t -> p j t"), a_[:].rearrange("p (j t) -> p j t", j=4), r=[a_], w=['actT'])
            yield

    def interleave(gens):
        gens = list(gens)
        while gens:
            for g_ in list(gens):
                try:
                    next(g_)
                except StopIteration:
                    gens.remove(g_)

    NS = S // 512
    interleave([ftiles(0)])
    for s in range(NS):
        gl = [fup(s)]
        if s + 1 < NS:
            gl.append(ftiles(s + 1))
        interleave(gl)
    ph.close()

    ph = Phase(kb, "f2")
    wdnb = ph.sb("wdnb", [128, 22, D], BF16)
    load_cast_weight(kb, ph, wdnb, io['ffn_w_down'], 22, D)
    aT = ph.sbn("aT", [128, 22, 128], BF16, 3)
    hs = ph.sbn("hs", [128, D], F32, 3)
    ot = ph.sbn("ot", [128, D], F32, 2)
    pD = ph.psn("pD", [128, 512], F32, 4)

    def ld(t):
        kb.dma('sp', aT[t % 3][:], io['actT'][t, :, :, :], r=['actT'], w=[aT[t % 3]])
        kb.dma('sp', hs[t % 3][:], io['h_s'][t * 128:(t + 1) * 128, :], r=['h_s'], w=[hs[t % 3]])

    ld(0)
    ld(1)
    for t in range(NT):
        b = t % 2
        a_, h_ = aT[t % 3], hs[t % 3]
        if t + 2 < NT:
            ld(t + 2)
        for half in range(2):
            p = pD[2 * b + half]
            for c in range(22):
                kb.op('pe', lambda E: E.matmul(p[:], lhsT=a_[:, c, :], rhs=wdnb[:, c, half * 512:(half + 1) * 512],
                                               start=(c == 0), stop=(c == 21)), r=[a_, (wdnb, c)], w=[p])
            kb.op('dve', lambda E: E.tensor_tensor(out=ot[b][:, half * 512:(half + 1) * 512], in0=p[:],
                                                   in1=h_[:, half * 512:(half + 1) * 512], op=ALU.add),
                  r=[p, h_], w=[(ot[b], half)])
        kb.dma('pool', io['out'][t * 128:(t + 1) * 128, :], ot[b][:], r=[(ot[b], 0), (ot[b], 1)], w=['out'])
    ph.close()


W_NAMES = ['attn_norm_w', 'mem_norm_w', 'w_in', 'gdn_conv_w', 'gdn_a_log', 'gdn_dt_bias', 'gdn_out_norm_w',
           'nsa_q_norm_w', 'nsa_kc_norm_w', 'nsa_ks_norm_w', 'nsa_kw_norm_w', 'nsa_cmp_pos_k', 'nsa_cmp_pos_v',
           'nsa_cmp_k_w1', 'nsa_cmp_k_w2', 'nsa_cmp_v_w1', 'nsa_cmp_v_w2', 'mem_w_kv', 'mem_q_norm_w', 'mem_k_norm_w',
           'w_out', 'ffn_norm_w', 'ffn_w_up', 'ffn_conv_w', 'ffn_w_down']
W_SHAPES = {
    'attn_norm_w': [D], 'mem_norm_w': [D], 'w_in': [D, INW], 'gdn_conv_w': [4, 1536], 'gdn_a_log': [4], 'gdn_dt_bias': [4],
    'gdn_out_norm_w': [128], 'nsa_q_norm_w': [64], 'nsa_kc_norm_w': [64], 'nsa_ks_norm_w': [64], 'nsa_kw_norm_w': [64],
    'nsa_cmp_pos_k': [32, 64], 'nsa_cmp_pos_v': [32, 64], 'nsa_cmp_k_w1': [2048, 64], 'nsa_cmp_k_w2': [64, 64],
    'nsa_cmp_v_w1': [2048, 64], 'nsa_cmp_v_w2': [64, 64], 'mem_w_kv': [D, 1024], 'mem_q_norm_w': [128], 'mem_k_norm_w': [128],
    'w_out': [1536, D], 'ffn_norm_w': [D], 'ffn_w_up': [D, 2 * FF], 'ffn_conv_w': [3, 2 * FF], 'ffn_w_down': [FF, D],
}


def make_consts():
    c = {}
    c['c_ident'] = np.eye(128, dtype=np.float32).astype(ml_dtypes.bfloat16)
    idx = np.arange(128)
    same = (idx[:, None] // 64) == (idx[None, :] // 64)
    c['c_btri'] = (same & (idx[:, None] <= idx[None, :])).astype(np.float32)
    c['c_bones'] = same.astype(np.float32)
    c['c_strict'] = (same & (idx[:, None] > idx[None, :])).astype(np.float32)
    c['c_mlow'] = np.where(same & (idx[:, None] >= idx[None, :]), 0.0, NEG).astype(np.float32)
    c['c_mup'] = np.ascontiguousarray(c['c_mlow'].T)
    c['c_mch'] = np.stack([(idx < 64), (idx >= 64)], axis=1).astype(np.float32)
    bf = ml_dtypes.bfloat16
    c['c_tril'] = np.where(idx[:, None] <= idx[None, :], 0.0, NEG).astype(np.float32).astype(bf)
    c['c_far'] = np.where(idx[None, :] < idx[:, None], 0.0, NEG).astype(np.float32).astype(bf)
    cm = np.zeros((9, 128, 512), np.float32)
    f = np.arange(512)
    for m in range(9):
        bt, s_ = (0, m) if m < 5 else (1, m - 1)
        blk = 128 * bt + idx
        vis = (16 * blk[:, None] + 31 <= 512 * s_ + f[None, :]) & (blk[:, None] < 255)
        cm[m] = np.where(vis, 0.0, NEG)
    c['c_cmask'] = cm.astype(bf)
    kk = np.arange(S)
    c['c_E'] = np.where((kk[None, :] // 64) == np.arange(64)[:, None], -NEG, 0.0).astype(np.float32).astype(bf)
    ci = np.arange(256) * 16
    sj = np.arange(64) * 64
    ovl = np.clip(np.minimum(ci[:, None] + 32, sj[None, :] + 64) - np.maximum(ci[:, None], sj[None, :]), 0, None) / 16.0
    ovl[255] = 0.0
    c['c_ovl'] = np.ascontiguousarray(ovl.reshape(2, 128, 64).transpose(1, 0, 2)).astype(np.float32).astype(bf)
    pos = np.arange(S, dtype=np.float32)
    inv = (1.0 / (np.float32(500000.0) ** (np.arange(0, 16, 2, dtype=np.float32) / np.float32(16)))).astype(np.float32)
    ang = pos[:, None] * inv[None, :]
    c['c_rope'] = np.concatenate([np.cos(ang), np.sin(ang)], axis=1).astype(np.float32)
    tt = np.arange(S)
    cur = tt // 64
    blk = np.arange(64)
    valid = blk[None, :] <= cur[:, None]
    forced = (blk[None, :] == 0) | (blk[None, :] == cur[:, None]) | (blk[None, :] == cur[:, None] - 1)
    c['c_A'] = (valid & ~forced).astype(np.float32).reshape(NT, 128, 64)
    c['c_B'] = np.where(valid, np.where(forced, 1e6, 0.0), -1e9).astype(np.float32).reshape(NT, 128, 64)
    return c


def build_program(dbg=False, phases=('ip', 'mem', 'gdn', 'nsa', 'ffn'), dbg_ocat=False):
    nc = bass.Bass("TRN2", target_bir_lowering=False)
    io = {}
    io['x'] = nc.dram_tensor("x", [S, D], F32, kind="ExternalInput").ap()
    io['mem'] = nc.dram_tensor("mem", [256, D], F32, kind="ExternalInput").ap()
    for n in W_NAMES:
        io[n] = nc.dram_tensor(n, W_SHAPES[n], F32, kind="ExternalInput").ap()
    for n, v in make_consts().items():
        io[n] = nc.dram_tensor(n, list(v.shape), BF16 if v.dtype == ml_dtypes.bfloat16 else F32, kind="ExternalInput").ap()
    io['out'] = nc.dram_tensor("out", [S, D], F32, kind="ExternalOutput").ap()
    sk = "ExternalOutput" if dbg else "Internal"
    io['tm'] = nc.dram_tensor("tm", [S, TMW], F32, kind=sk).ap()
    io['qkv_tm'] = nc.dram_tensor("qkv_tm", [S, 1536], BF16, kind=sk).ap()
    if dbg_ocat:
        io['ocatT'] = nc.dram_tensor("ocatT", [NT, 128, 12, 128], BF16, kind="ExternalInput").ap()
    else:
        io['ocatT'] = nc.dram_tensor("ocatT", [NT, 128, 12, 128], BF16, kind=sk).ap()
    io['h_s'] = nc.dram_tensor("h_s", [S, D], F32, kind=sk).ap()
    io['actT'] = nc.dram_tensor("actT", [NT, 128, 22, 128], BF16, kind="Internal").ap()
    kb = KB(nc)
    if 'ip' in phases:
        phase_inproj(kb, io)
    if 'mem' in phases:
        phase_mem(kb, io)
    if 'gdn' in phases:
        phase_gdn(kb, io)
    if 'nsa' in phases:
        phase_nsa(kb, io)
    if 'ffn' in phases:
        phase_ffn(kb, io)
    kb.finish()
    return nc, kb


def make_in_maps(inputs):
    consts = make_consts()
    maps = []
    for b in range(8):
        m = {'x': np.ascontiguousarray(inputs['x'][b]), 'mem': np.ascontiguousarray(inputs['mem'][b])}
        for n in W_NAMES:
            m[n] = np.ascontiguousarray(np.asarray(inputs[n])[0])
        m.update(consts)
        maps.append(m)
    return maps


def kernel(**inputs):
    nc, kb = build_program()
    maps = make_in_maps(inputs)
    res = run_bass_kernel_spmd(nc, maps, core_ids=list(range(8)))
    return np.stack([np.asarray(r['out'], dtype=np.float32) for r in res.results], axis=0)
```

```python
import os
import numpy as np
from contextlib import ExitStack
import concourse.bass as bass
import concourse.mybir as mybir
from concourse.bass_utils import run_bass_kernel_spmd
import ml_dtypes

F32 = mybir.dt.float32
BF16 = mybir.dt.bfloat16
AF = mybir.ActivationFunctionType
ALU = mybir.AluOpType
AX = mybir.AxisListType

S = 4096
D = 1024
NT = S // 128
INW = 3872
TMW = 2336
A_OFF, B_OFF, GATE_OFF, NQ_OFF = 0, 4, 8, 520
KC_OFF, VC_OFF, KS_OFF, VS_OFF, KW_OFF, VW_OFF = 1032, 1160, 1288, 1416, 1544, 1672
NG_OFF, MQ_OFF = 1800, 1824
FF = 2816
NEG = -30000.0
EPS = 1e-6


class T:
    def __init__(self, t, k):
        self.t = t
        self.k = k

    def __getitem__(self, idx):
        return self.t[idx]


class KB:
    NDS = 16

    def __init__(self, nc):
        self.nc = nc
        self.stack = ExitStack()
        self.eng = {'pe': nc.tensor, 'act': nc.scalar, 'dve': nc.vector, 'pool': nc.gpsimd, 'sp': nc.sync}
        self.sem = {}
        for e in self.eng:
            self.sem[e] = self.stack.enter_context(nc.semaphore("s_" + e))
        for j in range(self.NDS):
            self.sem[('d', j)] = self.stack.enter_context(nc.semaphore("d_%d" % j))
        self.cnt = {e: 0 for e in self.eng}
        self.seen = {e: {} for e in self.eng}
        self.state = {}
        self.dma_i = 0
        self.dma_uses = [0] * self.NDS
        self.nins = 0
        self.rr = 0
        self.excl = set()

    def _wait(self, e, evs):
        need = {}
        for (sk, v) in evs:
            if sk == e and e in ('pe', 'sp'):
                continue
            if self.seen[e].get(sk, 0) < v:
                need[sk] = max(need.get(sk, 0), v)
        for sk, v in need.items():
            self.eng[e].wait_ge(self.sem[sk], v)
            self.seen[e][sk] = v

    @staticmethod
    def _keys(lst):
        out = []
        for x in lst:
            if isinstance(x, T):
                out.append(x.k)
            elif isinstance(x, (list, tuple)) and len(x) and isinstance(x[0], T):
                out.append((x[0].k,) + tuple(x[1:]))
            else:
                out.append(x)
        return out

    def _deps(self, reads, writes):
        evs = []
        for k in reads:
            st = self.state.get(k)
            if st and st[0]:
                evs.append(st[0])
        for k in writes:
            st = self.state.get(k)
            if st:
                if st[0]:
                    evs.append(st[0])
                evs.extend(st[1])
        return evs

    def _update(self, ev, reads, writes):
        for k in reads:
            st = self.state.setdefault(k, [None, []])
            st[1].append(ev)
            if len(st[1]) > 12:
                best = {}
                for (sk, v) in st[1]:
                    best[sk] = max(best.get(sk, 0), v)
                st[1] = list(best.items())
        for k in writes:
            self.state[k] = [ev, []]

    def op(self, e, fn, r=(), w=()):
        r = self._keys(r)
        w = self._keys(w)
        w = w + [k for k in r if k in self.excl and k not in w]
        self._wait(e, self._deps(r, w))
        ins = fn(self.eng[e])
        self.cnt[e] += 1
        ins.then_inc(self.sem[e], 1)
        self._update((e, self.cnt[e]), r, w)
        self.nins += 1
        return ins

    def dma(self, q, out, in_, r=(), w=(), **kw):
        r = self._keys(r)
        w = self._keys(w)
        j = self.dma_i % self.NDS
        self.dma_i += 1
        evs = self._deps(r, w)
        if self.dma_uses[j] > 0:
            evs.append((('d', j), 16 * self.dma_uses[j]))
        self._wait(q, evs)
        ins = self.eng[q].dma_start(out=out, in_=in_, **kw)
        self.dma_uses[j] += 1
        ins.then_inc(self.sem[('d', j)], 16)
        ev = (('d', j), 16 * self.dma_uses[j])
        self._update(ev, r, w)
        self.nins += 1
        return ev

    def barrier(self):
        evs = [(f, self.cnt[f]) for f in self.eng if self.cnt[f]]
        evs += [(('d', j), 16 * self.dma_uses[j]) for j in range(self.NDS) if self.dma_uses[j]]
        for e in self.eng:
            self._wait(e, [ev for ev in evs if ev[0] != e])

    def finish(self):
        self.barrier()
        self.stack.close()

    def ew(self, with_act=False):
        self.rr += 1
        lst = ('dve', 'pool', 'act') if with_act else ('dve', 'pool')
        return lst[self.rr % len(lst)]


class Phase:
    def __init__(self, kb, tag):
        self.kb = kb
        self.nc = kb.nc
        self.tag = tag
        self.st = ExitStack()

    def sb(self, name, shape, dt):
        n = self.tag + "_" + name
        return T(self.st.enter_context(self.nc.sbuf_tensor(n, list(shape), dt)), n)

    def sbn(self, name, shape, dt, n):
        return [self.sb("%s%d" % (name, i), shape, dt) for i in range(n)]

    def ps(self, name, shape, dt=F32):
        n = self.tag + "_" + name
        self.kb.excl.add(n)
        return T(self.st.enter_context(self.nc.psum_tensor(n, list(shape), dt)), n)

    def psn(self, name, shape, dt, n):
        return [self.ps("%s%d" % (name, i), shape, dt) for i in range(n)]

    def close(self):
        self.kb.barrier()
        self.st.close()


def load_cast_weight(kb, ph, dst, src_ap, nchunks, ncols, gam=None, stage_cols=None):
    stage_cols = stage_cols or ncols
    stg = ph.sbn("stg_" + dst.k, [128, stage_cols], F32, 2)
    i = 0
    engs = ('dve', 'act', 'dve')
    for c in range(nchunks):
        for c0 in range(0, ncols, stage_cols):
            c1 = min(ncols, c0 + stage_cols)
            sg = stg[i % 2]
            kb.dma('sp', sg[:, 0:c1 - c0], src_ap[c * 128:(c + 1) * 128, c0:c1], w=[sg])
            e = engs[i % 3]
            o = dst[:, c, c0:c1]
            if gam is None:
                if e == 'act':
                    kb.op(e, lambda E: E.copy(out=o, in_=sg[:, 0:c1 - c0]), r=[sg], w=[(dst, c)])
                else:
                    kb.op(e, lambda E: E.tensor_copy(out=o, in_=sg[:, 0:c1 - c0]), r=[sg], w=[(dst, c)])
            else:
                if e == 'act':
                    kb.op(e, lambda E: E.activation(out=o, in_=sg[:, 0:c1 - c0], func=AF.Copy, scale=gam[:, c:c + 1]),
                          r=[sg, gam], w=[(dst, c)])
                else:
                    kb.op(e, lambda E: E.tensor_scalar(out=o, in0=sg[:, 0:c1 - c0], scalar1=gam[:, c:c + 1], scalar2=None,
                                                       op0=ALU.mult), r=[sg, gam], w=[(dst, c)])
            i += 1


def rms_rstd(kb, src_ap, junk, ss, rs, n, rkeys):
    kb.op('act', lambda E: E.activation(out=junk[:, 0:n], in_=src_ap, func=AF.Square, accum_out=ss[:]), r=rkeys, w=[junk, ss])
    kb.op('act', lambda E: E.activation(out=rs[:], in_=ss[:], func=AF.Sqrt, scale=1.0 / n, bias=EPS), r=[ss], w=[rs])
    kb.op('dve', lambda E: E.reciprocal(out=rs[:], in_=rs[:]), r=[rs], w=[rs])


def phase_inproj(kb, io):
    nc = kb.nc
    ph = Phase(kb, "ip")
    winb = ph.sb("winb", [128, 8, INW], BF16)
    gam = ph.sb("gam", [128, 8], F32)
    cw = ph.sb("cw", [128, 4, 12], F32)
    ident = ph.sb("ident", [128, 128], BF16)
    kb.dma('sp', gam[:], io['attn_norm_w'].rearrange("(c p) -> p c", p=128), w=[gam], allow_slow_non_contiguous=True)
    for j in range(4):
        kb.dma('sp', cw[:, j, :], io['gdn_conv_w'][j, :].rearrange("(c p) -> p c", p=128), w=[cw], allow_slow_non_contiguous=True)
    kb.dma('sp', ident[:], io['c_ident'][:, :], w=[ident])
    load_cast_weight(kb, ph, winb, io['w_in'], 8, INW, gam=gam, stage_cols=1936)

    xt = ph.sbn("xt", [128, D], F32, 2)
    junk = ph.sb("junk", [128, D], BF16)
    ss = ph.sbn("ss", [128, 1], F32, 2)
    rs = ph.sbn("rs", [128, 1], F32, 2)
    xn = ph.sbn("xn", [128, D], BF16, 2)
    xnT = ph.sbn("xnT", [128, 8, 512], BF16, 2)
    xc = ph.sbn("xc", [128, 515], F32, 3)
    acc = ph.sbn("acc", [128, 512], F32, 3)
    halo = ph.sb("halo", [128, 12, 3], F32)
    qT = ph.sb("qT", [128, 12, 512], BF16)
    qtm = ph.sbn("qtm", [128, 1536], BF16, 2)
    tmt = ph.sbn("tmt", [128, TMW], F32, 2)
    pT = ph.psn("pT", [128, 8, 128], BF16, 2)
    pF = ph.psn("pF", [128, 512], F32, 2)
    pM = ph.psn("pM", [128, 512], F32, 2)
    pQ = ph.psn("pQ", [128, 8, 128], BF16, 2)

    kb.op('pool', lambda E: E.memset(halo[:], 0.0), w=[halo])
    def iptiles(s):
        xT = xnT[s % 2]
        for t4 in range(4):
            t = s * 4 + t4
            b = t % 2
            kb.dma('sp', xt[b][:], io['x'][t * 128:(t + 1) * 128, :], w=[xt[b]])
            rms_rstd(kb, xt[b][:], junk, ss[b], rs[b], D, [xt[b]])
            kb.op('dve', lambda E: E.tensor_scalar(out=xn[b][:], in0=xt[b][:], scalar1=rs[b][:, 0:1], scalar2=None, op0=ALU.mult),
                  r=[xt[b], rs[b]], w=[xn[b]])
            yield
            for c in range(8):
                kb.op('pe', lambda E: E.transpose(out=pT[b][:, c, :], in_=xn[b][:, c * 128:(c + 1) * 128], identity=ident[:]),
                      r=[xn[b], ident], w=[pT[b]])
            kb.op('act', lambda E: E.copy(out=xT[:, :, t4 * 128:(t4 + 1) * 128], in_=pT[b][:]), r=[pT[b]], w=[xT])
            yield

    def ipmain(s):
        xT = xnT[s % 2]
        for cb in range(12):
            pf = pF[cb % 2]
            for c in range(8):
                kb.op('pe', lambda E: E.matmul(pf[:], lhsT=winb[:, c, cb * 128:(cb + 1) * 128], rhs=xT[:, c, :],
                                               start=(c == 0), stop=(c == 7)), r=[xT, (winb, c)], w=[pf])
            yield
            x3 = xc[cb % 3]
            ac = acc[cb % 3]
            kb.op('act', lambda E: E.copy(out=x3[:, 3:515], in_=pf[:]), r=[pf], w=[(x3, 'b')])
            kb.op('dve', lambda E: E.tensor_copy(out=x3[:, 0:3], in_=halo[:, cb, :]), r=[(halo, cb)], w=[(x3, 'h')])
            kb.op('dve', lambda E: E.tensor_copy(out=halo[:, cb, :], in_=x3[:, 512:515]), r=[(x3, 'b')], w=[(halo, cb)])
            kb.op('act', lambda E: E.activation(out=ac[:], in_=pf[:], func=AF.Copy, scale=cw[:, 3, cb:cb + 1]), r=[pf, cw], w=[ac])
            for j in range(3):
                kb.op('dve', lambda E: E.scalar_tensor_tensor(out=ac[:], in0=x3[:, j:j + 512], scalar=cw[:, j, cb:cb + 1], in1=ac[:],
                                                           op0=ALU.mult, op1=ALU.add), r=[(x3, 'b'), (x3, 'h'), cw, ac], w=[ac])
            kb.op('act', lambda E: E.activation(out=qT[:, cb, :], in_=ac[:], func=AF.Silu), r=[ac], w=[(qT, cb)])
            yield
        for t4 in range(4):
            t = s * 4 + t4
            qm = qtm[t % 2]
            for g3 in range(3):
                pq = pQ[g3 % 2]
                for j in range(4):
                    cb = g3 * 4 + j
                    kb.op('pe', lambda E: E.transpose(out=pq[:, j, :], in_=qT[:, cb, t4 * 128:(t4 + 1) * 128], identity=ident[:]),
                          r=[(qT, cb), ident], w=[pq])
                e1 = 'dve' if g3 % 2 == 0 else 'act'
                if e1 == 'dve':
                    kb.op('dve', lambda E: E.tensor_copy(out=qm[:, g3 * 512:(g3 + 1) * 512], in_=pq[:, 0:4, :].rearrange("p a b -> p (a b)")),
                          r=[pq], w=[(qm, g3)])
                else:
                    kb.op('act', lambda E: E.copy(out=qm[:, g3 * 512:(g3 + 1) * 512], in_=pq[:, 0:4, :].rearrange("p a b -> p (a b)")),
                          r=[pq], w=[(qm, g3)])
            kb.dma('pool', io['qkv_tm'][t * 128:(t + 1) * 128, :], qm[:], r=[(qm, 0), (qm, 1), (qm, 2)], w=['qkv_tm'])
            yield
            tm = tmt[t % 2]
            for ci, n0 in enumerate(range(0, TMW, 512)):
                n1 = min(TMW, n0 + 512)
                pm = pM[ci % 2]
                for c in range(8):
                    kb.op('pe', lambda E: E.matmul(pm[:, 0:n1 - n0], lhsT=xT[:, c, t4 * 128:(t4 + 1) * 128],
                                                   rhs=winb[:, c, 1536 + n0:1536 + n1], start=(c == 0), stop=(c == 7)),
                          r=[xT, (winb, c)], w=[pm])
                if ci % 2 == 0:
                    kb.op('dve', lambda E: E.tensor_copy(out=tm[:, n0:n1], in_=pm[:, 0:n1 - n0]), r=[pm], w=[(tm, ci)])
                else:
                    kb.op('act', lambda E: E.copy(out=tm[:, n0:n1], in_=pm[:, 0:n1 - n0]), r=[pm], w=[(tm, ci)])
            kb.dma('pool', io['tm'][t * 128:(t + 1) * 128, :], tm[:], r=[(tm, i) for i in range(5)], w=['tm'])

    def interleave(gens):
        gens = list(gens)
        while gens:
            for g_ in list(gens):
                try:
                    next(g_)
                except StopIteration:
                    gens.remove(g_)

    NS = S // 512
    interleave([iptiles(0)])
    for s in range(NS):
        gl = [ipmain(s)]
        if s + 1 < NS:
            gl.append(iptiles(s + 1))
        interleave(gl)
    ph.close()


def attn_block(kb, o_ps, PTbuf, pti, sc_ps, kt_specs, ident, rkeys_q, LA=2):
    n = len(kt_specs)
    pts = [None] * n
    started = set()
    npv = sum(sp['qt1'] - sp['qt0'] for sp in kt_specs)
    done = 0
    for i in range(n + LA):
        if i < n:
            sp = kt_specs[i]
            pss = sc_ps[pti[0] % len(sc_ps)]
            ptb = PTbuf[pti[0] % len(PTbuf)]
            pti[0] += 1
            pts[i] = ptb
            q0, q1 = sp['qt0'], sp['qt1']
            ncol = (q1 - q0) * 128
            nk = sp['nk']
            nm = len(sp['masks'])
            kb.op('pe', lambda E: E.matmul(pss[0:nk, 0:ncol], lhsT=sp['lhsT'], rhs=sp['rhs_fn'](q0, q1), start=True, stop=(nm == 0)),
                  r=sp['rk'] + rkeys_q, w=[pss])
            for mi, (c0, nc_, mask) in enumerate(sp['masks']):
                kb.op('pe', lambda E: E.matmul(pss[0:nk, c0:c0 + nc_], lhsT=ident[0:nk, 0:nk], rhs=mask, start=False, stop=(mi == nm - 1)),
                      r=[ident] + sp.get('mk', []), w=[pss])
            kb.op('act', lambda E: E.activation(out=ptb[0:nk, 0:ncol], in_=pss[0:nk, 0:ncol], func=AF.Exp), r=[pss], w=[ptb])
        j = i - LA
        if j >= 0:
            sp = kt_specs[j]
            ptb = pts[j]
            nk = sp['nk']
            for qt in range(sp['qt0'], sp['qt1']):
                c0 = (qt - sp['qt0']) * 128
                oap, okey, bank = o_ps(qt)
                st = bank not in started
                started.add(bank)
                done += 1
                kb.op('pe', lambda E: E.matmul(oap, lhsT=ptb[0:nk, c0:c0 + 128], rhs=sp['v_fn'](qt), start=st, stop=(done == npv),
                                               skip_group_check=True), r=[ptb] + sp['rv'], w=[okey])


def phase_mem(kb, io):
    nc = kb.nc
    ph = Phase(kb, "mm")
    ident = ph.sb("ident", [128, 128], BF16)
    kb.dma('sp', ident[:], io['c_ident'][:, :], w=[ident])
    wkv = ph.sb("wkv", [128, 8, 1024], BF16)
    gam = ph.sb("gam", [128, 8], F32)
    kb.dma('sp', gam[:], io['mem_norm_w'].rearrange("(c p) -> p c", p=128), w=[gam], allow_slow_non_contiguous=True)
    load_cast_weight(kb, ph, wkv, io['mem_w_kv'], 8, 1024, gam=gam)
    qnw = ph.sb("qnw", [128, 128], F32)
    knw = ph.sb("knw", [128, 128], F32)
    kb.dma('sp', qnw[:], io['mem_q_norm_w'].partition_broadcast(128), w=[qnw])
    kb.dma('sp', knw[:], io['mem_k_norm_w'].partition_broadcast(128), w=[knw])
    kb.op('dve', lambda E: E.tensor_scalar(out=qnw[:], in0=qnw[:], scalar1=128 ** -0.5, scalar2=None, op0=ALU.mult), r=[qnw], w=[qnw])

    mt = ph.sbn("mt", [128, D], F32, 2)
    junk = ph.sb("junk", [128, D], BF16)
    ss = ph.sb("ss", [128, 1], F32)
    rs = ph.sb("rs", [128, 1], F32)
    mn = ph.sb("mn", [128, D], BF16)
    mnT = ph.sb("mnT", [128, 8, 128], BF16)
    kvt = ph.sb("kvt", [128, 1024], F32)
    sq4 = ph.sb("sq4", [128, 4, 128], F32)
    ss4 = ph.sb("ss4", [128, 4], F32)
    rs4 = ph.sb("rs4", [128, 4], F32)
    kn = ph.sb("kn", [128, 4, 128], BF16)
    kT = ph.sb("kT", [128, 4, 256], BF16)
    v1 = ph.sb("v1", [128, 2, 4, 129], BF16)
    pT = ph.ps("pT", [128, 8, 128], BF16)
    pK = ph.psn("pK", [128, 512], F32, 2)
    kb.op('pool', lambda E: E.memset(v1[:], 1.0), w=[v1])
    for mt_i in range(2):
        m = mt[mt_i]
        kb.dma('sp', m[:], io['mem'][mt_i * 128:(mt_i + 1) * 128, :], w=[m])
        rms_rstd(kb, m[:], junk, ss, rs, D, [m])
        kb.op('dve', lambda E: E.tensor_scalar(out=mn[:], in0=m[:], scalar1=rs[:, 0:1], scalar2=None, op0=ALU.mult), r=[m, rs], w=[mn])
        for c in range(8):
            kb.op('pe', lambda E: E.transpose(out=pT[:, c, :], in_=mn[:, c * 128:(c + 1) * 128], identity=ident[:]), r=[mn, ident], w=[pT])
        kb.op('act', lambda E: E.copy(out=mnT[:], in_=pT[:]), r=[pT], w=[mnT])
        for half in range(2):
            pk = pK[half]
            for c in range(8):
                kb.op('pe', lambda E: E.matmul(pk[:], lhsT=mnT[:, c, :], rhs=wkv[:, c, half * 512:(half + 1) * 512],
                                               start=(c == 0), stop=(c == 7)), r=[mnT, (wkv, c)], w=[pk])
            kb.op('act', lambda E: E.copy(out=kvt[:, half * 512:(half + 1) * 512], in_=pk[:]), r=[pk], w=[(kvt, half)])
        k3 = kvt[:, 0:512].rearrange("p (h d) -> p h d", h=4)
        kb.op('dve', lambda E: E.tensor_tensor(out=sq4[:], in0=k3, in1=k3, op=ALU.mult), r=[(kvt, 0)], w=[sq4])
        kb.op('dve', lambda E: E.tensor_reduce(out=ss4[:], in_=sq4[:], axis=AX.X, op=ALU.add), r=[sq4], w=[ss4])
        kb.op('act', lambda E: E.activation(out=rs4[:], in_=ss4[:], func=AF.Sqrt, scale=1.0 / 128, bias=EPS), r=[ss4], w=[rs4])
        kb.op('dve', lambda E: E.reciprocal(out=rs4[:], in_=rs4[:]), r=[rs4], w=[rs4])
        kb.op('dve', lambda E: E.tensor_tensor(out=sq4[:], in0=k3, in1=rs4[:].unsqueeze(2).to_broadcast([128, 4, 128]), op=ALU.mult),
              r=[(kvt, 0), rs4], w=[sq4])
        kb.op('dve', lambda E: E.tensor_tensor(out=kn[:], in0=sq4[:], in1=knw[:].unsqueeze(1).to_broadcast([128, 4, 128]), op=ALU.mult),
              r=[sq4, knw], w=[kn])
        for h in range(4):
            kb.op('pe', lambda E: E.transpose(out=pT[:, h, :], in_=kn[:, h, :], identity=ident[:]), r=[kn, ident], w=[pT])
        kb.op('act', lambda E: E.copy(out=kT[:, :, mt_i * 128:(mt_i + 1) * 128], in_=pT[:, 0:4, :]), r=[pT], w=[kT])
        kb.op('dve', lambda E: E.tensor_copy(out=v1[:, mt_i, :, 0:128], in_=kvt[:, 512:1024].rearrange("p (h d) -> p h d", h=4)),
              r=[(kvt, 1)], w=[v1])

    qt_ = ph.sbn("qt", [128, 512], F32, 4)
    qsb = ph.sbn("qs", [128, 4, 128], F32, 4)
    qn = ph.sbn("qn", [128, 4, 128], BF16, 4)
    ss4b = ph.sbn("ss4q", [128, 4], F32, 4)
    rs4b = ph.sbn("rs4q", [128, 4], F32, 4)
    qT = ph.sbn("qT", [128, 4, 512], BF16, 2)
    PTb = ph.sbn("PT", [128, 512], BF16, 4)
    pti = [0]
    oc = ph.sbn("oc", [128, 4, 128], BF16, 4)
    rinv = ph.sb("rinv", [128, 1], F32)
    ocT = ph.sbn("ocT", [128, 4, 128], BF16, 2)
    sc_ps = ph.psn("sc", [128, 512], F32, 2)
    o_psA = ph.ps("oA", [128, 2, 256], F32)
    o_psB = ph.ps("oB", [128, 2, 256], F32)
    pO = ph.ps("pO", [128, 8, 128], BF16)
    for s in range(S // 512):
        qTs = qT[s % 2]
        def mqtile(t4, s=s, qTs=qTs):
            t = s * 4 + t4
            q = qt_[t4]
            qb = qn[t4]
            qs, ss4, rs4 = qsb[t4], ss4b[t4], rs4b[t4]
            kb.dma('sp', q[:], io['tm'][t * 128:(t + 1) * 128, MQ_OFF:MQ_OFF + 512], r=['tm'], w=[q])
            yield
            q3 = q[:].rearrange("p (h d) -> p h d", h=4)
            kb.op('pool', lambda E: E.tensor_tensor(out=qs[:], in0=q3, in1=q3, op=ALU.mult), r=[q], w=[qs])
            yield
            kb.op('dve', lambda E: E.tensor_reduce(out=ss4[:], in_=qs[:], axis=AX.X, op=ALU.add), r=[qs], w=[ss4])
            yield
            kb.op('act', lambda E: E.activation(out=rs4[:], in_=ss4[:], func=AF.Sqrt, scale=1.0 / 128, bias=EPS), r=[ss4], w=[rs4])
            yield
            kb.op('dve', lambda E: E.reciprocal(out=rs4[:], in_=rs4[:]), r=[rs4], w=[rs4])
            yield
            kb.op('dve', lambda E: E.tensor_tensor(out=qs[:], in0=q3, in1=rs4[:].unsqueeze(2).to_broadcast([128, 4, 128]), op=ALU.mult),
                  r=[q, rs4], w=[qs])
            yield
            kb.op('pool', lambda E: E.tensor_tensor(out=qb[:], in0=qs[:], in1=qnw[:].unsqueeze(1).to_broadcast([128, 4, 128]), op=ALU.mult),
                  r=[qs, qnw], w=[qb])
            yield
            for h in range(4):
                kb.op('pe', lambda E: E.transpose(out=pT[:, h, :], in_=qb[:, h, :], identity=ident[:]), r=[qb, ident], w=[pT])
            kb.op('act', lambda E: E.copy(out=qTs[:, :, t4 * 128:(t4 + 1) * 128], in_=pT[:, 0:4, :]), r=[pT], w=[qTs])
            yield
        _gens = [mqtile(i) for i in range(4)]
        while _gens:
            for g_ in list(_gens):
                try:
                    next(g_)
                except StopIteration:
                    _gens.remove(g_)
        for h in range(4):
            def o_ps(qt, h=h):
                return (o_psA[:, qt, 0:129], o_psA, 'A') if qt < 2 else (o_psB[:, qt - 2, 0:129], o_psB, 'B')
            specs = []
            for kt in range(2):
                specs.append(dict(lhsT=kT[:, h, kt * 128:(kt + 1) * 128], rhs_fn=lambda q0, q1, h=h: qTs[:, h, q0 * 128:q1 * 128],
                                  qt0=0, qt1=4, masks=[], v_fn=lambda qt, kt=kt, h=h: v1[:, kt, h, :], nk=128, rk=[kT], rv=[v1]))
            attn_block(kb, o_ps, PTb, pti, sc_ps, specs, ident, [qTs])
            for t4 in range(4):
                t = s * 4 + t4
                ob = oc[t4]
                oap, okey, _ = o_ps(t4)
                kb.op('dve', lambda E: E.reciprocal(out=rinv[:], in_=oap[:, 128:129]), r=[okey], w=[rinv])
                kb.op('dve', lambda E: E.tensor_scalar(out=ob[:, h, :], in0=oap[:, 0:128], scalar1=rinv[:, 0:1], scalar2=None, op0=ALU.mult),
                      r=[okey, rinv], w=[(ob, h)])
        for t4 in range(4):
            t = s * 4 + t4
            ob = oc[t4]
            oT = ocT[t % 2]
            for h in range(4):
                kb.op('pe', lambda E: E.transpose(out=pO[:, h, :], in_=ob[:, h, :], identity=ident[:]), r=[(ob, h), ident], w=[pO])
            kb.op('act', lambda E: E.copy(out=oT[:], in_=pO[:, 0:4, :]), r=[pO], w=[oT])
            kb.dma('sp', io['ocatT'][t, :, 8:12, :], oT[:], r=[oT], w=['ocatT_c'])
    ph.close()


def phase_gdn(kb, io):
    nc = kb.nc
    ph = Phase(kb, "gd")
    ident = ph.sb("ident", [128, 128], BF16)
    btri = ph.sb("btri", [128, 128], F32)
    bones = ph.sb("bones", [128, 128], F32)
    ones = ph.sb("ones", [128, 128], F32)
    mlow = ph.sb("mlow", [128, 4, 128], F32)
    mup = ph.sb("mup", [128, 4, 128], F32)
    strict = ph.sb("strict", [128, 128], F32)
    mch = ph.sb("mch", [128, 2], F32)
    kb.dma('sp', ident[:], io['c_ident'][:, :], w=[ident])
    kb.dma('sp', btri[:], io['c_btri'][:, :], w=[btri])
    kb.dma('sp', bones[:], io['c_bones'][:, :], w=[bones])
    kb.dma('sp', strict[:], io['c_strict'][:, :], w=[strict])
    kb.dma('sp', mch[:], io['c_mch'][:, :], w=[mch])
    for h in range(4):
        kb.dma('sp', mlow[:, h, :], io['c_mlow'][:, :], w=[mlow])
        kb.dma('sp', mup[:, h, :], io['c_mup'][:, :], w=[mup])
    kb.op('pool', lambda E: E.memset(ones[:], 1.0), w=[ones])
    dtb = ph.sb("dtb", [128, 4], F32)
    nA = ph.sb("nA", [128, 4], F32)
    gw = ph.sb("gw", [128, 128], F32)
    kb.dma('sp', dtb[:], io['gdn_dt_bias'].partition_broadcast(128), w=[dtb])
    kb.dma('sp', nA[:], io['gdn_a_log'].partition_broadcast(128), w=[nA])
    kb.dma('sp', gw[:], io['gdn_out_norm_w'].partition_broadcast(128), w=[gw])
    kb.op('act', lambda E: E.activation(out=nA[:], in_=nA[:], func=AF.Exp), r=[nA], w=[nA])
    kb.op('dve', lambda E: E.tensor_scalar(out=nA[:], in0=nA[:], scalar1=-1.0, scalar2=None, op0=ALU.mult), r=[nA], w=[nA])

    def B4(t_, n=4):
        return t_.unsqueeze(2).to_broadcast([128, n, 128])

    def M4(t_):
        return t_.unsqueeze(1).to_broadcast([128, 4, 128])

    qkv = ph.sbn("qkv", [128, 3, 4, 128], BF16, 2)
    ab = ph.sbn("ab", [128, 8], F32, 2)
    gt = ph.sbn("gt", [128, 512], F32, 2)
    sm = ph.sbn("sm", [128, 64], F32, 2)
    gs = ph.sbn("gs", [128, 16], F32, 2)
    gm = ph.sb("gm", [128, 8], F32)
    R1 = ph.sb("R1", [128, 4, 128], F32)
    R2 = ph.sb("R2", [128, 4, 128], F32)
    tmpA = ph.sb("tmpA", [128, 4, 128], F32)
    tmpB = ph.sb("tmpB", [128, 4, 128], F32)
    dec = ph.sb("dec", [128, 4, 128], F32)
    decT = ph.sb("decT", [128, 4, 128], F32)
    sq = ph.sb("sq", [128, 4, 128], F32)
    KBG = ph.sb("KBG", [128, 4, 128], BF16)
    Kdb = ph.sbn("Kd", [128, 4, 128], BF16, 4)
    VBb = ph.sbn("VB", [128, 4, 128], BF16, 2)
    dg = ph.sbn("dg", [128, 4, 128], BF16, 4)
    QT = ph.sb("QT", [128, 4, 128], BF16)
    QGb = ph.sbn("QG", [128, 4, 128], BF16, 4)
    KT = ph.sb("KT", [128, 4, 128], BF16)
    nbs = ph.sb("nbs", [128, 4, 128], F32)
    Xb = ph.sbn("X", [128, 4, 128], BF16, 2)
    Yb = ph.sbn("Y", [128, 4, 128], BF16, 2)
    Pbb = ph.sbn("P", [128, 4, 128], BF16, 4)
    aqkTb = ph.sbn("aqkT", [128, 4, 128], BF16, 2)
    negWTb = ph.sbn("negWT", [128, 4, 128], BF16, 2)
    vnew = ph.sb("vnew", [128, 4, 128], BF16)
    Sf = ph.sb("Sf", [128, 4, 128], F32)
    Sbf = ph.sbn("Sbf", [128, 4, 128], BF16, 3)
    osb = ph.sb("osb", [128, 4, 128], F32)
    sgt = ph.sb("sgt", [128, 512], F32)
    oa = ph.sbn("oa", [128, 4, 128], BF16, 2)
    oaT = ph.sbn("oaT", [128, 4, 128], BF16, 2)
    pS = ph.ps("pS", [128, 512], F32)
    pD = ph.ps("pD", [128, 4, 128], F32)
    pA = ph.psn("pA", [128, 4, 128], F32, 2)
    pB = ph.psn("pB", [128, 8, 128], BF16, 1)
    pV = ph.ps("pV", [128, 4, 128], F32)
    pdS = ph.ps("pdS", [128, 4, 128], F32)
    pO = ph.ps("pO", [128, 4, 128], F32)
    kb.op('pool', lambda E: E.memset(Sf[:], 0.0), w=[Sf])
    kb.op('pool', lambda E: E.memset(Sbf[0][:], 0.0), w=[Sbf[0]])
    kb.op('pool', lambda E: E.memset(vnew[:], 0.0), w=[vnew])
    pai = [0]

    def PA():
        pai[0] += 1
        return pA[pai[0] % 2]

    def mm4(p, lf, rf, rk):
        for h in range(4):
            kb.op('pe', lambda E: E.matmul(p[:, h, :], lhsT=lf(h), rhs=rf(h), start=True, stop=True), r=rk, w=[p])

    si = [0]

    def tile(t):
        b = t % 2
        x_ = qkv[b]
        VB, aqkT, negWT = VBb[b], aqkTb[b], negWTb[b]
        Kd = Kdb[2 * b:2 * b + 2]
        QG = QGb[2 * b:2 * b + 2]
        Pb = Pbb[2 * b:2 * b + 2]
        s_ = sm[b]
        kb.dma('sp', x_[:].rearrange("p a h d -> p (a h d)"), io['qkv_tm'][t * 128:(t + 1) * 128, :], r=['qkv_tm'], w=[x_])
        kb.dma('sp', ab[b][:], io['tm'][t * 128:(t + 1) * 128, 0:8], r=['tm'], w=[ab[b]])
        kb.dma('sp', gt[b][:], io['tm'][t * 128:(t + 1) * 128, GATE_OFF:GATE_OFF + 512], r=['tm'], w=[gt[b]])
        g = s_[:, 0:4]
        kb.op('dve', lambda E: E.tensor_tensor(out=g, in0=ab[b][:, 0:4], in1=dtb[:], op=ALU.add), r=[ab[b], dtb], w=[s_])
        kb.op('act', lambda E: E.activation(out=g, in_=g, func=AF.Exp), r=[s_], w=[s_])
        kb.op('act', lambda E: E.activation(out=g, in_=g, func=AF.Ln, bias=1.0), r=[s_], w=[s_])
        kb.op('dve', lambda E: E.tensor_tensor(out=g, in0=g, in1=nA[:], op=ALU.mult), r=[s_, nA], w=[s_])
        kb.op('dve', lambda E: E.tensor_scalar(out=s_[:, 4:8], in0=g, scalar1=-1.0, scalar2=None, op0=ALU.mult), r=[s_], w=[s_])
        kb.op('act', lambda E: E.activation(out=s_[:, 8:12], in_=ab[b][:, 4:8], func=AF.Exp, scale=-1.0), r=[ab[b], s_], w=[s_])
        kb.op('dve', lambda E: E.tensor_scalar(out=s_[:, 8:12], in0=s_[:, 8:12], scalar1=1.0, scalar2=None, op0=ALU.add), r=[s_], w=[s_])
        kb.op('dve', lambda E: E.reciprocal(out=s_[:, 8:12], in_=s_[:, 8:12]), r=[s_], w=[s_])
        kb.op('dve', lambda E: E.tensor_scalar(out=s_[:, 12:16], in0=s_[:, 8:12], scalar1=-1.0, scalar2=None, op0=ALU.mult), r=[s_], w=[s_])
        for j in range(2):
            kb.op('dve', lambda E: E.tensor_scalar(out=gm[:, 4 * j:4 * j + 4], in0=g, scalar1=mch[:, j:j + 1], scalar2=None, op0=ALU.mult),
                  r=[s_, mch], w=[gm])
        kb.op('pe', lambda E: E.matmul(pS[:, 0:4], lhsT=btri[:], rhs=g, start=True, stop=True), r=[btri, s_], w=[pS])
        kb.op('pe', lambda E: E.matmul(pS[:, 4:8], lhsT=bones[:], rhs=g, start=True, stop=True), r=[bones, s_], w=[pS])
        kb.op('pe', lambda E: E.matmul(pS[:, 8:16], lhsT=ones[:], rhs=gm[:], start=True, stop=True), r=[ones, gm], w=[pS])
        G = gs[b]
        kb.op('dve', lambda E: E.tensor_copy(out=G[:], in_=pS[:, 0:16]), r=[pS], w=[G])
        kb.op('act', lambda E: E.activation(out=s_[:, 16:20], in_=G[:, 0:4], func=AF.Exp), r=[G, s_], w=[s_])
        kb.op('dve', lambda E: E.tensor_tensor(out=G[:, 4:8], in0=G[:, 4:8], in1=G[:, 0:4], op=ALU.subtract), r=[G], w=[G])
        kb.op('act', lambda E: E.activation(out=s_[:, 20:24], in_=G[:, 4:8], func=AF.Exp), r=[G, s_], w=[s_])
        kb.op('act', lambda E: E.activation(out=s_[:, 56:64], in_=G[:, 8:16], func=AF.Exp), r=[G, s_], w=[s_])
        yield
        kb.op('pool', lambda E: E.tensor_copy(out=R1[:], in_=B4(g)), r=[s_], w=[R1])
        kb.op('pool', lambda E: E.tensor_tensor(out=R2[:], in0=M4(btri[:]), in1=B4(s_[:, 4:8]), op=ALU.mult), r=[s_, btri], w=[R2])
        kb.op('pe', lambda E: E.matmul(pD[:].rearrange("p a b -> p (a b)"), lhsT=btri[:], rhs=R1[:].rearrange("p a b -> p (a b)"),
                                       start=True, stop=False), r=[btri, R1], w=[pD])
        kb.op('pe', lambda E: E.matmul(pD[:].rearrange("p a b -> p (a b)"), lhsT=bones[:], rhs=R2[:].rearrange("p a b -> p (a b)"),
                                       start=False, stop=True), r=[bones, R2], w=[pD])
        kb.op('dve', lambda E: E.tensor_tensor(out=tmpA[:], in0=pD[:], in1=mlow[:], op=ALU.add), r=[pD, mlow], w=[tmpA])
        kb.op('act', lambda E: E.activation(out=dec[:], in_=tmpA[:], func=AF.Exp), r=[tmpA], w=[dec])
        kb.op('dve', lambda E: E.scalar_tensor_tensor(out=tmpB[:], in0=pD[:], scalar=-1.0, in1=mup[:], op0=ALU.mult, op1=ALU.add),
              r=[pD, mup], w=[tmpB])
        kb.op('act', lambda E: E.activation(out=decT[:], in_=tmpB[:], func=AF.Exp), r=[tmpB], w=[decT])
        yield
        for a_, c0 in ((0, 24), (1, 28)):
            kb.op('pool', lambda E: E.tensor_tensor(out=sq[:], in0=x_[:, a_, :, :], in1=x_[:, a_, :, :], op=ALU.mult), r=[x_], w=[sq])
            kb.op('dve', lambda E: E.tensor_reduce(out=s_[:, c0:c0 + 4], in_=sq[:], axis=AX.X, op=ALU.add), r=[sq, s_], w=[s_])
            kb.op('act', lambda E: E.activation(out=s_[:, c0:c0 + 4], in_=s_[:, c0:c0 + 4], func=AF.Sqrt, bias=EPS), r=[s_], w=[s_])
            kb.op('dve', lambda E: E.reciprocal(out=s_[:, c0:c0 + 4], in_=s_[:, c0:c0 + 4]), r=[s_], w=[s_])
        sc_ = lambda o, a, bb: kb.op('dve', lambda E: E.tensor_tensor(out=s_[:, o:o + 4], in0=a, in1=bb, op=ALU.mult), r=[s_, mch], w=[s_])
        kb.op('dve', lambda E: E.tensor_scalar(out=s_[:, 32:36], in0=s_[:, 24:28], scalar1=128 ** -0.5, scalar2=None, op0=ALU.mult), r=[s_], w=[s_])
        sc_(36, s_[:, 32:36], s_[:, 16:20])
        kb.op('dve', lambda E: E.tensor_scalar(out=s_[:, 40:44], in0=s_[:, 36:40], scalar1=mch[:, 1:2], scalar2=None, op0=ALU.mult), r=[s_, mch], w=[s_])
        kb.op('dve', lambda E: E.tensor_scalar(out=s_[:, 36:40], in0=s_[:, 36:40], scalar1=mch[:, 0:1], scalar2=None, op0=ALU.mult), r=[s_, mch], w=[s_])
        sc_(44, s_[:, 28:32], s_[:, 8:12])
        sc_(44, s_[:, 44:48], s_[:, 16:20])
        sc_(48, s_[:, 28:32], s_[:, 20:24])
        kb.op('dve', lambda E: E.tensor_scalar(out=s_[:, 52:56], in0=s_[:, 48:52], scalar1=mch[:, 1:2], scalar2=None, op0=ALU.mult), r=[s_, mch], w=[s_])
        kb.op('dve', lambda E: E.tensor_scalar(out=s_[:, 48:52], in0=s_[:, 48:52], scalar1=mch[:, 0:1], scalar2=None, op0=ALU.mult), r=[s_, mch], w=[s_])
        yield
        kx, vx, qx = x_[:, 1, :, :], x_[:, 2, :, :], x_[:, 0, :, :]
        kb.op('pool', lambda E: E.tensor_tensor(out=KBG[:], in0=kx, in1=B4(s_[:, 44:48]), op=ALU.mult), r=[x_, s_], w=[KBG])
        kb.op('pool', lambda E: E.tensor_tensor(out=Kd[0][:], in0=kx, in1=B4(s_[:, 48:52]), op=ALU.mult), r=[x_, s_], w=[Kd[0]])
        kb.op('pool', lambda E: E.tensor_tensor(out=Kd[1][:], in0=kx, in1=B4(s_[:, 52:56]), op=ALU.mult), r=[x_, s_], w=[Kd[1]])
        kb.op('pool', lambda E: E.tensor_tensor(out=VB[:], in0=vx, in1=B4(s_[:, 8:12]), op=ALU.mult), r=[x_, s_], w=[VB])
        yield
        for i_, c0 in enumerate((32, 36, 40, 28)):
            kb.op('dve', lambda E: E.tensor_tensor(out=dg[i_][:], in0=M4(ident[:]), in1=B4(s_[:, c0:c0 + 4]), op=ALU.mult), r=[ident, s_], w=[dg[i_]])
        for i_, (src, dst) in enumerate(((qx, QT), (qx, QG[0]), (qx, QG[1]), (kx, KT))):
            p = PA()
            mm4(p, lambda h: src[:, h, :], lambda h: dg[i_][:, h, :], [x_, dg[i_]])
            if i_ % 2 == 0:
                kb.op('act', lambda E: E.copy(out=dst[:], in_=p[:]), r=[p], w=[dst])
            else:
                kb.op('dve', lambda E: E.tensor_copy(out=dst[:], in_=p[:]), r=[p], w=[dst])
        yield
        p = PA()
        mm4(p, lambda h: KT[:, h, :], lambda h: KT[:, h, :], [KT])
        kb.op('dve', lambda E: E.tensor_tensor(out=tmpA[:], in0=p[:], in1=dec[:], op=ALU.mult), r=[p, dec], w=[tmpA])
        kb.op('pool', lambda E: E.tensor_tensor(out=nbs[:], in0=M4(strict[:]), in1=B4(s_[:, 12:16]), op=ALU.mult), r=[strict, s_], w=[nbs])
        X, Y, P = Xb[0], Yb[0], Pb[0]
        kb.op('pool', lambda E: E.tensor_tensor(out=X[:], in0=tmpA[:], in1=nbs[:], op=ALU.mult), r=[tmpA, nbs], w=[X])
        for h in range(4):
            kb.op('pe', lambda E: E.transpose(out=pB[0][:, h, :], in_=X[:, h, :], identity=ident[:]), r=[X, ident], w=[pB[0]])
        kb.op('act', lambda E: E.copy(out=Y[:], in_=pB[0][:, 0:4, :]), r=[pB[0]], w=[Y])
        kb.op('dve', lambda E: E.tensor_tensor(out=P[:], in0=pB[0][:, 0:4, :], in1=M4(ident[:]), op=ALU.add), r=[pB[0], ident], w=[P])
        yield
        p = PA()
        mm4(p, lambda h: KT[:, h, :], lambda h: QT[:, h, :], [KT, QT])
        kb.op('dve', lambda E: E.tensor_tensor(out=aqkT[:], in0=p[:], in1=decT[:], op=ALU.mult), r=[p, decT], w=[aqkT])
        yield
        for k_ in range(1, 6):
            Xn, Yn, Pn = Xb[k_ % 2], Yb[k_ % 2], Pb[k_ % 2]
            p = PA()
            mm4(p, lambda h: Y[:, h, :], lambda h: X[:, h, :], [X, Y])
            kb.op('act', lambda E: E.copy(out=Xn[:], in_=p[:]), r=[p], w=[Xn])
            if k_ < 5:
                p2 = PA()
                mm4(p2, lambda h: X[:, h, :], lambda h: Y[:, h, :], [X, Y])
                kb.op('dve', lambda E: E.tensor_copy(out=Yn[:], in_=p2[:]), r=[p2], w=[Yn])
            p3 = PA()
            mm4(p3, lambda h: Xn[:, h, :], lambda h: P[:, h, :], [Xn, P])
            kb.op('dve', lambda E: E.tensor_tensor(out=Pn[:], in0=p3[:], in1=P[:], op=ALU.add), r=[p3, P], w=[Pn])
            X, Y, P = Xn, Yn, Pn
            yield
        yield
        p = PA()
        mm4(p, lambda h: KBG[:, h, :], lambda h: P[:, h, :], [KBG, P])
        kb.op('act', lambda E: E.mul(out=negWT[:], in_=p[:], mul=-1.0), r=[p], w=[negWT])
        yield 'B'
        Sa = Sbf[si[0] % 3]
        Sb_ = Sbf[(si[0] + 1) % 3]
        Sc = Sbf[(si[0] + 2) % 3]
        si[0] += 2
        for j, (Scur, Snext) in enumerate(((Sa, Sb_), (Sb_, Sc))):
            for h in range(4):
                kb.op('pe', lambda E: E.matmul(pV[:, h, :], lhsT=P[:, h, :], rhs=VB[:, h, :], start=True, stop=False), r=[P, VB], w=[pV])
                kb.op('pe', lambda E: E.matmul(pV[:, h, :], lhsT=negWT[:, h, :], rhs=Scur[:, h, :], start=False, stop=True),
                      r=[negWT, Scur], w=[pV])
            yield
            r0 = 64 * j
            kb.op('act', lambda E: E.copy(out=vnew[r0:r0 + 64, :, :], in_=pV[r0:r0 + 64, :, :]), r=[pV], w=[vnew])
            yield
            mm4(pdS, lambda h: Kd[j][:, h, :], lambda h: vnew[:, h, :], [Kd[j], vnew])
            yield
            for h in range(4):
                kb.op('dve', lambda E: E.scalar_tensor_tensor(out=Sf[:, h, :], in0=Sf[:, h, :], scalar=s_[:, 56 + 4 * j + h:57 + 4 * j + h],
                                                              in1=pdS[:, h, :], op0=ALU.mult, op1=ALU.add), r=[Sf, s_, pdS], w=[Sf])
            kb.op('act', lambda E: E.copy(out=Snext[:], in_=Sf[:]), r=[Sf], w=[Snext])
            yield
        for h in range(4):
            kb.op('pe', lambda E: E.matmul(pO[:, h, :], lhsT=QG[0][:, h, :], rhs=Sa[:, h, :], start=True, stop=False), r=[QG[0], Sa], w=[pO])
            kb.op('pe', lambda E: E.matmul(pO[:, h, :], lhsT=QG[1][:, h, :], rhs=Sb_[:, h, :], start=False, stop=False), r=[QG[1], Sb_], w=[pO])
            kb.op('pe', lambda E: E.matmul(pO[:, h, :], lhsT=aqkT[:, h, :], rhs=vnew[:, h, :], start=False, stop=True), r=[aqkT, vnew], w=[pO])
        yield
        kb.op('act', lambda E: E.copy(out=osb[:], in_=pO[:]), r=[pO], w=[osb])
        kb.op('pool', lambda E: E.tensor_tensor(out=sq[:], in0=osb[:], in1=osb[:], op=ALU.mult), r=[osb], w=[sq])
        kb.op('dve', lambda E: E.tensor_reduce(out=G[:, 0:4], in_=sq[:], axis=AX.X, op=ALU.add), r=[sq, G], w=[G])
        kb.op('act', lambda E: E.activation(out=G[:, 0:4], in_=G[:, 0:4], func=AF.Sqrt, scale=1.0 / 128, bias=EPS), r=[G], w=[G])
        kb.op('dve', lambda E: E.reciprocal(out=G[:, 0:4], in_=G[:, 0:4]), r=[G], w=[G])
        yield
        kb.op('act', lambda E: E.activation(out=sgt[:], in_=gt[b][:], func=AF.Silu), r=[gt[b]], w=[sgt])
        kb.op('dve', lambda E: E.tensor_tensor(out=osb[:], in0=osb[:], in1=B4(G[:, 0:4]), op=ALU.mult), r=[osb, G], w=[osb])
        kb.op('pool', lambda E: E.tensor_tensor(out=osb[:], in0=osb[:], in1=M4(gw[:]), op=ALU.mult), r=[osb, gw], w=[osb])
        kb.op('dve', lambda E: E.tensor_tensor(out=oa[b][:], in0=osb[:], in1=sgt[:].rearrange("p (h d) -> p h d", h=4), op=ALU.mult),
              r=[osb, sgt], w=[oa[b]])
        for h in range(4):
            kb.op('pe', lambda E: E.transpose(out=pB[0][:, h, :], in_=oa[b][:, h, :], identity=ident[:]), r=[oa[b], ident], w=[pB[0]])
        kb.op('act', lambda E: E.copy(out=oaT[b][:], in_=pB[0][:, 0:4, :]), r=[pB[0]], w=[oaT[b]])
        kb.dma('sp', io['ocatT'][t, :, 0:4, :], oaT[b][:], r=[oaT[b]], w=['ocatT_a'])

    def to_boundary(g):
        for r in g:
            if r == 'B':
                return

    cur = tile(0)
    to_boundary(cur)
    for t in range(NT):
        nxt = tile(t + 1) if t + 1 < NT else None
        cur_done, nxt_done = False, nxt is None
        while not (cur_done and nxt_done):
            if not cur_done:
                try:
                    next(cur)
                except StopIteration:
                    cur_done = True
            if not nxt_done:
                if next(nxt) == 'B':
                    nxt_done = True
        cur = nxt
    ph.close()


def rope16(kb, R, G, cs, tmp, e1='dve', e2='pool'):
    c = cs[:, 0:8].unsqueeze(1).to_broadcast([128, G, 8])
    sn = cs[:, 8:16].unsqueeze(1).to_broadcast([128, G, 8])
    x1 = R[:, 0:G, 0:8]
    x2 = R[:, 0:G, 8:16]
    kb.op(e1, lambda E: E.tensor_tensor(out=tmp[:, 0:G, 0:8], in0=x1, in1=c, op=ALU.mult), r=[R, cs], w=[(tmp, 0)])
    yield
    kb.op(e2, lambda E: E.tensor_tensor(out=tmp[:, 0:G, 8:16], in0=x2, in1=sn, op=ALU.mult), r=[R, cs], w=[(tmp, 1)])
    yield
    kb.op(e1, lambda E: E.tensor_tensor(out=tmp[:, 0:G, 16:24], in0=x2, in1=c, op=ALU.mult), r=[R, cs], w=[(tmp, 2)])
    yield
    kb.op(e2, lambda E: E.tensor_tensor(out=tmp[:, 0:G, 24:32], in0=x1, in1=sn, op=ALU.mult), r=[R, cs], w=[(tmp, 3)])
    yield
    kb.op(e1, lambda E: E.tensor_tensor(out=x1, in0=tmp[:, 0:G, 0:8], in1=tmp[:, 0:G, 8:16], op=ALU.subtract),
          r=[(tmp, 0), (tmp, 1), (tmp, 2), (tmp, 3)], w=[R])
    yield
    kb.op(e1, lambda E: E.tensor_tensor(out=x2, in0=tmp[:, 0:G, 16:24], in1=tmp[:, 0:G, 24:32], op=ALU.add),
          r=[(tmp, 0), (tmp, 1), (tmp, 2), (tmp, 3)], w=[R])
    yield


def rms_groups(kb, src3, G, dst3, sq, ss, wt, e_sq='pool'):
    (src_ap, src_keys) = src3
    (dst_ap, dst_keys) = dst3
    kb.op(e_sq, lambda E: E.tensor_tensor(out=sq[:, 0:G, :], in0=src_ap, in1=src_ap, op=ALU.mult), r=src_keys, w=[sq])
    yield
    kb.op('dve', lambda E: E.tensor_reduce(out=ss[:, 0:G], in_=sq[:, 0:G, :], axis=AX.X, op=ALU.add), r=[sq], w=[ss])
    yield
    kb.op('act', lambda E: E.activation(out=ss[:, 0:G], in_=ss[:, 0:G], func=AF.Sqrt, scale=1.0 / 64, bias=EPS), r=[ss], w=[ss])
    yield
    kb.op('dve', lambda E: E.reciprocal(out=ss[:, 0:G], in_=ss[:, 0:G]), r=[ss], w=[ss])
    yield
    kb.op('dve', lambda E: E.tensor_tensor(out=dst_ap, in0=src_ap, in1=ss[:, 0:G].unsqueeze(2).to_broadcast([128, G, 64]), op=ALU.mult),
          r=src_keys + [ss], w=dst_keys)
    yield
    kb.op('pool', lambda E: E.tensor_tensor(out=dst_ap, in0=dst_ap, in1=wt[:].unsqueeze(1).to_broadcast([128, G, 64]), op=ALU.mult),
          r=dst_keys + [wt], w=dst_keys)
    yield


def phase_nsa(kb, io):
    nc = kb.nc
    ph = Phase(kb, "ns")
    ident = ph.sb("ident", [128, 128], BF16)
    tril = ph.sb("tril", [128, 128], BF16)
    far = ph.sb("far", [128, 128], BF16)
    cmask = ph.sb("cmask", [128, 9, 512], BF16)
    kvT = ph.sb("kvT", [128, 4, S], BF16)
    vs1 = ph.sb("vs1", [128, NT, 2, 65], BF16)
    vw1 = ph.sb("vw1", [128, NT, 2, 65], BF16)
    kcmpT = ph.sb("kcmpT", [64, 2, 256], BF16)
    rhs_cmp = ph.sb("rhs_cmp", [128, 2, 2, 128], BF16)
    kb.dma('sp', ident[:], io['c_ident'][:, :], w=[ident])
    kb.dma('sp', tril[:], io['c_tril'][:, :], w=[tril])
    kb.dma('sp', far[:], io['c_far'][:, :], w=[far])
    for m in range(9):
        kb.dma('sp', cmask[:, m, :], io['c_cmask'][m, :, :], w=[cmask])
    for g in range(2):
        kb.dma('sp', kvT[64:128, g, :], io['c_E'][:, :], w=[(kvT, 'E')])
    kb.op('pool', lambda E: E.memset(vs1[:], 1.0), w=[vs1])
    kb.op('pool', lambda E: E.memset(vw1[:], 1.0), w=[vw1])
    kb.op('pool', lambda E: E.memset(kcmpT[:], 0.0), w=[kcmpT])
    kb.op('pool', lambda E: E.memset(rhs_cmp[:], 0.0), w=[rhs_cmp])
    for bt in range(2):
        for g in range(2):
            kb.dma('sp', rhs_cmp[:, bt, g, 64:128], io['c_ovl'][:, bt, :], r=[rhs_cmp], w=[rhs_cmp])

    pp = Phase(kb, "np")
    kcT = pp.sb("kcT", [64, 4, S], BF16)
    ksw = pp.sb("ksw", [128, 64], F32)
    kww = pp.sb("kww", [128, 64], F32)
    kcw = pp.sb("kcw", [128, 64], F32)
    kb.dma('sp', ksw[:], io['nsa_ks_norm_w'].partition_broadcast(128), w=[ksw])
    kb.dma('sp', kww[:], io['nsa_kw_norm_w'].partition_broadcast(128), w=[kww])
    kb.dma('sp', kcw[:], io['nsa_kc_norm_w'].partition_broadcast(128), w=[kcw])
    kvb = pp.sbn("kvb", [128, 768], F32, 4)
    cst = pp.sbn("cst", [128, 16], F32, 4)
    R = pp.sbn("R", [128, 6, 64], F32, 4)
    sqb = pp.sbn("sq", [128, 2, 64], F32, 4)
    ssb = pp.sbn("ss", [128, 2], F32, 4)
    tmpb = pp.sbn("tmp", [128, 6, 32], F32, 4)
    sq, ss = sqb[0], ssb[0]
    k16 = pp.sbn("k16", [128, 8, 64], BF16, 4)
    pT8 = pp.psn("pT8", [128, 8, 128], BF16, 2)
    def kvtile(t):
        b = t % 4
        kv = kvb[b]
        Rb = R[b]
        sq, ss, tmp = sqb[b], ssb[b], tmpb[b]
        kb.dma('sp', kv[:], io['tm'][t * 128:(t + 1) * 128, KC_OFF:KC_OFF + 768], r=['tm'], w=[kv])
        kb.dma('sp', cst[b][:], io['c_rope'][t * 128:(t + 1) * 128, :], w=[cst[b]])
        v3 = lambda off: kv[:, off:off + 128].rearrange("p (g d) -> p g d", g=2)
        kb.op('pool', lambda E: E.tensor_copy(out=Rb[:, 0:2, :], in_=v3(0)), r=[kv], w=[Rb])
        yield
        yield from rms_groups(kb, (v3(256), [kv]), 2, (Rb[:, 2:4, :], [Rb]), sq, ss, ksw)
        yield from rms_groups(kb, (v3(512), [kv]), 2, (Rb[:, 4:6, :], [Rb]), sq, ss, kww)
        yield from rope16(kb, Rb, 6, cst[b], tmp)
        kk = k16[b]
        kb.op('act', lambda E: E.copy(out=kk[:, 0:6, :], in_=Rb[:]), r=[Rb], w=[kk])
        kb.op('pool', lambda E: E.tensor_copy(out=kk[:, 6:8, :], in_=v3(128)), r=[kv], w=[kk])
        kb.op('dve', lambda E: E.tensor_copy(out=vs1[:, t, :, 0:64], in_=v3(384)), r=[kv], w=[vs1])
        kb.op('pool', lambda E: E.tensor_copy(out=vw1[:, t, :, 0:64], in_=v3(640)), r=[kv], w=[vw1])
        yield
        p8 = pT8[t % 2]
        for i in range(8):
            kb.op('pe', lambda E: E.transpose(out=p8[0:64, i, :], in_=kk[:, i, :], identity=ident[:]), r=[kk, ident], w=[p8])
        kb.op('act', lambda E: E.copy(out=kvT[0:64, :, t * 128:(t + 1) * 128], in_=p8[0:64, 2:6, :]), r=[p8], w=[(kvT, 'k')])
        kb.op('dve', lambda E: E.tensor_copy(out=kcT[0:64, 0:2, t * 128:(t + 1) * 128], in_=p8[0:64, 0:2, :]), r=[p8], w=[kcT])
        kb.op('dve', lambda E: E.tensor_copy(out=kcT[0:64, 2:4, t * 128:(t + 1) * 128], in_=p8[0:64, 6:8, :]), r=[p8], w=[kcT])

    def interleave(gens):
        gens = list(gens)
        while gens:
            for g_ in list(gens):
                try:
                    next(g_)
                except StopIteration:
                    gens.remove(g_)

    for t in range(0, NT, 4):
        interleave([kvtile(t + i) for i in range(4)])
    w1f = pp.sb("w1f", [64, 32, 64], F32)
    w1b = pp.sbn("w1b", [64, 32, 64], BF16, 2)
    w2f = pp.sb("w2f", [64, 64], F32)
    w2b = pp.sbn("w2b", [64, 64], BF16, 2)
    posf = pp.sb("posf", [64, 32], F32)
    pos2 = pp.sbn("pos2", [64, 32, 2], BF16, 2)
    bias = pp.sb("bias", [64, 2], F32)
    h1T = pp.sb("h1T", [64, 256], BF16)
    o2 = pp.sb("o2", [128, 1, 64], F32)
    o2n = pp.sb("o2n", [128, 1, 64], F32)
    kcn = pp.sb("kcn", [128, 64], BF16)
    pH = pp.ps("pH", [128, 512], F32)
    pB_ = pp.ps("pBi", [128, 512], F32)
    pO2 = pp.ps("pO2", [128, 512], F32)
    kb.op('pool', lambda E: E.memset(h1T[:], 0.0), w=[h1T])
    for kind, (n1, n2, npos) in enumerate((('nsa_cmp_k_w1', 'nsa_cmp_k_w2', 'nsa_cmp_pos_k'), ('nsa_cmp_v_w1', 'nsa_cmp_v_w2', 'nsa_cmp_pos_v'))):
        kb.dma('sp', w1f[:], io[n1].rearrange("(l d) o -> d l o", d=64), w=[w1f])
        kb.op('dve', lambda E: E.tensor_copy(out=w1b[kind][:], in_=w1f[:]), r=[w1f], w=[w1b[kind]])
        kb.dma('sp', w2f[:], io[n2][:, :], w=[w2f])
        kb.op('dve', lambda E: E.tensor_copy(out=w2b[kind][:], in_=w2f[:]), r=[w2f], w=[w2b[kind]])
        kb.dma('sp', posf[:], io[npos].rearrange("l d -> d l"), w=[posf], allow_slow_non_contiguous=True)
        for j in range(2):
            kb.op('dve', lambda E: E.tensor_copy(out=pos2[kind][:, :, j], in_=posf[:]), r=[posf], w=[pos2[kind]])
        for l in range(32):
            kb.op('pe', lambda E: E.matmul(pB_[0:64, 0:2], lhsT=w1b[kind][:, l, :], rhs=pos2[kind][:, l, :], start=(l == 0), stop=(l == 31)),
                  r=[w1b[kind], pos2[kind]], w=[pB_])
        kb.op('dve', lambda E: E.tensor_copy(out=bias[:], in_=pB_[0:64, 0:2]), r=[pB_], w=[bias])
        for g in range(2):
            ki = kind * 2 + g
            for l in range(32):
                kb.op('pe', lambda E: E.matmul(pH[0:64, 0:255], lhsT=w1b[kind][:, l, :], rhs=kcT[0:64, ki, l:l + 16 * 254 + 1:16],
                                               start=(l == 0), stop=(l == 31)), r=[w1b[kind], kcT], w=[pH])
            kb.op('act', lambda E: E.activation(out=h1T[:, 0:255], in_=pH[0:64, 0:255], func=AF.Silu, bias=bias[:, 0:1]),
                  r=[pH, bias], w=[h1T])
            for bt in range(2):
                kb.op('pe', lambda E: E.matmul(pO2[:, 0:64], lhsT=h1T[:, bt * 128:(bt + 1) * 128], rhs=w2b[kind][:], start=True, stop=True),
                      r=[h1T, w2b[kind]], w=[pO2])
                if kind == 0:
                    kb.op('act', lambda E: E.copy(out=o2[:, 0, :], in_=pO2[:, 0:64]), r=[pO2], w=[o2])
                    for _ in rms_groups(kb, (o2[:], [o2]), 1, (o2n[:], [o2n]), sq, ss, kcw):
                        pass
                    kb.op('act', lambda E: E.copy(out=kcn[:], in_=o2n[:, 0, :]), r=[o2n], w=[kcn])
                    p8 = pT8[0]
                    kb.op('pe', lambda E: E.transpose(out=p8[0:64, 0, :], in_=kcn[:], identity=ident[:]), r=[kcn, ident], w=[p8])
                    kb.op('act', lambda E: E.copy(out=kcmpT[:, g, bt * 128:(bt + 1) * 128], in_=p8[0:64, 0, :]), r=[p8], w=[kcmpT])
                else:
                    kb.op('act', lambda E: E.copy(out=rhs_cmp[:, bt, g, 0:64], in_=pO2[:, 0:64]), r=[pO2], w=[rhs_cmp])
    pp.close()

    pa = Phase(kb, "na")
    NPT = 8
    PTb = pa.sbn("PT", [128, 512], BF16, NPT)
    pti = [0]
    qaug = pa.sbn("qaug", [128, 8, 512], BF16, 2)
    qnw = pa.sb("qnw", [128, 64], F32)
    kb.dma('sp', qnw[:], io['nsa_q_norm_w'].partition_broadcast(128), w=[qnw])
    kb.op('dve', lambda E: E.tensor_scalar(out=qnw[:], in0=qnw[:], scalar1=0.125, scalar2=None, op0=ALU.mult), r=[qnw], w=[qnw])
    qf = pa.sbn("qf", [128, 512], F32, 4)
    cst = pa.sbn("cst", [128, 16], F32, 4)
    Rqb = pa.sbn("Rq", [128, 8, 64], F32, 4)
    sqb = pa.sbn("sq", [128, 8, 64], F32, 4)
    ssb = pa.sbn("ss", [128, 8], F32, 4)
    tmpb = pa.sbn("tmp", [128, 8, 32], F32, 4)
    qa = pa.sbn("qa", [128, 8, 128], BF16, 4)
    gts = pa.sb("gts", [128, 4, 24], F32)
    Ab = pa.sbn("Ab", [128, 64], F32, 4)
    Bb = pa.sbn("Bb", [128, 64], F32, 4)
    ob = pa.sb("ob", [128, 4, 8, 64], F32)
    impacc = pa.sb("impacc", [128, 4, 64], F32)
    tmpi = pa.sb("tmpi", [128, 4, 64], F32)
    tmpo = pa.sb("tmpo", [128, 4, 64], F32)
    scr = pa.sb("scr", [128, 64], F32)
    scr2 = pa.sb("scr2", [128, 64], F32)
    m8 = pa.sb("m8", [128, 16], F32)
    nst = pa.sbn("nst", [128, 128], BF16, 2)
    fs = pa.sb("fs", [128, 12], F32)
    obb = pa.sbn("obb", [128, 512], BF16, 2)
    obT = pa.sbn("obT", [128, 4, 128], BF16, 2)
    sc_ps = pa.psn("sc", [128, 512], F32, 3)
    oC = pa.ps("oC", [128, 4, 128], F32)
    oW = pa.psn("oW", [128, 4, 128], F32, 1)
    oS = pa.psn("oS", [128, 4, 128], F32, 2)
    pTr = pa.psn("pTr", [128, 8, 128], BF16, 1)
    for i in range(4):
        kb.op('pool', lambda E: E.memset(qa[i][:], 0.0), w=[qa[i]])
    for i in range(2):
        kb.op('pool', lambda E: E.memset(nst[i][:], 0.0), w=[nst[i]])
    tri = 0

    def finalize(oX, h, br, first, cmp=False):
        if cmp:
            kb.op('dve', lambda E: E.tensor_reduce(out=fs[:, 0:4], in_=oX[:, :, 64:128], axis=AX.X, op=ALU.add), r=[oX], w=[fs])
            kb.op('dve', lambda E: E.tensor_scalar(out=fs[:, 0:4], in0=fs[:, 0:4], scalar1=0.5, scalar2=1e-30, op0=ALU.mult, op1=ALU.add),
                  r=[fs], w=[fs])
        else:
            kb.op('dve', lambda E: E.tensor_scalar(out=fs[:, 0:4], in0=oX[:, :, 64], scalar1=1e-30, scalar2=None, op0=ALU.add), r=[oX], w=[fs])
        kb.op('dve', lambda E: E.reciprocal(out=fs[:, 4:8], in_=fs[:, 0:4]), r=[fs], w=[fs])
        if cmp:
            dsti = impacc if h % 4 == 0 else tmpi
            kb.op('dve', lambda E: E.tensor_tensor(out=dsti[:], in0=oX[:, :, 64:128], in1=fs[:, 4:8].unsqueeze(2).to_broadcast([128, 4, 64]),
                                                   op=ALU.mult), r=[oX, fs], w=[dsti])
            if h % 4 != 0:
                kb.op('pool', lambda E: E.tensor_tensor(out=impacc[:], in0=impacc[:], in1=tmpi[:], op=ALU.add), r=[impacc, tmpi], w=[impacc])
        kb.op('dve', lambda E: E.tensor_tensor(out=fs[:, 8:12], in0=fs[:, 4:8], in1=gts[:, :, h * 3 + br], op=ALU.mult), r=[fs, gts], w=[fs])
        dst = ob[:, :, h, :] if first else tmpo[:]
        kb.op('dve', lambda E: E.tensor_tensor(out=dst, in0=oX[:, :, 0:64], in1=fs[:, 8:12].unsqueeze(2).to_broadcast([128, 4, 64]),
                                               op=ALU.mult), r=[oX, fs], w=[(ob, h) if first else tmpo])
        if not first:
            kb.op('pool', lambda E: E.tensor_tensor(out=ob[:, :, h, :], in0=ob[:, :, h, :], in1=tmpo[:], op=ALU.add),
                  r=[(ob, h), tmpo], w=[(ob, h)])

    for s in range(S // 512):
        qs_ = qaug[s % 2]
        def qtile(t4, s=s, qs_=qs_):
            t = 4 * s + t4
            b = t4
            q = qf[b]
            Rq, sq, ss, tmp = Rqb[b], sqb[b], ssb[b], tmpb[b]
            kb.dma('sp', q[:], io['tm'][t * 128:(t + 1) * 128, NQ_OFF:NQ_OFF + 512], r=['tm'], w=[q])
            kb.dma('sp', gts[:, t4, :], io['tm'][t * 128:(t + 1) * 128, NG_OFF:NG_OFF + 24], r=['tm'], w=[gts])
            kb.dma('sp', cst[b][:], io['c_rope'][t * 128:(t + 1) * 128, :], w=[cst[b]])
            kb.dma('sp', Ab[t4][:], io['c_A'][t, :, :], w=[Ab[t4]])
            kb.dma('sp', Bb[t4][:], io['c_B'][t, :, :], w=[Bb[t4]])
            kb.op('act', lambda E: E.activation(out=gts[:, t4, :], in_=gts[:, t4, :], func=AF.Exp, scale=-1.0), r=[gts], w=[gts])
            kb.op('dve', lambda E: E.tensor_scalar(out=gts[:, t4, :], in0=gts[:, t4, :], scalar1=1.0, scalar2=None, op0=ALU.add), r=[gts], w=[gts])
            kb.op('dve', lambda E: E.reciprocal(out=gts[:, t4, :], in_=gts[:, t4, :]), r=[gts], w=[gts])
            q3 = q[:].rearrange("p (h d) -> p h d", h=8)
            yield
            yield from rms_groups(kb, (q3, [q]), 8, (Rq[:], [Rq]), sq, ss, qnw)
            yield from rope16(kb, Rq, 8, cst[b], tmp)
            qab = qa[b]
            kb.op('act', lambda E: E.copy(out=qab[:, :, 0:64], in_=Rq[:]), r=[Rq], w=[qab])
            yield
            pt_ = pTr[0]
            for h in range(8):
                kb.op('pe', lambda E: E.transpose(out=pt_[:, h, :], in_=qab[:, h, :], identity=ident[:]), r=[qab, ident], w=[pt_])
            kb.op('act', lambda E: E.copy(out=qs_[:, :, t4 * 128:(t4 + 1) * 128], in_=pt_[:]), r=[pt_], w=[qs_])
        interleave([qtile(i) for i in range(4)])
        nbt = 1 if s < 4 else 2
        for g in range(2):
            for h in range(4 * g, 4 * g + 4):
                specs = []
                for bt in range(nbt):
                    m = (s if s <= 4 else None) if bt == 0 else 5 + (s - 4)
                    masks = [] if m is None else [(0, 512, cmask[:, m, :])]
                    specs.append(dict(lhsT=kcmpT[0:64, g, bt * 128:(bt + 1) * 128],
                                      rhs_fn=lambda q0, q1, h=h: qs_[0:64, h, q0 * 128:q1 * 128], qt0=0, qt1=4, masks=masks, mk=[cmask],
                                      v_fn=lambda qt, bt=bt, g=g: rhs_cmp[:, bt, g, :], nk=128, rk=[kcmpT], rv=[rhs_cmp]))
                attn_block(kb, lambda qt: (oC[:, qt, :], oC, 'C'), PTb, pti, sc_ps, specs, ident, [qs_])
                finalize(oC, h, 0, True, cmp=True)
            for qt in range(4):
                ns_ = nst[qt % 2]
                kb.op('dve', lambda E: E.tensor_tensor(out=scr[:], in0=impacc[:, qt, :], in1=Ab[qt][:], op=ALU.mult), r=[impacc, Ab[qt]], w=[scr])
                kb.op('dve', lambda E: E.tensor_tensor(out=scr[:], in0=scr[:], in1=Bb[qt][:], op=ALU.add), r=[scr, Bb[qt]], w=[scr])
                kb.op('dve', lambda E: E.max(out=m8[:, 0:8], in_=scr[:]), r=[scr], w=[(m8, 0)])
                kb.op('dve', lambda E: E.match_replace(out=scr2[:], in_to_replace=m8[:, 0:8], in_values=scr[:], imm_value=-1e30),
                      r=[scr, (m8, 0)], w=[scr2])
                kb.op('dve', lambda E: E.max(out=m8[:, 8:16], in_=scr2[:]), r=[scr2], w=[(m8, 1)])
                kb.op('dve', lambda E: E.tensor_scalar(out=ns_[:, 64:128], in0=scr[:], scalar1=m8[:, 15:16], scalar2=1.0, op0=ALU.is_ge,
                                                       op1=ALU.subtract), r=[scr, (m8, 1)], w=[ns_])
                pt_ = pTr[0]
                tri += 1
                kb.op('pe', lambda E: E.transpose(out=pt_[:, 0, :], in_=ns_[:], identity=ident[:]), r=[ns_, ident], w=[pt_])
                for h in range(4 * g, 4 * g + 4):
                    if h % 2 == 0:
                        kb.op('act', lambda E: E.copy(out=qs_[64:128, h, qt * 128:(qt + 1) * 128], in_=pt_[64:128, 0, :]), r=[pt_], w=[qs_])
                    else:
                        kb.op('dve', lambda E: E.tensor_copy(out=qs_[64:128, h, qt * 128:(qt + 1) * 128], in_=pt_[64:128, 0, :]), r=[pt_], w=[qs_])
        for h in range(8):
            g = h // 4
            specs = []
            for kt in range(max(0, 4 * s - 4), 4 * s + 4):
                lo = max(kt - 4 * s, 0)
                hi = min(kt + 4 - 4 * s, 3)
                masks = []
                if kt >= 4 * s:
                    masks.append(((kt - 4 * s - lo) * 128, 128, tril[:]))
                if kt + 4 <= 4 * s + 3:
                    masks.append(((kt + 4 - 4 * s - lo) * 128, 128, far[:]))
                specs.append(dict(lhsT=kvT[0:64, 2 + g, kt * 128:(kt + 1) * 128],
                                  rhs_fn=lambda q0, q1, h=h: qs_[0:64, h, q0 * 128:q1 * 128], qt0=lo, qt1=hi + 1, masks=masks, mk=[tril, far],
                                  v_fn=lambda qt, kt=kt, g=g: vw1[:, kt, g, :], nk=128, rk=[(kvT, 'k')], rv=[vw1]))
            oW_ = oW[0]
            attn_block(kb, lambda qt: (oW_[:, qt, 0:65], oW_, 'W'), PTb, pti, sc_ps, specs, ident, [qs_], LA=3)
            finalize(oW_, h, 2, False)
            specs = []
            for kt in range(0, 4 * s + 4):
                lo = max(kt - 4 * s, 0)
                masks = [(0, 128, tril[:])] if kt >= 4 * s else []
                specs.append(dict(lhsT=kvT[:, g, kt * 128:(kt + 1) * 128],
                                  rhs_fn=lambda q0, q1, h=h: qs_[:, h, q0 * 128:q1 * 128], qt0=lo, qt1=4, masks=masks, mk=[tril],
                                  v_fn=lambda qt, kt=kt, g=g: vs1[:, kt, g, :], nk=128, rk=[(kvT, 'k'), (kvT, 'E')], rv=[vs1]))
            oS_ = oS[h % 2]
            attn_block(kb, lambda qt: (oS_[:, qt, 0:65], oS_, 'S'), PTb, pti, sc_ps, specs, ident, [qs_], LA=3)
            finalize(oS_, h, 1, False)
        for qt in range(4):
            t = 4 * s + qt
            b = t % 2
            kb.op('act', lambda E: E.copy(out=obb[b][:], in_=ob[:, qt, :, :].rearrange("p h d -> p (h d)")), r=[(ob, h) for h in range(8)], w=[obb[b]])
            pt_ = pTr[0]
            tri += 1
            for c in range(4):
                kb.op('pe', lambda E: E.transpose(out=pt_[:, c, :], in_=obb[b][:, c * 128:(c + 1) * 128], identity=ident[:]), r=[obb[b], ident], w=[pt_])
            kb.op('act', lambda E: E.copy(out=obT[b][:], in_=pt_[:, 0:4, :]), r=[pt_], w=[obT[b]])
            kb.dma('sp', io['ocatT'][t, :, 4:8, :], obT[b][:], r=[obT[b]], w=['ocatT_b'])
    pa.close()
    ph.close()


def phase_ffn(kb, io):
    nc = kb.nc
    ph = Phase(kb, "f1")
    ident = ph.sb("ident", [128, 128], BF16)
    kb.dma('sp', ident[:], io['c_ident'][:, :], w=[ident])
    woutb = ph.sb("woutb", [128, 12, D], BF16)
    wupb = ph.sb("wupb", [128, 8, 2 * FF], BF16)
    gam = ph.sb("gam", [128, 8], F32)
    cw = ph.sb("cw", [128, 3, 44], F32)
    kb.dma('sp', gam[:], io['ffn_norm_w'].rearrange("(c p) -> p c", p=128), w=[gam], allow_slow_non_contiguous=True)
    for j in range(3):
        kb.dma('sp', cw[:, j, :], io['ffn_conv_w'][j, :].rearrange("(c p) -> p c", p=128), w=[cw], allow_slow_non_contiguous=True)
    load_cast_weight(kb, ph, woutb, io['w_out'], 12, D)
    load_cast_weight(kb, ph, wupb, io['ffn_w_up'], 8, 2 * FF, gam=gam, stage_cols=1408)

    oT = ph.sbn("oT", [128, 12, 128], BF16, 2)
    xt = ph.sbn("xt", [128, D], F32, 2)
    hs = ph.sbn("hs", [128, D], F32, 2)
    junk = ph.sb("junk", [128, D], BF16)
    ss = ph.sbn("ss", [128, 1], F32, 2)
    rs = ph.sbn("rs", [128, 1], F32, 2)
    hn = ph.sbn("hn", [128, D], BF16, 2)
    hnT = ph.sbn("hnT", [128, 8, 512], BF16, 2)
    ug = ph.sbn("ug", [128, 514], F32, 2)
    uv = ph.sbn("uv", [128, 514], F32, 2)
    ag = ph.sbn("ag", [128, 512], F32, 2)
    av = ph.sbn("av", [128, 512], F32, 2)
    sg = ph.sbn("sg", [128, 512], F32, 2)
    act = ph.sbn("act", [128, 512], BF16, 3)
    halo = ph.sb("halo", [128, 44, 2], F32)
    pH = ph.psn("pH", [128, 512], F32, 2)
    pT = ph.ps("pT", [128, 8, 128], BF16)
    pU = ph.psn("pU", [128, 512], F32, 4)
    kb.op('pool', lambda E: E.memset(halo[:], 0.0), w=[halo])

    def conv3(ps_, dst, src, fb):
        kb.op('act', lambda E: E.activation(out=dst[:], in_=ps_[:], func=AF.Copy, scale=cw[:, 2, fb:fb + 1]), r=[ps_, cw], w=[dst])
        for j in range(2):
            kb.op('dve', lambda E: E.scalar_tensor_tensor(out=dst[:], in0=src[:, j:j + 512], scalar=cw[:, j, fb:fb + 1], in1=dst[:],
                                                       op0=ALU.mult, op1=ALU.add), r=[(src, 'b'), (src, 'h'), cw, dst], w=[dst])

    def ftiles(s):
        hT = hnT[s % 2]
        for t4 in range(4):
            t = s * 4 + t4
            b = t % 2
            kb.dma('sp', oT[b][:], io['ocatT'][t, :, :, :],
                   r=['ocatT_a', 'ocatT_b', 'ocatT_c'], w=[oT[b]])
            kb.dma('sp', xt[b][:], io['x'][t * 128:(t + 1) * 128, :], w=[xt[b]])
            yield
            for half in range(2):
                p = pH[half]
                for c in range(12):
                    kb.op('pe', lambda E: E.matmul(p[:], lhsT=oT[b][:, c, :], rhs=woutb[:, c, half * 512:(half + 1) * 512],
                                                   start=(c == 0), stop=(c == 11)), r=[oT[b], (woutb, c)], w=[p])
                kb.op('dve', lambda E: E.tensor_tensor(out=hs[b][:, half * 512:(half + 1) * 512], in0=p[:],
                                                       in1=xt[b][:, half * 512:(half + 1) * 512], op=ALU.add),
                      r=[p, xt[b]], w=[(hs[b], half)])
            yield
            kb.dma('pool', io['h_s'][t * 128:(t + 1) * 128, :], hs[b][:], r=[(hs[b], 0), (hs[b], 1)], w=['h_s'])
            rms_rstd(kb, hs[b][:], junk, ss[b], rs[b], D, [(hs[b], 0), (hs[b], 1)])
            kb.op('act', lambda E: E.activation(out=hn[b][:], in_=hs[b][:], func=AF.Copy, scale=rs[b][:, 0:1]),
                  r=[(hs[b], 0), (hs[b], 1), rs[b]], w=[hn[b]])
            yield
            yield
            for c in range(8):
                kb.op('pe', lambda E: E.transpose(out=pT[:, c, :], in_=hn[b][:, c * 128:(c + 1) * 128], identity=ident[:]),
                      r=[hn[b], ident], w=[pT])
            kb.op('act', lambda E: E.copy(out=hT[:, :, t4 * 128:(t4 + 1) * 128], in_=pT[:]), r=[pT], w=[hT])
            yield

    def fup(s):
        hT = hnT[s % 2]
        for fb in range(22):
            k2 = fb % 2
            pg = pU[2 * k2]
            pv = pU[2 * k2 + 1]
            for (p_, f0) in ((pg, fb), (pv, 22 + fb)):
                for c in range(8):
                    kb.op('pe', lambda E: E.matmul(p_[:], lhsT=wupb[:, c, f0 * 128:(f0 + 1) * 128], rhs=hT[:, c, :],
                                                   start=(c == 0), stop=(c == 7)), r=[hT, (wupb, c)], w=[p_])
            yield
            g_, v_ = ug[k2], uv[k2]
            kb.op('act', lambda E: E.copy(out=g_[:, 2:514], in_=pg[:]), r=[pg], w=[(g_, 'b')])
            kb.op('act', lambda E: E.copy(out=v_[:, 2:514], in_=pv[:]), r=[pv], w=[(v_, 'b')])
            for (u_, f0) in ((g_, fb), (v_, 22 + fb)):
                kb.op('dve', lambda E: E.tensor_copy(out=u_[:, 0:2], in_=halo[:, f0, :]), r=[(halo, f0)], w=[(u_, 'h')])
                kb.op('dve', lambda E: E.tensor_copy(out=halo[:, f0, :], in_=u_[:, 512:514]), r=[(u_, 'b')], w=[(halo, f0)])
            conv3(pg, ag[k2], g_, fb)
            conv3(pv, av[k2], v_, 22 + fb)
            kb.op('act', lambda E: E.activation(out=sg[k2][:], in_=ag[k2][:], func=AF.Silu), r=[ag[k2]], w=[sg[k2]])
            a_ = act[fb % 3]
            kb.op('dve', lambda E: E.tensor_tensor(out=a_[:], in0=sg[k2][:], in1=av[k2][:], op=ALU.mult), r=[sg[k2], av[k2]], w=[a_])
            kb.dma('pool', io['actT'][4 * s:4 * s + 4, :, fb, :].rearrange("j p t -> p j t"), a_[:].rearrange("p (j t) -> p j t", j=4), r=[a_], w=['actT'])
            yield

    def interleave(gens):
        gens = list(gens)
        while gens:
            for g_ in list(gens):
                try:
                    next(g_)
                except StopIteration:
                    gens.remove(g_)

    NS = S // 512
    interleave([ftiles(0)])
    for s in range(NS):
        gl = [fup(s)]
        if s + 1 < NS:
            gl.append(ftiles(s + 1))
        interleave(gl)
    ph.close()

    ph = Phase(kb, "f2")
    wdnb = ph.sb("wdnb", [128, 22, D], BF16)
    load_cast_weight(kb, ph, wdnb, io['ffn_w_down'], 22, D)
    aT = ph.sbn("aT", [128, 22, 128], BF16, 3)
    hs = ph.sbn("hs", [128, D], F32, 3)
    ot = ph.sbn("ot", [128, D], F32, 2)
    pD = ph.psn("pD", [128, 512], F32, 4)

    def ld(t):
        kb.dma('sp', aT[t % 3][:], io['actT'][t, :, :, :], r=['actT'], w=[aT[t % 3]])
        kb.dma('sp', hs[t % 3][:], io['h_s'][t * 128:(t + 1) * 128, :], r=['h_s'], w=[hs[t % 3]])

    ld(0)
    ld(1)
    for t in range(NT):
        b = t % 2
        a_, h_ = aT[t % 3], hs[t % 3]
        if t + 2 < NT:
            ld(t + 2)
        for half in range(2):
            p = pD[2 * b + half]
            for c in range(22):
                kb.op('pe', lambda E: E.matmul(p[:], lhsT=a_[:, c, :], rhs=wdnb[:, c, half * 512:(half + 1) * 512],
                                               start=(c == 0), stop=(c == 21)), r=[a_, (wdnb, c)], w=[p])
            kb.op('dve', lambda E: E.tensor_tensor(out=ot[b][:, half * 512:(half + 1) * 512], in0=p[:],
                                                   in1=h_[:, half * 512:(half + 1) * 512], op=ALU.add),
                  r=[p, h_], w=[(ot[b], half)])
        kb.dma('pool', io['out'][t * 128:(t + 1) * 128, :], ot[b][:], r=[(ot[b], 0), (ot[b], 1)], w=['out'])
    ph.close()


W_NAMES = ['attn_norm_w', 'mem_norm_w', 'w_in', 'gdn_conv_w', 'gdn_a_log', 'gdn_dt_bias', 'gdn_out_norm_w',
           'nsa_q_norm_w', 'nsa_kc_norm_w', 'nsa_ks_norm_w', 'nsa_kw_norm_w', 'nsa_cmp_pos_k', 'nsa_cmp_pos_v',
           'nsa_cmp_k_w1', 'nsa_cmp_k_w2', 'nsa_cmp_v_w1', 'nsa_cmp_v_w2', 'mem_w_kv', 'mem_q_norm_w', 'mem_k_norm_w',
           'w_out', 'ffn_norm_w', 'ffn_w_up', 'ffn_conv_w', 'ffn_w_down']
W_SHAPES = {
    'attn_norm_w': [D], 'mem_norm_w': [D], 'w_in': [D, INW], 'gdn_conv_w': [4, 1536], 'gdn_a_log': [4], 'gdn_dt_bias': [4],
    'gdn_out_norm_w': [128], 'nsa_q_norm_w': [64], 'nsa_kc_norm_w': [64], 'nsa_ks_norm_w': [64], 'nsa_kw_norm_w': [64],
    'nsa_cmp_pos_k': [32, 64], 'nsa_cmp_pos_v': [32, 64], 'nsa_cmp_k_w1': [2048, 64], 'nsa_cmp_k_w2': [64, 64],
    'nsa_cmp_v_w1': [2048, 64], 'nsa_cmp_v_w2': [64, 64], 'mem_w_kv': [D, 1024], 'mem_q_norm_w': [128], 'mem_k_norm_w': [128],
    'w_out': [1536, D], 'ffn_norm_w': [D], 'ffn_w_up': [D, 2 * FF], 'ffn_conv_w': [3, 2 * FF], 'ffn_w_down': [FF, D],
}


def make_consts():
    c = {}
    c['c_ident'] = np.eye(128, dtype=np.float32).astype(ml_dtypes.bfloat16)
    idx = np.arange(128)
    same = (idx[:, None] // 64) == (idx[None, :] // 64)
    c['c_btri'] = (same & (idx[:, None] <= idx[None, :])).astype(np.float32)
    c['c_bones'] = same.astype(np.float32)
    c['c_strict'] = (same & (idx[:, None] > idx[None, :])).astype(np.float32)
    c['c_mlow'] = np.where(same & (idx[:, None] >= idx[None, :]), 0.0, NEG).astype(np.float32)
    c['c_mup'] = np.ascontiguousarray(c['c_mlow'].T)
    c['c_mch'] = np.stack([(idx < 64), (idx >= 64)], axis=1).astype(np.float32)
    bf = ml_dtypes.bfloat16
    c['c_tril'] = np.where(idx[:, None] <= idx[None, :], 0.0, NEG).astype(np.float32).astype(bf)
    c['c_far'] = np.where(idx[None, :] < idx[:, None], 0.0, NEG).astype(np.float32).astype(bf)
    cm = np.zeros((9, 128, 512), np.float32)
    f = np.arange(512)
    for m in range(9):
        bt, s_ = (0, m) if m < 5 else (1, m - 1)
        blk = 128 * bt + idx
        vis = (16 * blk[:, None] + 31 <= 512 * s_ + f[None, :]) & (blk[:, None] < 255)
        cm[m] = np.where(vis, 0.0, NEG)
    c['c_cmask'] = cm.astype(bf)
    kk = np.arange(S)
    c['c_E'] = np.where((kk[None, :] // 64) == np.arange(64)[:, None], -NEG, 0.0).astype(np.float32).astype(bf)
    ci = np.arange(256) * 16
    sj = np.arange(64) * 64
    ovl = np.clip(np.minimum(ci[:, None] + 32, sj[None, :] + 64) - np.maximum(ci[:, None], sj[None, :]), 0, None) / 16.0
    ovl[255] = 0.0
    c['c_ovl'] = np.ascontiguousarray(ovl.reshape(2, 128, 64).transpose(1, 0, 2)).astype(np.float32).astype(bf)
    pos = np.arange(S, dtype=np.float32)
    inv = (1.0 / (np.float32(500000.0) ** (np.arange(0, 16, 2, dtype=np.float32) / np.float32(16)))).astype(np.float32)
    ang = pos[:, None] * inv[None, :]
    c['c_rope'] = np.concatenate([np.cos(ang), np.sin(ang)], axis=1).astype(np.float32)
    tt = np.arange(S)
    cur = tt // 64
    blk = np.arange(64)
    valid = blk[None, :] <= cur[:, None]
    forced = (blk[None, :] == 0) | (blk[None, :] == cur[:, None]) | (blk[None, :] == cur[:, None] - 1)
    c['c_A'] = (valid & ~forced).astype(np.float32).reshape(NT, 128, 64)
    c['c_B'] = np.where(valid, np.where(forced, 1e6, 0.0), -1e9).astype(np.float32).reshape(NT, 128, 64)
    return c


def build_program(dbg=False, phases=('ip', 'mem', 'gdn', 'nsa', 'ffn'), dbg_ocat=False):
    nc = bass.Bass("TRN2", target_bir_lowering=False)
    io = {}
    io['x'] = nc.dram_tensor("x", [S, D], F32, kind="ExternalInput").ap()
    io['mem'] = nc.dram_tensor("mem", [256, D], F32, kind="ExternalInput").ap()
    for n in W_NAMES:
        io[n] = nc.dram_tensor(n, W_SHAPES[n], F32, kind="ExternalInput").ap()
    for n, v in make_consts().items():
        io[n] = nc.dram_tensor(n, list(v.shape), BF16 if v.dtype == ml_dtypes.bfloat16 else F32, kind="ExternalInput").ap()
    io['out'] = nc.dram_tensor("out", [S, D], F32, kind="ExternalOutput").ap()
    sk = "ExternalOutput" if dbg else "Internal"
    io['tm'] = nc.dram_tensor("tm", [S, TMW], F32, kind=sk).ap()
    io['qkv_tm'] = nc.dram_tensor("qkv_tm", [S, 1536], BF16, kind=sk).ap()
    if dbg_ocat:
        io['ocatT'] = nc.dram_tensor("ocatT", [NT, 128, 12, 128], BF16, kind="ExternalInput").ap()
    else:
        io['ocatT'] = nc.dram_tensor("ocatT", [NT, 128, 12, 128], BF16, kind=sk).ap()
    io['h_s'] = nc.dram_tensor("h_s", [S, D], F32, kind=sk).ap()
    io['actT'] = nc.dram_tensor("actT", [NT, 128, 22, 128], BF16, kind="Internal").ap()
    kb = KB(nc)
    if 'ip' in phases:
        phase_inproj(kb, io)
    if 'mem' in phases:
        phase_mem(kb, io)
    if 'gdn' in phases:
        phase_gdn(kb, io)
    if 'nsa' in phases:
        phase_nsa(kb, io)
    if 'ffn' in phases:
        phase_ffn(kb, io)
    kb.finish()
    return nc, kb


def make_in_maps(inputs):
    consts = make_consts()
    maps = []
    for b in range(8):
        m = {'x': np.ascontiguousarray(inputs['x'][b]), 'mem': np.ascontiguousarray(inputs['mem'][b])}
        for n in W_NAMES:
            m[n] = np.ascontiguousarray(np.asarray(inputs[n])[0])
        m.update(consts)
        maps.append(m)
    return maps


def kernel(**inputs):
    nc, kb = build_program()
    maps = make_in_maps(inputs)
    res = run_bass_kernel_spmd(nc, maps, core_ids=list(range(8)))
    return np.stack([np.asarray(r['out'], dtype=np.float32) for r in res.results], axis=0)
```
